# Optimizing a Trainium2 kernel written in Bass

```python
import math
import jax
import jax.numpy as jnp
from jax import lax
import numpy as np

D_MODEL = 1024
BATCH = 16
SEQ = 256
DEPTH = 4
DEC_BATCH = 8
DEC_SEQ = 4096
PAST_LEN = 512

GRID_W = 64
N_MIXERS = 4
EPS = 1e-6
ROPE_BASE = 10000.0
Q_BLOCK = 128

WA_HEADS = 16
WA_KV_HEADS = 4
WA_HEAD_DIM = D_MODEL // WA_HEADS
WA_GROUP = WA_HEADS // WA_KV_HEADS
WINDOW = 128

GLA_HEADS = 4
GLA_DK = D_MODEL // 2 // GLA_HEADS
GLA_DV = D_MODEL // GLA_HEADS
GLA_RANK = 16
GLA_TAU = 16.0
GLA_CHUNK = 64

DA_HEADS = 8
DA_HEAD_DIM = D_MODEL // DA_HEADS // 2

SSD_D_INNER = 2 * D_MODEL
SSD_HEAD_DIM = 64
SSD_HEADS = SSD_D_INNER // SSD_HEAD_DIM
SSD_GROUPS = 4
SSD_STATE = 128
SSD_CONV = 3
SSD_CHUNK = 128

D_FF = 256 * ((8 * D_MODEL // 3 + 255) // 256)
FFN_CONV = 3

kernel_name = 'hybrid_prefix_diffusion_step'


def rms_norm(x, g):
    xf = x.astype(jnp.float32)
    y = xf * lax.rsqrt(jnp.mean(xf * xf, axis=-1, keepdims=True) + EPS)
    return (y * g.astype(jnp.float32)).astype(x.dtype)


def modulate(h, shift, scale):
    return h * (1.0 + scale) + shift


def ada_modulation(cond, w, b):
    m = jax.nn.silu(cond) @ w + b
    m = m.reshape(cond.shape[:-1] + (6, 1, D_MODEL))
    return [m[..., k, :, :] for k in range(6)]


def _to_chunks(a, size):
    b, l = a.shape[:2]
    return jnp.moveaxis(a.reshape((b, l // size, size) + a.shape[2:]), 1, 0)


def _from_chunks(a):
    a = jnp.moveaxis(a, 0, 1)
    return a.reshape((a.shape[0], a.shape[1] * a.shape[2]) + a.shape[3:])


def dwconv_centered(x, w, b):
    k, ch = w.shape
    y = lax.conv_general_dilated(x, w.reshape(k, 1, ch).astype(x.dtype), window_strides=(1,),
                                 padding=[(k // 2, k // 2)], dimension_numbers=('NWC', 'WIO', 'NWC'),
                                 feature_group_count=ch)
    return y + b.astype(x.dtype)


def axial_rope_tables(n_tokens, dim):
    rows = n_tokens // GRID_W
    row = jnp.repeat(jnp.arange(rows, dtype=jnp.float32), GRID_W)
    col = jnp.tile(jnp.arange(GRID_W, dtype=jnp.float32), rows)
    axis_dim = dim // 2
    inv = ROPE_BASE ** (-jnp.arange(0, axis_dim, 2, dtype=jnp.float32) / axis_dim)
    ar = row[:, None] * inv
    ac = col[:, None] * inv
    return jnp.cos(ar), jnp.sin(ar), jnp.cos(ac), jnp.sin(ac)


def _rope_half(x, cos, sin):
    x1, x2 = jnp.split(x, 2, axis=-1)
    return jnp.concatenate([x1 * cos - x2 * sin, x2 * cos + x1 * sin], axis=-1)


def apply_axial_rope(x, tabs):
    cos_r, sin_r, cos_c, sin_c = tabs
    shape = (x.shape[1],) + (1,) * (x.ndim - 3) + (cos_r.shape[-1],)
    f = lambda a: a.reshape(shape).astype(x.dtype)
    xr, xc = jnp.split(x, 2, axis=-1)
    return jnp.concatenate([_rope_half(xr, f(cos_r), f(sin_r)), _rope_half(xc, f(cos_c), f(sin_c))], axis=-1)


def window_attn_qkv(h, w_qkv):
    b, l, _ = h.shape
    nq = WA_HEADS * WA_HEAD_DIM
    nkv = WA_KV_HEADS * WA_HEAD_DIM
    q, k, v = jnp.split(h @ w_qkv, [nq, nq + nkv], axis=-1)
    return (q.reshape(b, l, WA_KV_HEADS, WA_GROUP, WA_HEAD_DIM),
            k.reshape(b, l, WA_KV_HEADS, WA_HEAD_DIM),
            v.reshape(b, l, WA_KV_HEADS, WA_HEAD_DIM))


def gqa_sink_block(qb, kk, vv, mask, sink):
    s = jnp.einsum('bqhgd,bkhd->bhgqk', qb, kk, preferred_element_type=jnp.float32) * WA_HEAD_DIM ** -0.5
    if mask is not None:
        s = jnp.where(mask, s, -jnp.inf)
    sk = jnp.broadcast_to(sink.astype(jnp.float32).reshape(1, WA_KV_HEADS, WA_GROUP, 1, 1), s.shape[:-1] + (1,))
    p = jax.nn.softmax(jnp.concatenate([s, sk], axis=-1), axis=-1)[..., :-1]
    o = jnp.einsum('bhgqk,bkhd->bqhgd', p.astype(vv.dtype), vv)
    return o.reshape(qb.shape[0], qb.shape[1], WA_HEADS * WA_HEAD_DIM)


def window_attn_context(h, w_qkv, w_o, sink):
    q, k, v = window_attn_qkv(h, w_qkv)
    def block(i):
        qb = lax.dynamic_slice_in_dim(q, i * Q_BLOCK, Q_BLOCK, axis=1)
        return gqa_sink_block(qb, k, v, None, sink)
    o = _from_chunks(lax.map(block, jnp.arange(h.shape[1] // Q_BLOCK)))
    return o @ w_o, k, v


def window_attn_latent(h, k_ctx, v_ctx, w_qkv, w_o, sink):
    q, k, v = window_attn_qkv(h, w_qkv)
    l = h.shape[1]
    tabs = axial_rope_tables(l, WA_HEAD_DIM)
    q = apply_axial_rope(q, tabs)
    k = apply_axial_rope(k, tabs)
    pad = ((0, 0), (Q_BLOCK, Q_BLOCK), (0, 0), (0, 0))
    kp = jnp.pad(k, pad)
    vp = jnp.pad(v, pad)
    n_ctx = k_ctx.shape[1]
    k_ctx = k_ctx.astype(k.dtype)
    v_ctx = v_ctx.astype(v.dtype)
    diff = jnp.arange(Q_BLOCK)[:, None] - jnp.arange(3 * Q_BLOCK)[None, :] + Q_BLOCK
    band = jnp.abs(diff) <= WINDOW
    ctx_mask = jnp.ones((Q_BLOCK, n_ctx), dtype=bool)
    def block(i):
        qb = lax.dynamic_slice_in_dim(q, i * Q_BLOCK, Q_BLOCK, axis=1)
        kb = lax.dynamic_slice_in_dim(kp, i * Q_BLOCK, 3 * Q_BLOCK, axis=1)
        vb = lax.dynamic_slice_in_dim(vp, i * Q_BLOCK, 3 * Q_BLOCK, axis=1)
        kpos = (i - 1) * Q_BLOCK + jnp.arange(3 * Q_BLOCK)
        lat_mask = band & ((kpos >= 0) & (kpos < l))[None, :]
        mask = jnp.concatenate([lat_mask, ctx_mask], axis=1)
        kk = jnp.concatenate([kb, k_ctx], axis=1)
        vv = jnp.concatenate([vb, v_ctx], axis=1)
        return gqa_sink_block(qb, kk, vv, mask, sink)
    o = _from_chunks(lax.map(block, jnp.arange(l // Q_BLOCK)))
    return o @ w_o


def gla_scan(q, k, v, logg, s0):
    f32 = jnp.float32
    causal = jnp.tril(jnp.ones((GLA_CHUNK, GLA_CHUNK), dtype=bool))
    def step(s, inp):
        qc, kc, vc, gc = inp
        bcum = jnp.cumsum(gc, axis=1)
        blast = bcum[:, -1]
        qd = qc * jnp.exp(bcum)
        kd = kc * jnp.exp(-bcum)
        att = jnp.where(causal, jnp.einsum('bihd,bjhd->bhij', qd, kd), 0.0)
        o = jnp.einsum('bhij,bjhv->bihv', att, vc) + jnp.einsum('bihd,bhdv->bihv', qd, s)
        kr = kc * jnp.exp(blast[:, None] - bcum)
        s = s * jnp.exp(blast)[..., None] + jnp.einsum('bjhd,bjhv->bhdv', kr, vc)
        return s, o
    xs = tuple(_to_chunks(a.astype(f32), GLA_CHUNK) for a in (q, k, v, logg))
    s, o = lax.scan(step, s0.astype(f32), xs)
    return _from_chunks(o), s


def gla_mixer(h, s0_f, s0_b, w_qkvr, w_gf1, w_gf2, b_gf, w_gb1, w_gb2, b_gb, g_head, w_o):
    b, l, _ = h.shape
    nk = GLA_HEADS * GLA_DK
    nv = GLA_HEADS * GLA_DV
    q, k, v, r = jnp.split(h @ w_qkvr, [nk, 2 * nk, 2 * nk + nv], axis=-1)
    q = q.reshape(b, l, GLA_HEADS, GLA_DK) * GLA_DK ** -0.5
    k = k.reshape(b, l, GLA_HEADS, GLA_DK)
    v = v.reshape(b, l, GLA_HEADS, GLA_DV)
    def log_gate(w1, w2, bias):
        z = ((h @ w1) @ w2 + bias).astype(jnp.float32)
        return (jax.nn.log_sigmoid(z) / GLA_TAU).reshape(b, l, GLA_HEADS, GLA_DK)
    flip = lambda a: jnp.flip(a, axis=1)
    o_f, s_f = gla_scan(q, k, v, log_gate(w_gf1, w_gf2, b_gf), s0_f)
    o_b, s_b = gla_scan(flip(q), flip(k), flip(v), flip(log_gate(w_gb1, w_gb2, b_gb)), s0_b)
    o = rms_norm(o_f + flip(o_b), g_head).reshape(b, l, nv) * jax.nn.silu(r.astype(jnp.float32))
    return o.astype(h.dtype) @ w_o, s_f, s_b


def diff_qkv(h, w_qkv):
    b, l, _ = h.shape
    q, k, v = jnp.split(h @ w_qkv, 3, axis=-1)
    return (q.reshape(b, l, DA_HEADS, 2, DA_HEAD_DIM),
            k.reshape(b, l, DA_HEADS, 2, DA_HEAD_DIM),
            v.reshape(b, l, DA_HEADS, 2 * DA_HEAD_DIM))


def diff_lambda(lq1, lk1, lq2, lk2, lam_init):
    f = lambda a: a.astype(jnp.float32)
    return jnp.exp(jnp.sum(f(lq1) * f(lk1))) - jnp.exp(jnp.sum(f(lq2) * f(lk2))) + lam_init


def diff_sweep(q, kk, vv, lam):
    def block(i):
        qb = lax.dynamic_slice_in_dim(q, i * Q_BLOCK, Q_BLOCK, axis=1)
        s = jnp.einsum('bqhcd,bkhcd->bhcqk', qb, kk, preferred_element_type=jnp.float32) * DA_HEAD_DIM ** -0.5
        p = jax.nn.softmax(s, axis=-1)
        a = p[:, :, 0] - lam * p[:, :, 1]
        return jnp.einsum('bhqk,bkhe->bqhe', a.astype(vv.dtype), vv)
    return _from_chunks(lax.map(block, jnp.arange(q.shape[1] // Q_BLOCK)))


def diff_out(o, g_sub, lam_init, w_o):
    b, l = o.shape[:2]
    o = rms_norm(o, g_sub) * (1.0 - lam_init)
    return o.reshape(b, l, DA_HEADS * 2 * DA_HEAD_DIM) @ w_o


def diff_attn_context(h, w_qkv, lam, lam_init, g_sub, w_o):
    q, k, v = diff_qkv(h, w_qkv)
    return diff_out(diff_sweep(q, k, v, lam), g_sub, lam_init, w_o), k, v


def diff_attn_latent(h, k_ctx, v_ctx, w_qkv, lam, lam_init, g_sub, w_o):
    q, k, v = diff_qkv(h, w_qkv)
    tabs = axial_rope_tables(h.shape[1], DA_HEAD_DIM)
    q = apply_axial_rope(q, tabs)
    k = apply_axial_rope(k, tabs)
    kk = jnp.concatenate([k, k_ctx.astype(k.dtype)], axis=1)
    vv = jnp.concatenate([v, v_ctx.astype(v.dtype)], axis=1)
    return diff_out(diff_sweep(q, kk, vv, lam), g_sub, lam_init, w_o)


def ssd_scan(x, dt, a, bm, cm, s0):
    f32 = jnp.float32
    b, l = x.shape[:2]
    r = SSD_HEADS // SSD_GROUPS
    xg = x.astype(f32).reshape(b, l, SSD_GROUPS, r, SSD_HEAD_DIM)
    dtg = dt.astype(f32).reshape(b, l, SSD_GROUPS, r)
    lag = dtg * a.astype(f32).reshape(SSD_GROUPS, r)
    causal = jnp.tril(jnp.ones((SSD_CHUNK, SSD_CHUNK), dtype=bool))
    def step(s, inp):
        xc, dtc, lac, bc, cc = inp
        cum = jnp.cumsum(lac, axis=1)
        cum_t = jnp.moveaxis(cum, 1, -1)
        seg = cum_t[..., :, None] - cum_t[..., None, :]
        decay = jnp.exp(jnp.where(causal, seg, -jnp.inf))
        cb = jnp.einsum('bign,bjgn->bgij', cc, bc)
        w = cb[:, :, None] * decay * jnp.moveaxis(dtc, 1, -1)[..., None, :]
        y = jnp.einsum('bgrij,bjgrp->bigrp', w, xc)
        y = y + jnp.einsum('bign,bgrpn->bigrp', cc, s) * jnp.exp(cum)[..., None]
        to_end = jnp.exp(cum[:, -1:] - cum) * dtc
        s = s * jnp.exp(cum[:, -1])[..., None, None] + jnp.einsum('bjgr,bjgrp,bjgn->bgrpn', to_end, xc, bc)
        return s, y
    xs = tuple(_to_chunks(t, SSD_CHUNK) for t in (xg, dtg, lag, bm.astype(f32), cm.astype(f32)))
    s, y = lax.scan(step, s0.astype(f32).reshape(b, SSD_GROUPS, r, SSD_HEAD_DIM, SSD_STATE), xs)
    return (_from_chunks(y).reshape(b, l, SSD_HEADS, SSD_HEAD_DIM),
            s.reshape(b, SSD_HEADS, SSD_HEAD_DIM, SSD_STATE))


def ssd_mixer(h, s0_f, s0_b, w_in, conv_w, conv_b, a_log_f, a_log_b, dt_bias_f, dt_bias_b, d_skip, g_norm, w_out):
    b, l, _ = h.shape
    gn = SSD_GROUPS * SSD_STATE
    z, xbc, dt = jnp.split(h @ w_in, [SSD_D_INNER, 2 * SSD_D_INNER + 2 * gn], axis=-1)
    xbc = jax.nn.silu(dwconv_centered(xbc, conv_w, conv_b))
    x, bm, cm = jnp.split(xbc, [SSD_D_INNER, SSD_D_INNER + gn], axis=-1)
    x = x.reshape(b, l, SSD_HEADS, SSD_HEAD_DIM)
    bm = bm.reshape(b, l, SSD_GROUPS, SSD_STATE)
    cm = cm.reshape(b, l, SSD_GROUPS, SSD_STATE)
    dt = dt.astype(jnp.float32)
    dt_f = jax.nn.softplus(dt[..., :SSD_HEADS] + dt_bias_f.astype(jnp.float32))
    dt_b = jax.nn.softplus(dt[..., SSD_HEADS:] + dt_bias_b.astype(jnp.float32))
    flip = lambda a: jnp.flip(a, axis=1)
    y_f, s_f = ssd_scan(x, dt_f, -jnp.exp(a_log_f.astype(jnp.float32)), bm, cm, s0_f)
    y_b, s_b = ssd_scan(flip(x), flip(dt_b), -jnp.exp(a_log_b.astype(jnp.float32)), flip(bm), flip(cm), s0_b)
    y = y_f + flip(y_b) + x.astype(jnp.float32) * d_skip.astype(jnp.float32)[:, None]
    y = y.reshape(b, l, SSD_D_INNER) * jax.nn.silu(z.astype(jnp.float32))
    y = rms_norm(y, g_norm)
    return y.astype(h.dtype) @ w_out, s_f, s_b


def conv_ffn(h, w_up, conv_w, conv_b, w_down):
    u = dwconv_centered(h @ w_up, conv_w, conv_b)
    g, v = jnp.split(u, 2, axis=-1)
    return (jax.nn.silu(g) * v) @ w_down


def setup_inputs(seed: int = 0) -> dict:
    key = jax.random.key(seed)
    ks = jax.random.split(key, 64)
    counter = [0]
    f32 = jnp.float32
    def nxt():
        counter[0] += 1
        return ks[counter[0] - 1]
    def nrm(shape, scale=1.0):
        return jax.random.normal(nxt(), shape, f32) * scale
    def lin(shape):
        return nrm(shape, shape[-2] ** -0.5)
    def gain(shape):
        return 1.0 + nrm(shape, 0.02)
    na, nb, nc, nd = [len(range(m, DEPTH, N_MIXERS)) for m in range(N_MIXERS)]
    nk = GLA_HEADS * GLA_DK
    nv = GLA_HEADS * GLA_DV
    gn = SSD_GROUPS * SSD_STATE
    dt0 = jnp.exp(jax.random.uniform(nxt(), (nd, 2, SSD_HEADS), f32, math.log(1e-3), math.log(1e-1)))
    dt_bias = dt0 + jnp.log(-jnp.expm1(-dt0))
    a_log = jnp.log(jax.random.uniform(nxt(), (nd, 2, SSD_HEADS), f32, 1.0, 16.0))
    return {
        'x_prompt': nrm((BATCH, SEQ, D_MODEL)),
        'x_sample': nrm((DEC_BATCH, DEC_SEQ, D_MODEL)),
        'c': nrm((DEC_BATCH, D_MODEL)),
        'c_ctx': nrm((D_MODEL,)),
        'cache_win_k': nrm((DEC_BATCH, na, PAST_LEN, WA_KV_HEADS, WA_HEAD_DIM)),
        'cache_win_v': nrm((DEC_BATCH, na, PAST_LEN, WA_KV_HEADS, WA_HEAD_DIM)),
        'state_gla_fwd': nrm((DEC_BATCH, nb, GLA_HEADS, GLA_DK, GLA_DV), 0.1),
        'state_gla_bwd': nrm((DEC_BATCH, nb, GLA_HEADS, GLA_DK, GLA_DV), 0.1),
        'cache_diff_k': nrm((DEC_BATCH, nc, PAST_LEN, DA_HEADS, 2, DA_HEAD_DIM)),
        'cache_diff_v': nrm((DEC_BATCH, nc, PAST_LEN, DA_HEADS, 2 * DA_HEAD_DIM)),
        'state_ssd_fwd': nrm((DEC_BATCH, nd, SSD_HEADS, SSD_HEAD_DIM, SSD_STATE), 0.1),
        'state_ssd_bwd': nrm((DEC_BATCH, nd, SSD_HEADS, SSD_HEAD_DIM, SSD_STATE), 0.1),
        'ada_w': lin((DEPTH, D_MODEL, 6 * D_MODEL)),
        'ada_b': nrm((DEPTH, 6 * D_MODEL), 0.02),
        'norm_mix': gain((DEPTH, D_MODEL)),
        'norm_ffn': gain((DEPTH, D_MODEL)),
        'ffn_w_up': lin((DEPTH, D_MODEL, 2 * D_FF)),
        'ffn_conv_w': nrm((DEPTH, FFN_CONV, 2 * D_FF), FFN_CONV ** -0.5),
        'ffn_conv_b': nrm((DEPTH, 2 * D_FF), 0.01),
        'ffn_w_down': lin((DEPTH, D_FF, D_MODEL)),
        'final_norm': gain((D_MODEL,)),
        'win_w_qkv': lin((na, D_MODEL, (WA_HEADS + 2 * WA_KV_HEADS) * WA_HEAD_DIM)),
        'win_w_o': lin((na, WA_HEADS * WA_HEAD_DIM, D_MODEL)),
        'win_sink': nrm((na, WA_HEADS)),
        'gla_w_qkvr': lin((nb, D_MODEL, 2 * nk + 2 * nv)),
        'gla_w_gf1': lin((nb, D_MODEL, GLA_RANK)),
        'gla_w_gf2': lin((nb, GLA_RANK, nk)),
        'gla_b_gf': nrm((nb, nk), 0.01),
        'gla_w_gb1': lin((nb, D_MODEL, GLA_RANK)),
        'gla_w_gb2': lin((nb, GLA_RANK, nk)),
        'gla_b_gb': nrm((nb, nk), 0.01),
        'gla_norm': gain((nb, GLA_DV)),
        'gla_w_o': lin((nb, nv, D_MODEL)),
        'diff_w_qkv': lin((nc, D_MODEL, 3 * DA_HEADS * 2 * DA_HEAD_DIM)),
        'diff_lq1': nrm((nc, DA_HEAD_DIM), 0.1),
        'diff_lk1': nrm((nc, DA_HEAD_DIM), 0.1),
        'diff_lq2': nrm((nc, DA_HEAD_DIM), 0.1),
        'diff_lk2': nrm((nc, DA_HEAD_DIM), 0.1),
        'diff_norm': gain((nc, 2 * DA_HEAD_DIM)),
        'diff_w_o': lin((nc, DA_HEADS * 2 * DA_HEAD_DIM, D_MODEL)),
        'ssd_w_in': lin((nd, D_MODEL, 2 * SSD_D_INNER + 2 * gn + 2 * SSD_HEADS)),
        'ssd_conv_w': nrm((nd, SSD_CONV, SSD_D_INNER + 2 * gn), SSD_CONV ** -0.5),
        'ssd_conv_b': nrm((nd, SSD_D_INNER + 2 * gn), 0.01),
        'ssd_a_log_f': a_log[:, 0],
        'ssd_a_log_b': a_log[:, 1],
        'ssd_dt_bias_f': dt_bias[:, 0],
        'ssd_dt_bias_b': dt_bias[:, 1],
        'ssd_d': 1.0 + nrm((nd, SSD_HEADS), 0.1),
        'ssd_norm': gain((nd, SSD_D_INNER)),
        'ssd_w_out': lin((nd, SSD_D_INNER, D_MODEL)),
    }


def reference(x_prompt, x_sample, c, c_ctx, cache_win_k, cache_win_v, state_gla_fwd, state_gla_bwd,
              cache_diff_k, cache_diff_v, state_ssd_fwd, state_ssd_bwd,
              ada_w, ada_b, norm_mix, norm_ffn, ffn_w_up, ffn_conv_w, ffn_conv_b, ffn_w_down, final_norm,
              win_w_qkv, win_w_o, win_sink,
              gla_w_qkvr, gla_w_gf1, gla_w_gf2, gla_b_gf, gla_w_gb1, gla_w_gb2, gla_b_gb, gla_norm, gla_w_o,
              diff_w_qkv, diff_lq1, diff_lk1, diff_lq2, diff_lk2, diff_norm, diff_w_o,
              ssd_w_in, ssd_conv_w, ssd_conv_b, ssd_a_log_f, ssd_a_log_b, ssd_dt_bias_f, ssd_dt_bias_b,
              ssd_d, ssd_norm, ssd_w_out):
    xp, xs = x_prompt, x_sample
    bp = xp.shape[0]
    new_win_k, new_win_v, new_gla_f, new_gla_b = [], [], [], []
    new_diff_k, new_diff_v, new_ssd_f, new_ssd_b = [], [], [], []
    for i in range(DEPTH):
        kind, j = i % N_MIXERS, i // N_MIXERS
        mod_p = ada_modulation(c_ctx, ada_w[i], ada_b[i])
        mod_s = ada_modulation(c, ada_w[i], ada_b[i])
        hp = modulate(rms_norm(xp, norm_mix[i]), mod_p[0], mod_p[1])
        hs = modulate(rms_norm(xs, norm_mix[i]), mod_s[0], mod_s[1])
        if kind == 0:
            op, k_c, v_c = window_attn_context(hp, win_w_qkv[j], win_w_o[j], win_sink[j])
            os_ = window_attn_latent(hs, cache_win_k[:, j], cache_win_v[:, j], win_w_qkv[j], win_w_o[j], win_sink[j])
            new_win_k.append(k_c)
            new_win_v.append(v_c)
        elif kind == 1:
            gla_w = (gla_w_qkvr[j], gla_w_gf1[j], gla_w_gf2[j], gla_b_gf[j], gla_w_gb1[j], gla_w_gb2[j],
                     gla_b_gb[j], gla_norm[j], gla_w_o[j])
            zeros = jnp.zeros((bp, GLA_HEADS, GLA_DK, GLA_DV), jnp.float32)
            op, s_f, s_b = gla_mixer(hp, zeros, zeros, *gla_w)
            os_, _, _ = gla_mixer(hs, state_gla_fwd[:, j], state_gla_bwd[:, j], *gla_w)
            new_gla_f.append(s_f)
            new_gla_b.append(s_b)
        elif kind == 2:
            lam_init = 0.8 - 0.6 * math.exp(-0.3 * i)
            lam = diff_lambda(diff_lq1[j], diff_lk1[j], diff_lq2[j], diff_lk2[j], lam_init)
            op, k_c, v_c = diff_attn_context(hp, diff_w_qkv[j], lam, lam_init, diff_norm[j], diff_w_o[j])
            os_ = diff_attn_latent(hs, cache_diff_k[:, j], cache_diff_v[:, j], diff_w_qkv[j], lam, lam_init,
                                   diff_norm[j], diff_w_o[j])
            new_diff_k.append(k_c)
            new_diff_v.append(v_c)
        else:
            ssd_w = (ssd_w_in[j], ssd_conv_w[j], ssd_conv_b[j], ssd_a_log_f[j], ssd_a_log_b[j],
                     ssd_dt_bias_f[j], ssd_dt_bias_b[j], ssd_d[j], ssd_norm[j], ssd_w_out[j])
            zeros = jnp.zeros((bp, SSD_HEADS, SSD_HEAD_DIM, SSD_STATE), jnp.float32)
            op, s_f, s_b = ssd_mixer(hp, zeros, zeros, *ssd_w)
            os_, _, _ = ssd_mixer(hs, state_ssd_fwd[:, j], state_ssd_bwd[:, j], *ssd_w)
            new_ssd_f.append(s_f)
            new_ssd_b.append(s_b)
        xp = xp + mod_p[2] * op
        xs = xs + mod_s[2] * os_
        ffn_w = (ffn_w_up[i], ffn_conv_w[i], ffn_conv_b[i], ffn_w_down[i])
        hp = modulate(rms_norm(xp, norm_ffn[i]), mod_p[3], mod_p[4])
        hs = modulate(rms_norm(xs, norm_ffn[i]), mod_s[3], mod_s[4])
        xp = xp + mod_p[5] * conv_ffn(hp, *ffn_w)
        xs = xs + mod_s[5] * conv_ffn(hs, *ffn_w)
    y_prompt = rms_norm(xp, final_norm)
    y_sample = rms_norm(xs, final_norm)
    return (y_prompt, y_sample,
            jnp.stack(new_win_k, axis=1), jnp.stack(new_win_v, axis=1),
            jnp.stack(new_gla_f, axis=1), jnp.stack(new_gla_b, axis=1),
            jnp.stack(new_diff_k, axis=1), jnp.stack(new_diff_v, axis=1),
            jnp.stack(new_ssd_f, axis=1), jnp.stack(new_ssd_b, axis=1))
```

```python
import numpy as np
import concourse.bass as bass
import concourse.mybir as mybir
from concourse.bass_utils import run_bass_kernel_spmd

F32 = mybir.dt.float32
BF16 = mybir.dt.bfloat16
AF = mybir.ActivationFunctionType
ALU = mybir.AluOpType
AX = mybir.AxisListType
AP = bass.AP

SB_BASE = 16512
SB_TOP = 229344
EPOCH = 30000


class Buf:
    __slots__ = ("w", "rs", "name")

    def __init__(self, name=""):
        self.w = None
        self.rs = []
        self.name = name


class Op:
    __slots__ = ("eng", "fn", "deps", "dma", "ms", "sem", "val", "need", "prev")

    def __init__(self, eng, fn, dma):
        self.eng = eng
        self.fn = fn
        self.dma = dma
        self.deps = []
        self.ms = None
        self.sem = None
        self.val = None
        self.need = False
        self.prev = None


class Sched:
    ENGS = ("pe", "act", "dve", "pool", "sp")

    def __init__(self, nc):
        self.nc = nc
        self.ops = {e: [] for e in self.ENGS}
        self.dmas_since_barrier = []
        self.all_dmas = []
        self.nps = 0
        self.nacc = 0
        self.psum = []
        for i in range(8):
            t = nc.alloc_psum_tensor("psb%d" % i, [128, 512], F32)
            self.psum.append((t, Buf("ps%d" % i)))
        self.sb_off = SB_BASE
        self.sb_mark = SB_BASE
        self.nalloc = 0

    def alloc(self, shape, dtype, name=None):
        nbytes = int(np.prod(shape[1:])) * (4 if dtype == F32 else 2)
        nbytes = (nbytes + 63) // 64 * 64
        off = self.sb_off
        assert off + nbytes <= SB_TOP, "SBUF overflow %d" % (off + nbytes - SB_TOP)
        self.sb_off += nbytes
        self.nalloc += 1
        t = self.nc.alloc_sbuf_tensor_at("sb%d_%s" % (self.nalloc, name or "t"), list(shape), dtype, offset=off)
        return t

    def mark(self):
        self.sb_mark = self.sb_off

    def release(self):
        self.sb_off = self.sb_mark

    def ps(self, pool="a"):
        if pool == "a":
            t, b = self.psum[self.nps % 4]
            self.nps += 1
        else:
            t, b = self.psum[4 + self.nacc % 4]
            self.nacc += 1
        return t, b

    def op(self, eng, fn, reads=(), writes=(), dma=False):
        o = Op(eng, fn, dma)
        deps = {}
        for b in reads:
            d = b.w
            if d is not None:
                if (not dma) and (not d.dma) and d.eng == eng and eng == "pe":
                    continue
                deps[id(d)] = d
        for b in writes:
            cand = list(b.rs)
            if b.w is not None:
                cand.append(b.w)
            for d in cand:
                if d is o:
                    continue
                if (not dma) and (not d.dma) and d.eng == eng:
                    continue
                deps[id(d)] = d
        o.deps = list(deps.values())
        for b in reads:
            if not dma:
                b.rs = [r for r in b.rs if r.dma or r.eng != eng]
            b.rs.append(o)
        for b in writes:
            b.w = o
            b.rs = []
        self.ops[eng].append(o)
        if dma:
            self.dmas_since_barrier.append(o)
            self.all_dmas.append(o)
        return o

    def barrier(self):
        lasts = []
        for e in self.ENGS:
            for o in reversed(self.ops[e]):
                if not o.dma and o.fn is not None:
                    lasts.append(o)
                    break
        deps = lasts + self.dmas_since_barrier
        self.dmas_since_barrier = []
        for e in self.ENGS:
            o = Op(e, None, False)
            o.deps = list(deps)
            self.ops[e].append(o)

    def dma(self, out, in_, reads=(), writes=(), eng="sp"):
        return self.op(eng, lambda e: e.dma_start(out=out, in_=in_), reads, writes, dma=True)

    def mm(self, out, lhsT, rhs, start, stop, reads=(), writes=()):
        return self.op("pe", lambda e: e.matmul(out, lhsT, rhs, start=start, stop=stop), reads, writes)

    def emit(self):
        nc = self.nc
        for e in self.ENGS:
            for o in self.ops[e]:
                for d in o.deps:
                    d.need = True
        sem_ctx = []
        import contextlib
        with contextlib.ExitStack() as st:
            tl = {}
            for e in ("pe", "act", "dve", "pool"):
                tl[e] = [st.enter_context(nc.semaphore("tl_%s_%d" % (e, i))) for i in range(5)]
            npool = {"sp": 44, "pool": 24, "act": 12}
            dpool = {e: [st.enter_context(nc.semaphore("dq_%s_%d" % (e, i))) for i in range(n)] for e, n in npool.items()}
            for e in self.ENGS:
                m = 0
                k = 0
                for o in self.ops[e]:
                    if o.fn is None:
                        continue
                    if o.dma:
                        P = len(dpool[e])
                        o.sem = dpool[e][k % P]
                        o.val = 16 * (k // P + 1)
                        k += 1
                    elif o.need:
                        o.sem = tl[e][m // EPOCH]
                        o.val = m % EPOCH + 1
                        m += 1
                assert m < EPOCH * 5, (e, m)
            final = {}
            for e, n in npool.items():
                for o in self.ops[e]:
                    if o.dma:
                        final[id(o.sem)] = (o.sem, o.val)

            def stream(e, eng):
                seen = {}

                def wait(sem, val):
                    if seen.get(id(sem), 0) < val:
                        eng.wait_ge(sem, val)
                        seen[id(sem)] = val

                for o in self.ops[e]:
                    for d in o.deps:
                        wait(d.sem, d.val)
                    if o.fn is None:
                        continue
                    if o.dma and o.val > 16:
                        wait(o.sem, o.val - 16)
                    ins = o.fn(eng)
                    if o.dma:
                        ins.then_inc(o.sem, 16)
                    elif o.need:
                        ins.then_inc(o.sem, 1)
                if e == "sp":
                    for sem, val in final.values():
                        wait(sem, val)

            with nc.Block() as block:
                @block.tensor
                def _(eng):
                    stream("pe", eng)

                @block.scalar
                def _(eng):
                    stream("act", eng)

                @block.vector
                def _(eng):
                    stream("dve", eng)

                @block.gpsimd
                def _(eng):
                    stream("pool", eng)

                @block.sync
                def _(eng):
                    stream("sp", eng)
        return {e: len(v) for e, v in self.ops.items()}


def ins(method, *a, **kw):
    return lambda e: getattr(e, method)(*a, **kw)


T = 4608
LS = 4096
TILES = [(i * 512, 512, 0, 0, 4096) for i in range(8)] + [(4096, 256, 1, 4096, 4352), (4352, 256, 1, 4352, 4608)]
EPS = 1e-6
DFF = 2816
LAM_INIT = 0.8 - 0.6 * float(np.exp(-0.3 * 2))


def build_program(NLAYERS=4, dbg=False):
    nc = bass.Bass("TRN2", target_bir_lowering=False)
    s = Sched(nc)
    I = {}
    O = {}

    def din(name, shape):
        I[name] = nc.dram_tensor(name, list(shape), F32, kind="ExternalInput").ap()
        return I[name]

    def dout(name, shape):
        O[name] = nc.dram_tensor(name, list(shape), F32, kind="ExternalOutput").ap()
        return O[name]

    def dscr(name, shape, dt):
        return nc.dram_tensor(name, list(shape), dt).ap()

    xT = din("xT", [128, 8, T])
    condT = din("condT", [128, 8, 2])
    ada_w = din("ada_w", [4, 128, 8, 6144])
    ada_b = din("ada_b", [128, 4, 48])
    nrm = din("nrm", [128, 4, 2, 8])
    fnorm = din("fnorm", [128, 8])
    w_up = din("w_up", [4, 128, 8, 5632])
    w_dn = din("w_dn", [4, 128, 22, 1024])
    ffn_cw = din("ffn_cw", [128, 4, 3, 44])
    ffn_cb = din("ffn_cb", [128, 4, 44])
    perm_in = din("perm", [128, 128])
    rcos = din("rcos", [128, LS])
    rsin = din("rsin", [128, LS])
    wmask_in = din("wmask", [128, 6, 512])
    tri_in = din("tri", [128, 4, 128])
    m64_in = din("m64", [128, 512])
    w0_qk = din("w0_qk", [128, 8, 1536])
    w0_v = din("w0_v", [128, 8, 256])
    w0_o = din("w0_o", [128, 8, 1024])
    sink_in = din("sink", [16])
    ck0 = din("ck0", [128, 4, 512])
    cv0 = din("cv0", [512, 256])
    w1_qk = din("w1_qk", [128, 8, 1024])
    w1_v = din("w1_v", [128, 8, 1024])
    w1_r = din("w1_r", [128, 8, 1024])
    w1_g1 = din("w1_g1", [128, 8, 32])
    w1_g2 = din("w1_g2", [32, 1024])
    w1_gb = din("w1_gb", [128, 8])
    w1_gn = din("w1_gn", [128, 2])
    w1_o = din("w1_o", [128, 8, 1024])
    s1f = din("s1f", [128, 4, 256])
    s1b = din("s1b", [128, 4, 256])
    w2_qk = din("w2_qk", [128, 8, 2048])
    w2_v = din("w2_v", [128, 8, 1024])
    w2_o = din("w2_o", [128, 8, 1024])
    lqk = din("lqk", [4, 64])
    w2_gn = din("w2_gn", [128, 1])
    ck2 = din("ck2", [128, 8, 512])
    cv2 = din("cv2", [512, 1024])

    w3_z = din("w3_z", [128, 8, 2048])
    w3_xbc = din("w3_xbc", [128, 8, 3072])
    w3_dt = din("w3_dt", [128, 8, 64])
    w3_o = din("w3_o", [128, 16, 1024])
    ssd_cw = din("ssd_cw", [128, 3, 24])
    ssd_cb = din("ssd_cb", [128, 24])
    ssd_vec = din("ssd_vec", [5, 32])
    ssd_gn = din("ssd_gn", [2048])
    s3f = din("s3f", [128, 2048])
    s3b = din("s3b", [128, 2048])

    yT = dout("yT", [128, 8, T])
    o_sf = dout("o_sf", [2, 128, 2048])
    o_sb = dout("o_sb", [2, 128, 2048])
    o_wk = dout("o_wk", [4, 64, 512])
    o_wv = dout("o_wv", [512, 256])
    o_gf = dout("o_gf", [2, 128, 4, 256])
    o_gb = dout("o_gb", [2, 128, 4, 256])
    o_dk = dout("o_dk", [8, 128, 512])
    o_dv = dout("o_dv", [512, 1024])

    X = dscr("X", [128, 8, T], F32)
    U = dscr("U", [128, 44, T], BF16)
    QK = dscr("QK", [128, 16, T], BF16)
    VS = dscr("VS", [T, 1024], BF16)
    OS = dscr("OS", [128, 8, T], BF16)
    OS2 = dscr("OS2", [128, 8, T], BF16)
    RS = dscr("RS", [128, 8, T], BF16)
    QD = dscr("QD", [2, 128, 4, T], BF16)
    KD = dscr("KD", [2, 128, 4, T], BF16)
    KR = dscr("KR", [T, 8, 128], BF16)
    ZS = dscr("ZS", [T, 2048], BF16)
    DTS = dscr("DTS", [T, 128], F32)
    XTs = dscr("XTs", [T, 2048], BF16)
    BTs = dscr("BTs", [T, 512], BF16)
    YD = [dscr("YD0", [T, 2048], BF16), dscr("YD1", [T, 2048], BF16)]

    NT = len(TILES)

    def rows_view(dr, rowlen, r0, nb, c0, w):
        return AP(dr.tensor, r0 * rowlen + c0, [[rowlen, 128], [128 * rowlen, nb], [1, w]])

    def kr_view(r0, nb, j0, nj):
        return AP(KR.tensor, r0 * 1024 + j0 * 128, [[1024, 128], [128 * 1024, nb], [128, nj], [1, 128]])
    import os
    KQ = os.environ.get("KQ", "")

    def tb(name):
        return [Buf(name + str(i)) for i in range(NT)]

    xT_b = tb("xT")
    X_b = tb("X")

    MODS = s.alloc([128, 4, 6, 8, 2], F32, "mods")
    GS = s.alloc([128, 4, 2, 8, 2], F32, "gs")
    NRM = s.alloc([128, 4, 2, 8], F32, "nrm")
    FNRM = s.alloc([128, 8], F32, "fnrm")
    ones_bf = s.alloc([128, 128], BF16, "ones")
    ones_f = s.alloc([128, 128], F32, "onesf")
    ident_bf = s.alloc([128, 128], BF16, "ident")
    perm_bf = s.alloc([128, 128], BF16, "perm")
    tri_bf = s.alloc([128, 4, 128], BF16, "tri")
    m64 = s.alloc([128, 512], F32, "m64")
    cb_const = Buf("consts")
    stage = s.alloc([128, 4, 128], F32, "stage")
    bstage = Buf()

    s.dma(NRM[:], nrm, writes=[cb_const])
    s.dma(FNRM[:], fnorm, writes=[cb_const])
    s.dma(m64[:], m64_in, writes=[cb_const])
    s.op("pool", ins("memset", ones_f[:], 1.0), writes=[cb_const])
    s.op("dve", ins("tensor_copy", out=ones_bf[:], in_=ones_f[:]), reads=[cb_const], writes=[cb_const])
    s.dma(stage[:, 0, :], perm_in, writes=[bstage])
    s.op("dve", ins("tensor_copy", out=perm_bf[:], in_=stage[:, 0, :]), reads=[bstage], writes=[cb_const])
    s.op("pool", ins("memset", stage[:, 1, :], 1.0), reads=[], writes=[bstage])
    s.op("pool", ins("affine_select", out=stage[:, 1, :], in_=stage[:, 1, :], pattern=[[-1, 128]], compare_op=ALU.is_equal, fill=0.0, base=0, channel_multiplier=1), reads=[bstage], writes=[bstage])
    s.op("dve", ins("tensor_copy", out=ident_bf[:], in_=stage[:, 1, :]), reads=[bstage], writes=[cb_const])
    s.barrier()
    s.dma(stage[:], tri_in, writes=[bstage])
    s.op("dve", ins("tensor_copy", out=tri_bf[:], in_=stage[:]), reads=[bstage], writes=[cb_const])
    s.mark()

    def mod_ap(l, k, c, cond):
        return MODS[:, l, k, c, cond:cond + 1]

    def prologue():
        s.barrier()
        s.release()
        sc = s.alloc([128, 8, 2], F32, "sc")
        scb = Buf()
        adb = s.alloc([128, 4, 48], F32, "adb")
        adbb = Buf()
        s.dma(sc[:], condT, writes=[scb])
        s.dma(adb[:], ada_b, writes=[adbb])
        s.op("act", ins("activation", out=sc[:], in_=sc[:], func=AF.Silu), reads=[scb], writes=[scb])
        wb = [s.alloc([128, 8, 1024], F32, "adaw%d" % i) for i in range(2)]
        wbb = [[Buf() for _ in range(2)] for _ in range(2)]
        it = 0
        for l in range(NLAYERS):
            for g in range(6):
                w = wb[it % 2]
                bb = wbb[it % 2]
                for hh in range(2):
                    s.dma(w[:, hh * 4:(hh + 1) * 4, :], ada_w[l, :, hh * 4:(hh + 1) * 4, g * 1024:(g + 1) * 1024], writes=[bb[hh]], eng=("sp" if hh == 0 else "act"))
                ps, pb = s.ps()
                for cc in range(8):
                    for kc in range(8):
                        s.mm(ps[:, cc * 2:cc * 2 + 2], w[:, kc, cc * 128:(cc + 1) * 128], sc[:, kc, :], kc == 0, kc == 7, reads=[bb[kc // 4], scb], writes=[pb])
                s.op("dve", ins("tensor_tensor",
                    out=MODS[:, l, g, :, :], in0=ps[:, 0:16].rearrange("p (c t) -> p c t", t=2),
                    in1=adb[:, l, g * 8:(g + 1) * 8].unsqueeze(2).to_broadcast([128, 8, 2]), op=ALU.add),
                    reads=[pb, adbb], writes=[cb_const])
                it += 1
            for which in range(2):
                k = 1 if which == 0 else 4
                s.op("dve", ins("scalar_tensor_tensor",
                    out=GS[:, l, which, :, :], in0=MODS[:, l, k, :, :], scalar=1.0,
                    in1=NRM[:, l, which, :].unsqueeze(2).to_broadcast([128, 8, 2]), op0=ALU.add, op1=ALU.mult),
                    reads=[cb_const], writes=[cb_const])

    def load_w(dst, src, nk, split=1):
        bufs = []
        for kc in range(nk):
            b = Buf()
            s.dma(dst[:, kc, :], src[:, kc, :], writes=[b], eng="pool")
            bufs.append(b)
        return bufs

    class NormCtx:
        def __init__(self, nbuf=2):
            self.nbuf = nbuf
            self.xt = [(s.alloc([128, 8, 512], F32, "nxt"), Buf()) for _ in range(nbuf)]
            self.h = [(s.alloc([128, 8, 512], BF16, "nh"), Buf()) for _ in range(nbuf)]
            self.sq = (s.alloc([128, 8, 512], BF16, "nsq"), Buf())
            self.tmp = (s.alloc([128, 8, 512], F32, "ntmp"), Buf())
            self.rstd = [(s.alloc([128, 512], F32, "nrs"), Buf()) for _ in range(2)]
            self.k = 0

    def norm_tile(ctx, l, which, ti, Xsrc, Xsrc_b, final_out=None):
        t0, n, cond, _, _ = TILES[ti]
        k = ctx.k
        ctx.k += 1
        xt, xtb = ctx.xt[k % ctx.nbuf]
        h, hb = ctx.h[k % ctx.nbuf]
        sq, sqb = ctx.sq
        tmp, tmpb = ctx.tmp
        rstd, rb = ctx.rstd[k % 2]
        s.dma(xt[:, :, :n], Xsrc[:, :, t0:t0 + n], reads=[Xsrc_b[ti]], writes=[xtb])
        s.op("act", ins("activation", out=sq[:, :, :n], in_=xt[:, :, :n], func=AF.Square), reads=[xtb], writes=[sqb])
        ps, pb = s.ps()
        for c in range(8):
            s.mm(ps[:, :n], ones_bf[:], sq[:, c, :n], c == 0, c == 7, reads=[sqb, cb_const], writes=[pb])
        s.op("act", ins("activation", out=rstd[:, :n], in_=ps[:, :n], func=AF.Sqrt, scale=1.0 / 1024, bias=EPSB[:, 0:1]), reads=[pb], writes=[rb])
        s.op("dve", ins("reciprocal", out=rstd[:, :n], in_=rstd[:, :n]), reads=[rb], writes=[rb])
        s.op("dve", ins("tensor_tensor", out=tmp[:, :, :n], in0=xt[:, :, :n], in1=rstd[:, :n].unsqueeze(1).to_broadcast([128, 8, n]), op=ALU.mult),
             reads=[xtb, rb], writes=[tmpb])
        if final_out is None:
            for c in range(8):
                s.op("act", ins("activation", out=h[:, c, :n], in_=tmp[:, c, :n], func=AF.Identity,
                                                          scale=GS[:, l, which, c, cond:cond + 1], bias=mod_ap(l, 0 if which == 0 else 3, c, cond)),
                     reads=[tmpb, cb_const], writes=[hb])
            return h, hb
        else:
            for c in range(8):
                s.op("act", ins("activation", out=xt[:, c, :n], in_=tmp[:, c, :n], func=AF.Identity, scale=FNRM[:, c:c + 1]),
                     reads=[tmpb, cb_const], writes=[xtb])
            s.dma(final_out[:, :, t0:t0 + n], xt[:, :, :n], reads=[xtb])
            return None, None

    EPSB = s.alloc([128, 1], F32, "epsb")
    s.op("pool", ins("memset", EPSB[:], EPS), writes=[cb_const])
    s.mark()

    class ResCtx:
        def __init__(self):
            self.xo = [(s.alloc([128, 8, 512], F32, "xo"), Buf()) for _ in range(2)]
            self.k = 0

    def outproj_resid(rctx, l, gate_k, ti, W, wbufs, nk, rhs, rhsbufs, Xsrc, Xsrc_b, Xdst, Xdst_b):
        t0, n, cond, _, _ = TILES[ti]
        xo, xob = rctx.xo[rctx.k % 2]
        rctx.k += 1
        s.dma(xo[:, :, :n], Xsrc[:, :, t0:t0 + n], reads=[Xsrc_b[ti]], writes=[xob])
        for dc in range(8):
            ps, pb = s.ps()
            for kc in range(nk):
                s.mm(ps[:, :n], W[:, kc, dc * 128:(dc + 1) * 128], rhs[:, kc, :n], kc == 0, kc == nk - 1,
                     reads=[wbufs[kc]] + list(rhsbufs), writes=[pb])
            s.op("dve", ins("scalar_tensor_tensor", out=xo[:, dc, :n], in0=ps[:, :n], scalar=mod_ap(l, gate_k, dc, cond),
                                                                      in1=xo[:, dc, :n], op0=ALU.mult, op1=ALU.add),
                 reads=[pb, xob, cb_const], writes=[xob])
        s.dma(Xdst[:, :, t0:t0 + n], xo[:, :, :n], reads=[xob], writes=[Xdst_b[ti]])

    def ffn(l, Xsrc, Xsrc_b):
        s.barrier()
        s.release()
        Wup = s.alloc([128, 8, 5632], BF16, "wup")
        wb = load_w(Wup, w_up[l], 8)
        nctx = NormCtx()
        ub = [(s.alloc([128, 11, 512], BF16, "ub"), [Buf() for _ in range(11)]) for _ in range(2)]
        U_b = [[Buf() for _ in range(4)] for _ in range(NT)]
        ku = 0
        ev = 0
        for ti in range(NT):
            t0, n, cond, _, _ = TILES[ti]
            h, hb = norm_tile(nctx, l, 1, ti, Xsrc, Xsrc_b)
            for grp in range(4):
                ut, utb = ub[ku % 2]
                ku += 1
                for cc in range(11):
                    col = grp * 11 + cc
                    ps, pb = s.ps()
                    for kc in range(8):
                        s.mm(ps[:, :n], Wup[:, kc, col * 128:(col + 1) * 128], h[:, kc, :n], kc == 0, kc == 7, reads=[wb[kc], hb], writes=[pb])
                    if ev % 2 == 0:
                        s.op("act", ins("activation", out=ut[:, cc, :n], in_=ps[:, :n], func=AF.Identity), reads=[pb], writes=[utb[cc]])
                    else:
                        s.op("dve", ins("tensor_copy", out=ut[:, cc, :n], in_=ps[:, :n]), reads=[pb], writes=[utb[cc]])
                    ev += 1
                s.dma(U[:, grp * 11:(grp + 1) * 11, t0:t0 + n], ut[:, :, :n], reads=utb, writes=[U_b[ti][grp]])
        s.barrier()
        s.release()
        Wd = s.alloc([128, 22, 1024], BF16, "wd")
        wdb = load_w(Wd, w_dn[l], 22)
        cw = s.alloc([128, 3, 44], F32, "cw")
        cbias = s.alloc([128, 44], F32, "cbias")
        cwb = Buf()
        s.dma(cw[:], ffn_cw[:, l, :, :], writes=[cwb])
        s.dma(cbias[:], ffn_cb[:, l, :], writes=[cwb])
        ug = s.alloc([128, 44, 514], BF16, "ug")
        ugb = [Buf() for _ in range(4)]
        at = [(s.alloc([128, 22, 512], BF16, "at"), Buf()) for _ in range(2)]
        tmps = [[(s.alloc([128, 512], F32, "ft"), Buf()) for _ in range(3)] for _ in range(2)]
        rctx = ResCtx()
        kp = 0
        for ti in range(NT):
            t0, n, cond, s0, s1 = TILES[ti]
            lo = (t0 - 1) >= s0
            hi = (t0 + n) < s1
            for grp in range(4):
                gs_ = slice(grp * 11, (grp + 1) * 11)
                if not lo:
                    s.op("pool", ins("memset", ug[:, gs_, 0:1], 0.0), writes=[ugb[grp]])
                if not hi:
                    s.op("pool", ins("memset", ug[:, gs_, n + 1:n + 2], 0.0), writes=[ugb[grp]])
                c0 = 0 if lo else 1
                c1 = n + 2 if hi else n + 1
                rd = [U_b[ti][grp]]
                if lo:
                    rd.append(U_b[ti - 1][grp])
                if hi:
                    rd.append(U_b[ti + 1][grp])
                s.dma(ug[:, gs_, c0:c1], U[:, gs_, t0 - 1 + c0:t0 - 1 + c1], reads=rd, writes=[ugb[grp]])
            a, ab = at[ti % 2]
            for cc in range(22):
                res = []
                for half in range(2):
                    col = cc + 22 * half
                    tt, ttb = tmps[kp % 2][half]
                    grp = col // 11
                    s.op("act", ins("activation", out=tt[:, :n], in_=ug[:, col, 1:n + 1], func=AF.Identity,
                                                                      scale=cw[:, 1, col:col + 1], bias=cbias[:, col:col + 1]),
                         reads=[ugb[grp], cwb], writes=[ttb])
                    s.op("dve", ins("scalar_tensor_tensor", out=tt[:, :n], in0=ug[:, col, 0:n], scalar=cw[:, 0, col:col + 1],
                                                                               in1=tt[:, :n], op0=ALU.mult, op1=ALU.add),
                         reads=[ugb[grp], cwb, ttb], writes=[ttb])
                    s.op("dve", ins("scalar_tensor_tensor", out=tt[:, :n], in0=ug[:, col, 2:n + 2], scalar=cw[:, 2, col:col + 1],
                                                                               in1=tt[:, :n], op0=ALU.mult, op1=ALU.add),
                         reads=[ugb[grp], cwb, ttb], writes=[ttb])
                    res.append((tt, ttb))
                sg, sgb = tmps[kp % 2][2]
                kp += 1
                s.op("act", ins("activation", out=sg[:, :n], in_=res[0][0][:, :n], func=AF.Silu), reads=[res[0][1]], writes=[sgb])
                s.op("dve", ins("tensor_tensor", out=a[:, cc, :n], in0=sg[:, :n], in1=res[1][0][:, :n], op=ALU.mult),
                     reads=[sgb, res[1][1]], writes=[ab])
            outproj_resid(rctx, l, 5, ti, Wd, wdb, 22, a, [ab], Xsrc, Xsrc_b, X, X_b)

    def qkv_phase(l, Wqk_src, nqk, Wv_src, nv, Xsrc, Xsrc_b, rope, kout=None, vout=None, post=None):
        s.barrier()
        s.release()
        Wqk = s.alloc([128, 8, nqk * 128], BF16, "wqk")
        wqb = load_w(Wqk, Wqk_src, 8)
        Wv = s.alloc([128, 8, nv], BF16, "wv")
        wvb = load_w(Wv, Wv_src, 8)
        nctx = NormCtx()
        qk = [(s.alloc([128, nqk, 512], BF16, "qk"), [Buf() for _ in range(nqk)]) for _ in range(2)]
        vt = [(s.alloc([128, 4, nv], BF16, "vt"), Buf()) for _ in range(2)]
        qb = [(s.alloc([128, 512], BF16, "qb"), Buf()) for _ in range(2)]
        t12 = [[(s.alloc([128, 512], F32, "rt"), Buf()) for _ in range(2)] for _ in range(2)]
        cs = [(s.alloc([128, 2, 512], F32, "cs"), Buf()) for _ in range(2)]
        kf = [(s.alloc([128, 256], F32, "kf"), Buf()) for _ in range(2)]
        vf = [(s.alloc([128, 512], F32, "vf"), Buf()) for _ in range(2)]
        QK_b = [Buf() for _ in range(NT)]
        VS_b = [Buf() for _ in range(NT)]
        if "L" in KQ:
            return
        kq = 0
        kk = 0
        kv = 0
        for ti in range(NT):
            t0, n, cond, s0, s1 = TILES[ti]
            h, hb = norm_tile(nctx, l, 0, ti, Xsrc, Xsrc_b)
            if "N" in KQ:
                continue
            qkt, qkb = qk[ti % 2]
            dorope = rope and cond == 0 and ("r" not in KQ)
            if dorope:
                cst, csb = cs[ti % 2]
                s.dma(cst[:, 0, :], rcos[:, t0:t0 + n], writes=[csb])
                s.dma(cst[:, 1, :], rsin[:, t0:t0 + n], writes=[csb])
            for cc in range(nqk):
                ps, pb = s.ps()
                for kc in range(8):
                    s.mm(ps[:, :n], Wqk[:, kc, cc * 128:(cc + 1) * 128], h[:, kc, :n], kc == 0, kc == 7, reads=[wqb[kc], hb], writes=[pb])
                if dorope:
                    q_, q_b = qb[kq % 2]
                    t1, t1b = t12[kq % 2][0]
                    t2, t2b = t12[kq % 2][1]
                    kq += 1
                    s.op("act", ins("activation", out=q_[:, :n], in_=ps[:, :n], func=AF.Identity), reads=[pb], writes=[q_b])
                    ps2, pb2 = s.ps()
                    s.mm(ps2[:, :n], perm_bf[:], q_[:, :n], True, True, reads=[q_b, cb_const], writes=[pb2])
                    s.op("dve", ins("tensor_tensor", out=t1[:, :n], in0=q_[:, :n], in1=cst[:, 0, :n], op=ALU.mult),
                         reads=[q_b, csb], writes=[t1b])
                    s.op("dve", ins("tensor_tensor", out=t2[:, :n], in0=ps2[:, :n], in1=cst[:, 1, :n], op=ALU.mult),
                         reads=[pb2, csb], writes=[t2b])
                    s.op("pool", ins("tensor_tensor", out=qkt[:, cc, :n], in0=t1[:, :n], in1=t2[:, :n], op=ALU.add),
                         reads=[t1b, t2b], writes=[qkb[cc]])
                else:
                    if kout is not None and cond == 1 and cc >= kout[0]:
                        kft, kfb = kf[kk % 2]
                        kk += 1
                        rows = kout[2]
                        s.op("dve", ins("tensor_copy", out=kft[:, :n], in_=ps[:, :n]), reads=[pb], writes=[kfb])
                        s.op("act", ins("activation", out=qkt[:, cc, :n], in_=kft[:, :n], func=AF.Identity), reads=[kfb], writes=[qkb[cc]])
                        s.dma(kout[1][cc - kout[0], :, t0 - LS:t0 - LS + n], kft[0:rows, :n], reads=[kfb])
                    else:
                        s.op("act", ins("activation", out=qkt[:, cc, :n], in_=ps[:, :n], func=AF.Identity), reads=[pb], writes=[qkb[cc]])
                if post is not None:
                    post(ti, cc, ps, pb)
            vtt, vtb = vt[ti % 2]
            for tbk in range(n // 128 if "v" not in KQ else 0):
                for vg in range((nv + 511) // 512):
                    w_ = min(512, nv - vg * 512)
                    ps, pb = s.ps()
                    for kc in range(8):
                        s.mm(ps[:, :w_], h[:, kc, tbk * 128:(tbk + 1) * 128], Wv[:, kc, vg * 512:vg * 512 + w_], kc == 0, kc == 7, reads=[wvb[kc], hb], writes=[pb])
                    if vout is not None and cond == 1:
                        vft, vfb = vf[kv % 2]
                        kv += 1
                        s.op("dve", ins("tensor_copy", out=vft[:, :w_], in_=ps[:, :w_]), reads=[pb], writes=[vfb])
                        s.op("act", ins("activation", out=vtt[:, tbk, vg * 512:vg * 512 + w_], in_=vft[:, :w_], func=AF.Identity),
                             reads=[vfb], writes=[vtb])
                        r0 = t0 - LS + tbk * 128
                        s.dma(vout[r0:r0 + 128, vg * 512:vg * 512 + w_], vft[:, :w_], reads=[vfb])
                    else:
                        s.op("act", ins("activation", out=vtt[:, tbk, vg * 512:vg * 512 + w_], in_=ps[:, :w_], func=AF.Identity),
                             reads=[pb], writes=[vtb])
            if "q" not in KQ:
                s.dma(QK[:, 0:nqk, t0:t0 + n], qkt[:, :, :n], reads=qkb, writes=[QK_b[ti]])
            if "v" not in KQ and "s" not in KQ:
              s.dma(rows_view(VS, 1024, t0, n // 128, 0, nv), vtt[:, 0:n // 128, :], reads=[vtb], writes=[VS_b[ti]])

    def attn_phase(l, kind, Xsrc, Xsrc_b):
        s.barrier()
        s.release()
        win = kind == "win"
        nunits = 8
        kbase = 8
        Kt = [(s.alloc([128, T], BF16, "kt"), Buf()) for _ in range(2)]
        Va = [(s.alloc([128, 36, 128], BF16, "va"), Buf()) for _ in range(2)]
        Qt = [(s.alloc([128, LS], BF16, "qt"), Buf()) for _ in range(2)]
        Ot = [(s.alloc([128, LS], BF16, "ot"), Buf()) for _ in range(2)]
        pT = [(s.alloc([128, 512], BF16, "pT"), Buf()) for _ in range(6)]
        rec = [(s.alloc([128, 512], F32, "rec"), Buf()) for _ in range(4)]
        OS_b = [[Buf() for _ in range(nunits)] for _ in range(3)]
        cbuf = Buf()
        if win:
            masks = s.alloc([128, 6, 512], BF16, "masks")
            s.dma(masks[:], wmask_in, writes=[cbuf], eng="pool")
            esink = s.alloc([128, 16], F32, "esink")
            s.dma(esink[:], sink_in.partition_broadcast(128), writes=[cbuf])
            s.op("act", ins("activation", out=esink[:], in_=esink[:], func=AF.Exp), reads=[cbuf], writes=[cbuf])
            for i in range(2):
                s.op("pool", ins("memset", Va[i][0][:, :, 64:128], 1.0), writes=[Va[i][1]])
        else:
            acc = [(s.alloc([128, 512], F32, "acc"), Buf()) for _ in range(4)]
            osb = [(s.alloc([128, 512], F32, "osb"), Buf()) for _ in range(4)]
            sqd = [(s.alloc([128, 512], BF16, "sqd"), Buf()) for _ in range(2)]
            lq = s.alloc([128, 4, 64], F32, "lq")
            lam = s.alloc([128, 4], F32, "lam")
            gsub = s.alloc([128, 1], F32, "gsub")
            s.dma(lq[:], lqk.rearrange("a b -> (a b)").partition_broadcast(128).rearrange("p (a b) -> p a b", a=4), writes=[cbuf])
            s.dma(gsub[:], w2_gn, writes=[cbuf])
            s.op("dve", ins("tensor_tensor", out=lq[:, 0, :], in0=lq[:, 0, :], in1=lq[:, 1, :], op=ALU.mult), reads=[cbuf], writes=[cbuf])
            s.op("dve", ins("tensor_tensor", out=lq[:, 2, :], in0=lq[:, 2, :], in1=lq[:, 3, :], op=ALU.mult), reads=[cbuf], writes=[cbuf])
            s.op("dve", ins("reduce_sum", out=lam[:, 0:1], in_=lq[:, 0, :], axis=AX.X), reads=[cbuf], writes=[cbuf])
            s.op("dve", ins("reduce_sum", out=lam[:, 1:2], in_=lq[:, 2, :], axis=AX.X), reads=[cbuf], writes=[cbuf])
            s.op("act", ins("activation", out=lam[:, 0:2], in_=lam[:, 0:2], func=AF.Exp), reads=[cbuf], writes=[cbuf])
            s.op("dve", ins("tensor_tensor", out=lam[:, 2:3], in0=lam[:, 1:2], in1=lam[:, 0:1], op=ALU.subtract), reads=[cbuf], writes=[cbuf])
            s.op("dve", ins("tensor_scalar", out=lam[:, 2:3], in0=lam[:, 2:3], scalar1=-LAM_INIT, scalar2=None, op0=ALU.add), reads=[cbuf], writes=[cbuf])
            s.op("dve", ins("tensor_scalar", out=lam[:, 3:4], in0=gsub[:], scalar1=1.0 - LAM_INIT, scalar2=None, op0=ALU.mult), reads=[cbuf], writes=[cbuf])
        ku = 0
        kp = 0
        kr = 0
        ka = 0
        SEQS = [(0, 4096, 0), (4096, 256, 1), (4352, 256, 2)]
        for (q0, L, si) in SEQS:
            sample = si == 0
            nkb_lat = L // 128
            for u in range(nunits):
                kt, ktb = Kt[ku % 2]
                va, vab = Va[ku % 2]
                qt, qtb = Qt[ku % 2]
                ot, otb = Ot[ku % 2]
                ku += 1
                s.dma(qt[:, 0:L], QK[:, u, q0:q0 + L], writes=[qtb])
                if win:
                    g = u // 2
                    s.dma(kt[:, 0:L], QK[:, kbase + g, q0:q0 + L], writes=[ktb])
                    s.dma(va[:, 0:nkb_lat, 0:64], rows_view(VS, 1024, q0, nkb_lat, g * 64, 64), writes=[vab])
                    if sample:
                        s.dma(kt[:, L:L + 512], ck0[:, g, :], writes=[ktb], eng="pool")
                        s.dma(va[:, 32:36, 0:64], rows_view(cv0, 256, 0, 4, g * 64, 64), writes=[vab], eng="pool")
                else:
                    s.dma(kt[:, 0:L], QK[:, kbase + u, q0:q0 + L], writes=[ktb])
                    s.dma(va[:, 0:nkb_lat, :], rows_view(VS, 1024, q0, nkb_lat, u * 128, 128), writes=[vab])
                    if sample:
                        s.dma(kt[:, L:L + 512], ck2[:, u, :], writes=[ktb], eng="pool")
                        s.dma(va[:, 32:36, :], rows_view(cv2, 1024, 0, 4, u * 128, 128), writes=[vab], eng="pool")
                nq = 512 if sample else 256
                for qi in range(L // nq):
                    qs = slice(qi * nq, (qi + 1) * nq)
                    accs = []
                    for e_ in range(2):
                        pr = slice(e_ * 64, (e_ + 1) * 64)
                        if win and sample:
                            kbs = [(kb, kb - qi * 4 + 1) for kb in range(qi * 4 - 1, qi * 4 + 5) if 0 <= kb < 32] + [(32 + j, None) for j in range(4)]
                        elif sample:
                            kbs = [(kb, None) for kb in range(36)]
                        else:
                            kbs = [(kb, None) for kb in range(2)]
                        pso, pob = s.ps("b")
                        if not win:
                            ac, acb = acc[ka % 4]
                            ka += 1
                        for i, (kb, mi) in enumerate(kbs):
                            pss, psb_ = s.ps()
                            s.mm(pss[:, :nq], kt[pr, kb * 128:(kb + 1) * 128], qt[pr, qs], True, True, reads=[ktb, qtb], writes=[psb_])
                            p_, p_b = pT[kp % 6]
                            kp += 1
                            s.op("act", ins("activation", out=p_[:, :nq], in_=pss[:, :nq], func=AF.Exp, scale=0.125), reads=[psb_], writes=[p_b])
                            if mi is not None:
                                s.op("pool" if (kp % 2) else "dve", ins("tensor_tensor", out=p_[:, :nq], in0=p_[:, :nq], in1=masks[:, mi, :nq], op=ALU.mult),
                                     reads=[p_b, cbuf], writes=[p_b])
                            if not win:
                                eng = "pool" if e_ == 0 else "dve"
                                if i == 0:
                                    s.op(eng, ins("tensor_copy", out=ac[:, :nq], in_=p_[:, :nq]), reads=[p_b], writes=[acb])
                                else:
                                    s.op(eng, ins("tensor_tensor", out=ac[:, :nq], in0=ac[:, :nq], in1=p_[:, :nq], op=ALU.add), reads=[p_b, acb], writes=[acb])
                            s.mm(pso[:, :nq], va[:, kb, :], p_[:, :nq], i == 0, i == len(kbs) - 1, reads=[vab, p_b], writes=[pob])
                        if win:
                            hh = u * 2 + e_
                            r_, r_b = rec[kr % 4]
                            kr += 1
                            s.op("dve", ins("tensor_scalar", out=r_[64:128, :nq], in0=pso[64:128, :nq], scalar1=esink[64:128, hh:hh + 1], scalar2=None, op0=ALU.add),
                                 reads=[pob, cbuf], writes=[r_b])
                            s.op("dve", ins("reciprocal", out=r_[64:128, :nq], in_=r_[64:128, :nq]), reads=[r_b], writes=[r_b])
                            s.op("dve", ins("tensor_tensor", out=ot[pr, qs], in0=pso[0:64, :nq], in1=r_[64:128, :nq], op=ALU.mult),
                                 reads=[pob, r_b], writes=[otb])
                        else:
                            psd, pdb = s.ps()
                            s.mm(psd[:, :nq], ones_f[:], ac[:, :nq], True, True, reads=[acb, cb_const], writes=[pdb])
                            r_, r_b = rec[kr % 4]
                            kr += 1
                            s.op("dve", ins("reciprocal", out=r_[:, :nq], in_=psd[:, :nq]), reads=[pdb], writes=[r_b])
                            o_, o_b = osb[kr % 4]
                            s.op("dve", ins("tensor_tensor", out=o_[:, :nq], in0=pso[:, :nq], in1=r_[:, :nq], op=ALU.mult), reads=[pob, r_b], writes=[o_b])
                            accs.append((o_, o_b))
                    if not win:
                        (o0, o0b), (o1, o1b) = accs
                        s.op("dve", ins("scalar_tensor_tensor", out=o0[:, :nq], in0=o1[:, :nq], scalar=lam[:, 2:3], in1=o0[:, :nq], op0=ALU.mult, op1=ALU.add),
                             reads=[o0b, o1b, cbuf], writes=[o0b])
                        sq_, sq_b = sqd[qi % 2]
                        s.op("act", ins("activation", out=sq_[:, :nq], in_=o0[:, :nq], func=AF.Square), reads=[o0b], writes=[sq_b])
                        psn, pnb = s.ps()
                        s.mm(psn[:, :nq], ones_bf[:], sq_[:, :nq], True, True, reads=[sq_b, cb_const], writes=[pnb])
                        s.op("act", ins("activation", out=o1[:, :nq], in_=psn[:, :nq], func=AF.Sqrt, scale=1.0 / 128, bias=EPSB[:, 0:1]), reads=[pnb, o1b], writes=[o1b])
                        s.op("dve", ins("reciprocal", out=o1[:, :nq], in_=o1[:, :nq]), reads=[o1b], writes=[o1b])
                        s.op("dve", ins("scalar_tensor_tensor", out=ot[:, qs], in0=o0[:, :nq], scalar=lam[:, 3:4], in1=o1[:, :nq], op0=ALU.mult, op1=ALU.mult),
                             reads=[o0b, o1b, cbuf], writes=[otb])
                s.dma(OS[:, u, q0:q0 + L], ot[:, 0:L], reads=[otb], writes=[OS_b[si][u]])
        return OS_b

    def outproj_phase(l, Wsrc, nk, Osrc, O_bufs_fn, Xsrc, Xsrc_b):
        s.barrier()
        s.release()
        Wo = s.alloc([128, nk, 1024], BF16, "wo")
        wob = load_w(Wo, Wsrc, nk)
        rctx = ResCtx()
        oin = [(s.alloc([128, nk, 512], BF16, "oin"), Buf()) for _ in range(2)]
        for ti in range(NT):
            t0, n, cond, _, _ = TILES[ti]
            o_, o_b = oin[ti % 2]
            s.dma(o_[:, :, :n], Osrc[:, :, t0:t0 + n], writes=[o_b])
            outproj_resid(rctx, l, 2, ti, Wo, wob, nk, o_, [o_b], Xsrc, Xsrc_b, X, X_b)

    def gla_layer(l, Xsrc, Xsrc_b):
        s.barrier()
        s.release()
        Wqk = s.alloc([128, 8, 1024], BF16, "gwqk")
        wqb = load_w(Wqk, w1_qk, 8)
        Wv = s.alloc([128, 8, 1024], BF16, "gwv")
        wvb = load_w(Wv, w1_v, 8)
        Wr = s.alloc([128, 8, 1024], BF16, "gwr")
        wrb = load_w(Wr, w1_r, 8)
        Wg1 = s.alloc([128, 8, 32], BF16, "gwg1")
        wg1b = load_w(Wg1, w1_g1, 8)
        Wg2 = s.alloc([32, 1024], BF16, "gwg2")
        cbuf = Buf()
        s.dma(Wg2[:], w1_g2, writes=[cbuf], eng="pool")
        gb = s.alloc([128, 8], F32, "ggb")
        s.dma(gb[:], w1_gb, writes=[cbuf])
        s.op("dve", ins("tensor_scalar", out=gb[:], in0=gb[:], scalar1=-1.0, scalar2=None, op0=ALU.mult), reads=[cbuf], writes=[cbuf])
        ELt = s.alloc([128, 2, 4, 72], F32, "el")
        nctx = NormCtx(1)
        qkf = [(s.alloc([128, 8, 512], F32, "qkf"), Buf()) for _ in range(1)]
        lg = nctx.xt[0]
        cs_ = (s.alloc([128, 8, 512], F32, "cs"), Buf())
        c2 = nctx.tmp
        ex = [(s.alloc([128, 512], F32, "ex"), Buf()) for _ in range(3)]
        t1b_ = (s.alloc([32, 512], BF16, "t1b"), Buf())
        qd_t = [(s.alloc([128, 2, 4, 512], BF16, "qd"), Buf()) for _ in range(1)]
        kd_t = [(s.alloc([128, 2, 4, 512], BF16, "kd"), Buf()) for _ in range(1)]
        krT = [(s.alloc([128, 512], BF16, "krT"), Buf()) for _ in range(2)]
        krt = [(s.alloc([128, 4, 8, 128], BF16, "krt"), Buf()) for _ in range(1)]
        rt = [(s.alloc([128, 8, 512], BF16, "rt"), Buf()) for _ in range(1)]
        vt = [(s.alloc([128, 4, 1024], BF16, "vt"), Buf()) for _ in range(1)]
        G_b = [Buf() for _ in range(NT)]
        kx = 0
        kk = 0
        for ti in range(NT):
            t0, n, cond, s0, s1 = TILES[ti]
            nch = n // 64
            h, hb = norm_tile(nctx, l, 0, ti, Xsrc, Xsrc_b)
            qf, qfb = qkf[0]
            for cc in range(8):
                ps, pb = s.ps()
                for kc in range(8):
                    s.mm(ps[:, :n], Wqk[:, kc, cc * 128:(cc + 1) * 128], h[:, kc, :n], kc == 0, kc == 7, reads=[wqb[kc], hb], writes=[pb])
                sc_ = (128.0 ** -0.5) if cc < 4 else 1.0
                s.op("act", ins("activation", out=qf[:, cc, :n], in_=ps[:, :n], func=AF.Identity, scale=sc_), reads=[pb], writes=[qfb])
            r_, r_b = rt[0]
            for cc in range(8):
                ps, pb = s.ps()
                for kc in range(8):
                    s.mm(ps[:, :n], Wr[:, kc, cc * 128:(cc + 1) * 128], h[:, kc, :n], kc == 0, kc == 7, reads=[wrb[kc], hb], writes=[pb])
                s.op("act", ins("activation", out=r_[:, cc, :n], in_=ps[:, :n], func=AF.Silu), reads=[pb], writes=[r_b])
            s.dma(RS[:, :, t0:t0 + n], r_[:, :, :n], reads=[r_b], writes=[G_b[ti]])
            v_, v_b = vt[0]
            for tbk in range(n // 128):
                for vg in range(2):
                    ps, pb = s.ps()
                    for kc in range(8):
                        s.mm(ps[:, :512], h[:, kc, tbk * 128:(tbk + 1) * 128], Wv[:, kc, vg * 512:(vg + 1) * 512], kc == 0, kc == 7, reads=[wvb[kc], hb], writes=[pb])
                    s.op("act", ins("activation", out=v_[:, tbk, vg * 512:(vg + 1) * 512], in_=ps[:, :512], func=AF.Identity), reads=[pb], writes=[v_b])
            s.dma(rows_view(VS, 1024, t0, n // 128, 0, 1024), v_[:, 0:n // 128, :], reads=[v_b], writes=[G_b[ti]])
            ps, pb = s.ps()
            for kc in range(8):
                s.mm(ps[0:32, :n], Wg1[:, kc, :], h[:, kc, :n], kc == 0, kc == 7, reads=[wg1b[kc], hb], writes=[pb])
            t1_, t1bb = t1b_
            s.op("act", ins("activation", out=t1_[:, :n], in_=ps[0:32, :n], func=AF.Identity), reads=[pb], writes=[t1bb])
            lgt, lgb = lg
            for j in range(8):
                ps, pb = s.ps()
                s.mm(ps[:, :n], Wg2[:, j * 128:(j + 1) * 128], t1_[:, :n], True, True, reads=[cbuf, t1bb], writes=[pb])
                s.op("act", ins("activation", out=lgt[:, j, :n], in_=ps[:, :n], func=AF.Exp, scale=-1.0, bias=gb[:, j:j + 1]), reads=[pb, cbuf], writes=[lgb])
            s.op("act", ins("activation", out=lgt[:, :, :n], in_=lgt[:, :, :n], func=AF.Ln, bias=1.0, scale=1.0), reads=[lgb], writes=[lgb])
            cst, csb = cs_
            for j in range(8):
                s.op("dve", ins("tensor_tensor_scan", out=cst[:, j, :n], data0=m64[:, :n], data1=lgt[:, j, :n], initial=0.0, op0=ALU.mult, op1=ALU.add),
                     reads=[lgb, cb_const], writes=[csb])
            c2t, c2b = c2
            csv = cst[:, :, :n].rearrange("p j (c t) -> p j c t", t=64)
            lgv = lgt[:, :, :n].rearrange("p j (c t) -> p j c t", t=64)
            c2v = c2t[:, :, :n].rearrange("p j (c t) -> p j c t", t=64)
            nf = 4
            s.op("dve", ins("tensor_tensor", out=c2v[:, 0:nf, :, :], in0=csv[:, 0:nf, :, :], in1=csv[:, 0:nf, :, 63:64].to_broadcast([128, nf, nch, 64]), op=ALU.subtract),
                 reads=[csb], writes=[c2b])
            s.op("dve", ins("tensor_tensor", out=c2v[:, nf:2 * nf, :, :], in0=lgv[:, nf:2 * nf, :, :], in1=csv[:, nf:2 * nf, :, :], op=ALU.subtract),
                 reads=[csb, lgb], writes=[c2b])
            ci0 = t0 // 64
            s.op("act", ins("activation", out=ELt[:, :, :, ci0:ci0 + nch].rearrange("p d h c -> p (d h) c"), in_=cst[:, :, :n].rearrange("p j (c t) -> p j c t", t=64)[:, :, :, 63],
                                               func=AF.Exp, scale=-1.0 / 16), reads=[csb], writes=[cbuf])
            s.op("dve", ins("tensor_tensor", out=lgv[:, nf:2 * nf, :, :], in0=c2v[:, nf:2 * nf, :, :], in1=csv[:, nf:2 * nf, :, 63:64].to_broadcast([128, nf, nch, 64]), op=ALU.add),
                 reads=[csb, c2b, lgb], writes=[lgb])
            qd_, qdb = qd_t[0]
            kd_, kdb = kd_t[0]
            krt_, krtb = krt[0]
            for j in range(8):
                d = j // 4
                hh = j % 4
                ea, eab = ex[0]
                eb, ebb = ex[1]
                ec, ecb = ex[2]
                csrc = cst if j < 4 else lgt
                s.op("act", ins("activation", out=ea[:, :n], in_=csrc[:, j, :n], func=AF.Exp, scale=-1.0 / 16), reads=[csb, lgb], writes=[eab])
                s.op("act", ins("activation", out=eb[:, :n], in_=csrc[:, j, :n], func=AF.Exp, scale=1.0 / 16), reads=[csb, lgb], writes=[ebb])
                s.op("act", ins("activation", out=ec[:, :n], in_=c2t[:, j, :n], func=AF.Exp, scale=1.0 / 16), reads=[c2b], writes=[ecb])
                s.op("dve", ins("tensor_tensor", out=qd_[:, d, hh, :n], in0=qf[:, hh, :n], in1=ea[:, :n], op=ALU.mult), reads=[qfb, eab], writes=[qdb])
                s.op("dve", ins("tensor_tensor", out=kd_[:, d, hh, :n], in0=qf[:, 4 + hh, :n], in1=eb[:, :n], op=ALU.mult), reads=[qfb, ebb], writes=[kdb])
                kT, kTb = krT[kx % 2]
                kx += 1
                s.op("pool", ins("tensor_tensor", out=kT[:, :n], in0=qf[:, 4 + hh, :n], in1=ec[:, :n], op=ALU.mult), reads=[qfb, ecb], writes=[kTb])
                for tbk in range(n // 128):
                    ps, pb = s.ps()
                    psv = ps[:].bitcast(BF16)
                    s.op("pe", ins("transpose", psv[:, 0:128], kT[:, tbk * 128:(tbk + 1) * 128], ident_bf[:]), reads=[kTb, cb_const], writes=[pb])
                    s.op("act", ins("activation", out=krt_[:, tbk, j, :], in_=psv[:, 0:128], func=AF.Identity), reads=[pb], writes=[krtb])
            for d in range(2):
                s.dma(QD[d, :, :, t0:t0 + n], qd_[:, d, :, :n], reads=[qdb], writes=[G_b[ti]])
                s.dma(KD[d, :, :, t0:t0 + n], kd_[:, d, :, :n], reads=[kdb], writes=[G_b[ti]])
            s.dma(kr_view(t0, n // 128, 0, 8), krt_[:, 0:n // 128, :, :], reads=[krtb], writes=[G_b[ti]])
        ELd = dscr("ELd", [128, 2, 4, 72], F32)
        elb = Buf()
        s.dma(ELd, ELt[:], reads=[cbuf], writes=[elb])
        s.barrier()
        s.release()
        EL = s.alloc([128, 2, 4, 72], F32, "el2")
        elb2 = Buf()
        s.dma(EL[:], ELd, writes=[elb2])
        S = [(s.alloc([128, 4, 256], F32, "S"), [Buf() for _ in range(4)]) for _ in range(2)]
        Sb = [(s.alloc([128, 4, 256], BF16, "Sb"), [Buf() for _ in range(4)]) for _ in range(2)]
        qd_t = [[(s.alloc([128, 4, 512], BF16, "qd"), Buf()) for _ in range(2)] for _ in range(2)]
        kd_t = [[(s.alloc([128, 4, 512], BF16, "kd"), Buf()) for _ in range(2)] for _ in range(2)]
        kr_t = [[(s.alloc([128, 4, 4, 128], BF16, "kr"), Buf()) for _ in range(2)] for _ in range(2)]
        v_t = [[(s.alloc([128, 4, 1024], BF16, "v"), Buf()) for _ in range(2)] for _ in range(2)]
        of_t = [[(s.alloc([128, 8, 512], BF16, "of"), Buf()) for _ in range(2)] for _ in range(2)]
        attm = [(s.alloc([128, 64], BF16, "attm"), Buf()) for _ in range(8)]
        OD_b = [[Buf() for _ in range(NT)] for _ in range(2)]
        ODs = [OS, OS2]
        kat = 0
        SEQT = [list(range(8)), [8], [9]]
        states_in = [s1f, s1b]
        states_out = [o_gf, o_gb]
        for si, tiles in enumerate(SEQT):
            for d in range(2):
                St, Stb = S[d]
                Sbt, Sbb = Sb[d]
                if si == 0:
                    s.dma(St[:], states_in[d], writes=Stb)
                else:
                    s.op("pool", ins("memset", St[:], 0.0), writes=Stb)
                for hh in range(4):
                    s.op("act", ins("activation", out=Sbt[:, hh, :], in_=St[:, hh, :], func=AF.Identity), reads=[Stb[hh]], writes=[Sbb[hh]])
            nt_ = len(tiles)
            for step in range(nt_):
                cur = {}
                for d in range(2):
                    ti = tiles[step] if d == 0 else tiles[nt_ - 1 - step]
                    t0, n, cond, s0, s1 = TILES[ti]
                    qd_, qdb = qd_t[d][step % 2]
                    kd_, kdb = kd_t[d][step % 2]
                    kr_, krb = kr_t[d][step % 2]
                    v_, vb_ = v_t[d][step % 2]
                    of_, ofb = of_t[d][step % 2]
                    s.dma(qd_[:, :, :n], QD[d, :, :, t0:t0 + n], writes=[qdb])
                    s.dma(kd_[:, :, :n], KD[d, :, :, t0:t0 + n], writes=[kdb])
                    s.dma(kr_[:, 0:n // 128, :, :], kr_view(t0, n // 128, d * 4, 4), writes=[krb])
                    s.dma(v_[:, 0:n // 128, :], rows_view(VS, 1024, t0, n // 128, 0, 1024), writes=[vb_])
                    cur[d] = (ti, t0, n, qd_, qdb, kd_, kdb, kr_, krb, v_, vb_, of_, ofb)
                nch = cur[0][2] // 64
                for kc_ in range(nch):
                    for d in range(2):
                        ti, t0, n, qd_, qdb, kd_, kdb, kr_, krb, v_, vb_, of_, ofb = cur[d]
                        k = kc_ if d == 0 else nch - 1 - kc_
                        ci = t0 // 64 + k
                        tbk = k // 2
                        hp = slice((k % 2) * 64, (k % 2) * 64 + 64)
                        cs64 = slice(k * 64, (k + 1) * 64)
                        St, Stb = S[d]
                        Sbt, Sbb = Sb[d]
                        for hh in range(4):
                            ps, pb = s.ps()
                            s.mm(ps[0:64, 0:64], kd_[:, hh, cs64], qd_[:, hh, cs64], True, True, reads=[kdb, qdb], writes=[pb])
                            am, amb = attm[kat % 8]
                            kat += 1
                            s.op("dve", ins("tensor_tensor", out=am[hp, :], in0=ps[0:64, 0:64], in1=tri_bf[0:64, 2 + d, 0:64], op=ALU.mult),
                                 reads=[pb, cb_const], writes=[amb])
                            pso, pob = s.ps("b")
                            for vc in range(2):
                                s.mm(pso[:, vc * 64:(vc + 1) * 64], v_[hp, tbk, hh * 256 + vc * 128:hh * 256 + (vc + 1) * 128], am[hp, :], True, False, reads=[vb_, amb], writes=[pob])
                                s.mm(pso[:, vc * 64:(vc + 1) * 64], Sbt[:, hh, vc * 128:(vc + 1) * 128], qd_[:, hh, cs64], False, True, reads=[Sbb[hh], qdb], writes=[pob])
                            s.op("act", ins("activation", out=of_[:, hh * 2:hh * 2 + 2, cs64], in_=pso[:, 0:128].rearrange("p (v t) -> p v t", t=64), func=AF.Identity),
                                 reads=[pob], writes=[ofb])
                            ps2, pb2 = s.ps()
                            s.mm(ps2[:, 0:256], kr_[hp, tbk, hh, :], v_[hp, tbk, hh * 256:(hh + 1) * 256], True, True, reads=[krb, vb_], writes=[pb2])
                            s.op("dve", ins("scalar_tensor_tensor", out=St[:, hh, :], in0=St[:, hh, :], scalar=EL[:, d, hh, ci:ci + 1], in1=ps2[:, 0:256], op0=ALU.mult, op1=ALU.add),
                                 reads=[pb2, elb2, Stb[hh]], writes=[Stb[hh]])
                            s.op("pool", ins("tensor_copy", out=Sbt[:, hh, :], in_=St[:, hh, :]), reads=[Stb[hh]], writes=[Sbb[hh]])
                for d in range(2):
                    ti, t0, n, qd_, qdb, kd_, kdb, kr_, krb, v_, vb_, of_, ofb = cur[d]
                    s.dma(ODs[d][:, :, t0:t0 + n], of_[:, :, :n], reads=[ofb], writes=[OD_b[d][ti]])
            if si > 0:
                for d in range(2):
                    s.dma(states_out[d][si - 1], S[d][0][:], reads=S[d][1])
        s.barrier()
        s.release()
        Wo = s.alloc([128, 8, 1024], BF16, "wo")
        wob = load_w(Wo, w1_o, 8)
        gn = s.alloc([128, 2], F32, "gn")
        gnb = Buf()
        s.dma(gn[:], w1_gn, writes=[gnb])
        rctx = ResCtx()
        oa = [(s.alloc([128, 8, 512], BF16, "oa"), Buf()) for _ in range(2)]
        ob_ = [(s.alloc([128, 8, 512], BF16, "ob"), Buf()) for _ in range(2)]
        rr = [(s.alloc([128, 8, 512], BF16, "rr"), Buf()) for _ in range(2)]
        osum = (s.alloc([128, 8, 512], F32, "osum"), Buf())
        sq = (s.alloc([128, 8, 512], BF16, "sq"), Buf())
        rs_ = [(s.alloc([128, 512], F32, "rs"), Buf()) for _ in range(2)]
        ofin = [(s.alloc([128, 8, 512], BF16, "ofin"), Buf()) for _ in range(2)]
        for ti in range(NT):
            t0, n, cond, _, _ = TILES[ti]
            a_, a_b = oa[ti % 2]
            b_, b_b = ob_[ti % 2]
            r_, r_b = rr[ti % 2]
            s.dma(a_[:, :, :n], OS[:, :, t0:t0 + n], writes=[a_b])
            s.dma(b_[:, :, :n], OS2[:, :, t0:t0 + n], writes=[b_b])
            s.dma(r_[:, :, :n], RS[:, :, t0:t0 + n], writes=[r_b])
            os_, osb_ = osum
            sq_, sqb_ = sq
            s.op("dve", ins("tensor_tensor", out=os_[:, :, :n], in0=a_[:, :, :n], in1=b_[:, :, :n], op=ALU.add), reads=[a_b, b_b], writes=[osb_])
            s.op("act", ins("activation", out=sq_[:, :, :n], in_=os_[:, :, :n], func=AF.Square), reads=[osb_], writes=[sqb_])
            f_, f_b = ofin[ti % 2]
            for hh in range(4):
                ps, pb = s.ps()
                for vc in range(2):
                    s.mm(ps[:, :n], ones_bf[:], sq_[:, hh * 2 + vc, :n], vc == 0, vc == 1, reads=[sqb_, cb_const], writes=[pb])
                rt_, rtb = rs_[hh % 2]
                s.op("act", ins("activation", out=rt_[:, :n], in_=ps[:, :n], func=AF.Sqrt, scale=1.0 / 256, bias=EPSB[:, 0:1]), reads=[pb], writes=[rtb])
                s.op("dve", ins("reciprocal", out=rt_[:, :n], in_=rt_[:, :n]), reads=[rtb], writes=[rtb])
                for vc in range(2):
                    c = hh * 2 + vc
                    s.op("dve", ins("tensor_tensor", out=os_[:, c, :n], in0=os_[:, c, :n], in1=rt_[:, :n], op=ALU.mult), reads=[osb_, rtb], writes=[osb_])
                    s.op("dve", ins("scalar_tensor_tensor", out=f_[:, c, :n], in0=os_[:, c, :n], scalar=gn[:, vc:vc + 1], in1=r_[:, c, :n], op0=ALU.mult, op1=ALU.mult),
                         reads=[osb_, gnb, r_b], writes=[f_b])
            outproj_resid(rctx, l, 2, ti, Wo, wob, 8, f_, [f_b], Xsrc, Xsrc_b, X, X_b)

    def ssd_layer(l, Xsrc, Xsrc_b):
        XBu = U
        s.barrier()
        s.release()
        Wz = s.alloc([128, 8, 2048], BF16, "wz")
        wzb = load_w(Wz, w3_z, 8)
        Wx = s.alloc([128, 8, 3072], BF16, "wx")
        wxb = load_w(Wx, w3_xbc, 8)
        Wdt = s.alloc([128, 8, 64], BF16, "wdt")
        wdb = load_w(Wdt, w3_dt, 8)
        vec = s.alloc([128, 5, 32], F32, "vec")
        cbuf = Buf()
        s.dma(vec[:], ssd_vec.rearrange("a b -> (a b)").partition_broadcast(128).rearrange("p (a b) -> p a b", a=5), writes=[cbuf])
        avec = s.alloc([128, 64], F32, "avec")
        s.op("act", ins("activation", out=avec[:], in_=vec[:, 0:2, :].rearrange("p a b -> p (a b)"), func=AF.Exp), reads=[cbuf], writes=[cbuf])
        s.op("dve", ins("tensor_scalar", out=avec[:], in0=avec[:], scalar1=-1.0, scalar2=None, op0=ALU.mult), reads=[cbuf], writes=[cbuf])
        nctx = NormCtx(1)
        xbt = (s.alloc([128, 24, 512], BF16, "xbt"), [Buf() for _ in range(24)])
        zt = (s.alloc([128, 4, 2048], BF16, "zt"), Buf())
        dtt = (s.alloc([128, 4, 2, 64], F32, "dtt"), Buf())
        dtmp = (s.alloc([128, 64], F32, "dtmp"), Buf())
        S_b = [Buf() for _ in range(NT)]
        ev = 0
        for ti in range(NT):
            t0, n, cond, s0, s1 = TILES[ti]
            nb = n // 128
            h, hb = norm_tile(nctx, l, 0, ti, Xsrc, Xsrc_b)
            xb_, xbb = xbt
            for cc in range(24):
                ps, pb = s.ps()
                for kc in range(8):
                    s.mm(ps[:, :n], Wx[:, kc, cc * 128:(cc + 1) * 128], h[:, kc, :n], kc == 0, kc == 7, reads=[wxb[kc], hb], writes=[pb])
                if ev % 2 == 0:
                    s.op("act", ins("activation", out=xb_[:, cc, :n], in_=ps[:, :n], func=AF.Identity), reads=[pb], writes=[xbb[cc]])
                else:
                    s.op("dve", ins("tensor_copy", out=xb_[:, cc, :n], in_=ps[:, :n]), reads=[pb], writes=[xbb[cc]])
                ev += 1
            s.dma(XBu[:, 0:24, t0:t0 + n], xb_[:, :, :n], reads=xbb, writes=[S_b[ti]])
            z_, z_b = zt
            d_, d_b = dtt
            for tbk in range(nb):
                for vg in range(4):
                    ps, pb = s.ps()
                    for kc in range(8):
                        s.mm(ps[:, :512], h[:, kc, tbk * 128:(tbk + 1) * 128], Wz[:, kc, vg * 512:(vg + 1) * 512], kc == 0, kc == 7, reads=[wzb[kc], hb], writes=[pb])
                    s.op("act", ins("activation", out=z_[:, tbk, vg * 512:(vg + 1) * 512], in_=ps[:, :512], func=AF.Silu), reads=[pb], writes=[z_b])
                ps, pb = s.ps()
                for kc in range(8):
                    s.mm(ps[:, 0:64], h[:, kc, tbk * 128:(tbk + 1) * 128], Wdt[:, kc, :], kc == 0, kc == 7, reads=[wdb[kc], hb], writes=[pb])
                tm, tmb = dtmp
                s.op("dve", ins("tensor_tensor", out=tm[:], in0=ps[:, 0:64], in1=vec[:, 2:4, :].rearrange("p a b -> p (a b)"), op=ALU.add), reads=[pb, cbuf], writes=[tmb])
                s.op("act", ins("activation", out=tm[:], in_=tm[:], func=AF.Exp), reads=[tmb], writes=[tmb])
                s.op("act", ins("activation", out=d_[:, tbk, 0, :], in_=tm[:], func=AF.Ln, bias=1.0, scale=1.0), reads=[tmb], writes=[d_b])
                s.op("dve", ins("tensor_tensor", out=d_[:, tbk, 1, :], in0=d_[:, tbk, 0, :], in1=avec[:], op=ALU.mult), reads=[d_b, cbuf], writes=[d_b])
            s.dma(rows_view(ZS, 2048, t0, nb, 0, 2048), z_[:, 0:nb, :], reads=[z_b], writes=[S_b[ti]])
            s.dma(AP(DTS.tensor, t0 * 128, [[128, 128], [128 * 128, nb], [1, 128]]), d_[:, 0:nb, :, :].rearrange("p b a c -> p b (a c)"), reads=[d_b], writes=[S_b[ti]])
        s.barrier()
        s.release()
        cw = s.alloc([128, 3, 24], F32, "scw")
        cbias = s.alloc([128, 24], F32, "scb")
        cwb = Buf()
        s.dma(cw[:], ssd_cw, writes=[cwb])
        s.dma(cbias[:], ssd_cb, writes=[cwb])
        ug = s.alloc([128, 24, 514], BF16, "sug")
        ugb = [Buf() for _ in range(2)]
        xc = (s.alloc([128, 24, 512], BF16, "xc"), [Buf() for _ in range(24)])
        tmps = [(s.alloc([128, 512], F32, "st"), Buf()) for _ in range(2)]
        xtt = (s.alloc([128, 4, 2048], BF16, "xtt"), Buf())
        btt = (s.alloc([128, 4, 512], BF16, "btt"), Buf())
        kp = 0
        for ti in range(NT):
            t0, n, cond, s0, s1 = TILES[ti]
            nb = n // 128
            lo = (t0 - 1) >= s0
            hi = (t0 + n) < s1
            for grp in range(2):
                gs_ = slice(grp * 12, (grp + 1) * 12)
                if not lo:
                    s.op("pool", ins("memset", ug[:, gs_, 0:1], 0.0), writes=[ugb[grp]])
                if not hi:
                    s.op("pool", ins("memset", ug[:, gs_, n + 1:n + 2], 0.0), writes=[ugb[grp]])
                c0 = 0 if lo else 1
                c1 = n + 2 if hi else n + 1
                s.dma(ug[:, gs_, c0:c1], XBu[:, gs_, t0 - 1 + c0:t0 - 1 + c1], writes=[ugb[grp]])
            xc_, xcb = xc
            for col in range(24):
                grp = col // 12
                tt, ttb = tmps[kp % 2]
                kp += 1
                s.op("act", ins("activation", out=tt[:, :n], in_=ug[:, col, 1:n + 1], func=AF.Identity, scale=cw[:, 1, col:col + 1], bias=cbias[:, col:col + 1]),
                     reads=[ugb[grp], cwb], writes=[ttb])
                s.op("dve", ins("scalar_tensor_tensor", out=tt[:, :n], in0=ug[:, col, 0:n], scalar=cw[:, 0, col:col + 1], in1=tt[:, :n], op0=ALU.mult, op1=ALU.add),
                     reads=[ugb[grp], cwb, ttb], writes=[ttb])
                s.op("dve", ins("scalar_tensor_tensor", out=tt[:, :n], in0=ug[:, col, 2:n + 2], scalar=cw[:, 2, col:col + 1], in1=tt[:, :n], op0=ALU.mult, op1=ALU.add),
                     reads=[ugb[grp], cwb, ttb], writes=[ttb])
                s.op("act", ins("activation", out=xc_[:, col, :n], in_=tt[:, :n], func=AF.Silu), reads=[ttb], writes=[xcb[col]])
            s.dma(QK[:, 0:8, t0:t0 + n], xc_[:, 16:24, :n], reads=xcb[16:24], writes=[S_b[ti]])
            x_, x_b = xtt
            b_, b_b = btt
            tv = 0
            for tbk in range(nb):
                for col in range(20):
                    ps, pb = s.ps()
                    psv = ps[:].bitcast(BF16)
                    s.op("pe", ins("transpose", psv[:, 0:128], xc_[:, col, tbk * 128:(tbk + 1) * 128], ident_bf[:]), reads=[xcb[col], cb_const], writes=[pb])
                    dst = x_[:, tbk, col * 128:(col + 1) * 128] if col < 16 else b_[:, tbk, (col - 16) * 128:(col - 15) * 128]
                    dbuf = x_b if col < 16 else b_b
                    if tv % 2 == 0:
                        s.op("act", ins("activation", out=dst, in_=psv[:, 0:128], func=AF.Identity), reads=[pb], writes=[dbuf])
                    else:
                        s.op("dve", ins("tensor_copy", out=dst, in_=psv[:, 0:128]), reads=[pb], writes=[dbuf])
                    tv += 1
            s.dma(rows_view(XTs, 2048, t0, nb, 0, 2048), x_[:, 0:nb, :], reads=[x_b], writes=[S_b[ti]])
            s.dma(rows_view(BTs, 512, t0, nb, 0, 512), b_[:, 0:nb, :], reads=[b_b], writes=[S_b[ti]])
        s.barrier()
        s.release()
        vec = s.alloc([128, 5, 32], F32, "vec")
        negm = s.alloc([128, 2, 128], F32, "negm")
        cbuf = Buf()
        s.op("dve", ins("tensor_scalar", out=negm[:], in0=stage[:, 0:2, :], scalar1=-1.0, scalar2=30000.0, op0=ALU.add, op1=ALU.mult), reads=[bstage], writes=[cbuf])
        ST = [(s.alloc([128, 2048], F32, "ST"), [Buf() for _ in range(4)]) for _ in range(2)]
        SbT = [(s.alloc([128, 2048], BF16, "SbT"), [Buf() for _ in range(4)]) for _ in range(2)]
        bct = [(s.alloc([128, 8, 512], BF16, "bct"), Buf()) for _ in range(2)]
        xts = [(s.alloc([128, 4, 2048], BF16, "xts"), Buf()) for _ in range(2)]
        bts = [(s.alloc([128, 4, 512], BF16, "bts"), Buf()) for _ in range(2)]
        dts = [(s.alloc([128, 4, 2, 64], F32, "dts"), Buf()) for _ in range(2)]
        yts = [(s.alloc([128, 4, 2048], BF16, "yts"), Buf()) for _ in range(2)]
        cumT = [(s.alloc([128, 32], F32, "cumT"), Buf()) for _ in range(2)]
        cbT = [(s.alloc([128, 128], F32, "cbT"), Buf()) for _ in range(2)]
        lat = [(s.alloc([128, 4, 128], F32, "lat"), Buf()) for _ in range(2)]
        cB = [(s.alloc([128, 4, 128], F32, "cB"), Buf()) for _ in range(2)]
        seg = [(s.alloc([128, 4, 128], F32, "seg"), Buf()) for _ in range(2)]
        Et = [(s.alloc([128, 4, 128], F32, "Et"), Buf()) for _ in range(2)]
        ecB = [(s.alloc([128, 4, 128], F32, "ecB"), Buf()) for _ in range(4)]
        te = [(s.alloc([128, 4], F32, "te"), Buf()) for _ in range(2)]
        xs = [(s.alloc([128, 512], BF16, "xs"), Buf()) for _ in range(2)]
        Wt = [(s.alloc([128, 128], BF16, "Wt"), Buf()) for _ in range(8)]
        CEt = [(s.alloc([128, 128], BF16, "CEt"), Buf()) for _ in range(8)]
        SEQT = [list(range(8)), [8], [9]]
        st_in = [s3f, s3b]
        st_out = [o_sf, o_sb]
        Y_b = [Buf() for _ in range(NT)]
        kw = 0
        kq4 = 0
        kg = 0
        for si, tiles in enumerate(SEQT):
            for d in range(2):
                St, Stb = ST[d]
                Sbt, Sbb = SbT[d]
                if si == 0:
                    s.dma(St[:], st_in[d], writes=Stb)
                else:
                    s.op("pool", ins("memset", St[:], 0.0), writes=Stb)
                for g in range(4):
                    s.op("act", ins("activation", out=Sbt[:, g * 512:(g + 1) * 512], in_=St[:, g * 512:(g + 1) * 512], func=AF.Identity), reads=[Stb[g]], writes=[Sbb[g]])
            nt_ = len(tiles)
            for step in range(nt_):
                cur = {}
                for d in range(2):
                    ti = tiles[step] if d == 0 else tiles[nt_ - 1 - step]
                    t0, n, cond, s0, s1 = TILES[ti]
                    nb = n // 128
                    bc_, bcb = bct[d]
                    x_, x_b = xts[d]
                    b_, b_b = bts[d]
                    d_, d_b = dts[d]
                    y_, y_b = yts[d]
                    s.dma(bc_[:, :, :n], QK[:, 0:8, t0:t0 + n], writes=[bcb])
                    s.dma(x_[:, 0:nb, :], rows_view(XTs, 2048, t0, nb, 0, 2048), writes=[x_b])
                    s.dma(b_[:, 0:nb, :], rows_view(BTs, 512, t0, nb, 0, 512), writes=[b_b])
                    s.dma(d_[:, 0:nb, :, :].rearrange("p b a c -> p b (a c)"), AP(DTS.tensor, t0 * 128, [[128, 128], [128 * 128, nb], [1, 128]]), writes=[d_b])
                    cur[d] = (ti, t0, n, nb)
                nbs = cur[0][3]
                for kc_ in range(nbs):
                    for d in range(2):
                        ti, t0, n, nb = cur[d]
                        tbk = kc_ if d == 0 else nb - 1 - kc_
                        last = 127 if d == 0 else 0
                        bc_, bcb = bct[d]
                        x_, x_b = xts[d]
                        b_, b_b = bts[d]
                        d_, d_b = dts[d]
                        y_, y_b = yts[d]
                        St, Stb = ST[d]
                        Sbt, Sbb = SbT[d]
                        tk = slice(tbk * 128, (tbk + 1) * 128)
                        la = d_[:, tbk, 1, d * 32:(d + 1) * 32]
                        dtv = d_[:, tbk, 0, d * 32:(d + 1) * 32]
                        psc, pcb = s.ps()
                        s.mm(psc[:, 0:32], stage[:, d, :], la, True, True, reads=[bstage, d_b], writes=[pcb])
                        cT, cTb = cumT[d]
                        s.op("dve", ins("tensor_copy", out=cT[:], in_=psc[:, 0:32]), reads=[pcb], writes=[cTb])
                        for g in range(4):
                            ps, pb = s.ps()
                            s.mm(ps[:, 0:128], bc_[:, g, tk], bc_[:, 4 + g, tk], True, True, reads=[bcb], writes=[pb])
                            cb_, cb_b = cbT[kg % 2]
                            kg += 1
                            s.op("act", ins("activation", out=cb_[:], in_=ps[:, 0:128], func=AF.Identity), reads=[pb], writes=[cb_b])
                            psy, pyb = s.ps("b")
                            x4, x4b = xs[g % 2]
                            ecs = []
                            for hb_ in range(2):
                                h0 = g * 8 + hb_ * 4
                                lt, ltb = lat[kq4 % 2]
                                cb4, cb4b = cB[kq4 % 2]
                                sg, sgb = seg[kq4 % 2]
                                E_, E_b = Et[kq4 % 2]
                                ec, ecb = ecB[kq4 % 4]
                                te_, te_b = te[kq4 % 2]
                                kq4 += 1
                                ecs.append((ec, ecb))
                                s.op("dve", ins("tensor_tensor", out=lt[:], in0=stage[:, d, :].unsqueeze(1).to_broadcast([128, 4, 128]),
                                                                                          in1=la[:, h0:h0 + 4].unsqueeze(2).to_broadcast([128, 4, 128]), op=ALU.mult),
                                     reads=[bstage, d_b], writes=[ltb])
                                ps2, pb2 = s.ps()
                                s.mm(ps2[:, 0:512], ones_f[:], lt[:].rearrange("p a b -> p (a b)"), True, True, reads=[ltb, cb_const], writes=[pb2])
                                s.op("act", ins("activation", out=cb4[:].rearrange("p a b -> p (a b)"), in_=ps2[:, 0:512], func=AF.Identity), reads=[pb2], writes=[cb4b])
                                for hh in range(4):
                                    hq = h0 + hh
                                    s.op("dve", ins("scalar_tensor_tensor", out=sg[:, hh, :], in0=cb4[:, hh, :], scalar=cT[:, hq:hq + 1], in1=negm[:, d, :],
                                                                                                                 op0=ALU.subtract, op1=ALU.add),
                                         reads=[cb4b, cTb, cbuf], writes=[sgb])
                                s.op("act", ins("activation", out=E_[:], in_=sg[:], func=AF.Exp), reads=[sgb], writes=[E_b])
                                s.op("act", ins("activation", out=ec[:], in_=cb4[:], func=AF.Exp), reads=[cb4b], writes=[ecb])
                                s.op("dve", ins("tensor_tensor", out=te_[:], in0=E_[:, :, last], in1=dtv[:, h0:h0 + 4], op=ALU.mult),
                                     reads=[E_b, d_b], writes=[te_b])
                                s.op("dve", ins("tensor_tensor",
                                    out=x4[:, hb_ * 256:(hb_ + 1) * 256].rearrange("p (a b) -> p a b", b=64),
                                    in0=x_[:, tbk, h0 * 64:(h0 + 4) * 64].rearrange("p (a b) -> p a b", b=64),
                                    in1=te_[:].unsqueeze(2).to_broadcast([128, 4, 64]), op=ALU.mult),
                                     reads=[x_b, te_b], writes=[x4b])
                                for hh in range(4):
                                    hq = h0 + hh
                                    W_, W_b = Wt[kw % 8]
                                    C_, C_b = CEt[kw % 8]
                                    kw += 1
                                    s.op("dve", ins("scalar_tensor_tensor", out=W_[:], in0=E_[:, hh, :], scalar=dtv[:, hq:hq + 1], in1=cb_[:], op0=ALU.mult, op1=ALU.mult),
                                         reads=[E_b, d_b, cb_b], writes=[W_b])
                                    s.op("pool", ins("tensor_tensor", out=C_[:], in0=bc_[:, 4 + g, tk], in1=ec[:, hh, :], op=ALU.mult),
                                         reads=[bcb, ecb], writes=[C_b])
                                    ycol = slice((hq % 8) * 64, (hq % 8 + 1) * 64)
                                    s.mm(psy[:, ycol], W_[:], x_[:, tbk, hq * 64:(hq + 1) * 64], True, False, reads=[W_b, x_b], writes=[pyb])
                                    s.mm(psy[:, ycol], C_[:], Sbt[:, hq * 64:(hq + 1) * 64], False, True, reads=[C_b, Sbb[g]], writes=[pyb])
                            s.op("act", ins("activation", out=y_[:, tbk, g * 512:(g + 1) * 512], in_=psy[:, 0:512], func=AF.Identity), reads=[pyb], writes=[y_b])
                            pss, psb_ = s.ps()
                            s.mm(pss[:, 0:512], b_[:, tbk, g * 128:(g + 1) * 128], x4[:], True, True, reads=[b_b, x4b], writes=[psb_])
                            for hh in range(8):
                                hq = g * 8 + hh
                                ec, ecb = ecs[hh // 4]
                                s.op("dve", ins("scalar_tensor_tensor",
                                    out=St[:, hq * 64:(hq + 1) * 64], in0=St[:, hq * 64:(hq + 1) * 64], scalar=ec[:, hh % 4, last:last + 1], in1=pss[:, hh * 64:(hh + 1) * 64], op0=ALU.mult, op1=ALU.add),
                                     reads=[psb_, ecb, Stb[g]], writes=[Stb[g]])
                            s.op("pool", ins("tensor_copy", out=Sbt[:, g * 512:(g + 1) * 512], in_=St[:, g * 512:(g + 1) * 512]), reads=[Stb[g]], writes=[Sbb[g]])
                for d in range(2):
                    ti, t0, n, nb = cur[d]
                    y_, y_b = yts[d]
                    s.dma(rows_view(YD[d], 2048, t0, nb, 0, 2048), y_[:, 0:nb, :], reads=[y_b], writes=[Y_b[ti]])
            if si > 0:
                for d in range(2):
                    s.dma(st_out[d][si - 1], ST[d][0][:], reads=ST[d][1])
        s.barrier()
        s.release()
        Wo = s.alloc([128, 16, 1024], BF16, "wo")
        wob = load_w(Wo, w3_o, 16)
        vec = s.alloc([128, 5, 32], F32, "vec")
        gnb = s.alloc([128, 2048], F32, "gnb")
        cbuf = Buf()
        s.dma(vec[:], ssd_vec.rearrange("a b -> (a b)").partition_broadcast(128).rearrange("p (a b) -> p a b", a=5), writes=[cbuf])
        s.dma(gnb[:], ssd_gn.partition_broadcast(128), writes=[cbuf])
        rctx = ResCtx()
        yf = (s.alloc([128, 4, 2048], BF16, "yf"), Buf())
        yb = (s.alloc([128, 4, 2048], BF16, "yb"), Buf())
        xt_ = (s.alloc([128, 4, 2048], BF16, "xt4"), Buf())
        zs = (s.alloc([128, 4, 2048], BF16, "zs"), Buf())
        yv = [(s.alloc([128, 2048], F32, "yv"), Buf()) for _ in range(2)]
        tv_ = (s.alloc([128, 2048], F32, "tv"), Buf())
        ynb = (s.alloc([128, 2048], BF16, "ynb"), Buf())
        ssq = [(s.alloc([128, 1], F32, "ssq"), Buf()) for _ in range(2)]
        oT = [(s.alloc([128, 16, 512], BF16, "oT"), Buf()) for _ in range(2)]
        kk = 0
        tv = 0
        for ti in range(NT):
            t0, n, cond, _, _ = TILES[ti]
            nb = n // 128
            s.dma(yf[0][:, 0:nb, :], rows_view(YD[0], 2048, t0, nb, 0, 2048), writes=[yf[1]])
            s.dma(yb[0][:, 0:nb, :], rows_view(YD[1], 2048, t0, nb, 0, 2048), writes=[yb[1]])
            s.dma(xt_[0][:, 0:nb, :], rows_view(XTs, 2048, t0, nb, 0, 2048), writes=[xt_[1]])
            s.dma(zs[0][:, 0:nb, :], rows_view(ZS, 2048, t0, nb, 0, 2048), writes=[zs[1]])
            o_, o_b = oT[ti % 2]
            for tbk in range(nb):
                y_, y_b = yv[kk % 2]
                sq_, sq_b = ssq[kk % 2]
                kk += 1
                t_, t_b = tv_
                s.op("dve", ins("tensor_tensor", out=y_[:], in0=yf[0][:, tbk, :], in1=yb[0][:, tbk, :], op=ALU.add), reads=[yf[1], yb[1]], writes=[y_b])
                s.op("pool", ins("tensor_tensor", out=t_[:].rearrange("p (a b) -> p a b", b=64), in0=xt_[0][:, tbk, :].rearrange("p (a b) -> p a b", b=64),
                                                              in1=vec[:, 4, :].unsqueeze(2).to_broadcast([128, 32, 64]), op=ALU.mult), reads=[xt_[1], cbuf], writes=[t_b])
                s.op("dve", ins("tensor_tensor", out=y_[:], in0=y_[:], in1=t_[:], op=ALU.add), reads=[y_b, t_b], writes=[y_b])
                s.op("dve", ins("tensor_tensor", out=y_[:], in0=y_[:], in1=zs[0][:, tbk, :], op=ALU.mult), reads=[y_b, zs[1]], writes=[y_b])
                s.op("pool", ins("memset", sq_[:], 0.0), writes=[sq_b])
                s.op("act", ins("activation", out=t_[:], in_=y_[:], func=AF.Square, accum_out=sq_[:, 0:1]), reads=[y_b, sq_b], writes=[t_b, sq_b])
                s.op("act", ins("activation", out=sq_[:], in_=sq_[:], func=AF.Sqrt, scale=1.0 / 2048, bias=EPSB[:, 0:1]), reads=[sq_b], writes=[sq_b])
                s.op("dve", ins("reciprocal", out=sq_[:], in_=sq_[:]), reads=[sq_b], writes=[sq_b])
                yn, ynb_ = ynb
                s.op("dve", ins("scalar_tensor_tensor", out=yn[:], in0=y_[:], scalar=sq_[:, 0:1], in1=gnb[:], op0=ALU.mult, op1=ALU.mult), reads=[y_b, sq_b, cbuf], writes=[ynb_])
                for c in range(16):
                    ps, pb = s.ps()
                    psv = ps[:].bitcast(BF16)
                    s.op("pe", ins("transpose", psv[:, 0:128], yn[:, c * 128:(c + 1) * 128], ident_bf[:]), reads=[ynb_, cb_const], writes=[pb])
                    if tv % 2 == 0:
                        s.op("act", ins("activation", out=o_[:, c, tbk * 128:(tbk + 1) * 128], in_=psv[:, 0:128], func=AF.Identity), reads=[pb], writes=[o_b])
                    else:
                        s.op("dve", ins("tensor_copy", out=o_[:, c, tbk * 128:(tbk + 1) * 128], in_=psv[:, 0:128]), reads=[pb], writes=[o_b])
                    tv += 1
            outproj_resid(rctx, l, 2, ti, Wo, wob, 16, o_, [o_b], Xsrc, Xsrc_b, X, X_b)

    import os
    STOP = int(os.environ.get("KSTOP", "99"))

    class _Stop(Exception):
        pass

    def chk(k):
        if STOP == k:
            raise _Stop()

    try:
        prologue()
        chk(0)
        Xsrc, Xsrc_b = xT, xT_b
        for l in range(NLAYERS):
            kind = l % 4
            if kind == 0:
                qkv_phase(l, w0_qk, 12, w0_v, 256, Xsrc, Xsrc_b, True, kout=(8, o_wk, 64), vout=o_wv)
                chk(1)
                attn_phase(l, "win", Xsrc, Xsrc_b)
                chk(2)
                outproj_phase(l, w0_o, 8, OS, None, Xsrc, Xsrc_b)
                chk(3)
            elif kind == 1:
                gla_layer(l, Xsrc, Xsrc_b)
            elif kind == 2:
                qkv_phase(l, w2_qk, 16, w2_v, 1024, Xsrc, Xsrc_b, True, kout=(8, o_dk, 128), vout=o_dv)
                attn_phase(l, "diff", Xsrc, Xsrc_b)
                outproj_phase(l, w2_o, 8, OS, None, Xsrc, Xsrc_b)
            else:
                ssd_layer(l, Xsrc, Xsrc_b)
            Xsrc, Xsrc_b = X, X_b
            ffn(l, Xsrc, Xsrc_b)
            chk(10 + l)
        s.barrier()
        s.release()
        nctx = NormCtx()
        for ti in range(NT):
            norm_tile(nctx, 0, 0, ti, Xsrc, Xsrc_b, final_out=yT)
    except _Stop:
        pass
    s.barrier()
    counts = s.emit()
    return nc, counts

def _fm(w):
    K, N = w.shape
    return np.ascontiguousarray(w.reshape(K // 128, 128, N).transpose(1, 0, 2))


def _pc(v):
    return np.ascontiguousarray(v.reshape(-1, 128).T)


def _consts():
    f32 = np.float32
    t = np.arange(LS)
    row = (t // 64).astype(f32)
    col = (t % 64).astype(f32)
    inv = (10000.0 ** (-np.arange(0, 32, 2, dtype=f32) / 32)).astype(f32)
    cos = np.zeros((128, LS), f32)
    sin = np.zeros((128, LS), f32)
    perm = np.zeros((128, 128), f32)
    for p in range(128):
        d = p % 64
        axis = d // 32
        idx = d % 16
        second = (d % 32) >= 16
        ang = (row if axis == 0 else col) * inv[idx]
        cos[p] = np.cos(ang)
        sin[p] = np.sin(ang) * (1.0 if second else -1.0)
        partner = p + 16 if not second else p - 16
        perm[partner, p] = 1.0
    j = np.arange(128)[:, None]
    i = np.arange(512)[None, :]
    wmask = np.zeros((128, 6, 512), f32)
    for mi in range(6):
        o = mi - 1
        wmask[:, mi, :] = (np.abs(i - (o * 128 + j)) <= 128).astype(f32)
    tri = np.zeros((128, 4, 128), f32)
    jj = np.arange(128)[:, None]
    ii = np.arange(128)[None, :]
    tri[:, 0, :] = (jj <= ii)
    tri[:, 1, :] = (jj >= ii)
    tri[0:64, 2, 0:64] = (jj[0:64] <= ii[:, 0:64])
    tri[0:64, 3, 0:64] = (jj[0:64] >= ii[:, 0:64])
    m64 = np.ones((128, 512), f32)
    m64[:, ::64] = 0.0
    return dict(rcos=cos, rsin=sin, perm=perm, wmask=wmask, tri=tri, m64=m64)


_PROG = {}


def _get_prog(nl):
    if nl not in _PROG:
        _PROG[nl] = build_program(nl)
    return _PROG[nl]


def kernel(NLAYERS=4, **inp):
    f32 = np.float32
    g = {k: np.asarray(v) for k, v in inp.items()}
    nc, counts = _get_prog(NLAYERS)
    common = dict(_consts())
    common["ada_w"] = np.ascontiguousarray(g["ada_w"].reshape(4, 8, 128, 6144).transpose(0, 2, 1, 3))
    common["ada_b"] = np.ascontiguousarray(g["ada_b"].reshape(4, 48, 128).transpose(2, 0, 1))
    common["nrm"] = np.ascontiguousarray(np.stack([g["norm_mix"], g["norm_ffn"]], axis=1).reshape(4, 2, 8, 128).transpose(3, 0, 1, 2))
    common["fnorm"] = _pc(g["final_norm"])
    common["w_up"] = np.ascontiguousarray(g["ffn_w_up"].reshape(4, 8, 128, 5632).transpose(0, 2, 1, 3))
    common["w_dn"] = np.ascontiguousarray(g["ffn_w_down"].reshape(4, 22, 128, 1024).transpose(0, 2, 1, 3))
    common["ffn_cw"] = np.ascontiguousarray(g["ffn_conv_w"].reshape(4, 3, 44, 128).transpose(3, 0, 1, 2))
    common["ffn_cb"] = np.ascontiguousarray(g["ffn_conv_b"].reshape(4, 44, 128).transpose(2, 0, 1))
    wq = g["win_w_qkv"][0]
    qcols = wq[:, 0:1024]
    kcols = wq[:, 1024:1280]
    vcols = wq[:, 1280:1536]
    kdup = np.concatenate([np.concatenate([kcols[:, h * 64:(h + 1) * 64]] * 2, axis=1) for h in range(4)], axis=1)
    common["w0_qk"] = _fm(np.concatenate([qcols, kdup], axis=1))
    common["w0_v"] = _fm(vcols)
    common["w0_o"] = _fm(g["win_w_o"][0])
    common["sink"] = np.ascontiguousarray(g["win_sink"][0])
    wg = g["gla_w_qkvr"][0]
    common["w1_qk"] = _fm(wg[:, 0:1024])
    common["w1_v"] = _fm(wg[:, 1024:2048])
    common["w1_r"] = _fm(wg[:, 2048:3072])
    common["w1_g1"] = _fm(np.concatenate([g["gla_w_gf1"][0], g["gla_w_gb1"][0]], axis=1))
    g2 = np.zeros((32, 1024), f32)
    g2[0:16, 0:512] = g["gla_w_gf2"][0]
    g2[16:32, 512:1024] = g["gla_w_gb2"][0]
    common["w1_g2"] = g2
    common["w1_gb"] = _pc(np.concatenate([g["gla_b_gf"][0], g["gla_b_gb"][0]]))
    common["w1_gn"] = _pc(g["gla_norm"][0])
    common["w1_o"] = _fm(g["gla_w_o"][0])
    wd = g["diff_w_qkv"][0]
    common["w2_qk"] = _fm(wd[:, 0:2048])
    common["w2_v"] = _fm(wd[:, 2048:3072])
    common["w2_o"] = _fm(g["diff_w_o"][0])
    common["lqk"] = np.ascontiguousarray(np.stack([g["diff_lq1"][0], g["diff_lk1"][0], g["diff_lq2"][0], g["diff_lk2"][0]]))
    common["w2_gn"] = np.ascontiguousarray(g["diff_norm"][0].reshape(128, 1))
    common.update(_host_ssd_common(g))
    in_maps = []
    for b in range(8):
        m = dict(common)
        xs = g["x_sample"][b]
        xp = g["x_prompt"][2 * b:2 * b + 2].reshape(512, 1024)
        xa = np.concatenate([xs, xp], axis=0)
        m["xT"] = np.ascontiguousarray(xa.T.reshape(8, 128, T).transpose(1, 0, 2))
        cond = np.stack([g["c"][b], g["c_ctx"]], axis=1)
        m["condT"] = np.ascontiguousarray(cond.reshape(8, 128, 2).transpose(1, 0, 2))
        ck = g["cache_win_k"][b, 0]
        kT = ck.transpose(2, 1, 0)
        m["ck0"] = np.ascontiguousarray(np.concatenate([kT, kT], axis=0))
        m["cv0"] = np.ascontiguousarray(g["cache_win_v"][b, 0].reshape(512, 256))
        m["s1f"] = np.ascontiguousarray(g["state_gla_fwd"][b, 0].transpose(1, 0, 2))
        m["s1b"] = np.ascontiguousarray(g["state_gla_bwd"][b, 0].transpose(1, 0, 2))
        dk = g["cache_diff_k"][b, 0]
        m["ck2"] = np.ascontiguousarray(dk.transpose(2, 3, 1, 0).reshape(128, 8, 512))
        m["cv2"] = np.ascontiguousarray(g["cache_diff_v"][b, 0].reshape(512, 1024))
        m.update(_host_ssd_core(g, b))
        in_maps.append(m)
    import os
    ncores = int(os.environ.get("KCORES", "8"))
    res = run_bass_kernel_spmd(nc, in_maps[:ncores], core_ids=list(range(ncores)))
    R = list(res.results) + [res.results[0]] * (8 - ncores)
    y_prompt = np.zeros((16, 256, 1024), f32)
    y_sample = np.zeros((8, 4096, 1024), f32)
    win_k = np.zeros((16, 1, 256, 4, 64), f32)
    win_v = np.zeros((16, 1, 256, 4, 64), f32)
    gla_f = np.zeros((16, 1, 4, 128, 256), f32)
    gla_b = np.zeros((16, 1, 4, 128, 256), f32)
    diff_k = np.zeros((16, 1, 256, 8, 2, 64), f32)
    diff_v = np.zeros((16, 1, 256, 8, 128), f32)
    ssd_f = np.zeros((16, 1, 32, 64, 128), f32)
    ssd_b = np.zeros((16, 1, 32, 64, 128), f32)
    for b in range(8):
        r = R[b]
        y = r["yT"].transpose(1, 0, 2).reshape(1024, T).T
        y_sample[b] = y[0:4096]
        y_prompt[2 * b:2 * b + 2] = y[4096:].reshape(2, 256, 1024)
        wk = r["o_wk"]
        win_k[2 * b:2 * b + 2, 0] = wk.transpose(2, 0, 1).reshape(2, 256, 4, 64)
        win_v[2 * b:2 * b + 2, 0] = r["o_wv"].reshape(2, 256, 4, 64)
        gla_f[2 * b:2 * b + 2, 0] = r["o_gf"].transpose(0, 2, 1, 3)
        gla_b[2 * b:2 * b + 2, 0] = r["o_gb"].transpose(0, 2, 1, 3)
        dk = r["o_dk"]
        diff_k[2 * b:2 * b + 2, 0] = dk.transpose(2, 0, 1).reshape(2, 256, 8, 2, 64)
        diff_v[2 * b:2 * b + 2, 0] = r["o_dv"].reshape(2, 256, 8, 128)
        _host_ssd_out(r, b, ssd_f, ssd_b)
    return (y_prompt, y_sample, win_k, win_v, gla_f, gla_b, diff_k, diff_v, ssd_f, ssd_b)


def _host_ssd_common(g):
    w = g["ssd_w_in"][0]
    out = {}
    out["w3_z"] = _fm(w[:, 0:2048])
    out["w3_xbc"] = _fm(w[:, 2048:5120])
    out["w3_dt"] = _fm(w[:, 5120:5184])
    out["w3_o"] = _fm(g["ssd_w_out"][0])
    out["ssd_cw"] = np.ascontiguousarray(g["ssd_conv_w"][0].reshape(3, 24, 128).transpose(2, 0, 1))
    out["ssd_cb"] = _pc(g["ssd_conv_b"][0])
    out["ssd_vec"] = np.ascontiguousarray(np.stack([g["ssd_a_log_f"][0], g["ssd_a_log_b"][0], g["ssd_dt_bias_f"][0], g["ssd_dt_bias_b"][0], g["ssd_d"][0]]))
    out["ssd_gn"] = np.ascontiguousarray(g["ssd_norm"][0])
    return out


def _host_ssd_core(g, b):
    return {"s3f": np.ascontiguousarray(g["state_ssd_fwd"][b, 0].transpose(2, 0, 1).reshape(128, 2048)),
            "s3b": np.ascontiguousarray(g["state_ssd_bwd"][b, 0].transpose(2, 0, 1).reshape(128, 2048))}


def _host_ssd_out(r, b, ssd_f, ssd_b):
    ssd_f[2 * b:2 * b + 2, 0] = r["o_sf"].reshape(2, 128, 32, 64).transpose(0, 2, 3, 1)
    ssd_b[2 * b:2 * b + 2, 0] = r["o_sb"].reshape(2, 128, 32, 64).transpose(0, 2, 3, 1)
```

```python
import numpy as np
import concourse.bass as bass
import concourse.mybir as mybir
from concourse.bass_utils import run_bass_kernel_spmd

F32 = mybir.dt.float32
BF16 = mybir.dt.bfloat16
AF = mybir.ActivationFunctionType
ALU = mybir.AluOpType
AX = mybir.AxisListType
AP = bass.AP

SB_BASE = 16512
SB_TOP = 229344
EPOCH = 30000


class Buf:
    __slots__ = ("w", "rs", "name")

    def __init__(self, name=""):
        self.w = None
        self.rs = []
        self.name = name


class Op:
    __slots__ = ("eng", "fn", "deps", "dma", "ms", "sem", "val", "need", "prev")

    def __init__(self, eng, fn, dma):
        self.eng = eng
        self.fn = fn
        self.dma = dma
        self.deps = []
        self.ms = None
        self.sem = None
        self.val = None
        self.need = False
        self.prev = None


class Sched:
    ENGS = ("pe", "act", "dve", "pool", "sp")

    def __init__(self, nc):
        self.nc = nc
        self.ops = {e: [] for e in self.ENGS}
        self.dmas_since_barrier = []
        self.all_dmas = []
        self.nps = 0
        self.nacc = 0
        self.psum = []
        for i in range(8):
            t = nc.alloc_psum_tensor("psb%d" % i, [128, 512], F32)
            self.psum.append((t, Buf("ps%d" % i)))
        self.sb_off = SB_BASE
        self.sb_mark = SB_BASE
        self.nalloc = 0

    def alloc(self, shape, dtype, name=None):
        nbytes = int(np.prod(shape[1:])) * (4 if dtype == F32 else 2)
        nbytes = (nbytes + 63) // 64 * 64
        off = self.sb_off
        assert off + nbytes <= SB_TOP, "SBUF overflow %d" % (off + nbytes - SB_TOP)
        self.sb_off += nbytes
        self.nalloc += 1
        t = self.nc.alloc_sbuf_tensor_at("sb%d_%s" % (self.nalloc, name or "t"), list(shape), dtype, offset=off)
        return t

    def mark(self):
        self.sb_mark = self.sb_off

    def release(self):
        self.sb_off = self.sb_mark

    def ps(self, pool="a"):
        if pool == "a":
            t, b = self.psum[self.nps % 6]
            self.nps += 1
        else:
            t, b = self.psum[6 + self.nacc % 2]
            self.nacc += 1
        return t, b

    def op(self, eng, fn, reads=(), writes=(), dma=False):
        o = Op(eng, fn, dma)
        deps = {}
        for b in reads:
            d = b.w
            if d is not None:
                if (not dma) and (not d.dma) and d.eng == eng and eng == "pe":
                    continue
                deps[id(d)] = d
        for b in writes:
            cand = list(b.rs)
            if b.w is not None:
                cand.append(b.w)
            for d in cand:
                if d is o:
                    continue
                if (not dma) and (not d.dma) and d.eng == eng:
                    continue
                deps[id(d)] = d
        o.deps = list(deps.values())
        for b in reads:
            if not dma:
                b.rs = [r for r in b.rs if r.dma or r.eng != eng]
            b.rs.append(o)
        for b in writes:
            b.w = o
            b.rs = []
        self.ops[eng].append(o)
        if dma:
            self.dmas_since_barrier.append(o)
            self.all_dmas.append(o)
        return o

    def barrier(self):
        lasts = []
        for e in self.ENGS:
            for o in reversed(self.ops[e]):
                if not o.dma and o.fn is not None:
                    lasts.append(o)
                    break
        deps = lasts + self.dmas_since_barrier
        self.dmas_since_barrier = []
        for e in self.ENGS:
            o = Op(e, None, False)
            o.deps = list(deps)
            self.ops[e].append(o)

    def dma(self, out, in_, reads=(), writes=(), eng=None):
        if eng is None:
            eng = "pool" if type(out.tensor).__name__.startswith("DRam") else "sp"
        return self.op(eng, lambda e: e.dma_start(out=out, in_=in_), reads, writes, dma=True)

    def mm(self, out, lhsT, rhs, start, stop, reads=(), writes=()):
        return self.op("pe", lambda e: e.matmul(out, lhsT, rhs, start=start, stop=stop), reads, writes)

    def emit(self):
        nc = self.nc
        for e in self.ENGS:
            for o in self.ops[e]:
                for d in o.deps:
                    d.need = True
        sem_ctx = []
        import contextlib
        with contextlib.ExitStack() as st:
            tl = {}
            for e in ("pe", "act", "dve", "pool"):
                tl[e] = [st.enter_context(nc.semaphore("tl_%s_%d" % (e, i))) for i in range(5)]
            npool = {"sp": 36, "pool": 36, "act": 8}
            dpool = {e: [st.enter_context(nc.semaphore("dq_%s_%d" % (e, i))) for i in range(n)] for e, n in npool.items()}
            for e in self.ENGS:
                m = 0
                k = 0
                for o in self.ops[e]:
                    if o.fn is None:
                        continue
                    if o.dma:
                        P = len(dpool[e])
                        o.sem = dpool[e][k % P]
                        o.val = 16 * (k // P + 1)
                        k += 1
                    elif o.need:
                        o.sem = tl[e][m // EPOCH]
                        o.val = m % EPOCH + 1
                        m += 1
                assert m < EPOCH * 5, (e, m)
            final = {}
            for e, n in npool.items():
                for o in self.ops[e]:
                    if o.dma:
                        final[id(o.sem)] = (o.sem, o.val)

            def stream(e, eng):
                seen = {}

                def wait(sem, val):
                    if seen.get(id(sem), 0) < val:
                        eng.wait_ge(sem, val)
                        seen[id(sem)] = val

                for o in self.ops[e]:
                    for d in o.deps:
                        wait(d.sem, d.val)
                    if o.fn is None:
                        continue
                    if o.dma and o.val > 16:
                        wait(o.sem, o.val - 16)
                    ins = o.fn(eng)
                    if o.dma:
                        ins.then_inc(o.sem, 16)
                    elif o.need:
                        ins.then_inc(o.sem, 1)
                if e == "sp":
                    for sem, val in final.values():
                        wait(sem, val)

            with nc.Block() as block:
                @block.tensor
                def _(eng):
                    stream("pe", eng)

                @block.scalar
                def _(eng):
                    stream("act", eng)

                @block.vector
                def _(eng):
                    stream("dve", eng)

                @block.gpsimd
                def _(eng):
                    stream("pool", eng)

                @block.sync
                def _(eng):
                    stream("sp", eng)
        return {e: len(v) for e, v in self.ops.items()}


def ins(method, *a, **kw):
    return lambda e: getattr(e, method)(*a, **kw)


T = 4608
LS = 4096
TILES = [(i * 512, 512, 0, 0, 4096) for i in range(8)] + [(4096, 256, 1, 4096, 4352), (4352, 256, 1, 4352, 4608)]
EPS = 1e-6
DFF = 2816
LAM_INIT = 0.8 - 0.6 * float(np.exp(-0.3 * 2))


def build_program(NLAYERS=4, dbg=False):
    nc = bass.Bass("TRN2", target_bir_lowering=False)
    s = Sched(nc)
    I = {}
    O = {}

    def din(name, shape):
        I[name] = nc.dram_tensor(name, list(shape), F32, kind="ExternalInput").ap()
        return I[name]

    def dout(name, shape):
        O[name] = nc.dram_tensor(name, list(shape), F32, kind="ExternalOutput").ap()
        return O[name]

    def dscr(name, shape, dt):
        return nc.dram_tensor(name, list(shape), dt).ap()

    xT = din("xT", [128, 8, T])
    condT = din("condT", [128, 8, 2])
    ada_w = din("ada_w", [4, 128, 8, 6144])
    ada_b = din("ada_b", [128, 4, 48])
    nrm = din("nrm", [128, 4, 2, 8])
    fnorm = din("fnorm", [128, 8])
    w_up = din("w_up", [4, 128, 8, 5632])
    w_dn = din("w_dn", [4, 128, 22, 1024])
    ffn_cw = din("ffn_cw", [128, 4, 3, 44])
    ffn_cb = din("ffn_cb", [128, 4, 44])
    perm_in = din("perm", [128, 128])
    rcos = din("rcos", [128, LS])
    rsin = din("rsin", [128, LS])
    wmask_in = din("wmask", [128, 6, 512])
    tri_in = din("tri", [128, 4, 128])
    m64_in = din("m64", [128, 512])
    w0_qk = din("w0_qk", [128, 8, 1536])
    w0_v = din("w0_v", [128, 8, 256])
    w0_o = din("w0_o", [128, 8, 1024])
    sink_in = din("sink", [16])
    ck0 = din("ck0", [128, 4, 512])
    cv0 = din("cv0", [512, 256])
    w1_qk = din("w1_qk", [128, 8, 1024])
    w1_v = din("w1_v", [128, 8, 1024])
    w1_r = din("w1_r", [128, 8, 1024])
    w1_g1 = din("w1_g1", [128, 8, 32])
    w1_g2 = din("w1_g2", [32, 1024])
    w1_gb = din("w1_gb", [128, 8])
    w1_gn = din("w1_gn", [128, 2])
    w1_o = din("w1_o", [128, 8, 1024])
    s1f = din("s1f", [128, 4, 256])
    s1b = din("s1b", [128, 4, 256])
    w2_qk = din("w2_qk", [128, 8, 2048])
    w2_v = din("w2_v", [128, 8, 1024])
    w2_o = din("w2_o", [128, 8, 1024])
    lqk = din("lqk", [4, 64])
    w2_gn = din("w2_gn", [128, 1])
    ck2 = din("ck2", [128, 8, 512])
    cv2 = din("cv2", [512, 1024])

    w3_z = din("w3_z", [128, 8, 2048])
    w3_xbc = din("w3_xbc", [128, 8, 3072])
    w3_dt = din("w3_dt", [128, 8, 64])
    w3_o = din("w3_o", [128, 16, 1024])
    ssd_cw = din("ssd_cw", [128, 3, 24])
    ssd_cb = din("ssd_cb", [128, 24])
    ssd_vec = din("ssd_vec", [5, 32])
    ssd_gn = din("ssd_gn", [2048])
    s3f = din("s3f", [128, 2048])
    s3b = din("s3b", [128, 2048])

    yT = dout("yT", [128, 8, T])
    o_sf = dout("o_sf", [2, 128, 2048])
    o_sb = dout("o_sb", [2, 128, 2048])
    o_wk = dout("o_wk", [4, 64, 512])
    o_wv = dout("o_wv", [512, 256])
    o_gf = dout("o_gf", [2, 128, 4, 256])
    o_gb = dout("o_gb", [2, 128, 4, 256])
    o_dk = dout("o_dk", [8, 128, 512])
    o_dv = dout("o_dv", [512, 1024])

    X = dscr("X", [128, 8, T], F32)
    U = dscr("U", [128, 44, T], BF16)
    QK = dscr("QK", [128, 16, T], BF16)
    VS = dscr("VS", [T, 1024], BF16)
    OS = dscr("OS", [128, 8, T], BF16)
    OS2 = dscr("OS2", [128, 8, T], BF16)
    RS = dscr("RS", [128, 8, T], BF16)
    QD = dscr("QD", [2, 128, 4, T], BF16)
    KD = dscr("KD", [2, 128, 4, T], BF16)
    KR = dscr("KR", [T, 8, 128], BF16)
    ZS = dscr("ZS", [T, 2048], BF16)
    DTS = dscr("DTS", [T, 128], F32)
    XTs = dscr("XTs", [T, 2048], BF16)
    BTs = dscr("BTs", [T, 512], BF16)
    YD = [dscr("YD0", [T, 2048], BF16), dscr("YD1", [T, 2048], BF16)]

    NT = len(TILES)

    def rows_view(dr, rowlen, r0, nb, c0, w):
        return AP(dr.tensor, r0 * rowlen + c0, [[rowlen, 128], [128 * rowlen, nb], [1, w]])

    def kr_view(r0, nb, j0, nj):
        return AP(KR.tensor, r0 * 1024 + j0 * 128, [[1024, 128], [128 * 1024, nb], [128, nj], [1, 128]])
    import os
    KQ = os.environ.get("KQ", "")

    def tb(name):
        return [Buf(name + str(i)) for i in range(NT)]

    xT_b = tb("xT")
    X_b = tb("X")

    MODS = s.alloc([128, 4, 6, 8, 2], F32, "mods")
    GS = s.alloc([128, 4, 2, 8, 2], F32, "gs")
    NRM = s.alloc([128, 4, 2, 8], F32, "nrm")
    FNRM = s.alloc([128, 8], F32, "fnrm")
    ones_bf = s.alloc([128, 128], BF16, "ones")
    ones_f = s.alloc([128, 128], F32, "onesf")
    ident_bf = s.alloc([128, 128], BF16, "ident")
    perm_bf = s.alloc([128, 128], BF16, "perm")
    tri_bf = s.alloc([128, 4, 128], BF16, "tri")
    m64 = s.alloc([128, 512], F32, "m64")
    cb_const = Buf("consts")
    stage = s.alloc([128, 4, 128], F32, "stage")
    bstage = Buf()

    s.dma(NRM[:], nrm, writes=[cb_const])
    s.dma(FNRM[:], fnorm, writes=[cb_const])
    s.dma(m64[:], m64_in, writes=[cb_const])
    s.op("pool", ins("memset", ones_f[:], 1.0), writes=[cb_const])
    s.op("dve", ins("tensor_copy", out=ones_bf[:], in_=ones_f[:]), reads=[cb_const], writes=[cb_const])
    s.dma(stage[:, 0, :], perm_in, writes=[bstage])
    s.op("dve", ins("tensor_copy", out=perm_bf[:], in_=stage[:, 0, :]), reads=[bstage], writes=[cb_const])
    s.op("pool", ins("memset", stage[:, 1, :], 1.0), reads=[], writes=[bstage])
    s.op("pool", ins("affine_select", out=stage[:, 1, :], in_=stage[:, 1, :], pattern=[[-1, 128]], compare_op=ALU.is_equal, fill=0.0, base=0, channel_multiplier=1), reads=[bstage], writes=[bstage])
    s.op("dve", ins("tensor_copy", out=ident_bf[:], in_=stage[:, 1, :]), reads=[bstage], writes=[cb_const])
    s.barrier()
    s.dma(stage[:], tri_in, writes=[bstage])
    s.op("dve", ins("tensor_copy", out=tri_bf[:], in_=stage[:]), reads=[bstage], writes=[cb_const])
    s.mark()

    def mod_ap(l, k, c, cond):
        return MODS[:, l, k, c, cond:cond + 1]

    def prologue():
        s.barrier()
        s.release()
        sc = s.alloc([128, 8, 2], F32, "sc")
        scb = Buf()
        adb = s.alloc([128, 4, 48], F32, "adb")
        adbb = Buf()
        s.dma(sc[:], condT, writes=[scb])
        s.dma(adb[:], ada_b, writes=[adbb])
        s.op("act", ins("activation", out=sc[:], in_=sc[:], func=AF.Silu), reads=[scb], writes=[scb])
        wb = [s.alloc([128, 8, 1024], F32, "adaw%d" % i) for i in range(2)]
        wbb = [[Buf() for _ in range(2)] for _ in range(2)]
        it = 0
        for l in range(NLAYERS):
            for g in range(6):
                w = wb[it % 2]
                bb = wbb[it % 2]
                for hh in range(2):
                    s.dma(w[:, hh * 4:(hh + 1) * 4, :], ada_w[l, :, hh * 4:(hh + 1) * 4, g * 1024:(g + 1) * 1024], writes=[bb[hh]], eng=("sp" if hh == 0 else "act"))
                ps, pb = s.ps()
                for cc in range(8):
                    for kc in range(8):
                        s.mm(ps[:, cc * 2:cc * 2 + 2], w[:, kc, cc * 128:(cc + 1) * 128], sc[:, kc, :], kc == 0, kc == 7, reads=[bb[kc // 4], scb], writes=[pb])
                s.op("dve", ins("tensor_tensor",
                    out=MODS[:, l, g, :, :], in0=ps[:, 0:16].rearrange("p (c t) -> p c t", t=2),
                    in1=adb[:, l, g * 8:(g + 1) * 8].unsqueeze(2).to_broadcast([128, 8, 2]), op=ALU.add),
                    reads=[pb, adbb], writes=[cb_const])
                it += 1
            for which in range(2):
                k = 1 if which == 0 else 4
                s.op("dve", ins("scalar_tensor_tensor",
                    out=GS[:, l, which, :, :], in0=MODS[:, l, k, :, :], scalar=1.0,
                    in1=NRM[:, l, which, :].unsqueeze(2).to_broadcast([128, 8, 2]), op0=ALU.add, op1=ALU.mult),
                    reads=[cb_const], writes=[cb_const])

    def load_w(dst, src, nk, split=1):
        bufs = []
        for kc in range(nk):
            b = Buf()
            s.dma(dst[:, kc, :], src[:, kc, :], writes=[b], eng="pool")
            bufs.append(b)
        return bufs

    class NormCtx:
        def __init__(self, nbuf=2):
            self.nbuf = nbuf
            self.xt = [(s.alloc([128, 8, 512], F32, "nxt"), Buf()) for _ in range(nbuf)]
            self.h = [(s.alloc([128, 8, 512], BF16, "nh"), Buf()) for _ in range(nbuf)]
            self.sq = (s.alloc([128, 8, 512], BF16, "nsq"), Buf())
            self.tmp = (s.alloc([128, 8, 512], F32, "ntmp"), Buf())
            self.rstd = [(s.alloc([128, 512], F32, "nrs"), Buf()) for _ in range(2)]
            self.k = 0

    def norm_tile(ctx, l, which, ti, Xsrc, Xsrc_b, final_out=None):
        t0, n, cond, _, _ = TILES[ti]
        k = ctx.k
        ctx.k += 1
        xt, xtb = ctx.xt[k % ctx.nbuf]
        h, hb = ctx.h[k % ctx.nbuf]
        sq, sqb = ctx.sq
        tmp, tmpb = ctx.tmp
        rstd, rb = ctx.rstd[k % 2]
        s.dma(xt[:, :, :n], Xsrc[:, :, t0:t0 + n], reads=[Xsrc_b[ti]], writes=[xtb])
        s.op("act", ins("activation", out=sq[:, :, :n], in_=xt[:, :, :n], func=AF.Square), reads=[xtb], writes=[sqb])
        ps, pb = s.ps()
        for c in range(8):
            s.mm(ps[:, :n], ones_bf[:], sq[:, c, :n], c == 0, c == 7, reads=[sqb, cb_const], writes=[pb])
        s.op("act", ins("activation", out=rstd[:, :n], in_=ps[:, :n], func=AF.Sqrt, scale=1.0 / 1024, bias=EPSB[:, 0:1]), reads=[pb], writes=[rb])
        s.op("dve", ins("reciprocal", out=rstd[:, :n], in_=rstd[:, :n]), reads=[rb], writes=[rb])
        s.op("dve", ins("tensor_tensor", out=tmp[:, :, :n], in0=xt[:, :, :n], in1=rstd[:, :n].unsqueeze(1).to_broadcast([128, 8, n]), op=ALU.mult),
             reads=[xtb, rb], writes=[tmpb])
        if final_out is None:
            for c in range(8):
                s.op("act", ins("activation", out=h[:, c, :n], in_=tmp[:, c, :n], func=AF.Identity,
                                                          scale=GS[:, l, which, c, cond:cond + 1], bias=mod_ap(l, 0 if which == 0 else 3, c, cond)),
                     reads=[tmpb, cb_const], writes=[hb])
            return h, hb
        else:
            for c in range(8):
                s.op("act", ins("activation", out=xt[:, c, :n], in_=tmp[:, c, :n], func=AF.Identity, scale=FNRM[:, c:c + 1]),
                     reads=[tmpb, cb_const], writes=[xtb])
            s.dma(final_out[:, :, t0:t0 + n], xt[:, :, :n], reads=[xtb])
            return None, None

    EPSB = s.alloc([128, 1], F32, "epsb")
    s.op("pool", ins("memset", EPSB[:], EPS), writes=[cb_const])
    s.mark()

    class ResCtx:
        def __init__(self):
            self.xo = [(s.alloc([128, 8, 512], F32, "xo"), Buf()) for _ in range(2)]
            self.k = 0

    def outproj_resid(rctx, l, gate_k, ti, W, wbufs, nk, rhs, rhsbufs, Xsrc, Xsrc_b, Xdst, Xdst_b):
        t0, n, cond, _, _ = TILES[ti]
        xo, xob = rctx.xo[rctx.k % 2]
        rctx.k += 1
        s.dma(xo[:, :, :n], Xsrc[:, :, t0:t0 + n], reads=[Xsrc_b[ti]], writes=[xob])
        for dc in range(8):
            ps, pb = s.ps()
            for kc in range(nk):
                s.mm(ps[:, :n], W[:, kc, dc * 128:(dc + 1) * 128], rhs[:, kc, :n], kc == 0, kc == nk - 1,
                     reads=[wbufs[kc]] + list(rhsbufs), writes=[pb])
            s.op("dve", ins("scalar_tensor_tensor", out=xo[:, dc, :n], in0=ps[:, :n], scalar=mod_ap(l, gate_k, dc, cond),
                                                                      in1=xo[:, dc, :n], op0=ALU.mult, op1=ALU.add),
                 reads=[pb, xob, cb_const], writes=[xob])
        s.dma(Xdst[:, :, t0:t0 + n], xo[:, :, :n], reads=[xob], writes=[Xdst_b[ti]])

    def ffn(l, Xsrc, Xsrc_b):
        s.barrier()
        s.release()
        Wup = s.alloc([128, 8, 5632], BF16, "wup")
        wb = load_w(Wup, w_up[l], 8)
        nctx = NormCtx()
        ub = [(s.alloc([128, 11, 512], BF16, "ub"), [Buf() for _ in range(11)]) for _ in range(2)]
        U_b = [[Buf() for _ in range(4)] for _ in range(NT)]
        ku = 0
        ev = 0
        for ti in range(NT):
            t0, n, cond, _, _ = TILES[ti]
            h, hb = norm_tile(nctx, l, 1, ti, Xsrc, Xsrc_b)
            for grp in range(4):
                ut, utb = ub[ku % 2]
                ku += 1
                for cc in range(11):
                    col = grp * 11 + cc
                    ps, pb = s.ps()
                    for kc in range(8):
                        s.mm(ps[:, :n], Wup[:, kc, col * 128:(col + 1) * 128], h[:, kc, :n], kc == 0, kc == 7, reads=[wb[kc], hb], writes=[pb])
                    if ev % 2 == 0:
                        s.op("act", ins("activation", out=ut[:, cc, :n], in_=ps[:, :n], func=AF.Identity), reads=[pb], writes=[utb[cc]])
                    else:
                        s.op("dve", ins("tensor_copy", out=ut[:, cc, :n], in_=ps[:, :n]), reads=[pb], writes=[utb[cc]])
                    ev += 1
                s.dma(U[:, grp * 11:(grp + 1) * 11, t0:t0 + n], ut[:, :, :n], reads=utb, writes=[U_b[ti][grp]])
        s.barrier()
        s.release()
        Wd = s.alloc([128, 22, 1024], BF16, "wd")
        wdb = load_w(Wd, w_dn[l], 22)
        cw = s.alloc([128, 3, 44], F32, "cw")
        cbias = s.alloc([128, 44], F32, "cbias")
        cwb = Buf()
        s.dma(cw[:], ffn_cw[:, l, :, :], writes=[cwb])
        s.dma(cbias[:], ffn_cb[:, l, :], writes=[cwb])
        ug = s.alloc([128, 44, 514], BF16, "ug")
        ugb = [Buf() for _ in range(4)]
        at = [(s.alloc([128, 22, 512], BF16, "at"), Buf()) for _ in range(2)]
        tmps = [[(s.alloc([128, 512], F32, "ft"), Buf()) for _ in range(3)] for _ in range(2)]
        rctx = ResCtx()
        kp = 0
        for ti in range(NT):
            t0, n, cond, s0, s1 = TILES[ti]
            lo = (t0 - 1) >= s0
            hi = (t0 + n) < s1
            for grp in range(4):
                gs_ = slice(grp * 11, (grp + 1) * 11)
                if not lo:
                    s.op("pool", ins("memset", ug[:, gs_, 0:1], 0.0), writes=[ugb[grp]])
                if not hi:
                    s.op("pool", ins("memset", ug[:, gs_, n + 1:n + 2], 0.0), writes=[ugb[grp]])
                c0 = 0 if lo else 1
                c1 = n + 2 if hi else n + 1
                rd = [U_b[ti][grp]]
                if lo:
                    rd.append(U_b[ti - 1][grp])
                if hi:
                    rd.append(U_b[ti + 1][grp])
                s.dma(ug[:, gs_, c0:c1], U[:, gs_, t0 - 1 + c0:t0 - 1 + c1], reads=rd, writes=[ugb[grp]])
            a, ab = at[ti % 2]
            for cc in range(22):
                res = []
                for half in range(2):
                    col = cc + 22 * half
                    tt, ttb = tmps[kp % 2][half]
                    grp = col // 11
                    s.op("act", ins("activation", out=tt[:, :n], in_=ug[:, col, 1:n + 1], func=AF.Identity,
                                                                      scale=cw[:, 1, col:col + 1], bias=cbias[:, col:col + 1]),
                         reads=[ugb[grp], cwb], writes=[ttb])
                    s.op("dve", ins("scalar_tensor_tensor", out=tt[:, :n], in0=ug[:, col, 0:n], scalar=cw[:, 0, col:col + 1],
                                                                               in1=tt[:, :n], op0=ALU.mult, op1=ALU.add),
                         reads=[ugb[grp], cwb, ttb], writes=[ttb])
                    s.op("dve", ins("scalar_tensor_tensor", out=tt[:, :n], in0=ug[:, col, 2:n + 2], scalar=cw[:, 2, col:col + 1],
                                                                               in1=tt[:, :n], op0=ALU.mult, op1=ALU.add),
                         reads=[ugb[grp], cwb, ttb], writes=[ttb])
                    res.append((tt, ttb))
                sg, sgb = tmps[kp % 2][2]
                kp += 1
                s.op("act", ins("activation", out=sg[:, :n], in_=res[0][0][:, :n], func=AF.Silu), reads=[res[0][1]], writes=[sgb])
                s.op("dve", ins("tensor_tensor", out=a[:, cc, :n], in0=sg[:, :n], in1=res[1][0][:, :n], op=ALU.mult),
                     reads=[sgb, res[1][1]], writes=[ab])
            outproj_resid(rctx, l, 5, ti, Wd, wdb, 22, a, [ab], Xsrc, Xsrc_b, X, X_b)

    def qkv_phase(l, Wqk_src, nqk, Wv_src, nv, Xsrc, Xsrc_b, rope, kout=None, vout=None, post=None):
        s.barrier()
        s.release()
        Wqk = s.alloc([128, 8, nqk * 128], BF16, "wqk")
        wqb = load_w(Wqk, Wqk_src, 8)
        Wv = s.alloc([128, 8, nv], BF16, "wv")
        wvb = load_w(Wv, Wv_src, 8)
        nctx = NormCtx()
        qk = [(s.alloc([128, nqk, 512], BF16, "qk"), [Buf() for _ in range(nqk)]) for _ in range(2)]
        vt = [(s.alloc([128, 4, nv], BF16, "vt"), Buf()) for _ in range(2)]
        qb = [(s.alloc([128, 512], BF16, "qb"), Buf()) for _ in range(2)]
        t12 = [[(s.alloc([128, 512], F32, "rt"), Buf()) for _ in range(2)] for _ in range(2)]
        cs = [(s.alloc([128, 2, 512], F32, "cs"), Buf()) for _ in range(2)]
        kf = [(s.alloc([128, 256], F32, "kf"), Buf()) for _ in range(2)]
        vf = [(s.alloc([128, 512], F32, "vf"), Buf()) for _ in range(2)]
        QK_b = [Buf() for _ in range(NT)]
        VS_b = [Buf() for _ in range(NT)]
        if "L" in KQ:
            return
        kq = 0
        kk = 0
        kv = 0
        for ti in range(NT):
            t0, n, cond, s0, s1 = TILES[ti]
            h, hb = norm_tile(nctx, l, 0, ti, Xsrc, Xsrc_b)
            if "N" in KQ:
                continue
            qkt, qkb = qk[ti % 2]
            dorope = rope and cond == 0 and ("r" not in KQ)
            if dorope:
                cst, csb = cs[ti % 2]
                s.dma(cst[:, 0, :], rcos[:, t0:t0 + n], writes=[csb])
                s.dma(cst[:, 1, :], rsin[:, t0:t0 + n], writes=[csb])
            for cc in range(nqk):
                ps, pb = s.ps()
                for kc in range(8):
                    s.mm(ps[:, :n], Wqk[:, kc, cc * 128:(cc + 1) * 128], h[:, kc, :n], kc == 0, kc == 7, reads=[wqb[kc], hb], writes=[pb])
                if dorope:
                    q_, q_b = qb[kq % 2]
                    t1, t1b = t12[kq % 2][0]
                    t2, t2b = t12[kq % 2][1]
                    kq += 1
                    s.op("act", ins("activation", out=q_[:, :n], in_=ps[:, :n], func=AF.Identity), reads=[pb], writes=[q_b])
                    ps2, pb2 = s.ps()
                    s.mm(ps2[:, :n], perm_bf[:], q_[:, :n], True, True, reads=[q_b, cb_const], writes=[pb2])
                    s.op("dve", ins("tensor_tensor", out=t1[:, :n], in0=q_[:, :n], in1=cst[:, 0, :n], op=ALU.mult),
                         reads=[q_b, csb], writes=[t1b])
                    s.op("dve", ins("tensor_tensor", out=t2[:, :n], in0=ps2[:, :n], in1=cst[:, 1, :n], op=ALU.mult),
                         reads=[pb2, csb], writes=[t2b])
                    s.op("pool", ins("tensor_tensor", out=qkt[:, cc, :n], in0=t1[:, :n], in1=t2[:, :n], op=ALU.add),
                         reads=[t1b, t2b], writes=[qkb[cc]])
                else:
                    if kout is not None and cond == 1 and cc >= kout[0]:
                        kft, kfb = kf[kk % 2]
                        kk += 1
                        rows = kout[2]
                        s.op("dve", ins("tensor_copy", out=kft[:, :n], in_=ps[:, :n]), reads=[pb], writes=[kfb])
                        s.op("act", ins("activation", out=qkt[:, cc, :n], in_=kft[:, :n], func=AF.Identity), reads=[kfb], writes=[qkb[cc]])
                        s.dma(kout[1][cc - kout[0], :, t0 - LS:t0 - LS + n], kft[0:rows, :n], reads=[kfb])
                    else:
                        s.op("act", ins("activation", out=qkt[:, cc, :n], in_=ps[:, :n], func=AF.Identity), reads=[pb], writes=[qkb[cc]])
                if post is not None:
                    post(ti, cc, ps, pb)
            vtt, vtb = vt[ti % 2]
            for tbk in range(n // 128 if "v" not in KQ else 0):
                for vg in range((nv + 511) // 512):
                    w_ = min(512, nv - vg * 512)
                    ps, pb = s.ps()
                    for kc in range(8):
                        s.mm(ps[:, :w_], h[:, kc, tbk * 128:(tbk + 1) * 128], Wv[:, kc, vg * 512:vg * 512 + w_], kc == 0, kc == 7, reads=[wvb[kc], hb], writes=[pb])
                    if vout is not None and cond == 1:
                        vft, vfb = vf[kv % 2]
                        kv += 1
                        s.op("dve", ins("tensor_copy", out=vft[:, :w_], in_=ps[:, :w_]), reads=[pb], writes=[vfb])
                        s.op("act", ins("activation", out=vtt[:, tbk, vg * 512:vg * 512 + w_], in_=vft[:, :w_], func=AF.Identity),
                             reads=[vfb], writes=[vtb])
                        r0 = t0 - LS + tbk * 128
                        s.dma(vout[r0:r0 + 128, vg * 512:vg * 512 + w_], vft[:, :w_], reads=[vfb])
                    else:
                        s.op("act", ins("activation", out=vtt[:, tbk, vg * 512:vg * 512 + w_], in_=ps[:, :w_], func=AF.Identity),
                             reads=[pb], writes=[vtb])
            if "q" not in KQ:
                s.dma(QK[:, 0:nqk, t0:t0 + n], qkt[:, :, :n], reads=qkb, writes=[QK_b[ti]])
            if "v" not in KQ and "s" not in KQ:
              s.dma(rows_view(VS, 1024, t0, n // 128, 0, nv), vtt[:, 0:n // 128, :], reads=[vtb], writes=[VS_b[ti]])

    def attn_phase(l, kind, Xsrc, Xsrc_b):
        s.barrier()
        s.release()
        win = kind == "win"
        nunits = 8
        kbase = 8
        Kt = [(s.alloc([128, T], BF16, "kt"), Buf()) for _ in range(2)]
        Va = [(s.alloc([128, 36, 128], BF16, "va"), Buf()) for _ in range(2)]
        Qt = [(s.alloc([128, LS], BF16, "qt"), Buf()) for _ in range(2)]
        Ot = [(s.alloc([128, LS], BF16, "ot"), Buf()) for _ in range(2)]
        pT = [(s.alloc([128, 512], BF16, "pT"), Buf()) for _ in range(8)]
        rec = [(s.alloc([128, 512], F32, "rec"), Buf()) for _ in range(4)]
        OS_b = [[Buf() for _ in range(nunits)] for _ in range(3)]
        cbuf = Buf()
        if win:
            masks = s.alloc([128, 6, 512], BF16, "masks")
            s.dma(masks[:], wmask_in, writes=[cbuf], eng="pool")
            esink = s.alloc([128, 16], F32, "esink")
            s.dma(esink[:], sink_in.partition_broadcast(128), writes=[cbuf])
            s.op("act", ins("activation", out=esink[:], in_=esink[:], func=AF.Exp), reads=[cbuf], writes=[cbuf])
            for i in range(2):
                s.op("pool", ins("memset", Va[i][0][:, :, 64:128], 1.0), writes=[Va[i][1]])
        else:
            acc = [(s.alloc([128, 512], F32, "acc"), Buf()) for _ in range(4)]
            osb = [(s.alloc([128, 512], F32, "osb"), Buf()) for _ in range(4)]
            sqd = [(s.alloc([128, 512], BF16, "sqd"), Buf()) for _ in range(2)]
            lq = s.alloc([128, 4, 64], F32, "lq")
            lam = s.alloc([128, 4], F32, "lam")
            gsub = s.alloc([128, 1], F32, "gsub")
            s.dma(lq[:], lqk.rearrange("a b -> (a b)").partition_broadcast(128).rearrange("p (a b) -> p a b", a=4), writes=[cbuf])
            s.dma(gsub[:], w2_gn, writes=[cbuf])
            s.op("dve", ins("tensor_tensor", out=lq[:, 0, :], in0=lq[:, 0, :], in1=lq[:, 1, :], op=ALU.mult), reads=[cbuf], writes=[cbuf])
            s.op("dve", ins("tensor_tensor", out=lq[:, 2, :], in0=lq[:, 2, :], in1=lq[:, 3, :], op=ALU.mult), reads=[cbuf], writes=[cbuf])
            s.op("dve", ins("reduce_sum", out=lam[:, 0:1], in_=lq[:, 0, :], axis=AX.X), reads=[cbuf], writes=[cbuf])
            s.op("dve", ins("reduce_sum", out=lam[:, 1:2], in_=lq[:, 2, :], axis=AX.X), reads=[cbuf], writes=[cbuf])
            s.op("act", ins("activation", out=lam[:, 0:2], in_=lam[:, 0:2], func=AF.Exp), reads=[cbuf], writes=[cbuf])
            s.op("dve", ins("tensor_tensor", out=lam[:, 2:3], in0=lam[:, 1:2], in1=lam[:, 0:1], op=ALU.subtract), reads=[cbuf], writes=[cbuf])
            s.op("dve", ins("tensor_scalar", out=lam[:, 2:3], in0=lam[:, 2:3], scalar1=-LAM_INIT, scalar2=None, op0=ALU.add), reads=[cbuf], writes=[cbuf])
            s.op("dve", ins("tensor_scalar", out=lam[:, 3:4], in0=gsub[:], scalar1=1.0 - LAM_INIT, scalar2=None, op0=ALU.mult), reads=[cbuf], writes=[cbuf])
        ku = 0
        kp = 0
        kr = 0
        ka = 0
        SEQS = [(0, 4096, 0), (4096, 256, 1), (4352, 256, 2)]
        for (q0, L, si) in SEQS:
            sample = si == 0
            nkb_lat = L // 128
            for u in range(nunits):
                kt, ktb = Kt[ku % 2]
                va, vab = Va[ku % 2]
                qt, qtb = Qt[ku % 2]
                ot, otb = Ot[ku % 2]
                ku += 1
                s.dma(qt[:, 0:L], QK[:, u, q0:q0 + L], writes=[qtb])
                if win:
                    g = u // 2
                    s.dma(kt[:, 0:L], QK[:, kbase + g, q0:q0 + L], writes=[ktb])
                    s.dma(va[:, 0:nkb_lat, 0:64], rows_view(VS, 1024, q0, nkb_lat, g * 64, 64), writes=[vab])
                    if sample:
                        s.dma(kt[:, L:L + 512], ck0[:, g, :], writes=[ktb], eng="pool")
                        s.dma(va[:, 32:36, 0:64], rows_view(cv0, 256, 0, 4, g * 64, 64), writes=[vab], eng="pool")
                else:
                    s.dma(kt[:, 0:L], QK[:, kbase + u, q0:q0 + L], writes=[ktb])
                    s.dma(va[:, 0:nkb_lat, :], rows_view(VS, 1024, q0, nkb_lat, u * 128, 128), writes=[vab])
                    if sample:
                        s.dma(kt[:, L:L + 512], ck2[:, u, :], writes=[ktb], eng="pool")
                        s.dma(va[:, 32:36, :], rows_view(cv2, 1024, 0, 4, u * 128, 128), writes=[vab], eng="pool")
                nq = 512 if sample else 256
                LOOK = 4
                tasks = []
                for qi in range(L // nq):
                    for e_ in range(2):
                        if win and sample:
                            kbs = [(kb, kb - qi * 4 + 1) for kb in range(qi * 4 - 1, qi * 4 + 5) if 0 <= kb < 32] + [(32 + j, None) for j in range(4)]
                        elif sample:
                            kbs = [(kb, None) for kb in range(36)]
                        else:
                            kbs = [(kb, None) for kb in range(2)]
                        for i, (kb, mi) in enumerate(kbs):
                            tasks.append((qi, e_, i, kb, mi, i == len(kbs) - 1))
                stt = {}

                def stage1(tk_):
                    nonlocal kp, ka
                    qi, e_, i, kb, mi, lastb = tk_
                    qs = slice(qi * nq, (qi + 1) * nq)
                    pr = slice(e_ * 64, (e_ + 1) * 64)
                    if i == 0:
                        d_ = {}
                        d_["pso"], d_["pob"] = s.ps("b")
                        if not win:
                            d_["ac"], d_["acb"] = acc[ka % 4]
                            ka += 1
                        stt[(qi, e_)] = d_
                    d_ = stt[(qi, e_)]
                    pss, psb_ = s.ps()
                    s.mm(pss[:, :nq], kt[pr, kb * 128:(kb + 1) * 128], qt[pr, qs], True, True, reads=[ktb, qtb], writes=[psb_])
                    p_, p_b = pT[kp % 8]
                    kp += 1
                    s.op("act", ins("activation", out=p_[:, :nq], in_=pss[:, :nq], func=AF.Exp, scale=0.125), reads=[psb_], writes=[p_b])
                    if mi is not None:
                        s.op("pool" if (kp % 2) else "dve", ins("tensor_tensor", out=p_[:, :nq], in0=p_[:, :nq], in1=masks[:, mi, :nq], op=ALU.mult),
                             reads=[p_b, cbuf], writes=[p_b])
                    if not win:
                        eng = "pool" if e_ == 0 else "dve"
                        ac, acb = d_["ac"], d_["acb"]
                        if i == 0:
                            s.op(eng, ins("tensor_copy", out=ac[:, :nq], in_=p_[:, :nq]), reads=[p_b], writes=[acb])
                        else:
                            s.op(eng, ins("tensor_tensor", out=ac[:, :nq], in0=ac[:, :nq], in1=p_[:, :nq], op=ALU.add), reads=[p_b, acb], writes=[acb])
                    d_[("p", i)] = (p_, p_b)

                def stage2(tk_):
                    nonlocal kr
                    qi, e_, i, kb, mi, lastb = tk_
                    qs = slice(qi * nq, (qi + 1) * nq)
                    pr = slice(e_ * 64, (e_ + 1) * 64)
                    d_ = stt[(qi, e_)]
                    pso, pob = d_["pso"], d_["pob"]
                    p_, p_b = d_.pop(("p", i))
                    s.mm(pso[:, :nq], va[:, kb, :], p_[:, :nq], i == 0, lastb, reads=[vab, p_b], writes=[pob])
                    if not lastb:
                        return
                    if win:
                        hh = u * 2 + e_
                        r_, r_b = rec[kr % 4]
                        kr += 1
                        s.op("dve", ins("tensor_scalar", out=r_[64:128, :nq], in0=pso[64:128, :nq], scalar1=esink[64:128, hh:hh + 1], scalar2=None, op0=ALU.add),
                             reads=[pob, cbuf], writes=[r_b])
                        s.op("dve", ins("reciprocal", out=r_[64:128, :nq], in_=r_[64:128, :nq]), reads=[r_b], writes=[r_b])
                        s.op("dve", ins("tensor_tensor", out=ot[pr, qs], in0=pso[0:64, :nq], in1=r_[64:128, :nq], op=ALU.mult),
                             reads=[pob, r_b], writes=[otb])
                        del stt[(qi, e_)]
                        return
                    ac, acb = d_["ac"], d_["acb"]
                    psd, pdb = s.ps()
                    s.mm(psd[:, :nq], ones_f[:], ac[:, :nq], True, True, reads=[acb, cb_const], writes=[pdb])
                    r_, r_b = rec[kr % 4]
                    kr += 1
                    s.op("dve", ins("reciprocal", out=r_[:, :nq], in_=psd[:, :nq]), reads=[pdb], writes=[r_b])
                    o_, o_b = osb[kr % 4]
                    s.op("dve", ins("tensor_tensor", out=o_[:, :nq], in0=pso[:, :nq], in1=r_[:, :nq], op=ALU.mult), reads=[pob, r_b], writes=[o_b])
                    d_["o"] = (o_, o_b)
                    if e_ == 0:
                        return
                    (o0, o0b) = stt[(qi, 0)]["o"]
                    (o1, o1b) = d_["o"]
                    s.op("dve", ins("scalar_tensor_tensor", out=o0[:, :nq], in0=o1[:, :nq], scalar=lam[:, 2:3], in1=o0[:, :nq], op0=ALU.mult, op1=ALU.add),
                         reads=[o0b, o1b, cbuf], writes=[o0b])
                    sq_, sq_b = sqd[qi % 2]
                    s.op("act", ins("activation", out=sq_[:, :nq], in_=o0[:, :nq], func=AF.Square), reads=[o0b], writes=[sq_b])
                    psn, pnb = s.ps()
                    s.mm(psn[:, :nq], ones_bf[:], sq_[:, :nq], True, True, reads=[sq_b, cb_const], writes=[pnb])
                    s.op("act", ins("activation", out=o1[:, :nq], in_=psn[:, :nq], func=AF.Sqrt, scale=1.0 / 128, bias=EPSB[:, 0:1]), reads=[pnb, o1b], writes=[o1b])
                    s.op("dve", ins("reciprocal", out=o1[:, :nq], in_=o1[:, :nq]), reads=[o1b], writes=[o1b])
                    s.op("dve", ins("scalar_tensor_tensor", out=ot[:, qs], in0=o0[:, :nq], scalar=lam[:, 3:4], in1=o1[:, :nq], op0=ALU.mult, op1=ALU.mult),
                         reads=[o0b, o1b, cbuf], writes=[otb])
                    del stt[(qi, 0)]
                    del stt[(qi, 1)]

                for t_ in range(len(tasks) + LOOK):
                    if t_ < len(tasks):
                        stage1(tasks[t_])
                    if t_ >= LOOK:
                        stage2(tasks[t_ - LOOK])
                s.dma(OS[:, u, q0:q0 + L], ot[:, 0:L], reads=[otb], writes=[OS_b[si][u]])
        return OS_b

    def outproj_phase(l, Wsrc, nk, Osrc, O_bufs_fn, Xsrc, Xsrc_b):
        s.barrier()
        s.release()
        Wo = s.alloc([128, nk, 1024], BF16, "wo")
        wob = load_w(Wo, Wsrc, nk)
        rctx = ResCtx()
        oin = [(s.alloc([128, nk, 512], BF16, "oin"), Buf()) for _ in range(2)]
        for ti in range(NT):
            t0, n, cond, _, _ = TILES[ti]
            o_, o_b = oin[ti % 2]
            s.dma(o_[:, :, :n], Osrc[:, :, t0:t0 + n], writes=[o_b])
            outproj_resid(rctx, l, 2, ti, Wo, wob, nk, o_, [o_b], Xsrc, Xsrc_b, X, X_b)

    def gla_layer(l, Xsrc, Xsrc_b):
        s.barrier()
        s.release()
        Wqk = s.alloc([128, 8, 1024], BF16, "gwqk")
        wqb = load_w(Wqk, w1_qk, 8)
        Wv = s.alloc([128, 8, 1024], BF16, "gwv")
        wvb = load_w(Wv, w1_v, 8)
        Wr = s.alloc([128, 8, 1024], BF16, "gwr")
        wrb = load_w(Wr, w1_r, 8)
        Wg1 = s.alloc([128, 8, 32], BF16, "gwg1")
        wg1b = load_w(Wg1, w1_g1, 8)
        Wg2 = s.alloc([32, 1024], BF16, "gwg2")
        cbuf = Buf()
        s.dma(Wg2[:], w1_g2, writes=[cbuf], eng="pool")
        gb = s.alloc([128, 8], F32, "ggb")
        s.dma(gb[:], w1_gb, writes=[cbuf])
        s.op("dve", ins("tensor_scalar", out=gb[:], in0=gb[:], scalar1=-1.0, scalar2=None, op0=ALU.mult), reads=[cbuf], writes=[cbuf])
        ELt = s.alloc([128, 2, 4, 72], F32, "el")
        nctx = NormCtx(1)
        qkf = [(s.alloc([128, 8, 512], F32, "qkf"), Buf()) for _ in range(1)]
        lg = nctx.xt[0]
        cs_ = (s.alloc([128, 8, 512], F32, "cs"), Buf())
        c2 = nctx.tmp
        ex = [(s.alloc([128, 512], F32, "ex"), Buf()) for _ in range(3)]
        t1b_ = (s.alloc([32, 512], BF16, "t1b"), Buf())
        qd_t = [(s.alloc([128, 2, 4, 512], BF16, "qd"), Buf()) for _ in range(1)]
        kd_t = [(s.alloc([128, 2, 4, 512], BF16, "kd"), Buf()) for _ in range(1)]
        krT = [(s.alloc([128, 512], BF16, "krT"), Buf()) for _ in range(2)]
        krt = [(s.alloc([128, 4, 8, 128], BF16, "krt"), Buf()) for _ in range(1)]
        rt = [(s.alloc([128, 8, 512], BF16, "rt"), Buf()) for _ in range(1)]
        vt = [(s.alloc([128, 4, 1024], BF16, "vt"), Buf()) for _ in range(1)]
        G_b = [Buf() for _ in range(NT)]
        kx = 0
        kk = 0
        for ti in range(NT):
            t0, n, cond, s0, s1 = TILES[ti]
            nch = n // 64
            h, hb = norm_tile(nctx, l, 0, ti, Xsrc, Xsrc_b)
            qf, qfb = qkf[0]
            for cc in range(8):
                ps, pb = s.ps()
                for kc in range(8):
                    s.mm(ps[:, :n], Wqk[:, kc, cc * 128:(cc + 1) * 128], h[:, kc, :n], kc == 0, kc == 7, reads=[wqb[kc], hb], writes=[pb])
                sc_ = (128.0 ** -0.5) if cc < 4 else 1.0
                s.op("act", ins("activation", out=qf[:, cc, :n], in_=ps[:, :n], func=AF.Identity, scale=sc_), reads=[pb], writes=[qfb])
            r_, r_b = rt[0]
            for cc in range(8):
                ps, pb = s.ps()
                for kc in range(8):
                    s.mm(ps[:, :n], Wr[:, kc, cc * 128:(cc + 1) * 128], h[:, kc, :n], kc == 0, kc == 7, reads=[wrb[kc], hb], writes=[pb])
                s.op("act", ins("activation", out=r_[:, cc, :n], in_=ps[:, :n], func=AF.Silu), reads=[pb], writes=[r_b])
            s.dma(RS[:, :, t0:t0 + n], r_[:, :, :n], reads=[r_b], writes=[G_b[ti]])
            v_, v_b = vt[0]
            for tbk in range(n // 128):
                for vg in range(2):
                    ps, pb = s.ps()
                    for kc in range(8):
                        s.mm(ps[:, :512], h[:, kc, tbk * 128:(tbk + 1) * 128], Wv[:, kc, vg * 512:(vg + 1) * 512], kc == 0, kc == 7, reads=[wvb[kc], hb], writes=[pb])
                    s.op("act", ins("activation", out=v_[:, tbk, vg * 512:(vg + 1) * 512], in_=ps[:, :512], func=AF.Identity), reads=[pb], writes=[v_b])
            s.dma(rows_view(VS, 1024, t0, n // 128, 0, 1024), v_[:, 0:n // 128, :], reads=[v_b], writes=[G_b[ti]])
            ps, pb = s.ps()
            for kc in range(8):
                s.mm(ps[0:32, :n], Wg1[:, kc, :], h[:, kc, :n], kc == 0, kc == 7, reads=[wg1b[kc], hb], writes=[pb])
            t1_, t1bb = t1b_
            s.op("act", ins("activation", out=t1_[:, :n], in_=ps[0:32, :n], func=AF.Identity), reads=[pb], writes=[t1bb])
            lgt, lgb = lg
            for j in range(8):
                ps, pb = s.ps()
                s.mm(ps[:, :n], Wg2[:, j * 128:(j + 1) * 128], t1_[:, :n], True, True, reads=[cbuf, t1bb], writes=[pb])
                s.op("act", ins("activation", out=lgt[:, j, :n], in_=ps[:, :n], func=AF.Exp, scale=-1.0, bias=gb[:, j:j + 1]), reads=[pb, cbuf], writes=[lgb])
            s.op("act", ins("activation", out=lgt[:, :, :n], in_=lgt[:, :, :n], func=AF.Ln, bias=1.0, scale=1.0), reads=[lgb], writes=[lgb])
            cst, csb = cs_
            for j in range(8):
                s.op("dve", ins("tensor_tensor_scan", out=cst[:, j, :n], data0=m64[:, :n], data1=lgt[:, j, :n], initial=0.0, op0=ALU.mult, op1=ALU.add),
                     reads=[lgb, cb_const], writes=[csb])
            c2t, c2b = c2
            csv = cst[:, :, :n].rearrange("p j (c t) -> p j c t", t=64)
            lgv = lgt[:, :, :n].rearrange("p j (c t) -> p j c t", t=64)
            c2v = c2t[:, :, :n].rearrange("p j (c t) -> p j c t", t=64)
            nf = 4
            s.op("dve", ins("tensor_tensor", out=c2v[:, 0:nf, :, :], in0=csv[:, 0:nf, :, :], in1=csv[:, 0:nf, :, 63:64].to_broadcast([128, nf, nch, 64]), op=ALU.subtract),
                 reads=[csb], writes=[c2b])
            s.op("dve", ins("tensor_tensor", out=c2v[:, nf:2 * nf, :, :], in0=lgv[:, nf:2 * nf, :, :], in1=csv[:, nf:2 * nf, :, :], op=ALU.subtract),
                 reads=[csb, lgb], writes=[c2b])
            ci0 = t0 // 64
            s.op("act", ins("activation", out=ELt[:, :, :, ci0:ci0 + nch].rearrange("p d h c -> p (d h) c"), in_=cst[:, :, :n].rearrange("p j (c t) -> p j c t", t=64)[:, :, :, 63],
                                               func=AF.Exp, scale=-1.0 / 16), reads=[csb], writes=[cbuf])
            s.op("dve", ins("tensor_tensor", out=lgv[:, nf:2 * nf, :, :], in0=c2v[:, nf:2 * nf, :, :], in1=csv[:, nf:2 * nf, :, 63:64].to_broadcast([128, nf, nch, 64]), op=ALU.add),
                 reads=[csb, c2b, lgb], writes=[lgb])
            qd_, qdb = qd_t[0]
            kd_, kdb = kd_t[0]
            krt_, krtb = krt[0]
            for j in range(8):
                d = j // 4
                hh = j % 4
                ea, eab = ex[0]
                eb, ebb = ex[1]
                ec, ecb = ex[2]
                csrc = cst if j < 4 else lgt
                s.op("act", ins("activation", out=ea[:, :n], in_=csrc[:, j, :n], func=AF.Exp, scale=-1.0 / 16), reads=[csb, lgb], writes=[eab])
                s.op("act", ins("activation", out=eb[:, :n], in_=csrc[:, j, :n], func=AF.Exp, scale=1.0 / 16), reads=[csb, lgb], writes=[ebb])
                s.op("act", ins("activation", out=ec[:, :n], in_=c2t[:, j, :n], func=AF.Exp, scale=1.0 / 16), reads=[c2b], writes=[ecb])
                s.op("dve", ins("tensor_tensor", out=qd_[:, d, hh, :n], in0=qf[:, hh, :n], in1=ea[:, :n], op=ALU.mult), reads=[qfb, eab], writes=[qdb])
                s.op("dve", ins("tensor_tensor", out=kd_[:, d, hh, :n], in0=qf[:, 4 + hh, :n], in1=eb[:, :n], op=ALU.mult), reads=[qfb, ebb], writes=[kdb])
                kT, kTb = krT[kx % 2]
                kx += 1
                s.op("pool", ins("tensor_tensor", out=kT[:, :n], in0=qf[:, 4 + hh, :n], in1=ec[:, :n], op=ALU.mult), reads=[qfb, ecb], writes=[kTb])
                for tbk in range(n // 128):
                    ps, pb = s.ps()
                    psv = ps[:].bitcast(BF16)
                    s.op("pe", ins("transpose", psv[:, 0:128], kT[:, tbk * 128:(tbk + 1) * 128], ident_bf[:]), reads=[kTb, cb_const], writes=[pb])
                    s.op("act", ins("activation", out=krt_[:, tbk, j, :], in_=psv[:, 0:128], func=AF.Identity), reads=[pb], writes=[krtb])
            for d in range(2):
                s.dma(QD[d, :, :, t0:t0 + n], qd_[:, d, :, :n], reads=[qdb], writes=[G_b[ti]])
                s.dma(KD[d, :, :, t0:t0 + n], kd_[:, d, :, :n], reads=[kdb], writes=[G_b[ti]])
            s.dma(kr_view(t0, n // 128, 0, 8), krt_[:, 0:n // 128, :, :], reads=[krtb], writes=[G_b[ti]])
        ELd = dscr("ELd", [128, 2, 4, 72], F32)
        elb = Buf()
        s.dma(ELd, ELt[:], reads=[cbuf], writes=[elb])
        s.barrier()
        s.release()
        EL = s.alloc([128, 2, 4, 72], F32, "el2")
        elb2 = Buf()
        s.dma(EL[:], ELd, writes=[elb2])
        S = [(s.alloc([128, 4, 256], F32, "S"), [Buf() for _ in range(4)]) for _ in range(2)]
        Sb = [(s.alloc([128, 4, 256], BF16, "Sb"), [Buf() for _ in range(4)]) for _ in range(2)]
        qd_t = [[(s.alloc([128, 4, 512], BF16, "qd"), Buf()) for _ in range(2)] for _ in range(2)]
        kd_t = [[(s.alloc([128, 4, 512], BF16, "kd"), Buf()) for _ in range(2)] for _ in range(2)]
        kr_t = [[(s.alloc([128, 4, 4, 128], BF16, "kr"), Buf()) for _ in range(2)] for _ in range(2)]
        v_t = [[(s.alloc([128, 4, 1024], BF16, "v"), Buf()) for _ in range(2)] for _ in range(2)]
        of_t = [[(s.alloc([128, 8, 512], BF16, "of"), Buf()) for _ in range(2)] for _ in range(2)]
        attm = [(s.alloc([128, 64], BF16, "attm"), Buf()) for _ in range(8)]
        OD_b = [[Buf() for _ in range(NT)] for _ in range(2)]
        ODs = [OS, OS2]
        kat = 0
        SEQT = [list(range(8)), [8], [9]]
        states_in = [s1f, s1b]
        states_out = [o_gf, o_gb]
        for si, tiles in enumerate(SEQT):
            for d in range(2):
                St, Stb = S[d]
                Sbt, Sbb = Sb[d]
                if si == 0:
                    s.dma(St[:], states_in[d], writes=Stb)
                else:
                    s.op("pool", ins("memset", St[:], 0.0), writes=Stb)
                for hh in range(4):
                    s.op("act", ins("activation", out=Sbt[:, hh, :], in_=St[:, hh, :], func=AF.Identity), reads=[Stb[hh]], writes=[Sbb[hh]])
            nt_ = len(tiles)
            for step in range(nt_):
                cur = {}
                for d in range(2):
                    ti = tiles[step] if d == 0 else tiles[nt_ - 1 - step]
                    t0, n, cond, s0, s1 = TILES[ti]
                    qd_, qdb = qd_t[d][step % 2]
                    kd_, kdb = kd_t[d][step % 2]
                    kr_, krb = kr_t[d][step % 2]
                    v_, vb_ = v_t[d][step % 2]
                    of_, ofb = of_t[d][step % 2]
                    s.dma(qd_[:, :, :n], QD[d, :, :, t0:t0 + n], writes=[qdb])
                    s.dma(kd_[:, :, :n], KD[d, :, :, t0:t0 + n], writes=[kdb])
                    s.dma(kr_[:, 0:n // 128, :, :], kr_view(t0, n // 128, d * 4, 4), writes=[krb])
                    s.dma(v_[:, 0:n // 128, :], rows_view(VS, 1024, t0, n // 128, 0, 1024), writes=[vb_])
                    cur[d] = (ti, t0, n, qd_, qdb, kd_, kdb, kr_, krb, v_, vb_, of_, ofb)
                nch = cur[0][2] // 64
                for kc_ in range(nch):
                    for d in range(2):
                        ti, t0, n, qd_, qdb, kd_, kdb, kr_, krb, v_, vb_, of_, ofb = cur[d]
                        k = kc_ if d == 0 else nch - 1 - kc_
                        ci = t0 // 64 + k
                        tbk = k // 2
                        hp = slice((k % 2) * 64, (k % 2) * 64 + 64)
                        cs64 = slice(k * 64, (k + 1) * 64)
                        St, Stb = S[d]
                        Sbt, Sbb = Sb[d]
                        for hh in range(4):
                            ps, pb = s.ps()
                            s.mm(ps[0:64, 0:64], kd_[:, hh, cs64], qd_[:, hh, cs64], True, True, reads=[kdb, qdb], writes=[pb])
                            am, amb = attm[kat % 8]
                            kat += 1
                            s.op("dve", ins("tensor_tensor", out=am[hp, :], in0=ps[0:64, 0:64], in1=tri_bf[0:64, 2 + d, 0:64], op=ALU.mult),
                                 reads=[pb, cb_const], writes=[amb])
                            pso, pob = s.ps("b")
                            for vc in range(2):
                                s.mm(pso[:, vc * 64:(vc + 1) * 64], v_[hp, tbk, hh * 256 + vc * 128:hh * 256 + (vc + 1) * 128], am[hp, :], True, False, reads=[vb_, amb], writes=[pob])
                                s.mm(pso[:, vc * 64:(vc + 1) * 64], Sbt[:, hh, vc * 128:(vc + 1) * 128], qd_[:, hh, cs64], False, True, reads=[Sbb[hh], qdb], writes=[pob])
                            s.op("act", ins("activation", out=of_[:, hh * 2:hh * 2 + 2, cs64], in_=pso[:, 0:128].rearrange("p (v t) -> p v t", t=64), func=AF.Identity),
                                 reads=[pob], writes=[ofb])
                            ps2, pb2 = s.ps()
                            s.mm(ps2[:, 0:256], kr_[hp, tbk, hh, :], v_[hp, tbk, hh * 256:(hh + 1) * 256], True, True, reads=[krb, vb_], writes=[pb2])
                            s.op("dve", ins("scalar_tensor_tensor", out=St[:, hh, :], in0=St[:, hh, :], scalar=EL[:, d, hh, ci:ci + 1], in1=ps2[:, 0:256], op0=ALU.mult, op1=ALU.add),
                                 reads=[pb2, elb2, Stb[hh]], writes=[Stb[hh]])
                            s.op("pool", ins("tensor_copy", out=Sbt[:, hh, :], in_=St[:, hh, :]), reads=[Stb[hh]], writes=[Sbb[hh]])
                for d in range(2):
                    ti, t0, n, qd_, qdb, kd_, kdb, kr_, krb, v_, vb_, of_, ofb = cur[d]
                    s.dma(ODs[d][:, :, t0:t0 + n], of_[:, :, :n], reads=[ofb], writes=[OD_b[d][ti]])
            if si > 0:
                for d in range(2):
                    s.dma(states_out[d][si - 1], S[d][0][:], reads=S[d][1])
        s.barrier()
        s.release()
        Wo = s.alloc([128, 8, 1024], BF16, "wo")
        wob = load_w(Wo, w1_o, 8)
        gn = s.alloc([128, 2], F32, "gn")
        gnb = Buf()
        s.dma(gn[:], w1_gn, writes=[gnb])
        rctx = ResCtx()
        oa = [(s.alloc([128, 8, 512], BF16, "oa"), Buf()) for _ in range(2)]
        ob_ = [(s.alloc([128, 8, 512], BF16, "ob"), Buf()) for _ in range(2)]
        rr = [(s.alloc([128, 8, 512], BF16, "rr"), Buf()) for _ in range(2)]
        osum = (s.alloc([128, 8, 512], F32, "osum"), Buf())
        sq = (s.alloc([128, 8, 512], BF16, "sq"), Buf())
        rs_ = [(s.alloc([128, 512], F32, "rs"), Buf()) for _ in range(2)]
        ofin = [(s.alloc([128, 8, 512], BF16, "ofin"), Buf()) for _ in range(2)]
        for ti in range(NT):
            t0, n, cond, _, _ = TILES[ti]
            a_, a_b = oa[ti % 2]
            b_, b_b = ob_[ti % 2]
            r_, r_b = rr[ti % 2]
            s.dma(a_[:, :, :n], OS[:, :, t0:t0 + n], writes=[a_b])
            s.dma(b_[:, :, :n], OS2[:, :, t0:t0 + n], writes=[b_b])
            s.dma(r_[:, :, :n], RS[:, :, t0:t0 + n], writes=[r_b])
            os_, osb_ = osum
            sq_, sqb_ = sq
            s.op("dve", ins("tensor_tensor", out=os_[:, :, :n], in0=a_[:, :, :n], in1=b_[:, :, :n], op=ALU.add), reads=[a_b, b_b], writes=[osb_])
            s.op("act", ins("activation", out=sq_[:, :, :n], in_=os_[:, :, :n], func=AF.Square), reads=[osb_], writes=[sqb_])
            f_, f_b = ofin[ti % 2]
            for hh in range(4):
                ps, pb = s.ps()
                for vc in range(2):
                    s.mm(ps[:, :n], ones_bf[:], sq_[:, hh * 2 + vc, :n], vc == 0, vc == 1, reads=[sqb_, cb_const], writes=[pb])
                rt_, rtb = rs_[hh % 2]
                s.op("act", ins("activation", out=rt_[:, :n], in_=ps[:, :n], func=AF.Sqrt, scale=1.0 / 256, bias=EPSB[:, 0:1]), reads=[pb], writes=[rtb])
                s.op("dve", ins("reciprocal", out=rt_[:, :n], in_=rt_[:, :n]), reads=[rtb], writes=[rtb])
                for vc in range(2):
                    c = hh * 2 + vc
                    s.op("dve", ins("tensor_tensor", out=os_[:, c, :n], in0=os_[:, c, :n], in1=rt_[:, :n], op=ALU.mult), reads=[osb_, rtb], writes=[osb_])
                    s.op("dve", ins("scalar_tensor_tensor", out=f_[:, c, :n], in0=os_[:, c, :n], scalar=gn[:, vc:vc + 1], in1=r_[:, c, :n], op0=ALU.mult, op1=ALU.mult),
                         reads=[osb_, gnb, r_b], writes=[f_b])
            outproj_resid(rctx, l, 2, ti, Wo, wob, 8, f_, [f_b], Xsrc, Xsrc_b, X, X_b)

    def ssd_layer(l, Xsrc, Xsrc_b):
        XBu = U
        s.barrier()
        s.release()
        Wz = s.alloc([128, 8, 2048], BF16, "wz")
        wzb = load_w(Wz, w3_z, 8)
        Wx = s.alloc([128, 8, 3072], BF16, "wx")
        wxb = load_w(Wx, w3_xbc, 8)
        Wdt = s.alloc([128, 8, 64], BF16, "wdt")
        wdb = load_w(Wdt, w3_dt, 8)
        vec = s.alloc([128, 5, 32], F32, "vec")
        cbuf = Buf()
        s.dma(vec[:], ssd_vec.rearrange("a b -> (a b)").partition_broadcast(128).rearrange("p (a b) -> p a b", a=5), writes=[cbuf])
        avec = s.alloc([128, 64], F32, "avec")
        s.op("act", ins("activation", out=avec[:], in_=vec[:, 0:2, :].rearrange("p a b -> p (a b)"), func=AF.Exp), reads=[cbuf], writes=[cbuf])
        s.op("dve", ins("tensor_scalar", out=avec[:], in0=avec[:], scalar1=-1.0, scalar2=None, op0=ALU.mult), reads=[cbuf], writes=[cbuf])
        nctx = NormCtx(1)
        xbt = (s.alloc([128, 24, 512], BF16, "xbt"), [Buf() for _ in range(24)])
        zt = (s.alloc([128, 4, 2048], BF16, "zt"), Buf())
        dtt = (s.alloc([128, 4, 2, 64], F32, "dtt"), Buf())
        dtmp = (s.alloc([128, 64], F32, "dtmp"), Buf())
        S_b = [Buf() for _ in range(NT)]
        ev = 0
        for ti in range(NT):
            t0, n, cond, s0, s1 = TILES[ti]
            nb = n // 128
            h, hb = norm_tile(nctx, l, 0, ti, Xsrc, Xsrc_b)
            xb_, xbb = xbt
            for cc in range(24):
                ps, pb = s.ps()
                for kc in range(8):
                    s.mm(ps[:, :n], Wx[:, kc, cc * 128:(cc + 1) * 128], h[:, kc, :n], kc == 0, kc == 7, reads=[wxb[kc], hb], writes=[pb])
                if ev % 2 == 0:
                    s.op("act", ins("activation", out=xb_[:, cc, :n], in_=ps[:, :n], func=AF.Identity), reads=[pb], writes=[xbb[cc]])
                else:
                    s.op("dve", ins("tensor_copy", out=xb_[:, cc, :n], in_=ps[:, :n]), reads=[pb], writes=[xbb[cc]])
                ev += 1
            s.dma(XBu[:, 0:24, t0:t0 + n], xb_[:, :, :n], reads=xbb, writes=[S_b[ti]])
            z_, z_b = zt
            d_, d_b = dtt
            for tbk in range(nb):
                for vg in range(4):
                    ps, pb = s.ps()
                    for kc in range(8):
                        s.mm(ps[:, :512], h[:, kc, tbk * 128:(tbk + 1) * 128], Wz[:, kc, vg * 512:(vg + 1) * 512], kc == 0, kc == 7, reads=[wzb[kc], hb], writes=[pb])
                    s.op("act", ins("activation", out=z_[:, tbk, vg * 512:(vg + 1) * 512], in_=ps[:, :512], func=AF.Silu), reads=[pb], writes=[z_b])
                ps, pb = s.ps()
                for kc in range(8):
                    s.mm(ps[:, 0:64], h[:, kc, tbk * 128:(tbk + 1) * 128], Wdt[:, kc, :], kc == 0, kc == 7, reads=[wdb[kc], hb], writes=[pb])
                tm, tmb = dtmp
                s.op("dve", ins("tensor_tensor", out=tm[:], in0=ps[:, 0:64], in1=vec[:, 2:4, :].rearrange("p a b -> p (a b)"), op=ALU.add), reads=[pb, cbuf], writes=[tmb])
                s.op("act", ins("activation", out=tm[:], in_=tm[:], func=AF.Exp), reads=[tmb], writes=[tmb])
                s.op("act", ins("activation", out=d_[:, tbk, 0, :], in_=tm[:], func=AF.Ln, bias=1.0, scale=1.0), reads=[tmb], writes=[d_b])
                s.op("dve", ins("tensor_tensor", out=d_[:, tbk, 1, :], in0=d_[:, tbk, 0, :], in1=avec[:], op=ALU.mult), reads=[d_b, cbuf], writes=[d_b])
            s.dma(rows_view(ZS, 2048, t0, nb, 0, 2048), z_[:, 0:nb, :], reads=[z_b], writes=[S_b[ti]])
            s.dma(AP(DTS.tensor, t0 * 128, [[128, 128], [128 * 128, nb], [1, 128]]), d_[:, 0:nb, :, :].rearrange("p b a c -> p b (a c)"), reads=[d_b], writes=[S_b[ti]])
        s.barrier()
        s.release()
        cw = s.alloc([128, 3, 24], F32, "scw")
        cbias = s.alloc([128, 24], F32, "scb")
        cwb = Buf()
        s.dma(cw[:], ssd_cw, writes=[cwb])
        s.dma(cbias[:], ssd_cb, writes=[cwb])
        ug = s.alloc([128, 24, 514], BF16, "sug")
        ugb = [Buf() for _ in range(2)]
        xc = (s.alloc([128, 24, 512], BF16, "xc"), [Buf() for _ in range(24)])
        tmps = [(s.alloc([128, 512], F32, "st"), Buf()) for _ in range(2)]
        xtt = (s.alloc([128, 4, 2048], BF16, "xtt"), Buf())
        btt = (s.alloc([128, 4, 512], BF16, "btt"), Buf())
        kp = 0
        for ti in range(NT):
            t0, n, cond, s0, s1 = TILES[ti]
            nb = n // 128
            lo = (t0 - 1) >= s0
            hi = (t0 + n) < s1
            for grp in range(2):
                gs_ = slice(grp * 12, (grp + 1) * 12)
                if not lo:
                    s.op("pool", ins("memset", ug[:, gs_, 0:1], 0.0), writes=[ugb[grp]])
                if not hi:
                    s.op("pool", ins("memset", ug[:, gs_, n + 1:n + 2], 0.0), writes=[ugb[grp]])
                c0 = 0 if lo else 1
                c1 = n + 2 if hi else n + 1
                s.dma(ug[:, gs_, c0:c1], XBu[:, gs_, t0 - 1 + c0:t0 - 1 + c1], writes=[ugb[grp]])
            xc_, xcb = xc
            for col in range(24):
                grp = col // 12
                tt, ttb = tmps[kp % 2]
                kp += 1
                s.op("act", ins("activation", out=tt[:, :n], in_=ug[:, col, 1:n + 1], func=AF.Identity, scale=cw[:, 1, col:col + 1], bias=cbias[:, col:col + 1]),
                     reads=[ugb[grp], cwb], writes=[ttb])
                s.op("dve", ins("scalar_tensor_tensor", out=tt[:, :n], in0=ug[:, col, 0:n], scalar=cw[:, 0, col:col + 1], in1=tt[:, :n], op0=ALU.mult, op1=ALU.add),
                     reads=[ugb[grp], cwb, ttb], writes=[ttb])
                s.op("dve", ins("scalar_tensor_tensor", out=tt[:, :n], in0=ug[:, col, 2:n + 2], scalar=cw[:, 2, col:col + 1], in1=tt[:, :n], op0=ALU.mult, op1=ALU.add),
                     reads=[ugb[grp], cwb, ttb], writes=[ttb])
                s.op("act", ins("activation", out=xc_[:, col, :n], in_=tt[:, :n], func=AF.Silu), reads=[ttb], writes=[xcb[col]])
            s.dma(QK[:, 0:8, t0:t0 + n], xc_[:, 16:24, :n], reads=xcb[16:24], writes=[S_b[ti]])
            x_, x_b = xtt
            b_, b_b = btt
            tv = 0
            for tbk in range(nb):
                for col in range(20):
                    ps, pb = s.ps()
                    psv = ps[:].bitcast(BF16)
                    s.op("pe", ins("transpose", psv[:, 0:128], xc_[:, col, tbk * 128:(tbk + 1) * 128], ident_bf[:]), reads=[xcb[col], cb_const], writes=[pb])
                    dst = x_[:, tbk, col * 128:(col + 1) * 128] if col < 16 else b_[:, tbk, (col - 16) * 128:(col - 15) * 128]
                    dbuf = x_b if col < 16 else b_b
                    if tv % 2 == 0:
                        s.op("act", ins("activation", out=dst, in_=psv[:, 0:128], func=AF.Identity), reads=[pb], writes=[dbuf])
                    else:
                        s.op("dve", ins("tensor_copy", out=dst, in_=psv[:, 0:128]), reads=[pb], writes=[dbuf])
                    tv += 1
            s.dma(rows_view(XTs, 2048, t0, nb, 0, 2048), x_[:, 0:nb, :], reads=[x_b], writes=[S_b[ti]])
            s.dma(rows_view(BTs, 512, t0, nb, 0, 512), b_[:, 0:nb, :], reads=[b_b], writes=[S_b[ti]])
        s.barrier()
        s.release()
        vec = s.alloc([128, 5, 32], F32, "vec")
        negm = s.alloc([128, 2, 128], F32, "negm")
        cbuf = Buf()
        s.op("dve", ins("tensor_scalar", out=negm[:], in0=stage[:, 0:2, :], scalar1=-1.0, scalar2=30000.0, op0=ALU.add, op1=ALU.mult), reads=[bstage], writes=[cbuf])
        ST = [(s.alloc([128, 2048], F32, "ST"), [Buf() for _ in range(4)]) for _ in range(2)]
        SbT = [(s.alloc([128, 2048], BF16, "SbT"), [Buf() for _ in range(4)]) for _ in range(2)]
        bct = [(s.alloc([128, 8, 512], BF16, "bct"), Buf()) for _ in range(2)]
        xts = [(s.alloc([128, 4, 2048], BF16, "xts"), Buf()) for _ in range(2)]
        bts = [(s.alloc([128, 4, 512], BF16, "bts"), Buf()) for _ in range(2)]
        dts = [(s.alloc([128, 4, 2, 64], F32, "dts"), Buf()) for _ in range(2)]
        yts = [(s.alloc([128, 4, 2048], BF16, "yts"), Buf()) for _ in range(2)]
        cumT = [(s.alloc([128, 32], F32, "cumT"), Buf()) for _ in range(2)]
        cbT = [(s.alloc([128, 128], F32, "cbT"), Buf()) for _ in range(2)]
        lat = [(s.alloc([128, 4, 128], F32, "lat"), Buf()) for _ in range(2)]
        cB = [(s.alloc([128, 4, 128], F32, "cB"), Buf()) for _ in range(2)]
        seg = [(s.alloc([128, 4, 128], F32, "seg"), Buf()) for _ in range(2)]
        Et = [(s.alloc([128, 4, 128], F32, "Et"), Buf()) for _ in range(2)]
        ecB = [(s.alloc([128, 4, 128], F32, "ecB"), Buf()) for _ in range(4)]
        te = [(s.alloc([128, 4], F32, "te"), Buf()) for _ in range(2)]
        xs = [(s.alloc([128, 512], BF16, "xs"), Buf()) for _ in range(2)]
        Wt = [(s.alloc([128, 128], BF16, "Wt"), Buf()) for _ in range(8)]
        CEt = [(s.alloc([128, 128], BF16, "CEt"), Buf()) for _ in range(8)]
        SEQT = [list(range(8)), [8], [9]]
        st_in = [s3f, s3b]
        st_out = [o_sf, o_sb]
        Y_b = [Buf() for _ in range(NT)]
        kw = 0
        kq4 = 0
        kg = 0
        for si, tiles in enumerate(SEQT):
            for d in range(2):
                St, Stb = ST[d]
                Sbt, Sbb = SbT[d]
                if si == 0:
                    s.dma(St[:], st_in[d], writes=Stb)
                else:
                    s.op("pool", ins("memset", St[:], 0.0), writes=Stb)
                for g in range(4):
                    s.op("act", ins("activation", out=Sbt[:, g * 512:(g + 1) * 512], in_=St[:, g * 512:(g + 1) * 512], func=AF.Identity), reads=[Stb[g]], writes=[Sbb[g]])
            nt_ = len(tiles)
            for step in range(nt_):
                cur = {}
                for d in range(2):
                    ti = tiles[step] if d == 0 else tiles[nt_ - 1 - step]
                    t0, n, cond, s0, s1 = TILES[ti]
                    nb = n // 128
                    bc_, bcb = bct[d]
                    x_, x_b = xts[d]
                    b_, b_b = bts[d]
                    d_, d_b = dts[d]
                    y_, y_b = yts[d]
                    s.dma(bc_[:, :, :n], QK[:, 0:8, t0:t0 + n], writes=[bcb])
                    s.dma(x_[:, 0:nb, :], rows_view(XTs, 2048, t0, nb, 0, 2048), writes=[x_b])
                    s.dma(b_[:, 0:nb, :], rows_view(BTs, 512, t0, nb, 0, 512), writes=[b_b])
                    s.dma(d_[:, 0:nb, :, :].rearrange("p b a c -> p b (a c)"), AP(DTS.tensor, t0 * 128, [[128, 128], [128 * 128, nb], [1, 128]]), writes=[d_b])
                    cur[d] = (ti, t0, n, nb)
                nbs = cur[0][3]
                for kc_ in range(nbs):
                    for d in range(2):
                        ti, t0, n, nb = cur[d]
                        tbk = kc_ if d == 0 else nb - 1 - kc_
                        last = 127 if d == 0 else 0
                        bc_, bcb = bct[d]
                        x_, x_b = xts[d]
                        b_, b_b = bts[d]
                        d_, d_b = dts[d]
                        y_, y_b = yts[d]
                        St, Stb = ST[d]
                        Sbt, Sbb = SbT[d]
                        tk = slice(tbk * 128, (tbk + 1) * 128)
                        la = d_[:, tbk, 1, d * 32:(d + 1) * 32]
                        dtv = d_[:, tbk, 0, d * 32:(d + 1) * 32]
                        psc, pcb = s.ps()
                        s.mm(psc[:, 0:32], stage[:, d, :], la, True, True, reads=[bstage, d_b], writes=[pcb])
                        cT, cTb = cumT[d]
                        s.op("dve", ins("tensor_copy", out=cT[:], in_=psc[:, 0:32]), reads=[pcb], writes=[cTb])
                        for g in range(4):
                            ps, pb = s.ps()
                            s.mm(ps[:, 0:128], bc_[:, g, tk], bc_[:, 4 + g, tk], True, True, reads=[bcb], writes=[pb])
                            cb_, cb_b = cbT[kg % 2]
                            kg += 1
                            s.op("act", ins("activation", out=cb_[:], in_=ps[:, 0:128], func=AF.Identity), reads=[pb], writes=[cb_b])
                            psy, pyb = s.ps("b")
                            x4, x4b = xs[g % 2]
                            ecs = []
                            for hb_ in range(2):
                                h0 = g * 8 + hb_ * 4
                                lt, ltb = lat[kq4 % 2]
                                cb4, cb4b = cB[kq4 % 2]
                                sg, sgb = seg[kq4 % 2]
                                E_, E_b = Et[kq4 % 2]
                                ec, ecb = ecB[kq4 % 4]
                                te_, te_b = te[kq4 % 2]
                                kq4 += 1
                                ecs.append((ec, ecb))
                                s.op("dve", ins("tensor_tensor", out=lt[:], in0=stage[:, d, :].unsqueeze(1).to_broadcast([128, 4, 128]),
                                                                                          in1=la[:, h0:h0 + 4].unsqueeze(2).to_broadcast([128, 4, 128]), op=ALU.mult),
                                     reads=[bstage, d_b], writes=[ltb])
                                ps2, pb2 = s.ps()
                                s.mm(ps2[:, 0:512], ones_f[:], lt[:].rearrange("p a b -> p (a b)"), True, True, reads=[ltb, cb_const], writes=[pb2])
                                s.op("act", ins("activation", out=cb4[:].rearrange("p a b -> p (a b)"), in_=ps2[:, 0:512], func=AF.Identity), reads=[pb2], writes=[cb4b])
                                for hh in range(4):
                                    hq = h0 + hh
                                    s.op("dve", ins("scalar_tensor_tensor", out=sg[:, hh, :], in0=cb4[:, hh, :], scalar=cT[:, hq:hq + 1], in1=negm[:, d, :],
                                                                                                                 op0=ALU.subtract, op1=ALU.add),
                                         reads=[cb4b, cTb, cbuf], writes=[sgb])
                                s.op("act", ins("activation", out=E_[:], in_=sg[:], func=AF.Exp), reads=[sgb], writes=[E_b])
                                s.op("act", ins("activation", out=ec[:], in_=cb4[:], func=AF.Exp), reads=[cb4b], writes=[ecb])
                                s.op("dve", ins("tensor_tensor", out=te_[:], in0=E_[:, :, last], in1=dtv[:, h0:h0 + 4], op=ALU.mult),
                                     reads=[E_b, d_b], writes=[te_b])
                                s.op("dve", ins("tensor_tensor",
                                    out=x4[:, hb_ * 256:(hb_ + 1) * 256].rearrange("p (a b) -> p a b", b=64),
                                    in0=x_[:, tbk, h0 * 64:(h0 + 4) * 64].rearrange("p (a b) -> p a b", b=64),
                                    in1=te_[:].unsqueeze(2).to_broadcast([128, 4, 64]), op=ALU.mult),
                                     reads=[x_b, te_b], writes=[x4b])
                                for hh in range(4):
                                    hq = h0 + hh
                                    W_, W_b = Wt[kw % 8]
                                    C_, C_b = CEt[kw % 8]
                                    kw += 1
                                    s.op("dve", ins("scalar_tensor_tensor", out=W_[:], in0=E_[:, hh, :], scalar=dtv[:, hq:hq + 1], in1=cb_[:], op0=ALU.mult, op1=ALU.mult),
                                         reads=[E_b, d_b, cb_b], writes=[W_b])
                                    s.op("pool", ins("tensor_tensor", out=C_[:], in0=bc_[:, 4 + g, tk], in1=ec[:, hh, :], op=ALU.mult),
                                         reads=[bcb, ecb], writes=[C_b])
                                    ycol = slice((hq % 8) * 64, (hq % 8 + 1) * 64)
                                    s.mm(psy[:, ycol], W_[:], x_[:, tbk, hq * 64:(hq + 1) * 64], True, False, reads=[W_b, x_b], writes=[pyb])
                                    s.mm(psy[:, ycol], C_[:], Sbt[:, hq * 64:(hq + 1) * 64], False, True, reads=[C_b, Sbb[g]], writes=[pyb])
                            s.op("act", ins("activation", out=y_[:, tbk, g * 512:(g + 1) * 512], in_=psy[:, 0:512], func=AF.Identity), reads=[pyb], writes=[y_b])
                            pss, psb_ = s.ps()
                            s.mm(pss[:, 0:512], b_[:, tbk, g * 128:(g + 1) * 128], x4[:], True, True, reads=[b_b, x4b], writes=[psb_])
                            for hh in range(8):
                                hq = g * 8 + hh
                                ec, ecb = ecs[hh // 4]
                                s.op("dve", ins("scalar_tensor_tensor",
                                    out=St[:, hq * 64:(hq + 1) * 64], in0=St[:, hq * 64:(hq + 1) * 64], scalar=ec[:, hh % 4, last:last + 1], in1=pss[:, hh * 64:(hh + 1) * 64], op0=ALU.mult, op1=ALU.add),
                                     reads=[psb_, ecb, Stb[g]], writes=[Stb[g]])
                            s.op("pool", ins("tensor_copy", out=Sbt[:, g * 512:(g + 1) * 512], in_=St[:, g * 512:(g + 1) * 512]), reads=[Stb[g]], writes=[Sbb[g]])
                for d in range(2):
                    ti, t0, n, nb = cur[d]
                    y_, y_b = yts[d]
                    s.dma(rows_view(YD[d], 2048, t0, nb, 0, 2048), y_[:, 0:nb, :], reads=[y_b], writes=[Y_b[ti]])
            if si > 0:
                for d in range(2):
                    s.dma(st_out[d][si - 1], ST[d][0][:], reads=ST[d][1])
        s.barrier()
        s.release()
        Wo = s.alloc([128, 16, 1024], BF16, "wo")
        wob = load_w(Wo, w3_o, 16)
        vec = s.alloc([128, 5, 32], F32, "vec")
        gnb = s.alloc([128, 2048], F32, "gnb")
        cbuf = Buf()
        s.dma(vec[:], ssd_vec.rearrange("a b -> (a b)").partition_broadcast(128).rearrange("p (a b) -> p a b", a=5), writes=[cbuf])
        s.dma(gnb[:], ssd_gn.partition_broadcast(128), writes=[cbuf])
        rctx = ResCtx()
        yf = (s.alloc([128, 4, 2048], BF16, "yf"), Buf())
        yb = (s.alloc([128, 4, 2048], BF16, "yb"), Buf())
        xt_ = (s.alloc([128, 4, 2048], BF16, "xt4"), Buf())
        zs = (s.alloc([128, 4, 2048], BF16, "zs"), Buf())
        yv = [(s.alloc([128, 2048], F32, "yv"), Buf()) for _ in range(2)]
        tv_ = (s.alloc([128, 2048], F32, "tv"), Buf())
        ynb = (s.alloc([128, 2048], BF16, "ynb"), Buf())
        ssq = [(s.alloc([128, 1], F32, "ssq"), Buf()) for _ in range(2)]
        oT = [(s.alloc([128, 16, 512], BF16, "oT"), Buf()) for _ in range(2)]
        kk = 0
        tv = 0
        for ti in range(NT):
            t0, n, cond, _, _ = TILES[ti]
            nb = n // 128
            s.dma(yf[0][:, 0:nb, :], rows_view(YD[0], 2048, t0, nb, 0, 2048), writes=[yf[1]])
            s.dma(yb[0][:, 0:nb, :], rows_view(YD[1], 2048, t0, nb, 0, 2048), writes=[yb[1]])
            s.dma(xt_[0][:, 0:nb, :], rows_view(XTs, 2048, t0, nb, 0, 2048), writes=[xt_[1]])
            s.dma(zs[0][:, 0:nb, :], rows_view(ZS, 2048, t0, nb, 0, 2048), writes=[zs[1]])
            o_, o_b = oT[ti % 2]
            for tbk in range(nb):
                y_, y_b = yv[kk % 2]
                sq_, sq_b = ssq[kk % 2]
                kk += 1
                t_, t_b = tv_
                s.op("dve", ins("tensor_tensor", out=y_[:], in0=yf[0][:, tbk, :], in1=yb[0][:, tbk, :], op=ALU.add), reads=[yf[1], yb[1]], writes=[y_b])
                s.op("pool", ins("tensor_tensor", out=t_[:].rearrange("p (a b) -> p a b", b=64), in0=xt_[0][:, tbk, :].rearrange("p (a b) -> p a b", b=64),
                                                              in1=vec[:, 4, :].unsqueeze(2).to_broadcast([128, 32, 64]), op=ALU.mult), reads=[xt_[1], cbuf], writes=[t_b])
                s.op("dve", ins("tensor_tensor", out=y_[:], in0=y_[:], in1=t_[:], op=ALU.add), reads=[y_b, t_b], writes=[y_b])
                s.op("dve", ins("tensor_tensor", out=y_[:], in0=y_[:], in1=zs[0][:, tbk, :], op=ALU.mult), reads=[y_b, zs[1]], writes=[y_b])
                s.op("pool", ins("memset", sq_[:], 0.0), writes=[sq_b])
                s.op("act", ins("activation", out=t_[:], in_=y_[:], func=AF.Square, accum_out=sq_[:, 0:1]), reads=[y_b, sq_b], writes=[t_b, sq_b])
                s.op("act", ins("activation", out=sq_[:], in_=sq_[:], func=AF.Sqrt, scale=1.0 / 2048, bias=EPSB[:, 0:1]), reads=[sq_b], writes=[sq_b])
                s.op("dve", ins("reciprocal", out=sq_[:], in_=sq_[:]), reads=[sq_b], writes=[sq_b])
                yn, ynb_ = ynb
                s.op("dve", ins("scalar_tensor_tensor", out=yn[:], in0=y_[:], scalar=sq_[:, 0:1], in1=gnb[:], op0=ALU.mult, op1=ALU.mult), reads=[y_b, sq_b, cbuf], writes=[ynb_])
                for c in range(16):
                    ps, pb = s.ps()
                    psv = ps[:].bitcast(BF16)
                    s.op("pe", ins("transpose", psv[:, 0:128], yn[:, c * 128:(c + 1) * 128], ident_bf[:]), reads=[ynb_, cb_const], writes=[pb])
                    if tv % 2 == 0:
                        s.op("act", ins("activation", out=o_[:, c, tbk * 128:(tbk + 1) * 128], in_=psv[:, 0:128], func=AF.Identity), reads=[pb], writes=[o_b])
                    else:
                        s.op("dve", ins("tensor_copy", out=o_[:, c, tbk * 128:(tbk + 1) * 128], in_=psv[:, 0:128]), reads=[pb], writes=[o_b])
                    tv += 1
            outproj_resid(rctx, l, 2, ti, Wo, wob, 16, o_, [o_b], Xsrc, Xsrc_b, X, X_b)

    import os
    STOP = int(os.environ.get("KSTOP", "99"))

    class _Stop(Exception):
        pass

    def chk(k):
        if STOP == k:
            raise _Stop()

    try:
        prologue()
        chk(0)
        Xsrc, Xsrc_b = xT, xT_b
        for l in range(NLAYERS):
            kind = l % 4
            if kind == 0:
                qkv_phase(l, w0_qk, 12, w0_v, 256, Xsrc, Xsrc_b, True, kout=(8, o_wk, 64), vout=o_wv)
                chk(1)
                attn_phase(l, "win", Xsrc, Xsrc_b)
                chk(2)
                outproj_phase(l, w0_o, 8, OS, None, Xsrc, Xsrc_b)
                chk(3)
            elif kind == 1:
                gla_layer(l, Xsrc, Xsrc_b)
            elif kind == 2:
                qkv_phase(l, w2_qk, 16, w2_v, 1024, Xsrc, Xsrc_b, True, kout=(8, o_dk, 128), vout=o_dv)
                attn_phase(l, "diff", Xsrc, Xsrc_b)
                outproj_phase(l, w2_o, 8, OS, None, Xsrc, Xsrc_b)
            else:
                ssd_layer(l, Xsrc, Xsrc_b)
            Xsrc, Xsrc_b = X, X_b
            ffn(l, Xsrc, Xsrc_b)
            chk(10 + l)
        s.barrier()
        s.release()
        nctx = NormCtx()
        for ti in range(NT):
            norm_tile(nctx, 0, 0, ti, Xsrc, Xsrc_b, final_out=yT)
    except _Stop:
        pass
    s.barrier()
    counts = s.emit()
    return nc, counts

def _fm(w):
    K, N = w.shape
    return np.ascontiguousarray(w.reshape(K // 128, 128, N).transpose(1, 0, 2))


def _pc(v):
    return np.ascontiguousarray(v.reshape(-1, 128).T)


def _consts():
    f32 = np.float32
    t = np.arange(LS)
    row = (t // 64).astype(f32)
    col = (t % 64).astype(f32)
    inv = (10000.0 ** (-np.arange(0, 32, 2, dtype=f32) / 32)).astype(f32)
    cos = np.zeros((128, LS), f32)
    sin = np.zeros((128, LS), f32)
    perm = np.zeros((128, 128), f32)
    for p in range(128):
        d = p % 64
        axis = d // 32
        idx = d % 16
        second = (d % 32) >= 16
        ang = (row if axis == 0 else col) * inv[idx]
        cos[p] = np.cos(ang)
        sin[p] = np.sin(ang) * (1.0 if second else -1.0)
        partner = p + 16 if not second else p - 16
        perm[partner, p] = 1.0
    j = np.arange(128)[:, None]
    i = np.arange(512)[None, :]
    wmask = np.zeros((128, 6, 512), f32)
    for mi in range(6):
        o = mi - 1
        wmask[:, mi, :] = (np.abs(i - (o * 128 + j)) <= 128).astype(f32)
    tri = np.zeros((128, 4, 128), f32)
    jj = np.arange(128)[:, None]
    ii = np.arange(128)[None, :]
    tri[:, 0, :] = (jj <= ii)
    tri[:, 1, :] = (jj >= ii)
    tri[0:64, 2, 0:64] = (jj[0:64] <= ii[:, 0:64])
    tri[0:64, 3, 0:64] = (jj[0:64] >= ii[:, 0:64])
    m64 = np.ones((128, 512), f32)
    m64[:, ::64] = 0.0
    return dict(rcos=cos, rsin=sin, perm=perm, wmask=wmask, tri=tri, m64=m64)


_PROG = {}


def _get_prog(nl):
    if nl not in _PROG:
        _PROG[nl] = build_program(nl)
    return _PROG[nl]


def kernel(NLAYERS=4, **inp):
    f32 = np.float32
    g = {k: np.asarray(v) for k, v in inp.items()}
    nc, counts = _get_prog(NLAYERS)
    common = dict(_consts())
    common["ada_w"] = np.ascontiguousarray(g["ada_w"].reshape(4, 8, 128, 6144).transpose(0, 2, 1, 3))
    common["ada_b"] = np.ascontiguousarray(g["ada_b"].reshape(4, 48, 128).transpose(2, 0, 1))
    common["nrm"] = np.ascontiguousarray(np.stack([g["norm_mix"], g["norm_ffn"]], axis=1).reshape(4, 2, 8, 128).transpose(3, 0, 1, 2))
    common["fnorm"] = _pc(g["final_norm"])
    common["w_up"] = np.ascontiguousarray(g["ffn_w_up"].reshape(4, 8, 128, 5632).transpose(0, 2, 1, 3))
    common["w_dn"] = np.ascontiguousarray(g["ffn_w_down"].reshape(4, 22, 128, 1024).transpose(0, 2, 1, 3))
    common["ffn_cw"] = np.ascontiguousarray(g["ffn_conv_w"].reshape(4, 3, 44, 128).transpose(3, 0, 1, 2))
    common["ffn_cb"] = np.ascontiguousarray(g["ffn_conv_b"].reshape(4, 44, 128).transpose(2, 0, 1))
    wq = g["win_w_qkv"][0]
    qcols = wq[:, 0:1024]
    kcols = wq[:, 1024:1280]
    vcols = wq[:, 1280:1536]
    kdup = np.concatenate([np.concatenate([kcols[:, h * 64:(h + 1) * 64]] * 2, axis=1) for h in range(4)], axis=1)
    common["w0_qk"] = _fm(np.concatenate([qcols, kdup], axis=1))
    common["w0_v"] = _fm(vcols)
    common["w0_o"] = _fm(g["win_w_o"][0])
    common["sink"] = np.ascontiguousarray(g["win_sink"][0])
    wg = g["gla_w_qkvr"][0]
    common["w1_qk"] = _fm(wg[:, 0:1024])
    common["w1_v"] = _fm(wg[:, 1024:2048])
    common["w1_r"] = _fm(wg[:, 2048:3072])
    common["w1_g1"] = _fm(np.concatenate([g["gla_w_gf1"][0], g["gla_w_gb1"][0]], axis=1))
    g2 = np.zeros((32, 1024), f32)
    g2[0:16, 0:512] = g["gla_w_gf2"][0]
    g2[16:32, 512:1024] = g["gla_w_gb2"][0]
    common["w1_g2"] = g2
    common["w1_gb"] = _pc(np.concatenate([g["gla_b_gf"][0], g["gla_b_gb"][0]]))
    common["w1_gn"] = _pc(g["gla_norm"][0])
    common["w1_o"] = _fm(g["gla_w_o"][0])
    wd = g["diff_w_qkv"][0]
    common["w2_qk"] = _fm(wd[:, 0:2048])
    common["w2_v"] = _fm(wd[:, 2048:3072])
    common["w2_o"] = _fm(g["diff_w_o"][0])
    common["lqk"] = np.ascontiguousarray(np.stack([g["diff_lq1"][0], g["diff_lk1"][0], g["diff_lq2"][0], g["diff_lk2"][0]]))
    common["w2_gn"] = np.ascontiguousarray(g["diff_norm"][0].reshape(128, 1))
    common.update(_host_ssd_common(g))
    in_maps = []
    for b in range(8):
        m = dict(common)
        xs = g["x_sample"][b]
        xp = g["x_prompt"][2 * b:2 * b + 2].reshape(512, 1024)
        xa = np.concatenate([xs, xp], axis=0)
        m["xT"] = np.ascontiguousarray(xa.T.reshape(8, 128, T).transpose(1, 0, 2))
        cond = np.stack([g["c"][b], g["c_ctx"]], axis=1)
        m["condT"] = np.ascontiguousarray(cond.reshape(8, 128, 2).transpose(1, 0, 2))
        ck = g["cache_win_k"][b, 0]
        kT = ck.transpose(2, 1, 0)
        m["ck0"] = np.ascontiguousarray(np.concatenate([kT, kT], axis=0))
        m["cv0"] = np.ascontiguousarray(g["cache_win_v"][b, 0].reshape(512, 256))
        m["s1f"] = np.ascontiguousarray(g["state_gla_fwd"][b, 0].transpose(1, 0, 2))
        m["s1b"] = np.ascontiguousarray(g["state_gla_bwd"][b, 0].transpose(1, 0, 2))
        dk = g["cache_diff_k"][b, 0]
        m["ck2"] = np.ascontiguousarray(dk.transpose(2, 3, 1, 0).reshape(128, 8, 512))
        m["cv2"] = np.ascontiguousarray(g["cache_diff_v"][b, 0].reshape(512, 1024))
        m.update(_host_ssd_core(g, b))
        in_maps.append(m)
    import os
    ncores = int(os.environ.get("KCORES", "8"))
    ktrace = os.environ.get("KTRACE", "") == "1"
    res = run_bass_kernel_spmd(nc, in_maps[:ncores], core_ids=list(range(ncores)), trace=ktrace) if ktrace else run_bass_kernel_spmd(nc, in_maps[:ncores], core_ids=list(range(ncores)))
    if ktrace:
        print("EXEC_TIME_NS", res.exec_time_ns)
    R = list(res.results) + [res.results[0]] * (8 - ncores)
    y_prompt = np.zeros((16, 256, 1024), f32)
    y_sample = np.zeros((8, 4096, 1024), f32)
    win_k = np.zeros((16, 1, 256, 4, 64), f32)
    win_v = np.zeros((16, 1, 256, 4, 64), f32)
    gla_f = np.zeros((16, 1, 4, 128, 256), f32)
    gla_b = np.zeros((16, 1, 4, 128, 256), f32)
    diff_k = np.zeros((16, 1, 256, 8, 2, 64), f32)
    diff_v = np.zeros((16, 1, 256, 8, 128), f32)
    ssd_f = np.zeros((16, 1, 32, 64, 128), f32)
    ssd_b = np.zeros((16, 1, 32, 64, 128), f32)
    for b in range(8):
        r = R[b]
        y = r["yT"].transpose(1, 0, 2).reshape(1024, T).T
        y_sample[b] = y[0:4096]
        y_prompt[2 * b:2 * b + 2] = y[4096:].reshape(2, 256, 1024)
        wk = r["o_wk"]
        win_k[2 * b:2 * b + 2, 0] = wk.transpose(2, 0, 1).reshape(2, 256, 4, 64)
        win_v[2 * b:2 * b + 2, 0] = r["o_wv"].reshape(2, 256, 4, 64)
        gla_f[2 * b:2 * b + 2, 0] = r["o_gf"].transpose(0, 2, 1, 3)
        gla_b[2 * b:2 * b + 2, 0] = r["o_gb"].transpose(0, 2, 1, 3)
        dk = r["o_dk"]
        diff_k[2 * b:2 * b + 2, 0] = dk.transpose(2, 0, 1).reshape(2, 256, 8, 2, 64)
        diff_v[2 * b:2 * b + 2, 0] = r["o_dv"].reshape(2, 256, 8, 128)
        _host_ssd_out(r, b, ssd_f, ssd_b)
    return (y_prompt, y_sample, win_k, win_v, gla_f, gla_b, diff_k, diff_v, ssd_f, ssd_b)


def _host_ssd_common(g):
    w = g["ssd_w_in"][0]
    out = {}
    out["w3_z"] = _fm(w[:, 0:2048])
    out["w3_xbc"] = _fm(w[:, 2048:5120])
    out["w3_dt"] = _fm(w[:, 5120:5184])
    out["w3_o"] = _fm(g["ssd_w_out"][0])
    out["ssd_cw"] = np.ascontiguousarray(g["ssd_conv_w"][0].reshape(3, 24, 128).transpose(2, 0, 1))
    out["ssd_cb"] = _pc(g["ssd_conv_b"][0])
    out["ssd_vec"] = np.ascontiguousarray(np.stack([g["ssd_a_log_f"][0], g["ssd_a_log_b"][0], g["ssd_dt_bias_f"][0], g["ssd_dt_bias_b"][0], g["ssd_d"][0]]))
    out["ssd_gn"] = np.ascontiguousarray(g["ssd_norm"][0])
    return out


def _host_ssd_core(g, b):
    return {"s3f": np.ascontiguousarray(g["state_ssd_fwd"][b, 0].transpose(2, 0, 1).reshape(128, 2048)),
            "s3b": np.ascontiguousarray(g["state_ssd_bwd"][b, 0].transpose(2, 0, 1).reshape(128, 2048))}


def _host_ssd_out(r, b, ssd_f, ssd_b):
    ssd_f[2 * b:2 * b + 2, 0] = r["o_sf"].reshape(2, 128, 32, 64).transpose(0, 2, 3, 1)
    ssd_b[2 * b:2 * b + 2, 0] = r["o_sb"].reshape(2, 128, 32, 64).transpose(0, 2, 3, 1)
```

```python
import numpy as np
import concourse.bass as bass
import concourse.mybir as mybir
from concourse.bass_utils import run_bass_kernel_spmd

F32 = mybir.dt.float32
BF16 = mybir.dt.bfloat16
AF = mybir.ActivationFunctionType
ALU = mybir.AluOpType
AX = mybir.AxisListType
AP = bass.AP

SB_BASE = 16512
SB_TOP = 229344
EPOCH = 30000


class Buf:
    __slots__ = ("w", "rs", "name")

    def __init__(self, name=""):
        self.w = None
        self.rs = []
        self.name = name


class Op:
    __slots__ = ("eng", "fn", "deps", "dma", "ms", "sem", "val", "need", "prev")

    def __init__(self, eng, fn, dma):
        self.eng = eng
        self.fn = fn
        self.dma = dma
        self.deps = []
        self.ms = None
        self.sem = None
        self.val = None
        self.need = False
        self.prev = None


class Sched:
    ENGS = ("pe", "act", "dve", "pool", "sp")

    def __init__(self, nc):
        self.nc = nc
        self.ops = {e: [] for e in self.ENGS}
        self.dmas_since_barrier = []
        self.all_dmas = []
        self.nps = 0
        self.nacc = 0
        self.psum = []
        for i in range(8):
            t = nc.alloc_psum_tensor("psb%d" % i, [128, 512], F32)
            self.psum.append((t, Buf("ps%d" % i)))
        self.sb_off = SB_BASE
        self.sb_mark = SB_BASE
        self.nalloc = 0

    def alloc(self, shape, dtype, name=None):
        nbytes = int(np.prod(shape[1:])) * (4 if dtype == F32 else 2)
        nbytes = (nbytes + 63) // 64 * 64
        off = self.sb_off
        assert off + nbytes <= SB_TOP, "SBUF overflow %d" % (off + nbytes - SB_TOP)
        self.sb_off += nbytes
        self.nalloc += 1
        t = self.nc.alloc_sbuf_tensor_at("sb%d_%s" % (self.nalloc, name or "t"), list(shape), dtype, offset=off)
        return t

    def mark(self):
        self.sb_mark = self.sb_off

    def release(self):
        self.sb_off = self.sb_mark

    def ps(self, pool="a"):
        if pool == "a":
            t, b = self.psum[self.nps % 6]
            self.nps += 1
        else:
            t, b = self.psum[6 + self.nacc % 2]
            self.nacc += 1
        return t, b

    def op(self, eng, fn, reads=(), writes=(), dma=False):
        o = Op(eng, fn, dma)
        deps = {}
        for b in reads:
            d = b.w
            if d is not None:
                if (not dma) and (not d.dma) and d.eng == eng and eng == "pe":
                    continue
                deps[id(d)] = d
        for b in writes:
            cand = list(b.rs)
            if b.w is not None:
                cand.append(b.w)
            for d in cand:
                if d is o:
                    continue
                if (not dma) and (not d.dma) and d.eng == eng:
                    continue
                deps[id(d)] = d
        o.deps = list(deps.values())
        for b in reads:
            if not dma:
                b.rs = [r for r in b.rs if r.dma or r.eng != eng]
            b.rs.append(o)
        for b in writes:
            b.w = o
            b.rs = []
        self.ops[eng].append(o)
        if dma:
            self.dmas_since_barrier.append(o)
            self.all_dmas.append(o)
        return o

    def barrier(self):
        lasts = []
        for e in self.ENGS:
            for o in reversed(self.ops[e]):
                if not o.dma and o.fn is not None:
                    lasts.append(o)
                    break
        deps = lasts + self.dmas_since_barrier
        self.dmas_since_barrier = []
        for e in self.ENGS:
            o = Op(e, None, False)
            o.deps = list(deps)
            self.ops[e].append(o)

    def dma(self, out, in_, reads=(), writes=(), eng=None):
        if eng is None:
            eng = "pool" if type(out.tensor).__name__.startswith("DRam") else "sp"
        return self.op(eng, lambda e: e.dma_start(out=out, in_=in_), reads, writes, dma=True)

    def mm(self, out, lhsT, rhs, start, stop, reads=(), writes=()):
        return self.op("pe", lambda e: e.matmul(out, lhsT, rhs, start=start, stop=stop), reads, writes)

    def emit(self):
        nc = self.nc
        for e in self.ENGS:
            for o in self.ops[e]:
                for d in o.deps:
                    d.need = True
        sem_ctx = []
        import contextlib
        with contextlib.ExitStack() as st:
            tl = {}
            for e in ("pe", "act", "dve", "pool"):
                tl[e] = [st.enter_context(nc.semaphore("tl_%s_%d" % (e, i))) for i in range(5)]
            npool = {"sp": 36, "pool": 36, "act": 8}
            dpool = {e: [st.enter_context(nc.semaphore("dq_%s_%d" % (e, i))) for i in range(n)] for e, n in npool.items()}
            for e in self.ENGS:
                m = 0
                k = 0
                for o in self.ops[e]:
                    if o.fn is None:
                        continue
                    if o.dma:
                        P = len(dpool[e])
                        o.sem = dpool[e][k % P]
                        o.val = 16 * (k // P + 1)
                        k += 1
                    elif o.need:
                        o.sem = tl[e][m // EPOCH]
                        o.val = m % EPOCH + 1
                        m += 1
                assert m < EPOCH * 5, (e, m)
            final = {}
            for e, n in npool.items():
                for o in self.ops[e]:
                    if o.dma:
                        final[id(o.sem)] = (o.sem, o.val)

            def stream(e, eng):
                seen = {}

                def wait(sem, val):
                    if seen.get(id(sem), 0) < val:
                        eng.wait_ge(sem, val)
                        seen[id(sem)] = val

                for o in self.ops[e]:
                    for d in o.deps:
                        wait(d.sem, d.val)
                    if o.fn is None:
                        continue
                    if o.dma and o.val > 16:
                        wait(o.sem, o.val - 16)
                    ins = o.fn(eng)
                    if o.dma:
                        ins.then_inc(o.sem, 16)
                    elif o.need:
                        ins.then_inc(o.sem, 1)
                if e == "sp":
                    for sem, val in final.values():
                        wait(sem, val)

            with nc.Block() as block:
                @block.tensor
                def _(eng):
                    stream("pe", eng)

                @block.scalar
                def _(eng):
                    stream("act", eng)

                @block.vector
                def _(eng):
                    stream("dve", eng)

                @block.gpsimd
                def _(eng):
                    stream("pool", eng)

                @block.sync
                def _(eng):
                    stream("sp", eng)
        return {e: len(v) for e, v in self.ops.items()}


def ins(method, *a, **kw):
    return lambda e: getattr(e, method)(*a, **kw)


T = 4608
LS = 4096
TILES = [(i * 512, 512, 0, 0, 4096) for i in range(8)] + [(4096, 256, 1, 4096, 4352), (4352, 256, 1, 4352, 4608)]
EPS = 1e-6
DFF = 2816
LAM_INIT = 0.8 - 0.6 * float(np.exp(-0.3 * 2))


def build_program(NLAYERS=4, dbg=False):
    nc = bass.Bass("TRN2", target_bir_lowering=False)
    s = Sched(nc)
    I = {}
    O = {}

    def din(name, shape):
        I[name] = nc.dram_tensor(name, list(shape), F32, kind="ExternalInput").ap()
        return I[name]

    def dout(name, shape):
        O[name] = nc.dram_tensor(name, list(shape), F32, kind="ExternalOutput").ap()
        return O[name]

    def dscr(name, shape, dt):
        return nc.dram_tensor(name, list(shape), dt).ap()

    xT = din("xT", [128, 8, T])
    condT = din("condT", [128, 8, 2])
    ada_w = din("ada_w", [4, 128, 8, 6144])
    ada_b = din("ada_b", [128, 4, 48])
    nrm = din("nrm", [128, 4, 2, 8])
    fnorm = din("fnorm", [128, 8])
    w_up = din("w_up", [4, 128, 8, 5632])
    w_dn = din("w_dn", [4, 128, 22, 1024])
    ffn_cw = din("ffn_cw", [128, 4, 3, 44])
    ffn_cb = din("ffn_cb", [128, 4, 44])
    perm_in = din("perm", [128, 128])
    rcos = din("rcos", [128, LS])
    rsin = din("rsin", [128, LS])
    wmask_in = din("wmask", [128, 6, 512])
    tri_in = din("tri", [128, 4, 128])
    m64_in = din("m64", [128, 512])
    w0_qk = din("w0_qk", [128, 8, 1536])
    w0_v = din("w0_v", [128, 8, 256])
    w0_o = din("w0_o", [128, 8, 1024])
    sink_in = din("sink", [16])
    ck0 = din("ck0", [128, 4, 512])
    cv0 = din("cv0", [512, 256])
    w1_qk = din("w1_qk", [128, 8, 1024])
    w1_v = din("w1_v", [128, 8, 1024])
    w1_r = din("w1_r", [128, 8, 1024])
    w1_g1 = din("w1_g1", [128, 8, 32])
    w1_g2 = din("w1_g2", [32, 1024])
    w1_gb = din("w1_gb", [128, 8])
    w1_gn = din("w1_gn", [128, 2])
    w1_o = din("w1_o", [128, 8, 1024])
    s1f = din("s1f", [128, 4, 256])
    s1b = din("s1b", [128, 4, 256])
    w2_qk = din("w2_qk", [128, 8, 2048])
    w2_v = din("w2_v", [128, 8, 1024])
    w2_o = din("w2_o", [128, 8, 1024])
    lqk = din("lqk", [4, 64])
    w2_gn = din("w2_gn", [128, 1])
    ck2 = din("ck2", [128, 8, 512])
    cv2 = din("cv2", [512, 1024])

    w3_z = din("w3_z", [128, 8, 2048])
    w3_xbc = din("w3_xbc", [128, 8, 3072])
    w3_dt = din("w3_dt", [128, 8, 64])
    w3_o = din("w3_o", [128, 16, 1024])
    ssd_cw = din("ssd_cw", [128, 3, 24])
    ssd_cb = din("ssd_cb", [128, 24])
    ssd_vec = din("ssd_vec", [5, 32])
    ssd_gn = din("ssd_gn", [2048])
    s3f = din("s3f", [128, 2048])
    s3b = din("s3b", [128, 2048])

    yT = dout("yT", [128, 8, T])
    o_sf = dout("o_sf", [2, 128, 2048])
    o_sb = dout("o_sb", [2, 128, 2048])
    o_wk = dout("o_wk", [4, 64, 512])
    o_wv = dout("o_wv", [512, 256])
    o_gf = dout("o_gf", [2, 128, 4, 256])
    o_gb = dout("o_gb", [2, 128, 4, 256])
    o_dk = dout("o_dk", [8, 128, 512])
    o_dv = dout("o_dv", [512, 1024])

    X = dscr("X", [128, 8, T], F32)
    U = dscr("U", [128, 44, T], BF16)
    QK = dscr("QK", [128, 16, T], BF16)
    VS = dscr("VS", [T, 1024], BF16)
    OS = dscr("OS", [128, 8, T], BF16)
    OS2 = dscr("OS2", [128, 8, T], BF16)
    RS = dscr("RS", [128, 8, T], BF16)
    QD = dscr("QD", [2, 128, 4, T], BF16)
    KD = dscr("KD", [2, 128, 4, T], BF16)
    KR = dscr("KR", [T, 8, 128], BF16)
    ZS = dscr("ZS", [T, 2048], BF16)
    DTS = dscr("DTS", [T, 128], F32)
    XTs = dscr("XTs", [T, 2048], BF16)
    BTs = dscr("BTs", [T, 512], BF16)
    YD = [dscr("YD0", [T, 2048], BF16), dscr("YD1", [T, 2048], BF16)]

    NT = len(TILES)
    import os
    STOP = int(os.environ.get("KSTOP", "99"))

    def rows_view(dr, rowlen, r0, nb, c0, w):
        return AP(dr.tensor, r0 * rowlen + c0, [[rowlen, 128], [128 * rowlen, nb], [1, w]])

    def kr_view(r0, nb, j0, nj):
        return AP(KR.tensor, r0 * 1024 + j0 * 128, [[1024, 128], [128 * 1024, nb], [128, nj], [1, 128]])
    import os
    KQ = os.environ.get("KQ", "")

    def tb(name):
        return [Buf(name + str(i)) for i in range(NT)]

    xT_b = tb("xT")
    X_b = tb("X")

    MODS = s.alloc([128, 4, 6, 8, 2], F32, "mods")
    GS = s.alloc([128, 4, 2, 8, 2], F32, "gs")
    NRM = s.alloc([128, 4, 2, 8], F32, "nrm")
    FNRM = s.alloc([128, 8], F32, "fnrm")
    ones_bf = s.alloc([128, 128], BF16, "ones")
    ones_f = s.alloc([128, 128], F32, "onesf")
    ident_bf = s.alloc([128, 128], BF16, "ident")
    perm_bf = s.alloc([128, 128], BF16, "perm")
    tri_bf = s.alloc([128, 4, 128], BF16, "tri")
    m64 = s.alloc([128, 512], F32, "m64")
    cb_const = Buf("consts")
    stage = s.alloc([128, 4, 128], F32, "stage")
    bstage = Buf()

    s.dma(NRM[:], nrm, writes=[cb_const])
    s.dma(FNRM[:], fnorm, writes=[cb_const])
    s.dma(m64[:], m64_in, writes=[cb_const])
    s.op("pool", ins("memset", ones_f[:], 1.0), writes=[cb_const])
    s.op("dve", ins("tensor_copy", out=ones_bf[:], in_=ones_f[:]), reads=[cb_const], writes=[cb_const])
    s.dma(stage[:, 0, :], perm_in, writes=[bstage])
    s.op("dve", ins("tensor_copy", out=perm_bf[:], in_=stage[:, 0, :]), reads=[bstage], writes=[cb_const])
    s.op("pool", ins("memset", stage[:, 1, :], 1.0), reads=[], writes=[bstage])
    s.op("pool", ins("affine_select", out=stage[:, 1, :], in_=stage[:, 1, :], pattern=[[-1, 128]], compare_op=ALU.is_equal, fill=0.0, base=0, channel_multiplier=1), reads=[bstage], writes=[bstage])
    s.op("dve", ins("tensor_copy", out=ident_bf[:], in_=stage[:, 1, :]), reads=[bstage], writes=[cb_const])
    s.barrier()
    s.dma(stage[:], tri_in, writes=[bstage])
    s.op("dve", ins("tensor_copy", out=tri_bf[:], in_=stage[:]), reads=[bstage], writes=[cb_const])
    s.mark()

    def mod_ap(l, k, c, cond):
        return MODS[:, l, k, c, cond:cond + 1]

    def prologue():
        s.barrier()
        s.release()
        sc = s.alloc([128, 8, 2], F32, "sc")
        scb = Buf()
        adb = s.alloc([128, 4, 48], F32, "adb")
        adbb = Buf()
        s.dma(sc[:], condT, writes=[scb])
        s.dma(adb[:], ada_b, writes=[adbb])
        s.op("act", ins("activation", out=sc[:], in_=sc[:], func=AF.Silu), reads=[scb], writes=[scb])
        wb = [s.alloc([128, 8, 1024], F32, "adaw%d" % i) for i in range(2)]
        wbb = [[Buf() for _ in range(2)] for _ in range(2)]
        it = 0
        for l in range(NLAYERS):
            for g in range(6):
                w = wb[it % 2]
                bb = wbb[it % 2]
                for hh in range(2):
                    s.dma(w[:, hh * 4:(hh + 1) * 4, :], ada_w[l, :, hh * 4:(hh + 1) * 4, g * 1024:(g + 1) * 1024], writes=[bb[hh]], eng=("sp" if hh == 0 else "act"))
                ps, pb = s.ps()
                for cc in range(8):
                    for kc in range(8):
                        s.mm(ps[:, cc * 2:cc * 2 + 2], w[:, kc, cc * 128:(cc + 1) * 128], sc[:, kc, :], kc == 0, kc == 7, reads=[bb[kc // 4], scb], writes=[pb])
                s.op("dve", ins("tensor_tensor",
                    out=MODS[:, l, g, :, :], in0=ps[:, 0:16].rearrange("p (c t) -> p c t", t=2),
                    in1=adb[:, l, g * 8:(g + 1) * 8].unsqueeze(2).to_broadcast([128, 8, 2]), op=ALU.add),
                    reads=[pb, adbb], writes=[cb_const])
                it += 1
            for which in range(2):
                k = 1 if which == 0 else 4
                s.op("dve", ins("scalar_tensor_tensor",
                    out=GS[:, l, which, :, :], in0=MODS[:, l, k, :, :], scalar=1.0,
                    in1=NRM[:, l, which, :].unsqueeze(2).to_broadcast([128, 8, 2]), op0=ALU.add, op1=ALU.mult),
                    reads=[cb_const], writes=[cb_const])

    def load_w(dst, src, nk, split=1):
        bufs = []
        for kc in range(nk):
            b = Buf()
            s.dma(dst[:, kc, :], src[:, kc, :], writes=[b], eng="pool")
            bufs.append(b)
        return bufs

    class NormCtx:
        def __init__(self, nbuf=2):
            self.nbuf = nbuf
            self.xt = [(s.alloc([128, 8, 512], F32, "nxt"), Buf()) for _ in range(nbuf)]
            self.h = [(s.alloc([128, 8, 512], BF16, "nh"), Buf()) for _ in range(nbuf)]
            self.sq = (s.alloc([128, 8, 512], BF16, "nsq"), Buf())
            self.tmp = (s.alloc([128, 8, 512], F32, "ntmp"), Buf())
            self.rstd = [(s.alloc([128, 512], F32, "nrs"), Buf()) for _ in range(2)]
            self.k = 0

    def norm_tile(ctx, l, which, ti, Xsrc, Xsrc_b, final_out=None):
        t0, n, cond, _, _ = TILES[ti]
        k = ctx.k
        ctx.k += 1
        xt, xtb = ctx.xt[k % ctx.nbuf]
        h, hb = ctx.h[k % ctx.nbuf]
        sq, sqb = ctx.sq
        tmp, tmpb = ctx.tmp
        rstd, rb = ctx.rstd[k % 2]
        s.dma(xt[:, :, :n], Xsrc[:, :, t0:t0 + n], reads=[Xsrc_b[ti]], writes=[xtb])
        s.op("act", ins("activation", out=sq[:, :, :n], in_=xt[:, :, :n], func=AF.Square), reads=[xtb], writes=[sqb])
        ps, pb = s.ps()
        for c in range(8):
            s.mm(ps[:, :n], ones_bf[:], sq[:, c, :n], c == 0, c == 7, reads=[sqb, cb_const], writes=[pb])
        s.op("act", ins("activation", out=rstd[:, :n], in_=ps[:, :n], func=AF.Sqrt, scale=1.0 / 1024, bias=EPSB[:, 0:1]), reads=[pb], writes=[rb])
        s.op("dve", ins("reciprocal", out=rstd[:, :n], in_=rstd[:, :n]), reads=[rb], writes=[rb])
        s.op("dve", ins("tensor_tensor", out=tmp[:, :, :n], in0=xt[:, :, :n], in1=rstd[:, :n].unsqueeze(1).to_broadcast([128, 8, n]), op=ALU.mult),
             reads=[xtb, rb], writes=[tmpb])
        if final_out is None:
            for c in range(8):
                s.op("act", ins("activation", out=h[:, c, :n], in_=tmp[:, c, :n], func=AF.Identity,
                                                          scale=GS[:, l, which, c, cond:cond + 1], bias=mod_ap(l, 0 if which == 0 else 3, c, cond)),
                     reads=[tmpb, cb_const], writes=[hb])
            return h, hb
        else:
            for c in range(8):
                s.op("act", ins("activation", out=xt[:, c, :n], in_=tmp[:, c, :n], func=AF.Identity, scale=FNRM[:, c:c + 1]),
                     reads=[tmpb, cb_const], writes=[xtb])
            s.dma(final_out[:, :, t0:t0 + n], xt[:, :, :n], reads=[xtb])
            return None, None

    EPSB = s.alloc([128, 1], F32, "epsb")
    s.op("pool", ins("memset", EPSB[:], EPS), writes=[cb_const])
    s.mark()

    class ResCtx:
        def __init__(self):
            self.xo = [(s.alloc([128, 8, 512], F32, "xo"), Buf()) for _ in range(2)]
            self.k = 0

    def outproj_resid(rctx, l, gate_k, ti, W, wbufs, nk, rhs, rhsbufs, Xsrc, Xsrc_b, Xdst, Xdst_b):
        t0, n, cond, _, _ = TILES[ti]
        xo, xob = rctx.xo[rctx.k % 2]
        rctx.k += 1
        s.dma(xo[:, :, :n], Xsrc[:, :, t0:t0 + n], reads=[Xsrc_b[ti]], writes=[xob])
        for dc in range(8):
            ps, pb = s.ps()
            for kc in range(nk):
                s.mm(ps[:, :n], W[:, kc, dc * 128:(dc + 1) * 128], rhs[:, kc, :n], kc == 0, kc == nk - 1,
                     reads=[wbufs[kc]] + list(rhsbufs), writes=[pb])
            s.op("dve", ins("scalar_tensor_tensor", out=xo[:, dc, :n], in0=ps[:, :n], scalar=mod_ap(l, gate_k, dc, cond),
                                                                      in1=xo[:, dc, :n], op0=ALU.mult, op1=ALU.add),
                 reads=[pb, xob, cb_const], writes=[xob])
        s.dma(Xdst[:, :, t0:t0 + n], xo[:, :, :n], reads=[xob], writes=[Xdst_b[ti]])

    def ffn(l, Xsrc, Xsrc_b):
        s.barrier()
        s.release()
        Wup = s.alloc([128, 8, 5632], BF16, "wup")
        wb = load_w(Wup, w_up[l], 8)
        nctx = NormCtx()
        ub = [(s.alloc([128, 11, 512], BF16, "ub"), [Buf() for _ in range(11)]) for _ in range(2)]
        U_b = [[Buf() for _ in range(4)] for _ in range(NT)]
        ku = 0
        ev = 0
        for ti in range(NT):
            t0, n, cond, _, _ = TILES[ti]
            h, hb = norm_tile(nctx, l, 1, ti, Xsrc, Xsrc_b)
            for grp in range(4):
                ut, utb = ub[ku % 2]
                ku += 1
                for cc in range(11):
                    col = grp * 11 + cc
                    ps, pb = s.ps()
                    for kc in range(8):
                        s.mm(ps[:, :n], Wup[:, kc, col * 128:(col + 1) * 128], h[:, kc, :n], kc == 0, kc == 7, reads=[wb[kc], hb], writes=[pb])
                    if ev % 2 == 0:
                        s.op("act", ins("activation", out=ut[:, cc, :n], in_=ps[:, :n], func=AF.Identity), reads=[pb], writes=[utb[cc]])
                    else:
                        s.op("dve", ins("tensor_copy", out=ut[:, cc, :n], in_=ps[:, :n]), reads=[pb], writes=[utb[cc]])
                    ev += 1
                s.dma(U[:, grp * 11:(grp + 1) * 11, t0:t0 + n], ut[:, :, :n], reads=utb, writes=[U_b[ti][grp]])
        s.barrier()
        s.release()
        Wd = s.alloc([128, 22, 1024], BF16, "wd")
        wdb = load_w(Wd, w_dn[l], 22)
        cw = s.alloc([128, 3, 44], F32, "cw")
        cbias = s.alloc([128, 44], F32, "cbias")
        cwb = Buf()
        s.dma(cw[:], ffn_cw[:, l, :, :], writes=[cwb])
        s.dma(cbias[:], ffn_cb[:, l, :], writes=[cwb])
        ug = s.alloc([128, 44, 514], BF16, "ug")
        ugb = [Buf() for _ in range(4)]
        at = [(s.alloc([128, 22, 512], BF16, "at"), Buf()) for _ in range(2)]
        tmps = [[(s.alloc([128, 512], F32, "ft"), Buf()) for _ in range(3)] for _ in range(4)]
        rctx = ResCtx()
        kp = 0
        for ti in range(NT):
            t0, n, cond, s0, s1 = TILES[ti]
            lo = (t0 - 1) >= s0
            hi = (t0 + n) < s1
            for grp in range(4):
                gs_ = slice(grp * 11, (grp + 1) * 11)
                if not lo:
                    s.op("pool", ins("memset", ug[:, gs_, 0:1], 0.0), writes=[ugb[grp]])
                if not hi:
                    s.op("pool", ins("memset", ug[:, gs_, n + 1:n + 2], 0.0), writes=[ugb[grp]])
                c0 = 0 if lo else 1
                c1 = n + 2 if hi else n + 1
                rd = [U_b[ti][grp]]
                if lo:
                    rd.append(U_b[ti - 1][grp])
                if hi:
                    rd.append(U_b[ti + 1][grp])
                s.dma(ug[:, gs_, c0:c1], U[:, gs_, t0 - 1 + c0:t0 - 1 + c1], reads=rd, writes=[ugb[grp]])
            a, ab = at[ti % 2]
            for cc in range(22):
                res = []
                for half in range(2):
                    col = cc + 22 * half
                    tt, ttb = tmps[kp % 4][half]
                    grp = col // 11
                    s.op("act", ins("activation", out=tt[:, :n], in_=ug[:, col, 1:n + 1], func=AF.Identity,
                                                                      scale=cw[:, 1, col:col + 1], bias=cbias[:, col:col + 1]),
                         reads=[ugb[grp], cwb], writes=[ttb])
                    s.op("dve", ins("scalar_tensor_tensor", out=tt[:, :n], in0=ug[:, col, 0:n], scalar=cw[:, 0, col:col + 1],
                                                                               in1=tt[:, :n], op0=ALU.mult, op1=ALU.add),
                         reads=[ugb[grp], cwb, ttb], writes=[ttb])
                    s.op("dve", ins("scalar_tensor_tensor", out=tt[:, :n], in0=ug[:, col, 2:n + 2], scalar=cw[:, 2, col:col + 1],
                                                                               in1=tt[:, :n], op0=ALU.mult, op1=ALU.add),
                         reads=[ugb[grp], cwb, ttb], writes=[ttb])
                    res.append((tt, ttb))
                sg, sgb = tmps[kp % 4][2]
                kp += 1
                s.op("act", ins("activation", out=sg[:, :n], in_=res[0][0][:, :n], func=AF.Silu), reads=[res[0][1]], writes=[sgb])
                s.op("pool", ins("tensor_tensor", out=a[:, cc, :n], in0=sg[:, :n], in1=res[1][0][:, :n], op=ALU.mult),
                     reads=[sgb, res[1][1]], writes=[ab])
            outproj_resid(rctx, l, 5, ti, Wd, wdb, 22, a, [ab], Xsrc, Xsrc_b, X, X_b)

    def qkv_phase(l, Wqk_src, nqk, Wv_src, nv, Xsrc, Xsrc_b, rope, kout=None, vout=None, post=None):
        s.barrier()
        s.release()
        Wqk = s.alloc([128, 8, nqk * 128], BF16, "wqk")
        wqb = load_w(Wqk, Wqk_src, 8)
        Wv = s.alloc([128, 8, nv], BF16, "wv")
        wvb = load_w(Wv, Wv_src, 8)
        nctx = NormCtx()
        qk = [(s.alloc([128, nqk, 512], BF16, "qk"), [Buf() for _ in range(nqk)]) for _ in range(2)]
        vt = [(s.alloc([128, 4, nv], BF16, "vt"), Buf()) for _ in range(2)]
        qb = [(s.alloc([128, 512], BF16, "qb"), Buf()) for _ in range(2)]
        t12 = [[(s.alloc([128, 512], F32, "rt"), Buf()) for _ in range(2)] for _ in range(2)]
        cs = [(s.alloc([128, 2, 512], F32, "cs"), Buf()) for _ in range(2)]
        kf = [(s.alloc([128, 256], F32, "kf"), Buf()) for _ in range(2)]
        vf = [(s.alloc([128, 512], F32, "vf"), Buf()) for _ in range(2)]
        QK_b = [Buf() for _ in range(NT)]
        VS_b = [Buf() for _ in range(NT)]
        if "L" in KQ:
            return
        kq = 0
        kk = 0
        kv = 0
        for ti in range(NT):
            t0, n, cond, s0, s1 = TILES[ti]
            h, hb = norm_tile(nctx, l, 0, ti, Xsrc, Xsrc_b)
            if "N" in KQ:
                continue
            qkt, qkb = qk[ti % 2]
            dorope = rope and cond == 0 and ("r" not in KQ)
            if dorope:
                cst, csb = cs[ti % 2]
                s.dma(cst[:, 0, :], rcos[:, t0:t0 + n], writes=[csb])
                s.dma(cst[:, 1, :], rsin[:, t0:t0 + n], writes=[csb])
            for cc in range(nqk):
                ps, pb = s.ps()
                for kc in range(8):
                    s.mm(ps[:, :n], Wqk[:, kc, cc * 128:(cc + 1) * 128], h[:, kc, :n], kc == 0, kc == 7, reads=[wqb[kc], hb], writes=[pb])
                if dorope:
                    q_, q_b = qb[kq % 2]
                    t1, t1b = t12[kq % 2][0]
                    t2, t2b = t12[kq % 2][1]
                    kq += 1
                    s.op("act", ins("activation", out=q_[:, :n], in_=ps[:, :n], func=AF.Identity), reads=[pb], writes=[q_b])
                    ps2, pb2 = s.ps()
                    s.mm(ps2[:, :n], perm_bf[:], q_[:, :n], True, True, reads=[q_b, cb_const], writes=[pb2])
                    s.op("dve", ins("tensor_tensor", out=t1[:, :n], in0=q_[:, :n], in1=cst[:, 0, :n], op=ALU.mult),
                         reads=[q_b, csb], writes=[t1b])
                    s.op("dve", ins("tensor_tensor", out=t2[:, :n], in0=ps2[:, :n], in1=cst[:, 1, :n], op=ALU.mult),
                         reads=[pb2, csb], writes=[t2b])
                    s.op("pool", ins("tensor_tensor", out=qkt[:, cc, :n], in0=t1[:, :n], in1=t2[:, :n], op=ALU.add),
                         reads=[t1b, t2b], writes=[qkb[cc]])
                else:
                    if kout is not None and cond == 1 and cc >= kout[0]:
                        kft, kfb = kf[kk % 2]
                        kk += 1
                        rows = kout[2]
                        s.op("dve", ins("tensor_copy", out=kft[:, :n], in_=ps[:, :n]), reads=[pb], writes=[kfb])
                        s.op("act", ins("activation", out=qkt[:, cc, :n], in_=kft[:, :n], func=AF.Identity), reads=[kfb], writes=[qkb[cc]])
                        s.dma(kout[1][cc - kout[0], :, t0 - LS:t0 - LS + n], kft[0:rows, :n], reads=[kfb])
                    else:
                        s.op("act", ins("activation", out=qkt[:, cc, :n], in_=ps[:, :n], func=AF.Identity), reads=[pb], writes=[qkb[cc]])
                if post is not None:
                    post(ti, cc, ps, pb)
            vtt, vtb = vt[ti % 2]
            for tbk in range(n // 128 if "v" not in KQ else 0):
                for vg in range((nv + 511) // 512):
                    w_ = min(512, nv - vg * 512)
                    ps, pb = s.ps()
                    for kc in range(8):
                        s.mm(ps[:, :w_], h[:, kc, tbk * 128:(tbk + 1) * 128], Wv[:, kc, vg * 512:vg * 512 + w_], kc == 0, kc == 7, reads=[wvb[kc], hb], writes=[pb])
                    if vout is not None and cond == 1:
                        vft, vfb = vf[kv % 2]
                        kv += 1
                        s.op("dve", ins("tensor_copy", out=vft[:, :w_], in_=ps[:, :w_]), reads=[pb], writes=[vfb])
                        s.op("act", ins("activation", out=vtt[:, tbk, vg * 512:vg * 512 + w_], in_=vft[:, :w_], func=AF.Identity),
                             reads=[vfb], writes=[vtb])
                        r0 = t0 - LS + tbk * 128
                        s.dma(vout[r0:r0 + 128, vg * 512:vg * 512 + w_], vft[:, :w_], reads=[vfb])
                    else:
                        s.op("act", ins("activation", out=vtt[:, tbk, vg * 512:vg * 512 + w_], in_=ps[:, :w_], func=AF.Identity),
                             reads=[pb], writes=[vtb])
            if "q" not in KQ:
                s.dma(QK[:, 0:nqk, t0:t0 + n], qkt[:, :, :n], reads=qkb, writes=[QK_b[ti]])
            if "v" not in KQ and "s" not in KQ:
              s.dma(rows_view(VS, 1024, t0, n // 128, 0, nv), vtt[:, 0:n // 128, :], reads=[vtb], writes=[VS_b[ti]])

    def attn_phase(l, kind, Xsrc, Xsrc_b):
        s.barrier()
        s.release()
        win = kind == "win"
        nunits = 8
        kbase = 8
        Kt = [(s.alloc([128, T], BF16, "kt"), Buf()) for _ in range(2)]
        Va = [(s.alloc([128, 36, 128], BF16, "va"), Buf()) for _ in range(2)]
        Qt = [(s.alloc([128, LS], BF16, "qt"), Buf()) for _ in range(2)]
        Ot = [(s.alloc([128, LS], BF16, "ot"), Buf()) for _ in range(2)]
        pT = [(s.alloc([128, 512], BF16, "pT"), Buf()) for _ in range(8)]
        rec = [(s.alloc([128, 512], F32, "rec"), Buf()) for _ in range(4)]
        OS_b = [[Buf() for _ in range(nunits)] for _ in range(3)]
        cbuf = Buf()
        if win:
            masks = s.alloc([128, 6, 512], BF16, "masks")
            s.dma(masks[:], wmask_in, writes=[cbuf], eng="pool")
            esink = s.alloc([128, 16], F32, "esink")
            s.dma(esink[:], sink_in.partition_broadcast(128), writes=[cbuf])
            s.op("act", ins("activation", out=esink[:], in_=esink[:], func=AF.Exp), reads=[cbuf], writes=[cbuf])
            for i in range(2):
                s.op("pool", ins("memset", Va[i][0][:, :, 64:128], 1.0), writes=[Va[i][1]])
        else:
            acc = [(s.alloc([128, 512], F32, "acc"), Buf()) for _ in range(4)]
            osb = [(s.alloc([128, 512], F32, "osb"), Buf()) for _ in range(4)]
            sqd = [(s.alloc([128, 512], BF16, "sqd"), Buf()) for _ in range(2)]
            lq = s.alloc([128, 4, 64], F32, "lq")
            lam = s.alloc([128, 4], F32, "lam")
            gsub = s.alloc([128, 1], F32, "gsub")
            s.dma(lq[:], lqk.rearrange("a b -> (a b)").partition_broadcast(128).rearrange("p (a b) -> p a b", a=4), writes=[cbuf])
            s.dma(gsub[:], w2_gn, writes=[cbuf])
            s.op("dve", ins("tensor_tensor", out=lq[:, 0, :], in0=lq[:, 0, :], in1=lq[:, 1, :], op=ALU.mult), reads=[cbuf], writes=[cbuf])
            s.op("dve", ins("tensor_tensor", out=lq[:, 2, :], in0=lq[:, 2, :], in1=lq[:, 3, :], op=ALU.mult), reads=[cbuf], writes=[cbuf])
            s.op("dve", ins("reduce_sum", out=lam[:, 0:1], in_=lq[:, 0, :], axis=AX.X), reads=[cbuf], writes=[cbuf])
            s.op("dve", ins("reduce_sum", out=lam[:, 1:2], in_=lq[:, 2, :], axis=AX.X), reads=[cbuf], writes=[cbuf])
            s.op("act", ins("activation", out=lam[:, 0:2], in_=lam[:, 0:2], func=AF.Exp), reads=[cbuf], writes=[cbuf])
            s.op("dve", ins("tensor_tensor", out=lam[:, 2:3], in0=lam[:, 1:2], in1=lam[:, 0:1], op=ALU.subtract), reads=[cbuf], writes=[cbuf])
            s.op("dve", ins("tensor_scalar", out=lam[:, 2:3], in0=lam[:, 2:3], scalar1=-LAM_INIT, scalar2=None, op0=ALU.add), reads=[cbuf], writes=[cbuf])
            s.op("dve", ins("tensor_scalar", out=lam[:, 3:4], in0=gsub[:], scalar1=1.0 - LAM_INIT, scalar2=None, op0=ALU.mult), reads=[cbuf], writes=[cbuf])
        ku = 0
        kp = 0
        kr = 0
        ka = 0
        SEQS = [(0, 4096, 0), (4096, 256, 1), (4352, 256, 2)]
        for (q0, L, si) in SEQS:
            sample = si == 0
            nkb_lat = L // 128
            for u in range(nunits):
                kt, ktb = Kt[ku % 2]
                va, vab = Va[ku % 2]
                qt, qtb = Qt[ku % 2]
                ot, otb = Ot[ku % 2]
                ku += 1
                s.dma(qt[:, 0:L], QK[:, u, q0:q0 + L], writes=[qtb])
                if win:
                    g = u // 2
                    s.dma(kt[:, 0:L], QK[:, kbase + g, q0:q0 + L], writes=[ktb])
                    s.dma(va[:, 0:nkb_lat, 0:64], rows_view(VS, 1024, q0, nkb_lat, g * 64, 64), writes=[vab])
                    if sample:
                        s.dma(kt[:, L:L + 512], ck0[:, g, :], writes=[ktb], eng="pool")
                        s.dma(va[:, 32:36, 0:64], rows_view(cv0, 256, 0, 4, g * 64, 64), writes=[vab], eng="pool")
                else:
                    s.dma(kt[:, 0:L], QK[:, kbase + u, q0:q0 + L], writes=[ktb])
                    s.dma(va[:, 0:nkb_lat, :], rows_view(VS, 1024, q0, nkb_lat, u * 128, 128), writes=[vab])
                    if sample:
                        s.dma(kt[:, L:L + 512], ck2[:, u, :], writes=[ktb], eng="pool")
                        s.dma(va[:, 32:36, :], rows_view(cv2, 1024, 0, 4, u * 128, 128), writes=[vab], eng="pool")
                nq = 512 if sample else 256
                LOOK = 4
                tasks = []
                for qi in range(L // nq):
                    for e_ in range(2):
                        if win and sample:
                            kbs = [(kb, kb - qi * 4 + 1) for kb in range(qi * 4 - 1, qi * 4 + 5) if 0 <= kb < 32] + [(32 + j, None) for j in range(4)]
                        elif sample:
                            kbs = [(kb, None) for kb in range(36)]
                        else:
                            kbs = [(kb, None) for kb in range(2)]
                        for i, (kb, mi) in enumerate(kbs):
                            tasks.append((qi, e_, i, kb, mi, i == len(kbs) - 1))
                stt = {}

                def stage1(tk_):
                    nonlocal kp, ka
                    qi, e_, i, kb, mi, lastb = tk_
                    qs = slice(qi * nq, (qi + 1) * nq)
                    pr = slice(e_ * 64, (e_ + 1) * 64)
                    if i == 0:
                        d_ = {}
                        d_["pso"], d_["pob"] = s.ps("b")
                        if not win:
                            d_["ac"], d_["acb"] = acc[ka % 4]
                            ka += 1
                        stt[(qi, e_)] = d_
                    d_ = stt[(qi, e_)]
                    pss, psb_ = s.ps()
                    s.mm(pss[:, :nq], kt[pr, kb * 128:(kb + 1) * 128], qt[pr, qs], True, True, reads=[ktb, qtb], writes=[psb_])
                    p_, p_b = pT[kp % 8]
                    kp += 1
                    s.op("act", ins("activation", out=p_[:, :nq], in_=pss[:, :nq], func=AF.Exp, scale=0.125), reads=[psb_], writes=[p_b])
                    if mi is not None:
                        s.op("pool" if (kp % 2) else "dve", ins("tensor_tensor", out=p_[:, :nq], in0=p_[:, :nq], in1=masks[:, mi, :nq], op=ALU.mult),
                             reads=[p_b, cbuf], writes=[p_b])
                    if not win:
                        eng = "pool" if e_ == 0 else "dve"
                        ac, acb = d_["ac"], d_["acb"]
                        if i == 0:
                            s.op(eng, ins("tensor_copy", out=ac[:, :nq], in_=p_[:, :nq]), reads=[p_b], writes=[acb])
                        else:
                            s.op(eng, ins("tensor_tensor", out=ac[:, :nq], in0=ac[:, :nq], in1=p_[:, :nq], op=ALU.add), reads=[p_b, acb], writes=[acb])
                    d_[("p", i)] = (p_, p_b)

                def stage2(tk_):
                    nonlocal kr
                    qi, e_, i, kb, mi, lastb = tk_
                    qs = slice(qi * nq, (qi + 1) * nq)
                    pr = slice(e_ * 64, (e_ + 1) * 64)
                    d_ = stt[(qi, e_)]
                    pso, pob = d_["pso"], d_["pob"]
                    p_, p_b = d_.pop(("p", i))
                    s.mm(pso[:, :nq], va[:, kb, :], p_[:, :nq], i == 0, lastb, reads=[vab, p_b], writes=[pob])
                    if not lastb:
                        return
                    if win:
                        hh = u * 2 + e_
                        r_, r_b = rec[kr % 4]
                        kr += 1
                        s.op("dve", ins("tensor_scalar", out=r_[64:128, :nq], in0=pso[64:128, :nq], scalar1=esink[64:128, hh:hh + 1], scalar2=None, op0=ALU.add),
                             reads=[pob, cbuf], writes=[r_b])
                        s.op("dve", ins("reciprocal", out=r_[64:128, :nq], in_=r_[64:128, :nq]), reads=[r_b], writes=[r_b])
                        s.op("dve", ins("tensor_tensor", out=ot[pr, qs], in0=pso[0:64, :nq], in1=r_[64:128, :nq], op=ALU.mult),
                             reads=[pob, r_b], writes=[otb])
                        del stt[(qi, e_)]
                        return
                    ac, acb = d_["ac"], d_["acb"]
                    psd, pdb = s.ps()
                    s.mm(psd[:, :nq], ones_f[:], ac[:, :nq], True, True, reads=[acb, cb_const], writes=[pdb])
                    r_, r_b = rec[kr % 4]
                    kr += 1
                    s.op("dve", ins("reciprocal", out=r_[:, :nq], in_=psd[:, :nq]), reads=[pdb], writes=[r_b])
                    o_, o_b = osb[kr % 4]
                    s.op("dve", ins("tensor_tensor", out=o_[:, :nq], in0=pso[:, :nq], in1=r_[:, :nq], op=ALU.mult), reads=[pob, r_b], writes=[o_b])
                    d_["o"] = (o_, o_b)
                    if e_ == 0:
                        return
                    (o0, o0b) = stt[(qi, 0)]["o"]
                    (o1, o1b) = d_["o"]
                    s.op("dve", ins("scalar_tensor_tensor", out=o0[:, :nq], in0=o1[:, :nq], scalar=lam[:, 2:3], in1=o0[:, :nq], op0=ALU.mult, op1=ALU.add),
                         reads=[o0b, o1b, cbuf], writes=[o0b])
                    sq_, sq_b = sqd[qi % 2]
                    s.op("act", ins("activation", out=sq_[:, :nq], in_=o0[:, :nq], func=AF.Square), reads=[o0b], writes=[sq_b])
                    psn, pnb = s.ps()
                    s.mm(psn[:, :nq], ones_bf[:], sq_[:, :nq], True, True, reads=[sq_b, cb_const], writes=[pnb])
                    s.op("act", ins("activation", out=o1[:, :nq], in_=psn[:, :nq], func=AF.Sqrt, scale=1.0 / 128, bias=EPSB[:, 0:1]), reads=[pnb, o1b], writes=[o1b])
                    s.op("dve", ins("reciprocal", out=o1[:, :nq], in_=o1[:, :nq]), reads=[o1b], writes=[o1b])
                    s.op("dve", ins("scalar_tensor_tensor", out=ot[:, qs], in0=o0[:, :nq], scalar=lam[:, 3:4], in1=o1[:, :nq], op0=ALU.mult, op1=ALU.mult),
                         reads=[o0b, o1b, cbuf], writes=[otb])
                    del stt[(qi, 0)]
                    del stt[(qi, 1)]

                for t_ in range(len(tasks) + LOOK):
                    if t_ < len(tasks):
                        stage1(tasks[t_])
                    if t_ >= LOOK:
                        stage2(tasks[t_ - LOOK])
                s.dma(OS[:, u, q0:q0 + L], ot[:, 0:L], reads=[otb], writes=[OS_b[si][u]])
        return OS_b

    def outproj_phase(l, Wsrc, nk, Osrc, O_bufs_fn, Xsrc, Xsrc_b):
        s.barrier()
        s.release()
        Wo = s.alloc([128, nk, 1024], BF16, "wo")
        wob = load_w(Wo, Wsrc, nk)
        rctx = ResCtx()
        oin = [(s.alloc([128, nk, 512], BF16, "oin"), Buf()) for _ in range(2)]
        for ti in range(NT):
            t0, n, cond, _, _ = TILES[ti]
            o_, o_b = oin[ti % 2]
            s.dma(o_[:, :, :n], Osrc[:, :, t0:t0 + n], writes=[o_b])
            outproj_resid(rctx, l, 2, ti, Wo, wob, nk, o_, [o_b], Xsrc, Xsrc_b, X, X_b)

    def gla_layer(l, Xsrc, Xsrc_b):
        s.barrier()
        s.release()
        Wqk = s.alloc([128, 8, 1024], BF16, "gwqk")
        wqb = load_w(Wqk, w1_qk, 8)
        Wv = s.alloc([128, 8, 1024], BF16, "gwv")
        wvb = load_w(Wv, w1_v, 8)
        Wr = s.alloc([128, 8, 1024], BF16, "gwr")
        wrb = load_w(Wr, w1_r, 8)
        Wg1 = s.alloc([128, 8, 32], BF16, "gwg1")
        wg1b = load_w(Wg1, w1_g1, 8)
        Wg2 = s.alloc([32, 1024], BF16, "gwg2")
        cbuf = Buf()
        s.dma(Wg2[:], w1_g2, writes=[cbuf], eng="pool")
        gb = s.alloc([128, 8], F32, "ggb")
        s.dma(gb[:], w1_gb, writes=[cbuf])
        s.op("dve", ins("tensor_scalar", out=gb[:], in0=gb[:], scalar1=-1.0, scalar2=None, op0=ALU.mult), reads=[cbuf], writes=[cbuf])
        ELt = s.alloc([128, 2, 4, 72], F32, "el")
        nctx = NormCtx(1)
        qkf = [(s.alloc([128, 8, 512], F32, "qkf"), Buf()) for _ in range(1)]
        lg = nctx.xt[0]
        cs_ = (s.alloc([128, 8, 512], F32, "cs"), Buf())
        c2 = nctx.tmp
        ex = [(s.alloc([128, 512], F32, "ex"), Buf()) for _ in range(3)]
        t1b_ = (s.alloc([32, 512], BF16, "t1b"), Buf())
        qd_t = [(s.alloc([128, 2, 4, 512], BF16, "qd"), Buf()) for _ in range(1)]
        kd_t = [(s.alloc([128, 2, 4, 512], BF16, "kd"), Buf()) for _ in range(1)]
        krT = [(s.alloc([128, 512], BF16, "krT"), Buf()) for _ in range(2)]
        krt = [(s.alloc([128, 4, 8, 128], BF16, "krt"), Buf()) for _ in range(1)]
        rt = [(s.alloc([128, 8, 512], BF16, "rt"), Buf()) for _ in range(1)]
        vt = [(s.alloc([128, 4, 1024], BF16, "vt"), Buf()) for _ in range(1)]
        G_b = [Buf() for _ in range(NT)]
        kx = 0
        kk = 0
        for ti in range(NT):
            t0, n, cond, s0, s1 = TILES[ti]
            nch = n // 64
            h, hb = norm_tile(nctx, l, 0, ti, Xsrc, Xsrc_b)
            qf, qfb = qkf[0]
            for cc in range(8):
                ps, pb = s.ps()
                for kc in range(8):
                    s.mm(ps[:, :n], Wqk[:, kc, cc * 128:(cc + 1) * 128], h[:, kc, :n], kc == 0, kc == 7, reads=[wqb[kc], hb], writes=[pb])
                sc_ = (128.0 ** -0.5) if cc < 4 else 1.0
                s.op("act", ins("activation", out=qf[:, cc, :n], in_=ps[:, :n], func=AF.Identity, scale=sc_), reads=[pb], writes=[qfb])
            r_, r_b = rt[0]
            for cc in range(8):
                ps, pb = s.ps()
                for kc in range(8):
                    s.mm(ps[:, :n], Wr[:, kc, cc * 128:(cc + 1) * 128], h[:, kc, :n], kc == 0, kc == 7, reads=[wrb[kc], hb], writes=[pb])
                s.op("act", ins("activation", out=r_[:, cc, :n], in_=ps[:, :n], func=AF.Silu), reads=[pb], writes=[r_b])
            s.dma(RS[:, :, t0:t0 + n], r_[:, :, :n], reads=[r_b], writes=[G_b[ti]])
            v_, v_b = vt[0]
            for tbk in range(n // 128):
                for vg in range(2):
                    ps, pb = s.ps()
                    for kc in range(8):
                        s.mm(ps[:, :512], h[:, kc, tbk * 128:(tbk + 1) * 128], Wv[:, kc, vg * 512:(vg + 1) * 512], kc == 0, kc == 7, reads=[wvb[kc], hb], writes=[pb])
                    s.op("act", ins("activation", out=v_[:, tbk, vg * 512:(vg + 1) * 512], in_=ps[:, :512], func=AF.Identity), reads=[pb], writes=[v_b])
            s.dma(rows_view(VS, 1024, t0, n // 128, 0, 1024), v_[:, 0:n // 128, :], reads=[v_b], writes=[G_b[ti]])
            ps, pb = s.ps()
            for kc in range(8):
                s.mm(ps[0:32, :n], Wg1[:, kc, :], h[:, kc, :n], kc == 0, kc == 7, reads=[wg1b[kc], hb], writes=[pb])
            t1_, t1bb = t1b_
            s.op("act", ins("activation", out=t1_[:, :n], in_=ps[0:32, :n], func=AF.Identity), reads=[pb], writes=[t1bb])
            lgt, lgb = lg
            for j in range(8):
                ps, pb = s.ps()
                s.mm(ps[:, :n], Wg2[:, j * 128:(j + 1) * 128], t1_[:, :n], True, True, reads=[cbuf, t1bb], writes=[pb])
                s.op("act", ins("activation", out=lgt[:, j, :n], in_=ps[:, :n], func=AF.Exp, scale=-1.0, bias=gb[:, j:j + 1]), reads=[pb, cbuf], writes=[lgb])
            s.op("act", ins("activation", out=lgt[:, :, :n], in_=lgt[:, :, :n], func=AF.Ln, bias=1.0, scale=1.0), reads=[lgb], writes=[lgb])
            cst, csb = cs_
            for j in range(8):
                s.op("dve", ins("tensor_tensor_scan", out=cst[:, j, :n], data0=m64[:, :n], data1=lgt[:, j, :n], initial=0.0, op0=ALU.mult, op1=ALU.add),
                     reads=[lgb, cb_const], writes=[csb])
            c2t, c2b = c2
            csv = cst[:, :, :n].rearrange("p j (c t) -> p j c t", t=64)
            lgv = lgt[:, :, :n].rearrange("p j (c t) -> p j c t", t=64)
            c2v = c2t[:, :, :n].rearrange("p j (c t) -> p j c t", t=64)
            nf = 4
            s.op("dve", ins("tensor_tensor", out=c2v[:, 0:nf, :, :], in0=csv[:, 0:nf, :, :], in1=csv[:, 0:nf, :, 63:64].to_broadcast([128, nf, nch, 64]), op=ALU.subtract),
                 reads=[csb], writes=[c2b])
            s.op("dve", ins("tensor_tensor", out=c2v[:, nf:2 * nf, :, :], in0=lgv[:, nf:2 * nf, :, :], in1=csv[:, nf:2 * nf, :, :], op=ALU.subtract),
                 reads=[csb, lgb], writes=[c2b])
            ci0 = t0 // 64
            s.op("act", ins("activation", out=ELt[:, :, :, ci0:ci0 + nch].rearrange("p d h c -> p (d h) c"), in_=cst[:, :, :n].rearrange("p j (c t) -> p j c t", t=64)[:, :, :, 63],
                                               func=AF.Exp, scale=-1.0 / 16), reads=[csb], writes=[cbuf])
            s.op("dve", ins("tensor_tensor", out=lgv[:, nf:2 * nf, :, :], in0=c2v[:, nf:2 * nf, :, :], in1=csv[:, nf:2 * nf, :, 63:64].to_broadcast([128, nf, nch, 64]), op=ALU.add),
                 reads=[csb, c2b, lgb], writes=[lgb])
            qd_, qdb = qd_t[0]
            kd_, kdb = kd_t[0]
            krt_, krtb = krt[0]
            for j in range(8):
                d = j // 4
                hh = j % 4
                ea, eab = ex[0]
                eb, ebb = ex[1]
                ec, ecb = ex[2]
                csrc = cst if j < 4 else lgt
                s.op("act", ins("activation", out=ea[:, :n], in_=csrc[:, j, :n], func=AF.Exp, scale=-1.0 / 16), reads=[csb, lgb], writes=[eab])
                s.op("act", ins("activation", out=eb[:, :n], in_=csrc[:, j, :n], func=AF.Exp, scale=1.0 / 16), reads=[csb, lgb], writes=[ebb])
                s.op("act", ins("activation", out=ec[:, :n], in_=c2t[:, j, :n], func=AF.Exp, scale=1.0 / 16), reads=[c2b], writes=[ecb])
                s.op("dve", ins("tensor_tensor", out=qd_[:, d, hh, :n], in0=qf[:, hh, :n], in1=ea[:, :n], op=ALU.mult), reads=[qfb, eab], writes=[qdb])
                s.op("dve", ins("tensor_tensor", out=kd_[:, d, hh, :n], in0=qf[:, 4 + hh, :n], in1=eb[:, :n], op=ALU.mult), reads=[qfb, ebb], writes=[kdb])
                kT, kTb = krT[kx % 2]
                kx += 1
                s.op("pool", ins("tensor_tensor", out=kT[:, :n], in0=qf[:, 4 + hh, :n], in1=ec[:, :n], op=ALU.mult), reads=[qfb, ecb], writes=[kTb])
                for tbk in range(n // 128):
                    ps, pb = s.ps()
                    psv = ps[:].bitcast(BF16)
                    s.op("pe", ins("transpose", psv[:, 0:128], kT[:, tbk * 128:(tbk + 1) * 128], ident_bf[:]), reads=[kTb, cb_const], writes=[pb])
                    s.op("act", ins("activation", out=krt_[:, tbk, j, :], in_=psv[:, 0:128], func=AF.Identity), reads=[pb], writes=[krtb])
            for d in range(2):
                s.dma(QD[d, :, :, t0:t0 + n], qd_[:, d, :, :n], reads=[qdb], writes=[G_b[ti]])
                s.dma(KD[d, :, :, t0:t0 + n], kd_[:, d, :, :n], reads=[kdb], writes=[G_b[ti]])
            s.dma(kr_view(t0, n // 128, 0, 8), krt_[:, 0:n // 128, :, :], reads=[krtb], writes=[G_b[ti]])
        ELd = dscr("ELd", [128, 2, 4, 72], F32)
        elb = Buf()
        s.dma(ELd, ELt[:], reads=[cbuf], writes=[elb])
        s.barrier()
        s.release()
        EL = s.alloc([128, 2, 4, 72], F32, "el2")
        elb2 = Buf()
        s.dma(EL[:], ELd, writes=[elb2])
        S = [(s.alloc([128, 4, 256], F32, "S"), [Buf() for _ in range(4)]) for _ in range(2)]
        Sb = [(s.alloc([128, 4, 256], BF16, "Sb"), [Buf() for _ in range(4)]) for _ in range(2)]
        qd_t = [[(s.alloc([128, 4, 512], BF16, "qd"), Buf()) for _ in range(2)] for _ in range(2)]
        kd_t = [[(s.alloc([128, 4, 512], BF16, "kd"), Buf()) for _ in range(2)] for _ in range(2)]
        kr_t = [[(s.alloc([128, 4, 4, 128], BF16, "kr"), Buf()) for _ in range(2)] for _ in range(2)]
        v_t = [[(s.alloc([128, 4, 1024], BF16, "v"), Buf()) for _ in range(2)] for _ in range(2)]
        of_t = [[(s.alloc([128, 8, 512], BF16, "of"), Buf()) for _ in range(2)] for _ in range(2)]
        attm = [(s.alloc([128, 64], BF16, "attm"), Buf()) for _ in range(8)]
        OD_b = [[Buf() for _ in range(NT)] for _ in range(2)]
        ODs = [OS, OS2]
        kat = 0
        SEQT = [list(range(8)), [8], [9]]
        states_in = [s1f, s1b]
        states_out = [o_gf, o_gb]
        for si, tiles in enumerate(SEQT):
            for d in range(2):
                St, Stb = S[d]
                Sbt, Sbb = Sb[d]
                if si == 0:
                    s.dma(St[:], states_in[d], writes=Stb)
                else:
                    s.op("pool", ins("memset", St[:], 0.0), writes=Stb)
                for hh in range(4):
                    s.op("act", ins("activation", out=Sbt[:, hh, :], in_=St[:, hh, :], func=AF.Identity), reads=[Stb[hh]], writes=[Sbb[hh]])
            nt_ = len(tiles)
            for step in range(nt_):
                cur = {}
                for d in range(2):
                    ti = tiles[step] if d == 0 else tiles[nt_ - 1 - step]
                    t0, n, cond, s0, s1 = TILES[ti]
                    qd_, qdb = qd_t[d][step % 2]
                    kd_, kdb = kd_t[d][step % 2]
                    kr_, krb = kr_t[d][step % 2]
                    v_, vb_ = v_t[d][step % 2]
                    of_, ofb = of_t[d][step % 2]
                    s.dma(qd_[:, :, :n], QD[d, :, :, t0:t0 + n], writes=[qdb])
                    s.dma(kd_[:, :, :n], KD[d, :, :, t0:t0 + n], writes=[kdb])
                    s.dma(kr_[:, 0:n // 128, :, :], kr_view(t0, n // 128, d * 4, 4), writes=[krb])
                    s.dma(v_[:, 0:n // 128, :], rows_view(VS, 1024, t0, n // 128, 0, 1024), writes=[vb_])
                    cur[d] = (ti, t0, n, qd_, qdb, kd_, kdb, kr_, krb, v_, vb_, of_, ofb)
                nch = cur[0][2] // 64
                for kc_ in range(nch):
                    for d in range(2):
                        ti, t0, n, qd_, qdb, kd_, kdb, kr_, krb, v_, vb_, of_, ofb = cur[d]
                        k = kc_ if d == 0 else nch - 1 - kc_
                        ci = t0 // 64 + k
                        tbk = k // 2
                        hp = slice((k % 2) * 64, (k % 2) * 64 + 64)
                        cs64 = slice(k * 64, (k + 1) * 64)
                        St, Stb = S[d]
                        Sbt, Sbb = Sb[d]
                        for hh in range(4):
                            ps, pb = s.ps()
                            s.mm(ps[0:64, 0:64], kd_[:, hh, cs64], qd_[:, hh, cs64], True, True, reads=[kdb, qdb], writes=[pb])
                            am, amb = attm[kat % 8]
                            kat += 1
                            s.op("dve", ins("tensor_tensor", out=am[hp, :], in0=ps[0:64, 0:64], in1=tri_bf[0:64, 2 + d, 0:64], op=ALU.mult),
                                 reads=[pb, cb_const], writes=[amb])
                            pso, pob = s.ps("b")
                            for vc in range(2):
                                s.mm(pso[:, vc * 64:(vc + 1) * 64], v_[hp, tbk, hh * 256 + vc * 128:hh * 256 + (vc + 1) * 128], am[hp, :], True, False, reads=[vb_, amb], writes=[pob])
                                s.mm(pso[:, vc * 64:(vc + 1) * 64], Sbt[:, hh, vc * 128:(vc + 1) * 128], qd_[:, hh, cs64], False, True, reads=[Sbb[hh], qdb], writes=[pob])
                            s.op("act", ins("activation", out=of_[:, hh * 2:hh * 2 + 2, cs64], in_=pso[:, 0:128].rearrange("p (v t) -> p v t", t=64), func=AF.Identity),
                                 reads=[pob], writes=[ofb])
                            ps2, pb2 = s.ps()
                            s.mm(ps2[:, 0:256], kr_[hp, tbk, hh, :], v_[hp, tbk, hh * 256:(hh + 1) * 256], True, True, reads=[krb, vb_], writes=[pb2])
                            s.op("dve", ins("scalar_tensor_tensor", out=St[:, hh, :], in0=St[:, hh, :], scalar=EL[:, d, hh, ci:ci + 1], in1=ps2[:, 0:256], op0=ALU.mult, op1=ALU.add),
                                 reads=[pb2, elb2, Stb[hh]], writes=[Stb[hh]])
                            s.op("pool", ins("tensor_copy", out=Sbt[:, hh, :], in_=St[:, hh, :]), reads=[Stb[hh]], writes=[Sbb[hh]])
                for d in range(2):
                    ti, t0, n, qd_, qdb, kd_, kdb, kr_, krb, v_, vb_, of_, ofb = cur[d]
                    s.dma(ODs[d][:, :, t0:t0 + n], of_[:, :, :n], reads=[ofb], writes=[OD_b[d][ti]])
            if si > 0:
                for d in range(2):
                    s.dma(states_out[d][si - 1], S[d][0][:], reads=S[d][1])
        s.barrier()
        s.release()
        Wo = s.alloc([128, 8, 1024], BF16, "wo")
        wob = load_w(Wo, w1_o, 8)
        gn = s.alloc([128, 2], F32, "gn")
        gnb = Buf()
        s.dma(gn[:], w1_gn, writes=[gnb])
        rctx = ResCtx()
        oa = [(s.alloc([128, 8, 512], BF16, "oa"), Buf()) for _ in range(2)]
        ob_ = [(s.alloc([128, 8, 512], BF16, "ob"), Buf()) for _ in range(2)]
        rr = [(s.alloc([128, 8, 512], BF16, "rr"), Buf()) for _ in range(2)]
        osum = (s.alloc([128, 8, 512], F32, "osum"), Buf())
        sq = (s.alloc([128, 8, 512], BF16, "sq"), Buf())
        rs_ = [(s.alloc([128, 512], F32, "rs"), Buf()) for _ in range(2)]
        ofin = [(s.alloc([128, 8, 512], BF16, "ofin"), Buf()) for _ in range(2)]
        for ti in range(NT):
            t0, n, cond, _, _ = TILES[ti]
            a_, a_b = oa[ti % 2]
            b_, b_b = ob_[ti % 2]
            r_, r_b = rr[ti % 2]
            s.dma(a_[:, :, :n], OS[:, :, t0:t0 + n], writes=[a_b])
            s.dma(b_[:, :, :n], OS2[:, :, t0:t0 + n], writes=[b_b])
            s.dma(r_[:, :, :n], RS[:, :, t0:t0 + n], writes=[r_b])
            os_, osb_ = osum
            sq_, sqb_ = sq
            s.op("dve", ins("tensor_tensor", out=os_[:, :, :n], in0=a_[:, :, :n], in1=b_[:, :, :n], op=ALU.add), reads=[a_b, b_b], writes=[osb_])
            s.op("act", ins("activation", out=sq_[:, :, :n], in_=os_[:, :, :n], func=AF.Square), reads=[osb_], writes=[sqb_])
            f_, f_b = ofin[ti % 2]
            for hh in range(4):
                ps, pb = s.ps()
                for vc in range(2):
                    s.mm(ps[:, :n], ones_bf[:], sq_[:, hh * 2 + vc, :n], vc == 0, vc == 1, reads=[sqb_, cb_const], writes=[pb])
                rt_, rtb = rs_[hh % 2]
                s.op("act", ins("activation", out=rt_[:, :n], in_=ps[:, :n], func=AF.Sqrt, scale=1.0 / 256, bias=EPSB[:, 0:1]), reads=[pb], writes=[rtb])
                s.op("dve", ins("reciprocal", out=rt_[:, :n], in_=rt_[:, :n]), reads=[rtb], writes=[rtb])
                for vc in range(2):
                    c = hh * 2 + vc
                    s.op("dve", ins("tensor_tensor", out=os_[:, c, :n], in0=os_[:, c, :n], in1=rt_[:, :n], op=ALU.mult), reads=[osb_, rtb], writes=[osb_])
                    s.op("dve", ins("scalar_tensor_tensor", out=f_[:, c, :n], in0=os_[:, c, :n], scalar=gn[:, vc:vc + 1], in1=r_[:, c, :n], op0=ALU.mult, op1=ALU.mult),
                         reads=[osb_, gnb, r_b], writes=[f_b])
            outproj_resid(rctx, l, 2, ti, Wo, wob, 8, f_, [f_b], Xsrc, Xsrc_b, X, X_b)

    def ssd_layer(l, Xsrc, Xsrc_b):
        XBu = U
        s.barrier()
        s.release()
        Wz = s.alloc([128, 8, 2048], BF16, "wz")
        wzb = load_w(Wz, w3_z, 8)
        Wx = s.alloc([128, 8, 3072], BF16, "wx")
        wxb = load_w(Wx, w3_xbc, 8)
        Wdt = s.alloc([128, 8, 64], BF16, "wdt")
        wdb = load_w(Wdt, w3_dt, 8)
        vec = s.alloc([128, 5, 32], F32, "vec")
        cbuf = Buf()
        s.dma(vec[:], ssd_vec.rearrange("a b -> (a b)").partition_broadcast(128).rearrange("p (a b) -> p a b", a=5), writes=[cbuf])
        avec = s.alloc([128, 64], F32, "avec")
        s.op("act", ins("activation", out=avec[:], in_=vec[:, 0:2, :].rearrange("p a b -> p (a b)"), func=AF.Exp), reads=[cbuf], writes=[cbuf])
        s.op("dve", ins("tensor_scalar", out=avec[:], in0=avec[:], scalar1=-1.0, scalar2=None, op0=ALU.mult), reads=[cbuf], writes=[cbuf])
        nctx = NormCtx(1)
        xbt = (s.alloc([128, 24, 512], BF16, "xbt"), [Buf() for _ in range(24)])
        zt = (s.alloc([128, 4, 2048], BF16, "zt"), Buf())
        dtt = (s.alloc([128, 4, 2, 64], F32, "dtt"), Buf())
        dtmp = (s.alloc([128, 64], F32, "dtmp"), Buf())
        S_b = [Buf() for _ in range(NT)]
        ev = 0
        for ti in range(NT):
            t0, n, cond, s0, s1 = TILES[ti]
            nb = n // 128
            h, hb = norm_tile(nctx, l, 0, ti, Xsrc, Xsrc_b)
            xb_, xbb = xbt
            for cc in range(24):
                ps, pb = s.ps()
                for kc in range(8):
                    s.mm(ps[:, :n], Wx[:, kc, cc * 128:(cc + 1) * 128], h[:, kc, :n], kc == 0, kc == 7, reads=[wxb[kc], hb], writes=[pb])
                if ev % 2 == 0:
                    s.op("act", ins("activation", out=xb_[:, cc, :n], in_=ps[:, :n], func=AF.Identity), reads=[pb], writes=[xbb[cc]])
                else:
                    s.op("dve", ins("tensor_copy", out=xb_[:, cc, :n], in_=ps[:, :n]), reads=[pb], writes=[xbb[cc]])
                ev += 1
            s.dma(XBu[:, 0:24, t0:t0 + n], xb_[:, :, :n], reads=xbb, writes=[S_b[ti]])
            z_, z_b = zt
            d_, d_b = dtt
            for tbk in range(nb):
                for vg in range(4):
                    ps, pb = s.ps()
                    for kc in range(8):
                        s.mm(ps[:, :512], h[:, kc, tbk * 128:(tbk + 1) * 128], Wz[:, kc, vg * 512:(vg + 1) * 512], kc == 0, kc == 7, reads=[wzb[kc], hb], writes=[pb])
                    s.op("act", ins("activation", out=z_[:, tbk, vg * 512:(vg + 1) * 512], in_=ps[:, :512], func=AF.Silu), reads=[pb], writes=[z_b])
                ps, pb = s.ps()
                for kc in range(8):
                    s.mm(ps[:, 0:64], h[:, kc, tbk * 128:(tbk + 1) * 128], Wdt[:, kc, :], kc == 0, kc == 7, reads=[wdb[kc], hb], writes=[pb])
                tm, tmb = dtmp
                s.op("dve", ins("tensor_tensor", out=tm[:], in0=ps[:, 0:64], in1=vec[:, 2:4, :].rearrange("p a b -> p (a b)"), op=ALU.add), reads=[pb, cbuf], writes=[tmb])
                s.op("act", ins("activation", out=tm[:], in_=tm[:], func=AF.Exp), reads=[tmb], writes=[tmb])
                s.op("act", ins("activation", out=d_[:, tbk, 0, :], in_=tm[:], func=AF.Ln, bias=1.0, scale=1.0), reads=[tmb], writes=[d_b])
                s.op("dve", ins("tensor_tensor", out=d_[:, tbk, 1, :], in0=d_[:, tbk, 0, :], in1=avec[:], op=ALU.mult), reads=[d_b, cbuf], writes=[d_b])
            s.dma(rows_view(ZS, 2048, t0, nb, 0, 2048), z_[:, 0:nb, :], reads=[z_b], writes=[S_b[ti]])
            s.dma(AP(DTS.tensor, t0 * 128, [[128, 128], [128 * 128, nb], [1, 128]]), d_[:, 0:nb, :, :].rearrange("p b a c -> p b (a c)"), reads=[d_b], writes=[S_b[ti]])
        chk(20)
        s.barrier()
        s.release()
        cw = s.alloc([128, 3, 24], F32, "scw")
        cbias = s.alloc([128, 24], F32, "scb")
        cwb = Buf()
        s.dma(cw[:], ssd_cw, writes=[cwb])
        s.dma(cbias[:], ssd_cb, writes=[cwb])
        ug = s.alloc([128, 24, 514], BF16, "sug")
        ugb = [Buf() for _ in range(2)]
        xc = (s.alloc([128, 24, 512], BF16, "xc"), [Buf() for _ in range(24)])
        tmps = [(s.alloc([128, 512], F32, "st"), Buf()) for _ in range(2)]
        xtt = (s.alloc([128, 4, 2048], BF16, "xtt"), Buf())
        btt = (s.alloc([128, 4, 512], BF16, "btt"), Buf())
        kp = 0
        for ti in range(NT):
            t0, n, cond, s0, s1 = TILES[ti]
            nb = n // 128
            lo = (t0 - 1) >= s0
            hi = (t0 + n) < s1
            for grp in range(2):
                gs_ = slice(grp * 12, (grp + 1) * 12)
                if not lo:
                    s.op("pool", ins("memset", ug[:, gs_, 0:1], 0.0), writes=[ugb[grp]])
                if not hi:
                    s.op("pool", ins("memset", ug[:, gs_, n + 1:n + 2], 0.0), writes=[ugb[grp]])
                c0 = 0 if lo else 1
                c1 = n + 2 if hi else n + 1
                s.dma(ug[:, gs_, c0:c1], XBu[:, gs_, t0 - 1 + c0:t0 - 1 + c1], writes=[ugb[grp]])
            xc_, xcb = xc
            for col in range(24):
                grp = col // 12
                tt, ttb = tmps[kp % 2]
                kp += 1
                s.op("act", ins("activation", out=tt[:, :n], in_=ug[:, col, 1:n + 1], func=AF.Identity, scale=cw[:, 1, col:col + 1], bias=cbias[:, col:col + 1]),
                     reads=[ugb[grp], cwb], writes=[ttb])
                s.op("dve", ins("scalar_tensor_tensor", out=tt[:, :n], in0=ug[:, col, 0:n], scalar=cw[:, 0, col:col + 1], in1=tt[:, :n], op0=ALU.mult, op1=ALU.add),
                     reads=[ugb[grp], cwb, ttb], writes=[ttb])
                s.op("dve", ins("scalar_tensor_tensor", out=tt[:, :n], in0=ug[:, col, 2:n + 2], scalar=cw[:, 2, col:col + 1], in1=tt[:, :n], op0=ALU.mult, op1=ALU.add),
                     reads=[ugb[grp], cwb, ttb], writes=[ttb])
                s.op("act", ins("activation", out=xc_[:, col, :n], in_=tt[:, :n], func=AF.Silu), reads=[ttb], writes=[xcb[col]])
            s.dma(QK[:, 0:8, t0:t0 + n], xc_[:, 16:24, :n], reads=xcb[16:24], writes=[S_b[ti]])
            x_, x_b = xtt
            b_, b_b = btt
            tv = 0
            for tbk in range(nb):
                for col in range(20):
                    ps, pb = s.ps()
                    psv = ps[:].bitcast(BF16)
                    s.op("pe", ins("transpose", psv[:, 0:128], xc_[:, col, tbk * 128:(tbk + 1) * 128], ident_bf[:]), reads=[xcb[col], cb_const], writes=[pb])
                    dst = x_[:, tbk, col * 128:(col + 1) * 128] if col < 16 else b_[:, tbk, (col - 16) * 128:(col - 15) * 128]
                    dbuf = x_b if col < 16 else b_b
                    if tv % 2 == 0:
                        s.op("act", ins("activation", out=dst, in_=psv[:, 0:128], func=AF.Identity), reads=[pb], writes=[dbuf])
                    else:
                        s.op("dve", ins("tensor_copy", out=dst, in_=psv[:, 0:128]), reads=[pb], writes=[dbuf])
                    tv += 1
            s.dma(rows_view(XTs, 2048, t0, nb, 0, 2048), x_[:, 0:nb, :], reads=[x_b], writes=[S_b[ti]])
            s.dma(rows_view(BTs, 512, t0, nb, 0, 512), b_[:, 0:nb, :], reads=[b_b], writes=[S_b[ti]])
        chk(21)
        s.barrier()
        s.release()
        vec = s.alloc([128, 5, 32], F32, "vec")
        negm = s.alloc([128, 2, 128], F32, "negm")
        cbuf = Buf()
        s.op("dve", ins("tensor_scalar", out=negm[:], in0=stage[:, 0:2, :], scalar1=-1.0, scalar2=30000.0, op0=ALU.add, op1=ALU.mult), reads=[bstage], writes=[cbuf])
        ST = [(s.alloc([128, 2048], F32, "ST"), [Buf() for _ in range(4)]) for _ in range(2)]
        SbT = [(s.alloc([128, 2048], BF16, "SbT"), [Buf() for _ in range(4)]) for _ in range(2)]
        bct = [(s.alloc([128, 8, 512], BF16, "bct"), Buf()) for _ in range(2)]
        xts = [(s.alloc([128, 4, 2048], BF16, "xts"), Buf()) for _ in range(2)]
        bts = [(s.alloc([128, 4, 512], BF16, "bts"), Buf()) for _ in range(2)]
        dts = [(s.alloc([128, 4, 2, 64], F32, "dts"), Buf()) for _ in range(2)]
        yts = [(s.alloc([128, 4, 2048], BF16, "yts"), Buf()) for _ in range(2)]
        cumT = [(s.alloc([128, 32], F32, "cumT"), Buf()) for _ in range(2)]
        cbT = [(s.alloc([128, 128], F32, "cbT"), Buf()) for _ in range(6)]
        lat = [(s.alloc([128, 4, 128], F32, "lat"), Buf()) for _ in range(3)]
        cB = [(s.alloc([128, 4, 128], F32, "cB"), Buf()) for _ in range(4)]
        seg = [(s.alloc([128, 4, 128], F32, "seg"), Buf()) for _ in range(3)]
        Et = [(s.alloc([128, 4, 128], F32, "Et"), Buf()) for _ in range(6)]
        ecB = [(s.alloc([128, 4, 128], F32, "ecB"), Buf()) for _ in range(8)]
        te = [(s.alloc([128, 4], F32, "te"), Buf()) for _ in range(3)]
        xs = [(s.alloc([128, 512], BF16, "xs"), Buf()) for _ in range(4)]
        Wt = [(s.alloc([128, 128], BF16, "Wt"), Buf()) for _ in range(8)]
        CEt = [(s.alloc([128, 128], BF16, "CEt"), Buf()) for _ in range(8)]
        SEQT = [list(range(8)), [8], [9]]
        st_in = [s3f, s3b]
        st_out = [o_sf, o_sb]
        Y_b = [Buf() for _ in range(NT)]
        kw = 0
        kq4 = 0
        kg = 0
        kct = 0
        for si, tiles in enumerate(SEQT):
            for d in range(2):
                St, Stb = ST[d]
                Sbt, Sbb = SbT[d]
                if si == 0:
                    s.dma(St[:], st_in[d], writes=Stb)
                else:
                    s.op("pool", ins("memset", St[:], 0.0), writes=Stb)
                for g in range(4):
                    s.op("act", ins("activation", out=Sbt[:, g * 512:(g + 1) * 512], in_=St[:, g * 512:(g + 1) * 512], func=AF.Identity), reads=[Stb[g]], writes=[Sbb[g]])
            nt_ = len(tiles)
            for step in range(nt_):
                cur = {}
                for d in range(2):
                    ti = tiles[step] if d == 0 else tiles[nt_ - 1 - step]
                    t0, n, cond, s0, s1 = TILES[ti]
                    nb = n // 128
                    bc_, bcb = bct[d]
                    x_, x_b = xts[d]
                    b_, b_b = bts[d]
                    d_, d_b = dts[d]
                    y_, y_b = yts[d]
                    s.dma(bc_[:, :, :n], QK[:, 0:8, t0:t0 + n], writes=[bcb])
                    s.dma(x_[:, 0:nb, :], rows_view(XTs, 2048, t0, nb, 0, 2048), writes=[x_b])
                    s.dma(b_[:, 0:nb, :], rows_view(BTs, 512, t0, nb, 0, 512), writes=[b_b])
                    s.dma(d_[:, 0:nb, :, :].rearrange("p b a c -> p b (a c)"), AP(DTS.tensor, t0 * 128, [[128, 128], [128 * 128, nb], [1, 128]]), writes=[d_b])
                    cur[d] = (ti, t0, n, nb)
                nbs = cur[0][3]
                batches = []
                for kc_ in range(nbs):
                    for d in range(2):
                        for g in range(4):
                            for hb_ in range(2):
                                batches.append((kc_, d, g, hb_))
                bst = {}
                LOOKB = 3

                def geo(bt_):
                    kc_, d, g, hb_ = bt_
                    ti, t0, n, nb = cur[d]
                    tbk = kc_ if d == 0 else nb - 1 - kc_
                    last = 127 if d == 0 else 0
                    d_, d_b = dts[d]
                    la = d_[:, tbk, 1, d * 32:(d + 1) * 32]
                    dtv = d_[:, tbk, 0, d * 32:(d + 1) * 32]
                    return tbk, last, slice(tbk * 128, (tbk + 1) * 128), la, dtv, d_b

                def stageA1(bt_):
                    nonlocal kq4, kg, kct
                    kc_, d, g, hb_ = bt_
                    tbk, last, tk, la, dtv, d_b = geo(bt_)
                    bc_, bcb = bct[d]
                    if g == 0 and hb_ == 0:
                        psc, pcb = s.ps()
                        s.mm(psc[:, 0:32], stage[:, d, :], la, True, True, reads=[bstage, d_b], writes=[pcb])
                        cT, cTb = cumT[kct % 2]
                        kct += 1
                        s.op("dve", ins("tensor_copy", out=cT[:], in_=psc[:, 0:32]), reads=[pcb], writes=[cTb])
                        bst[("cT", kc_, d)] = (cT, cTb)
                    if hb_ == 0:
                        ps, pb = s.ps()
                        s.mm(ps[:, 0:128], bc_[:, g, tk], bc_[:, 4 + g, tk], True, True, reads=[bcb], writes=[pb])
                        cb_, cb_b = cbT[kg % 6]
                        s.op("act", ins("activation", out=cb_[:], in_=ps[:, 0:128], func=AF.Identity), reads=[pb], writes=[cb_b])
                        bst[("grp", kc_, d, g)] = dict(cb=(cb_, cb_b), x4=xs[kg % 4], ecs=[])
                        kg += 1
                    h0 = g * 8 + hb_ * 4
                    lt, ltb = lat[kq4 % 3]
                    cb4, cb4b = cB[kq4 % 4]
                    sg, sgb = seg[kq4 % 3]
                    E_, E_b = Et[kq4 % 6]
                    ec, ecb = ecB[kq4 % 8]
                    te_, te_b = te[kq4 % 3]
                    kq4 += 1
                    bst[("w",) + bt_] = (cb4, cb4b, sg, sgb, E_, E_b, ec, ecb, te_, te_b)
                    s.op("dve", ins("tensor_tensor", out=lt[:], in0=stage[:, d, :].unsqueeze(1).to_broadcast([128, 4, 128]),
                                    in1=la[:, h0:h0 + 4].unsqueeze(2).to_broadcast([128, 4, 128]), op=ALU.mult),
                         reads=[bstage, d_b], writes=[ltb])
                    ps2, pb2 = s.ps()
                    s.mm(ps2[:, 0:512], ones_f[:], lt[:].rearrange("p a b -> p (a b)"), True, True, reads=[ltb, cb_const], writes=[pb2])
                    s.op("act", ins("activation", out=cb4[:].rearrange("p a b -> p (a b)"), in_=ps2[:, 0:512], func=AF.Identity), reads=[pb2], writes=[cb4b])

                def stageA2a(bt_):
                    kc_, d, g, hb_ = bt_
                    cT, cTb = bst[("cT", kc_, d)]
                    G = bst[("grp", kc_, d, g)]
                    cb4, cb4b, sg, sgb, E_, E_b, ec, ecb, te_, te_b = bst[("w",) + bt_]
                    h0 = g * 8 + hb_ * 4
                    G["ecs"].append((ec, ecb))
                    for hh in range(4):
                        hq = h0 + hh
                        s.op("dve", ins("scalar_tensor_tensor", out=sg[:, hh, :], in0=cb4[:, hh, :], scalar=cT[:, hq:hq + 1], in1=negm[:, d, :],
                                        op0=ALU.subtract, op1=ALU.add),
                             reads=[cb4b, cTb, cbuf], writes=[sgb])
                    s.op("act", ins("activation", out=E_[:], in_=sg[:], func=AF.Exp), reads=[sgb], writes=[E_b])
                    s.op("act", ins("activation", out=ec[:], in_=cb4[:], func=AF.Exp), reads=[cb4b], writes=[ecb])

                def stageA2b(bt_):
                    kc_, d, g, hb_ = bt_
                    tbk, last, tk, la, dtv, d_b = geo(bt_)
                    x_, x_b = xts[d]
                    G = bst[("grp", kc_, d, g)]
                    x4, x4b = G["x4"]
                    cb4, cb4b, sg, sgb, E_, E_b, ec, ecb, te_, te_b = bst.pop(("w",) + bt_)
                    h0 = g * 8 + hb_ * 4
                    s.op("dve", ins("tensor_tensor", out=te_[:], in0=E_[:, :, last], in1=dtv[:, h0:h0 + 4], op=ALU.mult),
                         reads=[E_b, d_b], writes=[te_b])
                    s.op("dve", ins("tensor_tensor",
                                    out=x4[:, hb_ * 256:(hb_ + 1) * 256].rearrange("p (a b) -> p a b", b=64),
                                    in0=x_[:, tbk, h0 * 64:(h0 + 4) * 64].rearrange("p (a b) -> p a b", b=64),
                                    in1=te_[:].unsqueeze(2).to_broadcast([128, 4, 64]), op=ALU.mult),
                         reads=[x_b, te_b], writes=[x4b])
                    bst[("bat",) + bt_] = (E_, E_b, ec, ecb)

                def stageB(bt_):
                    nonlocal kw
                    kc_, d, g, hb_ = bt_
                    tbk, last, tk, la, dtv, d_b = geo(bt_)
                    bc_, bcb = bct[d]
                    x_, x_b = xts[d]
                    b_, b_b = bts[d]
                    y_, y_b = yts[d]
                    St, Stb = ST[d]
                    Sbt, Sbb = SbT[d]
                    G = bst[("grp", kc_, d, g)]
                    cb_, cb_b = G["cb"]
                    x4, x4b = G["x4"]
                    if hb_ == 0:
                        G["psy"] = s.ps("b")
                    psy, pyb = G["psy"]
                    E_, E_b, ec, ecb = bst.pop(("bat",) + bt_)
                    h0 = g * 8 + hb_ * 4
                    for hh in range(4):
                        hq = h0 + hh
                        W_, W_b = Wt[kw % 8]
                        C_, C_b = CEt[kw % 8]
                        kw += 1
                        s.op("dve", ins("scalar_tensor_tensor", out=W_[:], in0=E_[:, hh, :], scalar=dtv[:, hq:hq + 1], in1=cb_[:], op0=ALU.mult, op1=ALU.mult),
                             reads=[E_b, d_b, cb_b], writes=[W_b])
                        s.op("pool", ins("tensor_tensor", out=C_[:], in0=bc_[:, 4 + g, tk], in1=ec[:, hh, :], op=ALU.mult),
                             reads=[bcb, ecb], writes=[C_b])
                        ycol = slice((hq % 8) * 64, (hq % 8 + 1) * 64)
                        s.mm(psy[:, ycol], W_[:], x_[:, tbk, hq * 64:(hq + 1) * 64], True, False, reads=[W_b, x_b], writes=[pyb])
                        s.mm(psy[:, ycol], C_[:], Sbt[:, hq * 64:(hq + 1) * 64], False, True, reads=[C_b, Sbb[g]], writes=[pyb])
                    if hb_ == 0:
                        return
                    s.op("act", ins("activation", out=y_[:, tbk, g * 512:(g + 1) * 512], in_=psy[:, 0:512], func=AF.Identity), reads=[pyb], writes=[y_b])
                    pss, psb_ = s.ps()
                    s.mm(pss[:, 0:512], b_[:, tbk, g * 128:(g + 1) * 128], x4[:], True, True, reads=[b_b, x4b], writes=[psb_])
                    ecs = G["ecs"]
                    for hh in range(8):
                        hq = g * 8 + hh
                        ec2, ecb2 = ecs[hh // 4]
                        s.op("dve", ins("scalar_tensor_tensor",
                                        out=St[:, hq * 64:(hq + 1) * 64], in0=St[:, hq * 64:(hq + 1) * 64], scalar=ec2[:, hh % 4, last:last + 1], in1=pss[:, hh * 64:(hh + 1) * 64], op0=ALU.mult, op1=ALU.add),
                             reads=[psb_, ecb2, Stb[g]], writes=[Stb[g]])
                    s.op("pool", ins("tensor_copy", out=Sbt[:, g * 512:(g + 1) * 512], in_=St[:, g * 512:(g + 1) * 512]), reads=[Stb[g]], writes=[Sbb[g]])
                    del bst[("grp", kc_, d, g)]

                NB_ = len(batches)
                for t_ in range(NB_ + 3):
                    if t_ < NB_:
                        stageA1(batches[t_])
                    if 1 <= t_ < NB_ + 1:
                        stageA2a(batches[t_ - 1])
                    if 2 <= t_ < NB_ + 2:
                        stageA2b(batches[t_ - 2])
                    if t_ >= 3:
                        stageB(batches[t_ - 3])
                for d in range(2):
                    ti, t0, n, nb = cur[d]
                    y_, y_b = yts[d]
                    s.dma(rows_view(YD[d], 2048, t0, nb, 0, 2048), y_[:, 0:nb, :], reads=[y_b], writes=[Y_b[ti]])
            if si > 0:
                for d in range(2):
                    s.dma(st_out[d][si - 1], ST[d][0][:], reads=ST[d][1])
        chk(22)
        s.barrier()
        s.release()
        Wo = s.alloc([128, 16, 1024], BF16, "wo")
        wob = load_w(Wo, w3_o, 16)
        vec = s.alloc([128, 5, 32], F32, "vec")
        gnb = s.alloc([128, 2048], F32, "gnb")
        cbuf = Buf()
        s.dma(vec[:], ssd_vec.rearrange("a b -> (a b)").partition_broadcast(128).rearrange("p (a b) -> p a b", a=5), writes=[cbuf])
        s.dma(gnb[:], ssd_gn.partition_broadcast(128), writes=[cbuf])
        rctx = ResCtx()
        yf = (s.alloc([128, 4, 2048], BF16, "yf"), Buf())
        yb = (s.alloc([128, 4, 2048], BF16, "yb"), Buf())
        xt_ = (s.alloc([128, 4, 2048], BF16, "xt4"), Buf())
        zs = (s.alloc([128, 4, 2048], BF16, "zs"), Buf())
        yv = [(s.alloc([128, 2048], F32, "yv"), Buf()) for _ in range(2)]
        tv_ = (s.alloc([128, 2048], F32, "tv"), Buf())
        ynb = (s.alloc([128, 2048], BF16, "ynb"), Buf())
        ssq = [(s.alloc([128, 1], F32, "ssq"), Buf()) for _ in range(2)]
        oT = [(s.alloc([128, 16, 512], BF16, "oT"), Buf()) for _ in range(2)]
        kk = 0
        tv = 0
        for ti in range(NT):
            t0, n, cond, _, _ = TILES[ti]
            nb = n // 128
            s.dma(yf[0][:, 0:nb, :], rows_view(YD[0], 2048, t0, nb, 0, 2048), writes=[yf[1]])
            s.dma(yb[0][:, 0:nb, :], rows_view(YD[1], 2048, t0, nb, 0, 2048), writes=[yb[1]])
            s.dma(xt_[0][:, 0:nb, :], rows_view(XTs, 2048, t0, nb, 0, 2048), writes=[xt_[1]])
            s.dma(zs[0][:, 0:nb, :], rows_view(ZS, 2048, t0, nb, 0, 2048), writes=[zs[1]])
            o_, o_b = oT[ti % 2]
            for tbk in range(nb):
                y_, y_b = yv[kk % 2]
                sq_, sq_b = ssq[kk % 2]
                kk += 1
                t_, t_b = tv_
                s.op("dve", ins("tensor_tensor", out=y_[:], in0=yf[0][:, tbk, :], in1=yb[0][:, tbk, :], op=ALU.add), reads=[yf[1], yb[1]], writes=[y_b])
                s.op("pool", ins("tensor_tensor", out=t_[:].rearrange("p (a b) -> p a b", b=64), in0=xt_[0][:, tbk, :].rearrange("p (a b) -> p a b", b=64),
                                                              in1=vec[:, 4, :].unsqueeze(2).to_broadcast([128, 32, 64]), op=ALU.mult), reads=[xt_[1], cbuf], writes=[t_b])
                s.op("dve", ins("tensor_tensor", out=y_[:], in0=y_[:], in1=t_[:], op=ALU.add), reads=[y_b, t_b], writes=[y_b])
                s.op("dve", ins("tensor_tensor", out=y_[:], in0=y_[:], in1=zs[0][:, tbk, :], op=ALU.mult), reads=[y_b, zs[1]], writes=[y_b])
                s.op("pool", ins("memset", sq_[:], 0.0), writes=[sq_b])
                s.op("act", ins("activation", out=t_[:], in_=y_[:], func=AF.Square, accum_out=sq_[:, 0:1]), reads=[y_b, sq_b], writes=[t_b, sq_b])
                s.op("act", ins("activation", out=sq_[:], in_=sq_[:], func=AF.Sqrt, scale=1.0 / 2048, bias=EPSB[:, 0:1]), reads=[sq_b], writes=[sq_b])
                s.op("dve", ins("reciprocal", out=sq_[:], in_=sq_[:]), reads=[sq_b], writes=[sq_b])
                yn, ynb_ = ynb
                s.op("dve", ins("scalar_tensor_tensor", out=yn[:], in0=y_[:], scalar=sq_[:, 0:1], in1=gnb[:], op0=ALU.mult, op1=ALU.mult), reads=[y_b, sq_b, cbuf], writes=[ynb_])
                for c in range(16):
                    ps, pb = s.ps()
                    psv = ps[:].bitcast(BF16)
                    s.op("pe", ins("transpose", psv[:, 0:128], yn[:, c * 128:(c + 1) * 128], ident_bf[:]), reads=[ynb_, cb_const], writes=[pb])
                    if tv % 2 == 0:
                        s.op("act", ins("activation", out=o_[:, c, tbk * 128:(tbk + 1) * 128], in_=psv[:, 0:128], func=AF.Identity), reads=[pb], writes=[o_b])
                    else:
                        s.op("dve", ins("tensor_copy", out=o_[:, c, tbk * 128:(tbk + 1) * 128], in_=psv[:, 0:128]), reads=[pb], writes=[o_b])
                    tv += 1
            outproj_resid(rctx, l, 2, ti, Wo, wob, 16, o_, [o_b], Xsrc, Xsrc_b, X, X_b)

    import os
    STOP = int(os.environ.get("KSTOP", "99"))

    class _Stop(Exception):
        pass

    def chk(k):
        if STOP == k:
            raise _Stop()

    try:
        prologue()
        chk(0)
        Xsrc, Xsrc_b = xT, xT_b
        for l in range(NLAYERS):
            kind = l % 4
            if kind == 0:
                qkv_phase(l, w0_qk, 12, w0_v, 256, Xsrc, Xsrc_b, True, kout=(8, o_wk, 64), vout=o_wv)
                chk(1)
                attn_phase(l, "win", Xsrc, Xsrc_b)
                chk(2)
                outproj_phase(l, w0_o, 8, OS, None, Xsrc, Xsrc_b)
                chk(3)
            elif kind == 1:
                gla_layer(l, Xsrc, Xsrc_b)
            elif kind == 2:
                qkv_phase(l, w2_qk, 16, w2_v, 1024, Xsrc, Xsrc_b, True, kout=(8, o_dk, 128), vout=o_dv)
                attn_phase(l, "diff", Xsrc, Xsrc_b)
                outproj_phase(l, w2_o, 8, OS, None, Xsrc, Xsrc_b)
            else:
                ssd_layer(l, Xsrc, Xsrc_b)
            Xsrc, Xsrc_b = X, X_b
            ffn(l, Xsrc, Xsrc_b)
            chk(10 + l)
        s.barrier()
        s.release()
        nctx = NormCtx()
        for ti in range(NT):
            norm_tile(nctx, 0, 0, ti, Xsrc, Xsrc_b, final_out=yT)
    except _Stop:
        pass
    s.barrier()
    counts = s.emit()
    return nc, counts

def _fm(w):
    K, N = w.shape
    return np.ascontiguousarray(w.reshape(K // 128, 128, N).transpose(1, 0, 2))


def _pc(v):
    return np.ascontiguousarray(v.reshape(-1, 128).T)


def _consts():
    f32 = np.float32
    t = np.arange(LS)
    row = (t // 64).astype(f32)
    col = (t % 64).astype(f32)
    inv = (10000.0 ** (-np.arange(0, 32, 2, dtype=f32) / 32)).astype(f32)
    cos = np.zeros((128, LS), f32)
    sin = np.zeros((128, LS), f32)
    perm = np.zeros((128, 128), f32)
    for p in range(128):
        d = p % 64
        axis = d // 32
        idx = d % 16
        second = (d % 32) >= 16
        ang = (row if axis == 0 else col) * inv[idx]
        cos[p] = np.cos(ang)
        sin[p] = np.sin(ang) * (1.0 if second else -1.0)
        partner = p + 16 if not second else p - 16
        perm[partner, p] = 1.0
    j = np.arange(128)[:, None]
    i = np.arange(512)[None, :]
    wmask = np.zeros((128, 6, 512), f32)
    for mi in range(6):
        o = mi - 1
        wmask[:, mi, :] = (np.abs(i - (o * 128 + j)) <= 128).astype(f32)
    tri = np.zeros((128, 4, 128), f32)
    jj = np.arange(128)[:, None]
    ii = np.arange(128)[None, :]
    tri[:, 0, :] = (jj <= ii)
    tri[:, 1, :] = (jj >= ii)
    tri[0:64, 2, 0:64] = (jj[0:64] <= ii[:, 0:64])
    tri[0:64, 3, 0:64] = (jj[0:64] >= ii[:, 0:64])
    m64 = np.ones((128, 512), f32)
    m64[:, ::64] = 0.0
    return dict(rcos=cos, rsin=sin, perm=perm, wmask=wmask, tri=tri, m64=m64)


_PROG = {}


def _get_prog(nl):
    if nl not in _PROG:
        _PROG[nl] = build_program(nl)
    return _PROG[nl]


def kernel(NLAYERS=4, **inp):
    f32 = np.float32
    g = {k: np.asarray(v) for k, v in inp.items()}
    nc, counts = _get_prog(NLAYERS)
    common = dict(_consts())
    common["ada_w"] = np.ascontiguousarray(g["ada_w"].reshape(4, 8, 128, 6144).transpose(0, 2, 1, 3))
    common["ada_b"] = np.ascontiguousarray(g["ada_b"].reshape(4, 48, 128).transpose(2, 0, 1))
    common["nrm"] = np.ascontiguousarray(np.stack([g["norm_mix"], g["norm_ffn"]], axis=1).reshape(4, 2, 8, 128).transpose(3, 0, 1, 2))
    common["fnorm"] = _pc(g["final_norm"])
    common["w_up"] = np.ascontiguousarray(g["ffn_w_up"].reshape(4, 8, 128, 5632).transpose(0, 2, 1, 3))
    common["w_dn"] = np.ascontiguousarray(g["ffn_w_down"].reshape(4, 22, 128, 1024).transpose(0, 2, 1, 3))
    common["ffn_cw"] = np.ascontiguousarray(g["ffn_conv_w"].reshape(4, 3, 44, 128).transpose(3, 0, 1, 2))
    common["ffn_cb"] = np.ascontiguousarray(g["ffn_conv_b"].reshape(4, 44, 128).transpose(2, 0, 1))
    wq = g["win_w_qkv"][0]
    qcols = wq[:, 0:1024]
    kcols = wq[:, 1024:1280]
    vcols = wq[:, 1280:1536]
    kdup = np.concatenate([np.concatenate([kcols[:, h * 64:(h + 1) * 64]] * 2, axis=1) for h in range(4)], axis=1)
    common["w0_qk"] = _fm(np.concatenate([qcols, kdup], axis=1))
    common["w0_v"] = _fm(vcols)
    common["w0_o"] = _fm(g["win_w_o"][0])
    common["sink"] = np.ascontiguousarray(g["win_sink"][0])
    wg = g["gla_w_qkvr"][0]
    common["w1_qk"] = _fm(wg[:, 0:1024])
    common["w1_v"] = _fm(wg[:, 1024:2048])
    common["w1_r"] = _fm(wg[:, 2048:3072])
    common["w1_g1"] = _fm(np.concatenate([g["gla_w_gf1"][0], g["gla_w_gb1"][0]], axis=1))
    g2 = np.zeros((32, 1024), f32)
    g2[0:16, 0:512] = g["gla_w_gf2"][0]
    g2[16:32, 512:1024] = g["gla_w_gb2"][0]
    common["w1_g2"] = g2
    common["w1_gb"] = _pc(np.concatenate([g["gla_b_gf"][0], g["gla_b_gb"][0]]))
    common["w1_gn"] = _pc(g["gla_norm"][0])
    common["w1_o"] = _fm(g["gla_w_o"][0])
    wd = g["diff_w_qkv"][0]
    common["w2_qk"] = _fm(wd[:, 0:2048])
    common["w2_v"] = _fm(wd[:, 2048:3072])
    common["w2_o"] = _fm(g["diff_w_o"][0])
    common["lqk"] = np.ascontiguousarray(np.stack([g["diff_lq1"][0], g["diff_lk1"][0], g["diff_lq2"][0], g["diff_lk2"][0]]))
    common["w2_gn"] = np.ascontiguousarray(g["diff_norm"][0].reshape(128, 1))
    common.update(_host_ssd_common(g))
    in_maps = []
    for b in range(8):
        m = dict(common)
        xs = g["x_sample"][b]
        xp = g["x_prompt"][2 * b:2 * b + 2].reshape(512, 1024)
        xa = np.concatenate([xs, xp], axis=0)
        m["xT"] = np.ascontiguousarray(xa.T.reshape(8, 128, T).transpose(1, 0, 2))
        cond = np.stack([g["c"][b], g["c_ctx"]], axis=1)
        m["condT"] = np.ascontiguousarray(cond.reshape(8, 128, 2).transpose(1, 0, 2))
        ck = g["cache_win_k"][b, 0]
        kT = ck.transpose(2, 1, 0)
        m["ck0"] = np.ascontiguousarray(np.concatenate([kT, kT], axis=0))
        m["cv0"] = np.ascontiguousarray(g["cache_win_v"][b, 0].reshape(512, 256))
        m["s1f"] = np.ascontiguousarray(g["state_gla_fwd"][b, 0].transpose(1, 0, 2))
        m["s1b"] = np.ascontiguousarray(g["state_gla_bwd"][b, 0].transpose(1, 0, 2))
        dk = g["cache_diff_k"][b, 0]
        m["ck2"] = np.ascontiguousarray(dk.transpose(2, 3, 1, 0).reshape(128, 8, 512))
        m["cv2"] = np.ascontiguousarray(g["cache_diff_v"][b, 0].reshape(512, 1024))
        m.update(_host_ssd_core(g, b))
        in_maps.append(m)
    import os
    ncores = int(os.environ.get("KCORES", "8"))
    ktrace = os.environ.get("KTRACE", "") == "1"
    res = run_bass_kernel_spmd(nc, in_maps[:ncores], core_ids=list(range(ncores)), trace=ktrace) if ktrace else run_bass_kernel_spmd(nc, in_maps[:ncores], core_ids=list(range(ncores)))
    if ktrace:
        print("EXEC_TIME_NS", res.exec_time_ns)
    R = list(res.results) + [res.results[0]] * (8 - ncores)
    y_prompt = np.zeros((16, 256, 1024), f32)
    y_sample = np.zeros((8, 4096, 1024), f32)
    win_k = np.zeros((16, 1, 256, 4, 64), f32)
    win_v = np.zeros((16, 1, 256, 4, 64), f32)
    gla_f = np.zeros((16, 1, 4, 128, 256), f32)
    gla_b = np.zeros((16, 1, 4, 128, 256), f32)
    diff_k = np.zeros((16, 1, 256, 8, 2, 64), f32)
    diff_v = np.zeros((16, 1, 256, 8, 128), f32)
    ssd_f = np.zeros((16, 1, 32, 64, 128), f32)
    ssd_b = np.zeros((16, 1, 32, 64, 128), f32)
    for b in range(8):
        r = R[b]
        y = r["yT"].transpose(1, 0, 2).reshape(1024, T).T
        y_sample[b] = y[0:4096]
        y_prompt[2 * b:2 * b + 2] = y[4096:].reshape(2, 256, 1024)
        wk = r["o_wk"]
        win_k[2 * b:2 * b + 2, 0] = wk.transpose(2, 0, 1).reshape(2, 256, 4, 64)
        win_v[2 * b:2 * b + 2, 0] = r["o_wv"].reshape(2, 256, 4, 64)
        gla_f[2 * b:2 * b + 2, 0] = r["o_gf"].transpose(0, 2, 1, 3)
        gla_b[2 * b:2 * b + 2, 0] = r["o_gb"].transpose(0, 2, 1, 3)
        dk = r["o_dk"]
        diff_k[2 * b:2 * b + 2, 0] = dk.transpose(2, 0, 1).reshape(2, 256, 8, 2, 64)
        diff_v[2 * b:2 * b + 2, 0] = r["o_dv"].reshape(2, 256, 8, 128)
        _host_ssd_out(r, b, ssd_f, ssd_b)
    return (y_prompt, y_sample, win_k, win_v, gla_f, gla_b, diff_k, diff_v, ssd_f, ssd_b)


def _host_ssd_common(g):
    w = g["ssd_w_in"][0]
    out = {}
    out["w3_z"] = _fm(w[:, 0:2048])
    out["w3_xbc"] = _fm(w[:, 2048:5120])
    out["w3_dt"] = _fm(w[:, 5120:5184])
    out["w3_o"] = _fm(g["ssd_w_out"][0])
    out["ssd_cw"] = np.ascontiguousarray(g["ssd_conv_w"][0].reshape(3, 24, 128).transpose(2, 0, 1))
    out["ssd_cb"] = _pc(g["ssd_conv_b"][0])
    out["ssd_vec"] = np.ascontiguousarray(np.stack([g["ssd_a_log_f"][0], g["ssd_a_log_b"][0], g["ssd_dt_bias_f"][0], g["ssd_dt_bias_b"][0], g["ssd_d"][0]]))
    out["ssd_gn"] = np.ascontiguousarray(g["ssd_norm"][0])
    return out


def _host_ssd_core(g, b):
    return {"s3f": np.ascontiguousarray(g["state_ssd_fwd"][b, 0].transpose(2, 0, 1).reshape(128, 2048)),
            "s3b": np.ascontiguousarray(g["state_ssd_bwd"][b, 0].transpose(2, 0, 1).reshape(128, 2048))}


def _host_ssd_out(r, b, ssd_f, ssd_b):
    ssd_f[2 * b:2 * b + 2, 0] = r["o_sf"].reshape(2, 128, 32, 64).transpose(0, 2, 3, 1)
    ssd_b[2 * b:2 * b + 2, 0] = r["o_sb"].reshape(2, 128, 32, 64).transpose(0, 2, 3, 1)
```

```python
import numpy as np
import concourse.bass as bass
import concourse.mybir as mybir
from concourse.bass_utils import run_bass_kernel_spmd

F32 = mybir.dt.float32
BF16 = mybir.dt.bfloat16
AF = mybir.ActivationFunctionType
ALU = mybir.AluOpType
AX = mybir.AxisListType
AP = bass.AP

SB_BASE = 16512
SB_TOP = 229344
EPOCH = 30000


class Buf:
    __slots__ = ("w", "rs", "name")

    def __init__(self, name=""):
        self.w = None
        self.rs = []
        self.name = name


class Op:
    __slots__ = ("eng", "fn", "deps", "dma", "ms", "sem", "val", "need", "prev", "grp")

    def __init__(self, eng, fn, dma):
        self.eng = eng
        self.fn = fn
        self.dma = dma
        self.deps = []
        self.ms = None
        self.sem = None
        self.val = None
        self.need = False
        self.prev = None
        self.grp = None


class Sched:
    ENGS = ("pe", "act", "dve", "pool", "sp")

    def __init__(self, nc):
        self.nc = nc
        self.ops = {e: [] for e in self.ENGS}
        self.dmas_since_barrier = []
        self.all_dmas = []
        self.nps = 0
        self.nacc = 0
        self.psum = []
        for i in range(8):
            t = nc.alloc_psum_tensor("psb%d" % i, [128, 512], F32)
            self.psum.append((t, Buf("ps%d" % i)))
        self.sb_off = SB_BASE
        self.sb_mark = SB_BASE
        self.nalloc = 0

    def alloc(self, shape, dtype, name=None):
        nbytes = int(np.prod(shape[1:])) * (4 if dtype == F32 else 2)
        nbytes = (nbytes + 63) // 64 * 64
        off = self.sb_off
        assert off + nbytes <= SB_TOP, "SBUF overflow %d" % (off + nbytes - SB_TOP)
        self.sb_off += nbytes
        self.nalloc += 1
        t = self.nc.alloc_sbuf_tensor_at("sb%d_%s" % (self.nalloc, name or "t"), list(shape), dtype, offset=off)
        return t

    def mark(self):
        self.sb_mark = self.sb_off

    def release(self):
        self.sb_off = self.sb_mark

    def ps(self, pool="a"):
        if pool == "a":
            t, b = self.psum[self.nps % 6]
            self.nps += 1
        else:
            t, b = self.psum[6 + self.nacc % 2]
            self.nacc += 1
        return t, b

    def op(self, eng, fn, reads=(), writes=(), dma=False):
        o = Op(eng, fn, dma)
        deps = {}
        for b in reads:
            d = b.w
            if d is not None:
                if (not dma) and (not d.dma) and d.eng == eng and eng == "pe":
                    continue
                deps[id(d)] = d
        for b in writes:
            cand = list(b.rs)
            if b.w is not None:
                cand.append(b.w)
            for d in cand:
                if d is o:
                    continue
                if (not dma) and (not d.dma) and d.eng == eng:
                    continue
                deps[id(d)] = d
        o.deps = list(deps.values())
        for b in reads:
            if not dma:
                b.rs = [r for r in b.rs if r.dma or r.eng != eng]
            b.rs.append(o)
        for b in writes:
            b.w = o
            b.rs = []
        self.ops[eng].append(o)
        if dma:
            self.dmas_since_barrier.append(o)
            self.all_dmas.append(o)
        return o

    def barrier(self):
        lasts = []
        for e in self.ENGS:
            for o in reversed(self.ops[e]):
                if not o.dma and o.fn is not None:
                    lasts.append(o)
                    break
        deps = lasts + self.dmas_since_barrier
        self.dmas_since_barrier = []
        for e in self.ENGS:
            o = Op(e, None, False)
            o.deps = list(deps)
            self.ops[e].append(o)

    def dma(self, out, in_, reads=(), writes=(), eng=None):
        if eng is None:
            eng = "pool" if type(out.tensor).__name__.startswith("DRam") else "sp"
        return self.op(eng, lambda e: e.dma_start(out=out, in_=in_), reads, writes, dma=True)

    def mm(self, out, lhsT, rhs, start, stop, reads=(), writes=(), grp=None):
        o = self.op("pe", lambda e: e.matmul(out, lhsT, rhs, start=start, stop=stop), reads, writes)
        o.grp = grp
        return o

    def emit(self):
        nc = self.nc
        for e in self.ENGS:
            for o in self.ops[e]:
                for d in o.deps:
                    d.need = True
        sem_ctx = []
        import contextlib
        with contextlib.ExitStack() as st:
            tl = {}
            for e in ("pe", "act", "dve", "pool"):
                tl[e] = [st.enter_context(nc.semaphore("tl_%s_%d" % (e, i))) for i in range(5)]
            npool = {"sp": 36, "pool": 36, "act": 8}
            dpool = {e: [st.enter_context(nc.semaphore("dq_%s_%d" % (e, i))) for i in range(n)] for e, n in npool.items()}
            for e in self.ENGS:
                m = 0
                k = 0
                for o in self.ops[e]:
                    if o.fn is None:
                        continue
                    if o.dma:
                        P = len(dpool[e])
                        o.sem = dpool[e][k % P]
                        o.val = 16 * (k // P + 1)
                        k += 1
                    elif o.need:
                        o.sem = tl[e][m // EPOCH]
                        o.val = m % EPOCH + 1
                        m += 1
                assert m < EPOCH * 5, (e, m)
            final = {}
            for e, n in npool.items():
                for o in self.ops[e]:
                    if o.dma:
                        final[id(o.sem)] = (o.sem, o.val)

            def stream(e, eng):
                seen = {}

                def wait(sem, val):
                    if seen.get(id(sem), 0) < val:
                        eng.wait_ge(sem, val)
                        seen[id(sem)] = val

                ops_ = self.ops[e]
                i_ = 0
                while i_ < len(ops_):
                    o = ops_[i_]
                    j_ = i_ + 1
                    if o.grp is not None:
                        while j_ < len(ops_) and ops_[j_].grp == o.grp:
                            j_ += 1
                    for k_ in range(i_, j_):
                        for d in ops_[k_].deps:
                            wait(d.sem, d.val)
                    for k_ in range(i_, j_):
                        o = ops_[k_]
                        if o.fn is None:
                            continue
                        if o.dma and o.val > 16:
                            wait(o.sem, o.val - 16)
                        ins_ = o.fn(eng)
                        if o.dma:
                            ins_.then_inc(o.sem, 16)
                        elif o.need:
                            ins_.then_inc(o.sem, 1)
                    i_ = j_
                if e == "sp":
                    for sem, val in final.values():
                        wait(sem, val)

            with nc.Block() as block:
                @block.tensor
                def _(eng):
                    stream("pe", eng)

                @block.scalar
                def _(eng):
                    stream("act", eng)

                @block.vector
                def _(eng):
                    stream("dve", eng)

                @block.gpsimd
                def _(eng):
                    stream("pool", eng)

                @block.sync
                def _(eng):
                    stream("sp", eng)
        return {e: len(v) for e, v in self.ops.items()}


def ins(method, *a, **kw):
    return lambda e: getattr(e, method)(*a, **kw)


T = 4608
LS = 4096
TILES = [(i * 512, 512, 0, 0, 4096) for i in range(8)] + [(4096, 256, 1, 4096, 4352), (4352, 256, 1, 4352, 4608)]
EPS = 1e-6
DFF = 2816
LAM_INIT = 0.8 - 0.6 * float(np.exp(-0.3 * 2))


def build_program(NLAYERS=4, dbg=False):
    nc = bass.Bass("TRN2", target_bir_lowering=False)
    s = Sched(nc)
    I = {}
    O = {}

    def din(name, shape):
        I[name] = nc.dram_tensor(name, list(shape), F32, kind="ExternalInput").ap()
        return I[name]

    def dout(name, shape):
        O[name] = nc.dram_tensor(name, list(shape), F32, kind="ExternalOutput").ap()
        return O[name]

    def dscr(name, shape, dt):
        return nc.dram_tensor(name, list(shape), dt).ap()

    xT = din("xT", [128, 8, T])
    condT = din("condT", [128, 8, 2])
    ada_w = din("ada_w", [4, 128, 8, 6144])
    ada_b = din("ada_b", [128, 4, 48])
    nrm = din("nrm", [128, 4, 2, 8])
    fnorm = din("fnorm", [128, 8])
    w_up = din("w_up", [4, 128, 8, 5632])
    w_dn = din("w_dn", [4, 128, 22, 1024])
    ffn_cw = din("ffn_cw", [128, 4, 3, 44])
    ffn_cb = din("ffn_cb", [128, 4, 44])
    perm_in = din("perm", [128, 128])
    rcos = din("rcos", [128, LS])
    rsin = din("rsin", [128, LS])
    wmask_in = din("wmask", [128, 6, 512])
    tri_in = din("tri", [128, 4, 128])
    m64_in = din("m64", [128, 512])
    w0_qk = din("w0_qk", [128, 8, 1536])
    w0_v = din("w0_v", [128, 8, 256])
    w0_o = din("w0_o", [128, 8, 1024])
    sink_in = din("sink", [16])
    ck0 = din("ck0", [128, 4, 512])
    cv0 = din("cv0", [512, 256])
    w1_qk = din("w1_qk", [128, 8, 1024])
    w1_v = din("w1_v", [128, 8, 1024])
    w1_r = din("w1_r", [128, 8, 1024])
    w1_g1 = din("w1_g1", [128, 8, 32])
    w1_g2 = din("w1_g2", [32, 1024])
    w1_gb = din("w1_gb", [128, 8])
    w1_gn = din("w1_gn", [128, 2])
    w1_o = din("w1_o", [128, 8, 1024])
    s1f = din("s1f", [128, 4, 256])
    s1b = din("s1b", [128, 4, 256])
    w2_qk = din("w2_qk", [128, 8, 2048])
    w2_v = din("w2_v", [128, 8, 1024])
    w2_o = din("w2_o", [128, 8, 1024])
    lqk = din("lqk", [4, 64])
    w2_gn = din("w2_gn", [128, 1])
    ck2 = din("ck2", [128, 8, 512])
    cv2 = din("cv2", [512, 1024])

    w3_z = din("w3_z", [128, 8, 2048])
    w3_xbc = din("w3_xbc", [128, 8, 3072])
    w3_dt = din("w3_dt", [128, 8, 64])
    w3_o = din("w3_o", [128, 16, 1024])
    ssd_cw = din("ssd_cw", [128, 3, 24])
    ssd_cb = din("ssd_cb", [128, 24])
    ssd_vec = din("ssd_vec", [5, 32])
    ssd_gn = din("ssd_gn", [2048])
    s3f = din("s3f", [128, 2048])
    s3b = din("s3b", [128, 2048])

    yT = dout("yT", [128, 8, T])
    o_sf = dout("o_sf", [2, 128, 2048])
    o_sb = dout("o_sb", [2, 128, 2048])
    o_wk = dout("o_wk", [4, 64, 512])
    o_wv = dout("o_wv", [512, 256])
    o_gf = dout("o_gf", [2, 128, 4, 256])
    o_gb = dout("o_gb", [2, 128, 4, 256])
    o_dk = dout("o_dk", [8, 128, 512])
    o_dv = dout("o_dv", [512, 1024])

    X = dscr("X", [128, 8, T], F32)
    U = dscr("U", [128, 44, T], BF16)
    QK = dscr("QK", [128, 16, T], BF16)
    VS = dscr("VS", [T, 1024], BF16)
    OS = dscr("OS", [128, 8, T], BF16)
    OS2 = dscr("OS2", [128, 8, T], BF16)
    RS = dscr("RS", [128, 8, T], BF16)
    QD = dscr("QD", [2, 128, 4, T], BF16)
    KD = dscr("KD", [2, 128, 4, T], BF16)
    KR = dscr("KR", [T, 8, 128], BF16)
    ZS = dscr("ZS", [T, 2048], BF16)
    DTS = dscr("DTS", [T, 128], F32)
    XTs = dscr("XTs", [T, 2048], BF16)
    BTs = dscr("BTs", [T, 512], BF16)
    YD = [dscr("YD0", [T, 2048], BF16), dscr("YD1", [T, 2048], BF16)]

    NT = len(TILES)
    import os
    STOP = int(os.environ.get("KSTOP", "99"))

    def rows_view(dr, rowlen, r0, nb, c0, w):
        return AP(dr.tensor, r0 * rowlen + c0, [[rowlen, 128], [128 * rowlen, nb], [1, w]])

    def kr_view(r0, nb, j0, nj):
        return AP(KR.tensor, r0 * 1024 + j0 * 128, [[1024, 128], [128 * 1024, nb], [128, nj], [1, 128]])
    import os
    KQ = os.environ.get("KQ", "")

    def tb(name):
        return [Buf(name + str(i)) for i in range(NT)]

    xT_b = tb("xT")
    X_b = tb("X")

    MODS = s.alloc([128, 4, 6, 8, 2], F32, "mods")
    GS = s.alloc([128, 4, 2, 8, 2], F32, "gs")
    NRM = s.alloc([128, 4, 2, 8], F32, "nrm")
    FNRM = s.alloc([128, 8], F32, "fnrm")
    ones_bf = s.alloc([128, 128], BF16, "ones")
    ones_f = s.alloc([128, 128], F32, "onesf")
    ident_bf = s.alloc([128, 128], BF16, "ident")
    perm_bf = s.alloc([128, 128], BF16, "perm")
    tri_bf = s.alloc([128, 4, 128], BF16, "tri")
    m64 = s.alloc([128, 512], F32, "m64")
    cb_const = Buf("consts")
    stage = s.alloc([128, 4, 128], F32, "stage")
    bstage = Buf()

    s.dma(NRM[:], nrm, writes=[cb_const])
    s.dma(FNRM[:], fnorm, writes=[cb_const])
    s.dma(m64[:], m64_in, writes=[cb_const])
    s.op("pool", ins("memset", ones_f[:], 1.0), writes=[cb_const])
    s.op("dve", ins("tensor_copy", out=ones_bf[:], in_=ones_f[:]), reads=[cb_const], writes=[cb_const])
    s.dma(stage[:, 0, :], perm_in, writes=[bstage])
    s.op("dve", ins("tensor_copy", out=perm_bf[:], in_=stage[:, 0, :]), reads=[bstage], writes=[cb_const])
    s.op("pool", ins("memset", stage[:, 1, :], 1.0), reads=[], writes=[bstage])
    s.op("pool", ins("affine_select", out=stage[:, 1, :], in_=stage[:, 1, :], pattern=[[-1, 128]], compare_op=ALU.is_equal, fill=0.0, base=0, channel_multiplier=1), reads=[bstage], writes=[bstage])
    s.op("dve", ins("tensor_copy", out=ident_bf[:], in_=stage[:, 1, :]), reads=[bstage], writes=[cb_const])
    s.barrier()
    s.dma(stage[:], tri_in, writes=[bstage])
    s.op("dve", ins("tensor_copy", out=tri_bf[:], in_=stage[:]), reads=[bstage], writes=[cb_const])
    s.mark()

    def mod_ap(l, k, c, cond):
        return MODS[:, l, k, c, cond:cond + 1]

    def prologue():
        s.barrier()
        s.release()
        sc = s.alloc([128, 8, 2], F32, "sc")
        scb = Buf()
        adb = s.alloc([128, 4, 48], F32, "adb")
        adbb = Buf()
        s.dma(sc[:], condT, writes=[scb])
        s.dma(adb[:], ada_b, writes=[adbb])
        s.op("act", ins("activation", out=sc[:], in_=sc[:], func=AF.Silu), reads=[scb], writes=[scb])
        wb = [s.alloc([128, 8, 1024], F32, "adaw%d" % i) for i in range(2)]
        wbb = [[Buf() for _ in range(2)] for _ in range(2)]
        it = 0
        for l in range(NLAYERS):
            for g in range(6):
                w = wb[it % 2]
                bb = wbb[it % 2]
                for hh in range(2):
                    s.dma(w[:, hh * 4:(hh + 1) * 4, :], ada_w[l, :, hh * 4:(hh + 1) * 4, g * 1024:(g + 1) * 1024], writes=[bb[hh]], eng=("sp" if hh == 0 else "act"))
                ps, pb = s.ps()
                for cc in range(8):
                    for kc in range(8):
                        s.mm(ps[:, cc * 2:cc * 2 + 2], w[:, kc, cc * 128:(cc + 1) * 128], sc[:, kc, :], kc == 0, kc == 7, reads=[bb[kc // 4], scb], writes=[pb])
                s.op("dve", ins("tensor_tensor",
                    out=MODS[:, l, g, :, :], in0=ps[:, 0:16].rearrange("p (c t) -> p c t", t=2),
                    in1=adb[:, l, g * 8:(g + 1) * 8].unsqueeze(2).to_broadcast([128, 8, 2]), op=ALU.add),
                    reads=[pb, adbb], writes=[cb_const])
                it += 1
            for which in range(2):
                k = 1 if which == 0 else 4
                s.op("dve", ins("scalar_tensor_tensor",
                    out=GS[:, l, which, :, :], in0=MODS[:, l, k, :, :], scalar=1.0,
                    in1=NRM[:, l, which, :].unsqueeze(2).to_broadcast([128, 8, 2]), op0=ALU.add, op1=ALU.mult),
                    reads=[cb_const], writes=[cb_const])

    def load_w(dst, src, nk, split=1):
        bufs = []
        for kc in range(nk):
            b = Buf()
            s.dma(dst[:, kc, :], src[:, kc, :], writes=[b], eng="pool")
            bufs.append(b)
        return bufs

    class NormCtx:
        def __init__(self, nbuf=2):
            self.nbuf = nbuf
            self.xt = [(s.alloc([128, 8, 512], F32, "nxt"), Buf()) for _ in range(nbuf)]
            self.h = [(s.alloc([128, 8, 512], BF16, "nh"), Buf()) for _ in range(nbuf)]
            self.sq = (s.alloc([128, 8, 512], BF16, "nsq"), Buf())
            self.tmp = (s.alloc([128, 8, 512], F32, "ntmp"), Buf())
            self.rstd = [(s.alloc([128, 512], F32, "nrs"), Buf()) for _ in range(2)]
            self.k = 0

    def norm_tile(ctx, l, which, ti, Xsrc, Xsrc_b, final_out=None):
        t0, n, cond, _, _ = TILES[ti]
        k = ctx.k
        ctx.k += 1
        xt, xtb = ctx.xt[k % ctx.nbuf]
        h, hb = ctx.h[k % ctx.nbuf]
        sq, sqb = ctx.sq
        tmp, tmpb = ctx.tmp
        rstd, rb = ctx.rstd[k % 2]
        s.dma(xt[:, :, :n], Xsrc[:, :, t0:t0 + n], reads=[Xsrc_b[ti]], writes=[xtb])
        s.op("act", ins("activation", out=sq[:, :, :n], in_=xt[:, :, :n], func=AF.Square), reads=[xtb], writes=[sqb])
        ps, pb = s.ps()
        for c in range(8):
            s.mm(ps[:, :n], ones_bf[:], sq[:, c, :n], c == 0, c == 7, reads=[sqb, cb_const], writes=[pb])
        s.op("act", ins("activation", out=rstd[:, :n], in_=ps[:, :n], func=AF.Sqrt, scale=1.0 / 1024, bias=EPSB[:, 0:1]), reads=[pb], writes=[rb])
        s.op("dve", ins("reciprocal", out=rstd[:, :n], in_=rstd[:, :n]), reads=[rb], writes=[rb])
        s.op("dve", ins("tensor_tensor", out=tmp[:, :, :n], in0=xt[:, :, :n], in1=rstd[:, :n].unsqueeze(1).to_broadcast([128, 8, n]), op=ALU.mult),
             reads=[xtb, rb], writes=[tmpb])
        if final_out is None:
            for c in range(8):
                s.op("act", ins("activation", out=h[:, c, :n], in_=tmp[:, c, :n], func=AF.Identity,
                                                          scale=GS[:, l, which, c, cond:cond + 1], bias=mod_ap(l, 0 if which == 0 else 3, c, cond)),
                     reads=[tmpb, cb_const], writes=[hb])
            return h, hb
        else:
            for c in range(8):
                s.op("act", ins("activation", out=xt[:, c, :n], in_=tmp[:, c, :n], func=AF.Identity, scale=FNRM[:, c:c + 1]),
                     reads=[tmpb, cb_const], writes=[xtb])
            s.dma(final_out[:, :, t0:t0 + n], xt[:, :, :n], reads=[xtb])
            return None, None

    EPSB = s.alloc([128, 1], F32, "epsb")
    s.op("pool", ins("memset", EPSB[:], EPS), writes=[cb_const])
    s.mark()

    class ResCtx:
        def __init__(self):
            self.xo = [(s.alloc([128, 8, 512], F32, "xo"), Buf()) for _ in range(2)]
            self.k = 0

    def outproj_resid(rctx, l, gate_k, ti, W, wbufs, nk, rhs, rhsbufs, Xsrc, Xsrc_b, Xdst, Xdst_b):
        t0, n, cond, _, _ = TILES[ti]
        xo, xob = rctx.xo[rctx.k % 2]
        rctx.k += 1
        s.dma(xo[:, :, :n], Xsrc[:, :, t0:t0 + n], reads=[Xsrc_b[ti]], writes=[xob])
        for dc in range(8):
            ps, pb = s.ps()
            for kc in range(nk):
                s.mm(ps[:, :n], W[:, kc, dc * 128:(dc + 1) * 128], rhs[:, kc, :n], kc == 0, kc == nk - 1,
                     reads=[wbufs[kc]] + list(rhsbufs), writes=[pb])
            s.op("dve", ins("scalar_tensor_tensor", out=xo[:, dc, :n], in0=ps[:, :n], scalar=mod_ap(l, gate_k, dc, cond),
                                                                      in1=xo[:, dc, :n], op0=ALU.mult, op1=ALU.add),
                 reads=[pb, xob, cb_const], writes=[xob])
        s.dma(Xdst[:, :, t0:t0 + n], xo[:, :, :n], reads=[xob], writes=[Xdst_b[ti]])

    def ffn(l, Xsrc, Xsrc_b):
        s.barrier()
        s.release()
        Wup = s.alloc([128, 8, 5632], BF16, "wup")
        wb = load_w(Wup, w_up[l], 8)
        nctx = NormCtx()
        ub = [(s.alloc([128, 11, 512], BF16, "ub"), [Buf() for _ in range(11)]) for _ in range(2)]
        U_b = [[Buf() for _ in range(4)] for _ in range(NT)]
        ku = 0
        ev = 0
        for ti in range(NT):
            t0, n, cond, _, _ = TILES[ti]
            h, hb = norm_tile(nctx, l, 1, ti, Xsrc, Xsrc_b)
            for grp in range(4):
                ut, utb = ub[ku % 2]
                ku += 1
                for cc in range(11):
                    col = grp * 11 + cc
                    ps, pb = s.ps()
                    for kc in range(8):
                        s.mm(ps[:, :n], Wup[:, kc, col * 128:(col + 1) * 128], h[:, kc, :n], kc == 0, kc == 7, reads=[wb[kc], hb], writes=[pb])
                    if ev % 2 == 0:
                        s.op("act", ins("activation", out=ut[:, cc, :n], in_=ps[:, :n], func=AF.Identity), reads=[pb], writes=[utb[cc]])
                    else:
                        s.op("dve", ins("tensor_copy", out=ut[:, cc, :n], in_=ps[:, :n]), reads=[pb], writes=[utb[cc]])
                    ev += 1
                s.dma(U[:, grp * 11:(grp + 1) * 11, t0:t0 + n], ut[:, :, :n], reads=utb, writes=[U_b[ti][grp]])
        s.barrier()
        s.release()
        Wd = s.alloc([128, 22, 1024], BF16, "wd")
        wdb = load_w(Wd, w_dn[l], 22)
        cw = s.alloc([128, 3, 44], F32, "cw")
        cbias = s.alloc([128, 44], F32, "cbias")
        cwb = Buf()
        s.dma(cw[:], ffn_cw[:, l, :, :], writes=[cwb])
        s.dma(cbias[:], ffn_cb[:, l, :], writes=[cwb])
        ug = s.alloc([128, 44, 514], BF16, "ug")
        ugb = [Buf() for _ in range(4)]
        at = [(s.alloc([128, 22, 512], BF16, "at"), Buf()) for _ in range(2)]
        tmps = [[(s.alloc([128, 512], F32, "ft"), Buf()) for _ in range(3)] for _ in range(4)]
        rctx = ResCtx()
        kp = 0
        for ti in range(NT):
            t0, n, cond, s0, s1 = TILES[ti]
            lo = (t0 - 1) >= s0
            hi = (t0 + n) < s1
            for grp in range(4):
                gs_ = slice(grp * 11, (grp + 1) * 11)
                if not lo:
                    s.op("pool", ins("memset", ug[:, gs_, 0:1], 0.0), writes=[ugb[grp]])
                if not hi:
                    s.op("pool", ins("memset", ug[:, gs_, n + 1:n + 2], 0.0), writes=[ugb[grp]])
                c0 = 0 if lo else 1
                c1 = n + 2 if hi else n + 1
                rd = [U_b[ti][grp]]
                if lo:
                    rd.append(U_b[ti - 1][grp])
                if hi:
                    rd.append(U_b[ti + 1][grp])
                s.dma(ug[:, gs_, c0:c1], U[:, gs_, t0 - 1 + c0:t0 - 1 + c1], reads=rd, writes=[ugb[grp]])
            a, ab = at[ti % 2]
            for cc in range(22):
                res = []
                for half in range(2):
                    col = cc + 22 * half
                    tt, ttb = tmps[kp % 4][half]
                    grp = col // 11
                    s.op("act", ins("activation", out=tt[:, :n], in_=ug[:, col, 1:n + 1], func=AF.Identity,
                                                                      scale=cw[:, 1, col:col + 1], bias=cbias[:, col:col + 1]),
                         reads=[ugb[grp], cwb], writes=[ttb])
                    s.op("dve", ins("scalar_tensor_tensor", out=tt[:, :n], in0=ug[:, col, 0:n], scalar=cw[:, 0, col:col + 1],
                                                                               in1=tt[:, :n], op0=ALU.mult, op1=ALU.add),
                         reads=[ugb[grp], cwb, ttb], writes=[ttb])
                    s.op("dve", ins("scalar_tensor_tensor", out=tt[:, :n], in0=ug[:, col, 2:n + 2], scalar=cw[:, 2, col:col + 1],
                                                                               in1=tt[:, :n], op0=ALU.mult, op1=ALU.add),
                         reads=[ugb[grp], cwb, ttb], writes=[ttb])
                    res.append((tt, ttb))
                sg, sgb = tmps[kp % 4][2]
                kp += 1
                s.op("act", ins("activation", out=sg[:, :n], in_=res[0][0][:, :n], func=AF.Silu), reads=[res[0][1]], writes=[sgb])
                s.op("pool", ins("tensor_tensor", out=a[:, cc, :n], in0=sg[:, :n], in1=res[1][0][:, :n], op=ALU.mult),
                     reads=[sgb, res[1][1]], writes=[ab])
            outproj_resid(rctx, l, 5, ti, Wd, wdb, 22, a, [ab], Xsrc, Xsrc_b, X, X_b)

    def qkv_phase(l, Wqk_src, nqk, Wv_src, nv, Xsrc, Xsrc_b, rope, kout=None, vout=None, post=None):
        s.barrier()
        s.release()
        Wqk = s.alloc([128, 8, nqk * 128], BF16, "wqk")
        wqb = load_w(Wqk, Wqk_src, 8)
        Wv = s.alloc([128, 8, nv], BF16, "wv")
        wvb = load_w(Wv, Wv_src, 8)
        nctx = NormCtx()
        qk = [(s.alloc([128, nqk, 512], BF16, "qk"), [Buf() for _ in range(nqk)]) for _ in range(2)]
        vt = [(s.alloc([128, 4, nv], BF16, "vt"), Buf()) for _ in range(2)]
        qb = [(s.alloc([128, 512], BF16, "qb"), Buf()) for _ in range(2)]
        t12 = [[(s.alloc([128, 512], F32, "rt"), Buf()) for _ in range(2)] for _ in range(2)]
        cs = [(s.alloc([128, 2, 512], F32, "cs"), Buf()) for _ in range(2)]
        kf = [(s.alloc([128, 256], F32, "kf"), Buf()) for _ in range(2)]
        vf = [(s.alloc([128, 512], F32, "vf"), Buf()) for _ in range(2)]
        QK_b = [Buf() for _ in range(NT)]
        VS_b = [Buf() for _ in range(NT)]
        if "L" in KQ:
            return
        kq = 0
        kk = 0
        kv = 0
        for ti in range(NT):
            t0, n, cond, s0, s1 = TILES[ti]
            h, hb = norm_tile(nctx, l, 0, ti, Xsrc, Xsrc_b)
            if "N" in KQ:
                continue
            qkt, qkb = qk[ti % 2]
            dorope = rope and cond == 0 and ("r" not in KQ)
            if dorope:
                cst, csb = cs[ti % 2]
                s.dma(cst[:, 0, :], rcos[:, t0:t0 + n], writes=[csb])
                s.dma(cst[:, 1, :], rsin[:, t0:t0 + n], writes=[csb])
            for cc in range(nqk):
                ps, pb = s.ps()
                for kc in range(8):
                    s.mm(ps[:, :n], Wqk[:, kc, cc * 128:(cc + 1) * 128], h[:, kc, :n], kc == 0, kc == 7, reads=[wqb[kc], hb], writes=[pb])
                if dorope:
                    q_, q_b = qb[kq % 2]
                    t1, t1b = t12[kq % 2][0]
                    t2, t2b = t12[kq % 2][1]
                    kq += 1
                    s.op("act", ins("activation", out=q_[:, :n], in_=ps[:, :n], func=AF.Identity), reads=[pb], writes=[q_b])
                    ps2, pb2 = s.ps()
                    s.mm(ps2[:, :n], perm_bf[:], q_[:, :n], True, True, reads=[q_b, cb_const], writes=[pb2])
                    s.op("dve", ins("tensor_tensor", out=t1[:, :n], in0=q_[:, :n], in1=cst[:, 0, :n], op=ALU.mult),
                         reads=[q_b, csb], writes=[t1b])
                    s.op("dve", ins("tensor_tensor", out=t2[:, :n], in0=ps2[:, :n], in1=cst[:, 1, :n], op=ALU.mult),
                         reads=[pb2, csb], writes=[t2b])
                    s.op("pool", ins("tensor_tensor", out=qkt[:, cc, :n], in0=t1[:, :n], in1=t2[:, :n], op=ALU.add),
                         reads=[t1b, t2b], writes=[qkb[cc]])
                else:
                    if kout is not None and cond == 1 and cc >= kout[0]:
                        kft, kfb = kf[kk % 2]
                        kk += 1
                        rows = kout[2]
                        s.op("dve", ins("tensor_copy", out=kft[:, :n], in_=ps[:, :n]), reads=[pb], writes=[kfb])
                        s.op("act", ins("activation", out=qkt[:, cc, :n], in_=kft[:, :n], func=AF.Identity), reads=[kfb], writes=[qkb[cc]])
                        s.dma(kout[1][cc - kout[0], :, t0 - LS:t0 - LS + n], kft[0:rows, :n], reads=[kfb])
                    else:
                        s.op("act", ins("activation", out=qkt[:, cc, :n], in_=ps[:, :n], func=AF.Identity), reads=[pb], writes=[qkb[cc]])
                if post is not None:
                    post(ti, cc, ps, pb)
            vtt, vtb = vt[ti % 2]
            for tbk in range(n // 128 if "v" not in KQ else 0):
                for vg in range((nv + 511) // 512):
                    w_ = min(512, nv - vg * 512)
                    ps, pb = s.ps()
                    for kc in range(8):
                        s.mm(ps[:, :w_], h[:, kc, tbk * 128:(tbk + 1) * 128], Wv[:, kc, vg * 512:vg * 512 + w_], kc == 0, kc == 7, reads=[wvb[kc], hb], writes=[pb])
                    if vout is not None and cond == 1:
                        vft, vfb = vf[kv % 2]
                        kv += 1
                        s.op("dve", ins("tensor_copy", out=vft[:, :w_], in_=ps[:, :w_]), reads=[pb], writes=[vfb])
                        s.op("act", ins("activation", out=vtt[:, tbk, vg * 512:vg * 512 + w_], in_=vft[:, :w_], func=AF.Identity),
                             reads=[vfb], writes=[vtb])
                        r0 = t0 - LS + tbk * 128
                        s.dma(vout[r0:r0 + 128, vg * 512:vg * 512 + w_], vft[:, :w_], reads=[vfb])
                    else:
                        s.op("act", ins("activation", out=vtt[:, tbk, vg * 512:vg * 512 + w_], in_=ps[:, :w_], func=AF.Identity),
                             reads=[pb], writes=[vtb])
            if "q" not in KQ:
                s.dma(QK[:, 0:nqk, t0:t0 + n], qkt[:, :, :n], reads=qkb, writes=[QK_b[ti]])
            if "v" not in KQ and "s" not in KQ:
              s.dma(rows_view(VS, 1024, t0, n // 128, 0, nv), vtt[:, 0:n // 128, :], reads=[vtb], writes=[VS_b[ti]])

    def attn_phase(l, kind, Xsrc, Xsrc_b):
        s.barrier()
        s.release()
        win = kind == "win"
        nunits = 8
        kbase = 8
        Kt = [(s.alloc([128, T], BF16, "kt"), Buf()) for _ in range(2)]
        Va = [(s.alloc([128, 36, 128], BF16, "va"), Buf()) for _ in range(2)]
        Qt = [(s.alloc([128, LS], BF16, "qt"), Buf()) for _ in range(2)]
        Ot = [(s.alloc([128, LS], BF16, "ot"), Buf()) for _ in range(2)]
        pT = [(s.alloc([128, 512], BF16, "pT"), Buf()) for _ in range(8)]
        rec = [(s.alloc([128, 512], F32, "rec"), Buf()) for _ in range(8)]
        OS_b = [[Buf() for _ in range(nunits)] for _ in range(3)]
        cbuf = Buf()
        if win:
            masks = s.alloc([128, 6, 512], BF16, "masks")
            s.dma(masks[:], wmask_in, writes=[cbuf], eng="pool")
            esink = s.alloc([128, 16], F32, "esink")
            s.dma(esink[:], sink_in.partition_broadcast(128), writes=[cbuf])
            s.op("act", ins("activation", out=esink[:], in_=esink[:], func=AF.Exp), reads=[cbuf], writes=[cbuf])
            for i in range(2):
                s.op("pool", ins("memset", Va[i][0][:, :, 64:128], 1.0), writes=[Va[i][1]])
        else:
            acc = [(s.alloc([128, 512], F32, "acc"), Buf()) for _ in range(4)]
            osb = [(s.alloc([128, 512], F32, "osb"), Buf()) for _ in range(8)]
            sqd = [(s.alloc([128, 512], BF16, "sqd"), Buf()) for _ in range(4)]
            lq = s.alloc([128, 4, 64], F32, "lq")
            lam = s.alloc([128, 4], F32, "lam")
            gsub = s.alloc([128, 1], F32, "gsub")
            s.dma(lq[:], lqk.rearrange("a b -> (a b)").partition_broadcast(128).rearrange("p (a b) -> p a b", a=4), writes=[cbuf])
            s.dma(gsub[:], w2_gn, writes=[cbuf])
            s.op("dve", ins("tensor_tensor", out=lq[:, 0, :], in0=lq[:, 0, :], in1=lq[:, 1, :], op=ALU.mult), reads=[cbuf], writes=[cbuf])
            s.op("dve", ins("tensor_tensor", out=lq[:, 2, :], in0=lq[:, 2, :], in1=lq[:, 3, :], op=ALU.mult), reads=[cbuf], writes=[cbuf])
            s.op("dve", ins("reduce_sum", out=lam[:, 0:1], in_=lq[:, 0, :], axis=AX.X), reads=[cbuf], writes=[cbuf])
            s.op("dve", ins("reduce_sum", out=lam[:, 1:2], in_=lq[:, 2, :], axis=AX.X), reads=[cbuf], writes=[cbuf])
            s.op("act", ins("activation", out=lam[:, 0:2], in_=lam[:, 0:2], func=AF.Exp), reads=[cbuf], writes=[cbuf])
            s.op("dve", ins("tensor_tensor", out=lam[:, 2:3], in0=lam[:, 1:2], in1=lam[:, 0:1], op=ALU.subtract), reads=[cbuf], writes=[cbuf])
            s.op("dve", ins("tensor_scalar", out=lam[:, 2:3], in0=lam[:, 2:3], scalar1=-LAM_INIT, scalar2=None, op0=ALU.add), reads=[cbuf], writes=[cbuf])
            s.op("dve", ins("tensor_scalar", out=lam[:, 3:4], in0=gsub[:], scalar1=1.0 - LAM_INIT, scalar2=None, op0=ALU.mult), reads=[cbuf], writes=[cbuf])
        ku = 0
        kp = 0
        kr = 0
        ka = 0
        SEQS = [(0, 4096, 0), (4096, 256, 1), (4352, 256, 2)]
        for (q0, L, si) in SEQS:
            sample = si == 0
            nkb_lat = L // 128
            for u in range(nunits):
                kt, ktb = Kt[ku % 2]
                va, vab = Va[ku % 2]
                qt, qtb = Qt[ku % 2]
                ot, otb = Ot[ku % 2]
                ku += 1
                s.dma(qt[:, 0:L], QK[:, u, q0:q0 + L], writes=[qtb])
                if win:
                    g = u // 2
                    s.dma(kt[:, 0:L], QK[:, kbase + g, q0:q0 + L], writes=[ktb])
                    s.dma(va[:, 0:nkb_lat, 0:64], rows_view(VS, 1024, q0, nkb_lat, g * 64, 64), writes=[vab])
                    if sample:
                        s.dma(kt[:, L:L + 512], ck0[:, g, :], writes=[ktb], eng="pool")
                        s.dma(va[:, 32:36, 0:64], rows_view(cv0, 256, 0, 4, g * 64, 64), writes=[vab], eng="pool")
                else:
                    s.dma(kt[:, 0:L], QK[:, kbase + u, q0:q0 + L], writes=[ktb])
                    s.dma(va[:, 0:nkb_lat, :], rows_view(VS, 1024, q0, nkb_lat, u * 128, 128), writes=[vab])
                    if sample:
                        s.dma(kt[:, L:L + 512], ck2[:, u, :], writes=[ktb], eng="pool")
                        s.dma(va[:, 32:36, :], rows_view(cv2, 1024, 0, 4, u * 128, 128), writes=[vab], eng="pool")
                nq = 512 if sample else 256
                LOOK = 4
                tasks = []
                for qi in range(L // nq):
                    for e_ in range(2):
                        if win and sample:
                            kbs = [(kb, kb - qi * 4 + 1) for kb in range(qi * 4 - 1, qi * 4 + 5) if 0 <= kb < 32] + [(32 + j, None) for j in range(4)]
                        elif sample:
                            kbs = [(kb, None) for kb in range(36)]
                        else:
                            kbs = [(kb, None) for kb in range(2)]
                        for i, (kb, mi) in enumerate(kbs):
                            tasks.append((qi, e_, i, kb, mi, i == len(kbs) - 1))
                stt = {}

                def stage1(tk_):
                    nonlocal kp, ka
                    qi, e_, i, kb, mi, lastb = tk_
                    qs = slice(qi * nq, (qi + 1) * nq)
                    pr = slice(e_ * 64, (e_ + 1) * 64)
                    if i == 0:
                        d_ = {}
                        d_["pso"], d_["pob"] = s.ps("b")
                        if not win:
                            d_["ac"], d_["acb"] = acc[ka % 4]
                            ka += 1
                        stt[(qi, e_)] = d_
                    d_ = stt[(qi, e_)]
                    pss, psb_ = s.ps()
                    s.mm(pss[:, :nq], kt[pr, kb * 128:(kb + 1) * 128], qt[pr, qs], True, True, reads=[ktb, qtb], writes=[psb_], grp=("q", ku, cur_it[0] // 2))
                    p_, p_b = pT[kp % 8]
                    kp += 1
                    s.op("act", ins("activation", out=p_[:, :nq], in_=pss[:, :nq], func=AF.Exp, scale=0.125), reads=[psb_], writes=[p_b])
                    if mi is not None:
                        s.op("pool" if (kp % 2) else "dve", ins("tensor_tensor", out=p_[:, :nq], in0=p_[:, :nq], in1=masks[:, mi, :nq], op=ALU.mult),
                             reads=[p_b, cbuf], writes=[p_b])
                    if not win:
                        eng = "pool" if e_ == 0 else "dve"
                        ac, acb = d_["ac"], d_["acb"]
                        if i == 0:
                            s.op(eng, ins("tensor_copy", out=ac[:, :nq], in_=p_[:, :nq]), reads=[p_b], writes=[acb])
                        else:
                            s.op(eng, ins("tensor_tensor", out=ac[:, :nq], in0=ac[:, :nq], in1=p_[:, :nq], op=ALU.add), reads=[p_b, acb], writes=[acb])
                    d_[("p", i)] = (p_, p_b)

                def stage2(tk_):
                    nonlocal kr
                    qi, e_, i, kb, mi, lastb = tk_
                    qs = slice(qi * nq, (qi + 1) * nq)
                    pr = slice(e_ * 64, (e_ + 1) * 64)
                    d_ = stt[(qi, e_)]
                    pso, pob = d_["pso"], d_["pob"]
                    p_, p_b = d_.pop(("p", i))
                    s.mm(pso[:, :nq], va[:, kb, :], p_[:, :nq], i == 0, lastb, reads=[vab, p_b], writes=[pob], grp=("v", ku, cur_it[0] // 2))
                    if not lastb:
                        return
                    if win:
                        hh = u * 2 + e_
                        r_, r_b = rec[kr % 4]
                        kr += 1
                        s.op("dve", ins("tensor_scalar", out=r_[64:128, :nq], in0=pso[64:128, :nq], scalar1=esink[64:128, hh:hh + 1], scalar2=None, op0=ALU.add),
                             reads=[pob, cbuf], writes=[r_b])
                        s.op("dve", ins("reciprocal", out=r_[64:128, :nq], in_=r_[64:128, :nq]), reads=[r_b], writes=[r_b])
                        s.op("dve", ins("tensor_tensor", out=ot[pr, qs], in0=pso[0:64, :nq], in1=r_[64:128, :nq], op=ALU.mult),
                             reads=[pob, r_b], writes=[otb])
                        del stt[(qi, e_)]
                        return
                    ac, acb = d_["ac"], d_["acb"]
                    o_, o_b = osb[kr % 8]
                    r_, r_b = rec[kr % 8]
                    kr += 1
                    s.op("dve", ins("tensor_copy", out=o_[:, :nq], in_=pso[:, :nq]), reads=[pob], writes=[o_b])
                    d_["o"] = (o_, o_b)
                    key = (qi, e_)

                    def st1():
                        psd, pdb = s.ps()
                        s.mm(psd[:, :nq], ones_f[:], ac[:, :nq], True, True, reads=[acb, cb_const], writes=[pdb])
                        d_["psd"] = (psd, pdb)
                        defer(2, st2)

                    def st2():
                        psd, pdb = d_["psd"]
                        s.op("dve", ins("reciprocal", out=r_[:, :nq], in_=psd[:, :nq]), reads=[pdb], writes=[r_b])
                        defer(4, st3)

                    def st3():
                        s.op("dve", ins("tensor_tensor", out=o_[:, :nq], in0=o_[:, :nq], in1=r_[:, :nq], op=ALU.mult), reads=[o_b, r_b], writes=[o_b])
                        d_["done"] = True
                        if e_ == 1 or stt[(qi, 1)].get("done") if (qi, 1) in stt else False:
                            pass
                        if (qi, 0) in stt and (qi, 1) in stt and stt[(qi, 0)].get("done") and stt[(qi, 1)].get("done"):
                            defer(1, st4)

                    def st4():
                        (o0, o0b) = stt[(qi, 0)]["o"]
                        (o1, o1b) = stt[(qi, 1)]["o"]
                        s.op("dve", ins("scalar_tensor_tensor", out=o0[:, :nq], in0=o1[:, :nq], scalar=lam[:, 2:3], in1=o0[:, :nq], op0=ALU.mult, op1=ALU.add),
                             reads=[o0b, o1b, cbuf], writes=[o0b])
                        sq_, sq_b = sqd[qi % 4]
                        s.op("act", ins("activation", out=sq_[:, :nq], in_=o0[:, :nq], func=AF.Square), reads=[o0b], writes=[sq_b])
                        defer(2, st5)

                    def st5():
                        sq_, sq_b = sqd[qi % 4]
                        psn, pnb = s.ps()
                        s.mm(psn[:, :nq], ones_bf[:], sq_[:, :nq], True, True, reads=[sq_b, cb_const], writes=[pnb])
                        stt[(qi, 1)]["psn"] = (psn, pnb)
                        defer(2, st6)

                    def st6():
                        (o1, o1b) = stt[(qi, 1)]["o"]
                        psn, pnb = stt[(qi, 1)]["psn"]
                        s.op("act", ins("activation", out=o1[:, :nq], in_=psn[:, :nq], func=AF.Sqrt, scale=1.0 / 128, bias=EPSB[:, 0:1]), reads=[pnb, o1b], writes=[o1b])
                        defer(2, st7)

                    def st7():
                        (o1, o1b) = stt[(qi, 1)]["o"]
                        s.op("dve", ins("reciprocal", out=o1[:, :nq], in_=o1[:, :nq]), reads=[o1b], writes=[o1b])
                        defer(4, st8)

                    def st8():
                        (o0, o0b) = stt[(qi, 0)]["o"]
                        (o1, o1b) = stt[(qi, 1)]["o"]
                        s.op("dve", ins("scalar_tensor_tensor", out=ot[:, qs], in0=o0[:, :nq], scalar=lam[:, 3:4], in1=o1[:, :nq], op0=ALU.mult, op1=ALU.mult),
                             reads=[o0b, o1b, cbuf], writes=[otb])
                        del stt[(qi, 0)]
                        del stt[(qi, 1)]

                    defer(2, st1)

                pend = []
                cur_it = [0]

                def defer(dl, fn):
                    pend.append((cur_it[0] + dl, fn))

                def run_pending(flush=False):
                    while True:
                        ready = [p for p in pend if flush or p[0] <= cur_it[0]]
                        if not ready:
                            break
                        for p in ready:
                            pend.remove(p)
                        for due, fn in ready:
                            fn()
                        if not flush:
                            break

                for t2_ in range(0, len(tasks) + LOOK + 1, 2):
                    for t_ in (t2_, t2_ + 1):
                        cur_it[0] = t_
                        if t_ < len(tasks):
                            stage1(tasks[t_])
                    for t_ in (t2_, t2_ + 1):
                        cur_it[0] = t_
                        if LOOK <= t_ < len(tasks) + LOOK:
                            stage2(tasks[t_ - LOOK])
                    run_pending()
                while pend:
                    cur_it[0] += 1
                    run_pending(flush=True)
                s.dma(OS[:, u, q0:q0 + L], ot[:, 0:L], reads=[otb], writes=[OS_b[si][u]])
        return OS_b

    def outproj_phase(l, Wsrc, nk, Osrc, O_bufs_fn, Xsrc, Xsrc_b):
        s.barrier()
        s.release()
        Wo = s.alloc([128, nk, 1024], BF16, "wo")
        wob = load_w(Wo, Wsrc, nk)
        rctx = ResCtx()
        oin = [(s.alloc([128, nk, 512], BF16, "oin"), Buf()) for _ in range(2)]
        for ti in range(NT):
            t0, n, cond, _, _ = TILES[ti]
            o_, o_b = oin[ti % 2]
            s.dma(o_[:, :, :n], Osrc[:, :, t0:t0 + n], writes=[o_b])
            outproj_resid(rctx, l, 2, ti, Wo, wob, nk, o_, [o_b], Xsrc, Xsrc_b, X, X_b)

    def gla_layer(l, Xsrc, Xsrc_b):
        s.barrier()
        s.release()
        Wqk = s.alloc([128, 8, 1024], BF16, "gwqk")
        wqb = load_w(Wqk, w1_qk, 8)
        Wv = s.alloc([128, 8, 1024], BF16, "gwv")
        wvb = load_w(Wv, w1_v, 8)
        Wr = s.alloc([128, 8, 1024], BF16, "gwr")
        wrb = load_w(Wr, w1_r, 8)
        Wg1 = s.alloc([128, 8, 32], BF16, "gwg1")
        wg1b = load_w(Wg1, w1_g1, 8)
        Wg2 = s.alloc([32, 1024], BF16, "gwg2")
        cbuf = Buf()
        s.dma(Wg2[:], w1_g2, writes=[cbuf], eng="pool")
        gb = s.alloc([128, 8], F32, "ggb")
        s.dma(gb[:], w1_gb, writes=[cbuf])
        s.op("dve", ins("tensor_scalar", out=gb[:], in0=gb[:], scalar1=-1.0, scalar2=None, op0=ALU.mult), reads=[cbuf], writes=[cbuf])
        ELt = s.alloc([128, 2, 4, 72], F32, "el")
        nctx = NormCtx(1)
        qkf = [(s.alloc([128, 8, 512], F32, "qkf"), Buf()) for _ in range(1)]
        lg = nctx.xt[0]
        cs_ = (s.alloc([128, 8, 512], F32, "cs"), Buf())
        c2 = nctx.tmp
        ex = [(s.alloc([128, 512], F32, "ex"), Buf()) for _ in range(3)]
        t1b_ = (s.alloc([32, 512], BF16, "t1b"), Buf())
        qd_t = [(s.alloc([128, 2, 4, 512], BF16, "qd"), Buf()) for _ in range(1)]
        kd_t = [(s.alloc([128, 2, 4, 512], BF16, "kd"), Buf()) for _ in range(1)]
        krT = [(s.alloc([128, 512], BF16, "krT"), Buf()) for _ in range(2)]
        krt = [(s.alloc([128, 4, 8, 128], BF16, "krt"), Buf()) for _ in range(1)]
        rt = [(s.alloc([128, 8, 512], BF16, "rt"), Buf()) for _ in range(1)]
        vt = [(s.alloc([128, 4, 1024], BF16, "vt"), Buf()) for _ in range(1)]
        G_b = [Buf() for _ in range(NT)]
        kx = 0
        kk = 0
        for ti in range(NT):
            t0, n, cond, s0, s1 = TILES[ti]
            nch = n // 64
            h, hb = norm_tile(nctx, l, 0, ti, Xsrc, Xsrc_b)
            qf, qfb = qkf[0]
            for cc in range(8):
                ps, pb = s.ps()
                for kc in range(8):
                    s.mm(ps[:, :n], Wqk[:, kc, cc * 128:(cc + 1) * 128], h[:, kc, :n], kc == 0, kc == 7, reads=[wqb[kc], hb], writes=[pb])
                sc_ = (128.0 ** -0.5) if cc < 4 else 1.0
                s.op("act", ins("activation", out=qf[:, cc, :n], in_=ps[:, :n], func=AF.Identity, scale=sc_), reads=[pb], writes=[qfb])
            r_, r_b = rt[0]
            for cc in range(8):
                ps, pb = s.ps()
                for kc in range(8):
                    s.mm(ps[:, :n], Wr[:, kc, cc * 128:(cc + 1) * 128], h[:, kc, :n], kc == 0, kc == 7, reads=[wrb[kc], hb], writes=[pb])
                s.op("act", ins("activation", out=r_[:, cc, :n], in_=ps[:, :n], func=AF.Silu), reads=[pb], writes=[r_b])
            s.dma(RS[:, :, t0:t0 + n], r_[:, :, :n], reads=[r_b], writes=[G_b[ti]])
            v_, v_b = vt[0]
            for tbk in range(n // 128):
                for vg in range(2):
                    ps, pb = s.ps()
                    for kc in range(8):
                        s.mm(ps[:, :512], h[:, kc, tbk * 128:(tbk + 1) * 128], Wv[:, kc, vg * 512:(vg + 1) * 512], kc == 0, kc == 7, reads=[wvb[kc], hb], writes=[pb])
                    s.op("act", ins("activation", out=v_[:, tbk, vg * 512:(vg + 1) * 512], in_=ps[:, :512], func=AF.Identity), reads=[pb], writes=[v_b])
            s.dma(rows_view(VS, 1024, t0, n // 128, 0, 1024), v_[:, 0:n // 128, :], reads=[v_b], writes=[G_b[ti]])
            ps, pb = s.ps()
            for kc in range(8):
                s.mm(ps[0:32, :n], Wg1[:, kc, :], h[:, kc, :n], kc == 0, kc == 7, reads=[wg1b[kc], hb], writes=[pb])
            t1_, t1bb = t1b_
            s.op("act", ins("activation", out=t1_[:, :n], in_=ps[0:32, :n], func=AF.Identity), reads=[pb], writes=[t1bb])
            lgt, lgb = lg
            for j in range(8):
                ps, pb = s.ps()
                s.mm(ps[:, :n], Wg2[:, j * 128:(j + 1) * 128], t1_[:, :n], True, True, reads=[cbuf, t1bb], writes=[pb])
                s.op("act", ins("activation", out=lgt[:, j, :n], in_=ps[:, :n], func=AF.Exp, scale=-1.0, bias=gb[:, j:j + 1]), reads=[pb, cbuf], writes=[lgb])
            s.op("act", ins("activation", out=lgt[:, :, :n], in_=lgt[:, :, :n], func=AF.Ln, bias=1.0, scale=1.0), reads=[lgb], writes=[lgb])
            cst, csb = cs_
            for j in range(8):
                s.op("dve", ins("tensor_tensor_scan", out=cst[:, j, :n], data0=m64[:, :n], data1=lgt[:, j, :n], initial=0.0, op0=ALU.mult, op1=ALU.add),
                     reads=[lgb, cb_const], writes=[csb])
            c2t, c2b = c2
            csv = cst[:, :, :n].rearrange("p j (c t) -> p j c t", t=64)
            lgv = lgt[:, :, :n].rearrange("p j (c t) -> p j c t", t=64)
            c2v = c2t[:, :, :n].rearrange("p j (c t) -> p j c t", t=64)
            nf = 4
            s.op("dve", ins("tensor_tensor", out=c2v[:, 0:nf, :, :], in0=csv[:, 0:nf, :, :], in1=csv[:, 0:nf, :, 63:64].to_broadcast([128, nf, nch, 64]), op=ALU.subtract),
                 reads=[csb], writes=[c2b])
            s.op("dve", ins("tensor_tensor", out=c2v[:, nf:2 * nf, :, :], in0=lgv[:, nf:2 * nf, :, :], in1=csv[:, nf:2 * nf, :, :], op=ALU.subtract),
                 reads=[csb, lgb], writes=[c2b])
            ci0 = t0 // 64
            s.op("act", ins("activation", out=ELt[:, :, :, ci0:ci0 + nch].rearrange("p d h c -> p (d h) c"), in_=cst[:, :, :n].rearrange("p j (c t) -> p j c t", t=64)[:, :, :, 63],
                                               func=AF.Exp, scale=-1.0 / 16), reads=[csb], writes=[cbuf])
            s.op("dve", ins("tensor_tensor", out=lgv[:, nf:2 * nf, :, :], in0=c2v[:, nf:2 * nf, :, :], in1=csv[:, nf:2 * nf, :, 63:64].to_broadcast([128, nf, nch, 64]), op=ALU.add),
                 reads=[csb, c2b, lgb], writes=[lgb])
            qd_, qdb = qd_t[0]
            kd_, kdb = kd_t[0]
            krt_, krtb = krt[0]
            for j in range(8):
                d = j // 4
                hh = j % 4
                ea, eab = ex[0]
                eb, ebb = ex[1]
                ec, ecb = ex[2]
                csrc = cst if j < 4 else lgt
                s.op("act", ins("activation", out=ea[:, :n], in_=csrc[:, j, :n], func=AF.Exp, scale=-1.0 / 16), reads=[csb, lgb], writes=[eab])
                s.op("act", ins("activation", out=eb[:, :n], in_=csrc[:, j, :n], func=AF.Exp, scale=1.0 / 16), reads=[csb, lgb], writes=[ebb])
                s.op("act", ins("activation", out=ec[:, :n], in_=c2t[:, j, :n], func=AF.Exp, scale=1.0 / 16), reads=[c2b], writes=[ecb])
                s.op("dve", ins("tensor_tensor", out=qd_[:, d, hh, :n], in0=qf[:, hh, :n], in1=ea[:, :n], op=ALU.mult), reads=[qfb, eab], writes=[qdb])
                s.op("dve", ins("tensor_tensor", out=kd_[:, d, hh, :n], in0=qf[:, 4 + hh, :n], in1=eb[:, :n], op=ALU.mult), reads=[qfb, ebb], writes=[kdb])
                kT, kTb = krT[kx % 2]
                kx += 1
                s.op("pool", ins("tensor_tensor", out=kT[:, :n], in0=qf[:, 4 + hh, :n], in1=ec[:, :n], op=ALU.mult), reads=[qfb, ecb], writes=[kTb])
                for tbk in range(n // 128):
                    ps, pb = s.ps()
                    psv = ps[:].bitcast(BF16)
                    s.op("pe", ins("transpose", psv[:, 0:128], kT[:, tbk * 128:(tbk + 1) * 128], ident_bf[:]), reads=[kTb, cb_const], writes=[pb])
                    s.op("act", ins("activation", out=krt_[:, tbk, j, :], in_=psv[:, 0:128], func=AF.Identity), reads=[pb], writes=[krtb])
            for d in range(2):
                s.dma(QD[d, :, :, t0:t0 + n], qd_[:, d, :, :n], reads=[qdb], writes=[G_b[ti]])
                s.dma(KD[d, :, :, t0:t0 + n], kd_[:, d, :, :n], reads=[kdb], writes=[G_b[ti]])
            s.dma(kr_view(t0, n // 128, 0, 8), krt_[:, 0:n // 128, :, :], reads=[krtb], writes=[G_b[ti]])
        ELd = dscr("ELd", [128, 2, 4, 72], F32)
        elb = Buf()
        s.dma(ELd, ELt[:], reads=[cbuf], writes=[elb])
        s.barrier()
        s.release()
        EL = s.alloc([128, 2, 4, 72], F32, "el2")
        elb2 = Buf()
        s.dma(EL[:], ELd, writes=[elb2])
        S = [(s.alloc([128, 4, 256], F32, "S"), [Buf() for _ in range(4)]) for _ in range(2)]
        Sb = [(s.alloc([128, 4, 256], BF16, "Sb"), [Buf() for _ in range(4)]) for _ in range(2)]
        qd_t = [[(s.alloc([128, 4, 512], BF16, "qd"), Buf()) for _ in range(2)] for _ in range(2)]
        kd_t = [[(s.alloc([128, 4, 512], BF16, "kd"), Buf()) for _ in range(2)] for _ in range(2)]
        kr_t = [[(s.alloc([128, 4, 4, 128], BF16, "kr"), Buf()) for _ in range(2)] for _ in range(2)]
        v_t = [[(s.alloc([128, 4, 1024], BF16, "v"), Buf()) for _ in range(2)] for _ in range(2)]
        of_t = [[(s.alloc([128, 8, 512], BF16, "of"), Buf()) for _ in range(2)] for _ in range(2)]
        attm = [(s.alloc([128, 64], BF16, "attm"), Buf()) for _ in range(8)]
        OD_b = [[Buf() for _ in range(NT)] for _ in range(2)]
        ODs = [OS, OS2]
        kat = 0
        SEQT = [list(range(8)), [8], [9]]
        states_in = [s1f, s1b]
        states_out = [o_gf, o_gb]
        for si, tiles in enumerate(SEQT):
            for d in range(2):
                St, Stb = S[d]
                Sbt, Sbb = Sb[d]
                if si == 0:
                    s.dma(St[:], states_in[d], writes=Stb)
                else:
                    s.op("pool", ins("memset", St[:], 0.0), writes=Stb)
                for hh in range(4):
                    s.op("act", ins("activation", out=Sbt[:, hh, :], in_=St[:, hh, :], func=AF.Identity), reads=[Stb[hh]], writes=[Sbb[hh]])
            nt_ = len(tiles)
            for step in range(nt_):
                cur = {}
                for d in range(2):
                    ti = tiles[step] if d == 0 else tiles[nt_ - 1 - step]
                    t0, n, cond, s0, s1 = TILES[ti]
                    qd_, qdb = qd_t[d][step % 2]
                    kd_, kdb = kd_t[d][step % 2]
                    kr_, krb = kr_t[d][step % 2]
                    v_, vb_ = v_t[d][step % 2]
                    of_, ofb = of_t[d][step % 2]
                    s.dma(qd_[:, :, :n], QD[d, :, :, t0:t0 + n], writes=[qdb])
                    s.dma(kd_[:, :, :n], KD[d, :, :, t0:t0 + n], writes=[kdb])
                    s.dma(kr_[:, 0:n // 128, :, :], kr_view(t0, n // 128, d * 4, 4), writes=[krb])
                    s.dma(v_[:, 0:n // 128, :], rows_view(VS, 1024, t0, n // 128, 0, 1024), writes=[vb_])
                    cur[d] = (ti, t0, n, qd_, qdb, kd_, kdb, kr_, krb, v_, vb_, of_, ofb)
                nch = cur[0][2] // 64
                for kc_ in range(nch):
                    for d in range(2):
                        ti, t0, n, qd_, qdb, kd_, kdb, kr_, krb, v_, vb_, of_, ofb = cur[d]
                        k = kc_ if d == 0 else nch - 1 - kc_
                        ci = t0 // 64 + k
                        tbk = k // 2
                        hp = slice((k % 2) * 64, (k % 2) * 64 + 64)
                        cs64 = slice(k * 64, (k + 1) * 64)
                        St, Stb = S[d]
                        Sbt, Sbb = Sb[d]
                        for hh in range(4):
                            ps, pb = s.ps()
                            s.mm(ps[0:64, 0:64], kd_[:, hh, cs64], qd_[:, hh, cs64], True, True, reads=[kdb, qdb], writes=[pb])
                            am, amb = attm[kat % 8]
                            kat += 1
                            s.op("dve", ins("tensor_tensor", out=am[hp, :], in0=ps[0:64, 0:64], in1=tri_bf[0:64, 2 + d, 0:64], op=ALU.mult),
                                 reads=[pb, cb_const], writes=[amb])
                            pso, pob = s.ps("b")
                            for vc in range(2):
                                s.mm(pso[:, vc * 64:(vc + 1) * 64], v_[hp, tbk, hh * 256 + vc * 128:hh * 256 + (vc + 1) * 128], am[hp, :], True, False, reads=[vb_, amb], writes=[pob])
                                s.mm(pso[:, vc * 64:(vc + 1) * 64], Sbt[:, hh, vc * 128:(vc + 1) * 128], qd_[:, hh, cs64], False, True, reads=[Sbb[hh], qdb], writes=[pob])
                            s.op("act", ins("activation", out=of_[:, hh * 2:hh * 2 + 2, cs64], in_=pso[:, 0:128].rearrange("p (v t) -> p v t", t=64), func=AF.Identity),
                                 reads=[pob], writes=[ofb])
                            ps2, pb2 = s.ps()
                            s.mm(ps2[:, 0:256], kr_[hp, tbk, hh, :], v_[hp, tbk, hh * 256:(hh + 1) * 256], True, True, reads=[krb, vb_], writes=[pb2])
                            s.op("dve", ins("scalar_tensor_tensor", out=St[:, hh, :], in0=St[:, hh, :], scalar=EL[:, d, hh, ci:ci + 1], in1=ps2[:, 0:256], op0=ALU.mult, op1=ALU.add),
                                 reads=[pb2, elb2, Stb[hh]], writes=[Stb[hh]])
                            s.op("pool", ins("tensor_copy", out=Sbt[:, hh, :], in_=St[:, hh, :]), reads=[Stb[hh]], writes=[Sbb[hh]])
                for d in range(2):
                    ti, t0, n, qd_, qdb, kd_, kdb, kr_, krb, v_, vb_, of_, ofb = cur[d]
                    s.dma(ODs[d][:, :, t0:t0 + n], of_[:, :, :n], reads=[ofb], writes=[OD_b[d][ti]])
            if si > 0:
                for d in range(2):
                    s.dma(states_out[d][si - 1], S[d][0][:], reads=S[d][1])
        s.barrier()
        s.release()
        Wo = s.alloc([128, 8, 1024], BF16, "wo")
        wob = load_w(Wo, w1_o, 8)
        gn = s.alloc([128, 2], F32, "gn")
        gnb = Buf()
        s.dma(gn[:], w1_gn, writes=[gnb])
        rctx = ResCtx()
        oa = [(s.alloc([128, 8, 512], BF16, "oa"), Buf()) for _ in range(2)]
        ob_ = [(s.alloc([128, 8, 512], BF16, "ob"), Buf()) for _ in range(2)]
        rr = [(s.alloc([128, 8, 512], BF16, "rr"), Buf()) for _ in range(2)]
        osum = (s.alloc([128, 8, 512], F32, "osum"), Buf())
        sq = (s.alloc([128, 8, 512], BF16, "sq"), Buf())
        rs_ = [(s.alloc([128, 512], F32, "rs"), Buf()) for _ in range(2)]
        ofin = [(s.alloc([128, 8, 512], BF16, "ofin"), Buf()) for _ in range(2)]
        for ti in range(NT):
            t0, n, cond, _, _ = TILES[ti]
            a_, a_b = oa[ti % 2]
            b_, b_b = ob_[ti % 2]
            r_, r_b = rr[ti % 2]
            s.dma(a_[:, :, :n], OS[:, :, t0:t0 + n], writes=[a_b])
            s.dma(b_[:, :, :n], OS2[:, :, t0:t0 + n], writes=[b_b])
            s.dma(r_[:, :, :n], RS[:, :, t0:t0 + n], writes=[r_b])
            os_, osb_ = osum
            sq_, sqb_ = sq
            s.op("dve", ins("tensor_tensor", out=os_[:, :, :n], in0=a_[:, :, :n], in1=b_[:, :, :n], op=ALU.add), reads=[a_b, b_b], writes=[osb_])
            s.op("act", ins("activation", out=sq_[:, :, :n], in_=os_[:, :, :n], func=AF.Square), reads=[osb_], writes=[sqb_])
            f_, f_b = ofin[ti % 2]
            for hh in range(4):
                ps, pb = s.ps()
                for vc in range(2):
                    s.mm(ps[:, :n], ones_bf[:], sq_[:, hh * 2 + vc, :n], vc == 0, vc == 1, reads=[sqb_, cb_const], writes=[pb])
                rt_, rtb = rs_[hh % 2]
                s.op("act", ins("activation", out=rt_[:, :n], in_=ps[:, :n], func=AF.Sqrt, scale=1.0 / 256, bias=EPSB[:, 0:1]), reads=[pb], writes=[rtb])
                s.op("dve", ins("reciprocal", out=rt_[:, :n], in_=rt_[:, :n]), reads=[rtb], writes=[rtb])
                for vc in range(2):
                    c = hh * 2 + vc
                    s.op("dve", ins("tensor_tensor", out=os_[:, c, :n], in0=os_[:, c, :n], in1=rt_[:, :n], op=ALU.mult), reads=[osb_, rtb], writes=[osb_])
                    s.op("dve", ins("scalar_tensor_tensor", out=f_[:, c, :n], in0=os_[:, c, :n], scalar=gn[:, vc:vc + 1], in1=r_[:, c, :n], op0=ALU.mult, op1=ALU.mult),
                         reads=[osb_, gnb, r_b], writes=[f_b])
            outproj_resid(rctx, l, 2, ti, Wo, wob, 8, f_, [f_b], Xsrc, Xsrc_b, X, X_b)

    def ssd_layer(l, Xsrc, Xsrc_b):
        XBu = U
        s.barrier()
        s.release()
        Wz = s.alloc([128, 8, 2048], BF16, "wz")
        wzb = load_w(Wz, w3_z, 8)
        Wx = s.alloc([128, 8, 3072], BF16, "wx")
        wxb = load_w(Wx, w3_xbc, 8)
        Wdt = s.alloc([128, 8, 64], BF16, "wdt")
        wdb = load_w(Wdt, w3_dt, 8)
        vec = s.alloc([128, 5, 32], F32, "vec")
        cbuf = Buf()
        s.dma(vec[:], ssd_vec.rearrange("a b -> (a b)").partition_broadcast(128).rearrange("p (a b) -> p a b", a=5), writes=[cbuf])
        avec = s.alloc([128, 64], F32, "avec")
        s.op("act", ins("activation", out=avec[:], in_=vec[:, 0:2, :].rearrange("p a b -> p (a b)"), func=AF.Exp), reads=[cbuf], writes=[cbuf])
        s.op("dve", ins("tensor_scalar", out=avec[:], in0=avec[:], scalar1=-1.0, scalar2=None, op0=ALU.mult), reads=[cbuf], writes=[cbuf])
        nctx = NormCtx(1)
        xbt = (s.alloc([128, 24, 512], BF16, "xbt"), [Buf() for _ in range(24)])
        zt = (s.alloc([128, 4, 2048], BF16, "zt"), Buf())
        dtt = (s.alloc([128, 4, 2, 64], F32, "dtt"), Buf())
        dtmp = (s.alloc([128, 64], F32, "dtmp"), Buf())
        S_b = [Buf() for _ in range(NT)]
        ev = 0
        for ti in range(NT):
            t0, n, cond, s0, s1 = TILES[ti]
            nb = n // 128
            h, hb = norm_tile(nctx, l, 0, ti, Xsrc, Xsrc_b)
            xb_, xbb = xbt
            for cc in range(24):
                ps, pb = s.ps()
                for kc in range(8):
                    s.mm(ps[:, :n], Wx[:, kc, cc * 128:(cc + 1) * 128], h[:, kc, :n], kc == 0, kc == 7, reads=[wxb[kc], hb], writes=[pb])
                if ev % 2 == 0:
                    s.op("act", ins("activation", out=xb_[:, cc, :n], in_=ps[:, :n], func=AF.Identity), reads=[pb], writes=[xbb[cc]])
                else:
                    s.op("dve", ins("tensor_copy", out=xb_[:, cc, :n], in_=ps[:, :n]), reads=[pb], writes=[xbb[cc]])
                ev += 1
            s.dma(XBu[:, 0:24, t0:t0 + n], xb_[:, :, :n], reads=xbb, writes=[S_b[ti]])
            z_, z_b = zt
            d_, d_b = dtt
            for tbk in range(nb):
                for vg in range(4):
                    ps, pb = s.ps()
                    for kc in range(8):
                        s.mm(ps[:, :512], h[:, kc, tbk * 128:(tbk + 1) * 128], Wz[:, kc, vg * 512:(vg + 1) * 512], kc == 0, kc == 7, reads=[wzb[kc], hb], writes=[pb])
                    s.op("act", ins("activation", out=z_[:, tbk, vg * 512:(vg + 1) * 512], in_=ps[:, :512], func=AF.Silu), reads=[pb], writes=[z_b])
                ps, pb = s.ps()
                for kc in range(8):
                    s.mm(ps[:, 0:64], h[:, kc, tbk * 128:(tbk + 1) * 128], Wdt[:, kc, :], kc == 0, kc == 7, reads=[wdb[kc], hb], writes=[pb])
                tm, tmb = dtmp
                s.op("dve", ins("tensor_tensor", out=tm[:], in0=ps[:, 0:64], in1=vec[:, 2:4, :].rearrange("p a b -> p (a b)"), op=ALU.add), reads=[pb, cbuf], writes=[tmb])
                s.op("act", ins("activation", out=tm[:], in_=tm[:], func=AF.Exp), reads=[tmb], writes=[tmb])
                s.op("act", ins("activation", out=d_[:, tbk, 0, :], in_=tm[:], func=AF.Ln, bias=1.0, scale=1.0), reads=[tmb], writes=[d_b])
                s.op("dve", ins("tensor_tensor", out=d_[:, tbk, 1, :], in0=d_[:, tbk, 0, :], in1=avec[:], op=ALU.mult), reads=[d_b, cbuf], writes=[d_b])
            s.dma(rows_view(ZS, 2048, t0, nb, 0, 2048), z_[:, 0:nb, :], reads=[z_b], writes=[S_b[ti]])
            s.dma(AP(DTS.tensor, t0 * 128, [[128, 128], [128 * 128, nb], [1, 128]]), d_[:, 0:nb, :, :].rearrange("p b a c -> p b (a c)"), reads=[d_b], writes=[S_b[ti]])
        chk(20)
        s.barrier()
        s.release()
        cw = s.alloc([128, 3, 24], F32, "scw")
        cbias = s.alloc([128, 24], F32, "scb")
        cwb = Buf()
        s.dma(cw[:], ssd_cw, writes=[cwb])
        s.dma(cbias[:], ssd_cb, writes=[cwb])
        ug = s.alloc([128, 24, 514], BF16, "sug")
        ugb = [Buf() for _ in range(2)]
        xc = (s.alloc([128, 24, 512], BF16, "xc"), [Buf() for _ in range(24)])
        tmps = [(s.alloc([128, 512], F32, "st"), Buf()) for _ in range(2)]
        xtt = (s.alloc([128, 4, 2048], BF16, "xtt"), Buf())
        btt = (s.alloc([128, 4, 512], BF16, "btt"), Buf())
        kp = 0
        for ti in range(NT):
            t0, n, cond, s0, s1 = TILES[ti]
            nb = n // 128
            lo = (t0 - 1) >= s0
            hi = (t0 + n) < s1
            for grp in range(2):
                gs_ = slice(grp * 12, (grp + 1) * 12)
                if not lo:
                    s.op("pool", ins("memset", ug[:, gs_, 0:1], 0.0), writes=[ugb[grp]])
                if not hi:
                    s.op("pool", ins("memset", ug[:, gs_, n + 1:n + 2], 0.0), writes=[ugb[grp]])
                c0 = 0 if lo else 1
                c1 = n + 2 if hi else n + 1
                s.dma(ug[:, gs_, c0:c1], XBu[:, gs_, t0 - 1 + c0:t0 - 1 + c1], writes=[ugb[grp]])
            xc_, xcb = xc
            for col in range(24):
                grp = col // 12
                tt, ttb = tmps[kp % 2]
                kp += 1
                s.op("act", ins("activation", out=tt[:, :n], in_=ug[:, col, 1:n + 1], func=AF.Identity, scale=cw[:, 1, col:col + 1], bias=cbias[:, col:col + 1]),
                     reads=[ugb[grp], cwb], writes=[ttb])
                s.op("dve", ins("scalar_tensor_tensor", out=tt[:, :n], in0=ug[:, col, 0:n], scalar=cw[:, 0, col:col + 1], in1=tt[:, :n], op0=ALU.mult, op1=ALU.add),
                     reads=[ugb[grp], cwb, ttb], writes=[ttb])
                s.op("dve", ins("scalar_tensor_tensor", out=tt[:, :n], in0=ug[:, col, 2:n + 2], scalar=cw[:, 2, col:col + 1], in1=tt[:, :n], op0=ALU.mult, op1=ALU.add),
                     reads=[ugb[grp], cwb, ttb], writes=[ttb])
                s.op("act", ins("activation", out=xc_[:, col, :n], in_=tt[:, :n], func=AF.Silu), reads=[ttb], writes=[xcb[col]])
            s.dma(QK[:, 0:8, t0:t0 + n], xc_[:, 16:24, :n], reads=xcb[16:24], writes=[S_b[ti]])
            x_, x_b = xtt
            b_, b_b = btt
            tv = 0
            for tbk in range(nb):
                for col in range(20):
                    ps, pb = s.ps()
                    psv = ps[:].bitcast(BF16)
                    s.op("pe", ins("transpose", psv[:, 0:128], xc_[:, col, tbk * 128:(tbk + 1) * 128], ident_bf[:]), reads=[xcb[col], cb_const], writes=[pb])
                    dst = x_[:, tbk, col * 128:(col + 1) * 128] if col < 16 else b_[:, tbk, (col - 16) * 128:(col - 15) * 128]
                    dbuf = x_b if col < 16 else b_b
                    if tv % 2 == 0:
                        s.op("act", ins("activation", out=dst, in_=psv[:, 0:128], func=AF.Identity), reads=[pb], writes=[dbuf])
                    else:
                        s.op("dve", ins("tensor_copy", out=dst, in_=psv[:, 0:128]), reads=[pb], writes=[dbuf])
                    tv += 1
            s.dma(rows_view(XTs, 2048, t0, nb, 0, 2048), x_[:, 0:nb, :], reads=[x_b], writes=[S_b[ti]])
            s.dma(rows_view(BTs, 512, t0, nb, 0, 512), b_[:, 0:nb, :], reads=[b_b], writes=[S_b[ti]])
        chk(21)
        s.barrier()
        s.release()
        vec = s.alloc([128, 5, 32], F32, "vec")
        negm = s.alloc([128, 2, 128], F32, "negm")
        cbuf = Buf()
        s.op("dve", ins("tensor_scalar", out=negm[:], in0=stage[:, 0:2, :], scalar1=-1.0, scalar2=30000.0, op0=ALU.add, op1=ALU.mult), reads=[bstage], writes=[cbuf])
        ST = [(s.alloc([128, 2048], F32, "ST"), [Buf() for _ in range(4)]) for _ in range(2)]
        SbT = [(s.alloc([128, 2048], BF16, "SbT"), [Buf() for _ in range(4)]) for _ in range(2)]
        bct = [(s.alloc([128, 8, 512], BF16, "bct"), Buf()) for _ in range(2)]
        xts = [(s.alloc([128, 4, 2048], BF16, "xts"), Buf()) for _ in range(2)]
        bts = [(s.alloc([128, 4, 512], BF16, "bts"), Buf()) for _ in range(2)]
        dts = [(s.alloc([128, 4, 2, 64], F32, "dts"), Buf()) for _ in range(2)]
        yts = [(s.alloc([128, 4, 2048], BF16, "yts"), Buf()) for _ in range(2)]
        cumT = [(s.alloc([128, 32], F32, "cumT"), Buf()) for _ in range(2)]
        cbT = [(s.alloc([128, 128], F32, "cbT"), Buf()) for _ in range(6)]
        lat = [(s.alloc([128, 4, 128], F32, "lat"), Buf()) for _ in range(3)]
        cB = [(s.alloc([128, 4, 128], F32, "cB"), Buf()) for _ in range(4)]
        seg = [(s.alloc([128, 4, 128], F32, "seg"), Buf()) for _ in range(3)]
        Et = [(s.alloc([128, 4, 128], F32, "Et"), Buf()) for _ in range(6)]
        ecB = [(s.alloc([128, 4, 128], F32, "ecB"), Buf()) for _ in range(8)]
        te = [(s.alloc([128, 4], F32, "te"), Buf()) for _ in range(3)]
        xs = [(s.alloc([128, 512], BF16, "xs"), Buf()) for _ in range(4)]
        Wt = [(s.alloc([128, 128], BF16, "Wt"), Buf()) for _ in range(8)]
        CEt = [(s.alloc([128, 128], BF16, "CEt"), Buf()) for _ in range(8)]
        SEQT = [list(range(8)), [8], [9]]
        st_in = [s3f, s3b]
        st_out = [o_sf, o_sb]
        Y_b = [Buf() for _ in range(NT)]
        kw = 0
        kq4 = 0
        kg = 0
        kct = 0
        for si, tiles in enumerate(SEQT):
            for d in range(2):
                St, Stb = ST[d]
                Sbt, Sbb = SbT[d]
                if si == 0:
                    s.dma(St[:], st_in[d], writes=Stb)
                else:
                    s.op("pool", ins("memset", St[:], 0.0), writes=Stb)
                for g in range(4):
                    s.op("act", ins("activation", out=Sbt[:, g * 512:(g + 1) * 512], in_=St[:, g * 512:(g + 1) * 512], func=AF.Identity), reads=[Stb[g]], writes=[Sbb[g]])
            nt_ = len(tiles)
            for step in range(nt_):
                cur = {}
                for d in range(2):
                    ti = tiles[step] if d == 0 else tiles[nt_ - 1 - step]
                    t0, n, cond, s0, s1 = TILES[ti]
                    nb = n // 128
                    bc_, bcb = bct[d]
                    x_, x_b = xts[d]
                    b_, b_b = bts[d]
                    d_, d_b = dts[d]
                    y_, y_b = yts[d]
                    s.dma(bc_[:, :, :n], QK[:, 0:8, t0:t0 + n], writes=[bcb])
                    s.dma(x_[:, 0:nb, :], rows_view(XTs, 2048, t0, nb, 0, 2048), writes=[x_b])
                    s.dma(b_[:, 0:nb, :], rows_view(BTs, 512, t0, nb, 0, 512), writes=[b_b])
                    s.dma(d_[:, 0:nb, :, :].rearrange("p b a c -> p b (a c)"), AP(DTS.tensor, t0 * 128, [[128, 128], [128 * 128, nb], [1, 128]]), writes=[d_b])
                    cur[d] = (ti, t0, n, nb)
                nbs = cur[0][3]
                batches = []
                for kc_ in range(nbs):
                    for d in range(2):
                        for g in range(4):
                            for hb_ in range(2):
                                batches.append((kc_, d, g, hb_))
                bst = {}
                LOOKB = 3

                def geo(bt_):
                    kc_, d, g, hb_ = bt_
                    ti, t0, n, nb = cur[d]
                    tbk = kc_ if d == 0 else nb - 1 - kc_
                    last = 127 if d == 0 else 0
                    d_, d_b = dts[d]
                    la = d_[:, tbk, 1, d * 32:(d + 1) * 32]
                    dtv = d_[:, tbk, 0, d * 32:(d + 1) * 32]
                    return tbk, last, slice(tbk * 128, (tbk + 1) * 128), la, dtv, d_b

                def stageA1(bt_):
                    nonlocal kq4, kg, kct
                    kc_, d, g, hb_ = bt_
                    tbk, last, tk, la, dtv, d_b = geo(bt_)
                    bc_, bcb = bct[d]
                    if g == 0 and hb_ == 0:
                        psc, pcb = s.ps()
                        s.mm(psc[:, 0:32], stage[:, d, :], la, True, True, reads=[bstage, d_b], writes=[pcb])
                        cT, cTb = cumT[kct % 2]
                        kct += 1
                        s.op("dve", ins("tensor_copy", out=cT[:], in_=psc[:, 0:32]), reads=[pcb], writes=[cTb])
                        bst[("cT", kc_, d)] = (cT, cTb)
                    if hb_ == 0:
                        ps, pb = s.ps()
                        s.mm(ps[:, 0:128], bc_[:, g, tk], bc_[:, 4 + g, tk], True, True, reads=[bcb], writes=[pb])
                        cb_, cb_b = cbT[kg % 6]
                        s.op("act", ins("activation", out=cb_[:], in_=ps[:, 0:128], func=AF.Identity), reads=[pb], writes=[cb_b])
                        bst[("grp", kc_, d, g)] = dict(cb=(cb_, cb_b), x4=xs[kg % 4], ecs=[])
                        kg += 1
                    h0 = g * 8 + hb_ * 4
                    lt, ltb = lat[kq4 % 3]
                    cb4, cb4b = cB[kq4 % 4]
                    sg, sgb = seg[kq4 % 3]
                    E_, E_b = Et[kq4 % 6]
                    ec, ecb = ecB[kq4 % 8]
                    te_, te_b = te[kq4 % 3]
                    kq4 += 1
                    bst[("w",) + bt_] = (cb4, cb4b, sg, sgb, E_, E_b, ec, ecb, te_, te_b)
                    s.op("dve", ins("tensor_tensor", out=lt[:], in0=stage[:, d, :].unsqueeze(1).to_broadcast([128, 4, 128]),
                                    in1=la[:, h0:h0 + 4].unsqueeze(2).to_broadcast([128, 4, 128]), op=ALU.mult),
                         reads=[bstage, d_b], writes=[ltb])
                    ps2, pb2 = s.ps()
                    s.mm(ps2[:, 0:512], ones_f[:], lt[:].rearrange("p a b -> p (a b)"), True, True, reads=[ltb, cb_const], writes=[pb2])
                    s.op("act", ins("activation", out=cb4[:].rearrange("p a b -> p (a b)"), in_=ps2[:, 0:512], func=AF.Identity), reads=[pb2], writes=[cb4b])

                def stageA2a(bt_):
                    kc_, d, g, hb_ = bt_
                    cT, cTb = bst[("cT", kc_, d)]
                    G = bst[("grp", kc_, d, g)]
                    cb4, cb4b, sg, sgb, E_, E_b, ec, ecb, te_, te_b = bst[("w",) + bt_]
                    h0 = g * 8 + hb_ * 4
                    G["ecs"].append((ec, ecb))
                    for hh in range(4):
                        hq = h0 + hh
                        s.op("dve", ins("scalar_tensor_tensor", out=sg[:, hh, :], in0=cb4[:, hh, :], scalar=cT[:, hq:hq + 1], in1=negm[:, d, :],
                                        op0=ALU.subtract, op1=ALU.add),
                             reads=[cb4b, cTb, cbuf], writes=[sgb])
                    s.op("act", ins("activation", out=E_[:], in_=sg[:], func=AF.Exp), reads=[sgb], writes=[E_b])
                    s.op("act", ins("activation", out=ec[:], in_=cb4[:], func=AF.Exp), reads=[cb4b], writes=[ecb])

                def stageA2b(bt_):
                    kc_, d, g, hb_ = bt_
                    tbk, last, tk, la, dtv, d_b = geo(bt_)
                    x_, x_b = xts[d]
                    G = bst[("grp", kc_, d, g)]
                    x4, x4b = G["x4"]
                    cb4, cb4b, sg, sgb, E_, E_b, ec, ecb, te_, te_b = bst.pop(("w",) + bt_)
                    h0 = g * 8 + hb_ * 4
                    s.op("dve", ins("tensor_tensor", out=te_[:], in0=E_[:, :, last], in1=dtv[:, h0:h0 + 4], op=ALU.mult),
                         reads=[E_b, d_b], writes=[te_b])
                    s.op("dve", ins("tensor_tensor",
                                    out=x4[:, hb_ * 256:(hb_ + 1) * 256].rearrange("p (a b) -> p a b", b=64),
                                    in0=x_[:, tbk, h0 * 64:(h0 + 4) * 64].rearrange("p (a b) -> p a b", b=64),
                                    in1=te_[:].unsqueeze(2).to_broadcast([128, 4, 64]), op=ALU.mult),
                         reads=[x_b, te_b], writes=[x4b])
                    bst[("bat",) + bt_] = (E_, E_b, ec, ecb)

                def stageB(bt_):
                    nonlocal kw
                    kc_, d, g, hb_ = bt_
                    tbk, last, tk, la, dtv, d_b = geo(bt_)
                    bc_, bcb = bct[d]
                    x_, x_b = xts[d]
                    b_, b_b = bts[d]
                    y_, y_b = yts[d]
                    St, Stb = ST[d]
                    Sbt, Sbb = SbT[d]
                    G = bst[("grp", kc_, d, g)]
                    cb_, cb_b = G["cb"]
                    x4, x4b = G["x4"]
                    if hb_ == 0:
                        G["psy"] = s.ps("b")
                    psy, pyb = G["psy"]
                    E_, E_b, ec, ecb = bst.pop(("bat",) + bt_)
                    h0 = g * 8 + hb_ * 4
                    for hh in range(4):
                        hq = h0 + hh
                        W_, W_b = Wt[kw % 8]
                        C_, C_b = CEt[kw % 8]
                        kw += 1
                        s.op("dve", ins("scalar_tensor_tensor", out=W_[:], in0=E_[:, hh, :], scalar=dtv[:, hq:hq + 1], in1=cb_[:], op0=ALU.mult, op1=ALU.mult),
                             reads=[E_b, d_b, cb_b], writes=[W_b])
                        s.op("pool", ins("tensor_tensor", out=C_[:], in0=bc_[:, 4 + g, tk], in1=ec[:, hh, :], op=ALU.mult),
                             reads=[bcb, ecb], writes=[C_b])
                        ycol = slice((hq % 8) * 64, (hq % 8 + 1) * 64)
                        s.mm(psy[:, ycol], W_[:], x_[:, tbk, hq * 64:(hq + 1) * 64], True, False, reads=[W_b, x_b], writes=[pyb])
                        s.mm(psy[:, ycol], C_[:], Sbt[:, hq * 64:(hq + 1) * 64], False, True, reads=[C_b, Sbb[g]], writes=[pyb])
                    if hb_ == 0:
                        return
                    s.op("act", ins("activation", out=y_[:, tbk, g * 512:(g + 1) * 512], in_=psy[:, 0:512], func=AF.Identity), reads=[pyb], writes=[y_b])
                    pss, psb_ = s.ps()
                    s.mm(pss[:, 0:512], b_[:, tbk, g * 128:(g + 1) * 128], x4[:], True, True, reads=[b_b, x4b], writes=[psb_])
                    ecs = G["ecs"]
                    for hh in range(8):
                        hq = g * 8 + hh
                        ec2, ecb2 = ecs[hh // 4]
                        s.op("dve", ins("scalar_tensor_tensor",
                                        out=St[:, hq * 64:(hq + 1) * 64], in0=St[:, hq * 64:(hq + 1) * 64], scalar=ec2[:, hh % 4, last:last + 1], in1=pss[:, hh * 64:(hh + 1) * 64], op0=ALU.mult, op1=ALU.add),
                             reads=[psb_, ecb2, Stb[g]], writes=[Stb[g]])
                    s.op("pool", ins("tensor_copy", out=Sbt[:, g * 512:(g + 1) * 512], in_=St[:, g * 512:(g + 1) * 512]), reads=[Stb[g]], writes=[Sbb[g]])
                    del bst[("grp", kc_, d, g)]

                NB_ = len(batches)
                for t_ in range(NB_ + 3):
                    if t_ < NB_:
                        stageA1(batches[t_])
                    if 1 <= t_ < NB_ + 1:
                        stageA2a(batches[t_ - 1])
                    if 2 <= t_ < NB_ + 2:
                        stageA2b(batches[t_ - 2])
                    if t_ >= 3:
                        stageB(batches[t_ - 3])
                for d in range(2):
                    ti, t0, n, nb = cur[d]
                    y_, y_b = yts[d]
                    s.dma(rows_view(YD[d], 2048, t0, nb, 0, 2048), y_[:, 0:nb, :], reads=[y_b], writes=[Y_b[ti]])
            if si > 0:
                for d in range(2):
                    s.dma(st_out[d][si - 1], ST[d][0][:], reads=ST[d][1])
        chk(22)
        s.barrier()
        s.release()
        Wo = s.alloc([128, 16, 1024], BF16, "wo")
        wob = load_w(Wo, w3_o, 16)
        vec = s.alloc([128, 5, 32], F32, "vec")
        gnb = s.alloc([128, 2048], F32, "gnb")
        cbuf = Buf()
        s.dma(vec[:], ssd_vec.rearrange("a b -> (a b)").partition_broadcast(128).rearrange("p (a b) -> p a b", a=5), writes=[cbuf])
        s.dma(gnb[:], ssd_gn.partition_broadcast(128), writes=[cbuf])
        rctx = ResCtx()
        yf = (s.alloc([128, 4, 2048], BF16, "yf"), Buf())
        yb = (s.alloc([128, 4, 2048], BF16, "yb"), Buf())
        xt_ = (s.alloc([128, 4, 2048], BF16, "xt4"), Buf())
        zs = (s.alloc([128, 4, 2048], BF16, "zs"), Buf())
        yv = [(s.alloc([128, 2048], F32, "yv"), Buf()) for _ in range(2)]
        tv_ = (s.alloc([128, 2048], F32, "tv"), Buf())
        ynb = (s.alloc([128, 2048], BF16, "ynb"), Buf())
        ssq = [(s.alloc([128, 1], F32, "ssq"), Buf()) for _ in range(2)]
        oT = [(s.alloc([128, 16, 512], BF16, "oT"), Buf()) for _ in range(2)]
        kk = 0
        tv = 0
        for ti in range(NT):
            t0, n, cond, _, _ = TILES[ti]
            nb = n // 128
            s.dma(yf[0][:, 0:nb, :], rows_view(YD[0], 2048, t0, nb, 0, 2048), writes=[yf[1]])
            s.dma(yb[0][:, 0:nb, :], rows_view(YD[1], 2048, t0, nb, 0, 2048), writes=[yb[1]])
            s.dma(xt_[0][:, 0:nb, :], rows_view(XTs, 2048, t0, nb, 0, 2048), writes=[xt_[1]])
            s.dma(zs[0][:, 0:nb, :], rows_view(ZS, 2048, t0, nb, 0, 2048), writes=[zs[1]])
            o_, o_b = oT[ti % 2]
            for tbk in range(nb):
                y_, y_b = yv[kk % 2]
                sq_, sq_b = ssq[kk % 2]
                kk += 1
                t_, t_b = tv_
                s.op("dve", ins("tensor_tensor", out=y_[:], in0=yf[0][:, tbk, :], in1=yb[0][:, tbk, :], op=ALU.add), reads=[yf[1], yb[1]], writes=[y_b])
                s.op("pool", ins("tensor_tensor", out=t_[:].rearrange("p (a b) -> p a b", b=64), in0=xt_[0][:, tbk, :].rearrange("p (a b) -> p a b", b=64),
                                                              in1=vec[:, 4, :].unsqueeze(2).to_broadcast([128, 32, 64]), op=ALU.mult), reads=[xt_[1], cbuf], writes=[t_b])
                s.op("dve", ins("tensor_tensor", out=y_[:], in0=y_[:], in1=t_[:], op=ALU.add), reads=[y_b, t_b], writes=[y_b])
                s.op("dve", ins("tensor_tensor", out=y_[:], in0=y_[:], in1=zs[0][:, tbk, :], op=ALU.mult), reads=[y_b, zs[1]], writes=[y_b])
                s.op("pool", ins("memset", sq_[:], 0.0), writes=[sq_b])
                s.op("act", ins("activation", out=t_[:], in_=y_[:], func=AF.Square, accum_out=sq_[:, 0:1]), reads=[y_b, sq_b], writes=[t_b, sq_b])
                s.op("act", ins("activation", out=sq_[:], in_=sq_[:], func=AF.Sqrt, scale=1.0 / 2048, bias=EPSB[:, 0:1]), reads=[sq_b], writes=[sq_b])
                s.op("dve", ins("reciprocal", out=sq_[:], in_=sq_[:]), reads=[sq_b], writes=[sq_b])
                yn, ynb_ = ynb
                s.op("dve", ins("scalar_tensor_tensor", out=yn[:], in0=y_[:], scalar=sq_[:, 0:1], in1=gnb[:], op0=ALU.mult, op1=ALU.mult), reads=[y_b, sq_b, cbuf], writes=[ynb_])
                for c in range(16):
                    ps, pb = s.ps()
                    psv = ps[:].bitcast(BF16)
                    s.op("pe", ins("transpose", psv[:, 0:128], yn[:, c * 128:(c + 1) * 128], ident_bf[:]), reads=[ynb_, cb_const], writes=[pb])
                    if tv % 2 == 0:
                        s.op("act", ins("activation", out=o_[:, c, tbk * 128:(tbk + 1) * 128], in_=psv[:, 0:128], func=AF.Identity), reads=[pb], writes=[o_b])
                    else:
                        s.op("dve", ins("tensor_copy", out=o_[:, c, tbk * 128:(tbk + 1) * 128], in_=psv[:, 0:128]), reads=[pb], writes=[o_b])
                    tv += 1
            outproj_resid(rctx, l, 2, ti, Wo, wob, 16, o_, [o_b], Xsrc, Xsrc_b, X, X_b)

    import os
    STOP = int(os.environ.get("KSTOP", "99"))

    class _Stop(Exception):
        pass

    def chk(k):
        if STOP == k:
            raise _Stop()

    try:
        prologue()
        chk(0)
        Xsrc, Xsrc_b = xT, xT_b
        for l in range(NLAYERS):
            kind = l % 4
            if kind == 0:
                qkv_phase(l, w0_qk, 12, w0_v, 256, Xsrc, Xsrc_b, True, kout=(8, o_wk, 64), vout=o_wv)
                chk(1)
                attn_phase(l, "win", Xsrc, Xsrc_b)
                chk(2)
                outproj_phase(l, w0_o, 8, OS, None, Xsrc, Xsrc_b)
                chk(3)
            elif kind == 1:
                gla_layer(l, Xsrc, Xsrc_b)
            elif kind == 2:
                qkv_phase(l, w2_qk, 16, w2_v, 1024, Xsrc, Xsrc_b, True, kout=(8, o_dk, 128), vout=o_dv)
                attn_phase(l, "diff", Xsrc, Xsrc_b)
                outproj_phase(l, w2_o, 8, OS, None, Xsrc, Xsrc_b)
            else:
                ssd_layer(l, Xsrc, Xsrc_b)
            Xsrc, Xsrc_b = X, X_b
            ffn(l, Xsrc, Xsrc_b)
            chk(10 + l)
        s.barrier()
        s.release()
        nctx = NormCtx()
        for ti in range(NT):
            norm_tile(nctx, 0, 0, ti, Xsrc, Xsrc_b, final_out=yT)
    except _Stop:
        pass
    s.barrier()
    counts = s.emit()
    return nc, counts

def _fm(w):
    K, N = w.shape
    return np.ascontiguousarray(w.reshape(K // 128, 128, N).transpose(1, 0, 2))


def _pc(v):
    return np.ascontiguousarray(v.reshape(-1, 128).T)


def _consts():
    f32 = np.float32
    t = np.arange(LS)
    row = (t // 64).astype(f32)
    col = (t % 64).astype(f32)
    inv = (10000.0 ** (-np.arange(0, 32, 2, dtype=f32) / 32)).astype(f32)
    cos = np.zeros((128, LS), f32)
    sin = np.zeros((128, LS), f32)
    perm = np.zeros((128, 128), f32)
    for p in range(128):
        d = p % 64
        axis = d // 32
        idx = d % 16
        second = (d % 32) >= 16
        ang = (row if axis == 0 else col) * inv[idx]
        cos[p] = np.cos(ang)
        sin[p] = np.sin(ang) * (1.0 if second else -1.0)
        partner = p + 16 if not second else p - 16
        perm[partner, p] = 1.0
    j = np.arange(128)[:, None]
    i = np.arange(512)[None, :]
    wmask = np.zeros((128, 6, 512), f32)
    for mi in range(6):
        o = mi - 1
        wmask[:, mi, :] = (np.abs(i - (o * 128 + j)) <= 128).astype(f32)
    tri = np.zeros((128, 4, 128), f32)
    jj = np.arange(128)[:, None]
    ii = np.arange(128)[None, :]
    tri[:, 0, :] = (jj <= ii)
    tri[:, 1, :] = (jj >= ii)
    tri[0:64, 2, 0:64] = (jj[0:64] <= ii[:, 0:64])
    tri[0:64, 3, 0:64] = (jj[0:64] >= ii[:, 0:64])
    m64 = np.ones((128, 512), f32)
    m64[:, ::64] = 0.0
    return dict(rcos=cos, rsin=sin, perm=perm, wmask=wmask, tri=tri, m64=m64)


_PROG = {}


def _get_prog(nl):
    if nl not in _PROG:
        _PROG[nl] = build_program(nl)
    return _PROG[nl]


def kernel(NLAYERS=4, **inp):
    f32 = np.float32
    g = {k: np.asarray(v) for k, v in inp.items()}
    nc, counts = _get_prog(NLAYERS)
    common = dict(_consts())
    common["ada_w"] = np.ascontiguousarray(g["ada_w"].reshape(4, 8, 128, 6144).transpose(0, 2, 1, 3))
    common["ada_b"] = np.ascontiguousarray(g["ada_b"].reshape(4, 48, 128).transpose(2, 0, 1))
    common["nrm"] = np.ascontiguousarray(np.stack([g["norm_mix"], g["norm_ffn"]], axis=1).reshape(4, 2, 8, 128).transpose(3, 0, 1, 2))
    common["fnorm"] = _pc(g["final_norm"])
    common["w_up"] = np.ascontiguousarray(g["ffn_w_up"].reshape(4, 8, 128, 5632).transpose(0, 2, 1, 3))
    common["w_dn"] = np.ascontiguousarray(g["ffn_w_down"].reshape(4, 22, 128, 1024).transpose(0, 2, 1, 3))
    common["ffn_cw"] = np.ascontiguousarray(g["ffn_conv_w"].reshape(4, 3, 44, 128).transpose(3, 0, 1, 2))
    common["ffn_cb"] = np.ascontiguousarray(g["ffn_conv_b"].reshape(4, 44, 128).transpose(2, 0, 1))
    wq = g["win_w_qkv"][0]
    qcols = wq[:, 0:1024]
    kcols = wq[:, 1024:1280]
    vcols = wq[:, 1280:1536]
    kdup = np.concatenate([np.concatenate([kcols[:, h * 64:(h + 1) * 64]] * 2, axis=1) for h in range(4)], axis=1)
    common["w0_qk"] = _fm(np.concatenate([qcols, kdup], axis=1))
    common["w0_v"] = _fm(vcols)
    common["w0_o"] = _fm(g["win_w_o"][0])
    common["sink"] = np.ascontiguousarray(g["win_sink"][0])
    wg = g["gla_w_qkvr"][0]
    common["w1_qk"] = _fm(wg[:, 0:1024])
    common["w1_v"] = _fm(wg[:, 1024:2048])
    common["w1_r"] = _fm(wg[:, 2048:3072])
    common["w1_g1"] = _fm(np.concatenate([g["gla_w_gf1"][0], g["gla_w_gb1"][0]], axis=1))
    g2 = np.zeros((32, 1024), f32)
    g2[0:16, 0:512] = g["gla_w_gf2"][0]
    g2[16:32, 512:1024] = g["gla_w_gb2"][0]
    common["w1_g2"] = g2
    common["w1_gb"] = _pc(np.concatenate([g["gla_b_gf"][0], g["gla_b_gb"][0]]))
    common["w1_gn"] = _pc(g["gla_norm"][0])
    common["w1_o"] = _fm(g["gla_w_o"][0])
    wd = g["diff_w_qkv"][0]
    common["w2_qk"] = _fm(wd[:, 0:2048])
    common["w2_v"] = _fm(wd[:, 2048:3072])
    common["w2_o"] = _fm(g["diff_w_o"][0])
    common["lqk"] = np.ascontiguousarray(np.stack([g["diff_lq1"][0], g["diff_lk1"][0], g["diff_lq2"][0], g["diff_lk2"][0]]))
    common["w2_gn"] = np.ascontiguousarray(g["diff_norm"][0].reshape(128, 1))
    common.update(_host_ssd_common(g))
    in_maps = []
    for b in range(8):
        m = dict(common)
        xs = g["x_sample"][b]
        xp = g["x_prompt"][2 * b:2 * b + 2].reshape(512, 1024)
        xa = np.concatenate([xs, xp], axis=0)
        m["xT"] = np.ascontiguousarray(xa.T.reshape(8, 128, T).transpose(1, 0, 2))
        cond = np.stack([g["c"][b], g["c_ctx"]], axis=1)
        m["condT"] = np.ascontiguousarray(cond.reshape(8, 128, 2).transpose(1, 0, 2))
        ck = g["cache_win_k"][b, 0]
        kT = ck.transpose(2, 1, 0)
        m["ck0"] = np.ascontiguousarray(np.concatenate([kT, kT], axis=0))
        m["cv0"] = np.ascontiguousarray(g["cache_win_v"][b, 0].reshape(512, 256))
        m["s1f"] = np.ascontiguousarray(g["state_gla_fwd"][b, 0].transpose(1, 0, 2))
        m["s1b"] = np.ascontiguousarray(g["state_gla_bwd"][b, 0].transpose(1, 0, 2))
        dk = g["cache_diff_k"][b, 0]
        m["ck2"] = np.ascontiguousarray(dk.transpose(2, 3, 1, 0).reshape(128, 8, 512))
        m["cv2"] = np.ascontiguousarray(g["cache_diff_v"][b, 0].reshape(512, 1024))
        m.update(_host_ssd_core(g, b))
        in_maps.append(m)
    import os
    ncores = int(os.environ.get("KCORES", "8"))
    ktrace = os.environ.get("KTRACE", "") == "1"
    res = run_bass_kernel_spmd(nc, in_maps[:ncores], core_ids=list(range(ncores)), trace=ktrace) if ktrace else run_bass_kernel_spmd(nc, in_maps[:ncores], core_ids=list(range(ncores)))
    if ktrace:
        print("EXEC_TIME_NS", res.exec_time_ns)
    R = list(res.results) + [res.results[0]] * (8 - ncores)
    y_prompt = np.zeros((16, 256, 1024), f32)
    y_sample = np.zeros((8, 4096, 1024), f32)
    win_k = np.zeros((16, 1, 256, 4, 64), f32)
    win_v = np.zeros((16, 1, 256, 4, 64), f32)
    gla_f = np.zeros((16, 1, 4, 128, 256), f32)
    gla_b = np.zeros((16, 1, 4, 128, 256), f32)
    diff_k = np.zeros((16, 1, 256, 8, 2, 64), f32)
    diff_v = np.zeros((16, 1, 256, 8, 128), f32)
    ssd_f = np.zeros((16, 1, 32, 64, 128), f32)
    ssd_b = np.zeros((16, 1, 32, 64, 128), f32)
    for b in range(8):
        r = R[b]
        y = r["yT"].transpose(1, 0, 2).reshape(1024, T).T
        y_sample[b] = y[0:4096]
        y_prompt[2 * b:2 * b + 2] = y[4096:].reshape(2, 256, 1024)
        wk = r["o_wk"]
        win_k[2 * b:2 * b + 2, 0] = wk.transpose(2, 0, 1).reshape(2, 256, 4, 64)
        win_v[2 * b:2 * b + 2, 0] = r["o_wv"].reshape(2, 256, 4, 64)
        gla_f[2 * b:2 * b + 2, 0] = r["o_gf"].transpose(0, 2, 1, 3)
        gla_b[2 * b:2 * b + 2, 0] = r["o_gb"].transpose(0, 2, 1, 3)
        dk = r["o_dk"]
        diff_k[2 * b:2 * b + 2, 0] = dk.transpose(2, 0, 1).reshape(2, 256, 8, 2, 64)
        diff_v[2 * b:2 * b + 2, 0] = r["o_dv"].reshape(2, 256, 8, 128)
        _host_ssd_out(r, b, ssd_f, ssd_b)
    return (y_prompt, y_sample, win_k, win_v, gla_f, gla_b, diff_k, diff_v, ssd_f, ssd_b)


def _host_ssd_common(g):
    w = g["ssd_w_in"][0]
    out = {}
    out["w3_z"] = _fm(w[:, 0:2048])
    out["w3_xbc"] = _fm(w[:, 2048:5120])
    out["w3_dt"] = _fm(w[:, 5120:5184])
    out["w3_o"] = _fm(g["ssd_w_out"][0])
    out["ssd_cw"] = np.ascontiguousarray(g["ssd_conv_w"][0].reshape(3, 24, 128).transpose(2, 0, 1))
    out["ssd_cb"] = _pc(g["ssd_conv_b"][0])
    out["ssd_vec"] = np.ascontiguousarray(np.stack([g["ssd_a_log_f"][0], g["ssd_a_log_b"][0], g["ssd_dt_bias_f"][0], g["ssd_dt_bias_b"][0], g["ssd_d"][0]]))
    out["ssd_gn"] = np.ascontiguousarray(g["ssd_norm"][0])
    return out


def _host_ssd_core(g, b):
    return {"s3f": np.ascontiguousarray(g["state_ssd_fwd"][b, 0].transpose(2, 0, 1).reshape(128, 2048)),
            "s3b": np.ascontiguousarray(g["state_ssd_bwd"][b, 0].transpose(2, 0, 1).reshape(128, 2048))}


def _host_ssd_out(r, b, ssd_f, ssd_b):
    ssd_f[2 * b:2 * b + 2, 0] = r["o_sf"].reshape(2, 128, 32, 64).transpose(0, 2, 3, 1)
    ssd_b[2 * b:2 * b + 2, 0] = r["o_sb"].reshape(2, 128, 32, 64).transpose(0, 2, 3, 1)
```

```python
import numpy as np
import concourse.bass as bass
import concourse.mybir as mybir
from concourse.bass_utils import run_bass_kernel_spmd

F32 = mybir.dt.float32
BF16 = mybir.dt.bfloat16
AF = mybir.ActivationFunctionType
ALU = mybir.AluOpType
AX = mybir.AxisListType
AP = bass.AP

SB_BASE = 16512
SB_TOP = 229344
EPOCH = 30000


class Buf:
    __slots__ = ("w", "rs", "name")

    def __init__(self, name=""):
        self.w = None
        self.rs = []
        self.name = name


class Op:
    __slots__ = ("eng", "fn", "deps", "dma", "ms", "sem", "val", "need", "prev", "grp")

    def __init__(self, eng, fn, dma):
        self.eng = eng
        self.fn = fn
        self.dma = dma
        self.deps = []
        self.ms = None
        self.sem = None
        self.val = None
        self.need = False
        self.prev = None
        self.grp = None


class Sched:
    ENGS = ("pe", "act", "dve", "pool", "sp")

    def __init__(self, nc):
        self.nc = nc
        self.ops = {e: [] for e in self.ENGS}
        self.dmas_since_barrier = []
        self.all_dmas = []
        self.nps = 0
        self.nacc = 0
        self.psum = []
        for i in range(8):
            t = nc.alloc_psum_tensor("psb%d" % i, [128, 512], F32)
            self.psum.append((t, Buf("ps%d" % i)))
        self.sb_off = SB_BASE
        self.sb_mark = SB_BASE
        self.nalloc = 0

    def alloc(self, shape, dtype, name=None):
        nbytes = int(np.prod(shape[1:])) * (4 if dtype == F32 else 2)
        nbytes = (nbytes + 63) // 64 * 64
        off = self.sb_off
        assert off + nbytes <= SB_TOP, "SBUF overflow %d" % (off + nbytes - SB_TOP)
        self.sb_off += nbytes
        self.nalloc += 1
        t = self.nc.alloc_sbuf_tensor_at("sb%d_%s" % (self.nalloc, name or "t"), list(shape), dtype, offset=off)
        return t

    def mark(self):
        self.sb_mark = self.sb_off

    def release(self):
        self.sb_off = self.sb_mark

    def ps(self, pool="a"):
        if pool == "a":
            t, b = self.psum[self.nps % 6]
            self.nps += 1
        else:
            t, b = self.psum[6 + self.nacc % 2]
            self.nacc += 1
        return t, b

    def op(self, eng, fn, reads=(), writes=(), dma=False):
        o = Op(eng, fn, dma)
        deps = {}
        for b in reads:
            d = b.w
            if d is not None:
                if (not dma) and (not d.dma) and d.eng == eng and eng == "pe":
                    continue
                deps[id(d)] = d
        for b in writes:
            cand = list(b.rs)
            if b.w is not None:
                cand.append(b.w)
            for d in cand:
                if d is o:
                    continue
                if (not dma) and (not d.dma) and d.eng == eng:
                    continue
                deps[id(d)] = d
        o.deps = list(deps.values())
        for b in reads:
            if not dma:
                b.rs = [r for r in b.rs if r.dma or r.eng != eng]
            b.rs.append(o)
        for b in writes:
            b.w = o
            b.rs = []
        self.ops[eng].append(o)
        if dma:
            self.dmas_since_barrier.append(o)
            self.all_dmas.append(o)
        return o

    def barrier(self):
        lasts = []
        for e in self.ENGS:
            for o in reversed(self.ops[e]):
                if not o.dma and o.fn is not None:
                    lasts.append(o)
                    break
        deps = lasts + self.dmas_since_barrier
        self.dmas_since_barrier = []
        for e in self.ENGS:
            o = Op(e, None, False)
            o.deps = list(deps)
            self.ops[e].append(o)

    def dma(self, out, in_, reads=(), writes=(), eng=None):
        if eng is None:
            eng = "pool" if type(out.tensor).__name__.startswith("DRam") else "sp"
        return self.op(eng, lambda e: e.dma_start(out=out, in_=in_), reads, writes, dma=True)

    def mm(self, out, lhsT, rhs, start, stop, reads=(), writes=(), grp=None):
        o = self.op("pe", lambda e: e.matmul(out, lhsT, rhs, start=start, stop=stop), reads, writes)
        o.grp = grp
        return o

    def emit(self):
        nc = self.nc
        for e in self.ENGS:
            for o in self.ops[e]:
                for d in o.deps:
                    d.need = True
        sem_ctx = []
        import contextlib
        with contextlib.ExitStack() as st:
            tl = {}
            for e in ("pe", "act", "dve", "pool"):
                tl[e] = [st.enter_context(nc.semaphore("tl_%s_%d" % (e, i))) for i in range(5)]
            npool = {"sp": 36, "pool": 36, "act": 8}
            dpool = {e: [st.enter_context(nc.semaphore("dq_%s_%d" % (e, i))) for i in range(n)] for e, n in npool.items()}
            for e in self.ENGS:
                m = 0
                k = 0
                for o in self.ops[e]:
                    if o.fn is None:
                        continue
                    if o.dma:
                        P = len(dpool[e])
                        o.sem = dpool[e][k % P]
                        o.val = 16 * (k // P + 1)
                        k += 1
                    elif o.need:
                        o.sem = tl[e][m // EPOCH]
                        o.val = m % EPOCH + 1
                        m += 1
                assert m < EPOCH * 5, (e, m)
            final = {}
            for e, n in npool.items():
                for o in self.ops[e]:
                    if o.dma:
                        final[id(o.sem)] = (o.sem, o.val)

            def stream(e, eng):
                seen = {}

                def wait(sem, val):
                    if seen.get(id(sem), 0) < val:
                        eng.wait_ge(sem, val)
                        seen[id(sem)] = val

                ops_ = self.ops[e]
                i_ = 0
                while i_ < len(ops_):
                    o = ops_[i_]
                    j_ = i_ + 1
                    if o.grp is not None:
                        while j_ < len(ops_) and ops_[j_].grp == o.grp:
                            j_ += 1
                    for k_ in range(i_, j_):
                        for d in ops_[k_].deps:
                            wait(d.sem, d.val)
                    for k_ in range(i_, j_):
                        o = ops_[k_]
                        if o.fn is None:
                            continue
                        if o.dma and o.val > 16:
                            wait(o.sem, o.val - 16)
                        ins_ = o.fn(eng)
                        if o.dma:
                            ins_.then_inc(o.sem, 16)
                        elif o.need:
                            ins_.then_inc(o.sem, 1)
                    i_ = j_
                if e == "sp":
                    for sem, val in final.values():
                        wait(sem, val)

            with nc.Block() as block:
                @block.tensor
                def _(eng):
                    stream("pe", eng)

                @block.scalar
                def _(eng):
                    stream("act", eng)

                @block.vector
                def _(eng):
                    stream("dve", eng)

                @block.gpsimd
                def _(eng):
                    stream("pool", eng)

                @block.sync
                def _(eng):
                    stream("sp", eng)
        return {e: len(v) for e, v in self.ops.items()}


def ins(method, *a, **kw):
    return lambda e: getattr(e, method)(*a, **kw)


T = 4608
LS = 4096
TILES = [(i * 512, 512, 0, 0, 4096) for i in range(8)] + [(4096, 256, 1, 4096, 4352), (4352, 256, 1, 4352, 4608)]
EPS = 1e-6
DFF = 2816
LAM_INIT = 0.8 - 0.6 * float(np.exp(-0.3 * 2))


def build_program(NLAYERS=4, dbg=False):
    nc = bass.Bass("TRN2", target_bir_lowering=False)
    s = Sched(nc)
    I = {}
    O = {}

    def din(name, shape):
        I[name] = nc.dram_tensor(name, list(shape), F32, kind="ExternalInput").ap()
        return I[name]

    def dout(name, shape):
        O[name] = nc.dram_tensor(name, list(shape), F32, kind="ExternalOutput").ap()
        return O[name]

    def dscr(name, shape, dt):
        return nc.dram_tensor(name, list(shape), dt).ap()

    xT = din("xT", [128, 8, T])
    condT = din("condT", [128, 8, 2])
    ada_w = din("ada_w", [4, 128, 8, 6144])
    ada_b = din("ada_b", [128, 4, 48])
    nrm = din("nrm", [128, 4, 2, 8])
    fnorm = din("fnorm", [128, 8])
    w_up = din("w_up", [4, 128, 8, 5632])
    w_dn = din("w_dn", [4, 128, 22, 1024])
    ffn_cw = din("ffn_cw", [128, 4, 3, 44])
    ffn_cb = din("ffn_cb", [128, 4, 44])
    perm_in = din("perm", [128, 128])
    rcos = din("rcos", [128, LS])
    rsin = din("rsin", [128, LS])
    wmask_in = din("wmask", [128, 6, 512])
    tri_in = din("tri", [128, 4, 128])
    m64_in = din("m64", [128, 512])
    w0_qk = din("w0_qk", [128, 8, 1536])
    w0_v = din("w0_v", [128, 8, 256])
    w0_o = din("w0_o", [128, 8, 1024])
    sink_in = din("sink", [16])
    ck0 = din("ck0", [128, 4, 512])
    cv0 = din("cv0", [512, 256])
    w1_qk = din("w1_qk", [128, 8, 1024])
    w1_v = din("w1_v", [128, 8, 1024])
    w1_r = din("w1_r", [128, 8, 1024])
    w1_g1 = din("w1_g1", [128, 8, 32])
    w1_g2 = din("w1_g2", [32, 1024])
    w1_gb = din("w1_gb", [128, 8])
    w1_gn = din("w1_gn", [128, 2])
    w1_o = din("w1_o", [128, 8, 1024])
    s1f = din("s1f", [128, 4, 256])
    s1b = din("s1b", [128, 4, 256])
    w2_qk = din("w2_qk", [128, 8, 2048])
    w2_v = din("w2_v", [128, 8, 1024])
    w2_o = din("w2_o", [128, 8, 1024])
    lqk = din("lqk", [4, 64])
    w2_gn = din("w2_gn", [128, 1])
    ck2 = din("ck2", [128, 8, 512])
    cv2 = din("cv2", [512, 1024])

    w3_z = din("w3_z", [128, 8, 2048])
    w3_xbc = din("w3_xbc", [128, 8, 3072])
    w3_dt = din("w3_dt", [128, 8, 64])
    w3_o = din("w3_o", [128, 16, 1024])
    ssd_cw = din("ssd_cw", [128, 3, 24])
    ssd_cb = din("ssd_cb", [128, 24])
    ssd_vec = din("ssd_vec", [5, 32])
    ssd_gn = din("ssd_gn", [2048])
    s3f = din("s3f", [128, 2048])
    s3b = din("s3b", [128, 2048])

    yT = dout("yT", [128, 8, T])
    o_sf = dout("o_sf", [2, 128, 2048])
    o_sb = dout("o_sb", [2, 128, 2048])
    o_wk = dout("o_wk", [4, 64, 512])
    o_wv = dout("o_wv", [512, 256])
    o_gf = dout("o_gf", [2, 128, 4, 256])
    o_gb = dout("o_gb", [2, 128, 4, 256])
    o_dk = dout("o_dk", [8, 128, 512])
    o_dv = dout("o_dv", [512, 1024])

    X = dscr("X", [128, 8, T], F32)
    U = dscr("U", [128, 44, T], BF16)
    QK = dscr("QK", [128, 16, T], BF16)
    VS = dscr("VS", [T, 1024], BF16)
    OS = dscr("OS", [128, 8, T], BF16)
    OS2 = dscr("OS2", [128, 8, T], BF16)
    RS = dscr("RS", [128, 8, T], BF16)
    QD = dscr("QD", [2, 128, 4, T], BF16)
    KD = dscr("KD", [2, 128, 4, T], BF16)
    KR = dscr("KR", [T, 8, 128], BF16)
    ZS = dscr("ZS", [T, 2048], BF16)
    DTS = dscr("DTS", [T, 128], F32)
    XTs = dscr("XTs", [T, 2048], BF16)
    BTs = dscr("BTs", [T, 512], BF16)
    YD = [dscr("YD0", [T, 2048], BF16), dscr("YD1", [T, 2048], BF16)]

    NT = len(TILES)
    import os
    STOP = int(os.environ.get("KSTOP", "99"))

    def rows_view(dr, rowlen, r0, nb, c0, w):
        return AP(dr.tensor, r0 * rowlen + c0, [[rowlen, 128], [128 * rowlen, nb], [1, w]])

    def kr_view(r0, nb, j0, nj):
        return AP(KR.tensor, r0 * 1024 + j0 * 128, [[1024, 128], [128 * 1024, nb], [128, nj], [1, 128]])
    import os
    KQ = os.environ.get("KQ", "")

    def tb(name):
        return [Buf(name + str(i)) for i in range(NT)]

    xT_b = tb("xT")
    X_b = tb("X")

    MODS = s.alloc([128, 4, 6, 8, 2], F32, "mods")
    GS = s.alloc([128, 4, 2, 8, 2], F32, "gs")
    NRM = s.alloc([128, 4, 2, 8], F32, "nrm")
    FNRM = s.alloc([128, 8], F32, "fnrm")
    ones_bf = s.alloc([128, 128], BF16, "ones")
    ones_f = s.alloc([128, 128], F32, "onesf")
    ident_bf = s.alloc([128, 128], BF16, "ident")
    perm_bf = s.alloc([128, 128], BF16, "perm")
    tri_bf = s.alloc([128, 4, 128], BF16, "tri")
    m64 = s.alloc([128, 512], F32, "m64")
    cb_const = Buf("consts")
    stage = s.alloc([128, 4, 128], F32, "stage")
    bstage = Buf()

    s.dma(NRM[:], nrm, writes=[cb_const])
    s.dma(FNRM[:], fnorm, writes=[cb_const])
    s.dma(m64[:], m64_in, writes=[cb_const])
    s.op("pool", ins("memset", ones_f[:], 1.0), writes=[cb_const])
    s.op("dve", ins("tensor_copy", out=ones_bf[:], in_=ones_f[:]), reads=[cb_const], writes=[cb_const])
    s.dma(stage[:, 0, :], perm_in, writes=[bstage])
    s.op("dve", ins("tensor_copy", out=perm_bf[:], in_=stage[:, 0, :]), reads=[bstage], writes=[cb_const])
    s.op("pool", ins("memset", stage[:, 1, :], 1.0), reads=[], writes=[bstage])
    s.op("pool", ins("affine_select", out=stage[:, 1, :], in_=stage[:, 1, :], pattern=[[-1, 128]], compare_op=ALU.is_equal, fill=0.0, base=0, channel_multiplier=1), reads=[bstage], writes=[bstage])
    s.op("dve", ins("tensor_copy", out=ident_bf[:], in_=stage[:, 1, :]), reads=[bstage], writes=[cb_const])
    s.barrier()
    s.dma(stage[:], tri_in, writes=[bstage])
    s.op("dve", ins("tensor_copy", out=tri_bf[:], in_=stage[:]), reads=[bstage], writes=[cb_const])
    s.mark()

    def mod_ap(l, k, c, cond):
        return MODS[:, l, k, c, cond:cond + 1]

    def prologue():
        s.barrier()
        s.release()
        sc = s.alloc([128, 8, 2], F32, "sc")
        scb = Buf()
        adb = s.alloc([128, 4, 48], F32, "adb")
        adbb = Buf()
        s.dma(sc[:], condT, writes=[scb])
        s.dma(adb[:], ada_b, writes=[adbb])
        s.op("act", ins("activation", out=sc[:], in_=sc[:], func=AF.Silu), reads=[scb], writes=[scb])
        wb = [s.alloc([128, 8, 1024], F32, "adaw%d" % i) for i in range(2)]
        wbb = [[Buf() for _ in range(2)] for _ in range(2)]
        it = 0
        for l in range(NLAYERS):
            for g in range(6):
                w = wb[it % 2]
                bb = wbb[it % 2]
                for hh in range(2):
                    s.dma(w[:, hh * 4:(hh + 1) * 4, :], ada_w[l, :, hh * 4:(hh + 1) * 4, g * 1024:(g + 1) * 1024], writes=[bb[hh]], eng=("sp" if hh == 0 else "act"))
                ps, pb = s.ps()
                for cc in range(8):
                    for kc in range(8):
                        s.mm(ps[:, cc * 2:cc * 2 + 2], w[:, kc, cc * 128:(cc + 1) * 128], sc[:, kc, :], kc == 0, kc == 7, reads=[bb[kc // 4], scb], writes=[pb])
                s.op("dve", ins("tensor_tensor",
                    out=MODS[:, l, g, :, :], in0=ps[:, 0:16].rearrange("p (c t) -> p c t", t=2),
                    in1=adb[:, l, g * 8:(g + 1) * 8].unsqueeze(2).to_broadcast([128, 8, 2]), op=ALU.add),
                    reads=[pb, adbb], writes=[cb_const])
                it += 1
            for which in range(2):
                k = 1 if which == 0 else 4
                s.op("dve", ins("scalar_tensor_tensor",
                    out=GS[:, l, which, :, :], in0=MODS[:, l, k, :, :], scalar=1.0,
                    in1=NRM[:, l, which, :].unsqueeze(2).to_broadcast([128, 8, 2]), op0=ALU.add, op1=ALU.mult),
                    reads=[cb_const], writes=[cb_const])

    def load_w(dst, src, nk, split=1):
        bufs = []
        for kc in range(nk):
            b = Buf()
            s.dma(dst[:, kc, :], src[:, kc, :], writes=[b], eng="pool")
            bufs.append(b)
        return bufs

    class NormCtx:
        def __init__(self, nbuf=2):
            self.nbuf = nbuf
            self.xt = [(s.alloc([128, 8, 512], F32, "nxt"), Buf()) for _ in range(nbuf)]
            self.h = [(s.alloc([128, 8, 512], BF16, "nh"), Buf()) for _ in range(nbuf)]
            self.sq = (s.alloc([128, 8, 512], BF16, "nsq"), Buf())
            self.tmp = (s.alloc([128, 8, 512], F32, "ntmp"), Buf())
            self.rstd = [(s.alloc([128, 512], F32, "nrs"), Buf()) for _ in range(2)]
            self.k = 0

    def norm_tile(ctx, l, which, ti, Xsrc, Xsrc_b, final_out=None):
        t0, n, cond, _, _ = TILES[ti]
        k = ctx.k
        ctx.k += 1
        xt, xtb = ctx.xt[k % ctx.nbuf]
        h, hb = ctx.h[k % ctx.nbuf]
        sq, sqb = ctx.sq
        tmp, tmpb = ctx.tmp
        rstd, rb = ctx.rstd[k % 2]
        s.dma(xt[:, :, :n], Xsrc[:, :, t0:t0 + n], reads=[Xsrc_b[ti]], writes=[xtb])
        s.op("act", ins("activation", out=sq[:, :, :n], in_=xt[:, :, :n], func=AF.Square), reads=[xtb], writes=[sqb])
        ps, pb = s.ps()
        for c in range(8):
            s.mm(ps[:, :n], ones_bf[:], sq[:, c, :n], c == 0, c == 7, reads=[sqb, cb_const], writes=[pb])
        s.op("act", ins("activation", out=rstd[:, :n], in_=ps[:, :n], func=AF.Sqrt, scale=1.0 / 1024, bias=EPSB[:, 0:1]), reads=[pb], writes=[rb])
        s.op("dve", ins("reciprocal", out=rstd[:, :n], in_=rstd[:, :n]), reads=[rb], writes=[rb])
        s.op("dve", ins("tensor_tensor", out=tmp[:, :, :n], in0=xt[:, :, :n], in1=rstd[:, :n].unsqueeze(1).to_broadcast([128, 8, n]), op=ALU.mult),
             reads=[xtb, rb], writes=[tmpb])
        if final_out is None:
            for c in range(8):
                s.op("act", ins("activation", out=h[:, c, :n], in_=tmp[:, c, :n], func=AF.Identity,
                                                          scale=GS[:, l, which, c, cond:cond + 1], bias=mod_ap(l, 0 if which == 0 else 3, c, cond)),
                     reads=[tmpb, cb_const], writes=[hb])
            return h, hb
        else:
            for c in range(8):
                s.op("act", ins("activation", out=xt[:, c, :n], in_=tmp[:, c, :n], func=AF.Identity, scale=FNRM[:, c:c + 1]),
                     reads=[tmpb, cb_const], writes=[xtb])
            s.dma(final_out[:, :, t0:t0 + n], xt[:, :, :n], reads=[xtb])
            return None, None

    EPSB = s.alloc([128, 1], F32, "epsb")
    s.op("pool", ins("memset", EPSB[:], EPS), writes=[cb_const])
    s.mark()

    class ResCtx:
        def __init__(self):
            self.xo = [(s.alloc([128, 8, 512], F32, "xo"), Buf()) for _ in range(2)]
            self.k = 0

    def outproj_resid(rctx, l, gate_k, ti, W, wbufs, nk, rhs, rhsbufs, Xsrc, Xsrc_b, Xdst, Xdst_b):
        t0, n, cond, _, _ = TILES[ti]
        xo, xob = rctx.xo[rctx.k % 2]
        rctx.k += 1
        s.dma(xo[:, :, :n], Xsrc[:, :, t0:t0 + n], reads=[Xsrc_b[ti]], writes=[xob])
        for dc in range(8):
            ps, pb = s.ps()
            for kc in range(nk):
                s.mm(ps[:, :n], W[:, kc, dc * 128:(dc + 1) * 128], rhs[:, kc, :n], kc == 0, kc == nk - 1,
                     reads=[wbufs[kc]] + list(rhsbufs), writes=[pb])
            s.op("dve", ins("scalar_tensor_tensor", out=xo[:, dc, :n], in0=ps[:, :n], scalar=mod_ap(l, gate_k, dc, cond),
                                                                      in1=xo[:, dc, :n], op0=ALU.mult, op1=ALU.add),
                 reads=[pb, xob, cb_const], writes=[xob])
        s.dma(Xdst[:, :, t0:t0 + n], xo[:, :, :n], reads=[xob], writes=[Xdst_b[ti]])

    def ffn(l, Xsrc, Xsrc_b):
        s.barrier()
        s.release()
        Wup = s.alloc([128, 8, 5632], BF16, "wup")
        wb = load_w(Wup, w_up[l], 8)
        nctx = NormCtx()
        ub = [(s.alloc([128, 11, 512], BF16, "ub"), [Buf() for _ in range(11)]) for _ in range(2)]
        U_b = [[Buf() for _ in range(4)] for _ in range(NT)]
        ku = 0
        ev = 0
        for ti in range(NT):
            t0, n, cond, _, _ = TILES[ti]
            h, hb = norm_tile(nctx, l, 1, ti, Xsrc, Xsrc_b)
            for grp in range(4):
                ut, utb = ub[ku % 2]
                ku += 1
                for cc in range(11):
                    col = grp * 11 + cc
                    ps, pb = s.ps()
                    for kc in range(8):
                        s.mm(ps[:, :n], Wup[:, kc, col * 128:(col + 1) * 128], h[:, kc, :n], kc == 0, kc == 7, reads=[wb[kc], hb], writes=[pb])
                    if ev % 2 == 0:
                        s.op("act", ins("activation", out=ut[:, cc, :n], in_=ps[:, :n], func=AF.Identity), reads=[pb], writes=[utb[cc]])
                    else:
                        s.op("dve", ins("tensor_copy", out=ut[:, cc, :n], in_=ps[:, :n]), reads=[pb], writes=[utb[cc]])
                    ev += 1
                s.dma(U[:, grp * 11:(grp + 1) * 11, t0:t0 + n], ut[:, :, :n], reads=utb, writes=[U_b[ti][grp]])
        s.barrier()
        s.release()
        Wd = s.alloc([128, 22, 1024], BF16, "wd")
        wdb = load_w(Wd, w_dn[l], 22)
        cw = s.alloc([128, 3, 44], F32, "cw")
        cbias = s.alloc([128, 44], F32, "cbias")
        cwb = Buf()
        s.dma(cw[:], ffn_cw[:, l, :, :], writes=[cwb])
        s.dma(cbias[:], ffn_cb[:, l, :], writes=[cwb])
        ug = s.alloc([128, 44, 514], BF16, "ug")
        ugb = [Buf() for _ in range(4)]
        at = [(s.alloc([128, 22, 512], BF16, "at"), Buf()) for _ in range(2)]
        tmps = [[(s.alloc([128, 512], F32, "ft"), Buf()) for _ in range(3)] for _ in range(4)]
        rctx = ResCtx()
        kp = 0
        for ti in range(NT):
            t0, n, cond, s0, s1 = TILES[ti]
            lo = (t0 - 1) >= s0
            hi = (t0 + n) < s1
            for grp in range(4):
                gs_ = slice(grp * 11, (grp + 1) * 11)
                if not lo:
                    s.op("pool", ins("memset", ug[:, gs_, 0:1], 0.0), writes=[ugb[grp]])
                if not hi:
                    s.op("pool", ins("memset", ug[:, gs_, n + 1:n + 2], 0.0), writes=[ugb[grp]])
                c0 = 0 if lo else 1
                c1 = n + 2 if hi else n + 1
                rd = [U_b[ti][grp]]
                if lo:
                    rd.append(U_b[ti - 1][grp])
                if hi:
                    rd.append(U_b[ti + 1][grp])
                s.dma(ug[:, gs_, c0:c1], U[:, gs_, t0 - 1 + c0:t0 - 1 + c1], reads=rd, writes=[ugb[grp]])
            a, ab = at[ti % 2]
            pst = {}

            def f2s1(cc):
                nonlocal kp
                res = []
                for half in range(2):
                    col = cc + 22 * half
                    tt, ttb = tmps[kp % 4][half]
                    grp = col // 11
                    s.op("act", ins("activation", out=tt[:, :n], in_=ug[:, col, 1:n + 1], func=AF.Identity,
                                    scale=cw[:, 1, col:col + 1], bias=cbias[:, col:col + 1]),
                         reads=[ugb[grp], cwb], writes=[ttb])
                    res.append((tt, ttb, col, grp))
                sg, sgb = tmps[kp % 4][2]
                kp += 1
                pst[cc] = (res, sg, sgb)

            def f2s2(cc):
                res, sg, sgb = pst[cc]
                for (tt, ttb, col, grp) in res:
                    s.op("dve", ins("scalar_tensor_tensor", out=tt[:, :n], in0=ug[:, col, 0:n], scalar=cw[:, 0, col:col + 1],
                                    in1=tt[:, :n], op0=ALU.mult, op1=ALU.add),
                         reads=[ugb[grp], cwb, ttb], writes=[ttb])
                    s.op("dve", ins("scalar_tensor_tensor", out=tt[:, :n], in0=ug[:, col, 2:n + 2], scalar=cw[:, 2, col:col + 1],
                                    in1=tt[:, :n], op0=ALU.mult, op1=ALU.add),
                         reads=[ugb[grp], cwb, ttb], writes=[ttb])

            def f2s3(cc):
                res, sg, sgb = pst.pop(cc)
                s.op("act", ins("activation", out=sg[:, :n], in_=res[0][0][:, :n], func=AF.Silu), reads=[res[0][1]], writes=[sgb])
                s.op("pool", ins("tensor_tensor", out=a[:, cc, :n], in0=sg[:, :n], in1=res[1][0][:, :n], op=ALU.mult),
                     reads=[sgb, res[1][1]], writes=[ab])

            for t_ in range(22 + 2):
                if t_ < 22:
                    f2s1(t_)
                if 1 <= t_ < 23:
                    f2s2(t_ - 1)
                if t_ >= 2:
                    f2s3(t_ - 2)
            outproj_resid(rctx, l, 5, ti, Wd, wdb, 22, a, [ab], Xsrc, Xsrc_b, X, X_b)

    def qkv_phase(l, Wqk_src, nqk, Wv_src, nv, Xsrc, Xsrc_b, rope, kout=None, vout=None, post=None):
        s.barrier()
        s.release()
        Wqk = s.alloc([128, 8, nqk * 128], BF16, "wqk")
        wqb = load_w(Wqk, Wqk_src, 8)
        Wv = s.alloc([128, 8, nv], BF16, "wv")
        wvb = load_w(Wv, Wv_src, 8)
        nctx = NormCtx()
        qk = [(s.alloc([128, nqk, 512], BF16, "qk"), [Buf() for _ in range(nqk)]) for _ in range(2)]
        vt = [(s.alloc([128, 4, nv], BF16, "vt"), Buf()) for _ in range(2)]
        qb = [(s.alloc([128, 512], BF16, "qb"), Buf()) for _ in range(2)]
        t12 = [[(s.alloc([128, 512], F32, "rt"), Buf()) for _ in range(2)] for _ in range(2)]
        cs = [(s.alloc([128, 2, 512], F32, "cs"), Buf()) for _ in range(2)]
        kf = [(s.alloc([128, 256], F32, "kf"), Buf()) for _ in range(2)]
        vf = [(s.alloc([128, 512], F32, "vf"), Buf()) for _ in range(2)]
        QK_b = [Buf() for _ in range(NT)]
        VS_b = [Buf() for _ in range(NT)]
        if "L" in KQ:
            return
        kq = 0
        kk = 0
        kv = 0
        for ti in range(NT):
            t0, n, cond, s0, s1 = TILES[ti]
            h, hb = norm_tile(nctx, l, 0, ti, Xsrc, Xsrc_b)
            if "N" in KQ:
                continue
            qkt, qkb = qk[ti % 2]
            dorope = rope and cond == 0 and ("r" not in KQ)
            if dorope:
                cst, csb = cs[ti % 2]
                s.dma(cst[:, 0, :], rcos[:, t0:t0 + n], writes=[csb])
                s.dma(cst[:, 1, :], rsin[:, t0:t0 + n], writes=[csb])
            for cc in range(nqk):
                ps, pb = s.ps()
                for kc in range(8):
                    s.mm(ps[:, :n], Wqk[:, kc, cc * 128:(cc + 1) * 128], h[:, kc, :n], kc == 0, kc == 7, reads=[wqb[kc], hb], writes=[pb])
                if dorope:
                    q_, q_b = qb[kq % 2]
                    t1, t1b = t12[kq % 2][0]
                    t2, t2b = t12[kq % 2][1]
                    kq += 1
                    s.op("act", ins("activation", out=q_[:, :n], in_=ps[:, :n], func=AF.Identity), reads=[pb], writes=[q_b])
                    ps2, pb2 = s.ps()
                    s.mm(ps2[:, :n], perm_bf[:], q_[:, :n], True, True, reads=[q_b, cb_const], writes=[pb2])
                    s.op("dve", ins("tensor_tensor", out=t1[:, :n], in0=q_[:, :n], in1=cst[:, 0, :n], op=ALU.mult),
                         reads=[q_b, csb], writes=[t1b])
                    s.op("dve", ins("tensor_tensor", out=t2[:, :n], in0=ps2[:, :n], in1=cst[:, 1, :n], op=ALU.mult),
                         reads=[pb2, csb], writes=[t2b])
                    s.op("pool", ins("tensor_tensor", out=qkt[:, cc, :n], in0=t1[:, :n], in1=t2[:, :n], op=ALU.add),
                         reads=[t1b, t2b], writes=[qkb[cc]])
                else:
                    if kout is not None and cond == 1 and cc >= kout[0]:
                        kft, kfb = kf[kk % 2]
                        kk += 1
                        rows = kout[2]
                        s.op("dve", ins("tensor_copy", out=kft[:, :n], in_=ps[:, :n]), reads=[pb], writes=[kfb])
                        s.op("act", ins("activation", out=qkt[:, cc, :n], in_=kft[:, :n], func=AF.Identity), reads=[kfb], writes=[qkb[cc]])
                        s.dma(kout[1][cc - kout[0], :, t0 - LS:t0 - LS + n], kft[0:rows, :n], reads=[kfb])
                    else:
                        s.op("act", ins("activation", out=qkt[:, cc, :n], in_=ps[:, :n], func=AF.Identity), reads=[pb], writes=[qkb[cc]])
                if post is not None:
                    post(ti, cc, ps, pb)
            vtt, vtb = vt[ti % 2]
            for tbk in range(n // 128 if "v" not in KQ else 0):
                for vg in range((nv + 511) // 512):
                    w_ = min(512, nv - vg * 512)
                    ps, pb = s.ps()
                    for kc in range(8):
                        s.mm(ps[:, :w_], h[:, kc, tbk * 128:(tbk + 1) * 128], Wv[:, kc, vg * 512:vg * 512 + w_], kc == 0, kc == 7, reads=[wvb[kc], hb], writes=[pb])
                    if vout is not None and cond == 1:
                        vft, vfb = vf[kv % 2]
                        kv += 1
                        s.op("dve", ins("tensor_copy", out=vft[:, :w_], in_=ps[:, :w_]), reads=[pb], writes=[vfb])
                        s.op("act", ins("activation", out=vtt[:, tbk, vg * 512:vg * 512 + w_], in_=vft[:, :w_], func=AF.Identity),
                             reads=[vfb], writes=[vtb])
                        r0 = t0 - LS + tbk * 128
                        s.dma(vout[r0:r0 + 128, vg * 512:vg * 512 + w_], vft[:, :w_], reads=[vfb])
                    else:
                        s.op("act", ins("activation", out=vtt[:, tbk, vg * 512:vg * 512 + w_], in_=ps[:, :w_], func=AF.Identity),
                             reads=[pb], writes=[vtb])
            if "q" not in KQ:
                s.dma(QK[:, 0:nqk, t0:t0 + n], qkt[:, :, :n], reads=qkb, writes=[QK_b[ti]])
            if "v" not in KQ and "s" not in KQ:
              s.dma(rows_view(VS, 1024, t0, n // 128, 0, nv), vtt[:, 0:n // 128, :], reads=[vtb], writes=[VS_b[ti]])

    def attn_phase(l, kind, Xsrc, Xsrc_b):
        s.barrier()
        s.release()
        win = kind == "win"
        nunits = 8
        kbase = 8
        Kt = [(s.alloc([128, T], BF16, "kt"), Buf()) for _ in range(2)]
        Va = [(s.alloc([128, 36, 128], BF16, "va"), Buf()) for _ in range(2)]
        Qt = [(s.alloc([128, LS], BF16, "qt"), Buf()) for _ in range(2)]
        Ot = [(s.alloc([128, LS], BF16, "ot"), Buf()) for _ in range(2)]
        pT = [(s.alloc([128, 512], BF16, "pT"), Buf()) for _ in range(8)]
        rec = [(s.alloc([128, 512], F32, "rec"), Buf()) for _ in range(8)]
        OS_b = [[Buf() for _ in range(nunits)] for _ in range(3)]
        cbuf = Buf()
        if win:
            masks = s.alloc([128, 6, 512], BF16, "masks")
            s.dma(masks[:], wmask_in, writes=[cbuf], eng="pool")
            esink = s.alloc([128, 16], F32, "esink")
            s.dma(esink[:], sink_in.partition_broadcast(128), writes=[cbuf])
            s.op("act", ins("activation", out=esink[:], in_=esink[:], func=AF.Exp), reads=[cbuf], writes=[cbuf])
            for i in range(2):
                s.op("pool", ins("memset", Va[i][0][:, :, 64:128], 1.0), writes=[Va[i][1]])
        else:
            acc = [(s.alloc([128, 512], F32, "acc"), Buf()) for _ in range(4)]
            osb = [(s.alloc([128, 512], F32, "osb"), Buf()) for _ in range(8)]
            sqd = [(s.alloc([128, 512], BF16, "sqd"), Buf()) for _ in range(4)]
            lq = s.alloc([128, 4, 64], F32, "lq")
            lam = s.alloc([128, 4], F32, "lam")
            gsub = s.alloc([128, 1], F32, "gsub")
            s.dma(lq[:], lqk.rearrange("a b -> (a b)").partition_broadcast(128).rearrange("p (a b) -> p a b", a=4), writes=[cbuf])
            s.dma(gsub[:], w2_gn, writes=[cbuf])
            s.op("dve", ins("tensor_tensor", out=lq[:, 0, :], in0=lq[:, 0, :], in1=lq[:, 1, :], op=ALU.mult), reads=[cbuf], writes=[cbuf])
            s.op("dve", ins("tensor_tensor", out=lq[:, 2, :], in0=lq[:, 2, :], in1=lq[:, 3, :], op=ALU.mult), reads=[cbuf], writes=[cbuf])
            s.op("dve", ins("reduce_sum", out=lam[:, 0:1], in_=lq[:, 0, :], axis=AX.X), reads=[cbuf], writes=[cbuf])
            s.op("dve", ins("reduce_sum", out=lam[:, 1:2], in_=lq[:, 2, :], axis=AX.X), reads=[cbuf], writes=[cbuf])
            s.op("act", ins("activation", out=lam[:, 0:2], in_=lam[:, 0:2], func=AF.Exp), reads=[cbuf], writes=[cbuf])
            s.op("dve", ins("tensor_tensor", out=lam[:, 2:3], in0=lam[:, 1:2], in1=lam[:, 0:1], op=ALU.subtract), reads=[cbuf], writes=[cbuf])
            s.op("dve", ins("tensor_scalar", out=lam[:, 2:3], in0=lam[:, 2:3], scalar1=-LAM_INIT, scalar2=None, op0=ALU.add), reads=[cbuf], writes=[cbuf])
            s.op("dve", ins("tensor_scalar", out=lam[:, 3:4], in0=gsub[:], scalar1=1.0 - LAM_INIT, scalar2=None, op0=ALU.mult), reads=[cbuf], writes=[cbuf])
        ku = 0
        kp = 0
        kr = 0
        ka = 0
        SEQS = [(0, 4096, 0), (4096, 256, 1), (4352, 256, 2)]
        for (q0, L, si) in SEQS:
            sample = si == 0
            nkb_lat = L // 128
            for u in range(nunits):
                kt, ktb = Kt[ku % 2]
                va, vab = Va[ku % 2]
                qt, qtb = Qt[ku % 2]
                ot, otb = Ot[ku % 2]
                ku += 1
                s.dma(qt[:, 0:L], QK[:, u, q0:q0 + L], writes=[qtb])
                if win:
                    g = u // 2
                    s.dma(kt[:, 0:L], QK[:, kbase + g, q0:q0 + L], writes=[ktb])
                    s.dma(va[:, 0:nkb_lat, 0:64], rows_view(VS, 1024, q0, nkb_lat, g * 64, 64), writes=[vab])
                    if sample:
                        s.dma(kt[:, L:L + 512], ck0[:, g, :], writes=[ktb], eng="pool")
                        s.dma(va[:, 32:36, 0:64], rows_view(cv0, 256, 0, 4, g * 64, 64), writes=[vab], eng="pool")
                else:
                    s.dma(kt[:, 0:L], QK[:, kbase + u, q0:q0 + L], writes=[ktb])
                    s.dma(va[:, 0:nkb_lat, :], rows_view(VS, 1024, q0, nkb_lat, u * 128, 128), writes=[vab])
                    if sample:
                        s.dma(kt[:, L:L + 512], ck2[:, u, :], writes=[ktb], eng="pool")
                        s.dma(va[:, 32:36, :], rows_view(cv2, 1024, 0, 4, u * 128, 128), writes=[vab], eng="pool")
                nq = 512 if sample else 256
                LOOK = 4
                tasks = []
                for qi in range(L // nq):
                    for e_ in range(2):
                        if win and sample:
                            kbs = [(kb, kb - qi * 4 + 1) for kb in range(qi * 4 - 1, qi * 4 + 5) if 0 <= kb < 32] + [(32 + j, None) for j in range(4)]
                        elif sample:
                            kbs = [(kb, None) for kb in range(36)]
                        else:
                            kbs = [(kb, None) for kb in range(2)]
                        for i, (kb, mi) in enumerate(kbs):
                            tasks.append((qi, e_, i, kb, mi, i == len(kbs) - 1))
                stt = {}

                def stage1(tk_):
                    nonlocal kp, ka
                    qi, e_, i, kb, mi, lastb = tk_
                    qs = slice(qi * nq, (qi + 1) * nq)
                    pr = slice(e_ * 64, (e_ + 1) * 64)
                    if i == 0:
                        d_ = {}
                        d_["pso"], d_["pob"] = s.ps("b")
                        if not win:
                            d_["ac"], d_["acb"] = acc[ka % 4]
                            ka += 1
                        stt[(qi, e_)] = d_
                    d_ = stt[(qi, e_)]
                    pss, psb_ = s.ps()
                    s.mm(pss[:, :nq], kt[pr, kb * 128:(kb + 1) * 128], qt[pr, qs], True, True, reads=[ktb, qtb], writes=[psb_], grp=("q", ku, cur_it[0] // 2))
                    p_, p_b = pT[kp % 8]
                    kp += 1
                    s.op("act", ins("activation", out=p_[:, :nq], in_=pss[:, :nq], func=AF.Exp, scale=0.125), reads=[psb_], writes=[p_b])
                    if mi is not None:
                        s.op("pool" if (kp % 2) else "dve", ins("tensor_tensor", out=p_[:, :nq], in0=p_[:, :nq], in1=masks[:, mi, :nq], op=ALU.mult),
                             reads=[p_b, cbuf], writes=[p_b])
                    if not win:
                        eng = "pool" if e_ == 0 else "dve"
                        ac, acb = d_["ac"], d_["acb"]
                        if i == 0:
                            s.op(eng, ins("tensor_copy", out=ac[:, :nq], in_=p_[:, :nq]), reads=[p_b], writes=[acb])
                        else:
                            s.op(eng, ins("tensor_tensor", out=ac[:, :nq], in0=ac[:, :nq], in1=p_[:, :nq], op=ALU.add), reads=[p_b, acb], writes=[acb])
                    d_[("p", i)] = (p_, p_b)

                def stage2(tk_):
                    nonlocal kr
                    qi, e_, i, kb, mi, lastb = tk_
                    qs = slice(qi * nq, (qi + 1) * nq)
                    pr = slice(e_ * 64, (e_ + 1) * 64)
                    d_ = stt[(qi, e_)]
                    pso, pob = d_["pso"], d_["pob"]
                    p_, p_b = d_.pop(("p", i))
                    s.mm(pso[:, :nq], va[:, kb, :], p_[:, :nq], i == 0, lastb, reads=[vab, p_b], writes=[pob], grp=("v", ku, cur_it[0] // 2))
                    if not lastb:
                        return
                    if win:
                        hh = u * 2 + e_
                        r_, r_b = rec[kr % 4]
                        kr += 1
                        s.op("dve", ins("tensor_scalar", out=r_[64:128, :nq], in0=pso[64:128, :nq], scalar1=esink[64:128, hh:hh + 1], scalar2=None, op0=ALU.add),
                             reads=[pob, cbuf], writes=[r_b])
                        s.op("dve", ins("reciprocal", out=r_[64:128, :nq], in_=r_[64:128, :nq]), reads=[r_b], writes=[r_b])
                        s.op("dve", ins("tensor_tensor", out=ot[pr, qs], in0=pso[0:64, :nq], in1=r_[64:128, :nq], op=ALU.mult),
                             reads=[pob, r_b], writes=[otb])
                        del stt[(qi, e_)]
                        return
                    ac, acb = d_["ac"], d_["acb"]
                    o_, o_b = osb[kr % 8]
                    r_, r_b = rec[kr % 8]
                    kr += 1
                    s.op("dve", ins("tensor_copy", out=o_[:, :nq], in_=pso[:, :nq]), reads=[pob], writes=[o_b])
                    d_["o"] = (o_, o_b)
                    key = (qi, e_)

                    def st1():
                        psd, pdb = s.ps()
                        s.mm(psd[:, :nq], ones_f[:], ac[:, :nq], True, True, reads=[acb, cb_const], writes=[pdb])
                        d_["psd"] = (psd, pdb)
                        defer(2, st2)

                    def st2():
                        psd, pdb = d_["psd"]
                        s.op("dve", ins("reciprocal", out=r_[:, :nq], in_=psd[:, :nq]), reads=[pdb], writes=[r_b])
                        defer(4, st3)

                    def st3():
                        s.op("dve", ins("tensor_tensor", out=o_[:, :nq], in0=o_[:, :nq], in1=r_[:, :nq], op=ALU.mult), reads=[o_b, r_b], writes=[o_b])
                        d_["done"] = True
                        if e_ == 1 or stt[(qi, 1)].get("done") if (qi, 1) in stt else False:
                            pass
                        if (qi, 0) in stt and (qi, 1) in stt and stt[(qi, 0)].get("done") and stt[(qi, 1)].get("done"):
                            defer(1, st4)

                    def st4():
                        (o0, o0b) = stt[(qi, 0)]["o"]
                        (o1, o1b) = stt[(qi, 1)]["o"]
                        s.op("dve", ins("scalar_tensor_tensor", out=o0[:, :nq], in0=o1[:, :nq], scalar=lam[:, 2:3], in1=o0[:, :nq], op0=ALU.mult, op1=ALU.add),
                             reads=[o0b, o1b, cbuf], writes=[o0b])
                        sq_, sq_b = sqd[qi % 4]
                        s.op("act", ins("activation", out=sq_[:, :nq], in_=o0[:, :nq], func=AF.Square), reads=[o0b], writes=[sq_b])
                        defer(2, st5)

                    def st5():
                        sq_, sq_b = sqd[qi % 4]
                        psn, pnb = s.ps()
                        s.mm(psn[:, :nq], ones_bf[:], sq_[:, :nq], True, True, reads=[sq_b, cb_const], writes=[pnb])
                        stt[(qi, 1)]["psn"] = (psn, pnb)
                        defer(2, st6)

                    def st6():
                        (o1, o1b) = stt[(qi, 1)]["o"]
                        psn, pnb = stt[(qi, 1)]["psn"]
                        s.op("act", ins("activation", out=o1[:, :nq], in_=psn[:, :nq], func=AF.Sqrt, scale=1.0 / 128, bias=EPSB[:, 0:1]), reads=[pnb, o1b], writes=[o1b])
                        defer(2, st7)

                    def st7():
                        (o1, o1b) = stt[(qi, 1)]["o"]
                        s.op("dve", ins("reciprocal", out=o1[:, :nq], in_=o1[:, :nq]), reads=[o1b], writes=[o1b])
                        defer(4, st8)

                    def st8():
                        (o0, o0b) = stt[(qi, 0)]["o"]
                        (o1, o1b) = stt[(qi, 1)]["o"]
                        s.op("dve", ins("scalar_tensor_tensor", out=ot[:, qs], in0=o0[:, :nq], scalar=lam[:, 3:4], in1=o1[:, :nq], op0=ALU.mult, op1=ALU.mult),
                             reads=[o0b, o1b, cbuf], writes=[otb])
                        del stt[(qi, 0)]
                        del stt[(qi, 1)]

                    defer(2, st1)

                pend = []
                cur_it = [0]

                def defer(dl, fn):
                    pend.append((cur_it[0] + dl, fn))

                def run_pending(flush=False):
                    while True:
                        ready = [p for p in pend if flush or p[0] <= cur_it[0]]
                        if not ready:
                            break
                        for p in ready:
                            pend.remove(p)
                        for due, fn in ready:
                            fn()
                        if not flush:
                            break

                for t2_ in range(0, len(tasks) + LOOK + 1, 2):
                    for t_ in (t2_, t2_ + 1):
                        cur_it[0] = t_
                        if t_ < len(tasks):
                            stage1(tasks[t_])
                    for t_ in (t2_, t2_ + 1):
                        cur_it[0] = t_
                        if LOOK <= t_ < len(tasks) + LOOK:
                            stage2(tasks[t_ - LOOK])
                    run_pending()
                while pend:
                    cur_it[0] += 1
                    run_pending(flush=True)
                s.dma(OS[:, u, q0:q0 + L], ot[:, 0:L], reads=[otb], writes=[OS_b[si][u]])
        return OS_b

    def outproj_phase(l, Wsrc, nk, Osrc, O_bufs_fn, Xsrc, Xsrc_b):
        s.barrier()
        s.release()
        Wo = s.alloc([128, nk, 1024], BF16, "wo")
        wob = load_w(Wo, Wsrc, nk)
        rctx = ResCtx()
        oin = [(s.alloc([128, nk, 512], BF16, "oin"), Buf()) for _ in range(2)]
        for ti in range(NT):
            t0, n, cond, _, _ = TILES[ti]
            o_, o_b = oin[ti % 2]
            s.dma(o_[:, :, :n], Osrc[:, :, t0:t0 + n], writes=[o_b])
            outproj_resid(rctx, l, 2, ti, Wo, wob, nk, o_, [o_b], Xsrc, Xsrc_b, X, X_b)

    def gla_layer(l, Xsrc, Xsrc_b):
        s.barrier()
        s.release()
        Wqk = s.alloc([128, 8, 1024], BF16, "gwqk")
        wqb = load_w(Wqk, w1_qk, 8)
        Wv = s.alloc([128, 8, 1024], BF16, "gwv")
        wvb = load_w(Wv, w1_v, 8)
        Wr = s.alloc([128, 8, 1024], BF16, "gwr")
        wrb = load_w(Wr, w1_r, 8)
        Wg1 = s.alloc([128, 8, 32], BF16, "gwg1")
        wg1b = load_w(Wg1, w1_g1, 8)
        Wg2 = s.alloc([32, 1024], BF16, "gwg2")
        cbuf = Buf()
        s.dma(Wg2[:], w1_g2, writes=[cbuf], eng="pool")
        gb = s.alloc([128, 8], F32, "ggb")
        s.dma(gb[:], w1_gb, writes=[cbuf])
        s.op("dve", ins("tensor_scalar", out=gb[:], in0=gb[:], scalar1=-1.0, scalar2=None, op0=ALU.mult), reads=[cbuf], writes=[cbuf])
        ELt = s.alloc([128, 2, 4, 72], F32, "el")
        nctx = NormCtx(1)
        qkf = [(s.alloc([128, 8, 512], F32, "qkf"), Buf()) for _ in range(1)]
        lg = nctx.xt[0]
        cs_ = (s.alloc([128, 8, 512], F32, "cs"), Buf())
        c2 = nctx.tmp
        ex = [(s.alloc([128, 512], F32, "ex"), Buf()) for _ in range(3)]
        t1b_ = (s.alloc([32, 512], BF16, "t1b"), Buf())
        qd_t = [(s.alloc([128, 2, 4, 512], BF16, "qd"), Buf()) for _ in range(1)]
        kd_t = [(s.alloc([128, 2, 4, 512], BF16, "kd"), Buf()) for _ in range(1)]
        krT = [(s.alloc([128, 512], BF16, "krT"), Buf()) for _ in range(2)]
        krt = [(s.alloc([128, 4, 8, 128], BF16, "krt"), Buf()) for _ in range(1)]
        rt = [(s.alloc([128, 8, 512], BF16, "rt"), Buf()) for _ in range(1)]
        vt = [(s.alloc([128, 4, 1024], BF16, "vt"), Buf()) for _ in range(1)]
        G_b = [Buf() for _ in range(NT)]
        kx = 0
        kk = 0
        for ti in range(NT):
            t0, n, cond, s0, s1 = TILES[ti]
            nch = n // 64
            h, hb = norm_tile(nctx, l, 0, ti, Xsrc, Xsrc_b)
            qf, qfb = qkf[0]
            for cc in range(8):
                ps, pb = s.ps()
                for kc in range(8):
                    s.mm(ps[:, :n], Wqk[:, kc, cc * 128:(cc + 1) * 128], h[:, kc, :n], kc == 0, kc == 7, reads=[wqb[kc], hb], writes=[pb])
                sc_ = (128.0 ** -0.5) if cc < 4 else 1.0
                s.op("act", ins("activation", out=qf[:, cc, :n], in_=ps[:, :n], func=AF.Identity, scale=sc_), reads=[pb], writes=[qfb])
            r_, r_b = rt[0]
            for cc in range(8):
                ps, pb = s.ps()
                for kc in range(8):
                    s.mm(ps[:, :n], Wr[:, kc, cc * 128:(cc + 1) * 128], h[:, kc, :n], kc == 0, kc == 7, reads=[wrb[kc], hb], writes=[pb])
                s.op("act", ins("activation", out=r_[:, cc, :n], in_=ps[:, :n], func=AF.Silu), reads=[pb], writes=[r_b])
            s.dma(RS[:, :, t0:t0 + n], r_[:, :, :n], reads=[r_b], writes=[G_b[ti]])
            v_, v_b = vt[0]
            for tbk in range(n // 128):
                for vg in range(2):
                    ps, pb = s.ps()
                    for kc in range(8):
                        s.mm(ps[:, :512], h[:, kc, tbk * 128:(tbk + 1) * 128], Wv[:, kc, vg * 512:(vg + 1) * 512], kc == 0, kc == 7, reads=[wvb[kc], hb], writes=[pb])
                    s.op("act", ins("activation", out=v_[:, tbk, vg * 512:(vg + 1) * 512], in_=ps[:, :512], func=AF.Identity), reads=[pb], writes=[v_b])
            s.dma(rows_view(VS, 1024, t0, n // 128, 0, 1024), v_[:, 0:n // 128, :], reads=[v_b], writes=[G_b[ti]])
            ps, pb = s.ps()
            for kc in range(8):
                s.mm(ps[0:32, :n], Wg1[:, kc, :], h[:, kc, :n], kc == 0, kc == 7, reads=[wg1b[kc], hb], writes=[pb])
            t1_, t1bb = t1b_
            s.op("act", ins("activation", out=t1_[:, :n], in_=ps[0:32, :n], func=AF.Identity), reads=[pb], writes=[t1bb])
            lgt, lgb = lg
            for j in range(8):
                ps, pb = s.ps()
                s.mm(ps[:, :n], Wg2[:, j * 128:(j + 1) * 128], t1_[:, :n], True, True, reads=[cbuf, t1bb], writes=[pb])
                s.op("act", ins("activation", out=lgt[:, j, :n], in_=ps[:, :n], func=AF.Exp, scale=-1.0, bias=gb[:, j:j + 1]), reads=[pb, cbuf], writes=[lgb])
            s.op("act", ins("activation", out=lgt[:, :, :n], in_=lgt[:, :, :n], func=AF.Ln, bias=1.0, scale=1.0), reads=[lgb], writes=[lgb])
            cst, csb = cs_
            for j in range(8):
                s.op("dve", ins("tensor_tensor_scan", out=cst[:, j, :n], data0=m64[:, :n], data1=lgt[:, j, :n], initial=0.0, op0=ALU.mult, op1=ALU.add),
                     reads=[lgb, cb_const], writes=[csb])
            c2t, c2b = c2
            csv = cst[:, :, :n].rearrange("p j (c t) -> p j c t", t=64)
            lgv = lgt[:, :, :n].rearrange("p j (c t) -> p j c t", t=64)
            c2v = c2t[:, :, :n].rearrange("p j (c t) -> p j c t", t=64)
            nf = 4
            s.op("dve", ins("tensor_tensor", out=c2v[:, 0:nf, :, :], in0=csv[:, 0:nf, :, :], in1=csv[:, 0:nf, :, 63:64].to_broadcast([128, nf, nch, 64]), op=ALU.subtract),
                 reads=[csb], writes=[c2b])
            s.op("dve", ins("tensor_tensor", out=c2v[:, nf:2 * nf, :, :], in0=lgv[:, nf:2 * nf, :, :], in1=csv[:, nf:2 * nf, :, :], op=ALU.subtract),
                 reads=[csb, lgb], writes=[c2b])
            ci0 = t0 // 64
            s.op("act", ins("activation", out=ELt[:, :, :, ci0:ci0 + nch].rearrange("p d h c -> p (d h) c"), in_=cst[:, :, :n].rearrange("p j (c t) -> p j c t", t=64)[:, :, :, 63],
                                               func=AF.Exp, scale=-1.0 / 16), reads=[csb], writes=[cbuf])
            s.op("dve", ins("tensor_tensor", out=lgv[:, nf:2 * nf, :, :], in0=c2v[:, nf:2 * nf, :, :], in1=csv[:, nf:2 * nf, :, 63:64].to_broadcast([128, nf, nch, 64]), op=ALU.add),
                 reads=[csb, c2b, lgb], writes=[lgb])
            qd_, qdb = qd_t[0]
            kd_, kdb = kd_t[0]
            krt_, krtb = krt[0]
            for j in range(8):
                d = j // 4
                hh = j % 4
                ea, eab = ex[0]
                eb, ebb = ex[1]
                ec, ecb = ex[2]
                csrc = cst if j < 4 else lgt
                s.op("act", ins("activation", out=ea[:, :n], in_=csrc[:, j, :n], func=AF.Exp, scale=-1.0 / 16), reads=[csb, lgb], writes=[eab])
                s.op("act", ins("activation", out=eb[:, :n], in_=csrc[:, j, :n], func=AF.Exp, scale=1.0 / 16), reads=[csb, lgb], writes=[ebb])
                s.op("act", ins("activation", out=ec[:, :n], in_=c2t[:, j, :n], func=AF.Exp, scale=1.0 / 16), reads=[c2b], writes=[ecb])
                s.op("dve", ins("tensor_tensor", out=qd_[:, d, hh, :n], in0=qf[:, hh, :n], in1=ea[:, :n], op=ALU.mult), reads=[qfb, eab], writes=[qdb])
                s.op("dve", ins("tensor_tensor", out=kd_[:, d, hh, :n], in0=qf[:, 4 + hh, :n], in1=eb[:, :n], op=ALU.mult), reads=[qfb, ebb], writes=[kdb])
                kT, kTb = krT[kx % 2]
                kx += 1
                s.op("pool", ins("tensor_tensor", out=kT[:, :n], in0=qf[:, 4 + hh, :n], in1=ec[:, :n], op=ALU.mult), reads=[qfb, ecb], writes=[kTb])
                for tbk in range(n // 128):
                    ps, pb = s.ps()
                    psv = ps[:].bitcast(BF16)
                    s.op("pe", ins("transpose", psv[:, 0:128], kT[:, tbk * 128:(tbk + 1) * 128], ident_bf[:]), reads=[kTb, cb_const], writes=[pb])
                    s.op("act", ins("activation", out=krt_[:, tbk, j, :], in_=psv[:, 0:128], func=AF.Identity), reads=[pb], writes=[krtb])
            for d in range(2):
                s.dma(QD[d, :, :, t0:t0 + n], qd_[:, d, :, :n], reads=[qdb], writes=[G_b[ti]])
                s.dma(KD[d, :, :, t0:t0 + n], kd_[:, d, :, :n], reads=[kdb], writes=[G_b[ti]])
            s.dma(kr_view(t0, n // 128, 0, 8), krt_[:, 0:n // 128, :, :], reads=[krtb], writes=[G_b[ti]])
        ELd = dscr("ELd", [128, 2, 4, 72], F32)
        elb = Buf()
        s.dma(ELd, ELt[:], reads=[cbuf], writes=[elb])
        s.barrier()
        s.release()
        EL = s.alloc([128, 2, 4, 72], F32, "el2")
        elb2 = Buf()
        s.dma(EL[:], ELd, writes=[elb2])
        S = [(s.alloc([128, 4, 256], F32, "S"), [Buf() for _ in range(4)]) for _ in range(2)]
        Sb = [(s.alloc([128, 4, 256], BF16, "Sb"), [Buf() for _ in range(4)]) for _ in range(2)]
        qd_t = [[(s.alloc([128, 4, 512], BF16, "qd"), Buf()) for _ in range(2)] for _ in range(2)]
        kd_t = [[(s.alloc([128, 4, 512], BF16, "kd"), Buf()) for _ in range(2)] for _ in range(2)]
        kr_t = [[(s.alloc([128, 4, 4, 128], BF16, "kr"), Buf()) for _ in range(2)] for _ in range(2)]
        v_t = [[(s.alloc([128, 4, 1024], BF16, "v"), Buf()) for _ in range(2)] for _ in range(2)]
        of_t = [[(s.alloc([128, 8, 512], BF16, "of"), Buf()) for _ in range(2)] for _ in range(2)]
        attm = [(s.alloc([128, 64], BF16, "attm"), Buf()) for _ in range(8)]
        OD_b = [[Buf() for _ in range(NT)] for _ in range(2)]
        ODs = [OS, OS2]
        kat = 0
        SEQT = [list(range(8)), [8], [9]]
        states_in = [s1f, s1b]
        states_out = [o_gf, o_gb]
        for si, tiles in enumerate(SEQT):
            for d in range(2):
                St, Stb = S[d]
                Sbt, Sbb = Sb[d]
                if si == 0:
                    s.dma(St[:], states_in[d], writes=Stb)
                else:
                    s.op("pool", ins("memset", St[:], 0.0), writes=Stb)
                for hh in range(4):
                    s.op("act", ins("activation", out=Sbt[:, hh, :], in_=St[:, hh, :], func=AF.Identity), reads=[Stb[hh]], writes=[Sbb[hh]])
            nt_ = len(tiles)
            for step in range(nt_):
                cur = {}
                for d in range(2):
                    ti = tiles[step] if d == 0 else tiles[nt_ - 1 - step]
                    t0, n, cond, s0, s1 = TILES[ti]
                    qd_, qdb = qd_t[d][step % 2]
                    kd_, kdb = kd_t[d][step % 2]
                    kr_, krb = kr_t[d][step % 2]
                    v_, vb_ = v_t[d][step % 2]
                    of_, ofb = of_t[d][step % 2]
                    s.dma(qd_[:, :, :n], QD[d, :, :, t0:t0 + n], writes=[qdb])
                    s.dma(kd_[:, :, :n], KD[d, :, :, t0:t0 + n], writes=[kdb])
                    s.dma(kr_[:, 0:n // 128, :, :], kr_view(t0, n // 128, d * 4, 4), writes=[krb])
                    s.dma(v_[:, 0:n // 128, :], rows_view(VS, 1024, t0, n // 128, 0, 1024), writes=[vb_])
                    cur[d] = (ti, t0, n, qd_, qdb, kd_, kdb, kr_, krb, v_, vb_, of_, ofb)
                nch = cur[0][2] // 64
                for kc_ in range(nch):
                    units = []
                    for d in range(2):
                        ti, t0, n, qd_, qdb, kd_, kdb, kr_, krb, v_, vb_, of_, ofb = cur[d]
                        k = kc_ if d == 0 else nch - 1 - kc_
                        for hh in range(4):
                            units.append(dict(d=d, hh=hh, k=k, ci=t0 // 64 + k, tbk=k // 2, hp=slice((k % 2) * 64, (k % 2) * 64 + 64),
                                              cs64=slice(k * 64, (k + 1) * 64), qd_=qd_, qdb=qdb, kd_=kd_, kdb=kdb, kr_=kr_, krb=krb, v_=v_, vb_=vb_, of_=of_, ofb=ofb))
                    psA, pbA = s.ps()
                    for ui, u_ in enumerate(units):
                        s.mm(psA[0:64, ui * 64:(ui + 1) * 64], u_["kd_"][:, u_["hh"], u_["cs64"]], u_["qd_"][:, u_["hh"], u_["cs64"]], True, True,
                             reads=[u_["kdb"], u_["qdb"]], writes=[pbA], grp=("ga", kat))
                    for ui, u_ in enumerate(units):
                        am, amb = attm[ui]
                        u_["am"] = (am, amb)
                        s.op("dve", ins("tensor_tensor", out=am[u_["hp"], :], in0=psA[0:64, ui * 64:(ui + 1) * 64], in1=tri_bf[0:64, 2 + u_["d"], 0:64], op=ALU.mult),
                             reads=[pbA, cb_const], writes=[amb])
                    psO = [s.ps("b") for _ in range(2)]
                    for ui, u_ in enumerate(units):
                        pso, pob = psO[ui // 4]
                        am, amb = u_["am"]
                        St, Stb = S[u_["d"]]
                        Sbt, Sbb = Sb[u_["d"]]
                        hh, hp, tbk = u_["hh"], u_["hp"], u_["tbk"]
                        c0 = (ui % 4) * 128
                        for vc in range(2):
                            s.mm(pso[:, c0 + vc * 64:c0 + (vc + 1) * 64], u_["v_"][hp, tbk, hh * 256 + vc * 128:hh * 256 + (vc + 1) * 128], am[hp, :], True, False,
                                 reads=[u_["vb_"], amb], writes=[pob])
                            s.mm(pso[:, c0 + vc * 64:c0 + (vc + 1) * 64], Sbt[:, hh, vc * 128:(vc + 1) * 128], u_["qd_"][:, hh, u_["cs64"]], False, True,
                                 reads=[Sbb[hh], u_["qdb"]], writes=[pob])
                    for ui, u_ in enumerate(units):
                        pso, pob = psO[ui // 4]
                        c0 = (ui % 4) * 128
                        hh = u_["hh"]
                        s.op("act", ins("activation", out=u_["of_"][:, hh * 2:hh * 2 + 2, u_["cs64"]], in_=pso[:, c0:c0 + 128].rearrange("p (v t) -> p v t", t=64), func=AF.Identity),
                             reads=[pob], writes=[u_["ofb"]])
                    psS = [s.ps() for _ in range(4)]
                    for ui, u_ in enumerate(units):
                        ps2, pb2 = psS[ui // 2]
                        c0 = (ui % 2) * 256
                        hh, hp, tbk = u_["hh"], u_["hp"], u_["tbk"]
                        s.mm(ps2[:, c0:c0 + 256], u_["kr_"][hp, tbk, hh, :], u_["v_"][hp, tbk, hh * 256:(hh + 1) * 256], True, True, reads=[u_["krb"], u_["vb_"]], writes=[pb2])
                    for ui, u_ in enumerate(units):
                        ps2, pb2 = psS[ui // 2]
                        c0 = (ui % 2) * 256
                        hh, d = u_["hh"], u_["d"]
                        St, Stb = S[d]
                        Sbt, Sbb = Sb[d]
                        ci = u_["ci"]
                        s.op("dve", ins("scalar_tensor_tensor", out=St[:, hh, :], in0=St[:, hh, :], scalar=EL[:, d, hh, ci:ci + 1], in1=ps2[:, c0:c0 + 256], op0=ALU.mult, op1=ALU.add),
                             reads=[pb2, elb2, Stb[hh]], writes=[Stb[hh]])
                        s.op("pool", ins("tensor_copy", out=Sbt[:, hh, :], in_=St[:, hh, :]), reads=[Stb[hh]], writes=[Sbb[hh]])
                    kat += 1
                for d in range(2):
                    ti, t0, n, qd_, qdb, kd_, kdb, kr_, krb, v_, vb_, of_, ofb = cur[d]
                    s.dma(ODs[d][:, :, t0:t0 + n], of_[:, :, :n], reads=[ofb], writes=[OD_b[d][ti]])
            if si > 0:
                for d in range(2):
                    s.dma(states_out[d][si - 1], S[d][0][:], reads=S[d][1])
        s.barrier()
        s.release()
        Wo = s.alloc([128, 8, 1024], BF16, "wo")
        wob = load_w(Wo, w1_o, 8)
        gn = s.alloc([128, 2], F32, "gn")
        gnb = Buf()
        s.dma(gn[:], w1_gn, writes=[gnb])
        rctx = ResCtx()
        oa = [(s.alloc([128, 8, 512], BF16, "oa"), Buf()) for _ in range(2)]
        ob_ = [(s.alloc([128, 8, 512], BF16, "ob"), Buf()) for _ in range(2)]
        rr = [(s.alloc([128, 8, 512], BF16, "rr"), Buf()) for _ in range(2)]
        osum = (s.alloc([128, 8, 512], F32, "osum"), Buf())
        sq = (s.alloc([128, 8, 512], BF16, "sq"), Buf())
        rs_ = [(s.alloc([128, 512], F32, "rs"), Buf()) for _ in range(2)]
        ofin = [(s.alloc([128, 8, 512], BF16, "ofin"), Buf()) for _ in range(2)]
        for ti in range(NT):
            t0, n, cond, _, _ = TILES[ti]
            a_, a_b = oa[ti % 2]
            b_, b_b = ob_[ti % 2]
            r_, r_b = rr[ti % 2]
            s.dma(a_[:, :, :n], OS[:, :, t0:t0 + n], writes=[a_b])
            s.dma(b_[:, :, :n], OS2[:, :, t0:t0 + n], writes=[b_b])
            s.dma(r_[:, :, :n], RS[:, :, t0:t0 + n], writes=[r_b])
            os_, osb_ = osum
            sq_, sqb_ = sq
            s.op("dve", ins("tensor_tensor", out=os_[:, :, :n], in0=a_[:, :, :n], in1=b_[:, :, :n], op=ALU.add), reads=[a_b, b_b], writes=[osb_])
            s.op("act", ins("activation", out=sq_[:, :, :n], in_=os_[:, :, :n], func=AF.Square), reads=[osb_], writes=[sqb_])
            f_, f_b = ofin[ti % 2]
            for hh in range(4):
                ps, pb = s.ps()
                for vc in range(2):
                    s.mm(ps[:, :n], ones_bf[:], sq_[:, hh * 2 + vc, :n], vc == 0, vc == 1, reads=[sqb_, cb_const], writes=[pb])
                rt_, rtb = rs_[hh % 2]
                s.op("act", ins("activation", out=rt_[:, :n], in_=ps[:, :n], func=AF.Sqrt, scale=1.0 / 256, bias=EPSB[:, 0:1]), reads=[pb], writes=[rtb])
                s.op("dve", ins("reciprocal", out=rt_[:, :n], in_=rt_[:, :n]), reads=[rtb], writes=[rtb])
                for vc in range(2):
                    c = hh * 2 + vc
                    s.op("dve", ins("tensor_tensor", out=os_[:, c, :n], in0=os_[:, c, :n], in1=rt_[:, :n], op=ALU.mult), reads=[osb_, rtb], writes=[osb_])
                    s.op("dve", ins("scalar_tensor_tensor", out=f_[:, c, :n], in0=os_[:, c, :n], scalar=gn[:, vc:vc + 1], in1=r_[:, c, :n], op0=ALU.mult, op1=ALU.mult),
                         reads=[osb_, gnb, r_b], writes=[f_b])
            outproj_resid(rctx, l, 2, ti, Wo, wob, 8, f_, [f_b], Xsrc, Xsrc_b, X, X_b)

    def ssd_layer(l, Xsrc, Xsrc_b):
        XBu = U
        s.barrier()
        s.release()
        Wz = s.alloc([128, 8, 2048], BF16, "wz")
        wzb = load_w(Wz, w3_z, 8)
        Wx = s.alloc([128, 8, 3072], BF16, "wx")
        wxb = load_w(Wx, w3_xbc, 8)
        Wdt = s.alloc([128, 8, 64], BF16, "wdt")
        wdb = load_w(Wdt, w3_dt, 8)
        vec = s.alloc([128, 5, 32], F32, "vec")
        cbuf = Buf()
        s.dma(vec[:], ssd_vec.rearrange("a b -> (a b)").partition_broadcast(128).rearrange("p (a b) -> p a b", a=5), writes=[cbuf])
        avec = s.alloc([128, 64], F32, "avec")
        s.op("act", ins("activation", out=avec[:], in_=vec[:, 0:2, :].rearrange("p a b -> p (a b)"), func=AF.Exp), reads=[cbuf], writes=[cbuf])
        s.op("dve", ins("tensor_scalar", out=avec[:], in0=avec[:], scalar1=-1.0, scalar2=None, op0=ALU.mult), reads=[cbuf], writes=[cbuf])
        nctx = NormCtx(1)
        xbt = (s.alloc([128, 24, 512], BF16, "xbt"), [Buf() for _ in range(24)])
        zt = (s.alloc([128, 4, 2048], BF16, "zt"), Buf())
        dtt = (s.alloc([128, 4, 2, 64], F32, "dtt"), Buf())
        dtmp = (s.alloc([128, 64], F32, "dtmp"), Buf())
        S_b = [Buf() for _ in range(NT)]
        ev = 0
        for ti in range(NT):
            t0, n, cond, s0, s1 = TILES[ti]
            nb = n // 128
            h, hb = norm_tile(nctx, l, 0, ti, Xsrc, Xsrc_b)
            xb_, xbb = xbt
            for cc in range(24):
                ps, pb = s.ps()
                for kc in range(8):
                    s.mm(ps[:, :n], Wx[:, kc, cc * 128:(cc + 1) * 128], h[:, kc, :n], kc == 0, kc == 7, reads=[wxb[kc], hb], writes=[pb])
                if ev % 2 == 0:
                    s.op("act", ins("activation", out=xb_[:, cc, :n], in_=ps[:, :n], func=AF.Identity), reads=[pb], writes=[xbb[cc]])
                else:
                    s.op("dve", ins("tensor_copy", out=xb_[:, cc, :n], in_=ps[:, :n]), reads=[pb], writes=[xbb[cc]])
                ev += 1
            s.dma(XBu[:, 0:24, t0:t0 + n], xb_[:, :, :n], reads=xbb, writes=[S_b[ti]])
            z_, z_b = zt
            d_, d_b = dtt
            for tbk in range(nb):
                for vg in range(4):
                    ps, pb = s.ps()
                    for kc in range(8):
                        s.mm(ps[:, :512], h[:, kc, tbk * 128:(tbk + 1) * 128], Wz[:, kc, vg * 512:(vg + 1) * 512], kc == 0, kc == 7, reads=[wzb[kc], hb], writes=[pb])
                    s.op("act", ins("activation", out=z_[:, tbk, vg * 512:(vg + 1) * 512], in_=ps[:, :512], func=AF.Silu), reads=[pb], writes=[z_b])
                ps, pb = s.ps()
                for kc in range(8):
                    s.mm(ps[:, 0:64], h[:, kc, tbk * 128:(tbk + 1) * 128], Wdt[:, kc, :], kc == 0, kc == 7, reads=[wdb[kc], hb], writes=[pb])
                tm, tmb = dtmp
                s.op("dve", ins("tensor_tensor", out=tm[:], in0=ps[:, 0:64], in1=vec[:, 2:4, :].rearrange("p a b -> p (a b)"), op=ALU.add), reads=[pb, cbuf], writes=[tmb])
                s.op("act", ins("activation", out=tm[:], in_=tm[:], func=AF.Exp), reads=[tmb], writes=[tmb])
                s.op("act", ins("activation", out=d_[:, tbk, 0, :], in_=tm[:], func=AF.Ln, bias=1.0, scale=1.0), reads=[tmb], writes=[d_b])
                s.op("dve", ins("tensor_tensor", out=d_[:, tbk, 1, :], in0=d_[:, tbk, 0, :], in1=avec[:], op=ALU.mult), reads=[d_b, cbuf], writes=[d_b])
            s.dma(rows_view(ZS, 2048, t0, nb, 0, 2048), z_[:, 0:nb, :], reads=[z_b], writes=[S_b[ti]])
            s.dma(AP(DTS.tensor, t0 * 128, [[128, 128], [128 * 128, nb], [1, 128]]), d_[:, 0:nb, :, :].rearrange("p b a c -> p b (a c)"), reads=[d_b], writes=[S_b[ti]])
        chk(20)
        s.barrier()
        s.release()
        cw = s.alloc([128, 3, 24], F32, "scw")
        cbias = s.alloc([128, 24], F32, "scb")
        cwb = Buf()
        s.dma(cw[:], ssd_cw, writes=[cwb])
        s.dma(cbias[:], ssd_cb, writes=[cwb])
        ug = s.alloc([128, 24, 514], BF16, "sug")
        ugb = [Buf() for _ in range(2)]
        xc = (s.alloc([128, 24, 512], BF16, "xc"), [Buf() for _ in range(24)])
        tmps = [(s.alloc([128, 512], F32, "st"), Buf()) for _ in range(2)]
        xtt = (s.alloc([128, 4, 2048], BF16, "xtt"), Buf())
        btt = (s.alloc([128, 4, 512], BF16, "btt"), Buf())
        kp = 0
        for ti in range(NT):
            t0, n, cond, s0, s1 = TILES[ti]
            nb = n // 128
            lo = (t0 - 1) >= s0
            hi = (t0 + n) < s1
            for grp in range(2):
                gs_ = slice(grp * 12, (grp + 1) * 12)
                if not lo:
                    s.op("pool", ins("memset", ug[:, gs_, 0:1], 0.0), writes=[ugb[grp]])
                if not hi:
                    s.op("pool", ins("memset", ug[:, gs_, n + 1:n + 2], 0.0), writes=[ugb[grp]])
                c0 = 0 if lo else 1
                c1 = n + 2 if hi else n + 1
                s.dma(ug[:, gs_, c0:c1], XBu[:, gs_, t0 - 1 + c0:t0 - 1 + c1], writes=[ugb[grp]])
            xc_, xcb = xc
            for col in range(24):
                grp = col // 12
                tt, ttb = tmps[kp % 2]
                kp += 1
                s.op("act", ins("activation", out=tt[:, :n], in_=ug[:, col, 1:n + 1], func=AF.Identity, scale=cw[:, 1, col:col + 1], bias=cbias[:, col:col + 1]),
                     reads=[ugb[grp], cwb], writes=[ttb])
                s.op("dve", ins("scalar_tensor_tensor", out=tt[:, :n], in0=ug[:, col, 0:n], scalar=cw[:, 0, col:col + 1], in1=tt[:, :n], op0=ALU.mult, op1=ALU.add),
                     reads=[ugb[grp], cwb, ttb], writes=[ttb])
                s.op("dve", ins("scalar_tensor_tensor", out=tt[:, :n], in0=ug[:, col, 2:n + 2], scalar=cw[:, 2, col:col + 1], in1=tt[:, :n], op0=ALU.mult, op1=ALU.add),
                     reads=[ugb[grp], cwb, ttb], writes=[ttb])
                s.op("act", ins("activation", out=xc_[:, col, :n], in_=tt[:, :n], func=AF.Silu), reads=[ttb], writes=[xcb[col]])
            s.dma(QK[:, 0:8, t0:t0 + n], xc_[:, 16:24, :n], reads=xcb[16:24], writes=[S_b[ti]])
            x_, x_b = xtt
            b_, b_b = btt
            tv = 0
            for tbk in range(nb):
                for col in range(20):
                    ps, pb = s.ps()
                    psv = ps[:].bitcast(BF16)
                    s.op("pe", ins("transpose", psv[:, 0:128], xc_[:, col, tbk * 128:(tbk + 1) * 128], ident_bf[:]), reads=[xcb[col], cb_const], writes=[pb])
                    dst = x_[:, tbk, col * 128:(col + 1) * 128] if col < 16 else b_[:, tbk, (col - 16) * 128:(col - 15) * 128]
                    dbuf = x_b if col < 16 else b_b
                    if tv % 2 == 0:
                        s.op("act", ins("activation", out=dst, in_=psv[:, 0:128], func=AF.Identity), reads=[pb], writes=[dbuf])
                    else:
                        s.op("dve", ins("tensor_copy", out=dst, in_=psv[:, 0:128]), reads=[pb], writes=[dbuf])
                    tv += 1
            s.dma(rows_view(XTs, 2048, t0, nb, 0, 2048), x_[:, 0:nb, :], reads=[x_b], writes=[S_b[ti]])
            s.dma(rows_view(BTs, 512, t0, nb, 0, 512), b_[:, 0:nb, :], reads=[b_b], writes=[S_b[ti]])
        chk(21)
        s.barrier()
        s.release()
        vec = s.alloc([128, 5, 32], F32, "vec")
        negm = s.alloc([128, 2, 128], F32, "negm")
        cbuf = Buf()
        s.op("dve", ins("tensor_scalar", out=negm[:], in0=stage[:, 0:2, :], scalar1=-1.0, scalar2=30000.0, op0=ALU.add, op1=ALU.mult), reads=[bstage], writes=[cbuf])
        ST = [(s.alloc([128, 2048], F32, "ST"), [Buf() for _ in range(4)]) for _ in range(2)]
        SbT = [(s.alloc([128, 2048], BF16, "SbT"), [Buf() for _ in range(4)]) for _ in range(2)]
        bct = [(s.alloc([128, 8, 512], BF16, "bct"), Buf()) for _ in range(2)]
        xts = [(s.alloc([128, 4, 2048], BF16, "xts"), Buf()) for _ in range(2)]
        bts = [(s.alloc([128, 4, 512], BF16, "bts"), Buf()) for _ in range(2)]
        dts = [(s.alloc([128, 4, 2, 64], F32, "dts"), Buf()) for _ in range(2)]
        yts = [(s.alloc([128, 4, 2048], BF16, "yts"), Buf()) for _ in range(2)]
        cumT = [(s.alloc([128, 32], F32, "cumT"), Buf()) for _ in range(2)]
        cbT = [(s.alloc([128, 128], F32, "cbT"), Buf()) for _ in range(6)]
        lat = [(s.alloc([128, 4, 128], F32, "lat"), Buf()) for _ in range(3)]
        cB = [(s.alloc([128, 4, 128], F32, "cB"), Buf()) for _ in range(4)]
        seg = [(s.alloc([128, 4, 128], F32, "seg"), Buf()) for _ in range(3)]
        Et = [(s.alloc([128, 4, 128], F32, "Et"), Buf()) for _ in range(6)]
        ecB = [(s.alloc([128, 4, 128], F32, "ecB"), Buf()) for _ in range(8)]
        te = [(s.alloc([128, 4], F32, "te"), Buf()) for _ in range(3)]
        xs = [(s.alloc([128, 512], BF16, "xs"), Buf()) for _ in range(4)]
        Wt = [(s.alloc([128, 128], BF16, "Wt"), Buf()) for _ in range(8)]
        CEt = [(s.alloc([128, 128], BF16, "CEt"), Buf()) for _ in range(8)]
        SEQT = [list(range(8)), [8], [9]]
        st_in = [s3f, s3b]
        st_out = [o_sf, o_sb]
        Y_b = [Buf() for _ in range(NT)]
        kw = 0
        kq4 = 0
        kg = 0
        kct = 0
        for si, tiles in enumerate(SEQT):
            for d in range(2):
                St, Stb = ST[d]
                Sbt, Sbb = SbT[d]
                if si == 0:
                    s.dma(St[:], st_in[d], writes=Stb)
                else:
                    s.op("pool", ins("memset", St[:], 0.0), writes=Stb)
                for g in range(4):
                    s.op("act", ins("activation", out=Sbt[:, g * 512:(g + 1) * 512], in_=St[:, g * 512:(g + 1) * 512], func=AF.Identity), reads=[Stb[g]], writes=[Sbb[g]])
            nt_ = len(tiles)
            for step in range(nt_):
                cur = {}
                for d in range(2):
                    ti = tiles[step] if d == 0 else tiles[nt_ - 1 - step]
                    t0, n, cond, s0, s1 = TILES[ti]
                    nb = n // 128
                    bc_, bcb = bct[d]
                    x_, x_b = xts[d]
                    b_, b_b = bts[d]
                    d_, d_b = dts[d]
                    y_, y_b = yts[d]
                    s.dma(bc_[:, :, :n], QK[:, 0:8, t0:t0 + n], writes=[bcb])
                    s.dma(x_[:, 0:nb, :], rows_view(XTs, 2048, t0, nb, 0, 2048), writes=[x_b])
                    s.dma(b_[:, 0:nb, :], rows_view(BTs, 512, t0, nb, 0, 512), writes=[b_b])
                    s.dma(d_[:, 0:nb, :, :].rearrange("p b a c -> p b (a c)"), AP(DTS.tensor, t0 * 128, [[128, 128], [128 * 128, nb], [1, 128]]), writes=[d_b])
                    cur[d] = (ti, t0, n, nb)
                nbs = cur[0][3]
                batches = []
                for kc_ in range(nbs):
                    for d in range(2):
                        for g in range(4):
                            for hb_ in range(2):
                                batches.append((kc_, d, g, hb_))
                bst = {}
                LOOKB = 3

                def geo(bt_):
                    kc_, d, g, hb_ = bt_
                    ti, t0, n, nb = cur[d]
                    tbk = kc_ if d == 0 else nb - 1 - kc_
                    last = 127 if d == 0 else 0
                    d_, d_b = dts[d]
                    la = d_[:, tbk, 1, d * 32:(d + 1) * 32]
                    dtv = d_[:, tbk, 0, d * 32:(d + 1) * 32]
                    return tbk, last, slice(tbk * 128, (tbk + 1) * 128), la, dtv, d_b

                def stageA1(bt_):
                    nonlocal kq4, kg, kct
                    kc_, d, g, hb_ = bt_
                    tbk, last, tk, la, dtv, d_b = geo(bt_)
                    bc_, bcb = bct[d]
                    if g == 0 and hb_ == 0:
                        psc, pcb = s.ps()
                        s.mm(psc[:, 0:32], stage[:, d, :], la, True, True, reads=[bstage, d_b], writes=[pcb])
                        cT, cTb = cumT[kct % 2]
                        kct += 1
                        s.op("dve", ins("tensor_copy", out=cT[:], in_=psc[:, 0:32]), reads=[pcb], writes=[cTb])
                        bst[("cT", kc_, d)] = (cT, cTb)
                    if hb_ == 0:
                        ps, pb = s.ps()
                        s.mm(ps[:, 0:128], bc_[:, g, tk], bc_[:, 4 + g, tk], True, True, reads=[bcb], writes=[pb])
                        cb_, cb_b = cbT[kg % 6]
                        s.op("act", ins("activation", out=cb_[:], in_=ps[:, 0:128], func=AF.Identity), reads=[pb], writes=[cb_b])
                        bst[("grp", kc_, d, g)] = dict(cb=(cb_, cb_b), x4=xs[kg % 4], ecs=[])
                        kg += 1
                    h0 = g * 8 + hb_ * 4
                    lt, ltb = lat[kq4 % 3]
                    cb4, cb4b = cB[kq4 % 4]
                    sg, sgb = seg[kq4 % 3]
                    E_, E_b = Et[kq4 % 6]
                    ec, ecb = ecB[kq4 % 8]
                    te_, te_b = te[kq4 % 3]
                    kq4 += 1
                    bst[("w",) + bt_] = (cb4, cb4b, sg, sgb, E_, E_b, ec, ecb, te_, te_b)
                    s.op("dve", ins("tensor_tensor", out=lt[:], in0=stage[:, d, :].unsqueeze(1).to_broadcast([128, 4, 128]),
                                    in1=la[:, h0:h0 + 4].unsqueeze(2).to_broadcast([128, 4, 128]), op=ALU.mult),
                         reads=[bstage, d_b], writes=[ltb])
                    ps2, pb2 = s.ps()
                    s.mm(ps2[:, 0:512], ones_f[:], lt[:].rearrange("p a b -> p (a b)"), True, True, reads=[ltb, cb_const], writes=[pb2])
                    s.op("act", ins("activation", out=cb4[:].rearrange("p a b -> p (a b)"), in_=ps2[:, 0:512], func=AF.Identity), reads=[pb2], writes=[cb4b])

                def stageA2a(bt_):
                    kc_, d, g, hb_ = bt_
                    cT, cTb = bst[("cT", kc_, d)]
                    G = bst[("grp", kc_, d, g)]
                    cb4, cb4b, sg, sgb, E_, E_b, ec, ecb, te_, te_b = bst[("w",) + bt_]
                    h0 = g * 8 + hb_ * 4
                    G["ecs"].append((ec, ecb))
                    for hh in range(4):
                        hq = h0 + hh
                        s.op("dve", ins("scalar_tensor_tensor", out=sg[:, hh, :], in0=cb4[:, hh, :], scalar=cT[:, hq:hq + 1], in1=negm[:, d, :],
                                        op0=ALU.subtract, op1=ALU.add),
                             reads=[cb4b, cTb, cbuf], writes=[sgb])
                    s.op("act", ins("activation", out=E_[:], in_=sg[:], func=AF.Exp), reads=[sgb], writes=[E_b])
                    s.op("act", ins("activation", out=ec[:], in_=cb4[:], func=AF.Exp), reads=[cb4b], writes=[ecb])

                def stageA2b(bt_):
                    kc_, d, g, hb_ = bt_
                    tbk, last, tk, la, dtv, d_b = geo(bt_)
                    x_, x_b = xts[d]
                    G = bst[("grp", kc_, d, g)]
                    x4, x4b = G["x4"]
                    cb4, cb4b, sg, sgb, E_, E_b, ec, ecb, te_, te_b = bst.pop(("w",) + bt_)
                    h0 = g * 8 + hb_ * 4
                    s.op("dve", ins("tensor_tensor", out=te_[:], in0=E_[:, :, last], in1=dtv[:, h0:h0 + 4], op=ALU.mult),
                         reads=[E_b, d_b], writes=[te_b])
                    s.op("dve", ins("tensor_tensor",
                                    out=x4[:, hb_ * 256:(hb_ + 1) * 256].rearrange("p (a b) -> p a b", b=64),
                                    in0=x_[:, tbk, h0 * 64:(h0 + 4) * 64].rearrange("p (a b) -> p a b", b=64),
                                    in1=te_[:].unsqueeze(2).to_broadcast([128, 4, 64]), op=ALU.mult),
                         reads=[x_b, te_b], writes=[x4b])
                    bst[("bat",) + bt_] = (E_, E_b, ec, ecb)

                def stageB(bt_):
                    nonlocal kw
                    kc_, d, g, hb_ = bt_
                    tbk, last, tk, la, dtv, d_b = geo(bt_)
                    bc_, bcb = bct[d]
                    x_, x_b = xts[d]
                    b_, b_b = bts[d]
                    y_, y_b = yts[d]
                    St, Stb = ST[d]
                    Sbt, Sbb = SbT[d]
                    G = bst[("grp", kc_, d, g)]
                    cb_, cb_b = G["cb"]
                    x4, x4b = G["x4"]
                    if hb_ == 0:
                        G["psy"] = s.ps("b")
                    psy, pyb = G["psy"]
                    E_, E_b, ec, ecb = bst.pop(("bat",) + bt_)
                    h0 = g * 8 + hb_ * 4
                    for hh in range(4):
                        hq = h0 + hh
                        W_, W_b = Wt[kw % 8]
                        C_, C_b = CEt[kw % 8]
                        kw += 1
                        s.op("dve", ins("scalar_tensor_tensor", out=W_[:], in0=E_[:, hh, :], scalar=dtv[:, hq:hq + 1], in1=cb_[:], op0=ALU.mult, op1=ALU.mult),
                             reads=[E_b, d_b, cb_b], writes=[W_b])
                        s.op("pool", ins("tensor_tensor", out=C_[:], in0=bc_[:, 4 + g, tk], in1=ec[:, hh, :], op=ALU.mult),
                             reads=[bcb, ecb], writes=[C_b])
                        ycol = slice((hq % 8) * 64, (hq % 8 + 1) * 64)
                        s.mm(psy[:, ycol], W_[:], x_[:, tbk, hq * 64:(hq + 1) * 64], True, False, reads=[W_b, x_b], writes=[pyb])
                        s.mm(psy[:, ycol], C_[:], Sbt[:, hq * 64:(hq + 1) * 64], False, True, reads=[C_b, Sbb[g]], writes=[pyb])
                    if hb_ == 0:
                        return
                    s.op("act", ins("activation", out=y_[:, tbk, g * 512:(g + 1) * 512], in_=psy[:, 0:512], func=AF.Identity), reads=[pyb], writes=[y_b])
                    pss, psb_ = s.ps()
                    s.mm(pss[:, 0:512], b_[:, tbk, g * 128:(g + 1) * 128], x4[:], True, True, reads=[b_b, x4b], writes=[psb_])
                    ecs = G["ecs"]
                    for hh in range(8):
                        hq = g * 8 + hh
                        ec2, ecb2 = ecs[hh // 4]
                        s.op("dve", ins("scalar_tensor_tensor",
                                        out=St[:, hq * 64:(hq + 1) * 64], in0=St[:, hq * 64:(hq + 1) * 64], scalar=ec2[:, hh % 4, last:last + 1], in1=pss[:, hh * 64:(hh + 1) * 64], op0=ALU.mult, op1=ALU.add),
                             reads=[psb_, ecb2, Stb[g]], writes=[Stb[g]])
                    s.op("pool", ins("tensor_copy", out=Sbt[:, g * 512:(g + 1) * 512], in_=St[:, g * 512:(g + 1) * 512]), reads=[Stb[g]], writes=[Sbb[g]])
                    del bst[("grp", kc_, d, g)]

                NB_ = len(batches)
                for t_ in range(NB_ + 3):
                    if t_ < NB_:
                        stageA1(batches[t_])
                    if 1 <= t_ < NB_ + 1:
                        stageA2a(batches[t_ - 1])
                    if 2 <= t_ < NB_ + 2:
                        stageA2b(batches[t_ - 2])
                    if t_ >= 3:
                        stageB(batches[t_ - 3])
                for d in range(2):
                    ti, t0, n, nb = cur[d]
                    y_, y_b = yts[d]
                    s.dma(rows_view(YD[d], 2048, t0, nb, 0, 2048), y_[:, 0:nb, :], reads=[y_b], writes=[Y_b[ti]])
            if si > 0:
                for d in range(2):
                    s.dma(st_out[d][si - 1], ST[d][0][:], reads=ST[d][1])
        chk(22)
        s.barrier()
        s.release()
        Wo = s.alloc([128, 16, 1024], BF16, "wo")
        wob = load_w(Wo, w3_o, 16)
        vec = s.alloc([128, 5, 32], F32, "vec")
        gnb = s.alloc([128, 2048], F32, "gnb")
        cbuf = Buf()
        s.dma(vec[:], ssd_vec.rearrange("a b -> (a b)").partition_broadcast(128).rearrange("p (a b) -> p a b", a=5), writes=[cbuf])
        s.dma(gnb[:], ssd_gn.partition_broadcast(128), writes=[cbuf])
        rctx = ResCtx()
        yf = (s.alloc([128, 4, 2048], BF16, "yf"), Buf())
        yb = (s.alloc([128, 4, 2048], BF16, "yb"), Buf())
        xt_ = (s.alloc([128, 4, 2048], BF16, "xt4"), Buf())
        zs = (s.alloc([128, 4, 2048], BF16, "zs"), Buf())
        yv = [(s.alloc([128, 2048], F32, "yv"), Buf()) for _ in range(2)]
        tv_ = (s.alloc([128, 2048], F32, "tv"), Buf())
        ynb = (s.alloc([128, 2048], BF16, "ynb"), Buf())
        ssq = [(s.alloc([128, 1], F32, "ssq"), Buf()) for _ in range(2)]
        oT = [(s.alloc([128, 16, 512], BF16, "oT"), Buf()) for _ in range(2)]
        kk = 0
        tv = 0
        for ti in range(NT):
            t0, n, cond, _, _ = TILES[ti]
            nb = n // 128
            s.dma(yf[0][:, 0:nb, :], rows_view(YD[0], 2048, t0, nb, 0, 2048), writes=[yf[1]])
            s.dma(yb[0][:, 0:nb, :], rows_view(YD[1], 2048, t0, nb, 0, 2048), writes=[yb[1]])
            s.dma(xt_[0][:, 0:nb, :], rows_view(XTs, 2048, t0, nb, 0, 2048), writes=[xt_[1]])
            s.dma(zs[0][:, 0:nb, :], rows_view(ZS, 2048, t0, nb, 0, 2048), writes=[zs[1]])
            o_, o_b = oT[ti % 2]
            for tbk in range(nb):
                y_, y_b = yv[kk % 2]
                sq_, sq_b = ssq[kk % 2]
                kk += 1
                t_, t_b = tv_
                s.op("dve", ins("tensor_tensor", out=y_[:], in0=yf[0][:, tbk, :], in1=yb[0][:, tbk, :], op=ALU.add), reads=[yf[1], yb[1]], writes=[y_b])
                s.op("pool", ins("tensor_tensor", out=t_[:].rearrange("p (a b) -> p a b", b=64), in0=xt_[0][:, tbk, :].rearrange("p (a b) -> p a b", b=64),
                                                              in1=vec[:, 4, :].unsqueeze(2).to_broadcast([128, 32, 64]), op=ALU.mult), reads=[xt_[1], cbuf], writes=[t_b])
                s.op("dve", ins("tensor_tensor", out=y_[:], in0=y_[:], in1=t_[:], op=ALU.add), reads=[y_b, t_b], writes=[y_b])
                s.op("dve", ins("tensor_tensor", out=y_[:], in0=y_[:], in1=zs[0][:, tbk, :], op=ALU.mult), reads=[y_b, zs[1]], writes=[y_b])
                s.op("pool", ins("memset", sq_[:], 0.0), writes=[sq_b])
                s.op("act", ins("activation", out=t_[:], in_=y_[:], func=AF.Square, accum_out=sq_[:, 0:1]), reads=[y_b, sq_b], writes=[t_b, sq_b])
                s.op("act", ins("activation", out=sq_[:], in_=sq_[:], func=AF.Sqrt, scale=1.0 / 2048, bias=EPSB[:, 0:1]), reads=[sq_b], writes=[sq_b])
                s.op("dve", ins("reciprocal", out=sq_[:], in_=sq_[:]), reads=[sq_b], writes=[sq_b])
                yn, ynb_ = ynb
                s.op("dve", ins("scalar_tensor_tensor", out=yn[:], in0=y_[:], scalar=sq_[:, 0:1], in1=gnb[:], op0=ALU.mult, op1=ALU.mult), reads=[y_b, sq_b, cbuf], writes=[ynb_])
                for c in range(16):
                    ps, pb = s.ps()
                    psv = ps[:].bitcast(BF16)
                    s.op("pe", ins("transpose", psv[:, 0:128], yn[:, c * 128:(c + 1) * 128], ident_bf[:]), reads=[ynb_, cb_const], writes=[pb])
                    if tv % 2 == 0:
                        s.op("act", ins("activation", out=o_[:, c, tbk * 128:(tbk + 1) * 128], in_=psv[:, 0:128], func=AF.Identity), reads=[pb], writes=[o_b])
                    else:
                        s.op("dve", ins("tensor_copy", out=o_[:, c, tbk * 128:(tbk + 1) * 128], in_=psv[:, 0:128]), reads=[pb], writes=[o_b])
                    tv += 1
            outproj_resid(rctx, l, 2, ti, Wo, wob, 16, o_, [o_b], Xsrc, Xsrc_b, X, X_b)

    import os
    STOP = int(os.environ.get("KSTOP", "99"))

    class _Stop(Exception):
        pass

    def chk(k):
        if STOP == k:
            raise _Stop()

    try:
        prologue()
        chk(0)
        Xsrc, Xsrc_b = xT, xT_b
        for l in range(NLAYERS):
            kind = l % 4
            if kind == 0:
                qkv_phase(l, w0_qk, 12, w0_v, 256, Xsrc, Xsrc_b, True, kout=(8, o_wk, 64), vout=o_wv)
                chk(1)
                attn_phase(l, "win", Xsrc, Xsrc_b)
                chk(2)
                outproj_phase(l, w0_o, 8, OS, None, Xsrc, Xsrc_b)
                chk(3)
            elif kind == 1:
                gla_layer(l, Xsrc, Xsrc_b)
            elif kind == 2:
                qkv_phase(l, w2_qk, 16, w2_v, 1024, Xsrc, Xsrc_b, True, kout=(8, o_dk, 128), vout=o_dv)
                attn_phase(l, "diff", Xsrc, Xsrc_b)
                outproj_phase(l, w2_o, 8, OS, None, Xsrc, Xsrc_b)
            else:
                ssd_layer(l, Xsrc, Xsrc_b)
            Xsrc, Xsrc_b = X, X_b
            ffn(l, Xsrc, Xsrc_b)
            chk(10 + l)
        s.barrier()
        s.release()
        nctx = NormCtx()
        for ti in range(NT):
            norm_tile(nctx, 0, 0, ti, Xsrc, Xsrc_b, final_out=yT)
    except _Stop:
        pass
    s.barrier()
    counts = s.emit()
    return nc, counts

def _fm(w):
    K, N = w.shape
    return np.ascontiguousarray(w.reshape(K // 128, 128, N).transpose(1, 0, 2))


def _pc(v):
    return np.ascontiguousarray(v.reshape(-1, 128).T)


def _consts():
    f32 = np.float32
    t = np.arange(LS)
    row = (t // 64).astype(f32)
    col = (t % 64).astype(f32)
    inv = (10000.0 ** (-np.arange(0, 32, 2, dtype=f32) / 32)).astype(f32)
    cos = np.zeros((128, LS), f32)
    sin = np.zeros((128, LS), f32)
    perm = np.zeros((128, 128), f32)
    for p in range(128):
        d = p % 64
        axis = d // 32
        idx = d % 16
        second = (d % 32) >= 16
        ang = (row if axis == 0 else col) * inv[idx]
        cos[p] = np.cos(ang)
        sin[p] = np.sin(ang) * (1.0 if second else -1.0)
        partner = p + 16 if not second else p - 16
        perm[partner, p] = 1.0
    j = np.arange(128)[:, None]
    i = np.arange(512)[None, :]
    wmask = np.zeros((128, 6, 512), f32)
    for mi in range(6):
        o = mi - 1
        wmask[:, mi, :] = (np.abs(i - (o * 128 + j)) <= 128).astype(f32)
    tri = np.zeros((128, 4, 128), f32)
    jj = np.arange(128)[:, None]
    ii = np.arange(128)[None, :]
    tri[:, 0, :] = (jj <= ii)
    tri[:, 1, :] = (jj >= ii)
    tri[0:64, 2, 0:64] = (jj[0:64] <= ii[:, 0:64])
    tri[0:64, 3, 0:64] = (jj[0:64] >= ii[:, 0:64])
    m64 = np.ones((128, 512), f32)
    m64[:, ::64] = 0.0
    return dict(rcos=cos, rsin=sin, perm=perm, wmask=wmask, tri=tri, m64=m64)


_PROG = {}


def _get_prog(nl):
    if nl not in _PROG:
        _PROG[nl] = build_program(nl)
    return _PROG[nl]


def kernel(NLAYERS=4, **inp):
    f32 = np.float32
    g = {k: np.asarray(v) for k, v in inp.items()}
    nc, counts = _get_prog(NLAYERS)
    common = dict(_consts())
    common["ada_w"] = np.ascontiguousarray(g["ada_w"].reshape(4, 8, 128, 6144).transpose(0, 2, 1, 3))
    common["ada_b"] = np.ascontiguousarray(g["ada_b"].reshape(4, 48, 128).transpose(2, 0, 1))
    common["nrm"] = np.ascontiguousarray(np.stack([g["norm_mix"], g["norm_ffn"]], axis=1).reshape(4, 2, 8, 128).transpose(3, 0, 1, 2))
    common["fnorm"] = _pc(g["final_norm"])
    common["w_up"] = np.ascontiguousarray(g["ffn_w_up"].reshape(4, 8, 128, 5632).transpose(0, 2, 1, 3))
    common["w_dn"] = np.ascontiguousarray(g["ffn_w_down"].reshape(4, 22, 128, 1024).transpose(0, 2, 1, 3))
    common["ffn_cw"] = np.ascontiguousarray(g["ffn_conv_w"].reshape(4, 3, 44, 128).transpose(3, 0, 1, 2))
    common["ffn_cb"] = np.ascontiguousarray(g["ffn_conv_b"].reshape(4, 44, 128).transpose(2, 0, 1))
    wq = g["win_w_qkv"][0]
    qcols = wq[:, 0:1024]
    kcols = wq[:, 1024:1280]
    vcols = wq[:, 1280:1536]
    kdup = np.concatenate([np.concatenate([kcols[:, h * 64:(h + 1) * 64]] * 2, axis=1) for h in range(4)], axis=1)
    common["w0_qk"] = _fm(np.concatenate([qcols, kdup], axis=1))
    common["w0_v"] = _fm(vcols)
    common["w0_o"] = _fm(g["win_w_o"][0])
    common["sink"] = np.ascontiguousarray(g["win_sink"][0])
    wg = g["gla_w_qkvr"][0]
    common["w1_qk"] = _fm(wg[:, 0:1024])
    common["w1_v"] = _fm(wg[:, 1024:2048])
    common["w1_r"] = _fm(wg[:, 2048:3072])
    common["w1_g1"] = _fm(np.concatenate([g["gla_w_gf1"][0], g["gla_w_gb1"][0]], axis=1))
    g2 = np.zeros((32, 1024), f32)
    g2[0:16, 0:512] = g["gla_w_gf2"][0]
    g2[16:32, 512:1024] = g["gla_w_gb2"][0]
    common["w1_g2"] = g2
    common["w1_gb"] = _pc(np.concatenate([g["gla_b_gf"][0], g["gla_b_gb"][0]]))
    common["w1_gn"] = _pc(g["gla_norm"][0])
    common["w1_o"] = _fm(g["gla_w_o"][0])
    wd = g["diff_w_qkv"][0]
    common["w2_qk"] = _fm(wd[:, 0:2048])
    common["w2_v"] = _fm(wd[:, 2048:3072])
    common["w2_o"] = _fm(g["diff_w_o"][0])
    common["lqk"] = np.ascontiguousarray(np.stack([g["diff_lq1"][0], g["diff_lk1"][0], g["diff_lq2"][0], g["diff_lk2"][0]]))
    common["w2_gn"] = np.ascontiguousarray(g["diff_norm"][0].reshape(128, 1))
    common.update(_host_ssd_common(g))
    in_maps = []
    for b in range(8):
        m = dict(common)
        xs = g["x_sample"][b]
        xp = g["x_prompt"][2 * b:2 * b + 2].reshape(512, 1024)
        xa = np.concatenate([xs, xp], axis=0)
        m["xT"] = np.ascontiguousarray(xa.T.reshape(8, 128, T).transpose(1, 0, 2))
        cond = np.stack([g["c"][b], g["c_ctx"]], axis=1)
        m["condT"] = np.ascontiguousarray(cond.reshape(8, 128, 2).transpose(1, 0, 2))
        ck = g["cache_win_k"][b, 0]
        kT = ck.transpose(2, 1, 0)
        m["ck0"] = np.ascontiguousarray(np.concatenate([kT, kT], axis=0))
        m["cv0"] = np.ascontiguousarray(g["cache_win_v"][b, 0].reshape(512, 256))
        m["s1f"] = np.ascontiguousarray(g["state_gla_fwd"][b, 0].transpose(1, 0, 2))
        m["s1b"] = np.ascontiguousarray(g["state_gla_bwd"][b, 0].transpose(1, 0, 2))
        dk = g["cache_diff_k"][b, 0]
        m["ck2"] = np.ascontiguousarray(dk.transpose(2, 3, 1, 0).reshape(128, 8, 512))
        m["cv2"] = np.ascontiguousarray(g["cache_diff_v"][b, 0].reshape(512, 1024))
        m.update(_host_ssd_core(g, b))
        in_maps.append(m)
    import os
    ncores = int(os.environ.get("KCORES", "8"))
    ktrace = os.environ.get("KTRACE", "") == "1"
    res = run_bass_kernel_spmd(nc, in_maps[:ncores], core_ids=list(range(ncores)), trace=ktrace) if ktrace else run_bass_kernel_spmd(nc, in_maps[:ncores], core_ids=list(range(ncores)))
    if ktrace:
        print("EXEC_TIME_NS", res.exec_time_ns)
    R = list(res.results) + [res.results[0]] * (8 - ncores)
    y_prompt = np.zeros((16, 256, 1024), f32)
    y_sample = np.zeros((8, 4096, 1024), f32)
    win_k = np.zeros((16, 1, 256, 4, 64), f32)
    win_v = np.zeros((16, 1, 256, 4, 64), f32)
    gla_f = np.zeros((16, 1, 4, 128, 256), f32)
    gla_b = np.zeros((16, 1, 4, 128, 256), f32)
    diff_k = np.zeros((16, 1, 256, 8, 2, 64), f32)
    diff_v = np.zeros((16, 1, 256, 8, 128), f32)
    ssd_f = np.zeros((16, 1, 32, 64, 128), f32)
    ssd_b = np.zeros((16, 1, 32, 64, 128), f32)
    for b in range(8):
        r = R[b]
        y = r["yT"].transpose(1, 0, 2).reshape(1024, T).T
        y_sample[b] = y[0:4096]
        y_prompt[2 * b:2 * b + 2] = y[4096:].reshape(2, 256, 1024)
        wk = r["o_wk"]
        win_k[2 * b:2 * b + 2, 0] = wk.transpose(2, 0, 1).reshape(2, 256, 4, 64)
        win_v[2 * b:2 * b + 2, 0] = r["o_wv"].reshape(2, 256, 4, 64)
        gla_f[2 * b:2 * b + 2, 0] = r["o_gf"].transpose(0, 2, 1, 3)
        gla_b[2 * b:2 * b + 2, 0] = r["o_gb"].transpose(0, 2, 1, 3)
        dk = r["o_dk"]
        diff_k[2 * b:2 * b + 2, 0] = dk.transpose(2, 0, 1).reshape(2, 256, 8, 2, 64)
        diff_v[2 * b:2 * b + 2, 0] = r["o_dv"].reshape(2, 256, 8, 128)
        _host_ssd_out(r, b, ssd_f, ssd_b)
    return (y_prompt, y_sample, win_k, win_v, gla_f, gla_b, diff_k, diff_v, ssd_f, ssd_b)


def _host_ssd_common(g):
    w = g["ssd_w_in"][0]
    out = {}
    out["w3_z"] = _fm(w[:, 0:2048])
    out["w3_xbc"] = _fm(w[:, 2048:5120])
    out["w3_dt"] = _fm(w[:, 5120:5184])
    out["w3_o"] = _fm(g["ssd_w_out"][0])
    out["ssd_cw"] = np.ascontiguousarray(g["ssd_conv_w"][0].reshape(3, 24, 128).transpose(2, 0, 1))
    out["ssd_cb"] = _pc(g["ssd_conv_b"][0])
    out["ssd_vec"] = np.ascontiguousarray(np.stack([g["ssd_a_log_f"][0], g["ssd_a_log_b"][0], g["ssd_dt_bias_f"][0], g["ssd_dt_bias_b"][0], g["ssd_d"][0]]))
    out["ssd_gn"] = np.ascontiguousarray(g["ssd_norm"][0])
    return out


def _host_ssd_core(g, b):
    return {"s3f": np.ascontiguousarray(g["state_ssd_fwd"][b, 0].transpose(2, 0, 1).reshape(128, 2048)),
            "s3b": np.ascontiguousarray(g["state_ssd_bwd"][b, 0].transpose(2, 0, 1).reshape(128, 2048))}


def _host_ssd_out(r, b, ssd_f, ssd_b):
    ssd_f[2 * b:2 * b + 2, 0] = r["o_sf"].reshape(2, 128, 32, 64).transpose(0, 2, 3, 1)
    ssd_b[2 * b:2 * b + 2, 0] = r["o_sb"].reshape(2, 128, 32, 64).transpose(0, 2, 3, 1)
```

```python
import numpy as np
import concourse.bass as bass
import concourse.mybir as mybir
from concourse.bass_utils import run_bass_kernel_spmd

F32 = mybir.dt.float32
BF16 = mybir.dt.bfloat16
AF = mybir.ActivationFunctionType
ALU = mybir.AluOpType
AX = mybir.AxisListType
AP = bass.AP

SB_BASE = 16512
SB_TOP = 229344
EPOCH = 30000


class Buf:
    __slots__ = ("w", "rs", "name")

    def __init__(self, name=""):
        self.w = None
        self.rs = []
        self.name = name


class Op:
    __slots__ = ("eng", "fn", "deps", "dma", "ms", "sem", "val", "need", "prev", "grp")

    def __init__(self, eng, fn, dma):
        self.eng = eng
        self.fn = fn
        self.dma = dma
        self.deps = []
        self.ms = None
        self.sem = None
        self.val = None
        self.need = False
        self.prev = None
        self.grp = None


class Sched:
    ENGS = ("pe", "act", "dve", "pool", "sp")

    def __init__(self, nc):
        self.nc = nc
        self.ops = {e: [] for e in self.ENGS}
        self.dmas_since_barrier = []
        self.all_dmas = []
        self.nps = 0
        self.nacc = 0
        self.psum = []
        for i in range(8):
            t = nc.alloc_psum_tensor("psb%d" % i, [128, 512], F32)
            self.psum.append((t, Buf("ps%d" % i)))
        self.sb_off = SB_BASE
        self.sb_mark = SB_BASE
        self.nalloc = 0

    def alloc(self, shape, dtype, name=None):
        nbytes = int(np.prod(shape[1:])) * (4 if dtype == F32 else 2)
        nbytes = (nbytes + 63) // 64 * 64
        off = self.sb_off
        assert off + nbytes <= SB_TOP, "SBUF overflow %d" % (off + nbytes - SB_TOP)
        self.sb_off += nbytes
        self.nalloc += 1
        t = self.nc.alloc_sbuf_tensor_at("sb%d_%s" % (self.nalloc, name or "t"), list(shape), dtype, offset=off)
        return t

    def mark(self):
        self.sb_mark = self.sb_off

    def release(self):
        self.sb_off = self.sb_mark

    def ps(self, pool="a"):
        if pool == "a":
            t, b = self.psum[self.nps % 6]
            self.nps += 1
        else:
            t, b = self.psum[6 + self.nacc % 2]
            self.nacc += 1
        return t, b

    def op(self, eng, fn, reads=(), writes=(), dma=False):
        o = Op(eng, fn, dma)
        deps = {}
        for b in reads:
            d = b.w
            if d is not None:
                if (not dma) and (not d.dma) and d.eng == eng and eng == "pe":
                    continue
                deps[id(d)] = d
        for b in writes:
            cand = list(b.rs)
            if b.w is not None:
                cand.append(b.w)
            for d in cand:
                if d is o:
                    continue
                if (not dma) and (not d.dma) and d.eng == eng:
                    continue
                deps[id(d)] = d
        o.deps = list(deps.values())
        for b in reads:
            if not dma:
                b.rs = [r for r in b.rs if r.dma or r.eng != eng]
            b.rs.append(o)
        for b in writes:
            b.w = o
            b.rs = []
        self.ops[eng].append(o)
        if dma:
            self.dmas_since_barrier.append(o)
            self.all_dmas.append(o)
        return o

    def barrier(self):
        lasts = []
        for e in self.ENGS:
            for o in reversed(self.ops[e]):
                if not o.dma and o.fn is not None:
                    lasts.append(o)
                    break
        deps = lasts + self.dmas_since_barrier
        self.dmas_since_barrier = []
        for e in self.ENGS:
            o = Op(e, None, False)
            o.deps = list(deps)
            self.ops[e].append(o)

    def dma(self, out, in_, reads=(), writes=(), eng=None):
        if eng is None:
            eng = "pool" if type(out.tensor).__name__.startswith("DRam") else "sp"
        return self.op(eng, lambda e: e.dma_start(out=out, in_=in_), reads, writes, dma=True)

    def mm(self, out, lhsT, rhs, start, stop, reads=(), writes=(), grp=None):
        o = self.op("pe", lambda e: e.matmul(out, lhsT, rhs, start=start, stop=stop), reads, writes)
        o.grp = grp
        return o

    def emit(self):
        nc = self.nc
        for e in self.ENGS:
            for o in self.ops[e]:
                for d in o.deps:
                    d.need = True
        sem_ctx = []
        import contextlib
        with contextlib.ExitStack() as st:
            tl = {}
            for e in ("pe", "act", "dve", "pool"):
                tl[e] = [st.enter_context(nc.semaphore("tl_%s_%d" % (e, i))) for i in range(5)]
            npool = {"sp": 36, "pool": 36, "act": 8}
            dpool = {e: [st.enter_context(nc.semaphore("dq_%s_%d" % (e, i))) for i in range(n)] for e, n in npool.items()}
            for e in self.ENGS:
                m = 0
                k = 0
                for o in self.ops[e]:
                    if o.fn is None:
                        continue
                    if o.dma:
                        P = len(dpool[e])
                        o.sem = dpool[e][k % P]
                        o.val = 16 * (k // P + 1)
                        k += 1
                    elif o.need:
                        o.sem = tl[e][m // EPOCH]
                        o.val = m % EPOCH + 1
                        m += 1
                assert m < EPOCH * 5, (e, m)
            final = {}
            for e, n in npool.items():
                for o in self.ops[e]:
                    if o.dma:
                        final[id(o.sem)] = (o.sem, o.val)

            def stream(e, eng):
                seen = {}

                def wait(sem, val):
                    if seen.get(id(sem), 0) < val:
                        eng.wait_ge(sem, val)
                        seen[id(sem)] = val

                ops_ = self.ops[e]
                i_ = 0
                while i_ < len(ops_):
                    o = ops_[i_]
                    j_ = i_ + 1
                    if o.grp is not None:
                        while j_ < len(ops_) and ops_[j_].grp == o.grp:
                            j_ += 1
                    for k_ in range(i_, j_):
                        for d in ops_[k_].deps:
                            wait(d.sem, d.val)
                    for k_ in range(i_, j_):
                        o = ops_[k_]
                        if o.fn is None:
                            continue
                        if o.dma and o.val > 16:
                            wait(o.sem, o.val - 16)
                        ins_ = o.fn(eng)
                        if o.dma:
                            ins_.then_inc(o.sem, 16)
                        elif o.need:
                            ins_.then_inc(o.sem, 1)
                    i_ = j_
                if e == "sp":
                    for sem, val in final.values():
                        wait(sem, val)

            with nc.Block() as block:
                @block.tensor
                def _(eng):
                    stream("pe", eng)

                @block.scalar
                def _(eng):
                    stream("act", eng)

                @block.vector
                def _(eng):
                    stream("dve", eng)

                @block.gpsimd
                def _(eng):
                    stream("pool", eng)

                @block.sync
                def _(eng):
                    stream("sp", eng)
        return {e: len(v) for e, v in self.ops.items()}


def ins(method, *a, **kw):
    return lambda e: getattr(e, method)(*a, **kw)


T = 4608
LS = 4096
TILES = [(i * 512, 512, 0, 0, 4096) for i in range(8)] + [(4096, 256, 1, 4096, 4352), (4352, 256, 1, 4352, 4608)]
EPS = 1e-6
DFF = 2816
LAM_INIT = 0.8 - 0.6 * float(np.exp(-0.3 * 2))


def build_program(NLAYERS=4, dbg=False):
    nc = bass.Bass("TRN2", target_bir_lowering=False)
    s = Sched(nc)
    I = {}
    O = {}

    def din(name, shape):
        I[name] = nc.dram_tensor(name, list(shape), F32, kind="ExternalInput").ap()
        return I[name]

    def dout(name, shape):
        O[name] = nc.dram_tensor(name, list(shape), F32, kind="ExternalOutput").ap()
        return O[name]

    def dscr(name, shape, dt):
        return nc.dram_tensor(name, list(shape), dt).ap()

    xT = din("xT", [128, 8, T])
    condT = din("condT", [128, 8, 2])
    ada_w = din("ada_w", [4, 128, 8, 6144])
    ada_b = din("ada_b", [128, 4, 48])
    nrm = din("nrm", [128, 4, 2, 8])
    fnorm = din("fnorm", [128, 8])
    w_up = din("w_up", [4, 128, 8, 5632])
    w_dn = din("w_dn", [4, 128, 22, 1024])
    ffn_cw = din("ffn_cw", [128, 4, 3, 44])
    ffn_cb = din("ffn_cb", [128, 4, 44])
    perm_in = din("perm", [128, 128])
    rcos = din("rcos", [128, LS])
    rsin = din("rsin", [128, LS])
    wmask_in = din("wmask", [128, 6, 512])
    tri_in = din("tri", [128, 4, 128])
    m64_in = din("m64", [128, 512])
    w0_qk = din("w0_qk", [128, 8, 1536])
    w0_v = din("w0_v", [128, 8, 256])
    w0_o = din("w0_o", [128, 8, 1024])
    sink_in = din("sink", [16])
    ck0 = din("ck0", [128, 4, 512])
    cv0 = din("cv0", [512, 256])
    w1_qk = din("w1_qk", [128, 8, 1024])
    w1_v = din("w1_v", [128, 8, 1024])
    w1_r = din("w1_r", [128, 8, 1024])
    w1_g1 = din("w1_g1", [128, 8, 32])
    w1_g2 = din("w1_g2", [32, 1024])
    w1_gb = din("w1_gb", [128, 8])
    w1_gn = din("w1_gn", [128, 2])
    w1_o = din("w1_o", [128, 8, 1024])
    s1f = din("s1f", [128, 4, 256])
    s1b = din("s1b", [128, 4, 256])
    w2_qk = din("w2_qk", [128, 8, 2048])
    w2_v = din("w2_v", [128, 8, 1024])
    w2_o = din("w2_o", [128, 8, 1024])
    lqk = din("lqk", [4, 64])
    w2_gn = din("w2_gn", [128, 1])
    ck2 = din("ck2", [128, 8, 512])
    cv2 = din("cv2", [512, 1024])

    w3_z = din("w3_z", [128, 8, 2048])
    w3_xbc = din("w3_xbc", [128, 8, 3072])
    w3_dt = din("w3_dt", [128, 8, 64])
    w3_o = din("w3_o", [128, 16, 1024])
    ssd_cw = din("ssd_cw", [128, 3, 24])
    ssd_cb = din("ssd_cb", [128, 24])
    ssd_vec = din("ssd_vec", [5, 32])
    ssd_gn = din("ssd_gn", [2048])
    s3f = din("s3f", [128, 2048])
    s3b = din("s3b", [128, 2048])

    yT = dout("yT", [128, 8, T])
    o_sf = dout("o_sf", [2, 128, 2048])
    o_sb = dout("o_sb", [2, 128, 2048])
    o_wk = dout("o_wk", [4, 64, 512])
    o_wv = dout("o_wv", [512, 256])
    o_gf = dout("o_gf", [2, 128, 4, 256])
    o_gb = dout("o_gb", [2, 128, 4, 256])
    o_dk = dout("o_dk", [8, 128, 512])
    o_dv = dout("o_dv", [512, 1024])

    X = dscr("X", [128, 8, T], F32)
    U = dscr("U", [128, 44, T], BF16)
    QK = dscr("QK", [128, 16, T], BF16)
    VS = dscr("VS", [T, 1024], BF16)
    OS = dscr("OS", [128, 8, T], BF16)
    OS2 = dscr("OS2", [128, 8, T], BF16)
    RS = dscr("RS", [128, 8, T], BF16)
    QD = dscr("QD", [2, 128, 4, T], BF16)
    KD = dscr("KD", [2, 128, 4, T], BF16)
    KR = dscr("KR", [T, 8, 128], BF16)
    ZS = dscr("ZS", [T, 2048], BF16)
    DTS = dscr("DTS", [T, 128], F32)
    XTs = dscr("XTs", [T, 2048], BF16)
    BTs = dscr("BTs", [T, 512], BF16)
    YD = [dscr("YD0", [T, 2048], BF16), dscr("YD1", [T, 2048], BF16)]

    NT = len(TILES)
    import os
    STOP = int(os.environ.get("KSTOP", "99"))

    def rows_view(dr, rowlen, r0, nb, c0, w):
        return AP(dr.tensor, r0 * rowlen + c0, [[rowlen, 128], [128 * rowlen, nb], [1, w]])

    def kr_view(r0, nb, j0, nj):
        return AP(KR.tensor, r0 * 1024 + j0 * 128, [[1024, 128], [128 * 1024, nb], [128, nj], [1, 128]])
    import os
    KQ = os.environ.get("KQ", "")

    def tb(name):
        return [Buf(name + str(i)) for i in range(NT)]

    xT_b = tb("xT")
    X_b = tb("X")

    MODS = s.alloc([128, 4, 6, 8, 2], F32, "mods")
    GS = s.alloc([128, 4, 2, 8, 2], F32, "gs")
    NRM = s.alloc([128, 4, 2, 8], F32, "nrm")
    FNRM = s.alloc([128, 8], F32, "fnrm")
    ones_bf = s.alloc([128, 128], BF16, "ones")
    ones_f = s.alloc([128, 128], F32, "onesf")
    ident_bf = s.alloc([128, 128], BF16, "ident")
    perm_bf = s.alloc([128, 128], BF16, "perm")
    tri_bf = s.alloc([128, 4, 128], BF16, "tri")
    m64 = s.alloc([128, 512], F32, "m64")
    cb_const = Buf("consts")
    stage = s.alloc([128, 4, 128], F32, "stage")
    bstage = Buf()

    s.dma(NRM[:], nrm, writes=[cb_const])
    s.dma(FNRM[:], fnorm, writes=[cb_const])
    s.dma(m64[:], m64_in, writes=[cb_const])
    s.op("pool", ins("memset", ones_f[:], 1.0), writes=[cb_const])
    s.op("dve", ins("tensor_copy", out=ones_bf[:], in_=ones_f[:]), reads=[cb_const], writes=[cb_const])
    s.dma(stage[:, 0, :], perm_in, writes=[bstage])
    s.op("dve", ins("tensor_copy", out=perm_bf[:], in_=stage[:, 0, :]), reads=[bstage], writes=[cb_const])
    s.op("pool", ins("memset", stage[:, 1, :], 1.0), reads=[], writes=[bstage])
    s.op("pool", ins("affine_select", out=stage[:, 1, :], in_=stage[:, 1, :], pattern=[[-1, 128]], compare_op=ALU.is_equal, fill=0.0, base=0, channel_multiplier=1), reads=[bstage], writes=[bstage])
    s.op("dve", ins("tensor_copy", out=ident_bf[:], in_=stage[:, 1, :]), reads=[bstage], writes=[cb_const])
    s.barrier()
    s.dma(stage[:], tri_in, writes=[bstage])
    s.op("dve", ins("tensor_copy", out=tri_bf[:], in_=stage[:]), reads=[bstage], writes=[cb_const])
    s.mark()

    def mod_ap(l, k, c, cond):
        return MODS[:, l, k, c, cond:cond + 1]

    def prologue():
        s.barrier()
        s.release()
        sc = s.alloc([128, 8, 2], F32, "sc")
        scb = Buf()
        adb = s.alloc([128, 4, 48], F32, "adb")
        adbb = Buf()
        s.dma(sc[:], condT, writes=[scb])
        s.dma(adb[:], ada_b, writes=[adbb])
        s.op("act", ins("activation", out=sc[:], in_=sc[:], func=AF.Silu), reads=[scb], writes=[scb])
        wb = [s.alloc([128, 8, 1024], F32, "adaw%d" % i) for i in range(2)]
        wbb = [[Buf() for _ in range(2)] for _ in range(2)]
        it = 0
        for l in range(NLAYERS):
            for g in range(6):
                w = wb[it % 2]
                bb = wbb[it % 2]
                for hh in range(2):
                    s.dma(w[:, hh * 4:(hh + 1) * 4, :], ada_w[l, :, hh * 4:(hh + 1) * 4, g * 1024:(g + 1) * 1024], writes=[bb[hh]], eng=("sp" if hh == 0 else "act"))
                ps, pb = s.ps()
                for cc in range(8):
                    for kc in range(8):
                        s.mm(ps[:, cc * 2:cc * 2 + 2], w[:, kc, cc * 128:(cc + 1) * 128], sc[:, kc, :], kc == 0, kc == 7, reads=[bb[kc // 4], scb], writes=[pb])
                s.op("dve", ins("tensor_tensor",
                    out=MODS[:, l, g, :, :], in0=ps[:, 0:16].rearrange("p (c t) -> p c t", t=2),
                    in1=adb[:, l, g * 8:(g + 1) * 8].unsqueeze(2).to_broadcast([128, 8, 2]), op=ALU.add),
                    reads=[pb, adbb], writes=[cb_const])
                it += 1
            for which in range(2):
                k = 1 if which == 0 else 4
                s.op("dve", ins("scalar_tensor_tensor",
                    out=GS[:, l, which, :, :], in0=MODS[:, l, k, :, :], scalar=1.0,
                    in1=NRM[:, l, which, :].unsqueeze(2).to_broadcast([128, 8, 2]), op0=ALU.add, op1=ALU.mult),
                    reads=[cb_const], writes=[cb_const])

    def load_w(dst, src, nk, split=1):
        bufs = []
        for kc in range(nk):
            b = Buf()
            s.dma(dst[:, kc, :], src[:, kc, :], writes=[b], eng="pool")
            bufs.append(b)
        return bufs

    class NormCtx:
        def __init__(self, nbuf=2):
            self.nbuf = nbuf
            self.xt = [(s.alloc([128, 8, 512], F32, "nxt"), Buf()) for _ in range(nbuf)]
            self.h = [(s.alloc([128, 8, 512], BF16, "nh"), Buf()) for _ in range(nbuf)]
            self.sq = (s.alloc([128, 8, 512], BF16, "nsq"), Buf())
            self.tmp = (s.alloc([128, 8, 512], F32, "ntmp"), Buf())
            self.rstd = [(s.alloc([128, 512], F32, "nrs"), Buf()) for _ in range(2)]
            self.k = 0

    def norm_tile(ctx, l, which, ti, Xsrc, Xsrc_b, final_out=None):
        t0, n, cond, _, _ = TILES[ti]
        k = ctx.k
        ctx.k += 1
        xt, xtb = ctx.xt[k % ctx.nbuf]
        h, hb = ctx.h[k % ctx.nbuf]
        sq, sqb = ctx.sq
        tmp, tmpb = ctx.tmp
        rstd, rb = ctx.rstd[k % 2]
        s.dma(xt[:, :, :n], Xsrc[:, :, t0:t0 + n], reads=[Xsrc_b[ti]], writes=[xtb])
        s.op("act", ins("activation", out=sq[:, :, :n], in_=xt[:, :, :n], func=AF.Square), reads=[xtb], writes=[sqb])
        ps, pb = s.ps()
        for c in range(8):
            s.mm(ps[:, :n], ones_bf[:], sq[:, c, :n], c == 0, c == 7, reads=[sqb, cb_const], writes=[pb])
        s.op("act", ins("activation", out=rstd[:, :n], in_=ps[:, :n], func=AF.Sqrt, scale=1.0 / 1024, bias=EPSB[:, 0:1]), reads=[pb], writes=[rb])
        s.op("dve", ins("reciprocal", out=rstd[:, :n], in_=rstd[:, :n]), reads=[rb], writes=[rb])
        s.op("dve", ins("tensor_tensor", out=tmp[:, :, :n], in0=xt[:, :, :n], in1=rstd[:, :n].unsqueeze(1).to_broadcast([128, 8, n]), op=ALU.mult),
             reads=[xtb, rb], writes=[tmpb])
        if final_out is None:
            for c in range(8):
                s.op("act", ins("activation", out=h[:, c, :n], in_=tmp[:, c, :n], func=AF.Identity,
                                                          scale=GS[:, l, which, c, cond:cond + 1], bias=mod_ap(l, 0 if which == 0 else 3, c, cond)),
                     reads=[tmpb, cb_const], writes=[hb])
            return h, hb
        else:
            for c in range(8):
                s.op("act", ins("activation", out=xt[:, c, :n], in_=tmp[:, c, :n], func=AF.Identity, scale=FNRM[:, c:c + 1]),
                     reads=[tmpb, cb_const], writes=[xtb])
            s.dma(final_out[:, :, t0:t0 + n], xt[:, :, :n], reads=[xtb])
            return None, None

    EPSB = s.alloc([128, 1], F32, "epsb")
    s.op("pool", ins("memset", EPSB[:], EPS), writes=[cb_const])
    s.mark()

    class ResCtx:
        def __init__(self):
            self.xo = [(s.alloc([128, 8, 512], F32, "xo"), Buf()) for _ in range(2)]
            self.k = 0

    def outproj_resid(rctx, l, gate_k, ti, W, wbufs, nk, rhs, rhsbufs, Xsrc, Xsrc_b, Xdst, Xdst_b):
        t0, n, cond, _, _ = TILES[ti]
        xo, xob = rctx.xo[rctx.k % 2]
        rctx.k += 1
        s.dma(xo[:, :, :n], Xsrc[:, :, t0:t0 + n], reads=[Xsrc_b[ti]], writes=[xob])
        for dc in range(8):
            ps, pb = s.ps()
            for kc in range(nk):
                s.mm(ps[:, :n], W[:, kc, dc * 128:(dc + 1) * 128], rhs[:, kc, :n], kc == 0, kc == nk - 1,
                     reads=[wbufs[kc]] + list(rhsbufs), writes=[pb])
            s.op("dve", ins("scalar_tensor_tensor", out=xo[:, dc, :n], in0=ps[:, :n], scalar=mod_ap(l, gate_k, dc, cond),
                                                                      in1=xo[:, dc, :n], op0=ALU.mult, op1=ALU.add),
                 reads=[pb, xob, cb_const], writes=[xob])
        s.dma(Xdst[:, :, t0:t0 + n], xo[:, :, :n], reads=[xob], writes=[Xdst_b[ti]])

    def ffn(l, Xsrc, Xsrc_b):
        s.barrier()
        s.release()
        Wup = s.alloc([128, 8, 5632], BF16, "wup")
        wb = load_w(Wup, w_up[l], 8)
        nctx = NormCtx()
        ub = [(s.alloc([128, 11, 512], BF16, "ub"), [Buf() for _ in range(11)]) for _ in range(2)]
        U_b = [[Buf() for _ in range(4)] for _ in range(NT)]
        ku = 0
        ev = 0
        for ti in range(NT):
            t0, n, cond, _, _ = TILES[ti]
            h, hb = norm_tile(nctx, l, 1, ti, Xsrc, Xsrc_b)
            for grp in range(4):
                ut, utb = ub[ku % 2]
                ku += 1
                for cc in range(11):
                    col = grp * 11 + cc
                    ps, pb = s.ps()
                    for kc in range(8):
                        s.mm(ps[:, :n], Wup[:, kc, col * 128:(col + 1) * 128], h[:, kc, :n], kc == 0, kc == 7, reads=[wb[kc], hb], writes=[pb])
                    if ev % 2 == 0:
                        s.op("act", ins("activation", out=ut[:, cc, :n], in_=ps[:, :n], func=AF.Identity), reads=[pb], writes=[utb[cc]])
                    else:
                        s.op("dve", ins("tensor_copy", out=ut[:, cc, :n], in_=ps[:, :n]), reads=[pb], writes=[utb[cc]])
                    ev += 1
                s.dma(U[:, grp * 11:(grp + 1) * 11, t0:t0 + n], ut[:, :, :n], reads=utb, writes=[U_b[ti][grp]])
        s.barrier()
        s.release()
        Wd = s.alloc([128, 22, 1024], BF16, "wd")
        wdb = load_w(Wd, w_dn[l], 22)
        cw = s.alloc([128, 3, 44], F32, "cw")
        cbias = s.alloc([128, 44], F32, "cbias")
        cwb = Buf()
        s.dma(cw[:], ffn_cw[:, l, :, :], writes=[cwb])
        s.dma(cbias[:], ffn_cb[:, l, :], writes=[cwb])
        ug = s.alloc([128, 44, 514], BF16, "ug")
        ugb = [Buf() for _ in range(4)]
        at = [(s.alloc([128, 22, 512], BF16, "at"), Buf()) for _ in range(2)]
        tmps = [[(s.alloc([128, 512], F32, "ft"), Buf()) for _ in range(3)] for _ in range(4)]
        rctx = ResCtx()
        kp = 0
        for ti in range(NT):
            t0, n, cond, s0, s1 = TILES[ti]
            lo = (t0 - 1) >= s0
            hi = (t0 + n) < s1
            for grp in range(4):
                gs_ = slice(grp * 11, (grp + 1) * 11)
                if not lo:
                    s.op("pool", ins("memset", ug[:, gs_, 0:1], 0.0), writes=[ugb[grp]])
                if not hi:
                    s.op("pool", ins("memset", ug[:, gs_, n + 1:n + 2], 0.0), writes=[ugb[grp]])
                c0 = 0 if lo else 1
                c1 = n + 2 if hi else n + 1
                rd = [U_b[ti][grp]]
                if lo:
                    rd.append(U_b[ti - 1][grp])
                if hi:
                    rd.append(U_b[ti + 1][grp])
                s.dma(ug[:, gs_, c0:c1], U[:, gs_, t0 - 1 + c0:t0 - 1 + c1], reads=rd, writes=[ugb[grp]])
            a, ab = at[ti % 2]
            pst = {}

            def f2s1(cc):
                nonlocal kp
                res = []
                for half in range(2):
                    col = cc + 22 * half
                    tt, ttb = tmps[kp % 4][half]
                    grp = col // 11
                    s.op("act", ins("activation", out=tt[:, :n], in_=ug[:, col, 1:n + 1], func=AF.Identity,
                                    scale=cw[:, 1, col:col + 1], bias=cbias[:, col:col + 1]),
                         reads=[ugb[grp], cwb], writes=[ttb])
                    res.append((tt, ttb, col, grp))
                sg, sgb = tmps[kp % 4][2]
                kp += 1
                pst[cc] = (res, sg, sgb)

            def f2s2(cc):
                res, sg, sgb = pst[cc]
                for (tt, ttb, col, grp) in res:
                    s.op("dve", ins("scalar_tensor_tensor", out=tt[:, :n], in0=ug[:, col, 0:n], scalar=cw[:, 0, col:col + 1],
                                    in1=tt[:, :n], op0=ALU.mult, op1=ALU.add),
                         reads=[ugb[grp], cwb, ttb], writes=[ttb])
                    s.op("dve", ins("scalar_tensor_tensor", out=tt[:, :n], in0=ug[:, col, 2:n + 2], scalar=cw[:, 2, col:col + 1],
                                    in1=tt[:, :n], op0=ALU.mult, op1=ALU.add),
                         reads=[ugb[grp], cwb, ttb], writes=[ttb])

            def f2s3(cc):
                res, sg, sgb = pst.pop(cc)
                s.op("act", ins("activation", out=sg[:, :n], in_=res[0][0][:, :n], func=AF.Silu), reads=[res[0][1]], writes=[sgb])
                s.op("pool", ins("tensor_tensor", out=a[:, cc, :n], in0=sg[:, :n], in1=res[1][0][:, :n], op=ALU.mult),
                     reads=[sgb, res[1][1]], writes=[ab])

            for t_ in range(22 + 2):
                if t_ < 22:
                    f2s1(t_)
                if 1 <= t_ < 23:
                    f2s2(t_ - 1)
                if t_ >= 2:
                    f2s3(t_ - 2)
            outproj_resid(rctx, l, 5, ti, Wd, wdb, 22, a, [ab], Xsrc, Xsrc_b, X, X_b)

    def qkv_phase(l, Wqk_src, nqk, Wv_src, nv, Xsrc, Xsrc_b, rope, kout=None, vout=None, post=None):
        s.barrier()
        s.release()
        Wqk = s.alloc([128, 8, nqk * 128], BF16, "wqk")
        wqb = load_w(Wqk, Wqk_src, 8)
        Wv = s.alloc([128, 8, nv], BF16, "wv")
        wvb = load_w(Wv, Wv_src, 8)
        nctx = NormCtx()
        qk = [(s.alloc([128, nqk, 512], BF16, "qk"), [Buf() for _ in range(nqk)]) for _ in range(2)]
        vt = [(s.alloc([128, 4, nv], BF16, "vt"), Buf()) for _ in range(2)]
        qb = [(s.alloc([128, 512], BF16, "qb"), Buf()) for _ in range(3)]
        t12 = [[(s.alloc([128, 512], F32, "rt"), Buf()) for _ in range(2)] for _ in range(3)]
        cs = [(s.alloc([128, 2, 512], F32, "cs"), Buf()) for _ in range(1)]
        kf = [(s.alloc([128, 256], F32, "kf"), Buf()) for _ in range(2)]
        vf = [(s.alloc([128, 512], F32, "vf"), Buf()) for _ in range(2)]
        QK_b = [Buf() for _ in range(NT)]
        VS_b = [Buf() for _ in range(NT)]
        if "L" in KQ:
            return
        kq = 0
        kk = 0
        kv = 0
        for ti in range(NT):
            t0, n, cond, s0, s1 = TILES[ti]
            h, hb = norm_tile(nctx, l, 0, ti, Xsrc, Xsrc_b)
            if "N" in KQ:
                continue
            qkt, qkb = qk[ti % 2]
            dorope = rope and cond == 0 and ("r" not in KQ)
            if dorope:
                cst, csb = cs[0]
                s.dma(cst[:, 0, :], rcos[:, t0:t0 + n], writes=[csb])
                s.dma(cst[:, 1, :], rsin[:, t0:t0 + n], writes=[csb])
            ropeq = []

            def rope_fin(item):
                cc_, q_, q_b, t1, t1b, t2, t2b = item
                ps2, pb2 = s.ps()
                s.mm(ps2[:, :n], perm_bf[:], q_[:, :n], True, True, reads=[q_b, cb_const], writes=[pb2])
                s.op("dve", ins("tensor_tensor", out=t1[:, :n], in0=q_[:, :n], in1=cst[:, 0, :n], op=ALU.mult),
                     reads=[q_b, csb], writes=[t1b])
                s.op("dve", ins("tensor_tensor", out=t2[:, :n], in0=ps2[:, :n], in1=cst[:, 1, :n], op=ALU.mult),
                     reads=[pb2, csb], writes=[t2b])
                s.op("pool", ins("tensor_tensor", out=qkt[:, cc_, :n], in0=t1[:, :n], in1=t2[:, :n], op=ALU.add),
                     reads=[t1b, t2b], writes=[qkb[cc_]])

            for cc in range(nqk):
                ps, pb = s.ps()
                for kc in range(8):
                    s.mm(ps[:, :n], Wqk[:, kc, cc * 128:(cc + 1) * 128], h[:, kc, :n], kc == 0, kc == 7, reads=[wqb[kc], hb], writes=[pb])
                if dorope:
                    q_, q_b = qb[kq % 3]
                    t1, t1b = t12[kq % 3][0]
                    t2, t2b = t12[kq % 3][1]
                    kq += 1
                    s.op("act", ins("activation", out=q_[:, :n], in_=ps[:, :n], func=AF.Identity), reads=[pb], writes=[q_b])
                    ropeq.append((cc, q_, q_b, t1, t1b, t2, t2b))
                    if len(ropeq) > 1:
                        rope_fin(ropeq.pop(0))
                else:
                    if kout is not None and cond == 1 and cc >= kout[0]:
                        kft, kfb = kf[kk % 2]
                        kk += 1
                        rows = kout[2]
                        s.op("dve", ins("tensor_copy", out=kft[:, :n], in_=ps[:, :n]), reads=[pb], writes=[kfb])
                        s.op("act", ins("activation", out=qkt[:, cc, :n], in_=kft[:, :n], func=AF.Identity), reads=[kfb], writes=[qkb[cc]])
                        s.dma(kout[1][cc - kout[0], :, t0 - LS:t0 - LS + n], kft[0:rows, :n], reads=[kfb])
                    else:
                        s.op("act", ins("activation", out=qkt[:, cc, :n], in_=ps[:, :n], func=AF.Identity), reads=[pb], writes=[qkb[cc]])
                if post is not None:
                    post(ti, cc, ps, pb)
            while ropeq:
                rope_fin(ropeq.pop(0))
            vtt, vtb = vt[ti % 2]
            for tbk in range(n // 128 if "v" not in KQ else 0):
                for vg in range((nv + 511) // 512):
                    w_ = min(512, nv - vg * 512)
                    ps, pb = s.ps()
                    for kc in range(8):
                        s.mm(ps[:, :w_], h[:, kc, tbk * 128:(tbk + 1) * 128], Wv[:, kc, vg * 512:vg * 512 + w_], kc == 0, kc == 7, reads=[wvb[kc], hb], writes=[pb])
                    if vout is not None and cond == 1:
                        vft, vfb = vf[kv % 2]
                        kv += 1
                        s.op("dve", ins("tensor_copy", out=vft[:, :w_], in_=ps[:, :w_]), reads=[pb], writes=[vfb])
                        s.op("act", ins("activation", out=vtt[:, tbk, vg * 512:vg * 512 + w_], in_=vft[:, :w_], func=AF.Identity),
                             reads=[vfb], writes=[vtb])
                        r0 = t0 - LS + tbk * 128
                        s.dma(vout[r0:r0 + 128, vg * 512:vg * 512 + w_], vft[:, :w_], reads=[vfb])
                    else:
                        s.op("act", ins("activation", out=vtt[:, tbk, vg * 512:vg * 512 + w_], in_=ps[:, :w_], func=AF.Identity),
                             reads=[pb], writes=[vtb])
            if "q" not in KQ:
                s.dma(QK[:, 0:nqk, t0:t0 + n], qkt[:, :, :n], reads=qkb, writes=[QK_b[ti]])
            if "v" not in KQ and "s" not in KQ:
              s.dma(rows_view(VS, 1024, t0, n // 128, 0, nv), vtt[:, 0:n // 128, :], reads=[vtb], writes=[VS_b[ti]])

    def attn_phase(l, kind, Xsrc, Xsrc_b):
        s.barrier()
        s.release()
        win = kind == "win"
        nunits = 8
        kbase = 8
        Kt = [(s.alloc([128, T], BF16, "kt"), Buf()) for _ in range(2)]
        Va = [(s.alloc([128, 36, 128], BF16, "va"), Buf()) for _ in range(2)]
        Qt = [(s.alloc([128, LS], BF16, "qt"), Buf()) for _ in range(2)]
        Ot = [(s.alloc([128, LS], BF16, "ot"), Buf()) for _ in range(2)]
        pT = [(s.alloc([128, 512], BF16, "pT"), Buf()) for _ in range(8)]
        rec = [(s.alloc([128, 512], F32, "rec"), Buf()) for _ in range(8)]
        OS_b = [[Buf() for _ in range(nunits)] for _ in range(3)]
        cbuf = Buf()
        if win:
            masks = s.alloc([128, 6, 512], BF16, "masks")
            s.dma(masks[:], wmask_in, writes=[cbuf], eng="pool")
            esink = s.alloc([128, 16], F32, "esink")
            s.dma(esink[:], sink_in.partition_broadcast(128), writes=[cbuf])
            s.op("act", ins("activation", out=esink[:], in_=esink[:], func=AF.Exp), reads=[cbuf], writes=[cbuf])
            for i in range(2):
                s.op("pool", ins("memset", Va[i][0][:, :, 64:128], 1.0), writes=[Va[i][1]])
        else:
            acc = [(s.alloc([128, 512], F32, "acc"), Buf()) for _ in range(4)]
            osb = [(s.alloc([128, 512], F32, "osb"), Buf()) for _ in range(8)]
            sqd = [(s.alloc([128, 512], BF16, "sqd"), Buf()) for _ in range(4)]
            lq = s.alloc([128, 4, 64], F32, "lq")
            lam = s.alloc([128, 4], F32, "lam")
            gsub = s.alloc([128, 1], F32, "gsub")
            s.dma(lq[:], lqk.rearrange("a b -> (a b)").partition_broadcast(128).rearrange("p (a b) -> p a b", a=4), writes=[cbuf])
            s.dma(gsub[:], w2_gn, writes=[cbuf])
            s.op("dve", ins("tensor_tensor", out=lq[:, 0, :], in0=lq[:, 0, :], in1=lq[:, 1, :], op=ALU.mult), reads=[cbuf], writes=[cbuf])
            s.op("dve", ins("tensor_tensor", out=lq[:, 2, :], in0=lq[:, 2, :], in1=lq[:, 3, :], op=ALU.mult), reads=[cbuf], writes=[cbuf])
            s.op("dve", ins("reduce_sum", out=lam[:, 0:1], in_=lq[:, 0, :], axis=AX.X), reads=[cbuf], writes=[cbuf])
            s.op("dve", ins("reduce_sum", out=lam[:, 1:2], in_=lq[:, 2, :], axis=AX.X), reads=[cbuf], writes=[cbuf])
            s.op("act", ins("activation", out=lam[:, 0:2], in_=lam[:, 0:2], func=AF.Exp), reads=[cbuf], writes=[cbuf])
            s.op("dve", ins("tensor_tensor", out=lam[:, 2:3], in0=lam[:, 1:2], in1=lam[:, 0:1], op=ALU.subtract), reads=[cbuf], writes=[cbuf])
            s.op("dve", ins("tensor_scalar", out=lam[:, 2:3], in0=lam[:, 2:3], scalar1=-LAM_INIT, scalar2=None, op0=ALU.add), reads=[cbuf], writes=[cbuf])
            s.op("dve", ins("tensor_scalar", out=lam[:, 3:4], in0=gsub[:], scalar1=1.0 - LAM_INIT, scalar2=None, op0=ALU.mult), reads=[cbuf], writes=[cbuf])
        ku = 0
        kp = 0
        kr = 0
        ka = 0
        SEQS = [(0, 4096, 0), (4096, 256, 1), (4352, 256, 2)]
        for (q0, L, si) in SEQS:
            sample = si == 0
            nkb_lat = L // 128
            for u in range(nunits):
                kt, ktb = Kt[ku % 2]
                va, vab = Va[ku % 2]
                qt, qtb = Qt[ku % 2]
                ot, otb = Ot[ku % 2]
                ku += 1
                s.dma(qt[:, 0:L], QK[:, u, q0:q0 + L], writes=[qtb])
                if win:
                    g = u // 2
                    s.dma(kt[:, 0:L], QK[:, kbase + g, q0:q0 + L], writes=[ktb])
                    s.dma(va[:, 0:nkb_lat, 0:64], rows_view(VS, 1024, q0, nkb_lat, g * 64, 64), writes=[vab])
                    if sample:
                        s.dma(kt[:, L:L + 512], ck0[:, g, :], writes=[ktb], eng="pool")
                        s.dma(va[:, 32:36, 0:64], rows_view(cv0, 256, 0, 4, g * 64, 64), writes=[vab], eng="pool")
                else:
                    s.dma(kt[:, 0:L], QK[:, kbase + u, q0:q0 + L], writes=[ktb])
                    s.dma(va[:, 0:nkb_lat, :], rows_view(VS, 1024, q0, nkb_lat, u * 128, 128), writes=[vab])
                    if sample:
                        s.dma(kt[:, L:L + 512], ck2[:, u, :], writes=[ktb], eng="pool")
                        s.dma(va[:, 32:36, :], rows_view(cv2, 1024, 0, 4, u * 128, 128), writes=[vab], eng="pool")
                nq = 512 if sample else 256
                LOOK = 4
                tasks = []
                for qi in range(L // nq):
                    for e_ in range(2):
                        if win and sample:
                            kbs = [(kb, kb - qi * 4 + 1) for kb in range(qi * 4 - 1, qi * 4 + 5) if 0 <= kb < 32] + [(32 + j, None) for j in range(4)]
                        elif sample:
                            kbs = [(kb, None) for kb in range(36)]
                        else:
                            kbs = [(kb, None) for kb in range(2)]
                        for i, (kb, mi) in enumerate(kbs):
                            tasks.append((qi, e_, i, kb, mi, i == len(kbs) - 1))
                stt = {}

                def stage1(tk_):
                    nonlocal kp, ka
                    qi, e_, i, kb, mi, lastb = tk_
                    qs = slice(qi * nq, (qi + 1) * nq)
                    pr = slice(e_ * 64, (e_ + 1) * 64)
                    if i == 0:
                        d_ = {}
                        d_["pso"], d_["pob"] = s.ps("b")
                        if not win:
                            d_["ac"], d_["acb"] = acc[ka % 4]
                            ka += 1
                        stt[(qi, e_)] = d_
                    d_ = stt[(qi, e_)]
                    pss, psb_ = s.ps()
                    s.mm(pss[:, :nq], kt[pr, kb * 128:(kb + 1) * 128], qt[pr, qs], True, True, reads=[ktb, qtb], writes=[psb_], grp=("q", ku, cur_it[0] // 2))
                    p_, p_b = pT[kp % 8]
                    kp += 1
                    s.op("act", ins("activation", out=p_[:, :nq], in_=pss[:, :nq], func=AF.Exp, scale=0.125), reads=[psb_], writes=[p_b])
                    if mi is not None:
                        s.op("pool" if (kp % 2) else "dve", ins("tensor_tensor", out=p_[:, :nq], in0=p_[:, :nq], in1=masks[:, mi, :nq], op=ALU.mult),
                             reads=[p_b, cbuf], writes=[p_b])
                    if not win:
                        eng = "pool" if e_ == 0 else "dve"
                        ac, acb = d_["ac"], d_["acb"]
                        if i == 0:
                            s.op(eng, ins("tensor_copy", out=ac[:, :nq], in_=p_[:, :nq]), reads=[p_b], writes=[acb])
                        else:
                            s.op(eng, ins("tensor_tensor", out=ac[:, :nq], in0=ac[:, :nq], in1=p_[:, :nq], op=ALU.add), reads=[p_b, acb], writes=[acb])
                    d_[("p", i)] = (p_, p_b)

                def stage2(tk_):
                    nonlocal kr
                    qi, e_, i, kb, mi, lastb = tk_
                    qs = slice(qi * nq, (qi + 1) * nq)
                    pr = slice(e_ * 64, (e_ + 1) * 64)
                    d_ = stt[(qi, e_)]
                    pso, pob = d_["pso"], d_["pob"]
                    p_, p_b = d_.pop(("p", i))
                    s.mm(pso[:, :nq], va[:, kb, :], p_[:, :nq], i == 0, lastb, reads=[vab, p_b], writes=[pob], grp=("v", ku, cur_it[0] // 2))
                    if not lastb:
                        return
                    if win:
                        hh = u * 2 + e_
                        r_, r_b = rec[kr % 4]
                        kr += 1
                        s.op("dve", ins("tensor_scalar", out=r_[64:128, :nq], in0=pso[64:128, :nq], scalar1=esink[64:128, hh:hh + 1], scalar2=None, op0=ALU.add),
                             reads=[pob, cbuf], writes=[r_b])
                        s.op("dve", ins("reciprocal", out=r_[64:128, :nq], in_=r_[64:128, :nq]), reads=[r_b], writes=[r_b])
                        s.op("dve", ins("tensor_tensor", out=ot[pr, qs], in0=pso[0:64, :nq], in1=r_[64:128, :nq], op=ALU.mult),
                             reads=[pob, r_b], writes=[otb])
                        del stt[(qi, e_)]
                        return
                    ac, acb = d_["ac"], d_["acb"]
                    o_, o_b = osb[kr % 8]
                    r_, r_b = rec[kr % 8]
                    kr += 1
                    s.op("dve", ins("tensor_copy", out=o_[:, :nq], in_=pso[:, :nq]), reads=[pob], writes=[o_b])
                    d_["o"] = (o_, o_b)
                    key = (qi, e_)

                    def st1():
                        psd, pdb = s.ps()
                        s.mm(psd[:, :nq], ones_f[:], ac[:, :nq], True, True, reads=[acb, cb_const], writes=[pdb])
                        d_["psd"] = (psd, pdb)
                        defer(2, st2)

                    def st2():
                        psd, pdb = d_["psd"]
                        s.op("dve", ins("reciprocal", out=r_[:, :nq], in_=psd[:, :nq]), reads=[pdb], writes=[r_b])
                        defer(4, st3)

                    def st3():
                        s.op("dve", ins("tensor_tensor", out=o_[:, :nq], in0=o_[:, :nq], in1=r_[:, :nq], op=ALU.mult), reads=[o_b, r_b], writes=[o_b])
                        d_["done"] = True
                        if e_ == 1 or stt[(qi, 1)].get("done") if (qi, 1) in stt else False:
                            pass
                        if (qi, 0) in stt and (qi, 1) in stt and stt[(qi, 0)].get("done") and stt[(qi, 1)].get("done"):
                            defer(1, st4)

                    def st4():
                        (o0, o0b) = stt[(qi, 0)]["o"]
                        (o1, o1b) = stt[(qi, 1)]["o"]
                        s.op("dve", ins("scalar_tensor_tensor", out=o0[:, :nq], in0=o1[:, :nq], scalar=lam[:, 2:3], in1=o0[:, :nq], op0=ALU.mult, op1=ALU.add),
                             reads=[o0b, o1b, cbuf], writes=[o0b])
                        sq_, sq_b = sqd[qi % 4]
                        s.op("act", ins("activation", out=sq_[:, :nq], in_=o0[:, :nq], func=AF.Square), reads=[o0b], writes=[sq_b])
                        defer(2, st5)

                    def st5():
                        sq_, sq_b = sqd[qi % 4]
                        psn, pnb = s.ps()
                        s.mm(psn[:, :nq], ones_bf[:], sq_[:, :nq], True, True, reads=[sq_b, cb_const], writes=[pnb])
                        stt[(qi, 1)]["psn"] = (psn, pnb)
                        defer(2, st6)

                    def st6():
                        (o1, o1b) = stt[(qi, 1)]["o"]
                        psn, pnb = stt[(qi, 1)]["psn"]
                        s.op("act", ins("activation", out=o1[:, :nq], in_=psn[:, :nq], func=AF.Sqrt, scale=1.0 / 128, bias=EPSB[:, 0:1]), reads=[pnb, o1b], writes=[o1b])
                        defer(2, st7)

                    def st7():
                        (o1, o1b) = stt[(qi, 1)]["o"]
                        s.op("dve", ins("reciprocal", out=o1[:, :nq], in_=o1[:, :nq]), reads=[o1b], writes=[o1b])
                        defer(4, st8)

                    def st8():
                        (o0, o0b) = stt[(qi, 0)]["o"]
                        (o1, o1b) = stt[(qi, 1)]["o"]
                        s.op("dve", ins("scalar_tensor_tensor", out=ot[:, qs], in0=o0[:, :nq], scalar=lam[:, 3:4], in1=o1[:, :nq], op0=ALU.mult, op1=ALU.mult),
                             reads=[o0b, o1b, cbuf], writes=[otb])
                        del stt[(qi, 0)]
                        del stt[(qi, 1)]

                    defer(2, st1)

                pend = []
                cur_it = [0]

                def defer(dl, fn):
                    pend.append((cur_it[0] + dl, fn))

                def run_pending(flush=False):
                    while True:
                        ready = [p for p in pend if flush or p[0] <= cur_it[0]]
                        if not ready:
                            break
                        for p in ready:
                            pend.remove(p)
                        for due, fn in ready:
                            fn()
                        if not flush:
                            break

                for t2_ in range(0, len(tasks) + LOOK + 1, 2):
                    for t_ in (t2_, t2_ + 1):
                        cur_it[0] = t_
                        if t_ < len(tasks):
                            stage1(tasks[t_])
                    for t_ in (t2_, t2_ + 1):
                        cur_it[0] = t_
                        if LOOK <= t_ < len(tasks) + LOOK:
                            stage2(tasks[t_ - LOOK])
                    run_pending()
                while pend:
                    cur_it[0] += 1
                    run_pending(flush=True)
                s.dma(OS[:, u, q0:q0 + L], ot[:, 0:L], reads=[otb], writes=[OS_b[si][u]])
        return OS_b

    def outproj_phase(l, Wsrc, nk, Osrc, O_bufs_fn, Xsrc, Xsrc_b):
        s.barrier()
        s.release()
        Wo = s.alloc([128, nk, 1024], BF16, "wo")
        wob = load_w(Wo, Wsrc, nk)
        rctx = ResCtx()
        oin = [(s.alloc([128, nk, 512], BF16, "oin"), Buf()) for _ in range(2)]
        for ti in range(NT):
            t0, n, cond, _, _ = TILES[ti]
            o_, o_b = oin[ti % 2]
            s.dma(o_[:, :, :n], Osrc[:, :, t0:t0 + n], writes=[o_b])
            outproj_resid(rctx, l, 2, ti, Wo, wob, nk, o_, [o_b], Xsrc, Xsrc_b, X, X_b)

    def gla_layer(l, Xsrc, Xsrc_b):
        s.barrier()
        s.release()
        Wqk = s.alloc([128, 8, 1024], BF16, "gwqk")
        wqb = load_w(Wqk, w1_qk, 8)
        Wv = s.alloc([128, 8, 1024], BF16, "gwv")
        wvb = load_w(Wv, w1_v, 8)
        Wr = s.alloc([128, 8, 1024], BF16, "gwr")
        wrb = load_w(Wr, w1_r, 8)
        Wg1 = s.alloc([128, 8, 32], BF16, "gwg1")
        wg1b = load_w(Wg1, w1_g1, 8)
        Wg2 = s.alloc([32, 1024], BF16, "gwg2")
        cbuf = Buf()
        s.dma(Wg2[:], w1_g2, writes=[cbuf], eng="pool")
        gb = s.alloc([128, 8], F32, "ggb")
        s.dma(gb[:], w1_gb, writes=[cbuf])
        s.op("dve", ins("tensor_scalar", out=gb[:], in0=gb[:], scalar1=-1.0, scalar2=None, op0=ALU.mult), reads=[cbuf], writes=[cbuf])
        ELt = s.alloc([128, 2, 4, 72], F32, "el")
        nctx = NormCtx(1)
        qkf = [(s.alloc([128, 8, 512], F32, "qkf"), Buf()) for _ in range(1)]
        lg = nctx.xt[0]
        cs_ = (s.alloc([128, 8, 512], F32, "cs"), Buf())
        c2 = nctx.tmp
        ex = [(s.alloc([128, 512], F32, "ex"), Buf()) for _ in range(3)]
        t1b_ = (s.alloc([32, 512], BF16, "t1b"), Buf())
        qd_t = [(s.alloc([128, 2, 4, 512], BF16, "qd"), Buf()) for _ in range(1)]
        kd_t = [(s.alloc([128, 2, 4, 512], BF16, "kd"), Buf()) for _ in range(1)]
        krT = [(s.alloc([128, 512], BF16, "krT"), Buf()) for _ in range(2)]
        krt = [(s.alloc([128, 4, 8, 128], BF16, "krt"), Buf()) for _ in range(1)]
        rt = [(s.alloc([128, 8, 512], BF16, "rt"), Buf()) for _ in range(1)]
        vt = [(s.alloc([128, 4, 1024], BF16, "vt"), Buf()) for _ in range(1)]
        G_b = [Buf() for _ in range(NT)]
        kx = 0
        kk = 0
        for ti in range(NT):
            t0, n, cond, s0, s1 = TILES[ti]
            nch = n // 64
            h, hb = norm_tile(nctx, l, 0, ti, Xsrc, Xsrc_b)
            qf, qfb = qkf[0]
            for cc in range(8):
                ps, pb = s.ps()
                for kc in range(8):
                    s.mm(ps[:, :n], Wqk[:, kc, cc * 128:(cc + 1) * 128], h[:, kc, :n], kc == 0, kc == 7, reads=[wqb[kc], hb], writes=[pb])
                sc_ = (128.0 ** -0.5) if cc < 4 else 1.0
                s.op("act", ins("activation", out=qf[:, cc, :n], in_=ps[:, :n], func=AF.Identity, scale=sc_), reads=[pb], writes=[qfb])
            r_, r_b = rt[0]
            for cc in range(8):
                ps, pb = s.ps()
                for kc in range(8):
                    s.mm(ps[:, :n], Wr[:, kc, cc * 128:(cc + 1) * 128], h[:, kc, :n], kc == 0, kc == 7, reads=[wrb[kc], hb], writes=[pb])
                s.op("act", ins("activation", out=r_[:, cc, :n], in_=ps[:, :n], func=AF.Silu), reads=[pb], writes=[r_b])
            s.dma(RS[:, :, t0:t0 + n], r_[:, :, :n], reads=[r_b], writes=[G_b[ti]])
            v_, v_b = vt[0]
            for tbk in range(n // 128):
                for vg in range(2):
                    ps, pb = s.ps()
                    for kc in range(8):
                        s.mm(ps[:, :512], h[:, kc, tbk * 128:(tbk + 1) * 128], Wv[:, kc, vg * 512:(vg + 1) * 512], kc == 0, kc == 7, reads=[wvb[kc], hb], writes=[pb])
                    s.op("act", ins("activation", out=v_[:, tbk, vg * 512:(vg + 1) * 512], in_=ps[:, :512], func=AF.Identity), reads=[pb], writes=[v_b])
            s.dma(rows_view(VS, 1024, t0, n // 128, 0, 1024), v_[:, 0:n // 128, :], reads=[v_b], writes=[G_b[ti]])
            ps, pb = s.ps()
            for kc in range(8):
                s.mm(ps[0:32, :n], Wg1[:, kc, :], h[:, kc, :n], kc == 0, kc == 7, reads=[wg1b[kc], hb], writes=[pb])
            t1_, t1bb = t1b_
            s.op("act", ins("activation", out=t1_[:, :n], in_=ps[0:32, :n], func=AF.Identity), reads=[pb], writes=[t1bb])
            lgt, lgb = lg
            for j in range(8):
                ps, pb = s.ps()
                s.mm(ps[:, :n], Wg2[:, j * 128:(j + 1) * 128], t1_[:, :n], True, True, reads=[cbuf, t1bb], writes=[pb])
                s.op("act", ins("activation", out=lgt[:, j, :n], in_=ps[:, :n], func=AF.Exp, scale=-1.0, bias=gb[:, j:j + 1]), reads=[pb, cbuf], writes=[lgb])
            s.op("act", ins("activation", out=lgt[:, :, :n], in_=lgt[:, :, :n], func=AF.Ln, bias=1.0, scale=1.0), reads=[lgb], writes=[lgb])
            cst, csb = cs_
            for j in range(8):
                s.op("dve", ins("tensor_tensor_scan", out=cst[:, j, :n], data0=m64[:, :n], data1=lgt[:, j, :n], initial=0.0, op0=ALU.mult, op1=ALU.add),
                     reads=[lgb, cb_const], writes=[csb])
            c2t, c2b = c2
            csv = cst[:, :, :n].rearrange("p j (c t) -> p j c t", t=64)
            lgv = lgt[:, :, :n].rearrange("p j (c t) -> p j c t", t=64)
            c2v = c2t[:, :, :n].rearrange("p j (c t) -> p j c t", t=64)
            nf = 4
            s.op("dve", ins("tensor_tensor", out=c2v[:, 0:nf, :, :], in0=csv[:, 0:nf, :, :], in1=csv[:, 0:nf, :, 63:64].to_broadcast([128, nf, nch, 64]), op=ALU.subtract),
                 reads=[csb], writes=[c2b])
            s.op("dve", ins("tensor_tensor", out=c2v[:, nf:2 * nf, :, :], in0=lgv[:, nf:2 * nf, :, :], in1=csv[:, nf:2 * nf, :, :], op=ALU.subtract),
                 reads=[csb, lgb], writes=[c2b])
            ci0 = t0 // 64
            s.op("act", ins("activation", out=ELt[:, :, :, ci0:ci0 + nch].rearrange("p d h c -> p (d h) c"), in_=cst[:, :, :n].rearrange("p j (c t) -> p j c t", t=64)[:, :, :, 63],
                                               func=AF.Exp, scale=-1.0 / 16), reads=[csb], writes=[cbuf])
            s.op("dve", ins("tensor_tensor", out=lgv[:, nf:2 * nf, :, :], in0=c2v[:, nf:2 * nf, :, :], in1=csv[:, nf:2 * nf, :, 63:64].to_broadcast([128, nf, nch, 64]), op=ALU.add),
                 reads=[csb, c2b, lgb], writes=[lgb])
            qd_, qdb = qd_t[0]
            kd_, kdb = kd_t[0]
            krt_, krtb = krt[0]
            for j in range(8):
                d = j // 4
                hh = j % 4
                ea, eab = ex[0]
                eb, ebb = ex[1]
                ec, ecb = ex[2]
                csrc = cst if j < 4 else lgt
                s.op("act", ins("activation", out=ea[:, :n], in_=csrc[:, j, :n], func=AF.Exp, scale=-1.0 / 16), reads=[csb, lgb], writes=[eab])
                s.op("act", ins("activation", out=eb[:, :n], in_=csrc[:, j, :n], func=AF.Exp, scale=1.0 / 16), reads=[csb, lgb], writes=[ebb])
                s.op("act", ins("activation", out=ec[:, :n], in_=c2t[:, j, :n], func=AF.Exp, scale=1.0 / 16), reads=[c2b], writes=[ecb])
                s.op("dve", ins("tensor_tensor", out=qd_[:, d, hh, :n], in0=qf[:, hh, :n], in1=ea[:, :n], op=ALU.mult), reads=[qfb, eab], writes=[qdb])
                s.op("dve", ins("tensor_tensor", out=kd_[:, d, hh, :n], in0=qf[:, 4 + hh, :n], in1=eb[:, :n], op=ALU.mult), reads=[qfb, ebb], writes=[kdb])
                kT, kTb = krT[kx % 2]
                kx += 1
                s.op("pool", ins("tensor_tensor", out=kT[:, :n], in0=qf[:, 4 + hh, :n], in1=ec[:, :n], op=ALU.mult), reads=[qfb, ecb], writes=[kTb])
                for tbk in range(n // 128):
                    ps, pb = s.ps()
                    psv = ps[:].bitcast(BF16)
                    s.op("pe", ins("transpose", psv[:, 0:128], kT[:, tbk * 128:(tbk + 1) * 128], ident_bf[:]), reads=[kTb, cb_const], writes=[pb])
                    s.op("act", ins("activation", out=krt_[:, tbk, j, :], in_=psv[:, 0:128], func=AF.Identity), reads=[pb], writes=[krtb])
            for d in range(2):
                s.dma(QD[d, :, :, t0:t0 + n], qd_[:, d, :, :n], reads=[qdb], writes=[G_b[ti]])
                s.dma(KD[d, :, :, t0:t0 + n], kd_[:, d, :, :n], reads=[kdb], writes=[G_b[ti]])
            s.dma(kr_view(t0, n // 128, 0, 8), krt_[:, 0:n // 128, :, :], reads=[krtb], writes=[G_b[ti]])
        ELd = dscr("ELd", [128, 2, 4, 72], F32)
        elb = Buf()
        s.dma(ELd, ELt[:], reads=[cbuf], writes=[elb])
        s.barrier()
        s.release()
        EL = s.alloc([128, 2, 4, 72], F32, "el2")
        elb2 = Buf()
        s.dma(EL[:], ELd, writes=[elb2])
        S = [(s.alloc([128, 4, 256], F32, "S"), [Buf() for _ in range(4)]) for _ in range(2)]
        Sb = [(s.alloc([128, 4, 256], BF16, "Sb"), [Buf() for _ in range(4)]) for _ in range(2)]
        qd_t = [[(s.alloc([128, 4, 512], BF16, "qd"), Buf()) for _ in range(2)] for _ in range(2)]
        kd_t = [[(s.alloc([128, 4, 512], BF16, "kd"), Buf()) for _ in range(2)] for _ in range(2)]
        kr_t = [[(s.alloc([128, 4, 4, 128], BF16, "kr"), Buf()) for _ in range(2)] for _ in range(2)]
        v_t = [[(s.alloc([128, 4, 1024], BF16, "v"), Buf()) for _ in range(2)] for _ in range(2)]
        of_t = [[(s.alloc([128, 8, 512], BF16, "of"), Buf()) for _ in range(2)] for _ in range(2)]
        attm = [(s.alloc([128, 64], BF16, "attm"), Buf()) for _ in range(8)]
        OD_b = [[Buf() for _ in range(NT)] for _ in range(2)]
        ODs = [OS, OS2]
        kat = 0
        SEQT = [list(range(8)), [8], [9]]
        states_in = [s1f, s1b]
        states_out = [o_gf, o_gb]
        for si, tiles in enumerate(SEQT):
            for d in range(2):
                St, Stb = S[d]
                Sbt, Sbb = Sb[d]
                if si == 0:
                    s.dma(St[:], states_in[d], writes=Stb)
                else:
                    s.op("pool", ins("memset", St[:], 0.0), writes=Stb)
                for hh in range(4):
                    s.op("act", ins("activation", out=Sbt[:, hh, :], in_=St[:, hh, :], func=AF.Identity), reads=[Stb[hh]], writes=[Sbb[hh]])
            nt_ = len(tiles)
            for step in range(nt_):
                cur = {}
                for d in range(2):
                    ti = tiles[step] if d == 0 else tiles[nt_ - 1 - step]
                    t0, n, cond, s0, s1 = TILES[ti]
                    qd_, qdb = qd_t[d][step % 2]
                    kd_, kdb = kd_t[d][step % 2]
                    kr_, krb = kr_t[d][step % 2]
                    v_, vb_ = v_t[d][step % 2]
                    of_, ofb = of_t[d][step % 2]
                    s.dma(qd_[:, :, :n], QD[d, :, :, t0:t0 + n], writes=[qdb])
                    s.dma(kd_[:, :, :n], KD[d, :, :, t0:t0 + n], writes=[kdb])
                    s.dma(kr_[:, 0:n // 128, :, :], kr_view(t0, n // 128, d * 4, 4), writes=[krb])
                    s.dma(v_[:, 0:n // 128, :], rows_view(VS, 1024, t0, n // 128, 0, 1024), writes=[vb_])
                    cur[d] = (ti, t0, n, qd_, qdb, kd_, kdb, kr_, krb, v_, vb_, of_, ofb)
                nch = cur[0][2] // 64
                for kc_ in range(nch):
                    units = []
                    for d in range(2):
                        ti, t0, n, qd_, qdb, kd_, kdb, kr_, krb, v_, vb_, of_, ofb = cur[d]
                        k = kc_ if d == 0 else nch - 1 - kc_
                        for hh in range(4):
                            units.append(dict(d=d, hh=hh, k=k, ci=t0 // 64 + k, tbk=k // 2, hp=slice((k % 2) * 64, (k % 2) * 64 + 64),
                                              cs64=slice(k * 64, (k + 1) * 64), qd_=qd_, qdb=qdb, kd_=kd_, kdb=kdb, kr_=kr_, krb=krb, v_=v_, vb_=vb_, of_=of_, ofb=ofb))
                    psA, pbA = s.ps()
                    for ui, u_ in enumerate(units):
                        s.mm(psA[0:64, ui * 64:(ui + 1) * 64], u_["kd_"][:, u_["hh"], u_["cs64"]], u_["qd_"][:, u_["hh"], u_["cs64"]], True, True,
                             reads=[u_["kdb"], u_["qdb"]], writes=[pbA], grp=("ga", kat))
                    for ui, u_ in enumerate(units):
                        am, amb = attm[ui]
                        u_["am"] = (am, amb)
                        s.op("dve", ins("tensor_tensor", out=am[u_["hp"], :], in0=psA[0:64, ui * 64:(ui + 1) * 64], in1=tri_bf[0:64, 2 + u_["d"], 0:64], op=ALU.mult),
                             reads=[pbA, cb_const], writes=[amb])
                    psO = [s.ps("b") for _ in range(2)]
                    for ui, u_ in enumerate(units):
                        pso, pob = psO[ui // 4]
                        am, amb = u_["am"]
                        St, Stb = S[u_["d"]]
                        Sbt, Sbb = Sb[u_["d"]]
                        hh, hp, tbk = u_["hh"], u_["hp"], u_["tbk"]
                        c0 = (ui % 4) * 128
                        for vc in range(2):
                            s.mm(pso[:, c0 + vc * 64:c0 + (vc + 1) * 64], u_["v_"][hp, tbk, hh * 256 + vc * 128:hh * 256 + (vc + 1) * 128], am[hp, :], True, False,
                                 reads=[u_["vb_"], amb], writes=[pob])
                            s.mm(pso[:, c0 + vc * 64:c0 + (vc + 1) * 64], Sbt[:, hh, vc * 128:(vc + 1) * 128], u_["qd_"][:, hh, u_["cs64"]], False, True,
                                 reads=[Sbb[hh], u_["qdb"]], writes=[pob])
                    for ui, u_ in enumerate(units):
                        pso, pob = psO[ui // 4]
                        c0 = (ui % 4) * 128
                        hh = u_["hh"]
                        s.op("act", ins("activation", out=u_["of_"][:, hh * 2:hh * 2 + 2, u_["cs64"]], in_=pso[:, c0:c0 + 128].rearrange("p (v t) -> p v t", t=64), func=AF.Identity),
                             reads=[pob], writes=[u_["ofb"]])
                    psS = [s.ps() for _ in range(4)]
                    for ui, u_ in enumerate(units):
                        ps2, pb2 = psS[ui // 2]
                        c0 = (ui % 2) * 256
                        hh, hp, tbk = u_["hh"], u_["hp"], u_["tbk"]
                        s.mm(ps2[:, c0:c0 + 256], u_["kr_"][hp, tbk, hh, :], u_["v_"][hp, tbk, hh * 256:(hh + 1) * 256], True, True, reads=[u_["krb"], u_["vb_"]], writes=[pb2])
                    for ui, u_ in enumerate(units):
                        ps2, pb2 = psS[ui // 2]
                        c0 = (ui % 2) * 256
                        hh, d = u_["hh"], u_["d"]
                        St, Stb = S[d]
                        Sbt, Sbb = Sb[d]
                        ci = u_["ci"]
                        s.op("dve", ins("scalar_tensor_tensor", out=St[:, hh, :], in0=St[:, hh, :], scalar=EL[:, d, hh, ci:ci + 1], in1=ps2[:, c0:c0 + 256], op0=ALU.mult, op1=ALU.add),
                             reads=[pb2, elb2, Stb[hh]], writes=[Stb[hh]])
                        s.op("pool", ins("tensor_copy", out=Sbt[:, hh, :], in_=St[:, hh, :]), reads=[Stb[hh]], writes=[Sbb[hh]])
                    kat += 1
                for d in range(2):
                    ti, t0, n, qd_, qdb, kd_, kdb, kr_, krb, v_, vb_, of_, ofb = cur[d]
                    s.dma(ODs[d][:, :, t0:t0 + n], of_[:, :, :n], reads=[ofb], writes=[OD_b[d][ti]])
            if si > 0:
                for d in range(2):
                    s.dma(states_out[d][si - 1], S[d][0][:], reads=S[d][1])
        s.barrier()
        s.release()
        Wo = s.alloc([128, 8, 1024], BF16, "wo")
        wob = load_w(Wo, w1_o, 8)
        gn = s.alloc([128, 2], F32, "gn")
        gnb = Buf()
        s.dma(gn[:], w1_gn, writes=[gnb])
        rctx = ResCtx()
        oa = [(s.alloc([128, 8, 512], BF16, "oa"), Buf()) for _ in range(2)]
        ob_ = [(s.alloc([128, 8, 512], BF16, "ob"), Buf()) for _ in range(2)]
        rr = [(s.alloc([128, 8, 512], BF16, "rr"), Buf()) for _ in range(2)]
        osum = (s.alloc([128, 8, 512], F32, "osum"), Buf())
        sq = (s.alloc([128, 8, 512], BF16, "sq"), Buf())
        rs_ = [(s.alloc([128, 512], F32, "rs"), Buf()) for _ in range(2)]
        ofin = [(s.alloc([128, 8, 512], BF16, "ofin"), Buf()) for _ in range(2)]
        for ti in range(NT):
            t0, n, cond, _, _ = TILES[ti]
            a_, a_b = oa[ti % 2]
            b_, b_b = ob_[ti % 2]
            r_, r_b = rr[ti % 2]
            s.dma(a_[:, :, :n], OS[:, :, t0:t0 + n], writes=[a_b])
            s.dma(b_[:, :, :n], OS2[:, :, t0:t0 + n], writes=[b_b])
            s.dma(r_[:, :, :n], RS[:, :, t0:t0 + n], writes=[r_b])
            os_, osb_ = osum
            sq_, sqb_ = sq
            s.op("dve", ins("tensor_tensor", out=os_[:, :, :n], in0=a_[:, :, :n], in1=b_[:, :, :n], op=ALU.add), reads=[a_b, b_b], writes=[osb_])
            s.op("act", ins("activation", out=sq_[:, :, :n], in_=os_[:, :, :n], func=AF.Square), reads=[osb_], writes=[sqb_])
            f_, f_b = ofin[ti % 2]
            for hh in range(4):
                ps, pb = s.ps()
                for vc in range(2):
                    s.mm(ps[:, :n], ones_bf[:], sq_[:, hh * 2 + vc, :n], vc == 0, vc == 1, reads=[sqb_, cb_const], writes=[pb])
                rt_, rtb = rs_[hh % 2]
                s.op("act", ins("activation", out=rt_[:, :n], in_=ps[:, :n], func=AF.Sqrt, scale=1.0 / 256, bias=EPSB[:, 0:1]), reads=[pb], writes=[rtb])
                s.op("dve", ins("reciprocal", out=rt_[:, :n], in_=rt_[:, :n]), reads=[rtb], writes=[rtb])
                for vc in range(2):
                    c = hh * 2 + vc
                    s.op("dve", ins("tensor_tensor", out=os_[:, c, :n], in0=os_[:, c, :n], in1=rt_[:, :n], op=ALU.mult), reads=[osb_, rtb], writes=[osb_])
                    s.op("dve", ins("scalar_tensor_tensor", out=f_[:, c, :n], in0=os_[:, c, :n], scalar=gn[:, vc:vc + 1], in1=r_[:, c, :n], op0=ALU.mult, op1=ALU.mult),
                         reads=[osb_, gnb, r_b], writes=[f_b])
            outproj_resid(rctx, l, 2, ti, Wo, wob, 8, f_, [f_b], Xsrc, Xsrc_b, X, X_b)

    def ssd_layer(l, Xsrc, Xsrc_b):
        XBu = U
        s.barrier()
        s.release()
        Wz = s.alloc([128, 8, 2048], BF16, "wz")
        wzb = load_w(Wz, w3_z, 8)
        Wx = s.alloc([128, 8, 3072], BF16, "wx")
        wxb = load_w(Wx, w3_xbc, 8)
        Wdt = s.alloc([128, 8, 64], BF16, "wdt")
        wdb = load_w(Wdt, w3_dt, 8)
        vec = s.alloc([128, 5, 32], F32, "vec")
        cbuf = Buf()
        s.dma(vec[:], ssd_vec.rearrange("a b -> (a b)").partition_broadcast(128).rearrange("p (a b) -> p a b", a=5), writes=[cbuf])
        avec = s.alloc([128, 64], F32, "avec")
        s.op("act", ins("activation", out=avec[:], in_=vec[:, 0:2, :].rearrange("p a b -> p (a b)"), func=AF.Exp), reads=[cbuf], writes=[cbuf])
        s.op("dve", ins("tensor_scalar", out=avec[:], in0=avec[:], scalar1=-1.0, scalar2=None, op0=ALU.mult), reads=[cbuf], writes=[cbuf])
        nctx = NormCtx(1)
        xbt = (s.alloc([128, 24, 512], BF16, "xbt"), [Buf() for _ in range(24)])
        zt = (s.alloc([128, 4, 2048], BF16, "zt"), Buf())
        dtt = (s.alloc([128, 4, 2, 64], F32, "dtt"), Buf())
        dtmp = (s.alloc([128, 64], F32, "dtmp"), Buf())
        S_b = [Buf() for _ in range(NT)]
        ev = 0
        for ti in range(NT):
            t0, n, cond, s0, s1 = TILES[ti]
            nb = n // 128
            h, hb = norm_tile(nctx, l, 0, ti, Xsrc, Xsrc_b)
            xb_, xbb = xbt
            for cc in range(24):
                ps, pb = s.ps()
                for kc in range(8):
                    s.mm(ps[:, :n], Wx[:, kc, cc * 128:(cc + 1) * 128], h[:, kc, :n], kc == 0, kc == 7, reads=[wxb[kc], hb], writes=[pb])
                if ev % 2 == 0:
                    s.op("act", ins("activation", out=xb_[:, cc, :n], in_=ps[:, :n], func=AF.Identity), reads=[pb], writes=[xbb[cc]])
                else:
                    s.op("dve", ins("tensor_copy", out=xb_[:, cc, :n], in_=ps[:, :n]), reads=[pb], writes=[xbb[cc]])
                ev += 1
            s.dma(XBu[:, 0:24, t0:t0 + n], xb_[:, :, :n], reads=xbb, writes=[S_b[ti]])
            z_, z_b = zt
            d_, d_b = dtt
            for tbk in range(nb):
                for vg in range(4):
                    ps, pb = s.ps()
                    for kc in range(8):
                        s.mm(ps[:, :512], h[:, kc, tbk * 128:(tbk + 1) * 128], Wz[:, kc, vg * 512:(vg + 1) * 512], kc == 0, kc == 7, reads=[wzb[kc], hb], writes=[pb])
                    s.op("act", ins("activation", out=z_[:, tbk, vg * 512:(vg + 1) * 512], in_=ps[:, :512], func=AF.Silu), reads=[pb], writes=[z_b])
                ps, pb = s.ps()
                for kc in range(8):
                    s.mm(ps[:, 0:64], h[:, kc, tbk * 128:(tbk + 1) * 128], Wdt[:, kc, :], kc == 0, kc == 7, reads=[wdb[kc], hb], writes=[pb])
                tm, tmb = dtmp
                s.op("dve", ins("tensor_tensor", out=tm[:], in0=ps[:, 0:64], in1=vec[:, 2:4, :].rearrange("p a b -> p (a b)"), op=ALU.add), reads=[pb, cbuf], writes=[tmb])
                s.op("act", ins("activation", out=tm[:], in_=tm[:], func=AF.Exp), reads=[tmb], writes=[tmb])
                s.op("act", ins("activation", out=d_[:, tbk, 0, :], in_=tm[:], func=AF.Ln, bias=1.0, scale=1.0), reads=[tmb], writes=[d_b])
                s.op("dve", ins("tensor_tensor", out=d_[:, tbk, 1, :], in0=d_[:, tbk, 0, :], in1=avec[:], op=ALU.mult), reads=[d_b, cbuf], writes=[d_b])
            s.dma(rows_view(ZS, 2048, t0, nb, 0, 2048), z_[:, 0:nb, :], reads=[z_b], writes=[S_b[ti]])
            s.dma(AP(DTS.tensor, t0 * 128, [[128, 128], [128 * 128, nb], [1, 128]]), d_[:, 0:nb, :, :].rearrange("p b a c -> p b (a c)"), reads=[d_b], writes=[S_b[ti]])
        chk(20)
        s.barrier()
        s.release()
        cw = s.alloc([128, 3, 24], F32, "scw")
        cbias = s.alloc([128, 24], F32, "scb")
        cwb = Buf()
        s.dma(cw[:], ssd_cw, writes=[cwb])
        s.dma(cbias[:], ssd_cb, writes=[cwb])
        ug = s.alloc([128, 24, 514], BF16, "sug")
        ugb = [Buf() for _ in range(2)]
        xc = (s.alloc([128, 24, 512], BF16, "xc"), [Buf() for _ in range(24)])
        tmps = [(s.alloc([128, 512], F32, "st"), Buf()) for _ in range(2)]
        xtt = (s.alloc([128, 4, 2048], BF16, "xtt"), Buf())
        btt = (s.alloc([128, 4, 512], BF16, "btt"), Buf())
        kp = 0
        for ti in range(NT):
            t0, n, cond, s0, s1 = TILES[ti]
            nb = n // 128
            lo = (t0 - 1) >= s0
            hi = (t0 + n) < s1
            for grp in range(2):
                gs_ = slice(grp * 12, (grp + 1) * 12)
                if not lo:
                    s.op("pool", ins("memset", ug[:, gs_, 0:1], 0.0), writes=[ugb[grp]])
                if not hi:
                    s.op("pool", ins("memset", ug[:, gs_, n + 1:n + 2], 0.0), writes=[ugb[grp]])
                c0 = 0 if lo else 1
                c1 = n + 2 if hi else n + 1
                s.dma(ug[:, gs_, c0:c1], XBu[:, gs_, t0 - 1 + c0:t0 - 1 + c1], writes=[ugb[grp]])
            xc_, xcb = xc
            for col in range(24):
                grp = col // 12
                tt, ttb = tmps[kp % 2]
                kp += 1
                s.op("act", ins("activation", out=tt[:, :n], in_=ug[:, col, 1:n + 1], func=AF.Identity, scale=cw[:, 1, col:col + 1], bias=cbias[:, col:col + 1]),
                     reads=[ugb[grp], cwb], writes=[ttb])
                s.op("dve", ins("scalar_tensor_tensor", out=tt[:, :n], in0=ug[:, col, 0:n], scalar=cw[:, 0, col:col + 1], in1=tt[:, :n], op0=ALU.mult, op1=ALU.add),
                     reads=[ugb[grp], cwb, ttb], writes=[ttb])
                s.op("dve", ins("scalar_tensor_tensor", out=tt[:, :n], in0=ug[:, col, 2:n + 2], scalar=cw[:, 2, col:col + 1], in1=tt[:, :n], op0=ALU.mult, op1=ALU.add),
                     reads=[ugb[grp], cwb, ttb], writes=[ttb])
                s.op("act", ins("activation", out=xc_[:, col, :n], in_=tt[:, :n], func=AF.Silu), reads=[ttb], writes=[xcb[col]])
            s.dma(QK[:, 0:8, t0:t0 + n], xc_[:, 16:24, :n], reads=xcb[16:24], writes=[S_b[ti]])
            x_, x_b = xtt
            b_, b_b = btt
            tv = 0
            for tbk in range(nb):
                for col in range(20):
                    ps, pb = s.ps()
                    psv = ps[:].bitcast(BF16)
                    s.op("pe", ins("transpose", psv[:, 0:128], xc_[:, col, tbk * 128:(tbk + 1) * 128], ident_bf[:]), reads=[xcb[col], cb_const], writes=[pb])
                    dst = x_[:, tbk, col * 128:(col + 1) * 128] if col < 16 else b_[:, tbk, (col - 16) * 128:(col - 15) * 128]
                    dbuf = x_b if col < 16 else b_b
                    if tv % 2 == 0:
                        s.op("act", ins("activation", out=dst, in_=psv[:, 0:128], func=AF.Identity), reads=[pb], writes=[dbuf])
                    else:
                        s.op("dve", ins("tensor_copy", out=dst, in_=psv[:, 0:128]), reads=[pb], writes=[dbuf])
                    tv += 1
            s.dma(rows_view(XTs, 2048, t0, nb, 0, 2048), x_[:, 0:nb, :], reads=[x_b], writes=[S_b[ti]])
            s.dma(rows_view(BTs, 512, t0, nb, 0, 512), b_[:, 0:nb, :], reads=[b_b], writes=[S_b[ti]])
        chk(21)
        s.barrier()
        s.release()
        vec = s.alloc([128, 5, 32], F32, "vec")
        negm = s.alloc([128, 2, 128], F32, "negm")
        cbuf = Buf()
        s.op("dve", ins("tensor_scalar", out=negm[:], in0=stage[:, 0:2, :], scalar1=-1.0, scalar2=30000.0, op0=ALU.add, op1=ALU.mult), reads=[bstage], writes=[cbuf])
        ST = [(s.alloc([128, 2048], F32, "ST"), [Buf() for _ in range(4)]) for _ in range(2)]
        SbT = [(s.alloc([128, 2048], BF16, "SbT"), [Buf() for _ in range(4)]) for _ in range(2)]
        bct = [(s.alloc([128, 8, 512], BF16, "bct"), Buf()) for _ in range(2)]
        xts = [(s.alloc([128, 4, 2048], BF16, "xts"), Buf()) for _ in range(2)]
        bts = [(s.alloc([128, 4, 512], BF16, "bts"), Buf()) for _ in range(2)]
        dts = [(s.alloc([128, 4, 2, 64], F32, "dts"), Buf()) for _ in range(2)]
        yts = [(s.alloc([128, 4, 2048], BF16, "yts"), Buf()) for _ in range(2)]
        cumT = [(s.alloc([128, 32], F32, "cumT"), Buf()) for _ in range(2)]
        cbT = [(s.alloc([128, 128], F32, "cbT"), Buf()) for _ in range(6)]
        lat = [(s.alloc([128, 4, 128], F32, "lat"), Buf()) for _ in range(3)]
        cB = [(s.alloc([128, 4, 128], F32, "cB"), Buf()) for _ in range(4)]
        seg = [(s.alloc([128, 4, 128], F32, "seg"), Buf()) for _ in range(3)]
        Et = [(s.alloc([128, 4, 128], F32, "Et"), Buf()) for _ in range(6)]
        ecB = [(s.alloc([128, 4, 128], F32, "ecB"), Buf()) for _ in range(8)]
        te = [(s.alloc([128, 4], F32, "te"), Buf()) for _ in range(3)]
        xs = [(s.alloc([128, 512], BF16, "xs"), Buf()) for _ in range(4)]
        Wt = [(s.alloc([128, 128], BF16, "Wt"), Buf()) for _ in range(8)]
        CEt = [(s.alloc([128, 128], BF16, "CEt"), Buf()) for _ in range(8)]
        SEQT = [list(range(8)), [8], [9]]
        st_in = [s3f, s3b]
        st_out = [o_sf, o_sb]
        Y_b = [Buf() for _ in range(NT)]
        kw = 0
        kq4 = 0
        kg = 0
        kct = 0
        for si, tiles in enumerate(SEQT):
            for d in range(2):
                St, Stb = ST[d]
                Sbt, Sbb = SbT[d]
                if si == 0:
                    s.dma(St[:], st_in[d], writes=Stb)
                else:
                    s.op("pool", ins("memset", St[:], 0.0), writes=Stb)
                for g in range(4):
                    s.op("act", ins("activation", out=Sbt[:, g * 512:(g + 1) * 512], in_=St[:, g * 512:(g + 1) * 512], func=AF.Identity), reads=[Stb[g]], writes=[Sbb[g]])
            nt_ = len(tiles)
            for step in range(nt_):
                cur = {}
                for d in range(2):
                    ti = tiles[step] if d == 0 else tiles[nt_ - 1 - step]
                    t0, n, cond, s0, s1 = TILES[ti]
                    nb = n // 128
                    bc_, bcb = bct[d]
                    x_, x_b = xts[d]
                    b_, b_b = bts[d]
                    d_, d_b = dts[d]
                    y_, y_b = yts[d]
                    s.dma(bc_[:, :, :n], QK[:, 0:8, t0:t0 + n], writes=[bcb])
                    s.dma(x_[:, 0:nb, :], rows_view(XTs, 2048, t0, nb, 0, 2048), writes=[x_b])
                    s.dma(b_[:, 0:nb, :], rows_view(BTs, 512, t0, nb, 0, 512), writes=[b_b])
                    s.dma(d_[:, 0:nb, :, :].rearrange("p b a c -> p b (a c)"), AP(DTS.tensor, t0 * 128, [[128, 128], [128 * 128, nb], [1, 128]]), writes=[d_b])
                    cur[d] = (ti, t0, n, nb)
                nbs = cur[0][3]
                batches = []
                for kc_ in range(nbs):
                    for d in range(2):
                        for g in range(4):
                            for hb_ in range(2):
                                batches.append((kc_, d, g, hb_))
                bst = {}
                LOOKB = 3

                def geo(bt_):
                    kc_, d, g, hb_ = bt_
                    ti, t0, n, nb = cur[d]
                    tbk = kc_ if d == 0 else nb - 1 - kc_
                    last = 127 if d == 0 else 0
                    d_, d_b = dts[d]
                    la = d_[:, tbk, 1, d * 32:(d + 1) * 32]
                    dtv = d_[:, tbk, 0, d * 32:(d + 1) * 32]
                    return tbk, last, slice(tbk * 128, (tbk + 1) * 128), la, dtv, d_b

                def stageA1(bt_):
                    nonlocal kq4, kg, kct
                    kc_, d, g, hb_ = bt_
                    tbk, last, tk, la, dtv, d_b = geo(bt_)
                    bc_, bcb = bct[d]
                    if g == 0 and hb_ == 0:
                        psc, pcb = s.ps()
                        s.mm(psc[:, 0:32], stage[:, d, :], la, True, True, reads=[bstage, d_b], writes=[pcb])
                        cT, cTb = cumT[kct % 2]
                        kct += 1
                        s.op("dve", ins("tensor_copy", out=cT[:], in_=psc[:, 0:32]), reads=[pcb], writes=[cTb])
                        bst[("cT", kc_, d)] = (cT, cTb)
                    if hb_ == 0:
                        ps, pb = s.ps()
                        s.mm(ps[:, 0:128], bc_[:, g, tk], bc_[:, 4 + g, tk], True, True, reads=[bcb], writes=[pb])
                        cb_, cb_b = cbT[kg % 6]
                        s.op("act", ins("activation", out=cb_[:], in_=ps[:, 0:128], func=AF.Identity), reads=[pb], writes=[cb_b])
                        bst[("grp", kc_, d, g)] = dict(cb=(cb_, cb_b), x4=xs[kg % 4], ecs=[])
                        kg += 1
                    h0 = g * 8 + hb_ * 4
                    lt, ltb = lat[kq4 % 3]
                    cb4, cb4b = cB[kq4 % 4]
                    sg, sgb = seg[kq4 % 3]
                    E_, E_b = Et[kq4 % 6]
                    ec, ecb = ecB[kq4 % 8]
                    te_, te_b = te[kq4 % 3]
                    kq4 += 1
                    bst[("w",) + bt_] = (cb4, cb4b, sg, sgb, E_, E_b, ec, ecb, te_, te_b)
                    s.op("dve", ins("tensor_tensor", out=lt[:], in0=stage[:, d, :].unsqueeze(1).to_broadcast([128, 4, 128]),
                                    in1=la[:, h0:h0 + 4].unsqueeze(2).to_broadcast([128, 4, 128]), op=ALU.mult),
                         reads=[bstage, d_b], writes=[ltb])
                    ps2, pb2 = s.ps()
                    s.mm(ps2[:, 0:512], ones_f[:], lt[:].rearrange("p a b -> p (a b)"), True, True, reads=[ltb, cb_const], writes=[pb2])
                    s.op("act", ins("activation", out=cb4[:].rearrange("p a b -> p (a b)"), in_=ps2[:, 0:512], func=AF.Identity), reads=[pb2], writes=[cb4b])

                def stageA2a(bt_):
                    kc_, d, g, hb_ = bt_
                    cT, cTb = bst[("cT", kc_, d)]
                    G = bst[("grp", kc_, d, g)]
                    cb4, cb4b, sg, sgb, E_, E_b, ec, ecb, te_, te_b = bst[("w",) + bt_]
                    h0 = g * 8 + hb_ * 4
                    G["ecs"].append((ec, ecb))
                    for hh in range(4):
                        hq = h0 + hh
                        s.op("dve", ins("scalar_tensor_tensor", out=sg[:, hh, :], in0=cb4[:, hh, :], scalar=cT[:, hq:hq + 1], in1=negm[:, d, :],
                                        op0=ALU.subtract, op1=ALU.add),
                             reads=[cb4b, cTb, cbuf], writes=[sgb])
                    s.op("act", ins("activation", out=E_[:], in_=sg[:], func=AF.Exp), reads=[sgb], writes=[E_b])
                    s.op("act", ins("activation", out=ec[:], in_=cb4[:], func=AF.Exp), reads=[cb4b], writes=[ecb])

                def stageA2b(bt_):
                    kc_, d, g, hb_ = bt_
                    tbk, last, tk, la, dtv, d_b = geo(bt_)
                    x_, x_b = xts[d]
                    G = bst[("grp", kc_, d, g)]
                    x4, x4b = G["x4"]
                    cb4, cb4b, sg, sgb, E_, E_b, ec, ecb, te_, te_b = bst.pop(("w",) + bt_)
                    h0 = g * 8 + hb_ * 4
                    s.op("dve", ins("tensor_tensor", out=te_[:], in0=E_[:, :, last], in1=dtv[:, h0:h0 + 4], op=ALU.mult),
                         reads=[E_b, d_b], writes=[te_b])
                    s.op("dve", ins("tensor_tensor",
                                    out=x4[:, hb_ * 256:(hb_ + 1) * 256].rearrange("p (a b) -> p a b", b=64),
                                    in0=x_[:, tbk, h0 * 64:(h0 + 4) * 64].rearrange("p (a b) -> p a b", b=64),
                                    in1=te_[:].unsqueeze(2).to_broadcast([128, 4, 64]), op=ALU.mult),
                         reads=[x_b, te_b], writes=[x4b])
                    bst[("bat",) + bt_] = (E_, E_b, ec, ecb)

                def stageB(bt_):
                    nonlocal kw
                    kc_, d, g, hb_ = bt_
                    tbk, last, tk, la, dtv, d_b = geo(bt_)
                    bc_, bcb = bct[d]
                    x_, x_b = xts[d]
                    b_, b_b = bts[d]
                    y_, y_b = yts[d]
                    St, Stb = ST[d]
                    Sbt, Sbb = SbT[d]
                    G = bst[("grp", kc_, d, g)]
                    cb_, cb_b = G["cb"]
                    x4, x4b = G["x4"]
                    if hb_ == 0:
                        G["psy"] = s.ps("b")
                    psy, pyb = G["psy"]
                    E_, E_b, ec, ecb = bst.pop(("bat",) + bt_)
                    h0 = g * 8 + hb_ * 4
                    for hh in range(4):
                        hq = h0 + hh
                        W_, W_b = Wt[kw % 8]
                        C_, C_b = CEt[kw % 8]
                        kw += 1
                        s.op("dve", ins("scalar_tensor_tensor", out=W_[:], in0=E_[:, hh, :], scalar=dtv[:, hq:hq + 1], in1=cb_[:], op0=ALU.mult, op1=ALU.mult),
                             reads=[E_b, d_b, cb_b], writes=[W_b])
                        s.op("pool", ins("tensor_tensor", out=C_[:], in0=bc_[:, 4 + g, tk], in1=ec[:, hh, :], op=ALU.mult),
                             reads=[bcb, ecb], writes=[C_b])
                        ycol = slice((hq % 8) * 64, (hq % 8 + 1) * 64)
                        s.mm(psy[:, ycol], W_[:], x_[:, tbk, hq * 64:(hq + 1) * 64], True, False, reads=[W_b, x_b], writes=[pyb])
                        s.mm(psy[:, ycol], C_[:], Sbt[:, hq * 64:(hq + 1) * 64], False, True, reads=[C_b, Sbb[g]], writes=[pyb])
                    if hb_ == 0:
                        return
                    s.op("act", ins("activation", out=y_[:, tbk, g * 512:(g + 1) * 512], in_=psy[:, 0:512], func=AF.Identity), reads=[pyb], writes=[y_b])
                    pss, psb_ = s.ps()
                    s.mm(pss[:, 0:512], b_[:, tbk, g * 128:(g + 1) * 128], x4[:], True, True, reads=[b_b, x4b], writes=[psb_])
                    ecs = G["ecs"]
                    for hh in range(8):
                        hq = g * 8 + hh
                        ec2, ecb2 = ecs[hh // 4]
                        s.op("dve", ins("scalar_tensor_tensor",
                                        out=St[:, hq * 64:(hq + 1) * 64], in0=St[:, hq * 64:(hq + 1) * 64], scalar=ec2[:, hh % 4, last:last + 1], in1=pss[:, hh * 64:(hh + 1) * 64], op0=ALU.mult, op1=ALU.add),
                             reads=[psb_, ecb2, Stb[g]], writes=[Stb[g]])
                    s.op("pool", ins("tensor_copy", out=Sbt[:, g * 512:(g + 1) * 512], in_=St[:, g * 512:(g + 1) * 512]), reads=[Stb[g]], writes=[Sbb[g]])
                    del bst[("grp", kc_, d, g)]

                NB_ = len(batches)
                for t_ in range(NB_ + 3):
                    if t_ < NB_:
                        stageA1(batches[t_])
                    if 1 <= t_ < NB_ + 1:
                        stageA2a(batches[t_ - 1])
                    if 2 <= t_ < NB_ + 2:
                        stageA2b(batches[t_ - 2])
                    if t_ >= 3:
                        stageB(batches[t_ - 3])
                for d in range(2):
                    ti, t0, n, nb = cur[d]
                    y_, y_b = yts[d]
                    s.dma(rows_view(YD[d], 2048, t0, nb, 0, 2048), y_[:, 0:nb, :], reads=[y_b], writes=[Y_b[ti]])
            if si > 0:
                for d in range(2):
                    s.dma(st_out[d][si - 1], ST[d][0][:], reads=ST[d][1])
        chk(22)
        s.barrier()
        s.release()
        Wo = s.alloc([128, 16, 1024], BF16, "wo")
        wob = load_w(Wo, w3_o, 16)
        vec = s.alloc([128, 5, 32], F32, "vec")
        gnb = s.alloc([128, 2048], F32, "gnb")
        cbuf = Buf()
        s.dma(vec[:], ssd_vec.rearrange("a b -> (a b)").partition_broadcast(128).rearrange("p (a b) -> p a b", a=5), writes=[cbuf])
        s.dma(gnb[:], ssd_gn.partition_broadcast(128), writes=[cbuf])
        rctx = ResCtx()
        yf = (s.alloc([128, 4, 2048], BF16, "yf"), Buf())
        yb = (s.alloc([128, 4, 2048], BF16, "yb"), Buf())
        xt_ = (s.alloc([128, 4, 2048], BF16, "xt4"), Buf())
        zs = (s.alloc([128, 4, 2048], BF16, "zs"), Buf())
        yv = [(s.alloc([128, 2048], F32, "yv"), Buf()) for _ in range(2)]
        tv_ = (s.alloc([128, 2048], F32, "tv"), Buf())
        ynb = (s.alloc([128, 2048], BF16, "ynb"), Buf())
        ssq = [(s.alloc([128, 1], F32, "ssq"), Buf()) for _ in range(2)]
        oT = [(s.alloc([128, 16, 512], BF16, "oT"), Buf()) for _ in range(2)]
        kk = 0
        tv = 0
        for ti in range(NT):
            t0, n, cond, _, _ = TILES[ti]
            nb = n // 128
            s.dma(yf[0][:, 0:nb, :], rows_view(YD[0], 2048, t0, nb, 0, 2048), writes=[yf[1]])
            s.dma(yb[0][:, 0:nb, :], rows_view(YD[1], 2048, t0, nb, 0, 2048), writes=[yb[1]])
            s.dma(xt_[0][:, 0:nb, :], rows_view(XTs, 2048, t0, nb, 0, 2048), writes=[xt_[1]])
            s.dma(zs[0][:, 0:nb, :], rows_view(ZS, 2048, t0, nb, 0, 2048), writes=[zs[1]])
            o_, o_b = oT[ti % 2]
            for tbk in range(nb):
                y_, y_b = yv[kk % 2]
                sq_, sq_b = ssq[kk % 2]
                kk += 1
                t_, t_b = tv_
                s.op("dve", ins("tensor_tensor", out=y_[:], in0=yf[0][:, tbk, :], in1=yb[0][:, tbk, :], op=ALU.add), reads=[yf[1], yb[1]], writes=[y_b])
                s.op("pool", ins("tensor_tensor", out=t_[:].rearrange("p (a b) -> p a b", b=64), in0=xt_[0][:, tbk, :].rearrange("p (a b) -> p a b", b=64),
                                                              in1=vec[:, 4, :].unsqueeze(2).to_broadcast([128, 32, 64]), op=ALU.mult), reads=[xt_[1], cbuf], writes=[t_b])
                s.op("dve", ins("tensor_tensor", out=y_[:], in0=y_[:], in1=t_[:], op=ALU.add), reads=[y_b, t_b], writes=[y_b])
                s.op("dve", ins("tensor_tensor", out=y_[:], in0=y_[:], in1=zs[0][:, tbk, :], op=ALU.mult), reads=[y_b, zs[1]], writes=[y_b])
                s.op("pool", ins("memset", sq_[:], 0.0), writes=[sq_b])
                s.op("act", ins("activation", out=t_[:], in_=y_[:], func=AF.Square, accum_out=sq_[:, 0:1]), reads=[y_b, sq_b], writes=[t_b, sq_b])
                s.op("act", ins("activation", out=sq_[:], in_=sq_[:], func=AF.Sqrt, scale=1.0 / 2048, bias=EPSB[:, 0:1]), reads=[sq_b], writes=[sq_b])
                s.op("dve", ins("reciprocal", out=sq_[:], in_=sq_[:]), reads=[sq_b], writes=[sq_b])
                yn, ynb_ = ynb
                s.op("dve", ins("scalar_tensor_tensor", out=yn[:], in0=y_[:], scalar=sq_[:, 0:1], in1=gnb[:], op0=ALU.mult, op1=ALU.mult), reads=[y_b, sq_b, cbuf], writes=[ynb_])
                for c in range(16):
                    ps, pb = s.ps()
                    psv = ps[:].bitcast(BF16)
                    s.op("pe", ins("transpose", psv[:, 0:128], yn[:, c * 128:(c + 1) * 128], ident_bf[:]), reads=[ynb_, cb_const], writes=[pb])
                    if tv % 2 == 0:
                        s.op("act", ins("activation", out=o_[:, c, tbk * 128:(tbk + 1) * 128], in_=psv[:, 0:128], func=AF.Identity), reads=[pb], writes=[o_b])
                    else:
                        s.op("dve", ins("tensor_copy", out=o_[:, c, tbk * 128:(tbk + 1) * 128], in_=psv[:, 0:128]), reads=[pb], writes=[o_b])
                    tv += 1
            outproj_resid(rctx, l, 2, ti, Wo, wob, 16, o_, [o_b], Xsrc, Xsrc_b, X, X_b)

    import os
    STOP = int(os.environ.get("KSTOP", "99"))

    class _Stop(Exception):
        pass

    def chk(k):
        if STOP == k:
            raise _Stop()

    try:
        prologue()
        chk(0)
        Xsrc, Xsrc_b = xT, xT_b
        for l in range(NLAYERS):
            kind = l % 4
            if kind == 0:
                qkv_phase(l, w0_qk, 12, w0_v, 256, Xsrc, Xsrc_b, True, kout=(8, o_wk, 64), vout=o_wv)
                chk(1)
                attn_phase(l, "win", Xsrc, Xsrc_b)
                chk(2)
                outproj_phase(l, w0_o, 8, OS, None, Xsrc, Xsrc_b)
                chk(3)
            elif kind == 1:
                gla_layer(l, Xsrc, Xsrc_b)
            elif kind == 2:
                qkv_phase(l, w2_qk, 16, w2_v, 1024, Xsrc, Xsrc_b, True, kout=(8, o_dk, 128), vout=o_dv)
                attn_phase(l, "diff", Xsrc, Xsrc_b)
                outproj_phase(l, w2_o, 8, OS, None, Xsrc, Xsrc_b)
            else:
                ssd_layer(l, Xsrc, Xsrc_b)
            Xsrc, Xsrc_b = X, X_b
            ffn(l, Xsrc, Xsrc_b)
            chk(10 + l)
        s.barrier()
        s.release()
        nctx = NormCtx()
        for ti in range(NT):
            norm_tile(nctx, 0, 0, ti, Xsrc, Xsrc_b, final_out=yT)
    except _Stop:
        pass
    s.barrier()
    counts = s.emit()
    return nc, counts

def _fm(w):
    K, N = w.shape
    return np.ascontiguousarray(w.reshape(K // 128, 128, N).transpose(1, 0, 2))


def _pc(v):
    return np.ascontiguousarray(v.reshape(-1, 128).T)


def _consts():
    f32 = np.float32
    t = np.arange(LS)
    row = (t // 64).astype(f32)
    col = (t % 64).astype(f32)
    inv = (10000.0 ** (-np.arange(0, 32, 2, dtype=f32) / 32)).astype(f32)
    cos = np.zeros((128, LS), f32)
    sin = np.zeros((128, LS), f32)
    perm = np.zeros((128, 128), f32)
    for p in range(128):
        d = p % 64
        axis = d // 32
        idx = d % 16
        second = (d % 32) >= 16
        ang = (row if axis == 0 else col) * inv[idx]
        cos[p] = np.cos(ang)
        sin[p] = np.sin(ang) * (1.0 if second else -1.0)
        partner = p + 16 if not second else p - 16
        perm[partner, p] = 1.0
    j = np.arange(128)[:, None]
    i = np.arange(512)[None, :]
    wmask = np.zeros((128, 6, 512), f32)
    for mi in range(6):
        o = mi - 1
        wmask[:, mi, :] = (np.abs(i - (o * 128 + j)) <= 128).astype(f32)
    tri = np.zeros((128, 4, 128), f32)
    jj = np.arange(128)[:, None]
    ii = np.arange(128)[None, :]
    tri[:, 0, :] = (jj <= ii)
    tri[:, 1, :] = (jj >= ii)
    tri[0:64, 2, 0:64] = (jj[0:64] <= ii[:, 0:64])
    tri[0:64, 3, 0:64] = (jj[0:64] >= ii[:, 0:64])
    m64 = np.ones((128, 512), f32)
    m64[:, ::64] = 0.0
    return dict(rcos=cos, rsin=sin, perm=perm, wmask=wmask, tri=tri, m64=m64)


_PROG = {}


def _get_prog(nl):
    if nl not in _PROG:
        _PROG[nl] = build_program(nl)
    return _PROG[nl]


def kernel(NLAYERS=4, **inp):
    f32 = np.float32
    g = {k: np.asarray(v) for k, v in inp.items()}
    nc, counts = _get_prog(NLAYERS)
    common = dict(_consts())
    common["ada_w"] = np.ascontiguousarray(g["ada_w"].reshape(4, 8, 128, 6144).transpose(0, 2, 1, 3))
    common["ada_b"] = np.ascontiguousarray(g["ada_b"].reshape(4, 48, 128).transpose(2, 0, 1))
    common["nrm"] = np.ascontiguousarray(np.stack([g["norm_mix"], g["norm_ffn"]], axis=1).reshape(4, 2, 8, 128).transpose(3, 0, 1, 2))
    common["fnorm"] = _pc(g["final_norm"])
    common["w_up"] = np.ascontiguousarray(g["ffn_w_up"].reshape(4, 8, 128, 5632).transpose(0, 2, 1, 3))
    common["w_dn"] = np.ascontiguousarray(g["ffn_w_down"].reshape(4, 22, 128, 1024).transpose(0, 2, 1, 3))
    common["ffn_cw"] = np.ascontiguousarray(g["ffn_conv_w"].reshape(4, 3, 44, 128).transpose(3, 0, 1, 2))
    common["ffn_cb"] = np.ascontiguousarray(g["ffn_conv_b"].reshape(4, 44, 128).transpose(2, 0, 1))
    wq = g["win_w_qkv"][0]
    qcols = wq[:, 0:1024]
    kcols = wq[:, 1024:1280]
    vcols = wq[:, 1280:1536]
    kdup = np.concatenate([np.concatenate([kcols[:, h * 64:(h + 1) * 64]] * 2, axis=1) for h in range(4)], axis=1)
    common["w0_qk"] = _fm(np.concatenate([qcols, kdup], axis=1))
    common["w0_v"] = _fm(vcols)
    common["w0_o"] = _fm(g["win_w_o"][0])
    common["sink"] = np.ascontiguousarray(g["win_sink"][0])
    wg = g["gla_w_qkvr"][0]
    common["w1_qk"] = _fm(wg[:, 0:1024])
    common["w1_v"] = _fm(wg[:, 1024:2048])
    common["w1_r"] = _fm(wg[:, 2048:3072])
    common["w1_g1"] = _fm(np.concatenate([g["gla_w_gf1"][0], g["gla_w_gb1"][0]], axis=1))
    g2 = np.zeros((32, 1024), f32)
    g2[0:16, 0:512] = g["gla_w_gf2"][0]
    g2[16:32, 512:1024] = g["gla_w_gb2"][0]
    common["w1_g2"] = g2
    common["w1_gb"] = _pc(np.concatenate([g["gla_b_gf"][0], g["gla_b_gb"][0]]))
    common["w1_gn"] = _pc(g["gla_norm"][0])
    common["w1_o"] = _fm(g["gla_w_o"][0])
    wd = g["diff_w_qkv"][0]
    common["w2_qk"] = _fm(wd[:, 0:2048])
    common["w2_v"] = _fm(wd[:, 2048:3072])
    common["w2_o"] = _fm(g["diff_w_o"][0])
    common["lqk"] = np.ascontiguousarray(np.stack([g["diff_lq1"][0], g["diff_lk1"][0], g["diff_lq2"][0], g["diff_lk2"][0]]))
    common["w2_gn"] = np.ascontiguousarray(g["diff_norm"][0].reshape(128, 1))
    common.update(_host_ssd_common(g))
    in_maps = []
    for b in range(8):
        m = dict(common)
        xs = g["x_sample"][b]
        xp = g["x_prompt"][2 * b:2 * b + 2].reshape(512, 1024)
        xa = np.concatenate([xs, xp], axis=0)
        m["xT"] = np.ascontiguousarray(xa.T.reshape(8, 128, T).transpose(1, 0, 2))
        cond = np.stack([g["c"][b], g["c_ctx"]], axis=1)
        m["condT"] = np.ascontiguousarray(cond.reshape(8, 128, 2).transpose(1, 0, 2))
        ck = g["cache_win_k"][b, 0]
        kT = ck.transpose(2, 1, 0)
        m["ck0"] = np.ascontiguousarray(np.concatenate([kT, kT], axis=0))
        m["cv0"] = np.ascontiguousarray(g["cache_win_v"][b, 0].reshape(512, 256))
        m["s1f"] = np.ascontiguousarray(g["state_gla_fwd"][b, 0].transpose(1, 0, 2))
        m["s1b"] = np.ascontiguousarray(g["state_gla_bwd"][b, 0].transpose(1, 0, 2))
        dk = g["cache_diff_k"][b, 0]
        m["ck2"] = np.ascontiguousarray(dk.transpose(2, 3, 1, 0).reshape(128, 8, 512))
        m["cv2"] = np.ascontiguousarray(g["cache_diff_v"][b, 0].reshape(512, 1024))
        m.update(_host_ssd_core(g, b))
        in_maps.append(m)
    import os
    ncores = int(os.environ.get("KCORES", "8"))
    ktrace = os.environ.get("KTRACE", "") == "1"
    res = run_bass_kernel_spmd(nc, in_maps[:ncores], core_ids=list(range(ncores)), trace=ktrace) if ktrace else run_bass_kernel_spmd(nc, in_maps[:ncores], core_ids=list(range(ncores)))
    if ktrace:
        print("EXEC_TIME_NS", res.exec_time_ns)
    R = list(res.results) + [res.results[0]] * (8 - ncores)
    y_prompt = np.zeros((16, 256, 1024), f32)
    y_sample = np.zeros((8, 4096, 1024), f32)
    win_k = np.zeros((16, 1, 256, 4, 64), f32)
    win_v = np.zeros((16, 1, 256, 4, 64), f32)
    gla_f = np.zeros((16, 1, 4, 128, 256), f32)
    gla_b = np.zeros((16, 1, 4, 128, 256), f32)
    diff_k = np.zeros((16, 1, 256, 8, 2, 64), f32)
    diff_v = np.zeros((16, 1, 256, 8, 128), f32)
    ssd_f = np.zeros((16, 1, 32, 64, 128), f32)
    ssd_b = np.zeros((16, 1, 32, 64, 128), f32)
    for b in range(8):
        r = R[b]
        y = r["yT"].transpose(1, 0, 2).reshape(1024, T).T
        y_sample[b] = y[0:4096]
        y_prompt[2 * b:2 * b + 2] = y[4096:].reshape(2, 256, 1024)
        wk = r["o_wk"]
        win_k[2 * b:2 * b + 2, 0] = wk.transpose(2, 0, 1).reshape(2, 256, 4, 64)
        win_v[2 * b:2 * b + 2, 0] = r["o_wv"].reshape(2, 256, 4, 64)
        gla_f[2 * b:2 * b + 2, 0] = r["o_gf"].transpose(0, 2, 1, 3)
        gla_b[2 * b:2 * b + 2, 0] = r["o_gb"].transpose(0, 2, 1, 3)
        dk = r["o_dk"]
        diff_k[2 * b:2 * b + 2, 0] = dk.transpose(2, 0, 1).reshape(2, 256, 8, 2, 64)
        diff_v[2 * b:2 * b + 2, 0] = r["o_dv"].reshape(2, 256, 8, 128)
        _host_ssd_out(r, b, ssd_f, ssd_b)
    return (y_prompt, y_sample, win_k, win_v, gla_f, gla_b, diff_k, diff_v, ssd_f, ssd_b)


def _host_ssd_common(g):
    w = g["ssd_w_in"][0]
    out = {}
    out["w3_z"] = _fm(w[:, 0:2048])
    out["w3_xbc"] = _fm(w[:, 2048:5120])
    out["w3_dt"] = _fm(w[:, 5120:5184])
    out["w3_o"] = _fm(g["ssd_w_out"][0])
    out["ssd_cw"] = np.ascontiguousarray(g["ssd_conv_w"][0].reshape(3, 24, 128).transpose(2, 0, 1))
    out["ssd_cb"] = _pc(g["ssd_conv_b"][0])
    out["ssd_vec"] = np.ascontiguousarray(np.stack([g["ssd_a_log_f"][0], g["ssd_a_log_b"][0], g["ssd_dt_bias_f"][0], g["ssd_dt_bias_b"][0], g["ssd_d"][0]]))
    out["ssd_gn"] = np.ascontiguousarray(g["ssd_norm"][0])
    return out


def _host_ssd_core(g, b):
    return {"s3f": np.ascontiguousarray(g["state_ssd_fwd"][b, 0].transpose(2, 0, 1).reshape(128, 2048)),
            "s3b": np.ascontiguousarray(g["state_ssd_bwd"][b, 0].transpose(2, 0, 1).reshape(128, 2048))}


def _host_ssd_out(r, b, ssd_f, ssd_b):
    ssd_f[2 * b:2 * b + 2, 0] = r["o_sf"].reshape(2, 128, 32, 64).transpose(0, 2, 3, 1)
    ssd_b[2 * b:2 * b + 2, 0] = r["o_sb"].reshape(2, 128, 32, 64).transpose(0, 2, 3, 1)
```

```python
import numpy as np
import concourse.bass as bass
import concourse.mybir as mybir
from concourse.bass_utils import run_bass_kernel_spmd

F32 = mybir.dt.float32
BF16 = mybir.dt.bfloat16
AF = mybir.ActivationFunctionType
ALU = mybir.AluOpType
AX = mybir.AxisListType
AP = bass.AP

SB_BASE = 16512
SB_TOP = 229344
EPOCH = 30000


class Buf:
    __slots__ = ("w", "rs", "name")

    def __init__(self, name=""):
        self.w = None
        self.rs = []
        self.name = name


class Op:
    __slots__ = ("eng", "fn", "deps", "dma", "ms", "sem", "val", "need", "prev", "grp")

    def __init__(self, eng, fn, dma):
        self.eng = eng
        self.fn = fn
        self.dma = dma
        self.deps = []
        self.ms = None
        self.sem = None
        self.val = None
        self.need = False
        self.prev = None
        self.grp = None


class Sched:
    ENGS = ("pe", "act", "dve", "pool", "sp")

    def __init__(self, nc):
        self.nc = nc
        self.ops = {e: [] for e in self.ENGS}
        self.dmas_since_barrier = []
        self.all_dmas = []
        self.nps = 0
        self.nacc = 0
        self.psum = []
        for i in range(8):
            t = nc.alloc_psum_tensor("psb%d" % i, [128, 512], F32)
            self.psum.append((t, Buf("ps%d" % i)))
        self.sb_off = SB_BASE
        self.sb_mark = SB_BASE
        self.nalloc = 0

    def alloc(self, shape, dtype, name=None):
        nbytes = int(np.prod(shape[1:])) * (4 if dtype == F32 else 2)
        nbytes = (nbytes + 63) // 64 * 64
        off = self.sb_off
        assert off + nbytes <= SB_TOP, "SBUF overflow %d" % (off + nbytes - SB_TOP)
        self.sb_off += nbytes
        self.nalloc += 1
        t = self.nc.alloc_sbuf_tensor_at("sb%d_%s" % (self.nalloc, name or "t"), list(shape), dtype, offset=off)
        return t

    def mark(self):
        self.sb_mark = self.sb_off

    def release(self):
        self.sb_off = self.sb_mark

    def ps(self, pool="a"):
        if pool == "a":
            t, b = self.psum[self.nps % 6]
            self.nps += 1
        else:
            t, b = self.psum[6 + self.nacc % 2]
            self.nacc += 1
        return t, b

    def op(self, eng, fn, reads=(), writes=(), dma=False):
        o = Op(eng, fn, dma)
        deps = {}
        for b in reads:
            d = b.w
            if d is not None:
                if (not dma) and (not d.dma) and d.eng == eng and eng == "pe":
                    continue
                deps[id(d)] = d
        for b in writes:
            cand = list(b.rs)
            if b.w is not None:
                cand.append(b.w)
            for d in cand:
                if d is o:
                    continue
                if (not dma) and (not d.dma) and d.eng == eng:
                    continue
                deps[id(d)] = d
        o.deps = list(deps.values())
        for b in reads:
            if not dma:
                b.rs = [r for r in b.rs if r.dma or r.eng != eng]
            b.rs.append(o)
        for b in writes:
            b.w = o
            b.rs = []
        self.ops[eng].append(o)
        if dma:
            self.dmas_since_barrier.append(o)
            self.all_dmas.append(o)
        return o

    def barrier(self):
        lasts = []
        for e in self.ENGS:
            for o in reversed(self.ops[e]):
                if not o.dma and o.fn is not None:
                    lasts.append(o)
                    break
        deps = lasts + self.dmas_since_barrier
        self.dmas_since_barrier = []
        for e in self.ENGS:
            o = Op(e, None, False)
            o.deps = list(deps)
            self.ops[e].append(o)

    def dma(self, out, in_, reads=(), writes=(), eng=None):
        if eng is None:
            eng = "pool" if type(out.tensor).__name__.startswith("DRam") else "sp"
        return self.op(eng, lambda e: e.dma_start(out=out, in_=in_), reads, writes, dma=True)

    def mm(self, out, lhsT, rhs, start, stop, reads=(), writes=(), grp=None):
        o = self.op("pe", lambda e: e.matmul(out, lhsT, rhs, start=start, stop=stop), reads, writes)
        o.grp = grp
        return o

    def emit(self):
        nc = self.nc
        for e in self.ENGS:
            for o in self.ops[e]:
                for d in o.deps:
                    d.need = True
        sem_ctx = []
        import contextlib
        with contextlib.ExitStack() as st:
            tl = {}
            for e in ("pe", "act", "dve", "pool"):
                tl[e] = [st.enter_context(nc.semaphore("tl_%s_%d" % (e, i))) for i in range(5)]
            npool = {"sp": 36, "pool": 36, "act": 8}
            dpool = {e: [st.enter_context(nc.semaphore("dq_%s_%d" % (e, i))) for i in range(n)] for e, n in npool.items()}
            for e in self.ENGS:
                m = 0
                k = 0
                for o in self.ops[e]:
                    if o.fn is None:
                        continue
                    if o.dma:
                        P = len(dpool[e])
                        o.sem = dpool[e][k % P]
                        o.val = 16 * (k // P + 1)
                        k += 1
                    elif o.need:
                        o.sem = tl[e][m // EPOCH]
                        o.val = m % EPOCH + 1
                        m += 1
                assert m < EPOCH * 5, (e, m)
            final = {}
            for e, n in npool.items():
                for o in self.ops[e]:
                    if o.dma:
                        final[id(o.sem)] = (o.sem, o.val)

            def stream(e, eng):
                seen = {}

                def wait(sem, val):
                    if seen.get(id(sem), 0) < val:
                        eng.wait_ge(sem, val)
                        seen[id(sem)] = val

                ops_ = self.ops[e]
                i_ = 0
                while i_ < len(ops_):
                    o = ops_[i_]
                    j_ = i_ + 1
                    if o.grp is not None:
                        while j_ < len(ops_) and ops_[j_].grp == o.grp:
                            j_ += 1
                    for k_ in range(i_, j_):
                        for d in ops_[k_].deps:
                            wait(d.sem, d.val)
                    for k_ in range(i_, j_):
                        o = ops_[k_]
                        if o.fn is None:
                            continue
                        if o.dma and o.val > 16:
                            wait(o.sem, o.val - 16)
                        ins_ = o.fn(eng)
                        if o.dma:
                            ins_.then_inc(o.sem, 16)
                        elif o.need:
                            ins_.then_inc(o.sem, 1)
                    i_ = j_
                if e == "sp":
                    for sem, val in final.values():
                        wait(sem, val)

            with nc.Block() as block:
                @block.tensor
                def _(eng):
                    stream("pe", eng)

                @block.scalar
                def _(eng):
                    stream("act", eng)

                @block.vector
                def _(eng):
                    stream("dve", eng)

                @block.gpsimd
                def _(eng):
                    stream("pool", eng)

                @block.sync
                def _(eng):
                    stream("sp", eng)
        return {e: len(v) for e, v in self.ops.items()}


def ins(method, *a, **kw):
    return lambda e: getattr(e, method)(*a, **kw)


T = 4608
LS = 4096
TILES = [(i * 512, 512, 0, 0, 4096) for i in range(8)] + [(4096, 256, 1, 4096, 4352), (4352, 256, 1, 4352, 4608)]
EPS = 1e-6
DFF = 2816
LAM_INIT = 0.8 - 0.6 * float(np.exp(-0.3 * 2))


def build_program(NLAYERS=4, dbg=False):
    nc = bass.Bass("TRN2", target_bir_lowering=False)
    s = Sched(nc)
    I = {}
    O = {}

    def din(name, shape):
        I[name] = nc.dram_tensor(name, list(shape), F32, kind="ExternalInput").ap()
        return I[name]

    def dout(name, shape):
        O[name] = nc.dram_tensor(name, list(shape), F32, kind="ExternalOutput").ap()
        return O[name]

    def dscr(name, shape, dt):
        return nc.dram_tensor(name, list(shape), dt).ap()

    xT = din("xT", [128, 8, T])
    condT = din("condT", [128, 8, 2])
    ada_w = din("ada_w", [4, 128, 8, 6144])
    ada_b = din("ada_b", [128, 4, 48])
    nrm = din("nrm", [128, 4, 2, 8])
    fnorm = din("fnorm", [128, 8])
    w_up = din("w_up", [4, 128, 8, 5632])
    w_dn = din("w_dn", [4, 128, 22, 1024])
    ffn_cw = din("ffn_cw", [128, 4, 3, 44])
    ffn_cb = din("ffn_cb", [128, 4, 44])
    perm_in = din("perm", [128, 128])
    rcos = din("rcos", [128, LS])
    rsin = din("rsin", [128, LS])
    wmask_in = din("wmask", [128, 6, 512])
    tri_in = din("tri", [128, 4, 128])
    m64_in = din("m64", [128, 512])
    w0_qk = din("w0_qk", [128, 8, 1536])
    w0_v = din("w0_v", [128, 8, 256])
    w0_o = din("w0_o", [128, 8, 1024])
    sink_in = din("sink", [16])
    ck0 = din("ck0", [128, 4, 512])
    cv0 = din("cv0", [512, 256])
    w1_qk = din("w1_qk", [128, 8, 1024])
    w1_v = din("w1_v", [128, 8, 1024])
    w1_r = din("w1_r", [128, 8, 1024])
    w1_g1 = din("w1_g1", [128, 8, 32])
    w1_g2 = din("w1_g2", [32, 1024])
    w1_gb = din("w1_gb", [128, 8])
    w1_gn = din("w1_gn", [128, 2])
    w1_o = din("w1_o", [128, 8, 1024])
    s1f = din("s1f", [128, 4, 256])
    s1b = din("s1b", [128, 4, 256])
    w2_qk = din("w2_qk", [128, 8, 2048])
    w2_v = din("w2_v", [128, 8, 1024])
    w2_o = din("w2_o", [128, 8, 1024])
    lqk = din("lqk", [4, 64])
    w2_gn = din("w2_gn", [128, 1])
    ck2 = din("ck2", [128, 8, 512])
    cv2 = din("cv2", [512, 1024])

    w3_z = din("w3_z", [128, 8, 2048])
    w3_xbc = din("w3_xbc", [128, 8, 3072])
    w3_dt = din("w3_dt", [128, 8, 64])
    w3_o = din("w3_o", [128, 16, 1024])
    ssd_cw = din("ssd_cw", [128, 3, 24])
    ssd_cb = din("ssd_cb", [128, 24])
    ssd_vec = din("ssd_vec", [5, 32])
    ssd_gn = din("ssd_gn", [2048])
    s3f = din("s3f", [128, 2048])
    s3b = din("s3b", [128, 2048])

    yT = dout("yT", [128, 8, T])
    o_sf = dout("o_sf", [2, 128, 2048])
    o_sb = dout("o_sb", [2, 128, 2048])
    o_wk = dout("o_wk", [4, 64, 512])
    o_wv = dout("o_wv", [512, 256])
    o_gf = dout("o_gf", [2, 128, 4, 256])
    o_gb = dout("o_gb", [2, 128, 4, 256])
    o_dk = dout("o_dk", [8, 128, 512])
    o_dv = dout("o_dv", [512, 1024])

    X = dscr("X", [128, 8, T], F32)
    U = dscr("U", [128, 44, T], BF16)
    QK = dscr("QK", [128, 16, T], BF16)
    VS = dscr("VS", [T, 1024], BF16)
    OS = dscr("OS", [128, 8, T], BF16)
    OS2 = dscr("OS2", [128, 8, T], BF16)
    RS = dscr("RS", [128, 8, T], BF16)
    QD = dscr("QD", [2, 128, 4, T], BF16)
    KD = dscr("KD", [2, 128, 4, T], BF16)
    KR = dscr("KR", [T, 8, 128], BF16)
    ZS = dscr("ZS", [T, 2048], BF16)
    DTS = dscr("DTS", [T, 128], F32)
    XTs = dscr("XTs", [T, 2048], BF16)
    BTs = dscr("BTs", [T, 512], BF16)
    YD = [dscr("YD0", [T, 2048], BF16), dscr("YD1", [T, 2048], BF16)]

    NT = len(TILES)
    import os
    STOP = int(os.environ.get("KSTOP", "99"))

    def rows_view(dr, rowlen, r0, nb, c0, w):
        return AP(dr.tensor, r0 * rowlen + c0, [[rowlen, 128], [128 * rowlen, nb], [1, w]])

    def kr_view(r0, nb, j0, nj):
        return AP(KR.tensor, r0 * 1024 + j0 * 128, [[1024, 128], [128 * 1024, nb], [128, nj], [1, 128]])
    import os
    KQ = os.environ.get("KQ", "")

    def tb(name):
        return [Buf(name + str(i)) for i in range(NT)]

    xT_b = tb("xT")
    X_b = tb("X")

    MODS = s.alloc([128, 4, 6, 8, 2], F32, "mods")
    GS = s.alloc([128, 4, 2, 8, 2], F32, "gs")
    NRM = s.alloc([128, 4, 2, 8], F32, "nrm")
    FNRM = s.alloc([128, 8], F32, "fnrm")
    ones_bf = s.alloc([128, 128], BF16, "ones")
    ones_f = s.alloc([128, 128], F32, "onesf")
    ident_bf = s.alloc([128, 128], BF16, "ident")
    perm_bf = s.alloc([128, 128], BF16, "perm")
    tri_bf = s.alloc([128, 4, 128], BF16, "tri")
    m64 = s.alloc([128, 512], F32, "m64")
    cb_const = Buf("consts")
    stage = s.alloc([128, 4, 128], F32, "stage")
    bstage = Buf()

    s.dma(NRM[:], nrm, writes=[cb_const])
    s.dma(FNRM[:], fnorm, writes=[cb_const])
    s.dma(m64[:], m64_in, writes=[cb_const])
    s.op("pool", ins("memset", ones_f[:], 1.0), writes=[cb_const])
    s.op("dve", ins("tensor_copy", out=ones_bf[:], in_=ones_f[:]), reads=[cb_const], writes=[cb_const])
    s.dma(stage[:, 0, :], perm_in, writes=[bstage])
    s.op("dve", ins("tensor_copy", out=perm_bf[:], in_=stage[:, 0, :]), reads=[bstage], writes=[cb_const])
    s.op("pool", ins("memset", stage[:, 1, :], 1.0), reads=[], writes=[bstage])
    s.op("pool", ins("affine_select", out=stage[:, 1, :], in_=stage[:, 1, :], pattern=[[-1, 128]], compare_op=ALU.is_equal, fill=0.0, base=0, channel_multiplier=1), reads=[bstage], writes=[bstage])
    s.op("dve", ins("tensor_copy", out=ident_bf[:], in_=stage[:, 1, :]), reads=[bstage], writes=[cb_const])
    s.barrier()
    s.dma(stage[:], tri_in, writes=[bstage])
    s.op("dve", ins("tensor_copy", out=tri_bf[:], in_=stage[:]), reads=[bstage], writes=[cb_const])
    s.mark()

    def mod_ap(l, k, c, cond):
        return MODS[:, l, k, c, cond:cond + 1]

    def prologue():
        s.barrier()
        s.release()
        sc = s.alloc([128, 8, 2], F32, "sc")
        scb = Buf()
        adb = s.alloc([128, 4, 48], F32, "adb")
        adbb = Buf()
        s.dma(sc[:], condT, writes=[scb])
        s.dma(adb[:], ada_b, writes=[adbb])
        s.op("act", ins("activation", out=sc[:], in_=sc[:], func=AF.Silu), reads=[scb], writes=[scb])
        wb = [s.alloc([128, 8, 1024], F32, "adaw%d" % i) for i in range(2)]
        wbb = [[Buf() for _ in range(2)] for _ in range(2)]
        it = 0
        for l in range(NLAYERS):
            for g in range(6):
                w = wb[it % 2]
                bb = wbb[it % 2]
                for hh in range(2):
                    s.dma(w[:, hh * 4:(hh + 1) * 4, :], ada_w[l, :, hh * 4:(hh + 1) * 4, g * 1024:(g + 1) * 1024], writes=[bb[hh]], eng=("sp" if hh == 0 else "act"))
                ps, pb = s.ps()
                for cc in range(8):
                    for kc in range(8):
                        s.mm(ps[:, cc * 2:cc * 2 + 2], w[:, kc, cc * 128:(cc + 1) * 128], sc[:, kc, :], kc == 0, kc == 7, reads=[bb[kc // 4], scb], writes=[pb])
                s.op("dve", ins("tensor_tensor",
                    out=MODS[:, l, g, :, :], in0=ps[:, 0:16].rearrange("p (c t) -> p c t", t=2),
                    in1=adb[:, l, g * 8:(g + 1) * 8].unsqueeze(2).to_broadcast([128, 8, 2]), op=ALU.add),
                    reads=[pb, adbb], writes=[cb_const])
                it += 1
            for which in range(2):
                k = 1 if which == 0 else 4
                s.op("dve", ins("scalar_tensor_tensor",
                    out=GS[:, l, which, :, :], in0=MODS[:, l, k, :, :], scalar=1.0,
                    in1=NRM[:, l, which, :].unsqueeze(2).to_broadcast([128, 8, 2]), op0=ALU.add, op1=ALU.mult),
                    reads=[cb_const], writes=[cb_const])

    def load_w(dst, src, nk, split=1):
        bufs = []
        for kc in range(nk):
            b = Buf()
            s.dma(dst[:, kc, :], src[:, kc, :], writes=[b], eng="pool")
            bufs.append(b)
        return bufs

    class NormCtx:
        def __init__(self, nbuf=2):
            self.nbuf = nbuf
            self.xt = [(s.alloc([128, 8, 512], F32, "nxt"), Buf()) for _ in range(nbuf)]
            self.h = [(s.alloc([128, 8, 512], BF16, "nh"), Buf()) for _ in range(nbuf)]
            self.sq = (s.alloc([128, 8, 512], BF16, "nsq"), Buf())
            self.tmp = (s.alloc([128, 8, 512], F32, "ntmp"), Buf())
            self.rstd = [(s.alloc([128, 512], F32, "nrs"), Buf()) for _ in range(2)]
            self.k = 0

    def norm_tile(ctx, l, which, ti, Xsrc, Xsrc_b, final_out=None):
        t0, n, cond, _, _ = TILES[ti]
        k = ctx.k
        ctx.k += 1
        xt, xtb = ctx.xt[k % ctx.nbuf]
        h, hb = ctx.h[k % ctx.nbuf]
        sq, sqb = ctx.sq
        tmp, tmpb = ctx.tmp
        rstd, rb = ctx.rstd[k % 2]
        s.dma(xt[:, :, :n], Xsrc[:, :, t0:t0 + n], reads=[Xsrc_b[ti]], writes=[xtb])
        s.op("act", ins("activation", out=sq[:, :, :n], in_=xt[:, :, :n], func=AF.Square), reads=[xtb], writes=[sqb])
        ps, pb = s.ps()
        for c in range(8):
            s.mm(ps[:, :n], ones_bf[:], sq[:, c, :n], c == 0, c == 7, reads=[sqb, cb_const], writes=[pb])
        s.op("act", ins("activation", out=rstd[:, :n], in_=ps[:, :n], func=AF.Sqrt, scale=1.0 / 1024, bias=EPSB[:, 0:1]), reads=[pb], writes=[rb])
        s.op("dve", ins("reciprocal", out=rstd[:, :n], in_=rstd[:, :n]), reads=[rb], writes=[rb])
        s.op("dve", ins("tensor_tensor", out=tmp[:, :, :n], in0=xt[:, :, :n], in1=rstd[:, :n].unsqueeze(1).to_broadcast([128, 8, n]), op=ALU.mult),
             reads=[xtb, rb], writes=[tmpb])
        if final_out is None:
            for c in range(8):
                s.op("act", ins("activation", out=h[:, c, :n], in_=tmp[:, c, :n], func=AF.Identity,
                                                          scale=GS[:, l, which, c, cond:cond + 1], bias=mod_ap(l, 0 if which == 0 else 3, c, cond)),
                     reads=[tmpb, cb_const], writes=[hb])
            return h, hb
        else:
            for c in range(8):
                s.op("act", ins("activation", out=xt[:, c, :n], in_=tmp[:, c, :n], func=AF.Identity, scale=FNRM[:, c:c + 1]),
                     reads=[tmpb, cb_const], writes=[xtb])
            s.dma(final_out[:, :, t0:t0 + n], xt[:, :, :n], reads=[xtb])
            return None, None

    EPSB = s.alloc([128, 1], F32, "epsb")
    s.op("pool", ins("memset", EPSB[:], EPS), writes=[cb_const])
    s.mark()

    class ResCtx:
        def __init__(self):
            self.xo = [(s.alloc([128, 8, 512], F32, "xo"), Buf()) for _ in range(2)]
            self.k = 0

    def outproj_resid(rctx, l, gate_k, ti, W, wbufs, nk, rhs, rhsbufs, Xsrc, Xsrc_b, Xdst, Xdst_b):
        t0, n, cond, _, _ = TILES[ti]
        xo, xob = rctx.xo[rctx.k % 2]
        rctx.k += 1
        s.dma(xo[:, :, :n], Xsrc[:, :, t0:t0 + n], reads=[Xsrc_b[ti]], writes=[xob])
        for dc in range(8):
            ps, pb = s.ps()
            for kc in range(nk):
                s.mm(ps[:, :n], W[:, kc, dc * 128:(dc + 1) * 128], rhs[:, kc, :n], kc == 0, kc == nk - 1,
                     reads=[wbufs[kc]] + list(rhsbufs), writes=[pb])
            s.op("dve", ins("scalar_tensor_tensor", out=xo[:, dc, :n], in0=ps[:, :n], scalar=mod_ap(l, gate_k, dc, cond),
                                                                      in1=xo[:, dc, :n], op0=ALU.mult, op1=ALU.add),
                 reads=[pb, xob, cb_const], writes=[xob])
        s.dma(Xdst[:, :, t0:t0 + n], xo[:, :, :n], reads=[xob], writes=[Xdst_b[ti]])

    def ffn(l, Xsrc, Xsrc_b):
        s.barrier()
        s.release()
        Wup = s.alloc([128, 8, 5632], BF16, "wup")
        wb = load_w(Wup, w_up[l], 8)
        nctx = NormCtx()
        ub = [(s.alloc([128, 11, 512], BF16, "ub"), [Buf() for _ in range(11)]) for _ in range(2)]
        U_b = [[Buf() for _ in range(4)] for _ in range(NT)]
        ku = 0
        ev = 0
        for ti in range(NT):
            t0, n, cond, _, _ = TILES[ti]
            h, hb = norm_tile(nctx, l, 1, ti, Xsrc, Xsrc_b)
            for grp in range(4):
                ut, utb = ub[ku % 2]
                ku += 1
                for cc in range(11):
                    col = grp * 11 + cc
                    ps, pb = s.ps()
                    for kc in range(8):
                        s.mm(ps[:, :n], Wup[:, kc, col * 128:(col + 1) * 128], h[:, kc, :n], kc == 0, kc == 7, reads=[wb[kc], hb], writes=[pb])
                    if ev % 2 == 0:
                        s.op("act", ins("activation", out=ut[:, cc, :n], in_=ps[:, :n], func=AF.Identity), reads=[pb], writes=[utb[cc]])
                    else:
                        s.op("dve", ins("tensor_copy", out=ut[:, cc, :n], in_=ps[:, :n]), reads=[pb], writes=[utb[cc]])
                    ev += 1
                s.dma(U[:, grp * 11:(grp + 1) * 11, t0:t0 + n], ut[:, :, :n], reads=utb, writes=[U_b[ti][grp]])
        s.barrier()
        s.release()
        Wd = s.alloc([128, 22, 1024], BF16, "wd")
        wdb = load_w(Wd, w_dn[l], 22)
        cw = s.alloc([128, 3, 44], F32, "cw")
        cbias = s.alloc([128, 44], F32, "cbias")
        cwb = Buf()
        s.dma(cw[:], ffn_cw[:, l, :, :], writes=[cwb])
        s.dma(cbias[:], ffn_cb[:, l, :], writes=[cwb])
        ug = s.alloc([128, 44, 514], BF16, "ug")
        ugb = [Buf() for _ in range(4)]
        at = [(s.alloc([128, 22, 512], BF16, "at"), Buf()) for _ in range(2)]
        tmps = [[(s.alloc([128, 512], F32, "ft"), Buf()) for _ in range(3)] for _ in range(4)]
        rctx = ResCtx()
        kp = 0
        for ti in range(NT):
            t0, n, cond, s0, s1 = TILES[ti]
            lo = (t0 - 1) >= s0
            hi = (t0 + n) < s1
            for grp in range(4):
                gs_ = slice(grp * 11, (grp + 1) * 11)
                if not lo:
                    s.op("pool", ins("memset", ug[:, gs_, 0:1], 0.0), writes=[ugb[grp]])
                if not hi:
                    s.op("pool", ins("memset", ug[:, gs_, n + 1:n + 2], 0.0), writes=[ugb[grp]])
                c0 = 0 if lo else 1
                c1 = n + 2 if hi else n + 1
                rd = [U_b[ti][grp]]
                if lo:
                    rd.append(U_b[ti - 1][grp])
                if hi:
                    rd.append(U_b[ti + 1][grp])
                s.dma(ug[:, gs_, c0:c1], U[:, gs_, t0 - 1 + c0:t0 - 1 + c1], reads=rd, writes=[ugb[grp]])
            a, ab = at[ti % 2]
            pst = {}

            def f2s1(cc):
                nonlocal kp
                res = []
                for half in range(2):
                    col = cc + 22 * half
                    tt, ttb = tmps[kp % 4][half]
                    grp = col // 11
                    s.op("act", ins("activation", out=tt[:, :n], in_=ug[:, col, 1:n + 1], func=AF.Identity,
                                    scale=cw[:, 1, col:col + 1], bias=cbias[:, col:col + 1]),
                         reads=[ugb[grp], cwb], writes=[ttb])
                    res.append((tt, ttb, col, grp))
                sg, sgb = tmps[kp % 4][2]
                kp += 1
                pst[cc] = (res, sg, sgb)

            def f2s2(cc):
                res, sg, sgb = pst[cc]
                for (tt, ttb, col, grp) in res:
                    s.op("dve", ins("scalar_tensor_tensor", out=tt[:, :n], in0=ug[:, col, 0:n], scalar=cw[:, 0, col:col + 1],
                                    in1=tt[:, :n], op0=ALU.mult, op1=ALU.add),
                         reads=[ugb[grp], cwb, ttb], writes=[ttb])
                    s.op("dve", ins("scalar_tensor_tensor", out=tt[:, :n], in0=ug[:, col, 2:n + 2], scalar=cw[:, 2, col:col + 1],
                                    in1=tt[:, :n], op0=ALU.mult, op1=ALU.add),
                         reads=[ugb[grp], cwb, ttb], writes=[ttb])

            def f2s3(cc):
                res, sg, sgb = pst.pop(cc)
                s.op("act", ins("activation", out=sg[:, :n], in_=res[0][0][:, :n], func=AF.Silu), reads=[res[0][1]], writes=[sgb])
                s.op("pool", ins("tensor_tensor", out=a[:, cc, :n], in0=sg[:, :n], in1=res[1][0][:, :n], op=ALU.mult),
                     reads=[sgb, res[1][1]], writes=[ab])

            for t_ in range(22 + 2):
                if t_ < 22:
                    f2s1(t_)
                if 1 <= t_ < 23:
                    f2s2(t_ - 1)
                if t_ >= 2:
                    f2s3(t_ - 2)
            outproj_resid(rctx, l, 5, ti, Wd, wdb, 22, a, [ab], Xsrc, Xsrc_b, X, X_b)

    def qkv_phase(l, Wqk_src, nqk, Wv_src, nv, Xsrc, Xsrc_b, rope, kout=None, vout=None, post=None):
        s.barrier()
        s.release()
        Wqk = s.alloc([128, 8, nqk * 128], BF16, "wqk")
        wqb = load_w(Wqk, Wqk_src, 8)
        Wv = s.alloc([128, 8, nv], BF16, "wv")
        wvb = load_w(Wv, Wv_src, 8)
        nctx = NormCtx()
        qk = [(s.alloc([128, nqk, 512], BF16, "qk"), [Buf() for _ in range(nqk)]) for _ in range(2)]
        vt = [(s.alloc([128, 4, nv], BF16, "vt"), Buf()) for _ in range(2)]
        qb = [(s.alloc([128, 512], BF16, "qb"), Buf()) for _ in range(3)]
        t12 = [[(s.alloc([128, 512], F32, "rt"), Buf()) for _ in range(2)] for _ in range(3)]
        cs = [(s.alloc([128, 2, 512], F32, "cs"), Buf()) for _ in range(1)]
        kf = [(s.alloc([128, 256], F32, "kf"), Buf()) for _ in range(2)]
        vf = [(s.alloc([128, 512], F32, "vf"), Buf()) for _ in range(2)]
        QK_b = [Buf() for _ in range(NT)]
        VS_b = [Buf() for _ in range(NT)]
        if "L" in KQ:
            return
        kq = 0
        kk = 0
        kv = 0
        for ti in range(NT):
            t0, n, cond, s0, s1 = TILES[ti]
            h, hb = norm_tile(nctx, l, 0, ti, Xsrc, Xsrc_b)
            if "N" in KQ:
                continue
            qkt, qkb = qk[ti % 2]
            dorope = rope and cond == 0 and ("r" not in KQ)
            if dorope:
                cst, csb = cs[0]
                s.dma(cst[:, 0, :], rcos[:, t0:t0 + n], writes=[csb])
                s.dma(cst[:, 1, :], rsin[:, t0:t0 + n], writes=[csb])
            ropeq = []

            def rope_fin(item):
                cc_, q_, q_b, t1, t1b, t2, t2b = item
                ps2, pb2 = s.ps()
                s.mm(ps2[:, :n], perm_bf[:], q_[:, :n], True, True, reads=[q_b, cb_const], writes=[pb2])
                s.op("dve", ins("tensor_tensor", out=t1[:, :n], in0=q_[:, :n], in1=cst[:, 0, :n], op=ALU.mult),
                     reads=[q_b, csb], writes=[t1b])
                s.op("dve", ins("tensor_tensor", out=t2[:, :n], in0=ps2[:, :n], in1=cst[:, 1, :n], op=ALU.mult),
                     reads=[pb2, csb], writes=[t2b])
                s.op("pool", ins("tensor_tensor", out=qkt[:, cc_, :n], in0=t1[:, :n], in1=t2[:, :n], op=ALU.add),
                     reads=[t1b, t2b], writes=[qkb[cc_]])

            for cc in range(nqk):
                ps, pb = s.ps()
                for kc in range(8):
                    s.mm(ps[:, :n], Wqk[:, kc, cc * 128:(cc + 1) * 128], h[:, kc, :n], kc == 0, kc == 7, reads=[wqb[kc], hb], writes=[pb])
                if dorope:
                    q_, q_b = qb[kq % 3]
                    t1, t1b = t12[kq % 3][0]
                    t2, t2b = t12[kq % 3][1]
                    kq += 1
                    s.op("act", ins("activation", out=q_[:, :n], in_=ps[:, :n], func=AF.Identity), reads=[pb], writes=[q_b])
                    ropeq.append((cc, q_, q_b, t1, t1b, t2, t2b))
                    if len(ropeq) > 1:
                        rope_fin(ropeq.pop(0))
                else:
                    if kout is not None and cond == 1 and cc >= kout[0]:
                        kft, kfb = kf[kk % 2]
                        kk += 1
                        rows = kout[2]
                        s.op("dve", ins("tensor_copy", out=kft[:, :n], in_=ps[:, :n]), reads=[pb], writes=[kfb])
                        s.op("act", ins("activation", out=qkt[:, cc, :n], in_=kft[:, :n], func=AF.Identity), reads=[kfb], writes=[qkb[cc]])
                        s.dma(kout[1][cc - kout[0], :, t0 - LS:t0 - LS + n], kft[0:rows, :n], reads=[kfb])
                    else:
                        s.op("act", ins("activation", out=qkt[:, cc, :n], in_=ps[:, :n], func=AF.Identity), reads=[pb], writes=[qkb[cc]])
                if post is not None:
                    post(ti, cc, ps, pb)
            while ropeq:
                rope_fin(ropeq.pop(0))
            vtt, vtb = vt[ti % 2]
            for tbk in range(n // 128 if "v" not in KQ else 0):
                for vg in range((nv + 511) // 512):
                    w_ = min(512, nv - vg * 512)
                    ps, pb = s.ps()
                    for kc in range(8):
                        s.mm(ps[:, :w_], h[:, kc, tbk * 128:(tbk + 1) * 128], Wv[:, kc, vg * 512:vg * 512 + w_], kc == 0, kc == 7, reads=[wvb[kc], hb], writes=[pb])
                    if vout is not None and cond == 1:
                        vft, vfb = vf[kv % 2]
                        kv += 1
                        s.op("dve", ins("tensor_copy", out=vft[:, :w_], in_=ps[:, :w_]), reads=[pb], writes=[vfb])
                        s.op("act", ins("activation", out=vtt[:, tbk, vg * 512:vg * 512 + w_], in_=vft[:, :w_], func=AF.Identity),
                             reads=[vfb], writes=[vtb])
                        r0 = t0 - LS + tbk * 128
                        s.dma(vout[r0:r0 + 128, vg * 512:vg * 512 + w_], vft[:, :w_], reads=[vfb])
                    else:
                        s.op("act", ins("activation", out=vtt[:, tbk, vg * 512:vg * 512 + w_], in_=ps[:, :w_], func=AF.Identity),
                             reads=[pb], writes=[vtb])
            if "q" not in KQ:
                s.dma(QK[:, 0:nqk, t0:t0 + n], qkt[:, :, :n], reads=qkb, writes=[QK_b[ti]])
            if "v" not in KQ and "s" not in KQ:
              s.dma(rows_view(VS, 1024, t0, n // 128, 0, nv), vtt[:, 0:n // 128, :], reads=[vtb], writes=[VS_b[ti]])

    def attn_phase(l, kind, Xsrc, Xsrc_b):
        s.barrier()
        s.release()
        win = kind == "win"
        nunits = 8
        kbase = 8
        Kt = [(s.alloc([128, T], BF16, "kt"), Buf()) for _ in range(2)]
        Va = [(s.alloc([128, 36, 128], BF16, "va"), Buf()) for _ in range(2)]
        Qt = [(s.alloc([128, LS], BF16, "qt"), Buf()) for _ in range(2)]
        Ot = [(s.alloc([128, LS], BF16, "ot"), Buf()) for _ in range(2)]
        pT = [(s.alloc([128, 512], BF16, "pT"), Buf()) for _ in range(8)]
        rec = [(s.alloc([128, 512], F32, "rec"), Buf()) for _ in range(8)]
        OS_b = [[Buf() for _ in range(nunits)] for _ in range(3)]
        cbuf = Buf()
        if win:
            masks = s.alloc([128, 6, 512], BF16, "masks")
            s.dma(masks[:], wmask_in, writes=[cbuf], eng="pool")
            esink = s.alloc([128, 16], F32, "esink")
            s.dma(esink[:], sink_in.partition_broadcast(128), writes=[cbuf])
            s.op("act", ins("activation", out=esink[:], in_=esink[:], func=AF.Exp), reads=[cbuf], writes=[cbuf])
            for i in range(2):
                s.op("pool", ins("memset", Va[i][0][:, :, 64:128], 1.0), writes=[Va[i][1]])
        else:
            acc = [(s.alloc([128, 512], F32, "acc"), Buf()) for _ in range(8)]
            osb = [(s.alloc([128, 512], F32, "osb"), Buf()) for _ in range(8)]
            sqd = [(s.alloc([128, 512], BF16, "sqd"), Buf()) for _ in range(4)]
            lq = s.alloc([128, 4, 64], F32, "lq")
            lam = s.alloc([128, 4], F32, "lam")
            gsub = s.alloc([128, 1], F32, "gsub")
            s.dma(lq[:], lqk.rearrange("a b -> (a b)").partition_broadcast(128).rearrange("p (a b) -> p a b", a=4), writes=[cbuf])
            s.dma(gsub[:], w2_gn, writes=[cbuf])
            s.op("dve", ins("tensor_tensor", out=lq[:, 0, :], in0=lq[:, 0, :], in1=lq[:, 1, :], op=ALU.mult), reads=[cbuf], writes=[cbuf])
            s.op("dve", ins("tensor_tensor", out=lq[:, 2, :], in0=lq[:, 2, :], in1=lq[:, 3, :], op=ALU.mult), reads=[cbuf], writes=[cbuf])
            s.op("dve", ins("reduce_sum", out=lam[:, 0:1], in_=lq[:, 0, :], axis=AX.X), reads=[cbuf], writes=[cbuf])
            s.op("dve", ins("reduce_sum", out=lam[:, 1:2], in_=lq[:, 2, :], axis=AX.X), reads=[cbuf], writes=[cbuf])
            s.op("act", ins("activation", out=lam[:, 0:2], in_=lam[:, 0:2], func=AF.Exp), reads=[cbuf], writes=[cbuf])
            s.op("dve", ins("tensor_tensor", out=lam[:, 2:3], in0=lam[:, 1:2], in1=lam[:, 0:1], op=ALU.subtract), reads=[cbuf], writes=[cbuf])
            s.op("dve", ins("tensor_scalar", out=lam[:, 2:3], in0=lam[:, 2:3], scalar1=-LAM_INIT, scalar2=None, op0=ALU.add), reads=[cbuf], writes=[cbuf])
            s.op("dve", ins("tensor_scalar", out=lam[:, 3:4], in0=gsub[:], scalar1=1.0 - LAM_INIT, scalar2=None, op0=ALU.mult), reads=[cbuf], writes=[cbuf])
        ku = 0
        kp = 0
        kr = 0
        ka = 0
        SEQS = [(0, 4096, 0), (4096, 256, 1), (4352, 256, 2)]
        for (q0, L, si) in SEQS:
            sample = si == 0
            nkb_lat = L // 128
            for u in range(nunits):
                kt, ktb = Kt[ku % 2]
                va, vab = Va[ku % 2]
                qt, qtb = Qt[ku % 2]
                ot, otb = Ot[ku % 2]
                ku += 1
                s.dma(qt[:, 0:L], QK[:, u, q0:q0 + L], writes=[qtb])
                if win:
                    g = u // 2
                    s.dma(kt[:, 0:L], QK[:, kbase + g, q0:q0 + L], writes=[ktb])
                    s.dma(va[:, 0:nkb_lat, 0:64], rows_view(VS, 1024, q0, nkb_lat, g * 64, 64), writes=[vab])
                    if sample:
                        s.dma(kt[:, L:L + 512], ck0[:, g, :], writes=[ktb], eng="pool")
                        s.dma(va[:, 32:36, 0:64], rows_view(cv0, 256, 0, 4, g * 64, 64), writes=[vab], eng="pool")
                else:
                    s.dma(kt[:, 0:L], QK[:, kbase + u, q0:q0 + L], writes=[ktb])
                    s.dma(va[:, 0:nkb_lat, :], rows_view(VS, 1024, q0, nkb_lat, u * 128, 128), writes=[vab])
                    if sample:
                        s.dma(kt[:, L:L + 512], ck2[:, u, :], writes=[ktb], eng="pool")
                        s.dma(va[:, 32:36, :], rows_view(cv2, 1024, 0, 4, u * 128, 128), writes=[vab], eng="pool")
                nq = 512 if sample else 256
                LOOK = 4
                tasks = []
                for qi in range(L // nq):
                    for e_ in range(2):
                        if win and sample:
                            kbs = [(kb, kb - qi * 4 + 1) for kb in range(qi * 4 - 1, qi * 4 + 5) if 0 <= kb < 32] + [(32 + j, None) for j in range(4)]
                        elif sample:
                            kbs = [(kb, None) for kb in range(36)]
                        else:
                            kbs = [(kb, None) for kb in range(2)]
                        for i, (kb, mi) in enumerate(kbs):
                            tasks.append((qi, e_, i, kb, mi, i == len(kbs) - 1))
                stt = {}

                def stage1(tk_):
                    nonlocal kp, ka
                    qi, e_, i, kb, mi, lastb = tk_
                    qs = slice(qi * nq, (qi + 1) * nq)
                    pr = slice(e_ * 64, (e_ + 1) * 64)
                    if i == 0:
                        d_ = {}
                        d_["pso"], d_["pob"] = s.ps("b")
                        if not win:
                            d_["acs"] = {"pool": acc[ka % 8], "dve": acc[(ka + 1) % 8]}
                            d_["acn"] = {"pool": 0, "dve": 0}
                            ka += 2
                        stt[(qi, e_)] = d_
                    d_ = stt[(qi, e_)]
                    pss, psb_ = s.ps()
                    s.mm(pss[:, :nq], kt[pr, kb * 128:(kb + 1) * 128], qt[pr, qs], True, True, reads=[ktb, qtb], writes=[psb_], grp=("q", ku, cur_it[0] // 2))
                    p_, p_b = pT[kp % 8]
                    kp += 1
                    s.op("act", ins("activation", out=p_[:, :nq], in_=pss[:, :nq], func=AF.Exp, scale=0.125), reads=[psb_], writes=[p_b])
                    if mi is not None:
                        s.op("pool" if (kp % 2) else "dve", ins("tensor_tensor", out=p_[:, :nq], in0=p_[:, :nq], in1=masks[:, mi, :nq], op=ALU.mult),
                             reads=[p_b, cbuf], writes=[p_b])
                    if not win:
                        eng = "pool" if (i % 8) in (1, 4, 6) else "dve"
                        ac, acb = d_["acs"][eng]
                        if d_["acn"][eng] == 0:
                            s.op(eng, ins("tensor_copy", out=ac[:, :nq], in_=p_[:, :nq]), reads=[p_b], writes=[acb])
                        else:
                            s.op(eng, ins("tensor_tensor", out=ac[:, :nq], in0=ac[:, :nq], in1=p_[:, :nq], op=ALU.add), reads=[p_b, acb], writes=[acb])
                        d_["acn"][eng] += 1
                    d_[("p", i)] = (p_, p_b)

                def stage2(tk_):
                    nonlocal kr
                    qi, e_, i, kb, mi, lastb = tk_
                    qs = slice(qi * nq, (qi + 1) * nq)
                    pr = slice(e_ * 64, (e_ + 1) * 64)
                    d_ = stt[(qi, e_)]
                    pso, pob = d_["pso"], d_["pob"]
                    p_, p_b = d_.pop(("p", i))
                    s.mm(pso[:, :nq], va[:, kb, :], p_[:, :nq], i == 0, lastb, reads=[vab, p_b], writes=[pob], grp=("v", ku, cur_it[0] // 2))
                    if not lastb:
                        return
                    if win:
                        hh = u * 2 + e_
                        r_, r_b = rec[kr % 4]
                        kr += 1
                        s.op("dve", ins("tensor_scalar", out=r_[64:128, :nq], in0=pso[64:128, :nq], scalar1=esink[64:128, hh:hh + 1], scalar2=None, op0=ALU.add),
                             reads=[pob, cbuf], writes=[r_b])
                        s.op("dve", ins("reciprocal", out=r_[64:128, :nq], in_=r_[64:128, :nq]), reads=[r_b], writes=[r_b])
                        s.op("dve", ins("tensor_tensor", out=ot[pr, qs], in0=pso[0:64, :nq], in1=r_[64:128, :nq], op=ALU.mult),
                             reads=[pob, r_b], writes=[otb])
                        del stt[(qi, e_)]
                        return
                    acl = [d_["acs"][e2] for e2 in ("dve", "pool") if d_["acn"][e2] > 0]
                    o_, o_b = osb[kr % 8]
                    r_, r_b = rec[kr % 8]
                    kr += 1
                    s.op("dve", ins("tensor_copy", out=o_[:, :nq], in_=pso[:, :nq]), reads=[pob], writes=[o_b])
                    d_["o"] = (o_, o_b)
                    key = (qi, e_)

                    def st1():
                        psd, pdb = s.ps()
                        for ai, (ac, acb) in enumerate(acl):
                            s.mm(psd[:, :nq], ones_f[:], ac[:, :nq], ai == 0, ai == len(acl) - 1, reads=[acb, cb_const], writes=[pdb])
                        d_["psd"] = (psd, pdb)
                        defer(2, st2)

                    def st2():
                        psd, pdb = d_["psd"]
                        s.op("dve", ins("reciprocal", out=r_[:, :nq], in_=psd[:, :nq]), reads=[pdb], writes=[r_b])
                        defer(4, st3)

                    def st3():
                        s.op("dve", ins("tensor_tensor", out=o_[:, :nq], in0=o_[:, :nq], in1=r_[:, :nq], op=ALU.mult), reads=[o_b, r_b], writes=[o_b])
                        d_["done"] = True
                        if e_ == 1 or stt[(qi, 1)].get("done") if (qi, 1) in stt else False:
                            pass
                        if (qi, 0) in stt and (qi, 1) in stt and stt[(qi, 0)].get("done") and stt[(qi, 1)].get("done"):
                            defer(1, st4)

                    def st4():
                        (o0, o0b) = stt[(qi, 0)]["o"]
                        (o1, o1b) = stt[(qi, 1)]["o"]
                        s.op("dve", ins("scalar_tensor_tensor", out=o0[:, :nq], in0=o1[:, :nq], scalar=lam[:, 2:3], in1=o0[:, :nq], op0=ALU.mult, op1=ALU.add),
                             reads=[o0b, o1b, cbuf], writes=[o0b])
                        sq_, sq_b = sqd[qi % 4]
                        s.op("act", ins("activation", out=sq_[:, :nq], in_=o0[:, :nq], func=AF.Square), reads=[o0b], writes=[sq_b])
                        defer(2, st5)

                    def st5():
                        sq_, sq_b = sqd[qi % 4]
                        psn, pnb = s.ps()
                        s.mm(psn[:, :nq], ones_bf[:], sq_[:, :nq], True, True, reads=[sq_b, cb_const], writes=[pnb])
                        stt[(qi, 1)]["psn"] = (psn, pnb)
                        defer(2, st6)

                    def st6():
                        (o1, o1b) = stt[(qi, 1)]["o"]
                        psn, pnb = stt[(qi, 1)]["psn"]
                        s.op("act", ins("activation", out=o1[:, :nq], in_=psn[:, :nq], func=AF.Sqrt, scale=1.0 / 128, bias=EPSB[:, 0:1]), reads=[pnb, o1b], writes=[o1b])
                        defer(2, st7)

                    def st7():
                        (o1, o1b) = stt[(qi, 1)]["o"]
                        s.op("dve", ins("reciprocal", out=o1[:, :nq], in_=o1[:, :nq]), reads=[o1b], writes=[o1b])
                        defer(4, st8)

                    def st8():
                        (o0, o0b) = stt[(qi, 0)]["o"]
                        (o1, o1b) = stt[(qi, 1)]["o"]
                        s.op("dve", ins("scalar_tensor_tensor", out=ot[:, qs], in0=o0[:, :nq], scalar=lam[:, 3:4], in1=o1[:, :nq], op0=ALU.mult, op1=ALU.mult),
                             reads=[o0b, o1b, cbuf], writes=[otb])
                        del stt[(qi, 0)]
                        del stt[(qi, 1)]

                    defer(2, st1)

                pend = []
                cur_it = [0]

                def defer(dl, fn):
                    pend.append((cur_it[0] + dl, fn))

                def run_pending(flush=False):
                    while True:
                        ready = [p for p in pend if flush or p[0] <= cur_it[0]]
                        if not ready:
                            break
                        for p in ready:
                            pend.remove(p)
                        for due, fn in ready:
                            fn()
                        if not flush:
                            break

                for t2_ in range(0, len(tasks) + LOOK + 1, 2):
                    for t_ in (t2_, t2_ + 1):
                        cur_it[0] = t_
                        if t_ < len(tasks):
                            stage1(tasks[t_])
                    for t_ in (t2_, t2_ + 1):
                        cur_it[0] = t_
                        if LOOK <= t_ < len(tasks) + LOOK:
                            stage2(tasks[t_ - LOOK])
                    run_pending()
                while pend:
                    cur_it[0] += 1
                    run_pending(flush=True)
                s.dma(OS[:, u, q0:q0 + L], ot[:, 0:L], reads=[otb], writes=[OS_b[si][u]])
        return OS_b

    def outproj_phase(l, Wsrc, nk, Osrc, O_bufs_fn, Xsrc, Xsrc_b):
        s.barrier()
        s.release()
        Wo = s.alloc([128, nk, 1024], BF16, "wo")
        wob = load_w(Wo, Wsrc, nk)
        rctx = ResCtx()
        oin = [(s.alloc([128, nk, 512], BF16, "oin"), Buf()) for _ in range(2)]
        for ti in range(NT):
            t0, n, cond, _, _ = TILES[ti]
            o_, o_b = oin[ti % 2]
            s.dma(o_[:, :, :n], Osrc[:, :, t0:t0 + n], writes=[o_b])
            outproj_resid(rctx, l, 2, ti, Wo, wob, nk, o_, [o_b], Xsrc, Xsrc_b, X, X_b)

    def gla_layer(l, Xsrc, Xsrc_b):
        s.barrier()
        s.release()
        Wqk = s.alloc([128, 8, 1024], BF16, "gwqk")
        wqb = load_w(Wqk, w1_qk, 8)
        Wv = s.alloc([128, 8, 1024], BF16, "gwv")
        wvb = load_w(Wv, w1_v, 8)
        Wr = s.alloc([128, 8, 1024], BF16, "gwr")
        wrb = load_w(Wr, w1_r, 8)
        Wg1 = s.alloc([128, 8, 32], BF16, "gwg1")
        wg1b = load_w(Wg1, w1_g1, 8)
        Wg2 = s.alloc([32, 1024], BF16, "gwg2")
        cbuf = Buf()
        s.dma(Wg2[:], w1_g2, writes=[cbuf], eng="pool")
        gb = s.alloc([128, 8], F32, "ggb")
        s.dma(gb[:], w1_gb, writes=[cbuf])
        s.op("dve", ins("tensor_scalar", out=gb[:], in0=gb[:], scalar1=-1.0, scalar2=None, op0=ALU.mult), reads=[cbuf], writes=[cbuf])
        ELt = s.alloc([128, 2, 4, 72], F32, "el")
        nctx = NormCtx(1)
        qkf = [(s.alloc([128, 8, 512], F32, "qkf"), Buf()) for _ in range(1)]
        lg = nctx.xt[0]
        cs_ = (s.alloc([128, 8, 512], F32, "cs"), Buf())
        c2 = nctx.tmp
        ex = [(s.alloc([128, 512], F32, "ex"), Buf()) for _ in range(3)]
        t1b_ = (s.alloc([32, 512], BF16, "t1b"), Buf())
        qd_t = [(s.alloc([128, 2, 4, 512], BF16, "qd"), Buf()) for _ in range(1)]
        kd_t = [(s.alloc([128, 2, 4, 512], BF16, "kd"), Buf()) for _ in range(1)]
        krT = [(s.alloc([128, 512], BF16, "krT"), Buf()) for _ in range(2)]
        krt = [(s.alloc([128, 4, 8, 128], BF16, "krt"), Buf()) for _ in range(1)]
        rt = [(s.alloc([128, 8, 512], BF16, "rt"), Buf()) for _ in range(1)]
        vt = [(s.alloc([128, 4, 1024], BF16, "vt"), Buf()) for _ in range(1)]
        G_b = [Buf() for _ in range(NT)]
        kx = 0
        kk = 0
        for ti in range(NT):
            t0, n, cond, s0, s1 = TILES[ti]
            nch = n // 64
            h, hb = norm_tile(nctx, l, 0, ti, Xsrc, Xsrc_b)
            qf, qfb = qkf[0]
            for cc in range(8):
                ps, pb = s.ps()
                for kc in range(8):
                    s.mm(ps[:, :n], Wqk[:, kc, cc * 128:(cc + 1) * 128], h[:, kc, :n], kc == 0, kc == 7, reads=[wqb[kc], hb], writes=[pb])
                sc_ = (128.0 ** -0.5) if cc < 4 else 1.0
                s.op("act", ins("activation", out=qf[:, cc, :n], in_=ps[:, :n], func=AF.Identity, scale=sc_), reads=[pb], writes=[qfb])
            r_, r_b = rt[0]
            for cc in range(8):
                ps, pb = s.ps()
                for kc in range(8):
                    s.mm(ps[:, :n], Wr[:, kc, cc * 128:(cc + 1) * 128], h[:, kc, :n], kc == 0, kc == 7, reads=[wrb[kc], hb], writes=[pb])
                s.op("act", ins("activation", out=r_[:, cc, :n], in_=ps[:, :n], func=AF.Silu), reads=[pb], writes=[r_b])
            s.dma(RS[:, :, t0:t0 + n], r_[:, :, :n], reads=[r_b], writes=[G_b[ti]])
            v_, v_b = vt[0]
            for tbk in range(n // 128):
                for vg in range(2):
                    ps, pb = s.ps()
                    for kc in range(8):
                        s.mm(ps[:, :512], h[:, kc, tbk * 128:(tbk + 1) * 128], Wv[:, kc, vg * 512:(vg + 1) * 512], kc == 0, kc == 7, reads=[wvb[kc], hb], writes=[pb])
                    s.op("act", ins("activation", out=v_[:, tbk, vg * 512:(vg + 1) * 512], in_=ps[:, :512], func=AF.Identity), reads=[pb], writes=[v_b])
            s.dma(rows_view(VS, 1024, t0, n // 128, 0, 1024), v_[:, 0:n // 128, :], reads=[v_b], writes=[G_b[ti]])
            ps, pb = s.ps()
            for kc in range(8):
                s.mm(ps[0:32, :n], Wg1[:, kc, :], h[:, kc, :n], kc == 0, kc == 7, reads=[wg1b[kc], hb], writes=[pb])
            t1_, t1bb = t1b_
            s.op("act", ins("activation", out=t1_[:, :n], in_=ps[0:32, :n], func=AF.Identity), reads=[pb], writes=[t1bb])
            lgt, lgb = lg
            for j in range(8):
                ps, pb = s.ps()
                s.mm(ps[:, :n], Wg2[:, j * 128:(j + 1) * 128], t1_[:, :n], True, True, reads=[cbuf, t1bb], writes=[pb])
                s.op("act", ins("activation", out=lgt[:, j, :n], in_=ps[:, :n], func=AF.Exp, scale=-1.0, bias=gb[:, j:j + 1]), reads=[pb, cbuf], writes=[lgb])
            s.op("act", ins("activation", out=lgt[:, :, :n], in_=lgt[:, :, :n], func=AF.Ln, bias=1.0, scale=1.0), reads=[lgb], writes=[lgb])
            cst, csb = cs_
            for j in range(8):
                s.op("dve", ins("tensor_tensor_scan", out=cst[:, j, :n], data0=m64[:, :n], data1=lgt[:, j, :n], initial=0.0, op0=ALU.mult, op1=ALU.add),
                     reads=[lgb, cb_const], writes=[csb])
            c2t, c2b = c2
            csv = cst[:, :, :n].rearrange("p j (c t) -> p j c t", t=64)
            lgv = lgt[:, :, :n].rearrange("p j (c t) -> p j c t", t=64)
            c2v = c2t[:, :, :n].rearrange("p j (c t) -> p j c t", t=64)
            nf = 4
            s.op("dve", ins("tensor_tensor", out=c2v[:, 0:nf, :, :], in0=csv[:, 0:nf, :, :], in1=csv[:, 0:nf, :, 63:64].to_broadcast([128, nf, nch, 64]), op=ALU.subtract),
                 reads=[csb], writes=[c2b])
            s.op("dve", ins("tensor_tensor", out=c2v[:, nf:2 * nf, :, :], in0=lgv[:, nf:2 * nf, :, :], in1=csv[:, nf:2 * nf, :, :], op=ALU.subtract),
                 reads=[csb, lgb], writes=[c2b])
            ci0 = t0 // 64
            s.op("act", ins("activation", out=ELt[:, :, :, ci0:ci0 + nch].rearrange("p d h c -> p (d h) c"), in_=cst[:, :, :n].rearrange("p j (c t) -> p j c t", t=64)[:, :, :, 63],
                                               func=AF.Exp, scale=-1.0 / 16), reads=[csb], writes=[cbuf])
            s.op("dve", ins("tensor_tensor", out=lgv[:, nf:2 * nf, :, :], in0=c2v[:, nf:2 * nf, :, :], in1=csv[:, nf:2 * nf, :, 63:64].to_broadcast([128, nf, nch, 64]), op=ALU.add),
                 reads=[csb, c2b, lgb], writes=[lgb])
            qd_, qdb = qd_t[0]
            kd_, kdb = kd_t[0]
            krt_, krtb = krt[0]
            for j in range(8):
                d = j // 4
                hh = j % 4
                ea, eab = ex[0]
                eb, ebb = ex[1]
                ec, ecb = ex[2]
                csrc = cst if j < 4 else lgt
                s.op("act", ins("activation", out=ea[:, :n], in_=csrc[:, j, :n], func=AF.Exp, scale=-1.0 / 16), reads=[csb, lgb], writes=[eab])
                s.op("act", ins("activation", out=eb[:, :n], in_=csrc[:, j, :n], func=AF.Exp, scale=1.0 / 16), reads=[csb, lgb], writes=[ebb])
                s.op("act", ins("activation", out=ec[:, :n], in_=c2t[:, j, :n], func=AF.Exp, scale=1.0 / 16), reads=[c2b], writes=[ecb])
                s.op("dve", ins("tensor_tensor", out=qd_[:, d, hh, :n], in0=qf[:, hh, :n], in1=ea[:, :n], op=ALU.mult), reads=[qfb, eab], writes=[qdb])
                s.op("dve", ins("tensor_tensor", out=kd_[:, d, hh, :n], in0=qf[:, 4 + hh, :n], in1=eb[:, :n], op=ALU.mult), reads=[qfb, ebb], writes=[kdb])
                kT, kTb = krT[kx % 2]
                kx += 1
                s.op("pool", ins("tensor_tensor", out=kT[:, :n], in0=qf[:, 4 + hh, :n], in1=ec[:, :n], op=ALU.mult), reads=[qfb, ecb], writes=[kTb])
                for tbk in range(n // 128):
                    ps, pb = s.ps()
                    psv = ps[:].bitcast(BF16)
                    s.op("pe", ins("transpose", psv[:, 0:128], kT[:, tbk * 128:(tbk + 1) * 128], ident_bf[:]), reads=[kTb, cb_const], writes=[pb])
                    s.op("act", ins("activation", out=krt_[:, tbk, j, :], in_=psv[:, 0:128], func=AF.Identity), reads=[pb], writes=[krtb])
            for d in range(2):
                s.dma(QD[d, :, :, t0:t0 + n], qd_[:, d, :, :n], reads=[qdb], writes=[G_b[ti]])
                s.dma(KD[d, :, :, t0:t0 + n], kd_[:, d, :, :n], reads=[kdb], writes=[G_b[ti]])
            s.dma(kr_view(t0, n // 128, 0, 8), krt_[:, 0:n // 128, :, :], reads=[krtb], writes=[G_b[ti]])
        ELd = dscr("ELd", [128, 2, 4, 72], F32)
        elb = Buf()
        s.dma(ELd, ELt[:], reads=[cbuf], writes=[elb])
        s.barrier()
        s.release()
        EL = s.alloc([128, 2, 4, 72], F32, "el2")
        elb2 = Buf()
        s.dma(EL[:], ELd, writes=[elb2])
        S = [(s.alloc([128, 4, 256], F32, "S"), [Buf() for _ in range(4)]) for _ in range(2)]
        Sb = [(s.alloc([128, 4, 256], BF16, "Sb"), [Buf() for _ in range(4)]) for _ in range(2)]
        qd_t = [[(s.alloc([128, 4, 512], BF16, "qd"), Buf()) for _ in range(2)] for _ in range(2)]
        kd_t = [[(s.alloc([128, 4, 512], BF16, "kd"), Buf()) for _ in range(2)] for _ in range(2)]
        kr_t = [[(s.alloc([128, 4, 4, 128], BF16, "kr"), Buf()) for _ in range(2)] for _ in range(2)]
        v_t = [[(s.alloc([128, 4, 1024], BF16, "v"), Buf()) for _ in range(2)] for _ in range(2)]
        of_t = [[(s.alloc([128, 8, 512], BF16, "of"), Buf()) for _ in range(2)] for _ in range(2)]
        attm = [(s.alloc([128, 64], BF16, "attm"), Buf()) for _ in range(8)]
        OD_b = [[Buf() for _ in range(NT)] for _ in range(2)]
        ODs = [OS, OS2]
        kat = 0
        SEQT = [list(range(8)), [8], [9]]
        states_in = [s1f, s1b]
        states_out = [o_gf, o_gb]
        for si, tiles in enumerate(SEQT):
            for d in range(2):
                St, Stb = S[d]
                Sbt, Sbb = Sb[d]
                if si == 0:
                    s.dma(St[:], states_in[d], writes=Stb)
                else:
                    s.op("pool", ins("memset", St[:], 0.0), writes=Stb)
                for hh in range(4):
                    s.op("act", ins("activation", out=Sbt[:, hh, :], in_=St[:, hh, :], func=AF.Identity), reads=[Stb[hh]], writes=[Sbb[hh]])
            nt_ = len(tiles)
            for step in range(nt_):
                cur = {}
                for d in range(2):
                    ti = tiles[step] if d == 0 else tiles[nt_ - 1 - step]
                    t0, n, cond, s0, s1 = TILES[ti]
                    qd_, qdb = qd_t[d][step % 2]
                    kd_, kdb = kd_t[d][step % 2]
                    kr_, krb = kr_t[d][step % 2]
                    v_, vb_ = v_t[d][step % 2]
                    of_, ofb = of_t[d][step % 2]
                    s.dma(qd_[:, :, :n], QD[d, :, :, t0:t0 + n], writes=[qdb])
                    s.dma(kd_[:, :, :n], KD[d, :, :, t0:t0 + n], writes=[kdb])
                    s.dma(kr_[:, 0:n // 128, :, :], kr_view(t0, n // 128, d * 4, 4), writes=[krb])
                    s.dma(v_[:, 0:n // 128, :], rows_view(VS, 1024, t0, n // 128, 0, 1024), writes=[vb_])
                    cur[d] = (ti, t0, n, qd_, qdb, kd_, kdb, kr_, krb, v_, vb_, of_, ofb)
                nch = cur[0][2] // 64
                for kc_ in range(nch):
                    units = []
                    for d in range(2):
                        ti, t0, n, qd_, qdb, kd_, kdb, kr_, krb, v_, vb_, of_, ofb = cur[d]
                        k = kc_ if d == 0 else nch - 1 - kc_
                        for hh in range(4):
                            units.append(dict(d=d, hh=hh, k=k, ci=t0 // 64 + k, tbk=k // 2, hp=slice((k % 2) * 64, (k % 2) * 64 + 64),
                                              cs64=slice(k * 64, (k + 1) * 64), qd_=qd_, qdb=qdb, kd_=kd_, kdb=kdb, kr_=kr_, krb=krb, v_=v_, vb_=vb_, of_=of_, ofb=ofb))
                    psA, pbA = s.ps()
                    for ui, u_ in enumerate(units):
                        s.mm(psA[0:64, ui * 64:(ui + 1) * 64], u_["kd_"][:, u_["hh"], u_["cs64"]], u_["qd_"][:, u_["hh"], u_["cs64"]], True, True,
                             reads=[u_["kdb"], u_["qdb"]], writes=[pbA], grp=("ga", kat))
                    for ui, u_ in enumerate(units):
                        am, amb = attm[ui]
                        u_["am"] = (am, amb)
                        s.op("dve", ins("tensor_tensor", out=am[u_["hp"], :], in0=psA[0:64, ui * 64:(ui + 1) * 64], in1=tri_bf[0:64, 2 + u_["d"], 0:64], op=ALU.mult),
                             reads=[pbA, cb_const], writes=[amb])
                    psO = [s.ps("b") for _ in range(2)]
                    for ui, u_ in enumerate(units):
                        pso, pob = psO[ui // 4]
                        am, amb = u_["am"]
                        St, Stb = S[u_["d"]]
                        Sbt, Sbb = Sb[u_["d"]]
                        hh, hp, tbk = u_["hh"], u_["hp"], u_["tbk"]
                        c0 = (ui % 4) * 128
                        for vc in range(2):
                            s.mm(pso[:, c0 + vc * 64:c0 + (vc + 1) * 64], u_["v_"][hp, tbk, hh * 256 + vc * 128:hh * 256 + (vc + 1) * 128], am[hp, :], True, False,
                                 reads=[u_["vb_"], amb], writes=[pob])
                            s.mm(pso[:, c0 + vc * 64:c0 + (vc + 1) * 64], Sbt[:, hh, vc * 128:(vc + 1) * 128], u_["qd_"][:, hh, u_["cs64"]], False, True,
                                 reads=[Sbb[hh], u_["qdb"]], writes=[pob])
                    for ui, u_ in enumerate(units):
                        pso, pob = psO[ui // 4]
                        c0 = (ui % 4) * 128
                        hh = u_["hh"]
                        s.op("act", ins("activation", out=u_["of_"][:, hh * 2:hh * 2 + 2, u_["cs64"]], in_=pso[:, c0:c0 + 128].rearrange("p (v t) -> p v t", t=64), func=AF.Identity),
                             reads=[pob], writes=[u_["ofb"]])
                    psS = [s.ps() for _ in range(4)]
                    for ui, u_ in enumerate(units):
                        ps2, pb2 = psS[ui // 2]
                        c0 = (ui % 2) * 256
                        hh, hp, tbk = u_["hh"], u_["hp"], u_["tbk"]
                        s.mm(ps2[:, c0:c0 + 256], u_["kr_"][hp, tbk, hh, :], u_["v_"][hp, tbk, hh * 256:(hh + 1) * 256], True, True, reads=[u_["krb"], u_["vb_"]], writes=[pb2])
                    for ui, u_ in enumerate(units):
                        ps2, pb2 = psS[ui // 2]
                        c0 = (ui % 2) * 256
                        hh, d = u_["hh"], u_["d"]
                        St, Stb = S[d]
                        Sbt, Sbb = Sb[d]
                        ci = u_["ci"]
                        s.op("dve", ins("scalar_tensor_tensor", out=St[:, hh, :], in0=St[:, hh, :], scalar=EL[:, d, hh, ci:ci + 1], in1=ps2[:, c0:c0 + 256], op0=ALU.mult, op1=ALU.add),
                             reads=[pb2, elb2, Stb[hh]], writes=[Stb[hh]])
                        s.op("pool", ins("tensor_copy", out=Sbt[:, hh, :], in_=St[:, hh, :]), reads=[Stb[hh]], writes=[Sbb[hh]])
                    kat += 1
                for d in range(2):
                    ti, t0, n, qd_, qdb, kd_, kdb, kr_, krb, v_, vb_, of_, ofb = cur[d]
                    s.dma(ODs[d][:, :, t0:t0 + n], of_[:, :, :n], reads=[ofb], writes=[OD_b[d][ti]])
            if si > 0:
                for d in range(2):
                    s.dma(states_out[d][si - 1], S[d][0][:], reads=S[d][1])
        s.barrier()
        s.release()
        Wo = s.alloc([128, 8, 1024], BF16, "wo")
        wob = load_w(Wo, w1_o, 8)
        gn = s.alloc([128, 2], F32, "gn")
        gnb = Buf()
        s.dma(gn[:], w1_gn, writes=[gnb])
        rctx = ResCtx()
        oa = [(s.alloc([128, 8, 512], BF16, "oa"), Buf()) for _ in range(2)]
        ob_ = [(s.alloc([128, 8, 512], BF16, "ob"), Buf()) for _ in range(2)]
        rr = [(s.alloc([128, 8, 512], BF16, "rr"), Buf()) for _ in range(2)]
        osum = (s.alloc([128, 8, 512], F32, "osum"), Buf())
        sq = (s.alloc([128, 8, 512], BF16, "sq"), Buf())
        rs_ = [(s.alloc([128, 512], F32, "rs"), Buf()) for _ in range(2)]
        ofin = [(s.alloc([128, 8, 512], BF16, "ofin"), Buf()) for _ in range(2)]
        for ti in range(NT):
            t0, n, cond, _, _ = TILES[ti]
            a_, a_b = oa[ti % 2]
            b_, b_b = ob_[ti % 2]
            r_, r_b = rr[ti % 2]
            s.dma(a_[:, :, :n], OS[:, :, t0:t0 + n], writes=[a_b])
            s.dma(b_[:, :, :n], OS2[:, :, t0:t0 + n], writes=[b_b])
            s.dma(r_[:, :, :n], RS[:, :, t0:t0 + n], writes=[r_b])
            os_, osb_ = osum
            sq_, sqb_ = sq
            s.op("dve", ins("tensor_tensor", out=os_[:, :, :n], in0=a_[:, :, :n], in1=b_[:, :, :n], op=ALU.add), reads=[a_b, b_b], writes=[osb_])
            s.op("act", ins("activation", out=sq_[:, :, :n], in_=os_[:, :, :n], func=AF.Square), reads=[osb_], writes=[sqb_])
            f_, f_b = ofin[ti % 2]
            for hh in range(4):
                ps, pb = s.ps()
                for vc in range(2):
                    s.mm(ps[:, :n], ones_bf[:], sq_[:, hh * 2 + vc, :n], vc == 0, vc == 1, reads=[sqb_, cb_const], writes=[pb])
                rt_, rtb = rs_[hh % 2]
                s.op("act", ins("activation", out=rt_[:, :n], in_=ps[:, :n], func=AF.Sqrt, scale=1.0 / 256, bias=EPSB[:, 0:1]), reads=[pb], writes=[rtb])
                s.op("dve", ins("reciprocal", out=rt_[:, :n], in_=rt_[:, :n]), reads=[rtb], writes=[rtb])
                for vc in range(2):
                    c = hh * 2 + vc
                    s.op("dve", ins("tensor_tensor", out=os_[:, c, :n], in0=os_[:, c, :n], in1=rt_[:, :n], op=ALU.mult), reads=[osb_, rtb], writes=[osb_])
                    s.op("dve", ins("scalar_tensor_tensor", out=f_[:, c, :n], in0=os_[:, c, :n], scalar=gn[:, vc:vc + 1], in1=r_[:, c, :n], op0=ALU.mult, op1=ALU.mult),
                         reads=[osb_, gnb, r_b], writes=[f_b])
            outproj_resid(rctx, l, 2, ti, Wo, wob, 8, f_, [f_b], Xsrc, Xsrc_b, X, X_b)

    def ssd_layer(l, Xsrc, Xsrc_b):
        XBu = U
        s.barrier()
        s.release()
        Wz = s.alloc([128, 8, 2048], BF16, "wz")
        wzb = load_w(Wz, w3_z, 8)
        Wx = s.alloc([128, 8, 3072], BF16, "wx")
        wxb = load_w(Wx, w3_xbc, 8)
        Wdt = s.alloc([128, 8, 64], BF16, "wdt")
        wdb = load_w(Wdt, w3_dt, 8)
        vec = s.alloc([128, 5, 32], F32, "vec")
        cbuf = Buf()
        s.dma(vec[:], ssd_vec.rearrange("a b -> (a b)").partition_broadcast(128).rearrange("p (a b) -> p a b", a=5), writes=[cbuf])
        avec = s.alloc([128, 64], F32, "avec")
        s.op("act", ins("activation", out=avec[:], in_=vec[:, 0:2, :].rearrange("p a b -> p (a b)"), func=AF.Exp), reads=[cbuf], writes=[cbuf])
        s.op("dve", ins("tensor_scalar", out=avec[:], in0=avec[:], scalar1=-1.0, scalar2=None, op0=ALU.mult), reads=[cbuf], writes=[cbuf])
        nctx = NormCtx(1)
        xbt = (s.alloc([128, 24, 512], BF16, "xbt"), [Buf() for _ in range(24)])
        zt = (s.alloc([128, 4, 2048], BF16, "zt"), Buf())
        dtt = (s.alloc([128, 4, 2, 64], F32, "dtt"), Buf())
        dtmp = (s.alloc([128, 64], F32, "dtmp"), Buf())
        S_b = [Buf() for _ in range(NT)]
        ev = 0
        for ti in range(NT):
            t0, n, cond, s0, s1 = TILES[ti]
            nb = n // 128
            h, hb = norm_tile(nctx, l, 0, ti, Xsrc, Xsrc_b)
            xb_, xbb = xbt
            for cc in range(24):
                ps, pb = s.ps()
                for kc in range(8):
                    s.mm(ps[:, :n], Wx[:, kc, cc * 128:(cc + 1) * 128], h[:, kc, :n], kc == 0, kc == 7, reads=[wxb[kc], hb], writes=[pb])
                if ev % 2 == 0:
                    s.op("act", ins("activation", out=xb_[:, cc, :n], in_=ps[:, :n], func=AF.Identity), reads=[pb], writes=[xbb[cc]])
                else:
                    s.op("dve", ins("tensor_copy", out=xb_[:, cc, :n], in_=ps[:, :n]), reads=[pb], writes=[xbb[cc]])
                ev += 1
            s.dma(XBu[:, 0:24, t0:t0 + n], xb_[:, :, :n], reads=xbb, writes=[S_b[ti]])
            z_, z_b = zt
            d_, d_b = dtt
            for tbk in range(nb):
                for vg in range(4):
                    ps, pb = s.ps()
                    for kc in range(8):
                        s.mm(ps[:, :512], h[:, kc, tbk * 128:(tbk + 1) * 128], Wz[:, kc, vg * 512:(vg + 1) * 512], kc == 0, kc == 7, reads=[wzb[kc], hb], writes=[pb])
                    s.op("act", ins("activation", out=z_[:, tbk, vg * 512:(vg + 1) * 512], in_=ps[:, :512], func=AF.Silu), reads=[pb], writes=[z_b])
                ps, pb = s.ps()
                for kc in range(8):
                    s.mm(ps[:, 0:64], h[:, kc, tbk * 128:(tbk + 1) * 128], Wdt[:, kc, :], kc == 0, kc == 7, reads=[wdb[kc], hb], writes=[pb])
                tm, tmb = dtmp
                s.op("dve", ins("tensor_tensor", out=tm[:], in0=ps[:, 0:64], in1=vec[:, 2:4, :].rearrange("p a b -> p (a b)"), op=ALU.add), reads=[pb, cbuf], writes=[tmb])
                s.op("act", ins("activation", out=tm[:], in_=tm[:], func=AF.Exp), reads=[tmb], writes=[tmb])
                s.op("act", ins("activation", out=d_[:, tbk, 0, :], in_=tm[:], func=AF.Ln, bias=1.0, scale=1.0), reads=[tmb], writes=[d_b])
                s.op("dve", ins("tensor_tensor", out=d_[:, tbk, 1, :], in0=d_[:, tbk, 0, :], in1=avec[:], op=ALU.mult), reads=[d_b, cbuf], writes=[d_b])
            s.dma(rows_view(ZS, 2048, t0, nb, 0, 2048), z_[:, 0:nb, :], reads=[z_b], writes=[S_b[ti]])
            s.dma(AP(DTS.tensor, t0 * 128, [[128, 128], [128 * 128, nb], [1, 128]]), d_[:, 0:nb, :, :].rearrange("p b a c -> p b (a c)"), reads=[d_b], writes=[S_b[ti]])
        chk(20)
        s.barrier()
        s.release()
        cw = s.alloc([128, 3, 24], F32, "scw")
        cbias = s.alloc([128, 24], F32, "scb")
        cwb = Buf()
        s.dma(cw[:], ssd_cw, writes=[cwb])
        s.dma(cbias[:], ssd_cb, writes=[cwb])
        ug = s.alloc([128, 24, 514], BF16, "sug")
        ugb = [Buf() for _ in range(2)]
        xc = (s.alloc([128, 24, 512], BF16, "xc"), [Buf() for _ in range(24)])
        tmps = [(s.alloc([128, 512], F32, "st"), Buf()) for _ in range(2)]
        xtt = (s.alloc([128, 4, 2048], BF16, "xtt"), Buf())
        btt = (s.alloc([128, 4, 512], BF16, "btt"), Buf())
        kp = 0
        for ti in range(NT):
            t0, n, cond, s0, s1 = TILES[ti]
            nb = n // 128
            lo = (t0 - 1) >= s0
            hi = (t0 + n) < s1
            for grp in range(2):
                gs_ = slice(grp * 12, (grp + 1) * 12)
                if not lo:
                    s.op("pool", ins("memset", ug[:, gs_, 0:1], 0.0), writes=[ugb[grp]])
                if not hi:
                    s.op("pool", ins("memset", ug[:, gs_, n + 1:n + 2], 0.0), writes=[ugb[grp]])
                c0 = 0 if lo else 1
                c1 = n + 2 if hi else n + 1
                s.dma(ug[:, gs_, c0:c1], XBu[:, gs_, t0 - 1 + c0:t0 - 1 + c1], writes=[ugb[grp]])
            xc_, xcb = xc
            for col in range(24):
                grp = col // 12
                tt, ttb = tmps[kp % 2]
                kp += 1
                s.op("act", ins("activation", out=tt[:, :n], in_=ug[:, col, 1:n + 1], func=AF.Identity, scale=cw[:, 1, col:col + 1], bias=cbias[:, col:col + 1]),
                     reads=[ugb[grp], cwb], writes=[ttb])
                s.op("dve", ins("scalar_tensor_tensor", out=tt[:, :n], in0=ug[:, col, 0:n], scalar=cw[:, 0, col:col + 1], in1=tt[:, :n], op0=ALU.mult, op1=ALU.add),
                     reads=[ugb[grp], cwb, ttb], writes=[ttb])
                s.op("dve", ins("scalar_tensor_tensor", out=tt[:, :n], in0=ug[:, col, 2:n + 2], scalar=cw[:, 2, col:col + 1], in1=tt[:, :n], op0=ALU.mult, op1=ALU.add),
                     reads=[ugb[grp], cwb, ttb], writes=[ttb])
                s.op("act", ins("activation", out=xc_[:, col, :n], in_=tt[:, :n], func=AF.Silu), reads=[ttb], writes=[xcb[col]])
            s.dma(QK[:, 0:8, t0:t0 + n], xc_[:, 16:24, :n], reads=xcb[16:24], writes=[S_b[ti]])
            x_, x_b = xtt
            b_, b_b = btt
            tv = 0
            for tbk in range(nb):
                for col in range(20):
                    ps, pb = s.ps()
                    psv = ps[:].bitcast(BF16)
                    s.op("pe", ins("transpose", psv[:, 0:128], xc_[:, col, tbk * 128:(tbk + 1) * 128], ident_bf[:]), reads=[xcb[col], cb_const], writes=[pb])
                    dst = x_[:, tbk, col * 128:(col + 1) * 128] if col < 16 else b_[:, tbk, (col - 16) * 128:(col - 15) * 128]
                    dbuf = x_b if col < 16 else b_b
                    if tv % 2 == 0:
                        s.op("act", ins("activation", out=dst, in_=psv[:, 0:128], func=AF.Identity), reads=[pb], writes=[dbuf])
                    else:
                        s.op("dve", ins("tensor_copy", out=dst, in_=psv[:, 0:128]), reads=[pb], writes=[dbuf])
                    tv += 1
            s.dma(rows_view(XTs, 2048, t0, nb, 0, 2048), x_[:, 0:nb, :], reads=[x_b], writes=[S_b[ti]])
            s.dma(rows_view(BTs, 512, t0, nb, 0, 512), b_[:, 0:nb, :], reads=[b_b], writes=[S_b[ti]])
        chk(21)
        s.barrier()
        s.release()
        vec = s.alloc([128, 5, 32], F32, "vec")
        negm = s.alloc([128, 2, 128], F32, "negm")
        cbuf = Buf()
        s.op("dve", ins("tensor_scalar", out=negm[:], in0=stage[:, 0:2, :], scalar1=-1.0, scalar2=30000.0, op0=ALU.add, op1=ALU.mult), reads=[bstage], writes=[cbuf])
        ST = [(s.alloc([128, 2048], F32, "ST"), [Buf() for _ in range(4)]) for _ in range(2)]
        SbT = [(s.alloc([128, 2048], BF16, "SbT"), [Buf() for _ in range(4)]) for _ in range(2)]
        bct = [(s.alloc([128, 8, 512], BF16, "bct"), Buf()) for _ in range(2)]
        xts = [(s.alloc([128, 4, 2048], BF16, "xts"), Buf()) for _ in range(2)]
        bts = [(s.alloc([128, 4, 512], BF16, "bts"), Buf()) for _ in range(2)]
        dts = [(s.alloc([128, 4, 2, 64], F32, "dts"), Buf()) for _ in range(2)]
        yts = [(s.alloc([128, 4, 2048], BF16, "yts"), Buf()) for _ in range(2)]
        cumT = [(s.alloc([128, 32], F32, "cumT"), Buf()) for _ in range(2)]
        cbT = [(s.alloc([128, 128], F32, "cbT"), Buf()) for _ in range(6)]
        lat = [(s.alloc([128, 4, 128], F32, "lat"), Buf()) for _ in range(3)]
        cB = [(s.alloc([128, 4, 128], F32, "cB"), Buf()) for _ in range(4)]
        seg = [(s.alloc([128, 4, 128], F32, "seg"), Buf()) for _ in range(3)]
        Et = [(s.alloc([128, 4, 128], F32, "Et"), Buf()) for _ in range(6)]
        ecB = [(s.alloc([128, 4, 128], F32, "ecB"), Buf()) for _ in range(8)]
        te = [(s.alloc([128, 4], F32, "te"), Buf()) for _ in range(3)]
        xs = [(s.alloc([128, 512], BF16, "xs"), Buf()) for _ in range(4)]
        Wt = [(s.alloc([128, 128], BF16, "Wt"), Buf()) for _ in range(8)]
        CEt = [(s.alloc([128, 128], BF16, "CEt"), Buf()) for _ in range(8)]
        SEQT = [list(range(8)), [8], [9]]
        st_in = [s3f, s3b]
        st_out = [o_sf, o_sb]
        Y_b = [Buf() for _ in range(NT)]
        kw = 0
        kq4 = 0
        kg = 0
        kct = 0
        for si, tiles in enumerate(SEQT):
            for d in range(2):
                St, Stb = ST[d]
                Sbt, Sbb = SbT[d]
                if si == 0:
                    s.dma(St[:], st_in[d], writes=Stb)
                else:
                    s.op("pool", ins("memset", St[:], 0.0), writes=Stb)
                for g in range(4):
                    s.op("act", ins("activation", out=Sbt[:, g * 512:(g + 1) * 512], in_=St[:, g * 512:(g + 1) * 512], func=AF.Identity), reads=[Stb[g]], writes=[Sbb[g]])
            nt_ = len(tiles)
            for step in range(nt_):
                cur = {}
                for d in range(2):
                    ti = tiles[step] if d == 0 else tiles[nt_ - 1 - step]
                    t0, n, cond, s0, s1 = TILES[ti]
                    nb = n // 128
                    bc_, bcb = bct[d]
                    x_, x_b = xts[d]
                    b_, b_b = bts[d]
                    d_, d_b = dts[d]
                    y_, y_b = yts[d]
                    s.dma(bc_[:, :, :n], QK[:, 0:8, t0:t0 + n], writes=[bcb])
                    s.dma(x_[:, 0:nb, :], rows_view(XTs, 2048, t0, nb, 0, 2048), writes=[x_b])
                    s.dma(b_[:, 0:nb, :], rows_view(BTs, 512, t0, nb, 0, 512), writes=[b_b])
                    s.dma(d_[:, 0:nb, :, :].rearrange("p b a c -> p b (a c)"), AP(DTS.tensor, t0 * 128, [[128, 128], [128 * 128, nb], [1, 128]]), writes=[d_b])
                    cur[d] = (ti, t0, n, nb)
                nbs = cur[0][3]
                batches = []
                for kc_ in range(nbs):
                    for d in range(2):
                        for g in range(4):
                            for hb_ in range(2):
                                batches.append((kc_, d, g, hb_))
                bst = {}
                LOOKB = 3

                def geo(bt_):
                    kc_, d, g, hb_ = bt_
                    ti, t0, n, nb = cur[d]
                    tbk = kc_ if d == 0 else nb - 1 - kc_
                    last = 127 if d == 0 else 0
                    d_, d_b = dts[d]
                    la = d_[:, tbk, 1, d * 32:(d + 1) * 32]
                    dtv = d_[:, tbk, 0, d * 32:(d + 1) * 32]
                    return tbk, last, slice(tbk * 128, (tbk + 1) * 128), la, dtv, d_b

                def stageA1(bt_):
                    nonlocal kq4, kg, kct
                    kc_, d, g, hb_ = bt_
                    tbk, last, tk, la, dtv, d_b = geo(bt_)
                    bc_, bcb = bct[d]
                    if g == 0 and hb_ == 0:
                        psc, pcb = s.ps()
                        s.mm(psc[:, 0:32], stage[:, d, :], la, True, True, reads=[bstage, d_b], writes=[pcb])
                        cT, cTb = cumT[kct % 2]
                        kct += 1
                        s.op("dve", ins("tensor_copy", out=cT[:], in_=psc[:, 0:32]), reads=[pcb], writes=[cTb])
                        bst[("cT", kc_, d)] = (cT, cTb)
                    if hb_ == 0:
                        ps, pb = s.ps()
                        s.mm(ps[:, 0:128], bc_[:, g, tk], bc_[:, 4 + g, tk], True, True, reads=[bcb], writes=[pb])
                        cb_, cb_b = cbT[kg % 6]
                        s.op("act", ins("activation", out=cb_[:], in_=ps[:, 0:128], func=AF.Identity), reads=[pb], writes=[cb_b])
                        bst[("grp", kc_, d, g)] = dict(cb=(cb_, cb_b), x4=xs[kg % 4], ecs=[])
                        kg += 1
                    h0 = g * 8 + hb_ * 4
                    lt, ltb = lat[kq4 % 3]
                    cb4, cb4b = cB[kq4 % 4]
                    sg, sgb = seg[kq4 % 3]
                    E_, E_b = Et[kq4 % 6]
                    ec, ecb = ecB[kq4 % 8]
                    te_, te_b = te[kq4 % 3]
                    kq4 += 1
                    bst[("w",) + bt_] = (cb4, cb4b, sg, sgb, E_, E_b, ec, ecb, te_, te_b)
                    s.op("dve", ins("tensor_tensor", out=lt[:], in0=stage[:, d, :].unsqueeze(1).to_broadcast([128, 4, 128]),
                                    in1=la[:, h0:h0 + 4].unsqueeze(2).to_broadcast([128, 4, 128]), op=ALU.mult),
                         reads=[bstage, d_b], writes=[ltb])
                    ps2, pb2 = s.ps()
                    s.mm(ps2[:, 0:512], ones_f[:], lt[:].rearrange("p a b -> p (a b)"), True, True, reads=[ltb, cb_const], writes=[pb2])
                    s.op("act", ins("activation", out=cb4[:].rearrange("p a b -> p (a b)"), in_=ps2[:, 0:512], func=AF.Identity), reads=[pb2], writes=[cb4b])

                def stageA2a(bt_):
                    kc_, d, g, hb_ = bt_
                    cT, cTb = bst[("cT", kc_, d)]
                    G = bst[("grp", kc_, d, g)]
                    cb4, cb4b, sg, sgb, E_, E_b, ec, ecb, te_, te_b = bst[("w",) + bt_]
                    h0 = g * 8 + hb_ * 4
                    G["ecs"].append((ec, ecb))
                    for hh in range(4):
                        hq = h0 + hh
                        s.op("dve", ins("scalar_tensor_tensor", out=sg[:, hh, :], in0=cb4[:, hh, :], scalar=cT[:, hq:hq + 1], in1=negm[:, d, :],
                                        op0=ALU.subtract, op1=ALU.add),
                             reads=[cb4b, cTb, cbuf], writes=[sgb])
                    s.op("act", ins("activation", out=E_[:], in_=sg[:], func=AF.Exp), reads=[sgb], writes=[E_b])
                    s.op("act", ins("activation", out=ec[:], in_=cb4[:], func=AF.Exp), reads=[cb4b], writes=[ecb])

                def stageA2b(bt_):
                    kc_, d, g, hb_ = bt_
                    tbk, last, tk, la, dtv, d_b = geo(bt_)
                    x_, x_b = xts[d]
                    G = bst[("grp", kc_, d, g)]
                    x4, x4b = G["x4"]
                    cb4, cb4b, sg, sgb, E_, E_b, ec, ecb, te_, te_b = bst.pop(("w",) + bt_)
                    h0 = g * 8 + hb_ * 4
                    s.op("dve", ins("tensor_tensor", out=te_[:], in0=E_[:, :, last], in1=dtv[:, h0:h0 + 4], op=ALU.mult),
                         reads=[E_b, d_b], writes=[te_b])
                    s.op("dve", ins("tensor_tensor",
                                    out=x4[:, hb_ * 256:(hb_ + 1) * 256].rearrange("p (a b) -> p a b", b=64),
                                    in0=x_[:, tbk, h0 * 64:(h0 + 4) * 64].rearrange("p (a b) -> p a b", b=64),
                                    in1=te_[:].unsqueeze(2).to_broadcast([128, 4, 64]), op=ALU.mult),
                         reads=[x_b, te_b], writes=[x4b])
                    bst[("bat",) + bt_] = (E_, E_b, ec, ecb)

                def stageB(bt_):
                    nonlocal kw
                    kc_, d, g, hb_ = bt_
                    tbk, last, tk, la, dtv, d_b = geo(bt_)
                    bc_, bcb = bct[d]
                    x_, x_b = xts[d]
                    b_, b_b = bts[d]
                    y_, y_b = yts[d]
                    St, Stb = ST[d]
                    Sbt, Sbb = SbT[d]
                    G = bst[("grp", kc_, d, g)]
                    cb_, cb_b = G["cb"]
                    x4, x4b = G["x4"]
                    if hb_ == 0:
                        G["psy"] = s.ps("b")
                    psy, pyb = G["psy"]
                    E_, E_b, ec, ecb = bst.pop(("bat",) + bt_)
                    h0 = g * 8 + hb_ * 4
                    for hh in range(4):
                        hq = h0 + hh
                        W_, W_b = Wt[kw % 8]
                        C_, C_b = CEt[kw % 8]
                        kw += 1
                        s.op("dve", ins("scalar_tensor_tensor", out=W_[:], in0=E_[:, hh, :], scalar=dtv[:, hq:hq + 1], in1=cb_[:], op0=ALU.mult, op1=ALU.mult),
                             reads=[E_b, d_b, cb_b], writes=[W_b])
                        s.op("pool", ins("tensor_tensor", out=C_[:], in0=bc_[:, 4 + g, tk], in1=ec[:, hh, :], op=ALU.mult),
                             reads=[bcb, ecb], writes=[C_b])
                        ycol = slice((hq % 8) * 64, (hq % 8 + 1) * 64)
                        s.mm(psy[:, ycol], W_[:], x_[:, tbk, hq * 64:(hq + 1) * 64], True, False, reads=[W_b, x_b], writes=[pyb])
                        s.mm(psy[:, ycol], C_[:], Sbt[:, hq * 64:(hq + 1) * 64], False, True, reads=[C_b, Sbb[g]], writes=[pyb])
                    if hb_ == 0:
                        return
                    s.op("act", ins("activation", out=y_[:, tbk, g * 512:(g + 1) * 512], in_=psy[:, 0:512], func=AF.Identity), reads=[pyb], writes=[y_b])
                    pss, psb_ = s.ps()
                    s.mm(pss[:, 0:512], b_[:, tbk, g * 128:(g + 1) * 128], x4[:], True, True, reads=[b_b, x4b], writes=[psb_])
                    ecs = G["ecs"]
                    for hh in range(8):
                        hq = g * 8 + hh
                        ec2, ecb2 = ecs[hh // 4]
                        s.op("dve", ins("scalar_tensor_tensor",
                                        out=St[:, hq * 64:(hq + 1) * 64], in0=St[:, hq * 64:(hq + 1) * 64], scalar=ec2[:, hh % 4, last:last + 1], in1=pss[:, hh * 64:(hh + 1) * 64], op0=ALU.mult, op1=ALU.add),
                             reads=[psb_, ecb2, Stb[g]], writes=[Stb[g]])
                    s.op("pool", ins("tensor_copy", out=Sbt[:, g * 512:(g + 1) * 512], in_=St[:, g * 512:(g + 1) * 512]), reads=[Stb[g]], writes=[Sbb[g]])
                    del bst[("grp", kc_, d, g)]

                NB_ = len(batches)
                for t_ in range(NB_ + 3):
                    if t_ < NB_:
                        stageA1(batches[t_])
                    if 1 <= t_ < NB_ + 1:
                        stageA2a(batches[t_ - 1])
                    if 2 <= t_ < NB_ + 2:
                        stageA2b(batches[t_ - 2])
                    if t_ >= 3:
                        stageB(batches[t_ - 3])
                for d in range(2):
                    ti, t0, n, nb = cur[d]
                    y_, y_b = yts[d]
                    s.dma(rows_view(YD[d], 2048, t0, nb, 0, 2048), y_[:, 0:nb, :], reads=[y_b], writes=[Y_b[ti]])
            if si > 0:
                for d in range(2):
                    s.dma(st_out[d][si - 1], ST[d][0][:], reads=ST[d][1])
        chk(22)
        s.barrier()
        s.release()
        Wo = s.alloc([128, 16, 1024], BF16, "wo")
        wob = load_w(Wo, w3_o, 16)
        vec = s.alloc([128, 5, 32], F32, "vec")
        gnb = s.alloc([128, 2048], F32, "gnb")
        cbuf = Buf()
        s.dma(vec[:], ssd_vec.rearrange("a b -> (a b)").partition_broadcast(128).rearrange("p (a b) -> p a b", a=5), writes=[cbuf])
        s.dma(gnb[:], ssd_gn.partition_broadcast(128), writes=[cbuf])
        rctx = ResCtx()
        yf = (s.alloc([128, 4, 2048], BF16, "yf"), Buf())
        yb = (s.alloc([128, 4, 2048], BF16, "yb"), Buf())
        xt_ = (s.alloc([128, 4, 2048], BF16, "xt4"), Buf())
        zs = (s.alloc([128, 4, 2048], BF16, "zs"), Buf())
        yv = [(s.alloc([128, 2048], F32, "yv"), Buf()) for _ in range(2)]
        tv_ = (s.alloc([128, 2048], F32, "tv"), Buf())
        ynb = (s.alloc([128, 2048], BF16, "ynb"), Buf())
        ssq = [(s.alloc([128, 1], F32, "ssq"), Buf()) for _ in range(2)]
        oT = [(s.alloc([128, 16, 512], BF16, "oT"), Buf()) for _ in range(2)]
        kk = 0
        tv = 0
        for ti in range(NT):
            t0, n, cond, _, _ = TILES[ti]
            nb = n // 128
            s.dma(yf[0][:, 0:nb, :], rows_view(YD[0], 2048, t0, nb, 0, 2048), writes=[yf[1]])
            s.dma(yb[0][:, 0:nb, :], rows_view(YD[1], 2048, t0, nb, 0, 2048), writes=[yb[1]])
            s.dma(xt_[0][:, 0:nb, :], rows_view(XTs, 2048, t0, nb, 0, 2048), writes=[xt_[1]])
            s.dma(zs[0][:, 0:nb, :], rows_view(ZS, 2048, t0, nb, 0, 2048), writes=[zs[1]])
            o_, o_b = oT[ti % 2]
            for tbk in range(nb):
                y_, y_b = yv[kk % 2]
                sq_, sq_b = ssq[kk % 2]
                kk += 1
                t_, t_b = tv_
                s.op("dve", ins("tensor_tensor", out=y_[:], in0=yf[0][:, tbk, :], in1=yb[0][:, tbk, :], op=ALU.add), reads=[yf[1], yb[1]], writes=[y_b])
                s.op("pool", ins("tensor_tensor", out=t_[:].rearrange("p (a b) -> p a b", b=64), in0=xt_[0][:, tbk, :].rearrange("p (a b) -> p a b", b=64),
                                                              in1=vec[:, 4, :].unsqueeze(2).to_broadcast([128, 32, 64]), op=ALU.mult), reads=[xt_[1], cbuf], writes=[t_b])
                s.op("dve", ins("tensor_tensor", out=y_[:], in0=y_[:], in1=t_[:], op=ALU.add), reads=[y_b, t_b], writes=[y_b])
                s.op("dve", ins("tensor_tensor", out=y_[:], in0=y_[:], in1=zs[0][:, tbk, :], op=ALU.mult), reads=[y_b, zs[1]], writes=[y_b])
                s.op("pool", ins("memset", sq_[:], 0.0), writes=[sq_b])
                s.op("act", ins("activation", out=t_[:], in_=y_[:], func=AF.Square, accum_out=sq_[:, 0:1]), reads=[y_b, sq_b], writes=[t_b, sq_b])
                s.op("act", ins("activation", out=sq_[:], in_=sq_[:], func=AF.Sqrt, scale=1.0 / 2048, bias=EPSB[:, 0:1]), reads=[sq_b], writes=[sq_b])
                s.op("dve", ins("reciprocal", out=sq_[:], in_=sq_[:]), reads=[sq_b], writes=[sq_b])
                yn, ynb_ = ynb
                s.op("dve", ins("scalar_tensor_tensor", out=yn[:], in0=y_[:], scalar=sq_[:, 0:1], in1=gnb[:], op0=ALU.mult, op1=ALU.mult), reads=[y_b, sq_b, cbuf], writes=[ynb_])
                for c in range(16):
                    ps, pb = s.ps()
                    psv = ps[:].bitcast(BF16)
                    s.op("pe", ins("transpose", psv[:, 0:128], yn[:, c * 128:(c + 1) * 128], ident_bf[:]), reads=[ynb_, cb_const], writes=[pb])
                    if tv % 2 == 0:
                        s.op("act", ins("activation", out=o_[:, c, tbk * 128:(tbk + 1) * 128], in_=psv[:, 0:128], func=AF.Identity), reads=[pb], writes=[o_b])
                    else:
                        s.op("dve", ins("tensor_copy", out=o_[:, c, tbk * 128:(tbk + 1) * 128], in_=psv[:, 0:128]), reads=[pb], writes=[o_b])
                    tv += 1
            outproj_resid(rctx, l, 2, ti, Wo, wob, 16, o_, [o_b], Xsrc, Xsrc_b, X, X_b)

    import os
    STOP = int(os.environ.get("KSTOP", "99"))

    class _Stop(Exception):
        pass

    def chk(k):
        if STOP == k:
            raise _Stop()

    try:
        prologue()
        chk(0)
        Xsrc, Xsrc_b = xT, xT_b
        for l in range(NLAYERS):
            kind = l % 4
            if kind == 0:
                qkv_phase(l, w0_qk, 12, w0_v, 256, Xsrc, Xsrc_b, True, kout=(8, o_wk, 64), vout=o_wv)
                chk(1)
                attn_phase(l, "win", Xsrc, Xsrc_b)
                chk(2)
                outproj_phase(l, w0_o, 8, OS, None, Xsrc, Xsrc_b)
                chk(3)
            elif kind == 1:
                gla_layer(l, Xsrc, Xsrc_b)
            elif kind == 2:
                qkv_phase(l, w2_qk, 16, w2_v, 1024, Xsrc, Xsrc_b, True, kout=(8, o_dk, 128), vout=o_dv)
                attn_phase(l, "diff", Xsrc, Xsrc_b)
                outproj_phase(l, w2_o, 8, OS, None, Xsrc, Xsrc_b)
            else:
                ssd_layer(l, Xsrc, Xsrc_b)
            Xsrc, Xsrc_b = X, X_b
            ffn(l, Xsrc, Xsrc_b)
            chk(10 + l)
        s.barrier()
        s.release()
        nctx = NormCtx()
        for ti in range(NT):
            norm_tile(nctx, 0, 0, ti, Xsrc, Xsrc_b, final_out=yT)
    except _Stop:
        pass
    s.barrier()
    counts = s.emit()
    return nc, counts

def _fm(w):
    K, N = w.shape
    return np.ascontiguousarray(w.reshape(K // 128, 128, N).transpose(1, 0, 2))


def _pc(v):
    return np.ascontiguousarray(v.reshape(-1, 128).T)


def _consts():
    f32 = np.float32
    t = np.arange(LS)
    row = (t // 64).astype(f32)
    col = (t % 64).astype(f32)
    inv = (10000.0 ** (-np.arange(0, 32, 2, dtype=f32) / 32)).astype(f32)
    cos = np.zeros((128, LS), f32)
    sin = np.zeros((128, LS), f32)
    perm = np.zeros((128, 128), f32)
    for p in range(128):
        d = p % 64
        axis = d // 32
        idx = d % 16
        second = (d % 32) >= 16
        ang = (row if axis == 0 else col) * inv[idx]
        cos[p] = np.cos(ang)
        sin[p] = np.sin(ang) * (1.0 if second else -1.0)
        partner = p + 16 if not second else p - 16
        perm[partner, p] = 1.0
    j = np.arange(128)[:, None]
    i = np.arange(512)[None, :]
    wmask = np.zeros((128, 6, 512), f32)
    for mi in range(6):
        o = mi - 1
        wmask[:, mi, :] = (np.abs(i - (o * 128 + j)) <= 128).astype(f32)
    tri = np.zeros((128, 4, 128), f32)
    jj = np.arange(128)[:, None]
    ii = np.arange(128)[None, :]
    tri[:, 0, :] = (jj <= ii)
    tri[:, 1, :] = (jj >= ii)
    tri[0:64, 2, 0:64] = (jj[0:64] <= ii[:, 0:64])
    tri[0:64, 3, 0:64] = (jj[0:64] >= ii[:, 0:64])
    m64 = np.ones((128, 512), f32)
    m64[:, ::64] = 0.0
    return dict(rcos=cos, rsin=sin, perm=perm, wmask=wmask, tri=tri, m64=m64)


_PROG = {}


def _get_prog(nl):
    if nl not in _PROG:
        _PROG[nl] = build_program(nl)
    return _PROG[nl]


def kernel(NLAYERS=4, **inp):
    f32 = np.float32
    g = {k: np.asarray(v) for k, v in inp.items()}
    nc, counts = _get_prog(NLAYERS)
    common = dict(_consts())
    common["ada_w"] = np.ascontiguousarray(g["ada_w"].reshape(4, 8, 128, 6144).transpose(0, 2, 1, 3))
    common["ada_b"] = np.ascontiguousarray(g["ada_b"].reshape(4, 48, 128).transpose(2, 0, 1))
    common["nrm"] = np.ascontiguousarray(np.stack([g["norm_mix"], g["norm_ffn"]], axis=1).reshape(4, 2, 8, 128).transpose(3, 0, 1, 2))
    common["fnorm"] = _pc(g["final_norm"])
    common["w_up"] = np.ascontiguousarray(g["ffn_w_up"].reshape(4, 8, 128, 5632).transpose(0, 2, 1, 3))
    common["w_dn"] = np.ascontiguousarray(g["ffn_w_down"].reshape(4, 22, 128, 1024).transpose(0, 2, 1, 3))
    common["ffn_cw"] = np.ascontiguousarray(g["ffn_conv_w"].reshape(4, 3, 44, 128).transpose(3, 0, 1, 2))
    common["ffn_cb"] = np.ascontiguousarray(g["ffn_conv_b"].reshape(4, 44, 128).transpose(2, 0, 1))
    wq = g["win_w_qkv"][0]
    qcols = wq[:, 0:1024]
    kcols = wq[:, 1024:1280]
    vcols = wq[:, 1280:1536]
    kdup = np.concatenate([np.concatenate([kcols[:, h * 64:(h + 1) * 64]] * 2, axis=1) for h in range(4)], axis=1)
    common["w0_qk"] = _fm(np.concatenate([qcols, kdup], axis=1))
    common["w0_v"] = _fm(vcols)
    common["w0_o"] = _fm(g["win_w_o"][0])
    common["sink"] = np.ascontiguousarray(g["win_sink"][0])
    wg = g["gla_w_qkvr"][0]
    common["w1_qk"] = _fm(wg[:, 0:1024])
    common["w1_v"] = _fm(wg[:, 1024:2048])
    common["w1_r"] = _fm(wg[:, 2048:3072])
    common["w1_g1"] = _fm(np.concatenate([g["gla_w_gf1"][0], g["gla_w_gb1"][0]], axis=1))
    g2 = np.zeros((32, 1024), f32)
    g2[0:16, 0:512] = g["gla_w_gf2"][0]
    g2[16:32, 512:1024] = g["gla_w_gb2"][0]
    common["w1_g2"] = g2
    common["w1_gb"] = _pc(np.concatenate([g["gla_b_gf"][0], g["gla_b_gb"][0]]))
    common["w1_gn"] = _pc(g["gla_norm"][0])
    common["w1_o"] = _fm(g["gla_w_o"][0])
    wd = g["diff_w_qkv"][0]
    common["w2_qk"] = _fm(wd[:, 0:2048])
    common["w2_v"] = _fm(wd[:, 2048:3072])
    common["w2_o"] = _fm(g["diff_w_o"][0])
    common["lqk"] = np.ascontiguousarray(np.stack([g["diff_lq1"][0], g["diff_lk1"][0], g["diff_lq2"][0], g["diff_lk2"][0]]))
    common["w2_gn"] = np.ascontiguousarray(g["diff_norm"][0].reshape(128, 1))
    common.update(_host_ssd_common(g))
    in_maps = []
    for b in range(8):
        m = dict(common)
        xs = g["x_sample"][b]
        xp = g["x_prompt"][2 * b:2 * b + 2].reshape(512, 1024)
        xa = np.concatenate([xs, xp], axis=0)
        m["xT"] = np.ascontiguousarray(xa.T.reshape(8, 128, T).transpose(1, 0, 2))
        cond = np.stack([g["c"][b], g["c_ctx"]], axis=1)
        m["condT"] = np.ascontiguousarray(cond.reshape(8, 128, 2).transpose(1, 0, 2))
        ck = g["cache_win_k"][b, 0]
        kT = ck.transpose(2, 1, 0)
        m["ck0"] = np.ascontiguousarray(np.concatenate([kT, kT], axis=0))
        m["cv0"] = np.ascontiguousarray(g["cache_win_v"][b, 0].reshape(512, 256))
        m["s1f"] = np.ascontiguousarray(g["state_gla_fwd"][b, 0].transpose(1, 0, 2))
        m["s1b"] = np.ascontiguousarray(g["state_gla_bwd"][b, 0].transpose(1, 0, 2))
        dk = g["cache_diff_k"][b, 0]
        m["ck2"] = np.ascontiguousarray(dk.transpose(2, 3, 1, 0).reshape(128, 8, 512))
        m["cv2"] = np.ascontiguousarray(g["cache_diff_v"][b, 0].reshape(512, 1024))
        m.update(_host_ssd_core(g, b))
        in_maps.append(m)
    import os
    ncores = int(os.environ.get("KCORES", "8"))
    ktrace = os.environ.get("KTRACE", "") == "1"
    res = run_bass_kernel_spmd(nc, in_maps[:ncores], core_ids=list(range(ncores)), trace=ktrace) if ktrace else run_bass_kernel_spmd(nc, in_maps[:ncores], core_ids=list(range(ncores)))
    if ktrace:
        print("EXEC_TIME_NS", res.exec_time_ns)
    R = list(res.results) + [res.results[0]] * (8 - ncores)
    y_prompt = np.zeros((16, 256, 1024), f32)
    y_sample = np.zeros((8, 4096, 1024), f32)
    win_k = np.zeros((16, 1, 256, 4, 64), f32)
    win_v = np.zeros((16, 1, 256, 4, 64), f32)
    gla_f = np.zeros((16, 1, 4, 128, 256), f32)
    gla_b = np.zeros((16, 1, 4, 128, 256), f32)
    diff_k = np.zeros((16, 1, 256, 8, 2, 64), f32)
    diff_v = np.zeros((16, 1, 256, 8, 128), f32)
    ssd_f = np.zeros((16, 1, 32, 64, 128), f32)
    ssd_b = np.zeros((16, 1, 32, 64, 128), f32)
    for b in range(8):
        r = R[b]
        y = r["yT"].transpose(1, 0, 2).reshape(1024, T).T
        y_sample[b] = y[0:4096]
        y_prompt[2 * b:2 * b + 2] = y[4096:].reshape(2, 256, 1024)
        wk = r["o_wk"]
        win_k[2 * b:2 * b + 2, 0] = wk.transpose(2, 0, 1).reshape(2, 256, 4, 64)
        win_v[2 * b:2 * b + 2, 0] = r["o_wv"].reshape(2, 256, 4, 64)
        gla_f[2 * b:2 * b + 2, 0] = r["o_gf"].transpose(0, 2, 1, 3)
        gla_b[2 * b:2 * b + 2, 0] = r["o_gb"].transpose(0, 2, 1, 3)
        dk = r["o_dk"]
        diff_k[2 * b:2 * b + 2, 0] = dk.transpose(2, 0, 1).reshape(2, 256, 8, 2, 64)
        diff_v[2 * b:2 * b + 2, 0] = r["o_dv"].reshape(2, 256, 8, 128)
        _host_ssd_out(r, b, ssd_f, ssd_b)
    return (y_prompt, y_sample, win_k, win_v, gla_f, gla_b, diff_k, diff_v, ssd_f, ssd_b)


def _host_ssd_common(g):
    w = g["ssd_w_in"][0]
    out = {}
    out["w3_z"] = _fm(w[:, 0:2048])
    out["w3_xbc"] = _fm(w[:, 2048:5120])
    out["w3_dt"] = _fm(w[:, 5120:5184])
    out["w3_o"] = _fm(g["ssd_w_out"][0])
    out["ssd_cw"] = np.ascontiguousarray(g["ssd_conv_w"][0].reshape(3, 24, 128).transpose(2, 0, 1))
    out["ssd_cb"] = _pc(g["ssd_conv_b"][0])
    out["ssd_vec"] = np.ascontiguousarray(np.stack([g["ssd_a_log_f"][0], g["ssd_a_log_b"][0], g["ssd_dt_bias_f"][0], g["ssd_dt_bias_b"][0], g["ssd_d"][0]]))
    out["ssd_gn"] = np.ascontiguousarray(g["ssd_norm"][0])
    return out


def _host_ssd_core(g, b):
    return {"s3f": np.ascontiguousarray(g["state_ssd_fwd"][b, 0].transpose(2, 0, 1).reshape(128, 2048)),
            "s3b": np.ascontiguousarray(g["state_ssd_bwd"][b, 0].transpose(2, 0, 1).reshape(128, 2048))}


def _host_ssd_out(r, b, ssd_f, ssd_b):
    ssd_f[2 * b:2 * b + 2, 0] = r["o_sf"].reshape(2, 128, 32, 64).transpose(0, 2, 3, 1)
    ssd_b[2 * b:2 * b + 2, 0] = r["o_sb"].reshape(2, 128, 32, 64).transpose(0, 2, 3, 1)
```

```python
import numpy as np
import concourse.bass as bass
import concourse.mybir as mybir
from concourse.bass_utils import run_bass_kernel_spmd

F32 = mybir.dt.float32
BF16 = mybir.dt.bfloat16
AF = mybir.ActivationFunctionType
ALU = mybir.AluOpType
AX = mybir.AxisListType
AP = bass.AP

SB_BASE = 16512
SB_TOP = 229344
EPOCH = 30000


class Buf:
    __slots__ = ("w", "rs", "name")

    def __init__(self, name=""):
        self.w = None
        self.rs = []
        self.name = name


class Op:
    __slots__ = ("eng", "fn", "deps", "dma", "ms", "sem", "val", "need", "prev", "grp")

    def __init__(self, eng, fn, dma):
        self.eng = eng
        self.fn = fn
        self.dma = dma
        self.deps = []
        self.ms = None
        self.sem = None
        self.val = None
        self.need = False
        self.prev = None
        self.grp = None


class Sched:
    ENGS = ("pe", "act", "dve", "pool", "sp")

    def __init__(self, nc):
        self.nc = nc
        self.ops = {e: [] for e in self.ENGS}
        self.dmas_since_barrier = []
        self.all_dmas = []
        self.nps = 0
        self.nacc = 0
        self.psum = []
        for i in range(8):
            t = nc.alloc_psum_tensor("psb%d" % i, [128, 512], F32)
            self.psum.append((t, Buf("ps%d" % i)))
        self.sb_off = SB_BASE
        self.sb_mark = SB_BASE
        self.nalloc = 0

    def alloc(self, shape, dtype, name=None):
        nbytes = int(np.prod(shape[1:])) * (4 if dtype == F32 else 2)
        nbytes = (nbytes + 63) // 64 * 64
        off = self.sb_off
        assert off + nbytes <= SB_TOP, "SBUF overflow %d" % (off + nbytes - SB_TOP)
        self.sb_off += nbytes
        self.nalloc += 1
        t = self.nc.alloc_sbuf_tensor_at("sb%d_%s" % (self.nalloc, name or "t"), list(shape), dtype, offset=off)
        return t

    def mark(self):
        self.sb_mark = self.sb_off

    def release(self):
        self.sb_off = self.sb_mark

    def ps(self, pool="a"):
        if pool == "a":
            t, b = self.psum[self.nps % 6]
            self.nps += 1
        else:
            t, b = self.psum[6 + self.nacc % 2]
            self.nacc += 1
        return t, b

    def op(self, eng, fn, reads=(), writes=(), dma=False):
        o = Op(eng, fn, dma)
        deps = {}
        for b in reads:
            d = b.w
            if d is not None:
                if (not dma) and (not d.dma) and d.eng == eng and eng == "pe":
                    continue
                deps[id(d)] = d
        for b in writes:
            cand = list(b.rs)
            if b.w is not None:
                cand.append(b.w)
            for d in cand:
                if d is o:
                    continue
                if (not dma) and (not d.dma) and d.eng == eng:
                    continue
                deps[id(d)] = d
        o.deps = list(deps.values())
        for b in reads:
            if not dma:
                b.rs = [r for r in b.rs if r.dma or r.eng != eng]
            b.rs.append(o)
        for b in writes:
            b.w = o
            b.rs = []
        self.ops[eng].append(o)
        if dma:
            self.dmas_since_barrier.append(o)
            self.all_dmas.append(o)
        return o

    def barrier(self):
        lasts = []
        for e in self.ENGS:
            for o in reversed(self.ops[e]):
                if not o.dma and o.fn is not None:
                    lasts.append(o)
                    break
        deps = lasts + self.dmas_since_barrier
        self.dmas_since_barrier = []
        for e in self.ENGS:
            o = Op(e, None, False)
            o.deps = list(deps)
            self.ops[e].append(o)

    def dma(self, out, in_, reads=(), writes=(), eng=None):
        if eng is None:
            eng = "pool" if type(out.tensor).__name__.startswith("DRam") else "sp"
        return self.op(eng, lambda e: e.dma_start(out=out, in_=in_), reads, writes, dma=True)

    def mm(self, out, lhsT, rhs, start, stop, reads=(), writes=(), grp=None):
        o = self.op("pe", lambda e: e.matmul(out, lhsT, rhs, start=start, stop=stop), reads, writes)
        o.grp = grp
        return o

    def emit(self):
        nc = self.nc
        for e in self.ENGS:
            for o in self.ops[e]:
                for d in o.deps:
                    d.need = True
        sem_ctx = []
        import contextlib
        with contextlib.ExitStack() as st:
            tl = {}
            for e in ("pe", "act", "dve", "pool"):
                tl[e] = [st.enter_context(nc.semaphore("tl_%s_%d" % (e, i))) for i in range(5)]
            npool = {"sp": 36, "pool": 36, "act": 8}
            dpool = {e: [st.enter_context(nc.semaphore("dq_%s_%d" % (e, i))) for i in range(n)] for e, n in npool.items()}
            for e in self.ENGS:
                m = 0
                k = 0
                for o in self.ops[e]:
                    if o.fn is None:
                        continue
                    if o.dma:
                        P = len(dpool[e])
                        o.sem = dpool[e][k % P]
                        o.val = 16 * (k // P + 1)
                        k += 1
                    elif o.need:
                        o.sem = tl[e][m // EPOCH]
                        o.val = m % EPOCH + 1
                        m += 1
                assert m < EPOCH * 5, (e, m)
            final = {}
            for e, n in npool.items():
                for o in self.ops[e]:
                    if o.dma:
                        final[id(o.sem)] = (o.sem, o.val)

            def stream(e, eng):
                seen = {}

                def wait(sem, val):
                    if seen.get(id(sem), 0) < val:
                        eng.wait_ge(sem, val)
                        seen[id(sem)] = val

                ops_ = self.ops[e]
                i_ = 0
                while i_ < len(ops_):
                    o = ops_[i_]
                    j_ = i_ + 1
                    if o.grp is not None:
                        while j_ < len(ops_) and ops_[j_].grp == o.grp:
                            j_ += 1
                    for k_ in range(i_, j_):
                        for d in ops_[k_].deps:
                            wait(d.sem, d.val)
                    for k_ in range(i_, j_):
                        o = ops_[k_]
                        if o.fn is None:
                            continue
                        if o.dma and o.val > 16:
                            wait(o.sem, o.val - 16)
                        ins_ = o.fn(eng)
                        if o.dma:
                            ins_.then_inc(o.sem, 16)
                        elif o.need:
                            ins_.then_inc(o.sem, 1)
                    i_ = j_
                if e == "sp":
                    for sem, val in final.values():
                        wait(sem, val)

            with nc.Block() as block:
                @block.tensor
                def _(eng):
                    stream("pe", eng)

                @block.scalar
                def _(eng):
                    stream("act", eng)

                @block.vector
                def _(eng):
                    stream("dve", eng)

                @block.gpsimd
                def _(eng):
                    stream("pool", eng)

                @block.sync
                def _(eng):
                    stream("sp", eng)
        return {e: len(v) for e, v in self.ops.items()}


def ins(method, *a, **kw):
    return lambda e: getattr(e, method)(*a, **kw)


T = 4608
LS = 4096
TILES = [(i * 512, 512, 0, 0, 4096) for i in range(8)] + [(4096, 256, 1, 4096, 4352), (4352, 256, 1, 4352, 4608)]
EPS = 1e-6
DFF = 2816
LAM_INIT = 0.8 - 0.6 * float(np.exp(-0.3 * 2))


def build_program(NLAYERS=4, dbg=False):
    nc = bass.Bass("TRN2", target_bir_lowering=False)
    s = Sched(nc)
    I = {}
    O = {}

    def din(name, shape):
        I[name] = nc.dram_tensor(name, list(shape), F32, kind="ExternalInput").ap()
        return I[name]

    def dout(name, shape):
        O[name] = nc.dram_tensor(name, list(shape), F32, kind="ExternalOutput").ap()
        return O[name]

    def dscr(name, shape, dt):
        return nc.dram_tensor(name, list(shape), dt).ap()

    xT = din("xT", [128, 8, T])
    condT = din("condT", [128, 8, 2])
    ada_w = din("ada_w", [4, 128, 8, 6144])
    ada_b = din("ada_b", [128, 4, 48])
    nrm = din("nrm", [128, 4, 2, 8])
    fnorm = din("fnorm", [128, 8])
    w_up = din("w_up", [4, 128, 8, 5632])
    w_dn = din("w_dn", [4, 128, 22, 1024])
    ffn_cw = din("ffn_cw", [128, 4, 3, 44])
    ffn_cb = din("ffn_cb", [128, 4, 44])
    perm_in = din("perm", [128, 128])
    rcos = din("rcos", [128, LS])
    rsin = din("rsin", [128, LS])
    wmask_in = din("wmask", [128, 6, 512])
    tri_in = din("tri", [128, 4, 128])
    m64_in = din("m64", [128, 512])
    w0_qk = din("w0_qk", [128, 8, 1536])
    w0_v = din("w0_v", [128, 8, 256])
    w0_o = din("w0_o", [128, 8, 1024])
    sink_in = din("sink", [16])
    ck0 = din("ck0", [128, 4, 512])
    cv0 = din("cv0", [512, 256])
    w1_qk = din("w1_qk", [128, 8, 1024])
    w1_v = din("w1_v", [128, 8, 1024])
    w1_r = din("w1_r", [128, 8, 1024])
    w1_g1 = din("w1_g1", [128, 8, 32])
    w1_g2 = din("w1_g2", [32, 1024])
    w1_gb = din("w1_gb", [128, 8])
    w1_gn = din("w1_gn", [128, 2])
    w1_o = din("w1_o", [128, 8, 1024])
    s1f = din("s1f", [128, 4, 256])
    s1b = din("s1b", [128, 4, 256])
    w2_qk = din("w2_qk", [128, 8, 2048])
    w2_v = din("w2_v", [128, 8, 1024])
    w2_o = din("w2_o", [128, 8, 1024])
    lqk = din("lqk", [4, 64])
    w2_gn = din("w2_gn", [128, 1])
    ck2 = din("ck2", [128, 8, 512])
    cv2 = din("cv2", [512, 1024])

    w3_z = din("w3_z", [128, 8, 2048])
    w3_xbc = din("w3_xbc", [128, 8, 3072])
    w3_dt = din("w3_dt", [128, 8, 64])
    w3_o = din("w3_o", [128, 16, 1024])
    ssd_cw = din("ssd_cw", [128, 3, 24])
    ssd_cb = din("ssd_cb", [128, 24])
    ssd_vec = din("ssd_vec", [5, 32])
    ssd_gn = din("ssd_gn", [2048])
    s3f = din("s3f", [128, 2048])
    s3b = din("s3b", [128, 2048])

    yT = dout("yT", [128, 8, T])
    o_sf = dout("o_sf", [2, 128, 2048])
    o_sb = dout("o_sb", [2, 128, 2048])
    o_wk = dout("o_wk", [4, 64, 512])
    o_wv = dout("o_wv", [512, 256])
    o_gf = dout("o_gf", [2, 128, 4, 256])
    o_gb = dout("o_gb", [2, 128, 4, 256])
    o_dk = dout("o_dk", [8, 128, 512])
    o_dv = dout("o_dv", [512, 1024])

    X = dscr("X", [128, 8, T], F32)
    U = dscr("U", [128, 44, T], BF16)
    QK = dscr("QK", [128, 16, T], BF16)
    VS = dscr("VS", [T, 1024], BF16)
    OS = dscr("OS", [128, 8, T], BF16)
    OS2 = dscr("OS2", [128, 8, T], BF16)
    RS = dscr("RS", [128, 8, T], BF16)
    QD = dscr("QD", [2, 128, 4, T], BF16)
    KD = dscr("KD", [2, 128, 4, T], BF16)
    KR = dscr("KR", [T, 8, 128], BF16)
    ZS = dscr("ZS", [T, 2048], BF16)
    DTS = dscr("DTS", [T, 128], F32)
    XTs = dscr("XTs", [T, 2048], BF16)
    BTs = dscr("BTs", [T, 512], BF16)
    YD = [dscr("YD0", [T, 2048], BF16), dscr("YD1", [T, 2048], BF16)]

    NT = len(TILES)
    import os
    STOP = int(os.environ.get("KSTOP", "99"))

    def rows_view(dr, rowlen, r0, nb, c0, w):
        return AP(dr.tensor, r0 * rowlen + c0, [[rowlen, 128], [128 * rowlen, nb], [1, w]])

    def kr_view(r0, nb, j0, nj):
        return AP(KR.tensor, r0 * 1024 + j0 * 128, [[1024, 128], [128 * 1024, nb], [128, nj], [1, 128]])
    import os
    KQ = os.environ.get("KQ", "")

    def tb(name):
        return [Buf(name + str(i)) for i in range(NT)]

    xT_b = tb("xT")
    X_b = tb("X")

    MODS = s.alloc([128, 4, 6, 8, 2], F32, "mods")
    GS = s.alloc([128, 4, 2, 8, 2], F32, "gs")
    NRM = s.alloc([128, 4, 2, 8], F32, "nrm")
    FNRM = s.alloc([128, 8], F32, "fnrm")
    ones_bf = s.alloc([128, 128], BF16, "ones")
    ones_f = s.alloc([128, 128], F32, "onesf")
    ident_bf = s.alloc([128, 128], BF16, "ident")
    perm_bf = s.alloc([128, 128], BF16, "perm")
    tri_bf = s.alloc([128, 4, 128], BF16, "tri")
    m64 = s.alloc([128, 512], F32, "m64")
    cb_const = Buf("consts")
    stage = s.alloc([128, 4, 128], F32, "stage")
    bstage = Buf()

    s.dma(NRM[:], nrm, writes=[cb_const])
    s.dma(FNRM[:], fnorm, writes=[cb_const])
    s.dma(m64[:], m64_in, writes=[cb_const])
    s.op("pool", ins("memset", ones_f[:], 1.0), writes=[cb_const])
    s.op("dve", ins("tensor_copy", out=ones_bf[:], in_=ones_f[:]), reads=[cb_const], writes=[cb_const])
    s.dma(stage[:, 0, :], perm_in, writes=[bstage])
    s.op("dve", ins("tensor_copy", out=perm_bf[:], in_=stage[:, 0, :]), reads=[bstage], writes=[cb_const])
    s.op("pool", ins("memset", stage[:, 1, :], 1.0), reads=[], writes=[bstage])
    s.op("pool", ins("affine_select", out=stage[:, 1, :], in_=stage[:, 1, :], pattern=[[-1, 128]], compare_op=ALU.is_equal, fill=0.0, base=0, channel_multiplier=1), reads=[bstage], writes=[bstage])
    s.op("dve", ins("tensor_copy", out=ident_bf[:], in_=stage[:, 1, :]), reads=[bstage], writes=[cb_const])
    s.barrier()
    s.dma(stage[:], tri_in, writes=[bstage])
    s.op("dve", ins("tensor_copy", out=tri_bf[:], in_=stage[:]), reads=[bstage], writes=[cb_const])
    s.mark()

    def mod_ap(l, k, c, cond):
        return MODS[:, l, k, c, cond:cond + 1]

    def prologue():
        s.barrier()
        s.release()
        sc = s.alloc([128, 8, 2], F32, "sc")
        scb = Buf()
        adb = s.alloc([128, 4, 48], F32, "adb")
        adbb = Buf()
        s.dma(sc[:], condT, writes=[scb])
        s.dma(adb[:], ada_b, writes=[adbb])
        s.op("act", ins("activation", out=sc[:], in_=sc[:], func=AF.Silu), reads=[scb], writes=[scb])
        wb = [s.alloc([128, 8, 1024], F32, "adaw%d" % i) for i in range(2)]
        wbb = [[Buf() for _ in range(2)] for _ in range(2)]
        it = 0
        for l in range(NLAYERS):
            for g in range(6):
                w = wb[it % 2]
                bb = wbb[it % 2]
                for hh in range(2):
                    s.dma(w[:, hh * 4:(hh + 1) * 4, :], ada_w[l, :, hh * 4:(hh + 1) * 4, g * 1024:(g + 1) * 1024], writes=[bb[hh]], eng=("sp" if hh == 0 else "act"))
                ps, pb = s.ps()
                for cc in range(8):
                    for kc in range(8):
                        s.mm(ps[:, cc * 2:cc * 2 + 2], w[:, kc, cc * 128:(cc + 1) * 128], sc[:, kc, :], kc == 0, kc == 7, reads=[bb[kc // 4], scb], writes=[pb])
                s.op("dve", ins("tensor_tensor",
                    out=MODS[:, l, g, :, :], in0=ps[:, 0:16].rearrange("p (c t) -> p c t", t=2),
                    in1=adb[:, l, g * 8:(g + 1) * 8].unsqueeze(2).to_broadcast([128, 8, 2]), op=ALU.add),
                    reads=[pb, adbb], writes=[cb_const])
                it += 1
            for which in range(2):
                k = 1 if which == 0 else 4
                s.op("dve", ins("scalar_tensor_tensor",
                    out=GS[:, l, which, :, :], in0=MODS[:, l, k, :, :], scalar=1.0,
                    in1=NRM[:, l, which, :].unsqueeze(2).to_broadcast([128, 8, 2]), op0=ALU.add, op1=ALU.mult),
                    reads=[cb_const], writes=[cb_const])

    def load_w(dst, src, nk, split=1):
        bufs = []
        for kc in range(nk):
            b = Buf()
            s.dma(dst[:, kc, :], src[:, kc, :], writes=[b], eng="pool")
            bufs.append(b)
        return bufs

    class NormCtx:
        def __init__(self, nbuf=2):
            self.nbuf = nbuf
            self.xt = [(s.alloc([128, 8, 512], F32, "nxt"), Buf()) for _ in range(nbuf)]
            self.h = [(s.alloc([128, 8, 512], BF16, "nh"), Buf()) for _ in range(nbuf)]
            self.sq = (s.alloc([128, 8, 512], BF16, "nsq"), Buf())
            self.tmp = (s.alloc([128, 8, 512], F32, "ntmp"), Buf())
            self.rstd = [(s.alloc([128, 512], F32, "nrs"), Buf()) for _ in range(2)]
            self.k = 0

    def norm_tile(ctx, l, which, ti, Xsrc, Xsrc_b, final_out=None):
        t0, n, cond, _, _ = TILES[ti]
        k = ctx.k
        ctx.k += 1
        xt, xtb = ctx.xt[k % ctx.nbuf]
        h, hb = ctx.h[k % ctx.nbuf]
        sq, sqb = ctx.sq
        tmp, tmpb = ctx.tmp
        rstd, rb = ctx.rstd[k % 2]
        s.dma(xt[:, :, :n], Xsrc[:, :, t0:t0 + n], reads=[Xsrc_b[ti]], writes=[xtb])
        s.op("act", ins("activation", out=sq[:, :, :n], in_=xt[:, :, :n], func=AF.Square), reads=[xtb], writes=[sqb])
        ps, pb = s.ps()
        for c in range(8):
            s.mm(ps[:, :n], ones_bf[:], sq[:, c, :n], c == 0, c == 7, reads=[sqb, cb_const], writes=[pb])
        s.op("act", ins("activation", out=rstd[:, :n], in_=ps[:, :n], func=AF.Sqrt, scale=1.0 / 1024, bias=EPSB[:, 0:1]), reads=[pb], writes=[rb])
        s.op("dve", ins("reciprocal", out=rstd[:, :n], in_=rstd[:, :n]), reads=[rb], writes=[rb])
        s.op("dve", ins("tensor_tensor", out=tmp[:, :, :n], in0=xt[:, :, :n], in1=rstd[:, :n].unsqueeze(1).to_broadcast([128, 8, n]), op=ALU.mult),
             reads=[xtb, rb], writes=[tmpb])
        if final_out is None:
            for c in range(8):
                s.op("act", ins("activation", out=h[:, c, :n], in_=tmp[:, c, :n], func=AF.Identity,
                                                          scale=GS[:, l, which, c, cond:cond + 1], bias=mod_ap(l, 0 if which == 0 else 3, c, cond)),
                     reads=[tmpb, cb_const], writes=[hb])
            return h, hb
        else:
            for c in range(8):
                s.op("act", ins("activation", out=xt[:, c, :n], in_=tmp[:, c, :n], func=AF.Identity, scale=FNRM[:, c:c + 1]),
                     reads=[tmpb, cb_const], writes=[xtb])
            s.dma(final_out[:, :, t0:t0 + n], xt[:, :, :n], reads=[xtb])
            return None, None

    EPSB = s.alloc([128, 1], F32, "epsb")
    s.op("pool", ins("memset", EPSB[:], EPS), writes=[cb_const])
    s.mark()

    class ResCtx:
        def __init__(self):
            self.xo = [(s.alloc([128, 8, 512], F32, "xo"), Buf()) for _ in range(2)]
            self.k = 0

    def outproj_resid(rctx, l, gate_k, ti, W, wbufs, nk, rhs, rhsbufs, Xsrc, Xsrc_b, Xdst, Xdst_b):
        t0, n, cond, _, _ = TILES[ti]
        xo, xob = rctx.xo[rctx.k % 2]
        rctx.k += 1
        s.dma(xo[:, :, :n], Xsrc[:, :, t0:t0 + n], reads=[Xsrc_b[ti]], writes=[xob])
        for dc in range(8):
            ps, pb = s.ps()
            for kc in range(nk):
                s.mm(ps[:, :n], W[:, kc, dc * 128:(dc + 1) * 128], rhs[:, kc, :n], kc == 0, kc == nk - 1,
                     reads=[wbufs[kc]] + list(rhsbufs), writes=[pb])
            s.op("dve", ins("scalar_tensor_tensor", out=xo[:, dc, :n], in0=ps[:, :n], scalar=mod_ap(l, gate_k, dc, cond),
                                                                      in1=xo[:, dc, :n], op0=ALU.mult, op1=ALU.add),
                 reads=[pb, xob, cb_const], writes=[xob])
        s.dma(Xdst[:, :, t0:t0 + n], xo[:, :, :n], reads=[xob], writes=[Xdst_b[ti]])

    def ffn(l, Xsrc, Xsrc_b):
        s.barrier()
        s.release()
        Wup = s.alloc([128, 8, 5632], BF16, "wup")
        wb = load_w(Wup, w_up[l], 8)
        nctx = NormCtx()
        ub = [(s.alloc([128, 11, 512], BF16, "ub"), [Buf() for _ in range(11)]) for _ in range(2)]
        U_b = [[Buf() for _ in range(4)] for _ in range(NT)]
        ku = 0
        ev = 0
        for ti in range(NT):
            t0, n, cond, _, _ = TILES[ti]
            h, hb = norm_tile(nctx, l, 1, ti, Xsrc, Xsrc_b)
            for grp in range(4):
                ut, utb = ub[ku % 2]
                ku += 1
                for cc in range(11):
                    col = grp * 11 + cc
                    ps, pb = s.ps()
                    for kc in range(8):
                        s.mm(ps[:, :n], Wup[:, kc, col * 128:(col + 1) * 128], h[:, kc, :n], kc == 0, kc == 7, reads=[wb[kc], hb], writes=[pb])
                    if ev % 2 == 0:
                        s.op("act", ins("activation", out=ut[:, cc, :n], in_=ps[:, :n], func=AF.Identity), reads=[pb], writes=[utb[cc]])
                    else:
                        s.op("dve", ins("tensor_copy", out=ut[:, cc, :n], in_=ps[:, :n]), reads=[pb], writes=[utb[cc]])
                    ev += 1
                s.dma(U[:, grp * 11:(grp + 1) * 11, t0:t0 + n], ut[:, :, :n], reads=utb, writes=[U_b[ti][grp]])
        s.barrier()
        s.release()
        Wd = s.alloc([128, 22, 1024], BF16, "wd")
        wdb = load_w(Wd, w_dn[l], 22)
        cw = s.alloc([128, 3, 44], F32, "cw")
        cbias = s.alloc([128, 44], F32, "cbias")
        cwb = Buf()
        s.dma(cw[:], ffn_cw[:, l, :, :], writes=[cwb])
        s.dma(cbias[:], ffn_cb[:, l, :], writes=[cwb])
        ug = s.alloc([128, 44, 514], BF16, "ug")
        ugb = [Buf() for _ in range(4)]
        at = [(s.alloc([128, 22, 512], BF16, "at"), Buf()) for _ in range(2)]
        tmps = [[(s.alloc([128, 512], F32, "ft"), Buf()) for _ in range(3)] for _ in range(4)]
        rctx = ResCtx()
        kp = 0
        for ti in range(NT):
            t0, n, cond, s0, s1 = TILES[ti]
            lo = (t0 - 1) >= s0
            hi = (t0 + n) < s1
            for grp in range(4):
                gs_ = slice(grp * 11, (grp + 1) * 11)
                if not lo:
                    s.op("pool", ins("memset", ug[:, gs_, 0:1], 0.0), writes=[ugb[grp]])
                if not hi:
                    s.op("pool", ins("memset", ug[:, gs_, n + 1:n + 2], 0.0), writes=[ugb[grp]])
                c0 = 0 if lo else 1
                c1 = n + 2 if hi else n + 1
                rd = [U_b[ti][grp]]
                if lo:
                    rd.append(U_b[ti - 1][grp])
                if hi:
                    rd.append(U_b[ti + 1][grp])
                s.dma(ug[:, gs_, c0:c1], U[:, gs_, t0 - 1 + c0:t0 - 1 + c1], reads=rd, writes=[ugb[grp]])
            a, ab = at[ti % 2]
            pst = {}

            def f2s1(cc):
                nonlocal kp
                res = []
                for half in range(2):
                    col = cc + 22 * half
                    tt, ttb = tmps[kp % 4][half]
                    grp = col // 11
                    s.op("act", ins("activation", out=tt[:, :n], in_=ug[:, col, 1:n + 1], func=AF.Identity,
                                    scale=cw[:, 1, col:col + 1], bias=cbias[:, col:col + 1]),
                         reads=[ugb[grp], cwb], writes=[ttb])
                    res.append((tt, ttb, col, grp))
                sg, sgb = tmps[kp % 4][2]
                kp += 1
                pst[cc] = (res, sg, sgb)

            def f2s2(cc):
                res, sg, sgb = pst[cc]
                for (tt, ttb, col, grp) in res:
                    s.op("dve", ins("scalar_tensor_tensor", out=tt[:, :n], in0=ug[:, col, 0:n], scalar=cw[:, 0, col:col + 1],
                                    in1=tt[:, :n], op0=ALU.mult, op1=ALU.add),
                         reads=[ugb[grp], cwb, ttb], writes=[ttb])
                    s.op("dve", ins("scalar_tensor_tensor", out=tt[:, :n], in0=ug[:, col, 2:n + 2], scalar=cw[:, 2, col:col + 1],
                                    in1=tt[:, :n], op0=ALU.mult, op1=ALU.add),
                         reads=[ugb[grp], cwb, ttb], writes=[ttb])

            def f2s3(cc):
                res, sg, sgb = pst.pop(cc)
                s.op("act", ins("activation", out=sg[:, :n], in_=res[0][0][:, :n], func=AF.Silu), reads=[res[0][1]], writes=[sgb])
                s.op("pool", ins("tensor_tensor", out=a[:, cc, :n], in0=sg[:, :n], in1=res[1][0][:, :n], op=ALU.mult),
                     reads=[sgb, res[1][1]], writes=[ab])

            for t_ in range(22 + 2):
                if t_ < 22:
                    f2s1(t_)
                if 1 <= t_ < 23:
                    f2s2(t_ - 1)
                if t_ >= 2:
                    f2s3(t_ - 2)
            outproj_resid(rctx, l, 5, ti, Wd, wdb, 22, a, [ab], Xsrc, Xsrc_b, X, X_b)

    def qkv_phase(l, Wqk_src, nqk, Wv_src, nv, Xsrc, Xsrc_b, rope, kout=None, vout=None, post=None):
        s.barrier()
        s.release()
        Wqk = s.alloc([128, 8, nqk * 128], BF16, "wqk")
        wqb = load_w(Wqk, Wqk_src, 8)
        Wv = s.alloc([128, 8, nv], BF16, "wv")
        wvb = load_w(Wv, Wv_src, 8)
        nctx = NormCtx()
        qk = [(s.alloc([128, nqk, 512], BF16, "qk"), [Buf() for _ in range(nqk)]) for _ in range(2)]
        vt = [(s.alloc([128, 4, nv], BF16, "vt"), Buf()) for _ in range(2)]
        qb = [(s.alloc([128, 512], BF16, "qb"), Buf()) for _ in range(3)]
        t12 = [[(s.alloc([128, 512], F32, "rt"), Buf()) for _ in range(2)] for _ in range(3)]
        cs = [(s.alloc([128, 2, 512], F32, "cs"), Buf()) for _ in range(1)]
        kf = [(s.alloc([128, 256], F32, "kf"), Buf()) for _ in range(2)]
        vf = [(s.alloc([128, 512], F32, "vf"), Buf()) for _ in range(2)]
        QK_b = [Buf() for _ in range(NT)]
        VS_b = [Buf() for _ in range(NT)]
        if "L" in KQ:
            return
        kq = 0
        kk = 0
        kv = 0
        for ti in range(NT):
            t0, n, cond, s0, s1 = TILES[ti]
            h, hb = norm_tile(nctx, l, 0, ti, Xsrc, Xsrc_b)
            if "N" in KQ:
                continue
            qkt, qkb = qk[ti % 2]
            dorope = rope and cond == 0 and ("r" not in KQ)
            if dorope:
                cst, csb = cs[0]
                s.dma(cst[:, 0, :], rcos[:, t0:t0 + n], writes=[csb])
                s.dma(cst[:, 1, :], rsin[:, t0:t0 + n], writes=[csb])
            ropeq = []

            def rope_fin(item):
                cc_, q_, q_b, t1, t1b, t2, t2b = item
                ps2, pb2 = s.ps()
                s.mm(ps2[:, :n], perm_bf[:], q_[:, :n], True, True, reads=[q_b, cb_const], writes=[pb2])
                s.op("dve", ins("tensor_tensor", out=t1[:, :n], in0=q_[:, :n], in1=cst[:, 0, :n], op=ALU.mult),
                     reads=[q_b, csb], writes=[t1b])
                s.op("dve", ins("tensor_tensor", out=t2[:, :n], in0=ps2[:, :n], in1=cst[:, 1, :n], op=ALU.mult),
                     reads=[pb2, csb], writes=[t2b])
                s.op("pool", ins("tensor_tensor", out=qkt[:, cc_, :n], in0=t1[:, :n], in1=t2[:, :n], op=ALU.add),
                     reads=[t1b, t2b], writes=[qkb[cc_]])

            for cc in range(nqk):
                ps, pb = s.ps()
                for kc in range(8):
                    s.mm(ps[:, :n], Wqk[:, kc, cc * 128:(cc + 1) * 128], h[:, kc, :n], kc == 0, kc == 7, reads=[wqb[kc], hb], writes=[pb])
                if dorope:
                    q_, q_b = qb[kq % 3]
                    t1, t1b = t12[kq % 3][0]
                    t2, t2b = t12[kq % 3][1]
                    kq += 1
                    s.op("act", ins("activation", out=q_[:, :n], in_=ps[:, :n], func=AF.Identity), reads=[pb], writes=[q_b])
                    ropeq.append((cc, q_, q_b, t1, t1b, t2, t2b))
                    if len(ropeq) > 1:
                        rope_fin(ropeq.pop(0))
                else:
                    if kout is not None and cond == 1 and cc >= kout[0]:
                        kft, kfb = kf[kk % 2]
                        kk += 1
                        rows = kout[2]
                        s.op("dve", ins("tensor_copy", out=kft[:, :n], in_=ps[:, :n]), reads=[pb], writes=[kfb])
                        s.op("act", ins("activation", out=qkt[:, cc, :n], in_=kft[:, :n], func=AF.Identity), reads=[kfb], writes=[qkb[cc]])
                        s.dma(kout[1][cc - kout[0], :, t0 - LS:t0 - LS + n], kft[0:rows, :n], reads=[kfb])
                    else:
                        s.op("act", ins("activation", out=qkt[:, cc, :n], in_=ps[:, :n], func=AF.Identity), reads=[pb], writes=[qkb[cc]])
                if post is not None:
                    post(ti, cc, ps, pb)
            while ropeq:
                rope_fin(ropeq.pop(0))
            vtt, vtb = vt[ti % 2]
            for tbk in range(n // 128 if "v" not in KQ else 0):
                for vg in range((nv + 511) // 512):
                    w_ = min(512, nv - vg * 512)
                    ps, pb = s.ps()
                    for kc in range(8):
                        s.mm(ps[:, :w_], h[:, kc, tbk * 128:(tbk + 1) * 128], Wv[:, kc, vg * 512:vg * 512 + w_], kc == 0, kc == 7, reads=[wvb[kc], hb], writes=[pb])
                    if vout is not None and cond == 1:
                        vft, vfb = vf[kv % 2]
                        kv += 1
                        s.op("dve", ins("tensor_copy", out=vft[:, :w_], in_=ps[:, :w_]), reads=[pb], writes=[vfb])
                        s.op("act", ins("activation", out=vtt[:, tbk, vg * 512:vg * 512 + w_], in_=vft[:, :w_], func=AF.Identity),
                             reads=[vfb], writes=[vtb])
                        r0 = t0 - LS + tbk * 128
                        s.dma(vout[r0:r0 + 128, vg * 512:vg * 512 + w_], vft[:, :w_], reads=[vfb])
                    else:
                        s.op("act", ins("activation", out=vtt[:, tbk, vg * 512:vg * 512 + w_], in_=ps[:, :w_], func=AF.Identity),
                             reads=[pb], writes=[vtb])
            if "q" not in KQ:
                s.dma(QK[:, 0:nqk, t0:t0 + n], qkt[:, :, :n], reads=qkb, writes=[QK_b[ti]])
            if "v" not in KQ and "s" not in KQ:
              s.dma(rows_view(VS, 1024, t0, n // 128, 0, nv), vtt[:, 0:n // 128, :], reads=[vtb], writes=[VS_b[ti]])

    def attn_phase(l, kind, Xsrc, Xsrc_b):
        s.barrier()
        s.release()
        win = kind == "win"
        nunits = 8
        kbase = 8
        Kt = [(s.alloc([128, T], BF16, "kt"), Buf()) for _ in range(2)]
        Va = [(s.alloc([128, 36, 128], BF16, "va"), Buf()) for _ in range(2)]
        Qt = [(s.alloc([128, LS], BF16, "qt"), Buf()) for _ in range(2)]
        Ot = [(s.alloc([128, LS], BF16, "ot"), Buf()) for _ in range(2)]
        pT = [(s.alloc([128, 512], BF16, "pT"), Buf()) for _ in range(8)]
        rec = [(s.alloc([128, 512], F32, "rec"), Buf()) for _ in range(8)]
        OS_b = [[Buf() for _ in range(nunits)] for _ in range(3)]
        cbuf = Buf()
        if win:
            masks = s.alloc([128, 6, 512], BF16, "masks")
            s.dma(masks[:], wmask_in, writes=[cbuf], eng="pool")
            esink = s.alloc([128, 16], F32, "esink")
            s.dma(esink[:], sink_in.partition_broadcast(128), writes=[cbuf])
            s.op("act", ins("activation", out=esink[:], in_=esink[:], func=AF.Exp), reads=[cbuf], writes=[cbuf])
            for i in range(2):
                s.op("pool", ins("memset", Va[i][0][:, :, 64:128], 1.0), writes=[Va[i][1]])
        else:
            acc = [(s.alloc([128, 512], F32, "acc"), Buf()) for _ in range(8)]
            osb = [(s.alloc([128, 512], F32, "osb"), Buf()) for _ in range(8)]
            sqd = [(s.alloc([128, 512], BF16, "sqd"), Buf()) for _ in range(4)]
            lq = s.alloc([128, 4, 64], F32, "lq")
            lam = s.alloc([128, 4], F32, "lam")
            gsub = s.alloc([128, 1], F32, "gsub")
            s.dma(lq[:], lqk.rearrange("a b -> (a b)").partition_broadcast(128).rearrange("p (a b) -> p a b", a=4), writes=[cbuf])
            s.dma(gsub[:], w2_gn, writes=[cbuf])
            s.op("dve", ins("tensor_tensor", out=lq[:, 0, :], in0=lq[:, 0, :], in1=lq[:, 1, :], op=ALU.mult), reads=[cbuf], writes=[cbuf])
            s.op("dve", ins("tensor_tensor", out=lq[:, 2, :], in0=lq[:, 2, :], in1=lq[:, 3, :], op=ALU.mult), reads=[cbuf], writes=[cbuf])
            s.op("dve", ins("reduce_sum", out=lam[:, 0:1], in_=lq[:, 0, :], axis=AX.X), reads=[cbuf], writes=[cbuf])
            s.op("dve", ins("reduce_sum", out=lam[:, 1:2], in_=lq[:, 2, :], axis=AX.X), reads=[cbuf], writes=[cbuf])
            s.op("act", ins("activation", out=lam[:, 0:2], in_=lam[:, 0:2], func=AF.Exp), reads=[cbuf], writes=[cbuf])
            s.op("dve", ins("tensor_tensor", out=lam[:, 2:3], in0=lam[:, 1:2], in1=lam[:, 0:1], op=ALU.subtract), reads=[cbuf], writes=[cbuf])
            s.op("dve", ins("tensor_scalar", out=lam[:, 2:3], in0=lam[:, 2:3], scalar1=-LAM_INIT, scalar2=None, op0=ALU.add), reads=[cbuf], writes=[cbuf])
            s.op("dve", ins("tensor_scalar", out=lam[:, 3:4], in0=gsub[:], scalar1=1.0 - LAM_INIT, scalar2=None, op0=ALU.mult), reads=[cbuf], writes=[cbuf])
        ku = 0
        kp = 0
        kr = 0
        ka = 0
        SEQS = [(0, 4096, 0), (4096, 256, 1), (4352, 256, 2)]
        for (q0, L, si) in SEQS:
            sample = si == 0
            nkb_lat = L // 128
            for u in range(nunits):
                kt, ktb = Kt[ku % 2]
                va, vab = Va[ku % 2]
                qt, qtb = Qt[ku % 2]
                ot, otb = Ot[ku % 2]
                ku += 1
                s.dma(qt[:, 0:L], QK[:, u, q0:q0 + L], writes=[qtb])
                if win:
                    g = u // 2
                    s.dma(kt[:, 0:L], QK[:, kbase + g, q0:q0 + L], writes=[ktb])
                    s.dma(va[:, 0:nkb_lat, 0:64], rows_view(VS, 1024, q0, nkb_lat, g * 64, 64), writes=[vab])
                    if sample:
                        s.dma(kt[:, L:L + 512], ck0[:, g, :], writes=[ktb], eng="pool")
                        s.dma(va[:, 32:36, 0:64], rows_view(cv0, 256, 0, 4, g * 64, 64), writes=[vab], eng="pool")
                else:
                    s.dma(kt[:, 0:L], QK[:, kbase + u, q0:q0 + L], writes=[ktb])
                    s.dma(va[:, 0:nkb_lat, :], rows_view(VS, 1024, q0, nkb_lat, u * 128, 128), writes=[vab])
                    if sample:
                        s.dma(kt[:, L:L + 512], ck2[:, u, :], writes=[ktb], eng="pool")
                        s.dma(va[:, 32:36, :], rows_view(cv2, 1024, 0, 4, u * 128, 128), writes=[vab], eng="pool")
                nq = 512 if sample else 256
                LOOK = 4
                tasks = []
                for qi in range(L // nq):
                    for e_ in range(2):
                        if win and sample:
                            kbs = [(kb, kb - qi * 4 + 1) for kb in range(qi * 4 - 1, qi * 4 + 5) if 0 <= kb < 32] + [(32 + j, None) for j in range(4)]
                        elif sample:
                            kbs = [(kb, None) for kb in range(36)]
                        else:
                            kbs = [(kb, None) for kb in range(2)]
                        for i, (kb, mi) in enumerate(kbs):
                            tasks.append((qi, e_, i, kb, mi, i == len(kbs) - 1))
                stt = {}

                def stage1(tk_):
                    nonlocal kp, ka
                    qi, e_, i, kb, mi, lastb = tk_
                    qs = slice(qi * nq, (qi + 1) * nq)
                    pr = slice(e_ * 64, (e_ + 1) * 64)
                    if i == 0:
                        d_ = {}
                        d_["pso"], d_["pob"] = s.ps("b")
                        if not win:
                            d_["acs"] = {"pool": acc[ka % 8], "dve": acc[(ka + 1) % 8]}
                            d_["acn"] = {"pool": 0, "dve": 0}
                            ka += 2
                        stt[(qi, e_)] = d_
                    d_ = stt[(qi, e_)]
                    pss, psb_ = s.ps()
                    s.mm(pss[:, :nq], kt[pr, kb * 128:(kb + 1) * 128], qt[pr, qs], True, True, reads=[ktb, qtb], writes=[psb_], grp=("q", ku, cur_it[0] // 2))
                    p_, p_b = pT[kp % 8]
                    kp += 1
                    s.op("act", ins("activation", out=p_[:, :nq], in_=pss[:, :nq], func=AF.Exp, scale=0.125), reads=[psb_], writes=[p_b])
                    if mi is not None:
                        s.op("pool" if (kp % 2) else "dve", ins("tensor_tensor", out=p_[:, :nq], in0=p_[:, :nq], in1=masks[:, mi, :nq], op=ALU.mult),
                             reads=[p_b, cbuf], writes=[p_b])
                    if not win:
                        eng = "pool" if (i % 8) in (1, 4, 6) else "dve"
                        ac, acb = d_["acs"][eng]
                        if d_["acn"][eng] == 0:
                            s.op(eng, ins("tensor_copy", out=ac[:, :nq], in_=p_[:, :nq]), reads=[p_b], writes=[acb])
                        else:
                            s.op(eng, ins("tensor_tensor", out=ac[:, :nq], in0=ac[:, :nq], in1=p_[:, :nq], op=ALU.add), reads=[p_b, acb], writes=[acb])
                        d_["acn"][eng] += 1
                    d_[("p", i)] = (p_, p_b)

                def stage2(tk_):
                    nonlocal kr
                    qi, e_, i, kb, mi, lastb = tk_
                    qs = slice(qi * nq, (qi + 1) * nq)
                    pr = slice(e_ * 64, (e_ + 1) * 64)
                    d_ = stt[(qi, e_)]
                    pso, pob = d_["pso"], d_["pob"]
                    p_, p_b = d_.pop(("p", i))
                    s.mm(pso[:, :nq], va[:, kb, :], p_[:, :nq], i == 0, lastb, reads=[vab, p_b], writes=[pob], grp=("v", ku, cur_it[0] // 2))
                    if not lastb:
                        return
                    if win:
                        hh = u * 2 + e_
                        r_, r_b = rec[kr % 4]
                        kr += 1
                        s.op("dve", ins("tensor_scalar", out=r_[64:128, :nq], in0=pso[64:128, :nq], scalar1=esink[64:128, hh:hh + 1], scalar2=None, op0=ALU.add),
                             reads=[pob, cbuf], writes=[r_b])
                        s.op("dve", ins("reciprocal", out=r_[64:128, :nq], in_=r_[64:128, :nq]), reads=[r_b], writes=[r_b])
                        s.op("dve", ins("tensor_tensor", out=ot[pr, qs], in0=pso[0:64, :nq], in1=r_[64:128, :nq], op=ALU.mult),
                             reads=[pob, r_b], writes=[otb])
                        del stt[(qi, e_)]
                        return
                    acl = [d_["acs"][e2] for e2 in ("dve", "pool") if d_["acn"][e2] > 0]
                    o_, o_b = osb[kr % 8]
                    r_, r_b = rec[kr % 8]
                    kr += 1
                    s.op("dve", ins("tensor_copy", out=o_[:, :nq], in_=pso[:, :nq]), reads=[pob], writes=[o_b])
                    d_["o"] = (o_, o_b)
                    key = (qi, e_)

                    def st1():
                        psd, pdb = s.ps()
                        for ai, (ac, acb) in enumerate(acl):
                            s.mm(psd[:, :nq], ones_f[:], ac[:, :nq], ai == 0, ai == len(acl) - 1, reads=[acb, cb_const], writes=[pdb])
                        d_["psd"] = (psd, pdb)
                        defer(2, st2)

                    def st2():
                        psd, pdb = d_["psd"]
                        s.op("dve", ins("reciprocal", out=r_[:, :nq], in_=psd[:, :nq]), reads=[pdb], writes=[r_b])
                        defer(4, st3)

                    def st3():
                        s.op("dve", ins("tensor_tensor", out=o_[:, :nq], in0=o_[:, :nq], in1=r_[:, :nq], op=ALU.mult), reads=[o_b, r_b], writes=[o_b])
                        d_["done"] = True
                        if e_ == 1 or stt[(qi, 1)].get("done") if (qi, 1) in stt else False:
                            pass
                        if (qi, 0) in stt and (qi, 1) in stt and stt[(qi, 0)].get("done") and stt[(qi, 1)].get("done"):
                            defer(1, st4)

                    def st4():
                        (o0, o0b) = stt[(qi, 0)]["o"]
                        (o1, o1b) = stt[(qi, 1)]["o"]
                        s.op("dve", ins("scalar_tensor_tensor", out=o0[:, :nq], in0=o1[:, :nq], scalar=lam[:, 2:3], in1=o0[:, :nq], op0=ALU.mult, op1=ALU.add),
                             reads=[o0b, o1b, cbuf], writes=[o0b])
                        sq_, sq_b = sqd[qi % 4]
                        s.op("act", ins("activation", out=sq_[:, :nq], in_=o0[:, :nq], func=AF.Square), reads=[o0b], writes=[sq_b])
                        defer(2, st5)

                    def st5():
                        sq_, sq_b = sqd[qi % 4]
                        psn, pnb = s.ps()
                        s.mm(psn[:, :nq], ones_bf[:], sq_[:, :nq], True, True, reads=[sq_b, cb_const], writes=[pnb])
                        stt[(qi, 1)]["psn"] = (psn, pnb)
                        defer(2, st6)

                    def st6():
                        (o1, o1b) = stt[(qi, 1)]["o"]
                        psn, pnb = stt[(qi, 1)]["psn"]
                        s.op("act", ins("activation", out=o1[:, :nq], in_=psn[:, :nq], func=AF.Sqrt, scale=1.0 / 128, bias=EPSB[:, 0:1]), reads=[pnb, o1b], writes=[o1b])
                        defer(2, st7)

                    def st7():
                        (o1, o1b) = stt[(qi, 1)]["o"]
                        s.op("dve", ins("reciprocal", out=o1[:, :nq], in_=o1[:, :nq]), reads=[o1b], writes=[o1b])
                        defer(4, st8)

                    def st8():
                        (o0, o0b) = stt[(qi, 0)]["o"]
                        (o1, o1b) = stt[(qi, 1)]["o"]
                        s.op("dve", ins("scalar_tensor_tensor", out=ot[:, qs], in0=o0[:, :nq], scalar=lam[:, 3:4], in1=o1[:, :nq], op0=ALU.mult, op1=ALU.mult),
                             reads=[o0b, o1b, cbuf], writes=[otb])
                        del stt[(qi, 0)]
                        del stt[(qi, 1)]

                    defer(2, st1)

                pend = []
                cur_it = [0]

                def defer(dl, fn):
                    pend.append((cur_it[0] + dl, fn))

                def run_pending(flush=False):
                    while True:
                        ready = [p for p in pend if flush or p[0] <= cur_it[0]]
                        if not ready:
                            break
                        for p in ready:
                            pend.remove(p)
                        for due, fn in ready:
                            fn()
                        if not flush:
                            break

                for t2_ in range(0, len(tasks) + LOOK + 1, 2):
                    for t_ in (t2_, t2_ + 1):
                        cur_it[0] = t_
                        if t_ < len(tasks):
                            stage1(tasks[t_])
                    for t_ in (t2_, t2_ + 1):
                        cur_it[0] = t_
                        if LOOK <= t_ < len(tasks) + LOOK:
                            stage2(tasks[t_ - LOOK])
                    run_pending()
                while pend:
                    cur_it[0] += 1
                    run_pending(flush=True)
                s.dma(OS[:, u, q0:q0 + L], ot[:, 0:L], reads=[otb], writes=[OS_b[si][u]])
        return OS_b

    def outproj_phase(l, Wsrc, nk, Osrc, O_bufs_fn, Xsrc, Xsrc_b):
        s.barrier()
        s.release()
        Wo = s.alloc([128, nk, 1024], BF16, "wo")
        wob = load_w(Wo, Wsrc, nk)
        rctx = ResCtx()
        oin = [(s.alloc([128, nk, 512], BF16, "oin"), Buf()) for _ in range(2)]
        for ti in range(NT):
            t0, n, cond, _, _ = TILES[ti]
            o_, o_b = oin[ti % 2]
            s.dma(o_[:, :, :n], Osrc[:, :, t0:t0 + n], writes=[o_b])
            outproj_resid(rctx, l, 2, ti, Wo, wob, nk, o_, [o_b], Xsrc, Xsrc_b, X, X_b)

    def gla_layer(l, Xsrc, Xsrc_b):
        s.barrier()
        s.release()
        Wqk = s.alloc([128, 8, 1024], BF16, "gwqk")
        wqb = load_w(Wqk, w1_qk, 8)
        Wv = s.alloc([128, 8, 1024], BF16, "gwv")
        wvb = load_w(Wv, w1_v, 8)
        Wr = s.alloc([128, 8, 1024], BF16, "gwr")
        wrb = load_w(Wr, w1_r, 8)
        Wg1 = s.alloc([128, 8, 32], BF16, "gwg1")
        wg1b = load_w(Wg1, w1_g1, 8)
        Wg2 = s.alloc([32, 1024], BF16, "gwg2")
        cbuf = Buf()
        s.dma(Wg2[:], w1_g2, writes=[cbuf], eng="pool")
        gb = s.alloc([128, 8], F32, "ggb")
        s.dma(gb[:], w1_gb, writes=[cbuf])
        s.op("dve", ins("tensor_scalar", out=gb[:], in0=gb[:], scalar1=-1.0, scalar2=None, op0=ALU.mult), reads=[cbuf], writes=[cbuf])
        ELt = s.alloc([128, 2, 4, 72], F32, "el")
        nctx = NormCtx(1)
        qkf = [(s.alloc([128, 8, 512], F32, "qkf"), Buf()) for _ in range(1)]
        lg = nctx.xt[0]
        cs_ = (s.alloc([128, 8, 512], F32, "cs"), Buf())
        c2 = nctx.tmp
        ex = [(s.alloc([128, 512], F32, "ex"), Buf()) for _ in range(3)]
        t1b_ = (s.alloc([32, 512], BF16, "t1b"), Buf())
        qd_t = [(s.alloc([128, 2, 4, 512], BF16, "qd"), Buf()) for _ in range(1)]
        kd_t = [(s.alloc([128, 2, 4, 512], BF16, "kd"), Buf()) for _ in range(1)]
        krT = [(s.alloc([128, 512], BF16, "krT"), Buf()) for _ in range(2)]
        krt = [(s.alloc([128, 4, 8, 128], BF16, "krt"), Buf()) for _ in range(1)]
        rt = [(s.alloc([128, 8, 512], BF16, "rt"), Buf()) for _ in range(1)]
        vt = [(s.alloc([128, 4, 1024], BF16, "vt"), Buf()) for _ in range(1)]
        G_b = [Buf() for _ in range(NT)]
        kx = 0
        kk = 0
        for ti in range(NT):
            t0, n, cond, s0, s1 = TILES[ti]
            nch = n // 64
            h, hb = norm_tile(nctx, l, 0, ti, Xsrc, Xsrc_b)
            qf, qfb = qkf[0]
            for cc in range(8):
                ps, pb = s.ps()
                for kc in range(8):
                    s.mm(ps[:, :n], Wqk[:, kc, cc * 128:(cc + 1) * 128], h[:, kc, :n], kc == 0, kc == 7, reads=[wqb[kc], hb], writes=[pb])
                sc_ = (128.0 ** -0.5) if cc < 4 else 1.0
                s.op("act", ins("activation", out=qf[:, cc, :n], in_=ps[:, :n], func=AF.Identity, scale=sc_), reads=[pb], writes=[qfb])
            r_, r_b = rt[0]
            for cc in range(8):
                ps, pb = s.ps()
                for kc in range(8):
                    s.mm(ps[:, :n], Wr[:, kc, cc * 128:(cc + 1) * 128], h[:, kc, :n], kc == 0, kc == 7, reads=[wrb[kc], hb], writes=[pb])
                s.op("act", ins("activation", out=r_[:, cc, :n], in_=ps[:, :n], func=AF.Silu), reads=[pb], writes=[r_b])
            s.dma(RS[:, :, t0:t0 + n], r_[:, :, :n], reads=[r_b], writes=[G_b[ti]])
            v_, v_b = vt[0]
            for tbk in range(n // 128):
                for vg in range(2):
                    ps, pb = s.ps()
                    for kc in range(8):
                        s.mm(ps[:, :512], h[:, kc, tbk * 128:(tbk + 1) * 128], Wv[:, kc, vg * 512:(vg + 1) * 512], kc == 0, kc == 7, reads=[wvb[kc], hb], writes=[pb])
                    s.op("act", ins("activation", out=v_[:, tbk, vg * 512:(vg + 1) * 512], in_=ps[:, :512], func=AF.Identity), reads=[pb], writes=[v_b])
            s.dma(rows_view(VS, 1024, t0, n // 128, 0, 1024), v_[:, 0:n // 128, :], reads=[v_b], writes=[G_b[ti]])
            ps, pb = s.ps()
            for kc in range(8):
                s.mm(ps[0:32, :n], Wg1[:, kc, :], h[:, kc, :n], kc == 0, kc == 7, reads=[wg1b[kc], hb], writes=[pb])
            t1_, t1bb = t1b_
            s.op("act", ins("activation", out=t1_[:, :n], in_=ps[0:32, :n], func=AF.Identity), reads=[pb], writes=[t1bb])
            lgt, lgb = lg
            for j in range(8):
                ps, pb = s.ps()
                s.mm(ps[:, :n], Wg2[:, j * 128:(j + 1) * 128], t1_[:, :n], True, True, reads=[cbuf, t1bb], writes=[pb])
                s.op("act", ins("activation", out=lgt[:, j, :n], in_=ps[:, :n], func=AF.Exp, scale=-1.0, bias=gb[:, j:j + 1]), reads=[pb, cbuf], writes=[lgb])
            s.op("act", ins("activation", out=lgt[:, :, :n], in_=lgt[:, :, :n], func=AF.Ln, bias=1.0, scale=1.0), reads=[lgb], writes=[lgb])
            cst, csb = cs_
            for j in range(8):
                s.op("dve", ins("tensor_tensor_scan", out=cst[:, j, :n], data0=m64[:, :n], data1=lgt[:, j, :n], initial=0.0, op0=ALU.mult, op1=ALU.add),
                     reads=[lgb, cb_const], writes=[csb])
            c2t, c2b = c2
            csv = cst[:, :, :n].rearrange("p j (c t) -> p j c t", t=64)
            lgv = lgt[:, :, :n].rearrange("p j (c t) -> p j c t", t=64)
            c2v = c2t[:, :, :n].rearrange("p j (c t) -> p j c t", t=64)
            nf = 4
            s.op("dve", ins("tensor_tensor", out=c2v[:, 0:nf, :, :], in0=csv[:, 0:nf, :, :], in1=csv[:, 0:nf, :, 63:64].to_broadcast([128, nf, nch, 64]), op=ALU.subtract),
                 reads=[csb], writes=[c2b])
            s.op("dve", ins("tensor_tensor", out=c2v[:, nf:2 * nf, :, :], in0=lgv[:, nf:2 * nf, :, :], in1=csv[:, nf:2 * nf, :, :], op=ALU.subtract),
                 reads=[csb, lgb], writes=[c2b])
            ci0 = t0 // 64
            s.op("act", ins("activation", out=ELt[:, :, :, ci0:ci0 + nch].rearrange("p d h c -> p (d h) c"), in_=cst[:, :, :n].rearrange("p j (c t) -> p j c t", t=64)[:, :, :, 63],
                                               func=AF.Exp, scale=-1.0 / 16), reads=[csb], writes=[cbuf])
            s.op("dve", ins("tensor_tensor", out=lgv[:, nf:2 * nf, :, :], in0=c2v[:, nf:2 * nf, :, :], in1=csv[:, nf:2 * nf, :, 63:64].to_broadcast([128, nf, nch, 64]), op=ALU.add),
                 reads=[csb, c2b, lgb], writes=[lgb])
            qd_, qdb = qd_t[0]
            kd_, kdb = kd_t[0]
            krt_, krtb = krt[0]
            for j in range(8):
                d = j // 4
                hh = j % 4
                ea, eab = ex[0]
                eb, ebb = ex[1]
                ec, ecb = ex[2]
                csrc = cst if j < 4 else lgt
                s.op("act", ins("activation", out=ea[:, :n], in_=csrc[:, j, :n], func=AF.Exp, scale=-1.0 / 16), reads=[csb, lgb], writes=[eab])
                s.op("act", ins("activation", out=eb[:, :n], in_=csrc[:, j, :n], func=AF.Exp, scale=1.0 / 16), reads=[csb, lgb], writes=[ebb])
                s.op("act", ins("activation", out=ec[:, :n], in_=c2t[:, j, :n], func=AF.Exp, scale=1.0 / 16), reads=[c2b], writes=[ecb])
                s.op("dve", ins("tensor_tensor", out=qd_[:, d, hh, :n], in0=qf[:, hh, :n], in1=ea[:, :n], op=ALU.mult), reads=[qfb, eab], writes=[qdb])
                s.op("dve", ins("tensor_tensor", out=kd_[:, d, hh, :n], in0=qf[:, 4 + hh, :n], in1=eb[:, :n], op=ALU.mult), reads=[qfb, ebb], writes=[kdb])
                kT, kTb = krT[kx % 2]
                kx += 1
                s.op("pool", ins("tensor_tensor", out=kT[:, :n], in0=qf[:, 4 + hh, :n], in1=ec[:, :n], op=ALU.mult), reads=[qfb, ecb], writes=[kTb])
                for tbk in range(n // 128):
                    ps, pb = s.ps()
                    psv = ps[:].bitcast(BF16)
                    s.op("pe", ins("transpose", psv[:, 0:128], kT[:, tbk * 128:(tbk + 1) * 128], ident_bf[:]), reads=[kTb, cb_const], writes=[pb])
                    s.op("act", ins("activation", out=krt_[:, tbk, j, :], in_=psv[:, 0:128], func=AF.Identity), reads=[pb], writes=[krtb])
            for d in range(2):
                s.dma(QD[d, :, :, t0:t0 + n], qd_[:, d, :, :n], reads=[qdb], writes=[G_b[ti]])
                s.dma(KD[d, :, :, t0:t0 + n], kd_[:, d, :, :n], reads=[kdb], writes=[G_b[ti]])
            s.dma(kr_view(t0, n // 128, 0, 8), krt_[:, 0:n // 128, :, :], reads=[krtb], writes=[G_b[ti]])
        ELd = dscr("ELd", [128, 2, 4, 72], F32)
        elb = Buf()
        s.dma(ELd, ELt[:], reads=[cbuf], writes=[elb])
        s.barrier()
        s.release()
        EL = s.alloc([128, 2, 4, 72], F32, "el2")
        elb2 = Buf()
        s.dma(EL[:], ELd, writes=[elb2])
        S = [(s.alloc([128, 4, 256], F32, "S"), [Buf() for _ in range(4)]) for _ in range(2)]
        Sb = [(s.alloc([128, 4, 256], BF16, "Sb"), [Buf() for _ in range(4)]) for _ in range(2)]
        qd_t = [[(s.alloc([128, 4, 512], BF16, "qd"), Buf()) for _ in range(2)] for _ in range(2)]
        kd_t = [[(s.alloc([128, 4, 512], BF16, "kd"), Buf()) for _ in range(2)] for _ in range(2)]
        kr_t = [[(s.alloc([128, 4, 4, 128], BF16, "kr"), Buf()) for _ in range(2)] for _ in range(2)]
        v_t = [[(s.alloc([128, 4, 1024], BF16, "v"), Buf()) for _ in range(2)] for _ in range(2)]
        of_t = [[(s.alloc([128, 8, 512], BF16, "of"), Buf()) for _ in range(2)] for _ in range(2)]
        attm = [(s.alloc([128, 64], BF16, "attm"), Buf()) for _ in range(8)]
        OD_b = [[Buf() for _ in range(NT)] for _ in range(2)]
        ODs = [OS, OS2]
        kat = 0
        SEQT = [list(range(8)), [8], [9]]
        states_in = [s1f, s1b]
        states_out = [o_gf, o_gb]
        for si, tiles in enumerate(SEQT):
            for d in range(2):
                St, Stb = S[d]
                Sbt, Sbb = Sb[d]
                if si == 0:
                    s.dma(St[:], states_in[d], writes=Stb)
                else:
                    s.op("pool", ins("memset", St[:], 0.0), writes=Stb)
                for hh in range(4):
                    s.op("act", ins("activation", out=Sbt[:, hh, :], in_=St[:, hh, :], func=AF.Identity), reads=[Stb[hh]], writes=[Sbb[hh]])
            nt_ = len(tiles)
            for step in range(nt_):
                cur = {}
                for d in range(2):
                    ti = tiles[step] if d == 0 else tiles[nt_ - 1 - step]
                    t0, n, cond, s0, s1 = TILES[ti]
                    qd_, qdb = qd_t[d][step % 2]
                    kd_, kdb = kd_t[d][step % 2]
                    kr_, krb = kr_t[d][step % 2]
                    v_, vb_ = v_t[d][step % 2]
                    of_, ofb = of_t[d][step % 2]
                    s.dma(qd_[:, :, :n], QD[d, :, :, t0:t0 + n], writes=[qdb])
                    s.dma(kd_[:, :, :n], KD[d, :, :, t0:t0 + n], writes=[kdb])
                    s.dma(kr_[:, 0:n // 128, :, :], kr_view(t0, n // 128, d * 4, 4), writes=[krb])
                    s.dma(v_[:, 0:n // 128, :], rows_view(VS, 1024, t0, n // 128, 0, 1024), writes=[vb_])
                    cur[d] = (ti, t0, n, qd_, qdb, kd_, kdb, kr_, krb, v_, vb_, of_, ofb)
                nch = cur[0][2] // 64
                for kc_ in range(nch):
                    units = []
                    for d in range(2):
                        ti, t0, n, qd_, qdb, kd_, kdb, kr_, krb, v_, vb_, of_, ofb = cur[d]
                        k = kc_ if d == 0 else nch - 1 - kc_
                        for hh in range(4):
                            units.append(dict(d=d, hh=hh, k=k, ci=t0 // 64 + k, tbk=k // 2, hp=slice((k % 2) * 64, (k % 2) * 64 + 64),
                                              cs64=slice(k * 64, (k + 1) * 64), qd_=qd_, qdb=qdb, kd_=kd_, kdb=kdb, kr_=kr_, krb=krb, v_=v_, vb_=vb_, of_=of_, ofb=ofb))
                    psA, pbA = s.ps()
                    for ui, u_ in enumerate(units):
                        s.mm(psA[0:64, ui * 64:(ui + 1) * 64], u_["kd_"][:, u_["hh"], u_["cs64"]], u_["qd_"][:, u_["hh"], u_["cs64"]], True, True,
                             reads=[u_["kdb"], u_["qdb"]], writes=[pbA], grp=("ga", kat))
                    for ui, u_ in enumerate(units):
                        am, amb = attm[ui]
                        u_["am"] = (am, amb)
                        s.op("dve", ins("tensor_tensor", out=am[u_["hp"], :], in0=psA[0:64, ui * 64:(ui + 1) * 64], in1=tri_bf[0:64, 2 + u_["d"], 0:64], op=ALU.mult),
                             reads=[pbA, cb_const], writes=[amb])
                    psO = [s.ps("b") for _ in range(2)]
                    for ui, u_ in enumerate(units):
                        pso, pob = psO[ui // 4]
                        am, amb = u_["am"]
                        St, Stb = S[u_["d"]]
                        Sbt, Sbb = Sb[u_["d"]]
                        hh, hp, tbk = u_["hh"], u_["hp"], u_["tbk"]
                        c0 = (ui % 4) * 128
                        for vc in range(2):
                            s.mm(pso[:, c0 + vc * 64:c0 + (vc + 1) * 64], u_["v_"][hp, tbk, hh * 256 + vc * 128:hh * 256 + (vc + 1) * 128], am[hp, :], True, False,
                                 reads=[u_["vb_"], amb], writes=[pob])
                            s.mm(pso[:, c0 + vc * 64:c0 + (vc + 1) * 64], Sbt[:, hh, vc * 128:(vc + 1) * 128], u_["qd_"][:, hh, u_["cs64"]], False, True,
                                 reads=[Sbb[hh], u_["qdb"]], writes=[pob])
                    for ui, u_ in enumerate(units):
                        pso, pob = psO[ui // 4]
                        c0 = (ui % 4) * 128
                        hh = u_["hh"]
                        s.op("act", ins("activation", out=u_["of_"][:, hh * 2:hh * 2 + 2, u_["cs64"]], in_=pso[:, c0:c0 + 128].rearrange("p (v t) -> p v t", t=64), func=AF.Identity),
                             reads=[pob], writes=[u_["ofb"]])
                    psS = [s.ps() for _ in range(4)]
                    for ui, u_ in enumerate(units):
                        ps2, pb2 = psS[ui // 2]
                        c0 = (ui % 2) * 256
                        hh, hp, tbk = u_["hh"], u_["hp"], u_["tbk"]
                        s.mm(ps2[:, c0:c0 + 256], u_["kr_"][hp, tbk, hh, :], u_["v_"][hp, tbk, hh * 256:(hh + 1) * 256], True, True, reads=[u_["krb"], u_["vb_"]], writes=[pb2])
                    for ui, u_ in enumerate(units):
                        ps2, pb2 = psS[ui // 2]
                        c0 = (ui % 2) * 256
                        hh, d = u_["hh"], u_["d"]
                        St, Stb = S[d]
                        Sbt, Sbb = Sb[d]
                        ci = u_["ci"]
                        s.op("dve", ins("scalar_tensor_tensor", out=St[:, hh, :], in0=St[:, hh, :], scalar=EL[:, d, hh, ci:ci + 1], in1=ps2[:, c0:c0 + 256], op0=ALU.mult, op1=ALU.add),
                             reads=[pb2, elb2, Stb[hh]], writes=[Stb[hh]])
                        s.op("pool", ins("tensor_copy", out=Sbt[:, hh, :], in_=St[:, hh, :]), reads=[Stb[hh]], writes=[Sbb[hh]])
                    kat += 1
                for d in range(2):
                    ti, t0, n, qd_, qdb, kd_, kdb, kr_, krb, v_, vb_, of_, ofb = cur[d]
                    s.dma(ODs[d][:, :, t0:t0 + n], of_[:, :, :n], reads=[ofb], writes=[OD_b[d][ti]])
            if si > 0:
                for d in range(2):
                    s.dma(states_out[d][si - 1], S[d][0][:], reads=S[d][1])
        s.barrier()
        s.release()
        Wo = s.alloc([128, 8, 1024], BF16, "wo")
        wob = load_w(Wo, w1_o, 8)
        gn = s.alloc([128, 2], F32, "gn")
        gnb = Buf()
        s.dma(gn[:], w1_gn, writes=[gnb])
        rctx = ResCtx()
        oa = [(s.alloc([128, 8, 512], BF16, "oa"), Buf()) for _ in range(2)]
        ob_ = [(s.alloc([128, 8, 512], BF16, "ob"), Buf()) for _ in range(2)]
        rr = [(s.alloc([128, 8, 512], BF16, "rr"), Buf()) for _ in range(2)]
        osum = (s.alloc([128, 8, 512], F32, "osum"), Buf())
        sq = (s.alloc([128, 8, 512], BF16, "sq"), Buf())
        rs_ = [(s.alloc([128, 512], F32, "rs"), Buf()) for _ in range(2)]
        ofin = [(s.alloc([128, 8, 512], BF16, "ofin"), Buf()) for _ in range(2)]
        for ti in range(NT):
            t0, n, cond, _, _ = TILES[ti]
            a_, a_b = oa[ti % 2]
            b_, b_b = ob_[ti % 2]
            r_, r_b = rr[ti % 2]
            s.dma(a_[:, :, :n], OS[:, :, t0:t0 + n], writes=[a_b])
            s.dma(b_[:, :, :n], OS2[:, :, t0:t0 + n], writes=[b_b])
            s.dma(r_[:, :, :n], RS[:, :, t0:t0 + n], writes=[r_b])
            os_, osb_ = osum
            sq_, sqb_ = sq
            s.op("dve", ins("tensor_tensor", out=os_[:, :, :n], in0=a_[:, :, :n], in1=b_[:, :, :n], op=ALU.add), reads=[a_b, b_b], writes=[osb_])
            s.op("act", ins("activation", out=sq_[:, :, :n], in_=os_[:, :, :n], func=AF.Square), reads=[osb_], writes=[sqb_])
            f_, f_b = ofin[ti % 2]
            for hh in range(4):
                ps, pb = s.ps()
                for vc in range(2):
                    s.mm(ps[:, :n], ones_bf[:], sq_[:, hh * 2 + vc, :n], vc == 0, vc == 1, reads=[sqb_, cb_const], writes=[pb])
                rt_, rtb = rs_[hh % 2]
                s.op("act", ins("activation", out=rt_[:, :n], in_=ps[:, :n], func=AF.Sqrt, scale=1.0 / 256, bias=EPSB[:, 0:1]), reads=[pb], writes=[rtb])
                s.op("dve", ins("reciprocal", out=rt_[:, :n], in_=rt_[:, :n]), reads=[rtb], writes=[rtb])
                for vc in range(2):
                    c = hh * 2 + vc
                    s.op("dve", ins("tensor_tensor", out=os_[:, c, :n], in0=os_[:, c, :n], in1=rt_[:, :n], op=ALU.mult), reads=[osb_, rtb], writes=[osb_])
                    s.op("dve", ins("scalar_tensor_tensor", out=f_[:, c, :n], in0=os_[:, c, :n], scalar=gn[:, vc:vc + 1], in1=r_[:, c, :n], op0=ALU.mult, op1=ALU.mult),
                         reads=[osb_, gnb, r_b], writes=[f_b])
            outproj_resid(rctx, l, 2, ti, Wo, wob, 8, f_, [f_b], Xsrc, Xsrc_b, X, X_b)

    def ssd_layer(l, Xsrc, Xsrc_b):
        XBu = U
        s.barrier()
        s.release()
        Wz = s.alloc([128, 8, 2048], BF16, "wz")
        wzb = load_w(Wz, w3_z, 8)
        Wx = s.alloc([128, 8, 3072], BF16, "wx")
        wxb = load_w(Wx, w3_xbc, 8)
        Wdt = s.alloc([128, 8, 64], BF16, "wdt")
        wdb = load_w(Wdt, w3_dt, 8)
        vec = s.alloc([128, 5, 32], F32, "vec")
        cbuf = Buf()
        s.dma(vec[:], ssd_vec.rearrange("a b -> (a b)").partition_broadcast(128).rearrange("p (a b) -> p a b", a=5), writes=[cbuf])
        avec = s.alloc([128, 64], F32, "avec")
        s.op("act", ins("activation", out=avec[:], in_=vec[:, 0:2, :].rearrange("p a b -> p (a b)"), func=AF.Exp), reads=[cbuf], writes=[cbuf])
        s.op("dve", ins("tensor_scalar", out=avec[:], in0=avec[:], scalar1=-1.0, scalar2=None, op0=ALU.mult), reads=[cbuf], writes=[cbuf])
        nctx = NormCtx(1)
        xbt = (s.alloc([128, 24, 512], BF16, "xbt"), [Buf() for _ in range(24)])
        zt = (s.alloc([128, 4, 2048], BF16, "zt"), Buf())
        dtt = (s.alloc([128, 4, 2, 64], F32, "dtt"), Buf())
        dtmp = (s.alloc([128, 64], F32, "dtmp"), Buf())
        S_b = [Buf() for _ in range(NT)]
        ev = 0
        for ti in range(NT):
            t0, n, cond, s0, s1 = TILES[ti]
            nb = n // 128
            h, hb = norm_tile(nctx, l, 0, ti, Xsrc, Xsrc_b)
            xb_, xbb = xbt
            for cc in range(24):
                ps, pb = s.ps()
                for kc in range(8):
                    s.mm(ps[:, :n], Wx[:, kc, cc * 128:(cc + 1) * 128], h[:, kc, :n], kc == 0, kc == 7, reads=[wxb[kc], hb], writes=[pb])
                if ev % 2 == 0:
                    s.op("act", ins("activation", out=xb_[:, cc, :n], in_=ps[:, :n], func=AF.Identity), reads=[pb], writes=[xbb[cc]])
                else:
                    s.op("dve", ins("tensor_copy", out=xb_[:, cc, :n], in_=ps[:, :n]), reads=[pb], writes=[xbb[cc]])
                ev += 1
            s.dma(XBu[:, 0:24, t0:t0 + n], xb_[:, :, :n], reads=xbb, writes=[S_b[ti]])
            z_, z_b = zt
            d_, d_b = dtt
            for tbk in range(nb):
                for vg in range(4):
                    ps, pb = s.ps()
                    for kc in range(8):
                        s.mm(ps[:, :512], h[:, kc, tbk * 128:(tbk + 1) * 128], Wz[:, kc, vg * 512:(vg + 1) * 512], kc == 0, kc == 7, reads=[wzb[kc], hb], writes=[pb])
                    s.op("act", ins("activation", out=z_[:, tbk, vg * 512:(vg + 1) * 512], in_=ps[:, :512], func=AF.Silu), reads=[pb], writes=[z_b])
                ps, pb = s.ps()
                for kc in range(8):
                    s.mm(ps[:, 0:64], h[:, kc, tbk * 128:(tbk + 1) * 128], Wdt[:, kc, :], kc == 0, kc == 7, reads=[wdb[kc], hb], writes=[pb])
                tm, tmb = dtmp
                s.op("dve", ins("tensor_tensor", out=tm[:], in0=ps[:, 0:64], in1=vec[:, 2:4, :].rearrange("p a b -> p (a b)"), op=ALU.add), reads=[pb, cbuf], writes=[tmb])
                s.op("act", ins("activation", out=tm[:], in_=tm[:], func=AF.Exp), reads=[tmb], writes=[tmb])
                s.op("act", ins("activation", out=d_[:, tbk, 0, :], in_=tm[:], func=AF.Ln, bias=1.0, scale=1.0), reads=[tmb], writes=[d_b])
                s.op("dve", ins("tensor_tensor", out=d_[:, tbk, 1, :], in0=d_[:, tbk, 0, :], in1=avec[:], op=ALU.mult), reads=[d_b, cbuf], writes=[d_b])
            s.dma(rows_view(ZS, 2048, t0, nb, 0, 2048), z_[:, 0:nb, :], reads=[z_b], writes=[S_b[ti]])
            s.dma(AP(DTS.tensor, t0 * 128, [[128, 128], [128 * 128, nb], [1, 128]]), d_[:, 0:nb, :, :].rearrange("p b a c -> p b (a c)"), reads=[d_b], writes=[S_b[ti]])
        chk(20)
        s.barrier()
        s.release()
        cw = s.alloc([128, 3, 24], F32, "scw")
        cbias = s.alloc([128, 24], F32, "scb")
        cwb = Buf()
        s.dma(cw[:], ssd_cw, writes=[cwb])
        s.dma(cbias[:], ssd_cb, writes=[cwb])
        ug = s.alloc([128, 24, 514], BF16, "sug")
        ugb = [Buf() for _ in range(2)]
        xc = (s.alloc([128, 24, 512], BF16, "xc"), [Buf() for _ in range(24)])
        tmps = [(s.alloc([128, 512], F32, "st"), Buf()) for _ in range(2)]
        xtt = (s.alloc([128, 4, 2048], BF16, "xtt"), Buf())
        btt = (s.alloc([128, 4, 512], BF16, "btt"), Buf())
        kp = 0
        for ti in range(NT):
            t0, n, cond, s0, s1 = TILES[ti]
            nb = n // 128
            lo = (t0 - 1) >= s0
            hi = (t0 + n) < s1
            for grp in range(2):
                gs_ = slice(grp * 12, (grp + 1) * 12)
                if not lo:
                    s.op("pool", ins("memset", ug[:, gs_, 0:1], 0.0), writes=[ugb[grp]])
                if not hi:
                    s.op("pool", ins("memset", ug[:, gs_, n + 1:n + 2], 0.0), writes=[ugb[grp]])
                c0 = 0 if lo else 1
                c1 = n + 2 if hi else n + 1
                s.dma(ug[:, gs_, c0:c1], XBu[:, gs_, t0 - 1 + c0:t0 - 1 + c1], writes=[ugb[grp]])
            xc_, xcb = xc
            for col in range(24):
                grp = col // 12
                tt, ttb = tmps[kp % 2]
                kp += 1
                s.op("act", ins("activation", out=tt[:, :n], in_=ug[:, col, 1:n + 1], func=AF.Identity, scale=cw[:, 1, col:col + 1], bias=cbias[:, col:col + 1]),
                     reads=[ugb[grp], cwb], writes=[ttb])
                s.op("dve", ins("scalar_tensor_tensor", out=tt[:, :n], in0=ug[:, col, 0:n], scalar=cw[:, 0, col:col + 1], in1=tt[:, :n], op0=ALU.mult, op1=ALU.add),
                     reads=[ugb[grp], cwb, ttb], writes=[ttb])
                s.op("dve", ins("scalar_tensor_tensor", out=tt[:, :n], in0=ug[:, col, 2:n + 2], scalar=cw[:, 2, col:col + 1], in1=tt[:, :n], op0=ALU.mult, op1=ALU.add),
                     reads=[ugb[grp], cwb, ttb], writes=[ttb])
                s.op("act", ins("activation", out=xc_[:, col, :n], in_=tt[:, :n], func=AF.Silu), reads=[ttb], writes=[xcb[col]])
            s.dma(QK[:, 0:8, t0:t0 + n], xc_[:, 16:24, :n], reads=xcb[16:24], writes=[S_b[ti]])
            x_, x_b = xtt
            b_, b_b = btt
            tv = 0
            for tbk in range(nb):
                for col in range(20):
                    ps, pb = s.ps()
                    psv = ps[:].bitcast(BF16)
                    s.op("pe", ins("transpose", psv[:, 0:128], xc_[:, col, tbk * 128:(tbk + 1) * 128], ident_bf[:]), reads=[xcb[col], cb_const], writes=[pb])
                    dst = x_[:, tbk, col * 128:(col + 1) * 128] if col < 16 else b_[:, tbk, (col - 16) * 128:(col - 15) * 128]
                    dbuf = x_b if col < 16 else b_b
                    if tv % 2 == 0:
                        s.op("act", ins("activation", out=dst, in_=psv[:, 0:128], func=AF.Identity), reads=[pb], writes=[dbuf])
                    else:
                        s.op("dve", ins("tensor_copy", out=dst, in_=psv[:, 0:128]), reads=[pb], writes=[dbuf])
                    tv += 1
            s.dma(rows_view(XTs, 2048, t0, nb, 0, 2048), x_[:, 0:nb, :], reads=[x_b], writes=[S_b[ti]])
            s.dma(rows_view(BTs, 512, t0, nb, 0, 512), b_[:, 0:nb, :], reads=[b_b], writes=[S_b[ti]])
        chk(21)
        s.barrier()
        s.release()
        vec = s.alloc([128, 5, 32], F32, "vec")
        negm = s.alloc([128, 2, 128], F32, "negm")
        cbuf = Buf()
        s.op("dve", ins("tensor_scalar", out=negm[:], in0=stage[:, 0:2, :], scalar1=-1.0, scalar2=30000.0, op0=ALU.add, op1=ALU.mult), reads=[bstage], writes=[cbuf])
        ST = [(s.alloc([128, 2048], F32, "ST"), [Buf() for _ in range(4)]) for _ in range(2)]
        SbT = [(s.alloc([128, 2048], BF16, "SbT"), [Buf() for _ in range(4)]) for _ in range(2)]
        bct = [(s.alloc([128, 8, 512], BF16, "bct"), Buf()) for _ in range(2)]
        xts = [(s.alloc([128, 4, 2048], BF16, "xts"), Buf()) for _ in range(2)]
        bts = [(s.alloc([128, 4, 512], BF16, "bts"), Buf()) for _ in range(2)]
        dts = [(s.alloc([128, 4, 2, 64], F32, "dts"), Buf()) for _ in range(2)]
        yts = [(s.alloc([128, 4, 2048], BF16, "yts"), Buf()) for _ in range(2)]
        cumT = [(s.alloc([128, 32], F32, "cumT"), Buf()) for _ in range(2)]
        cbT = [(s.alloc([128, 128], F32, "cbT"), Buf()) for _ in range(6)]
        lat = [(s.alloc([128, 4, 128], F32, "lat"), Buf()) for _ in range(3)]
        cB = [(s.alloc([128, 4, 128], F32, "cB"), Buf()) for _ in range(4)]
        seg = [(s.alloc([128, 4, 128], F32, "seg"), Buf()) for _ in range(3)]
        Et = [(s.alloc([128, 4, 128], F32, "Et"), Buf()) for _ in range(6)]
        ecB = [(s.alloc([128, 4, 128], F32, "ecB"), Buf()) for _ in range(8)]
        te = [(s.alloc([128, 4], F32, "te"), Buf()) for _ in range(3)]
        xs = [(s.alloc([128, 512], BF16, "xs"), Buf()) for _ in range(4)]
        Wt = [(s.alloc([128, 128], BF16, "Wt"), Buf()) for _ in range(8)]
        CEt = [(s.alloc([128, 128], BF16, "CEt"), Buf()) for _ in range(8)]
        SEQT = [list(range(8)), [8], [9]]
        st_in = [s3f, s3b]
        st_out = [o_sf, o_sb]
        Y_b = [Buf() for _ in range(NT)]
        kw = 0
        kq4 = 0
        kg = 0
        kct = 0
        for si, tiles in enumerate(SEQT):
            for d in range(2):
                St, Stb = ST[d]
                Sbt, Sbb = SbT[d]
                if si == 0:
                    s.dma(St[:], st_in[d], writes=Stb)
                else:
                    s.op("pool", ins("memset", St[:], 0.0), writes=Stb)
                for g in range(4):
                    s.op("act", ins("activation", out=Sbt[:, g * 512:(g + 1) * 512], in_=St[:, g * 512:(g + 1) * 512], func=AF.Identity), reads=[Stb[g]], writes=[Sbb[g]])
            nt_ = len(tiles)
            for step in range(nt_):
                cur = {}
                for d in range(2):
                    ti = tiles[step] if d == 0 else tiles[nt_ - 1 - step]
                    t0, n, cond, s0, s1 = TILES[ti]
                    nb = n // 128
                    bc_, bcb = bct[d]
                    x_, x_b = xts[d]
                    b_, b_b = bts[d]
                    d_, d_b = dts[d]
                    y_, y_b = yts[d]
                    s.dma(bc_[:, :, :n], QK[:, 0:8, t0:t0 + n], writes=[bcb])
                    s.dma(x_[:, 0:nb, :], rows_view(XTs, 2048, t0, nb, 0, 2048), writes=[x_b])
                    s.dma(b_[:, 0:nb, :], rows_view(BTs, 512, t0, nb, 0, 512), writes=[b_b])
                    s.dma(d_[:, 0:nb, :, :].rearrange("p b a c -> p b (a c)"), AP(DTS.tensor, t0 * 128, [[128, 128], [128 * 128, nb], [1, 128]]), writes=[d_b])
                    cur[d] = (ti, t0, n, nb)
                nbs = cur[0][3]
                batches = []
                for kc_ in range(nbs):
                    for d in range(2):
                        for g in range(4):
                            for hb_ in range(2):
                                batches.append((kc_, d, g, hb_))
                bst = {}
                LOOKB = 3

                def geo(bt_):
                    kc_, d, g, hb_ = bt_
                    ti, t0, n, nb = cur[d]
                    tbk = kc_ if d == 0 else nb - 1 - kc_
                    last = 127 if d == 0 else 0
                    d_, d_b = dts[d]
                    la = d_[:, tbk, 1, d * 32:(d + 1) * 32]
                    dtv = d_[:, tbk, 0, d * 32:(d + 1) * 32]
                    return tbk, last, slice(tbk * 128, (tbk + 1) * 128), la, dtv, d_b

                def stageA1(bt_):
                    nonlocal kq4, kg, kct
                    kc_, d, g, hb_ = bt_
                    tbk, last, tk, la, dtv, d_b = geo(bt_)
                    bc_, bcb = bct[d]
                    if g == 0 and hb_ == 0:
                        psc, pcb = s.ps()
                        s.mm(psc[:, 0:32], stage[:, d, :], la, True, True, reads=[bstage, d_b], writes=[pcb])
                        cT, cTb = cumT[kct % 2]
                        kct += 1
                        s.op("dve", ins("tensor_copy", out=cT[:], in_=psc[:, 0:32]), reads=[pcb], writes=[cTb])
                        bst[("cT", kc_, d)] = (cT, cTb)
                    if hb_ == 0:
                        ps, pb = s.ps()
                        s.mm(ps[:, 0:128], bc_[:, g, tk], bc_[:, 4 + g, tk], True, True, reads=[bcb], writes=[pb])
                        cb_, cb_b = cbT[kg % 6]
                        s.op("act", ins("activation", out=cb_[:], in_=ps[:, 0:128], func=AF.Identity), reads=[pb], writes=[cb_b])
                        bst[("grp", kc_, d, g)] = dict(cb=(cb_, cb_b), x4=xs[kg % 4], ecs=[])
                        kg += 1
                    h0 = g * 8 + hb_ * 4
                    lt, ltb = lat[kq4 % 3]
                    cb4, cb4b = cB[kq4 % 4]
                    sg, sgb = seg[kq4 % 3]
                    E_, E_b = Et[kq4 % 6]
                    ec, ecb = ecB[kq4 % 8]
                    te_, te_b = te[kq4 % 3]
                    kq4 += 1
                    bst[("w",) + bt_] = (cb4, cb4b, sg, sgb, E_, E_b, ec, ecb, te_, te_b)
                    s.op("dve", ins("tensor_tensor", out=lt[:], in0=stage[:, d, :].unsqueeze(1).to_broadcast([128, 4, 128]),
                                    in1=la[:, h0:h0 + 4].unsqueeze(2).to_broadcast([128, 4, 128]), op=ALU.mult),
                         reads=[bstage, d_b], writes=[ltb])
                    ps2, pb2 = s.ps()
                    s.mm(ps2[:, 0:512], ones_f[:], lt[:].rearrange("p a b -> p (a b)"), True, True, reads=[ltb, cb_const], writes=[pb2])
                    s.op("act", ins("activation", out=cb4[:].rearrange("p a b -> p (a b)"), in_=ps2[:, 0:512], func=AF.Identity), reads=[pb2], writes=[cb4b])

                def stageA2a(bt_):
                    kc_, d, g, hb_ = bt_
                    cT, cTb = bst[("cT", kc_, d)]
                    G = bst[("grp", kc_, d, g)]
                    cb4, cb4b, sg, sgb, E_, E_b, ec, ecb, te_, te_b = bst[("w",) + bt_]
                    h0 = g * 8 + hb_ * 4
                    G["ecs"].append((ec, ecb))
                    for hh in range(4):
                        hq = h0 + hh
                        s.op("dve", ins("scalar_tensor_tensor", out=sg[:, hh, :], in0=cb4[:, hh, :], scalar=cT[:, hq:hq + 1], in1=negm[:, d, :],
                                        op0=ALU.subtract, op1=ALU.add),
                             reads=[cb4b, cTb, cbuf], writes=[sgb])
                    s.op("act", ins("activation", out=E_[:], in_=sg[:], func=AF.Exp), reads=[sgb], writes=[E_b])
                    s.op("act", ins("activation", out=ec[:], in_=cb4[:], func=AF.Exp), reads=[cb4b], writes=[ecb])

                def stageA2b(bt_):
                    kc_, d, g, hb_ = bt_
                    tbk, last, tk, la, dtv, d_b = geo(bt_)
                    x_, x_b = xts[d]
                    G = bst[("grp", kc_, d, g)]
                    x4, x4b = G["x4"]
                    cb4, cb4b, sg, sgb, E_, E_b, ec, ecb, te_, te_b = bst.pop(("w",) + bt_)
                    h0 = g * 8 + hb_ * 4
                    s.op("pool", ins("tensor_tensor", out=te_[:], in0=E_[:, :, last], in1=dtv[:, h0:h0 + 4], op=ALU.mult),
                         reads=[E_b, d_b], writes=[te_b])
                    s.op("pool", ins("tensor_tensor",
                                    out=x4[:, hb_ * 256:(hb_ + 1) * 256].rearrange("p (a b) -> p a b", b=64),
                                    in0=x_[:, tbk, h0 * 64:(h0 + 4) * 64].rearrange("p (a b) -> p a b", b=64),
                                    in1=te_[:].unsqueeze(2).to_broadcast([128, 4, 64]), op=ALU.mult),
                         reads=[x_b, te_b], writes=[x4b])
                    bst[("bat",) + bt_] = (E_, E_b, ec, ecb)

                def stageB(bt_):
                    nonlocal kw
                    kc_, d, g, hb_ = bt_
                    tbk, last, tk, la, dtv, d_b = geo(bt_)
                    bc_, bcb = bct[d]
                    x_, x_b = xts[d]
                    b_, b_b = bts[d]
                    y_, y_b = yts[d]
                    St, Stb = ST[d]
                    Sbt, Sbb = SbT[d]
                    G = bst[("grp", kc_, d, g)]
                    cb_, cb_b = G["cb"]
                    x4, x4b = G["x4"]
                    if hb_ == 0:
                        G["psy"] = s.ps("b")
                    psy, pyb = G["psy"]
                    E_, E_b, ec, ecb = bst.pop(("bat",) + bt_)
                    h0 = g * 8 + hb_ * 4
                    for hh in range(4):
                        hq = h0 + hh
                        W_, W_b = Wt[kw % 8]
                        C_, C_b = CEt[kw % 8]
                        kw += 1
                        s.op("dve", ins("scalar_tensor_tensor", out=W_[:], in0=E_[:, hh, :], scalar=dtv[:, hq:hq + 1], in1=cb_[:], op0=ALU.mult, op1=ALU.mult),
                             reads=[E_b, d_b, cb_b], writes=[W_b])
                        s.op("pool", ins("tensor_tensor", out=C_[:], in0=bc_[:, 4 + g, tk], in1=ec[:, hh, :], op=ALU.mult),
                             reads=[bcb, ecb], writes=[C_b])
                        ycol = slice((hq % 8) * 64, (hq % 8 + 1) * 64)
                        s.mm(psy[:, ycol], W_[:], x_[:, tbk, hq * 64:(hq + 1) * 64], True, False, reads=[W_b, x_b], writes=[pyb])
                        s.mm(psy[:, ycol], C_[:], Sbt[:, hq * 64:(hq + 1) * 64], False, True, reads=[C_b, Sbb[g]], writes=[pyb])
                    if hb_ == 0:
                        return
                    s.op("act", ins("activation", out=y_[:, tbk, g * 512:(g + 1) * 512], in_=psy[:, 0:512], func=AF.Identity), reads=[pyb], writes=[y_b])
                    pss, psb_ = s.ps()
                    s.mm(pss[:, 0:512], b_[:, tbk, g * 128:(g + 1) * 128], x4[:], True, True, reads=[b_b, x4b], writes=[psb_])
                    ecs = G["ecs"]
                    for hh in range(8):
                        hq = g * 8 + hh
                        ec2, ecb2 = ecs[hh // 4]
                        s.op("dve", ins("scalar_tensor_tensor",
                                        out=St[:, hq * 64:(hq + 1) * 64], in0=St[:, hq * 64:(hq + 1) * 64], scalar=ec2[:, hh % 4, last:last + 1], in1=pss[:, hh * 64:(hh + 1) * 64], op0=ALU.mult, op1=ALU.add),
                             reads=[psb_, ecb2, Stb[g]], writes=[Stb[g]])
                    s.op("act", ins("activation", out=Sbt[:, g * 512:(g + 1) * 512], in_=St[:, g * 512:(g + 1) * 512], func=AF.Identity), reads=[Stb[g]], writes=[Sbb[g]])
                    del bst[("grp", kc_, d, g)]

                NB_ = len(batches)
                for t_ in range(NB_ + 3):
                    if t_ < NB_:
                        stageA1(batches[t_])
                    if 1 <= t_ < NB_ + 1:
                        stageA2a(batches[t_ - 1])
                    if 2 <= t_ < NB_ + 2:
                        stageA2b(batches[t_ - 2])
                    if t_ >= 3:
                        stageB(batches[t_ - 3])
                for d in range(2):
                    ti, t0, n, nb = cur[d]
                    y_, y_b = yts[d]
                    s.dma(rows_view(YD[d], 2048, t0, nb, 0, 2048), y_[:, 0:nb, :], reads=[y_b], writes=[Y_b[ti]])
            if si > 0:
                for d in range(2):
                    s.dma(st_out[d][si - 1], ST[d][0][:], reads=ST[d][1])
        chk(22)
        s.barrier()
        s.release()
        Wo = s.alloc([128, 16, 1024], BF16, "wo")
        wob = load_w(Wo, w3_o, 16)
        vec = s.alloc([128, 5, 32], F32, "vec")
        gnb = s.alloc([128, 2048], F32, "gnb")
        cbuf = Buf()
        s.dma(vec[:], ssd_vec.rearrange("a b -> (a b)").partition_broadcast(128).rearrange("p (a b) -> p a b", a=5), writes=[cbuf])
        s.dma(gnb[:], ssd_gn.partition_broadcast(128), writes=[cbuf])
        rctx = ResCtx()
        yf = (s.alloc([128, 4, 2048], BF16, "yf"), Buf())
        yb = (s.alloc([128, 4, 2048], BF16, "yb"), Buf())
        xt_ = (s.alloc([128, 4, 2048], BF16, "xt4"), Buf())
        zs = (s.alloc([128, 4, 2048], BF16, "zs"), Buf())
        yv = [(s.alloc([128, 2048], F32, "yv"), Buf()) for _ in range(2)]
        tv_ = (s.alloc([128, 2048], F32, "tv"), Buf())
        ynb = (s.alloc([128, 2048], BF16, "ynb"), Buf())
        ssq = [(s.alloc([128, 1], F32, "ssq"), Buf()) for _ in range(2)]
        oT = [(s.alloc([128, 16, 512], BF16, "oT"), Buf()) for _ in range(2)]
        kk = 0
        tv = 0
        for ti in range(NT):
            t0, n, cond, _, _ = TILES[ti]
            nb = n // 128
            s.dma(yf[0][:, 0:nb, :], rows_view(YD[0], 2048, t0, nb, 0, 2048), writes=[yf[1]])
            s.dma(yb[0][:, 0:nb, :], rows_view(YD[1], 2048, t0, nb, 0, 2048), writes=[yb[1]])
            s.dma(xt_[0][:, 0:nb, :], rows_view(XTs, 2048, t0, nb, 0, 2048), writes=[xt_[1]])
            s.dma(zs[0][:, 0:nb, :], rows_view(ZS, 2048, t0, nb, 0, 2048), writes=[zs[1]])
            o_, o_b = oT[ti % 2]
            for tbk in range(nb):
                y_, y_b = yv[kk % 2]
                sq_, sq_b = ssq[kk % 2]
                kk += 1
                t_, t_b = tv_
                s.op("dve", ins("tensor_tensor", out=y_[:], in0=yf[0][:, tbk, :], in1=yb[0][:, tbk, :], op=ALU.add), reads=[yf[1], yb[1]], writes=[y_b])
                s.op("pool", ins("tensor_tensor", out=t_[:].rearrange("p (a b) -> p a b", b=64), in0=xt_[0][:, tbk, :].rearrange("p (a b) -> p a b", b=64),
                                                              in1=vec[:, 4, :].unsqueeze(2).to_broadcast([128, 32, 64]), op=ALU.mult), reads=[xt_[1], cbuf], writes=[t_b])
                s.op("dve", ins("tensor_tensor", out=y_[:], in0=y_[:], in1=t_[:], op=ALU.add), reads=[y_b, t_b], writes=[y_b])
                s.op("dve", ins("tensor_tensor", out=y_[:], in0=y_[:], in1=zs[0][:, tbk, :], op=ALU.mult), reads=[y_b, zs[1]], writes=[y_b])
                s.op("pool", ins("memset", sq_[:], 0.0), writes=[sq_b])
                s.op("act", ins("activation", out=t_[:], in_=y_[:], func=AF.Square, accum_out=sq_[:, 0:1]), reads=[y_b, sq_b], writes=[t_b, sq_b])
                s.op("act", ins("activation", out=sq_[:], in_=sq_[:], func=AF.Sqrt, scale=1.0 / 2048, bias=EPSB[:, 0:1]), reads=[sq_b], writes=[sq_b])
                s.op("dve", ins("reciprocal", out=sq_[:], in_=sq_[:]), reads=[sq_b], writes=[sq_b])
                yn, ynb_ = ynb
                s.op("dve", ins("scalar_tensor_tensor", out=yn[:], in0=y_[:], scalar=sq_[:, 0:1], in1=gnb[:], op0=ALU.mult, op1=ALU.mult), reads=[y_b, sq_b, cbuf], writes=[ynb_])
                for c in range(16):
                    ps, pb = s.ps()
                    psv = ps[:].bitcast(BF16)
                    s.op("pe", ins("transpose", psv[:, 0:128], yn[:, c * 128:(c + 1) * 128], ident_bf[:]), reads=[ynb_, cb_const], writes=[pb])
                    if tv % 2 == 0:
                        s.op("act", ins("activation", out=o_[:, c, tbk * 128:(tbk + 1) * 128], in_=psv[:, 0:128], func=AF.Identity), reads=[pb], writes=[o_b])
                    else:
                        s.op("dve", ins("tensor_copy", out=o_[:, c, tbk * 128:(tbk + 1) * 128], in_=psv[:, 0:128]), reads=[pb], writes=[o_b])
                    tv += 1
            outproj_resid(rctx, l, 2, ti, Wo, wob, 16, o_, [o_b], Xsrc, Xsrc_b, X, X_b)

    import os
    STOP = int(os.environ.get("KSTOP", "99"))

    class _Stop(Exception):
        pass

    def chk(k):
        if STOP == k:
            raise _Stop()

    try:
        prologue()
        chk(0)
        Xsrc, Xsrc_b = xT, xT_b
        for l in range(NLAYERS):
            kind = l % 4
            if kind == 0:
                qkv_phase(l, w0_qk, 12, w0_v, 256, Xsrc, Xsrc_b, True, kout=(8, o_wk, 64), vout=o_wv)
                chk(1)
                attn_phase(l, "win", Xsrc, Xsrc_b)
                chk(2)
                outproj_phase(l, w0_o, 8, OS, None, Xsrc, Xsrc_b)
                chk(3)
            elif kind == 1:
                gla_layer(l, Xsrc, Xsrc_b)
            elif kind == 2:
                qkv_phase(l, w2_qk, 16, w2_v, 1024, Xsrc, Xsrc_b, True, kout=(8, o_dk, 128), vout=o_dv)
                attn_phase(l, "diff", Xsrc, Xsrc_b)
                outproj_phase(l, w2_o, 8, OS, None, Xsrc, Xsrc_b)
            else:
                ssd_layer(l, Xsrc, Xsrc_b)
            Xsrc, Xsrc_b = X, X_b
            ffn(l, Xsrc, Xsrc_b)
            chk(10 + l)
        s.barrier()
        s.release()
        nctx = NormCtx()
        for ti in range(NT):
            norm_tile(nctx, 0, 0, ti, Xsrc, Xsrc_b, final_out=yT)
    except _Stop:
        pass
    s.barrier()
    counts = s.emit()
    return nc, counts

def _fm(w):
    K, N = w.shape
    return np.ascontiguousarray(w.reshape(K // 128, 128, N).transpose(1, 0, 2))


def _pc(v):
    return np.ascontiguousarray(v.reshape(-1, 128).T)


def _consts():
    f32 = np.float32
    t = np.arange(LS)
    row = (t // 64).astype(f32)
    col = (t % 64).astype(f32)
    inv = (10000.0 ** (-np.arange(0, 32, 2, dtype=f32) / 32)).astype(f32)
    cos = np.zeros((128, LS), f32)
    sin = np.zeros((128, LS), f32)
    perm = np.zeros((128, 128), f32)
    for p in range(128):
        d = p % 64
        axis = d // 32
        idx = d % 16
        second = (d % 32) >= 16
        ang = (row if axis == 0 else col) * inv[idx]
        cos[p] = np.cos(ang)
        sin[p] = np.sin(ang) * (1.0 if second else -1.0)
        partner = p + 16 if not second else p - 16
        perm[partner, p] = 1.0
    j = np.arange(128)[:, None]
    i = np.arange(512)[None, :]
    wmask = np.zeros((128, 6, 512), f32)
    for mi in range(6):
        o = mi - 1
        wmask[:, mi, :] = (np.abs(i - (o * 128 + j)) <= 128).astype(f32)
    tri = np.zeros((128, 4, 128), f32)
    jj = np.arange(128)[:, None]
    ii = np.arange(128)[None, :]
    tri[:, 0, :] = (jj <= ii)
    tri[:, 1, :] = (jj >= ii)
    tri[0:64, 2, 0:64] = (jj[0:64] <= ii[:, 0:64])
    tri[0:64, 3, 0:64] = (jj[0:64] >= ii[:, 0:64])
    m64 = np.ones((128, 512), f32)
    m64[:, ::64] = 0.0
    return dict(rcos=cos, rsin=sin, perm=perm, wmask=wmask, tri=tri, m64=m64)


_PROG = {}


def _get_prog(nl):
    if nl not in _PROG:
        _PROG[nl] = build_program(nl)
    return _PROG[nl]


def kernel(NLAYERS=4, **inp):
    f32 = np.float32
    g = {k: np.asarray(v) for k, v in inp.items()}
    nc, counts = _get_prog(NLAYERS)
    common = dict(_consts())
    common["ada_w"] = np.ascontiguousarray(g["ada_w"].reshape(4, 8, 128, 6144).transpose(0, 2, 1, 3))
    common["ada_b"] = np.ascontiguousarray(g["ada_b"].reshape(4, 48, 128).transpose(2, 0, 1))
    common["nrm"] = np.ascontiguousarray(np.stack([g["norm_mix"], g["norm_ffn"]], axis=1).reshape(4, 2, 8, 128).transpose(3, 0, 1, 2))
    common["fnorm"] = _pc(g["final_norm"])
    common["w_up"] = np.ascontiguousarray(g["ffn_w_up"].reshape(4, 8, 128, 5632).transpose(0, 2, 1, 3))
    common["w_dn"] = np.ascontiguousarray(g["ffn_w_down"].reshape(4, 22, 128, 1024).transpose(0, 2, 1, 3))
    common["ffn_cw"] = np.ascontiguousarray(g["ffn_conv_w"].reshape(4, 3, 44, 128).transpose(3, 0, 1, 2))
    common["ffn_cb"] = np.ascontiguousarray(g["ffn_conv_b"].reshape(4, 44, 128).transpose(2, 0, 1))
    wq = g["win_w_qkv"][0]
    qcols = wq[:, 0:1024]
    kcols = wq[:, 1024:1280]
    vcols = wq[:, 1280:1536]
    kdup = np.concatenate([np.concatenate([kcols[:, h * 64:(h + 1) * 64]] * 2, axis=1) for h in range(4)], axis=1)
    common["w0_qk"] = _fm(np.concatenate([qcols, kdup], axis=1))
    common["w0_v"] = _fm(vcols)
    common["w0_o"] = _fm(g["win_w_o"][0])
    common["sink"] = np.ascontiguousarray(g["win_sink"][0])
    wg = g["gla_w_qkvr"][0]
    common["w1_qk"] = _fm(wg[:, 0:1024])
    common["w1_v"] = _fm(wg[:, 1024:2048])
    common["w1_r"] = _fm(wg[:, 2048:3072])
    common["w1_g1"] = _fm(np.concatenate([g["gla_w_gf1"][0], g["gla_w_gb1"][0]], axis=1))
    g2 = np.zeros((32, 1024), f32)
    g2[0:16, 0:512] = g["gla_w_gf2"][0]
    g2[16:32, 512:1024] = g["gla_w_gb2"][0]
    common["w1_g2"] = g2
    common["w1_gb"] = _pc(np.concatenate([g["gla_b_gf"][0], g["gla_b_gb"][0]]))
    common["w1_gn"] = _pc(g["gla_norm"][0])
    common["w1_o"] = _fm(g["gla_w_o"][0])
    wd = g["diff_w_qkv"][0]
    common["w2_qk"] = _fm(wd[:, 0:2048])
    common["w2_v"] = _fm(wd[:, 2048:3072])
    common["w2_o"] = _fm(g["diff_w_o"][0])
    common["lqk"] = np.ascontiguousarray(np.stack([g["diff_lq1"][0], g["diff_lk1"][0], g["diff_lq2"][0], g["diff_lk2"][0]]))
    common["w2_gn"] = np.ascontiguousarray(g["diff_norm"][0].reshape(128, 1))
    common.update(_host_ssd_common(g))
    in_maps = []
    for b in range(8):
        m = dict(common)
        xs = g["x_sample"][b]
        xp = g["x_prompt"][2 * b:2 * b + 2].reshape(512, 1024)
        xa = np.concatenate([xs, xp], axis=0)
        m["xT"] = np.ascontiguousarray(xa.T.reshape(8, 128, T).transpose(1, 0, 2))
        cond = np.stack([g["c"][b], g["c_ctx"]], axis=1)
        m["condT"] = np.ascontiguousarray(cond.reshape(8, 128, 2).transpose(1, 0, 2))
        ck = g["cache_win_k"][b, 0]
        kT = ck.transpose(2, 1, 0)
        m["ck0"] = np.ascontiguousarray(np.concatenate([kT, kT], axis=0))
        m["cv0"] = np.ascontiguousarray(g["cache_win_v"][b, 0].reshape(512, 256))
        m["s1f"] = np.ascontiguousarray(g["state_gla_fwd"][b, 0].transpose(1, 0, 2))
        m["s1b"] = np.ascontiguousarray(g["state_gla_bwd"][b, 0].transpose(1, 0, 2))
        dk = g["cache_diff_k"][b, 0]
        m["ck2"] = np.ascontiguousarray(dk.transpose(2, 3, 1, 0).reshape(128, 8, 512))
        m["cv2"] = np.ascontiguousarray(g["cache_diff_v"][b, 0].reshape(512, 1024))
        m.update(_host_ssd_core(g, b))
        in_maps.append(m)
    import os
    ncores = int(os.environ.get("KCORES", "8"))
    ktrace = os.environ.get("KTRACE", "") == "1"
    res = run_bass_kernel_spmd(nc, in_maps[:ncores], core_ids=list(range(ncores)), trace=ktrace) if ktrace else run_bass_kernel_spmd(nc, in_maps[:ncores], core_ids=list(range(ncores)))
    if ktrace:
        print("EXEC_TIME_NS", res.exec_time_ns)
    R = list(res.results) + [res.results[0]] * (8 - ncores)
    y_prompt = np.zeros((16, 256, 1024), f32)
    y_sample = np.zeros((8, 4096, 1024), f32)
    win_k = np.zeros((16, 1, 256, 4, 64), f32)
    win_v = np.zeros((16, 1, 256, 4, 64), f32)
    gla_f = np.zeros((16, 1, 4, 128, 256), f32)
    gla_b = np.zeros((16, 1, 4, 128, 256), f32)
    diff_k = np.zeros((16, 1, 256, 8, 2, 64), f32)
    diff_v = np.zeros((16, 1, 256, 8, 128), f32)
    ssd_f = np.zeros((16, 1, 32, 64, 128), f32)
    ssd_b = np.zeros((16, 1, 32, 64, 128), f32)
    for b in range(8):
        r = R[b]
        y = r["yT"].transpose(1, 0, 2).reshape(1024, T).T
        y_sample[b] = y[0:4096]
        y_prompt[2 * b:2 * b + 2] = y[4096:].reshape(2, 256, 1024)
        wk = r["o_wk"]
        win_k[2 * b:2 * b + 2, 0] = wk.transpose(2, 0, 1).reshape(2, 256, 4, 64)
        win_v[2 * b:2 * b + 2, 0] = r["o_wv"].reshape(2, 256, 4, 64)
        gla_f[2 * b:2 * b + 2, 0] = r["o_gf"].transpose(0, 2, 1, 3)
        gla_b[2 * b:2 * b + 2, 0] = r["o_gb"].transpose(0, 2, 1, 3)
        dk = r["o_dk"]
        diff_k[2 * b:2 * b + 2, 0] = dk.transpose(2, 0, 1).reshape(2, 256, 8, 2, 64)
        diff_v[2 * b:2 * b + 2, 0] = r["o_dv"].reshape(2, 256, 8, 128)
        _host_ssd_out(r, b, ssd_f, ssd_b)
    return (y_prompt, y_sample, win_k, win_v, gla_f, gla_b, diff_k, diff_v, ssd_f, ssd_b)


def _host_ssd_common(g):
    w = g["ssd_w_in"][0]
    out = {}
    out["w3_z"] = _fm(w[:, 0:2048])
    out["w3_xbc"] = _fm(w[:, 2048:5120])
    out["w3_dt"] = _fm(w[:, 5120:5184])
    out["w3_o"] = _fm(g["ssd_w_out"][0])
    out["ssd_cw"] = np.ascontiguousarray(g["ssd_conv_w"][0].reshape(3, 24, 128).transpose(2, 0, 1))
    out["ssd_cb"] = _pc(g["ssd_conv_b"][0])
    out["ssd_vec"] = np.ascontiguousarray(np.stack([g["ssd_a_log_f"][0], g["ssd_a_log_b"][0], g["ssd_dt_bias_f"][0], g["ssd_dt_bias_b"][0], g["ssd_d"][0]]))
    out["ssd_gn"] = np.ascontiguousarray(g["ssd_norm"][0])
    return out


def _host_ssd_core(g, b):
    return {"s3f": np.ascontiguousarray(g["state_ssd_fwd"][b, 0].transpose(2, 0, 1).reshape(128, 2048)),
            "s3b": np.ascontiguousarray(g["state_ssd_bwd"][b, 0].transpose(2, 0, 1).reshape(128, 2048))}


def _host_ssd_out(r, b, ssd_f, ssd_b):
    ssd_f[2 * b:2 * b + 2, 0] = r["o_sf"].reshape(2, 128, 32, 64).transpose(0, 2, 3, 1)
    ssd_b[2 * b:2 * b + 2, 0] = r["o_sb"].reshape(2, 128, 32, 64).transpose(0, 2, 3, 1)
```

```python
import numpy as np
import concourse.bass as bass
import concourse.mybir as mybir
from concourse.bass_utils import run_bass_kernel_spmd

F32 = mybir.dt.float32
BF16 = mybir.dt.bfloat16
AF = mybir.ActivationFunctionType
ALU = mybir.AluOpType
AX = mybir.AxisListType
AP = bass.AP

SB_BASE = 16512
SB_TOP = 229344
EPOCH = 30000


class Buf:
    __slots__ = ("w", "rs", "name")

    def __init__(self, name=""):
        self.w = None
        self.rs = []
        self.name = name


class Op:
    __slots__ = ("eng", "fn", "deps", "dma", "ms", "sem", "val", "need", "prev", "grp")

    def __init__(self, eng, fn, dma):
        self.eng = eng
        self.fn = fn
        self.dma = dma
        self.deps = []
        self.ms = None
        self.sem = None
        self.val = None
        self.need = False
        self.prev = None
        self.grp = None


class Sched:
    ENGS = ("pe", "act", "dve", "pool", "sp")

    def __init__(self, nc):
        self.nc = nc
        self.ops = {e: [] for e in self.ENGS}
        self.dmas_since_barrier = []
        self.all_dmas = []
        self.nps = 0
        self.nacc = 0
        self.psum = []
        for i in range(8):
            t = nc.alloc_psum_tensor("psb%d" % i, [128, 512], F32)
            self.psum.append((t, Buf("ps%d" % i)))
        self.sb_off = SB_BASE
        self.sb_mark = SB_BASE
        self.nalloc = 0

    def alloc(self, shape, dtype, name=None):
        nbytes = int(np.prod(shape[1:])) * (4 if dtype == F32 else 2)
        nbytes = (nbytes + 63) // 64 * 64
        off = self.sb_off
        assert off + nbytes <= SB_TOP, "SBUF overflow %d" % (off + nbytes - SB_TOP)
        self.sb_off += nbytes
        self.nalloc += 1
        t = self.nc.alloc_sbuf_tensor_at("sb%d_%s" % (self.nalloc, name or "t"), list(shape), dtype, offset=off)
        return t

    def mark(self):
        self.sb_mark = self.sb_off

    def release(self):
        self.sb_off = self.sb_mark

    def ps(self, pool="a"):
        if pool == "a":
            t, b = self.psum[self.nps % 6]
            self.nps += 1
        else:
            t, b = self.psum[6 + self.nacc % 2]
            self.nacc += 1
        return t, b

    def op(self, eng, fn, reads=(), writes=(), dma=False):
        o = Op(eng, fn, dma)
        deps = {}
        for b in reads:
            d = b.w
            if d is not None:
                if (not dma) and (not d.dma) and d.eng == eng and eng == "pe":
                    continue
                deps[id(d)] = d
        for b in writes:
            cand = list(b.rs)
            if b.w is not None:
                cand.append(b.w)
            for d in cand:
                if d is o:
                    continue
                if (not dma) and (not d.dma) and d.eng == eng:
                    continue
                deps[id(d)] = d
        o.deps = list(deps.values())
        for b in reads:
            if not dma:
                b.rs = [r for r in b.rs if r.dma or r.eng != eng]
            b.rs.append(o)
        for b in writes:
            b.w = o
            b.rs = []
        self.ops[eng].append(o)
        if dma:
            self.dmas_since_barrier.append(o)
            self.all_dmas.append(o)
        return o

    def barrier(self):
        lasts = []
        for e in self.ENGS:
            for o in reversed(self.ops[e]):
                if not o.dma and o.fn is not None:
                    lasts.append(o)
                    break
        deps = lasts + self.dmas_since_barrier
        self.dmas_since_barrier = []
        for e in self.ENGS:
            o = Op(e, None, False)
            o.deps = list(deps)
            self.ops[e].append(o)

    def dma(self, out, in_, reads=(), writes=(), eng=None):
        if eng is None:
            eng = "pool" if type(out.tensor).__name__.startswith("DRam") else "sp"
        return self.op(eng, lambda e: e.dma_start(out=out, in_=in_), reads, writes, dma=True)

    def mm(self, out, lhsT, rhs, start, stop, reads=(), writes=(), grp=None):
        o = self.op("pe", lambda e: e.matmul(out, lhsT, rhs, start=start, stop=stop), reads, writes)
        o.grp = grp
        return o

    def emit(self):
        nc = self.nc
        for e in self.ENGS:
            for o in self.ops[e]:
                for d in o.deps:
                    d.need = True
        sem_ctx = []
        import contextlib
        with contextlib.ExitStack() as st:
            tl = {}
            for e in ("pe", "act", "dve", "pool"):
                tl[e] = [st.enter_context(nc.semaphore("tl_%s_%d" % (e, i))) for i in range(5)]
            npool = {"sp": 36, "pool": 36, "act": 8}
            dpool = {e: [st.enter_context(nc.semaphore("dq_%s_%d" % (e, i))) for i in range(n)] for e, n in npool.items()}
            for e in self.ENGS:
                m = 0
                k = 0
                for o in self.ops[e]:
                    if o.fn is None:
                        continue
                    if o.dma:
                        P = len(dpool[e])
                        o.sem = dpool[e][k % P]
                        o.val = 16 * (k // P + 1)
                        k += 1
                    elif o.need:
                        o.sem = tl[e][m // EPOCH]
                        o.val = m % EPOCH + 1
                        m += 1
                assert m < EPOCH * 5, (e, m)
            final = {}
            for e, n in npool.items():
                for o in self.ops[e]:
                    if o.dma:
                        final[id(o.sem)] = (o.sem, o.val)

            def stream(e, eng):
                seen = {}

                def wait(sem, val):
                    if seen.get(id(sem), 0) < val:
                        eng.wait_ge(sem, val)
                        seen[id(sem)] = val

                ops_ = self.ops[e]
                i_ = 0
                while i_ < len(ops_):
                    o = ops_[i_]
                    j_ = i_ + 1
                    if o.grp is not None:
                        while j_ < len(ops_) and ops_[j_].grp == o.grp:
                            j_ += 1
                    for k_ in range(i_, j_):
                        for d in ops_[k_].deps:
                            wait(d.sem, d.val)
                    for k_ in range(i_, j_):
                        o = ops_[k_]
                        if o.fn is None:
                            continue
                        if o.dma and o.val > 16:
                            wait(o.sem, o.val - 16)
                        ins_ = o.fn(eng)
                        if o.dma:
                            ins_.then_inc(o.sem, 16)
                        elif o.need:
                            ins_.then_inc(o.sem, 1)
                    i_ = j_
                if e == "sp":
                    for sem, val in final.values():
                        wait(sem, val)

            with nc.Block() as block:
                @block.tensor
                def _(eng):
                    stream("pe", eng)

                @block.scalar
                def _(eng):
                    stream("act", eng)

                @block.vector
                def _(eng):
                    stream("dve", eng)

                @block.gpsimd
                def _(eng):
                    stream("pool", eng)

                @block.sync
                def _(eng):
                    stream("sp", eng)
        return {e: len(v) for e, v in self.ops.items()}


def ins(method, *a, **kw):
    return lambda e: getattr(e, method)(*a, **kw)


T = 4608
LS = 4096
TILES = [(i * 512, 512, 0, 0, 4096) for i in range(8)] + [(4096, 256, 1, 4096, 4352), (4352, 256, 1, 4352, 4608)]
EPS = 1e-6
DFF = 2816
LAM_INIT = 0.8 - 0.6 * float(np.exp(-0.3 * 2))


def build_program(NLAYERS=4, dbg=False):
    nc = bass.Bass("TRN2", target_bir_lowering=False)
    s = Sched(nc)
    I = {}
    O = {}

    def din(name, shape):
        I[name] = nc.dram_tensor(name, list(shape), F32, kind="ExternalInput").ap()
        return I[name]

    def dout(name, shape):
        O[name] = nc.dram_tensor(name, list(shape), F32, kind="ExternalOutput").ap()
        return O[name]

    def dscr(name, shape, dt):
        return nc.dram_tensor(name, list(shape), dt).ap()

    xT = din("xT", [128, 8, T])
    condT = din("condT", [128, 8, 2])
    ada_w = din("ada_w", [4, 128, 8, 6144])
    ada_b = din("ada_b", [128, 4, 48])
    nrm = din("nrm", [128, 4, 2, 8])
    fnorm = din("fnorm", [128, 8])
    w_up = din("w_up", [4, 128, 8, 5632])
    w_dn = din("w_dn", [4, 128, 22, 1024])
    ffn_cw = din("ffn_cw", [128, 4, 3, 44])
    ffn_cb = din("ffn_cb", [128, 4, 44])
    perm_in = din("perm", [128, 128])
    rcos = din("rcos", [128, LS])
    rsin = din("rsin", [128, LS])
    wmask_in = din("wmask", [128, 6, 512])
    tri_in = din("tri", [128, 4, 128])
    m64_in = din("m64", [128, 512])
    w0_qk = din("w0_qk", [128, 8, 1536])
    w0_v = din("w0_v", [128, 8, 256])
    w0_o = din("w0_o", [128, 8, 1024])
    sink_in = din("sink", [16])
    ck0 = din("ck0", [128, 4, 512])
    cv0 = din("cv0", [512, 256])
    w1_qk = din("w1_qk", [128, 8, 1024])
    w1_v = din("w1_v", [128, 8, 1024])
    w1_r = din("w1_r", [128, 8, 1024])
    w1_g1 = din("w1_g1", [128, 8, 32])
    w1_g2 = din("w1_g2", [32, 1024])
    w1_gb = din("w1_gb", [128, 8])
    w1_gn = din("w1_gn", [128, 2])
    w1_o = din("w1_o", [128, 8, 1024])
    s1f = din("s1f", [128, 4, 256])
    s1b = din("s1b", [128, 4, 256])
    w2_qk = din("w2_qk", [128, 8, 2048])
    w2_v = din("w2_v", [128, 8, 1024])
    w2_o = din("w2_o", [128, 8, 1024])
    lqk = din("lqk", [4, 64])
    w2_gn = din("w2_gn", [128, 1])
    ck2 = din("ck2", [128, 8, 512])
    cv2 = din("cv2", [512, 1024])

    w3_z = din("w3_z", [128, 8, 2048])
    w3_xbc = din("w3_xbc", [128, 8, 3072])
    w3_dt = din("w3_dt", [128, 8, 64])
    w3_o = din("w3_o", [128, 16, 1024])
    ssd_cw = din("ssd_cw", [128, 3, 24])
    ssd_cb = din("ssd_cb", [128, 24])
    ssd_vec = din("ssd_vec", [5, 32])
    ssd_gn = din("ssd_gn", [2048])
    s3f = din("s3f", [128, 2048])
    s3b = din("s3b", [128, 2048])

    yT = dout("yT", [128, 8, T])
    o_sf = dout("o_sf", [2, 128, 2048])
    o_sb = dout("o_sb", [2, 128, 2048])
    o_wk = dout("o_wk", [4, 64, 512])
    o_wv = dout("o_wv", [512, 256])
    o_gf = dout("o_gf", [2, 128, 4, 256])
    o_gb = dout("o_gb", [2, 128, 4, 256])
    o_dk = dout("o_dk", [8, 128, 512])
    o_dv = dout("o_dv", [512, 1024])

    X = dscr("X", [128, 8, T], F32)
    U = dscr("U", [128, 44, T], BF16)
    QK = dscr("QK", [128, 16, T], BF16)
    VS = dscr("VS", [T, 1024], BF16)
    OS = dscr("OS", [128, 8, T], BF16)
    OS2 = dscr("OS2", [128, 8, T], BF16)
    RS = dscr("RS", [128, 8, T], BF16)
    QD = dscr("QD", [2, 128, 4, T], BF16)
    KD = dscr("KD", [2, 128, 4, T], BF16)
    KR = dscr("KR", [T, 8, 128], BF16)
    ZS = dscr("ZS", [T, 2048], BF16)
    DTS = dscr("DTS", [T, 128], F32)
    XTs = dscr("XTs", [T, 2048], BF16)
    BTs = dscr("BTs", [T, 512], BF16)
    YD = [dscr("YD0", [T, 2048], BF16), dscr("YD1", [T, 2048], BF16)]

    NT = len(TILES)
    import os
    STOP = int(os.environ.get("KSTOP", "99"))

    def rows_view(dr, rowlen, r0, nb, c0, w):
        return AP(dr.tensor, r0 * rowlen + c0, [[rowlen, 128], [128 * rowlen, nb], [1, w]])

    def kr_view(r0, nb, j0, nj):
        return AP(KR.tensor, r0 * 1024 + j0 * 128, [[1024, 128], [128 * 1024, nb], [128, nj], [1, 128]])
    import os
    KQ = os.environ.get("KQ", "")

    def tb(name):
        return [Buf(name + str(i)) for i in range(NT)]

    xT_b = tb("xT")
    X_b = tb("X")

    MODS = s.alloc([128, 4, 6, 8, 2], F32, "mods")
    GS = s.alloc([128, 4, 2, 8, 2], F32, "gs")
    NRM = s.alloc([128, 4, 2, 8], F32, "nrm")
    FNRM = s.alloc([128, 8], F32, "fnrm")
    ones_bf = s.alloc([128, 128], BF16, "ones")
    ones_f = s.alloc([128, 128], F32, "onesf")
    ident_bf = s.alloc([128, 128], BF16, "ident")
    perm_bf = s.alloc([128, 128], BF16, "perm")
    tri_bf = s.alloc([128, 4, 128], BF16, "tri")
    m64 = s.alloc([128, 512], F32, "m64")
    cb_const = Buf("consts")
    stage = s.alloc([128, 4, 128], F32, "stage")
    bstage = Buf()

    s.dma(NRM[:], nrm, writes=[cb_const])
    s.dma(FNRM[:], fnorm, writes=[cb_const])
    s.dma(m64[:], m64_in, writes=[cb_const])
    s.op("pool", ins("memset", ones_f[:], 1.0), writes=[cb_const])
    s.op("dve", ins("tensor_copy", out=ones_bf[:], in_=ones_f[:]), reads=[cb_const], writes=[cb_const])
    s.dma(stage[:, 0, :], perm_in, writes=[bstage])
    s.op("dve", ins("tensor_copy", out=perm_bf[:], in_=stage[:, 0, :]), reads=[bstage], writes=[cb_const])
    s.op("pool", ins("memset", stage[:, 1, :], 1.0), reads=[], writes=[bstage])
    s.op("pool", ins("affine_select", out=stage[:, 1, :], in_=stage[:, 1, :], pattern=[[-1, 128]], compare_op=ALU.is_equal, fill=0.0, base=0, channel_multiplier=1), reads=[bstage], writes=[bstage])
    s.op("dve", ins("tensor_copy", out=ident_bf[:], in_=stage[:, 1, :]), reads=[bstage], writes=[cb_const])
    s.barrier()
    s.dma(stage[:], tri_in, writes=[bstage])
    s.op("dve", ins("tensor_copy", out=tri_bf[:], in_=stage[:]), reads=[bstage], writes=[cb_const])
    s.mark()

    def mod_ap(l, k, c, cond):
        return MODS[:, l, k, c, cond:cond + 1]

    def prologue():
        s.barrier()
        s.release()
        sc = s.alloc([128, 8, 2], F32, "sc")
        scb = Buf()
        adb = s.alloc([128, 4, 48], F32, "adb")
        adbb = Buf()
        s.dma(sc[:], condT, writes=[scb])
        s.dma(adb[:], ada_b, writes=[adbb])
        s.op("act", ins("activation", out=sc[:], in_=sc[:], func=AF.Silu), reads=[scb], writes=[scb])
        wb = [s.alloc([128, 8, 1024], F32, "adaw%d" % i) for i in range(2)]
        wbb = [[Buf() for _ in range(2)] for _ in range(2)]
        it = 0
        for l in range(NLAYERS):
            for g in range(6):
                w = wb[it % 2]
                bb = wbb[it % 2]
                for hh in range(2):
                    s.dma(w[:, hh * 4:(hh + 1) * 4, :], ada_w[l, :, hh * 4:(hh + 1) * 4, g * 1024:(g + 1) * 1024], writes=[bb[hh]], eng=("sp" if hh == 0 else "act"))
                ps, pb = s.ps()
                for cc in range(8):
                    for kc in range(8):
                        s.mm(ps[:, cc * 2:cc * 2 + 2], w[:, kc, cc * 128:(cc + 1) * 128], sc[:, kc, :], kc == 0, kc == 7, reads=[bb[kc // 4], scb], writes=[pb])
                s.op("dve", ins("tensor_tensor",
                    out=MODS[:, l, g, :, :], in0=ps[:, 0:16].rearrange("p (c t) -> p c t", t=2),
                    in1=adb[:, l, g * 8:(g + 1) * 8].unsqueeze(2).to_broadcast([128, 8, 2]), op=ALU.add),
                    reads=[pb, adbb], writes=[cb_const])
                it += 1
            for which in range(2):
                k = 1 if which == 0 else 4
                s.op("dve", ins("scalar_tensor_tensor",
                    out=GS[:, l, which, :, :], in0=MODS[:, l, k, :, :], scalar=1.0,
                    in1=NRM[:, l, which, :].unsqueeze(2).to_broadcast([128, 8, 2]), op0=ALU.add, op1=ALU.mult),
                    reads=[cb_const], writes=[cb_const])

    def load_w(dst, src, nk, split=1):
        bufs = []
        for kc in range(nk):
            b = Buf()
            s.dma(dst[:, kc, :], src[:, kc, :], writes=[b], eng="pool")
            bufs.append(b)
        return bufs

    class NormCtx:
        def __init__(self, nbuf=2):
            self.nbuf = nbuf
            self.xt = [(s.alloc([128, 8, 512], F32, "nxt"), Buf()) for _ in range(nbuf)]
            self.h = [(s.alloc([128, 8, 512], BF16, "nh"), Buf()) for _ in range(nbuf)]
            self.sq = (s.alloc([128, 8, 512], BF16, "nsq"), Buf())
            self.tmp = (s.alloc([128, 8, 512], F32, "ntmp"), Buf())
            self.rstd = [(s.alloc([128, 512], F32, "nrs"), Buf()) for _ in range(2)]
            self.k = 0

    def norm_tile(ctx, l, which, ti, Xsrc, Xsrc_b, final_out=None):
        t0, n, cond, _, _ = TILES[ti]
        k = ctx.k
        ctx.k += 1
        xt, xtb = ctx.xt[k % ctx.nbuf]
        h, hb = ctx.h[k % ctx.nbuf]
        sq, sqb = ctx.sq
        tmp, tmpb = ctx.tmp
        rstd, rb = ctx.rstd[k % 2]
        s.dma(xt[:, :, :n], Xsrc[:, :, t0:t0 + n], reads=[Xsrc_b[ti]], writes=[xtb])
        s.op("act", ins("activation", out=sq[:, :, :n], in_=xt[:, :, :n], func=AF.Square), reads=[xtb], writes=[sqb])
        ps, pb = s.ps()
        for c in range(8):
            s.mm(ps[:, :n], ones_bf[:], sq[:, c, :n], c == 0, c == 7, reads=[sqb, cb_const], writes=[pb])
        s.op("act", ins("activation", out=rstd[:, :n], in_=ps[:, :n], func=AF.Sqrt, scale=1.0 / 1024, bias=EPSB[:, 0:1]), reads=[pb], writes=[rb])
        s.op("dve", ins("reciprocal", out=rstd[:, :n], in_=rstd[:, :n]), reads=[rb], writes=[rb])
        s.op("dve", ins("tensor_tensor", out=tmp[:, :, :n], in0=xt[:, :, :n], in1=rstd[:, :n].unsqueeze(1).to_broadcast([128, 8, n]), op=ALU.mult),
             reads=[xtb, rb], writes=[tmpb])
        if final_out is None:
            for c in range(8):
                s.op("act", ins("activation", out=h[:, c, :n], in_=tmp[:, c, :n], func=AF.Identity,
                                                          scale=GS[:, l, which, c, cond:cond + 1], bias=mod_ap(l, 0 if which == 0 else 3, c, cond)),
                     reads=[tmpb, cb_const], writes=[hb])
            return h, hb
        else:
            for c in range(8):
                s.op("act", ins("activation", out=xt[:, c, :n], in_=tmp[:, c, :n], func=AF.Identity, scale=FNRM[:, c:c + 1]),
                     reads=[tmpb, cb_const], writes=[xtb])
            s.dma(final_out[:, :, t0:t0 + n], xt[:, :, :n], reads=[xtb])
            return None, None

    EPSB = s.alloc([128, 1], F32, "epsb")
    s.op("pool", ins("memset", EPSB[:], EPS), writes=[cb_const])
    s.mark()

    class ResCtx:
        def __init__(self):
            self.xo = [(s.alloc([128, 8, 512], F32, "xo"), Buf()) for _ in range(2)]
            self.k = 0

    def outproj_resid(rctx, l, gate_k, ti, W, wbufs, nk, rhs, rhsbufs, Xsrc, Xsrc_b, Xdst, Xdst_b):
        t0, n, cond, _, _ = TILES[ti]
        xo, xob = rctx.xo[rctx.k % 2]
        rctx.k += 1
        s.dma(xo[:, :, :n], Xsrc[:, :, t0:t0 + n], reads=[Xsrc_b[ti]], writes=[xob])
        for dc in range(8):
            ps, pb = s.ps()
            for kc in range(nk):
                s.mm(ps[:, :n], W[:, kc, dc * 128:(dc + 1) * 128], rhs[:, kc, :n], kc == 0, kc == nk - 1,
                     reads=[wbufs[kc]] + list(rhsbufs), writes=[pb])
            s.op("dve", ins("scalar_tensor_tensor", out=xo[:, dc, :n], in0=ps[:, :n], scalar=mod_ap(l, gate_k, dc, cond),
                                                                      in1=xo[:, dc, :n], op0=ALU.mult, op1=ALU.add),
                 reads=[pb, xob, cb_const], writes=[xob])
        s.dma(Xdst[:, :, t0:t0 + n], xo[:, :, :n], reads=[xob], writes=[Xdst_b[ti]])

    def ffn(l, Xsrc, Xsrc_b):
        s.barrier()
        s.release()
        Wup = s.alloc([128, 8, 5632], BF16, "wup")
        wb = load_w(Wup, w_up[l], 8)
        nctx = NormCtx()
        ub = [(s.alloc([128, 11, 512], BF16, "ub"), [Buf() for _ in range(11)]) for _ in range(2)]
        U_b = [[Buf() for _ in range(4)] for _ in range(NT)]
        ku = 0
        ev = 0
        for ti in range(NT):
            t0, n, cond, _, _ = TILES[ti]
            h, hb = norm_tile(nctx, l, 1, ti, Xsrc, Xsrc_b)
            for grp in range(4):
                ut, utb = ub[ku % 2]
                ku += 1
                for cc in range(11):
                    col = grp * 11 + cc
                    ps, pb = s.ps()
                    for kc in range(8):
                        s.mm(ps[:, :n], Wup[:, kc, col * 128:(col + 1) * 128], h[:, kc, :n], kc == 0, kc == 7, reads=[wb[kc], hb], writes=[pb])
                    if ev % 2 == 0:
                        s.op("act", ins("activation", out=ut[:, cc, :n], in_=ps[:, :n], func=AF.Identity), reads=[pb], writes=[utb[cc]])
                    else:
                        s.op("dve", ins("tensor_copy", out=ut[:, cc, :n], in_=ps[:, :n]), reads=[pb], writes=[utb[cc]])
                    ev += 1
                s.dma(U[:, grp * 11:(grp + 1) * 11, t0:t0 + n], ut[:, :, :n], reads=utb, writes=[U_b[ti][grp]])
        s.barrier()
        s.release()
        Wd = s.alloc([128, 22, 1024], BF16, "wd")
        wdb = load_w(Wd, w_dn[l], 22)
        cw = s.alloc([128, 3, 44], F32, "cw")
        cbias = s.alloc([128, 44], F32, "cbias")
        cwb = Buf()
        s.dma(cw[:], ffn_cw[:, l, :, :], writes=[cwb])
        s.dma(cbias[:], ffn_cb[:, l, :], writes=[cwb])
        ug = s.alloc([128, 44, 514], BF16, "ug")
        ugb = [Buf() for _ in range(4)]
        at = [(s.alloc([128, 22, 512], BF16, "at"), Buf()) for _ in range(2)]
        tmps = [[(s.alloc([128, 512], F32, "ft"), Buf()) for _ in range(3)] for _ in range(4)]
        rctx = ResCtx()
        kp = 0
        for ti in range(NT):
            t0, n, cond, s0, s1 = TILES[ti]
            lo = (t0 - 1) >= s0
            hi = (t0 + n) < s1
            for grp in range(4):
                gs_ = slice(grp * 11, (grp + 1) * 11)
                if not lo:
                    s.op("pool", ins("memset", ug[:, gs_, 0:1], 0.0), writes=[ugb[grp]])
                if not hi:
                    s.op("pool", ins("memset", ug[:, gs_, n + 1:n + 2], 0.0), writes=[ugb[grp]])
                c0 = 0 if lo else 1
                c1 = n + 2 if hi else n + 1
                rd = [U_b[ti][grp]]
                if lo:
                    rd.append(U_b[ti - 1][grp])
                if hi:
                    rd.append(U_b[ti + 1][grp])
                s.dma(ug[:, gs_, c0:c1], U[:, gs_, t0 - 1 + c0:t0 - 1 + c1], reads=rd, writes=[ugb[grp]])
            a, ab = at[ti % 2]
            pst = {}

            def f2s1(cc):
                nonlocal kp
                res = []
                for half in range(2):
                    col = cc + 22 * half
                    tt, ttb = tmps[kp % 4][half]
                    grp = col // 11
                    s.op("act", ins("activation", out=tt[:, :n], in_=ug[:, col, 1:n + 1], func=AF.Identity,
                                    scale=cw[:, 1, col:col + 1], bias=cbias[:, col:col + 1]),
                         reads=[ugb[grp], cwb], writes=[ttb])
                    res.append((tt, ttb, col, grp))
                sg, sgb = tmps[kp % 4][2]
                kp += 1
                pst[cc] = (res, sg, sgb)

            def f2s2(cc):
                res, sg, sgb = pst[cc]
                for (tt, ttb, col, grp) in res:
                    s.op("dve", ins("scalar_tensor_tensor", out=tt[:, :n], in0=ug[:, col, 0:n], scalar=cw[:, 0, col:col + 1],
                                    in1=tt[:, :n], op0=ALU.mult, op1=ALU.add),
                         reads=[ugb[grp], cwb, ttb], writes=[ttb])
                    s.op("dve", ins("scalar_tensor_tensor", out=tt[:, :n], in0=ug[:, col, 2:n + 2], scalar=cw[:, 2, col:col + 1],
                                    in1=tt[:, :n], op0=ALU.mult, op1=ALU.add),
                         reads=[ugb[grp], cwb, ttb], writes=[ttb])

            def f2s3(cc):
                res, sg, sgb = pst.pop(cc)
                s.op("act", ins("activation", out=sg[:, :n], in_=res[0][0][:, :n], func=AF.Silu), reads=[res[0][1]], writes=[sgb])
                s.op("pool", ins("tensor_tensor", out=a[:, cc, :n], in0=sg[:, :n], in1=res[1][0][:, :n], op=ALU.mult),
                     reads=[sgb, res[1][1]], writes=[ab])

            for t_ in range(22 + 2):
                if t_ < 22:
                    f2s1(t_)
                if 1 <= t_ < 23:
                    f2s2(t_ - 1)
                if t_ >= 2:
                    f2s3(t_ - 2)
            outproj_resid(rctx, l, 5, ti, Wd, wdb, 22, a, [ab], Xsrc, Xsrc_b, X, X_b)

    def qkv_phase(l, Wqk_src, nqk, Wv_src, nv, Xsrc, Xsrc_b, rope, kout=None, vout=None, post=None):
        s.barrier()
        s.release()
        Wqk = s.alloc([128, 8, nqk * 128], BF16, "wqk")
        wqb = load_w(Wqk, Wqk_src, 8)
        Wv = s.alloc([128, 8, nv], BF16, "wv")
        wvb = load_w(Wv, Wv_src, 8)
        nctx = NormCtx()
        qk = [(s.alloc([128, nqk, 512], BF16, "qk"), [Buf() for _ in range(nqk)]) for _ in range(2)]
        vt = [(s.alloc([128, 4, nv], BF16, "vt"), Buf()) for _ in range(2)]
        qb = [(s.alloc([128, 512], BF16, "qb"), Buf()) for _ in range(3)]
        t12 = [[(s.alloc([128, 512], F32, "rt"), Buf()) for _ in range(2)] for _ in range(3)]
        cs = [(s.alloc([128, 2, 512], F32, "cs"), Buf()) for _ in range(1)]
        kf = [(s.alloc([128, 256], F32, "kf"), Buf()) for _ in range(2)]
        vf = [(s.alloc([128, 512], F32, "vf"), Buf()) for _ in range(2)]
        QK_b = [Buf() for _ in range(NT)]
        VS_b = [Buf() for _ in range(NT)]
        if "L" in KQ:
            return
        kq = 0
        kk = 0
        kv = 0
        for ti in range(NT):
            t0, n, cond, s0, s1 = TILES[ti]
            h, hb = norm_tile(nctx, l, 0, ti, Xsrc, Xsrc_b)
            if "N" in KQ:
                continue
            qkt, qkb = qk[ti % 2]
            dorope = rope and cond == 0 and ("r" not in KQ)
            if dorope:
                cst, csb = cs[0]
                s.dma(cst[:, 0, :], rcos[:, t0:t0 + n], writes=[csb])
                s.dma(cst[:, 1, :], rsin[:, t0:t0 + n], writes=[csb])
            ropeq = []

            def rope_fin(item):
                cc_, q_, q_b, t1, t1b, t2, t2b = item
                ps2, pb2 = s.ps()
                s.mm(ps2[:, :n], perm_bf[:], q_[:, :n], True, True, reads=[q_b, cb_const], writes=[pb2])
                s.op("dve", ins("tensor_tensor", out=t1[:, :n], in0=q_[:, :n], in1=cst[:, 0, :n], op=ALU.mult),
                     reads=[q_b, csb], writes=[t1b])
                s.op("dve", ins("tensor_tensor", out=t2[:, :n], in0=ps2[:, :n], in1=cst[:, 1, :n], op=ALU.mult),
                     reads=[pb2, csb], writes=[t2b])
                s.op("pool", ins("tensor_tensor", out=qkt[:, cc_, :n], in0=t1[:, :n], in1=t2[:, :n], op=ALU.add),
                     reads=[t1b, t2b], writes=[qkb[cc_]])

            for cc in range(nqk):
                ps, pb = s.ps()
                for kc in range(8):
                    s.mm(ps[:, :n], Wqk[:, kc, cc * 128:(cc + 1) * 128], h[:, kc, :n], kc == 0, kc == 7, reads=[wqb[kc], hb], writes=[pb])
                if dorope:
                    q_, q_b = qb[kq % 3]
                    t1, t1b = t12[kq % 3][0]
                    t2, t2b = t12[kq % 3][1]
                    kq += 1
                    s.op("act", ins("activation", out=q_[:, :n], in_=ps[:, :n], func=AF.Identity), reads=[pb], writes=[q_b])
                    ropeq.append((cc, q_, q_b, t1, t1b, t2, t2b))
                    if len(ropeq) > 1:
                        rope_fin(ropeq.pop(0))
                else:
                    if kout is not None and cond == 1 and cc >= kout[0]:
                        kft, kfb = kf[kk % 2]
                        kk += 1
                        rows = kout[2]
                        s.op("dve", ins("tensor_copy", out=kft[:, :n], in_=ps[:, :n]), reads=[pb], writes=[kfb])
                        s.op("act", ins("activation", out=qkt[:, cc, :n], in_=kft[:, :n], func=AF.Identity), reads=[kfb], writes=[qkb[cc]])
                        s.dma(kout[1][cc - kout[0], :, t0 - LS:t0 - LS + n], kft[0:rows, :n], reads=[kfb])
                    else:
                        s.op("act", ins("activation", out=qkt[:, cc, :n], in_=ps[:, :n], func=AF.Identity), reads=[pb], writes=[qkb[cc]])
                if post is not None:
                    post(ti, cc, ps, pb)
            while ropeq:
                rope_fin(ropeq.pop(0))
            vtt, vtb = vt[ti % 2]
            for tbk in range(n // 128 if "v" not in KQ else 0):
                for vg in range((nv + 511) // 512):
                    w_ = min(512, nv - vg * 512)
                    ps, pb = s.ps()
                    for kc in range(8):
                        s.mm(ps[:, :w_], h[:, kc, tbk * 128:(tbk + 1) * 128], Wv[:, kc, vg * 512:vg * 512 + w_], kc == 0, kc == 7, reads=[wvb[kc], hb], writes=[pb])
                    if vout is not None and cond == 1:
                        vft, vfb = vf[kv % 2]
                        kv += 1
                        s.op("dve", ins("tensor_copy", out=vft[:, :w_], in_=ps[:, :w_]), reads=[pb], writes=[vfb])
                        s.op("act", ins("activation", out=vtt[:, tbk, vg * 512:vg * 512 + w_], in_=vft[:, :w_], func=AF.Identity),
                             reads=[vfb], writes=[vtb])
                        r0 = t0 - LS + tbk * 128
                        s.dma(vout[r0:r0 + 128, vg * 512:vg * 512 + w_], vft[:, :w_], reads=[vfb])
                    else:
                        s.op("act", ins("activation", out=vtt[:, tbk, vg * 512:vg * 512 + w_], in_=ps[:, :w_], func=AF.Identity),
                             reads=[pb], writes=[vtb])
            if "q" not in KQ:
                s.dma(QK[:, 0:nqk, t0:t0 + n], qkt[:, :, :n], reads=qkb, writes=[QK_b[ti]])
            if "v" not in KQ and "s" not in KQ:
              s.dma(rows_view(VS, 1024, t0, n // 128, 0, nv), vtt[:, 0:n // 128, :], reads=[vtb], writes=[VS_b[ti]])

    def attn_phase(l, kind, Xsrc, Xsrc_b):
        s.barrier()
        s.release()
        win = kind == "win"
        nunits = 8
        kbase = 8
        Kt = [(s.alloc([128, T], BF16, "kt"), Buf()) for _ in range(2)]
        Va = [(s.alloc([128, 36, 128], BF16, "va"), Buf()) for _ in range(2)]
        Qt = [(s.alloc([128, LS], BF16, "qt"), Buf()) for _ in range(2)]
        Ot = [(s.alloc([128, LS], BF16, "ot"), Buf()) for _ in range(2)]
        pT = [(s.alloc([128, 512], BF16, "pT"), Buf()) for _ in range(8)]
        rec = [(s.alloc([128, 512], F32, "rec"), Buf()) for _ in range(8)]
        OS_b = [[Buf() for _ in range(nunits)] for _ in range(3)]
        cbuf = Buf()
        if win:
            masks = s.alloc([128, 6, 512], BF16, "masks")
            s.dma(masks[:], wmask_in, writes=[cbuf], eng="pool")
            esink = s.alloc([128, 16], F32, "esink")
            s.dma(esink[:], sink_in.partition_broadcast(128), writes=[cbuf])
            s.op("act", ins("activation", out=esink[:], in_=esink[:], func=AF.Exp), reads=[cbuf], writes=[cbuf])
            for i in range(2):
                s.op("pool", ins("memset", Va[i][0][:, :, 64:128], 1.0), writes=[Va[i][1]])
        else:
            acc = [(s.alloc([128, 512], F32, "acc"), Buf()) for _ in range(8)]
            osb = [(s.alloc([128, 512], F32, "osb"), Buf()) for _ in range(8)]
            sqd = [(s.alloc([128, 512], BF16, "sqd"), Buf()) for _ in range(4)]
            lq = s.alloc([128, 4, 64], F32, "lq")
            lam = s.alloc([128, 4], F32, "lam")
            gsub = s.alloc([128, 1], F32, "gsub")
            s.dma(lq[:], lqk.rearrange("a b -> (a b)").partition_broadcast(128).rearrange("p (a b) -> p a b", a=4), writes=[cbuf])
            s.dma(gsub[:], w2_gn, writes=[cbuf])
            s.op("dve", ins("tensor_tensor", out=lq[:, 0, :], in0=lq[:, 0, :], in1=lq[:, 1, :], op=ALU.mult), reads=[cbuf], writes=[cbuf])
            s.op("dve", ins("tensor_tensor", out=lq[:, 2, :], in0=lq[:, 2, :], in1=lq[:, 3, :], op=ALU.mult), reads=[cbuf], writes=[cbuf])
            s.op("dve", ins("reduce_sum", out=lam[:, 0:1], in_=lq[:, 0, :], axis=AX.X), reads=[cbuf], writes=[cbuf])
            s.op("dve", ins("reduce_sum", out=lam[:, 1:2], in_=lq[:, 2, :], axis=AX.X), reads=[cbuf], writes=[cbuf])
            s.op("act", ins("activation", out=lam[:, 0:2], in_=lam[:, 0:2], func=AF.Exp), reads=[cbuf], writes=[cbuf])
            s.op("dve", ins("tensor_tensor", out=lam[:, 2:3], in0=lam[:, 1:2], in1=lam[:, 0:1], op=ALU.subtract), reads=[cbuf], writes=[cbuf])
            s.op("dve", ins("tensor_scalar", out=lam[:, 2:3], in0=lam[:, 2:3], scalar1=-LAM_INIT, scalar2=None, op0=ALU.add), reads=[cbuf], writes=[cbuf])
            s.op("dve", ins("tensor_scalar", out=lam[:, 3:4], in0=gsub[:], scalar1=1.0 - LAM_INIT, scalar2=None, op0=ALU.mult), reads=[cbuf], writes=[cbuf])
        ku = 0
        kp = 0
        kr = 0
        ka = 0
        SEQS = [(0, 4096, 0), (4096, 256, 1), (4352, 256, 2)]
        for (q0, L, si) in SEQS:
            sample = si == 0
            nkb_lat = L // 128
            for u in range(nunits):
                kt, ktb = Kt[ku % 2]
                va, vab = Va[ku % 2]
                qt, qtb = Qt[ku % 2]
                ot, otb = Ot[ku % 2]
                ku += 1
                s.dma(qt[:, 0:L], QK[:, u, q0:q0 + L], writes=[qtb])
                if win:
                    g = u // 2
                    s.dma(kt[:, 0:L], QK[:, kbase + g, q0:q0 + L], writes=[ktb])
                    s.dma(va[:, 0:nkb_lat, 0:64], rows_view(VS, 1024, q0, nkb_lat, g * 64, 64), writes=[vab])
                    if sample:
                        s.dma(kt[:, L:L + 512], ck0[:, g, :], writes=[ktb], eng="pool")
                        s.dma(va[:, 32:36, 0:64], rows_view(cv0, 256, 0, 4, g * 64, 64), writes=[vab], eng="pool")
                else:
                    s.dma(kt[:, 0:L], QK[:, kbase + u, q0:q0 + L], writes=[ktb])
                    s.dma(va[:, 0:nkb_lat, :], rows_view(VS, 1024, q0, nkb_lat, u * 128, 128), writes=[vab])
                    if sample:
                        s.dma(kt[:, L:L + 512], ck2[:, u, :], writes=[ktb], eng="pool")
                        s.dma(va[:, 32:36, :], rows_view(cv2, 1024, 0, 4, u * 128, 128), writes=[vab], eng="pool")
                nq = 512 if sample else 256
                LOOK = 4
                tasks = []
                for qi in range(L // nq):
                    for e_ in range(2):
                        if win and sample:
                            kbs = [(kb, kb - qi * 4 + 1) for kb in range(qi * 4 - 1, qi * 4 + 5) if 0 <= kb < 32] + [(32 + j, None) for j in range(4)]
                        elif sample:
                            kbs = [(kb, None) for kb in range(36)]
                        else:
                            kbs = [(kb, None) for kb in range(2)]
                        for i, (kb, mi) in enumerate(kbs):
                            tasks.append((qi, e_, i, kb, mi, i == len(kbs) - 1))
                stt = {}

                def stage1(tk_):
                    nonlocal kp, ka
                    qi, e_, i, kb, mi, lastb = tk_
                    qs = slice(qi * nq, (qi + 1) * nq)
                    pr = slice(e_ * 64, (e_ + 1) * 64)
                    if i == 0:
                        d_ = {}
                        d_["pso"], d_["pob"] = s.ps("b")
                        if not win:
                            d_["acs"] = {"pool": acc[ka % 8], "dve": acc[(ka + 1) % 8]}
                            d_["acn"] = {"pool": 0, "dve": 0}
                            ka += 2
                        stt[(qi, e_)] = d_
                    d_ = stt[(qi, e_)]
                    pss, psb_ = s.ps()
                    s.mm(pss[:, :nq], kt[pr, kb * 128:(kb + 1) * 128], qt[pr, qs], True, True, reads=[ktb, qtb], writes=[psb_], grp=("q", ku, cur_it[0] // 2))
                    p_, p_b = pT[kp % 8]
                    kp += 1
                    s.op("act", ins("activation", out=p_[:, :nq], in_=pss[:, :nq], func=AF.Exp, scale=0.125), reads=[psb_], writes=[p_b])
                    if mi is not None:
                        s.op("pool" if (kp % 2) else "dve", ins("tensor_tensor", out=p_[:, :nq], in0=p_[:, :nq], in1=masks[:, mi, :nq], op=ALU.mult),
                             reads=[p_b, cbuf], writes=[p_b])
                    if not win:
                        eng = "pool" if (i % 8) in (1, 4, 6) else "dve"
                        ac, acb = d_["acs"][eng]
                        if d_["acn"][eng] == 0:
                            s.op(eng, ins("tensor_copy", out=ac[:, :nq], in_=p_[:, :nq]), reads=[p_b], writes=[acb])
                        else:
                            s.op(eng, ins("tensor_tensor", out=ac[:, :nq], in0=ac[:, :nq], in1=p_[:, :nq], op=ALU.add), reads=[p_b, acb], writes=[acb])
                        d_["acn"][eng] += 1
                    d_[("p", i)] = (p_, p_b)

                def stage2(tk_):
                    nonlocal kr
                    qi, e_, i, kb, mi, lastb = tk_
                    qs = slice(qi * nq, (qi + 1) * nq)
                    pr = slice(e_ * 64, (e_ + 1) * 64)
                    d_ = stt[(qi, e_)]
                    pso, pob = d_["pso"], d_["pob"]
                    p_, p_b = d_.pop(("p", i))
                    s.mm(pso[:, :nq], va[:, kb, :], p_[:, :nq], i == 0, lastb, reads=[vab, p_b], writes=[pob], grp=("v", ku, cur_it[0] // 2))
                    if not lastb:
                        return
                    if win:
                        hh = u * 2 + e_
                        r_, r_b = rec[kr % 4]
                        kr += 1
                        s.op("dve", ins("tensor_scalar", out=r_[64:128, :nq], in0=pso[64:128, :nq], scalar1=esink[64:128, hh:hh + 1], scalar2=None, op0=ALU.add),
                             reads=[pob, cbuf], writes=[r_b])
                        s.op("dve", ins("reciprocal", out=r_[64:128, :nq], in_=r_[64:128, :nq]), reads=[r_b], writes=[r_b])
                        s.op("dve", ins("tensor_tensor", out=ot[pr, qs], in0=pso[0:64, :nq], in1=r_[64:128, :nq], op=ALU.mult),
                             reads=[pob, r_b], writes=[otb])
                        del stt[(qi, e_)]
                        return
                    acl = [d_["acs"][e2] for e2 in ("dve", "pool") if d_["acn"][e2] > 0]
                    o_, o_b = osb[kr % 8]
                    r_, r_b = rec[kr % 8]
                    kr += 1
                    s.op("dve", ins("tensor_copy", out=o_[:, :nq], in_=pso[:, :nq]), reads=[pob], writes=[o_b])
                    d_["o"] = (o_, o_b)
                    key = (qi, e_)

                    def st1():
                        psd, pdb = s.ps()
                        for ai, (ac, acb) in enumerate(acl):
                            s.mm(psd[:, :nq], ones_f[:], ac[:, :nq], ai == 0, ai == len(acl) - 1, reads=[acb, cb_const], writes=[pdb])
                        d_["psd"] = (psd, pdb)
                        defer(2, st2)

                    def st2():
                        psd, pdb = d_["psd"]
                        s.op("dve", ins("reciprocal", out=r_[:, :nq], in_=psd[:, :nq]), reads=[pdb], writes=[r_b])
                        defer(4, st3)

                    def st3():
                        s.op("dve", ins("tensor_tensor", out=o_[:, :nq], in0=o_[:, :nq], in1=r_[:, :nq], op=ALU.mult), reads=[o_b, r_b], writes=[o_b])
                        d_["done"] = True
                        if e_ == 1 or stt[(qi, 1)].get("done") if (qi, 1) in stt else False:
                            pass
                        if (qi, 0) in stt and (qi, 1) in stt and stt[(qi, 0)].get("done") and stt[(qi, 1)].get("done"):
                            defer(1, st4)

                    def st4():
                        (o0, o0b) = stt[(qi, 0)]["o"]
                        (o1, o1b) = stt[(qi, 1)]["o"]
                        s.op("dve", ins("scalar_tensor_tensor", out=o0[:, :nq], in0=o1[:, :nq], scalar=lam[:, 2:3], in1=o0[:, :nq], op0=ALU.mult, op1=ALU.add),
                             reads=[o0b, o1b, cbuf], writes=[o0b])
                        sq_, sq_b = sqd[qi % 4]
                        s.op("act", ins("activation", out=sq_[:, :nq], in_=o0[:, :nq], func=AF.Square), reads=[o0b], writes=[sq_b])
                        defer(2, st5)

                    def st5():
                        sq_, sq_b = sqd[qi % 4]
                        psn, pnb = s.ps()
                        s.mm(psn[:, :nq], ones_bf[:], sq_[:, :nq], True, True, reads=[sq_b, cb_const], writes=[pnb])
                        stt[(qi, 1)]["psn"] = (psn, pnb)
                        defer(2, st6)

                    def st6():
                        (o1, o1b) = stt[(qi, 1)]["o"]
                        psn, pnb = stt[(qi, 1)]["psn"]
                        s.op("act", ins("activation", out=o1[:, :nq], in_=psn[:, :nq], func=AF.Sqrt, scale=1.0 / 128, bias=EPSB[:, 0:1]), reads=[pnb, o1b], writes=[o1b])
                        defer(2, st7)

                    def st7():
                        (o1, o1b) = stt[(qi, 1)]["o"]
                        s.op("dve", ins("reciprocal", out=o1[:, :nq], in_=o1[:, :nq]), reads=[o1b], writes=[o1b])
                        defer(4, st8)

                    def st8():
                        (o0, o0b) = stt[(qi, 0)]["o"]
                        (o1, o1b) = stt[(qi, 1)]["o"]
                        s.op("dve", ins("scalar_tensor_tensor", out=ot[:, qs], in0=o0[:, :nq], scalar=lam[:, 3:4], in1=o1[:, :nq], op0=ALU.mult, op1=ALU.mult),
                             reads=[o0b, o1b, cbuf], writes=[otb])
                        del stt[(qi, 0)]
                        del stt[(qi, 1)]

                    defer(2, st1)

                pend = []
                cur_it = [0]

                def defer(dl, fn):
                    pend.append((cur_it[0] + dl, fn))

                def run_pending(flush=False):
                    while True:
                        ready = [p for p in pend if flush or p[0] <= cur_it[0]]
                        if not ready:
                            break
                        for p in ready:
                            pend.remove(p)
                        for due, fn in ready:
                            fn()
                        if not flush:
                            break

                for t2_ in range(0, len(tasks) + LOOK + 1, 2):
                    for t_ in (t2_, t2_ + 1):
                        cur_it[0] = t_
                        if t_ < len(tasks):
                            stage1(tasks[t_])
                    for t_ in (t2_, t2_ + 1):
                        cur_it[0] = t_
                        if LOOK <= t_ < len(tasks) + LOOK:
                            stage2(tasks[t_ - LOOK])
                    run_pending()
                while pend:
                    cur_it[0] += 1
                    run_pending(flush=True)
                s.dma(OS[:, u, q0:q0 + L], ot[:, 0:L], reads=[otb], writes=[OS_b[si][u]])
        return OS_b

    def outproj_phase(l, Wsrc, nk, Osrc, O_bufs_fn, Xsrc, Xsrc_b):
        s.barrier()
        s.release()
        Wo = s.alloc([128, nk, 1024], BF16, "wo")
        wob = load_w(Wo, Wsrc, nk)
        rctx = ResCtx()
        oin = [(s.alloc([128, nk, 512], BF16, "oin"), Buf()) for _ in range(2)]
        for ti in range(NT):
            t0, n, cond, _, _ = TILES[ti]
            o_, o_b = oin[ti % 2]
            s.dma(o_[:, :, :n], Osrc[:, :, t0:t0 + n], writes=[o_b])
            outproj_resid(rctx, l, 2, ti, Wo, wob, nk, o_, [o_b], Xsrc, Xsrc_b, X, X_b)

    def gla_layer(l, Xsrc, Xsrc_b):
        s.barrier()
        s.release()
        Wqk = s.alloc([128, 8, 1024], BF16, "gwqk")
        wqb = load_w(Wqk, w1_qk, 8)
        Wv = s.alloc([128, 8, 1024], BF16, "gwv")
        wvb = load_w(Wv, w1_v, 8)
        Wr = s.alloc([128, 8, 1024], BF16, "gwr")
        wrb = load_w(Wr, w1_r, 8)
        Wg1 = s.alloc([128, 8, 32], BF16, "gwg1")
        wg1b = load_w(Wg1, w1_g1, 8)
        Wg2 = s.alloc([32, 1024], BF16, "gwg2")
        cbuf = Buf()
        s.dma(Wg2[:], w1_g2, writes=[cbuf], eng="pool")
        gb = s.alloc([128, 8], F32, "ggb")
        s.dma(gb[:], w1_gb, writes=[cbuf])
        s.op("dve", ins("tensor_scalar", out=gb[:], in0=gb[:], scalar1=-1.0, scalar2=None, op0=ALU.mult), reads=[cbuf], writes=[cbuf])
        ELt = s.alloc([128, 2, 4, 72], F32, "el")
        nctx = NormCtx(1)
        qkf = [(s.alloc([128, 8, 512], F32, "qkf"), Buf()) for _ in range(1)]
        lg = nctx.xt[0]
        cs_ = (s.alloc([128, 8, 512], F32, "cs"), Buf())
        c2 = nctx.tmp
        ex = [(s.alloc([128, 512], F32, "ex"), Buf()) for _ in range(3)]
        t1b_ = (s.alloc([32, 512], BF16, "t1b"), Buf())
        qd_t = [(s.alloc([128, 2, 4, 512], BF16, "qd"), Buf()) for _ in range(1)]
        kd_t = [(s.alloc([128, 2, 4, 512], BF16, "kd"), Buf()) for _ in range(1)]
        krT = [(s.alloc([128, 512], BF16, "krT"), Buf()) for _ in range(2)]
        krt = [(s.alloc([128, 4, 8, 128], BF16, "krt"), Buf()) for _ in range(1)]
        rt = [(s.alloc([128, 8, 512], BF16, "rt"), Buf()) for _ in range(1)]
        vt = [(s.alloc([128, 4, 1024], BF16, "vt"), Buf()) for _ in range(1)]
        G_b = [Buf() for _ in range(NT)]
        kx = 0
        kk = 0
        for ti in range(NT):
            t0, n, cond, s0, s1 = TILES[ti]
            nch = n // 64
            h, hb = norm_tile(nctx, l, 0, ti, Xsrc, Xsrc_b)
            qf, qfb = qkf[0]
            for cc in range(8):
                ps, pb = s.ps()
                for kc in range(8):
                    s.mm(ps[:, :n], Wqk[:, kc, cc * 128:(cc + 1) * 128], h[:, kc, :n], kc == 0, kc == 7, reads=[wqb[kc], hb], writes=[pb])
                sc_ = (128.0 ** -0.5) if cc < 4 else 1.0
                s.op("act", ins("activation", out=qf[:, cc, :n], in_=ps[:, :n], func=AF.Identity, scale=sc_), reads=[pb], writes=[qfb])
            r_, r_b = rt[0]
            for cc in range(8):
                ps, pb = s.ps()
                for kc in range(8):
                    s.mm(ps[:, :n], Wr[:, kc, cc * 128:(cc + 1) * 128], h[:, kc, :n], kc == 0, kc == 7, reads=[wrb[kc], hb], writes=[pb])
                s.op("act", ins("activation", out=r_[:, cc, :n], in_=ps[:, :n], func=AF.Silu), reads=[pb], writes=[r_b])
            s.dma(RS[:, :, t0:t0 + n], r_[:, :, :n], reads=[r_b], writes=[G_b[ti]])
            v_, v_b = vt[0]
            for tbk in range(n // 128):
                for vg in range(2):
                    ps, pb = s.ps()
                    for kc in range(8):
                        s.mm(ps[:, :512], h[:, kc, tbk * 128:(tbk + 1) * 128], Wv[:, kc, vg * 512:(vg + 1) * 512], kc == 0, kc == 7, reads=[wvb[kc], hb], writes=[pb])
                    s.op("act", ins("activation", out=v_[:, tbk, vg * 512:(vg + 1) * 512], in_=ps[:, :512], func=AF.Identity), reads=[pb], writes=[v_b])
            s.dma(rows_view(VS, 1024, t0, n // 128, 0, 1024), v_[:, 0:n // 128, :], reads=[v_b], writes=[G_b[ti]])
            ps, pb = s.ps()
            for kc in range(8):
                s.mm(ps[0:32, :n], Wg1[:, kc, :], h[:, kc, :n], kc == 0, kc == 7, reads=[wg1b[kc], hb], writes=[pb])
            t1_, t1bb = t1b_
            s.op("act", ins("activation", out=t1_[:, :n], in_=ps[0:32, :n], func=AF.Identity), reads=[pb], writes=[t1bb])
            lgt, lgb = lg
            for j in range(8):
                ps, pb = s.ps()
                s.mm(ps[:, :n], Wg2[:, j * 128:(j + 1) * 128], t1_[:, :n], True, True, reads=[cbuf, t1bb], writes=[pb])
                s.op("act", ins("activation", out=lgt[:, j, :n], in_=ps[:, :n], func=AF.Exp, scale=-1.0, bias=gb[:, j:j + 1]), reads=[pb, cbuf], writes=[lgb])
            s.op("act", ins("activation", out=lgt[:, :, :n], in_=lgt[:, :, :n], func=AF.Ln, bias=1.0, scale=1.0), reads=[lgb], writes=[lgb])
            cst, csb = cs_
            for j in range(8):
                s.op("dve", ins("tensor_tensor_scan", out=cst[:, j, :n], data0=m64[:, :n], data1=lgt[:, j, :n], initial=0.0, op0=ALU.mult, op1=ALU.add),
                     reads=[lgb, cb_const], writes=[csb])
            c2t, c2b = c2
            csv = cst[:, :, :n].rearrange("p j (c t) -> p j c t", t=64)
            lgv = lgt[:, :, :n].rearrange("p j (c t) -> p j c t", t=64)
            c2v = c2t[:, :, :n].rearrange("p j (c t) -> p j c t", t=64)
            nf = 4
            s.op("dve", ins("tensor_tensor", out=c2v[:, 0:nf, :, :], in0=csv[:, 0:nf, :, :], in1=csv[:, 0:nf, :, 63:64].to_broadcast([128, nf, nch, 64]), op=ALU.subtract),
                 reads=[csb], writes=[c2b])
            s.op("dve", ins("tensor_tensor", out=c2v[:, nf:2 * nf, :, :], in0=lgv[:, nf:2 * nf, :, :], in1=csv[:, nf:2 * nf, :, :], op=ALU.subtract),
                 reads=[csb, lgb], writes=[c2b])
            ci0 = t0 // 64
            s.op("act", ins("activation", out=ELt[:, :, :, ci0:ci0 + nch].rearrange("p d h c -> p (d h) c"), in_=cst[:, :, :n].rearrange("p j (c t) -> p j c t", t=64)[:, :, :, 63],
                                               func=AF.Exp, scale=-1.0 / 16), reads=[csb], writes=[cbuf])
            s.op("dve", ins("tensor_tensor", out=lgv[:, nf:2 * nf, :, :], in0=c2v[:, nf:2 * nf, :, :], in1=csv[:, nf:2 * nf, :, 63:64].to_broadcast([128, nf, nch, 64]), op=ALU.add),
                 reads=[csb, c2b, lgb], writes=[lgb])
            qd_, qdb = qd_t[0]
            kd_, kdb = kd_t[0]
            krt_, krtb = krt[0]
            for j in range(8):
                d = j // 4
                hh = j % 4
                ea, eab = ex[0]
                eb, ebb = ex[1]
                ec, ecb = ex[2]
                csrc = cst if j < 4 else lgt
                s.op("act", ins("activation", out=ea[:, :n], in_=csrc[:, j, :n], func=AF.Exp, scale=-1.0 / 16), reads=[csb, lgb], writes=[eab])
                s.op("act", ins("activation", out=eb[:, :n], in_=csrc[:, j, :n], func=AF.Exp, scale=1.0 / 16), reads=[csb, lgb], writes=[ebb])
                s.op("act", ins("activation", out=ec[:, :n], in_=c2t[:, j, :n], func=AF.Exp, scale=1.0 / 16), reads=[c2b], writes=[ecb])
                s.op("dve", ins("tensor_tensor", out=qd_[:, d, hh, :n], in0=qf[:, hh, :n], in1=ea[:, :n], op=ALU.mult), reads=[qfb, eab], writes=[qdb])
                s.op("dve", ins("tensor_tensor", out=kd_[:, d, hh, :n], in0=qf[:, 4 + hh, :n], in1=eb[:, :n], op=ALU.mult), reads=[qfb, ebb], writes=[kdb])
                kT, kTb = krT[kx % 2]
                kx += 1
                s.op("pool", ins("tensor_tensor", out=kT[:, :n], in0=qf[:, 4 + hh, :n], in1=ec[:, :n], op=ALU.mult), reads=[qfb, ecb], writes=[kTb])
                for tbk in range(n // 128):
                    ps, pb = s.ps()
                    psv = ps[:].bitcast(BF16)
                    s.op("pe", ins("transpose", psv[:, 0:128], kT[:, tbk * 128:(tbk + 1) * 128], ident_bf[:]), reads=[kTb, cb_const], writes=[pb])
                    s.op("act", ins("activation", out=krt_[:, tbk, j, :], in_=psv[:, 0:128], func=AF.Identity), reads=[pb], writes=[krtb])
            for d in range(2):
                s.dma(QD[d, :, :, t0:t0 + n], qd_[:, d, :, :n], reads=[qdb], writes=[G_b[ti]])
                s.dma(KD[d, :, :, t0:t0 + n], kd_[:, d, :, :n], reads=[kdb], writes=[G_b[ti]])
            s.dma(kr_view(t0, n // 128, 0, 8), krt_[:, 0:n // 128, :, :], reads=[krtb], writes=[G_b[ti]])
        ELd = dscr("ELd", [128, 2, 4, 72], F32)
        elb = Buf()
        s.dma(ELd, ELt[:], reads=[cbuf], writes=[elb])
        s.barrier()
        s.release()
        EL = s.alloc([128, 2, 4, 72], F32, "el2")
        elb2 = Buf()
        s.dma(EL[:], ELd, writes=[elb2])
        S = [(s.alloc([128, 4, 256], F32, "S"), [Buf() for _ in range(4)]) for _ in range(2)]
        Sb = [(s.alloc([128, 4, 256], BF16, "Sb"), [Buf() for _ in range(4)]) for _ in range(2)]
        qd_t = [[(s.alloc([128, 4, 512], BF16, "qd"), Buf()) for _ in range(2)] for _ in range(2)]
        kd_t = [[(s.alloc([128, 4, 512], BF16, "kd"), Buf()) for _ in range(2)] for _ in range(2)]
        kr_t = [[(s.alloc([128, 4, 4, 128], BF16, "kr"), Buf()) for _ in range(2)] for _ in range(2)]
        v_t = [[(s.alloc([128, 4, 1024], BF16, "v"), Buf()) for _ in range(2)] for _ in range(2)]
        of_t = [[(s.alloc([128, 8, 512], BF16, "of"), Buf()) for _ in range(2)] for _ in range(2)]
        attm = [(s.alloc([128, 64], BF16, "attm"), Buf()) for _ in range(8)]
        OD_b = [[Buf() for _ in range(NT)] for _ in range(2)]
        ODs = [OS, OS2]
        kat = 0
        SEQT = [list(range(8)), [8], [9]]
        states_in = [s1f, s1b]
        states_out = [o_gf, o_gb]
        for si, tiles in enumerate(SEQT):
            for d in range(2):
                St, Stb = S[d]
                Sbt, Sbb = Sb[d]
                if si == 0:
                    s.dma(St[:], states_in[d], writes=Stb)
                else:
                    s.op("pool", ins("memset", St[:], 0.0), writes=Stb)
                for hh in range(4):
                    s.op("act", ins("activation", out=Sbt[:, hh, :], in_=St[:, hh, :], func=AF.Identity), reads=[Stb[hh]], writes=[Sbb[hh]])
            nt_ = len(tiles)
            for step in range(nt_):
                cur = {}
                for d in range(2):
                    ti = tiles[step] if d == 0 else tiles[nt_ - 1 - step]
                    t0, n, cond, s0, s1 = TILES[ti]
                    qd_, qdb = qd_t[d][step % 2]
                    kd_, kdb = kd_t[d][step % 2]
                    kr_, krb = kr_t[d][step % 2]
                    v_, vb_ = v_t[d][step % 2]
                    of_, ofb = of_t[d][step % 2]
                    s.dma(qd_[:, :, :n], QD[d, :, :, t0:t0 + n], writes=[qdb])
                    s.dma(kd_[:, :, :n], KD[d, :, :, t0:t0 + n], writes=[kdb])
                    s.dma(kr_[:, 0:n // 128, :, :], kr_view(t0, n // 128, d * 4, 4), writes=[krb])
                    s.dma(v_[:, 0:n // 128, :], rows_view(VS, 1024, t0, n // 128, 0, 1024), writes=[vb_])
                    cur[d] = (ti, t0, n, qd_, qdb, kd_, kdb, kr_, krb, v_, vb_, of_, ofb)
                nch = cur[0][2] // 64
                for kc_ in range(nch):
                    units = []
                    for d in range(2):
                        ti, t0, n, qd_, qdb, kd_, kdb, kr_, krb, v_, vb_, of_, ofb = cur[d]
                        k = kc_ if d == 0 else nch - 1 - kc_
                        for hh in range(4):
                            units.append(dict(d=d, hh=hh, k=k, ci=t0 // 64 + k, tbk=k // 2, hp=slice((k % 2) * 64, (k % 2) * 64 + 64),
                                              cs64=slice(k * 64, (k + 1) * 64), qd_=qd_, qdb=qdb, kd_=kd_, kdb=kdb, kr_=kr_, krb=krb, v_=v_, vb_=vb_, of_=of_, ofb=ofb))
                    psA, pbA = s.ps()
                    for ui, u_ in enumerate(units):
                        s.mm(psA[0:64, ui * 64:(ui + 1) * 64], u_["kd_"][:, u_["hh"], u_["cs64"]], u_["qd_"][:, u_["hh"], u_["cs64"]], True, True,
                             reads=[u_["kdb"], u_["qdb"]], writes=[pbA], grp=("ga", kat))
                    for ui, u_ in enumerate(units):
                        am, amb = attm[ui]
                        u_["am"] = (am, amb)
                        s.op("dve", ins("tensor_tensor", out=am[u_["hp"], :], in0=psA[0:64, ui * 64:(ui + 1) * 64], in1=tri_bf[0:64, 2 + u_["d"], 0:64], op=ALU.mult),
                             reads=[pbA, cb_const], writes=[amb])
                    psO = [s.ps("b") for _ in range(2)]
                    for ui, u_ in enumerate(units):
                        pso, pob = psO[ui // 4]
                        am, amb = u_["am"]
                        St, Stb = S[u_["d"]]
                        Sbt, Sbb = Sb[u_["d"]]
                        hh, hp, tbk = u_["hh"], u_["hp"], u_["tbk"]
                        c0 = (ui % 4) * 128
                        for vc in range(2):
                            s.mm(pso[:, c0 + vc * 64:c0 + (vc + 1) * 64], u_["v_"][hp, tbk, hh * 256 + vc * 128:hh * 256 + (vc + 1) * 128], am[hp, :], True, False,
                                 reads=[u_["vb_"], amb], writes=[pob])
                            s.mm(pso[:, c0 + vc * 64:c0 + (vc + 1) * 64], Sbt[:, hh, vc * 128:(vc + 1) * 128], u_["qd_"][:, hh, u_["cs64"]], False, True,
                                 reads=[Sbb[hh], u_["qdb"]], writes=[pob])
                    for ui, u_ in enumerate(units):
                        pso, pob = psO[ui // 4]
                        c0 = (ui % 4) * 128
                        hh = u_["hh"]
                        s.op("act", ins("activation", out=u_["of_"][:, hh * 2:hh * 2 + 2, u_["cs64"]], in_=pso[:, c0:c0 + 128].rearrange("p (v t) -> p v t", t=64), func=AF.Identity),
                             reads=[pob], writes=[u_["ofb"]])
                    psS = [s.ps() for _ in range(4)]
                    for ui, u_ in enumerate(units):
                        ps2, pb2 = psS[ui // 2]
                        c0 = (ui % 2) * 256
                        hh, hp, tbk = u_["hh"], u_["hp"], u_["tbk"]
                        s.mm(ps2[:, c0:c0 + 256], u_["kr_"][hp, tbk, hh, :], u_["v_"][hp, tbk, hh * 256:(hh + 1) * 256], True, True, reads=[u_["krb"], u_["vb_"]], writes=[pb2])
                    for ui, u_ in enumerate(units):
                        ps2, pb2 = psS[ui // 2]
                        c0 = (ui % 2) * 256
                        hh, d = u_["hh"], u_["d"]
                        St, Stb = S[d]
                        Sbt, Sbb = Sb[d]
                        ci = u_["ci"]
                        s.op("dve", ins("scalar_tensor_tensor", out=St[:, hh, :], in0=St[:, hh, :], scalar=EL[:, d, hh, ci:ci + 1], in1=ps2[:, c0:c0 + 256], op0=ALU.mult, op1=ALU.add),
                             reads=[pb2, elb2, Stb[hh]], writes=[Stb[hh]])
                        s.op("pool", ins("tensor_copy", out=Sbt[:, hh, :], in_=St[:, hh, :]), reads=[Stb[hh]], writes=[Sbb[hh]])
                    kat += 1
                for d in range(2):
                    ti, t0, n, qd_, qdb, kd_, kdb, kr_, krb, v_, vb_, of_, ofb = cur[d]
                    s.dma(ODs[d][:, :, t0:t0 + n], of_[:, :, :n], reads=[ofb], writes=[OD_b[d][ti]])
            if si > 0:
                for d in range(2):
                    s.dma(states_out[d][si - 1], S[d][0][:], reads=S[d][1])
        s.barrier()
        s.release()
        Wo = s.alloc([128, 8, 1024], BF16, "wo")
        wob = load_w(Wo, w1_o, 8)
        gn = s.alloc([128, 2], F32, "gn")
        gnb = Buf()
        s.dma(gn[:], w1_gn, writes=[gnb])
        rctx = ResCtx()
        oa = [(s.alloc([128, 8, 512], BF16, "oa"), Buf()) for _ in range(2)]
        ob_ = [(s.alloc([128, 8, 512], BF16, "ob"), Buf()) for _ in range(2)]
        rr = [(s.alloc([128, 8, 512], BF16, "rr"), Buf()) for _ in range(2)]
        osum = (s.alloc([128, 8, 512], F32, "osum"), Buf())
        sq = (s.alloc([128, 8, 512], BF16, "sq"), Buf())
        rs_ = [(s.alloc([128, 512], F32, "rs"), Buf()) for _ in range(2)]
        ofin = [(s.alloc([128, 8, 512], BF16, "ofin"), Buf()) for _ in range(2)]
        for ti in range(NT):
            t0, n, cond, _, _ = TILES[ti]
            a_, a_b = oa[ti % 2]
            b_, b_b = ob_[ti % 2]
            r_, r_b = rr[ti % 2]
            s.dma(a_[:, :, :n], OS[:, :, t0:t0 + n], writes=[a_b])
            s.dma(b_[:, :, :n], OS2[:, :, t0:t0 + n], writes=[b_b])
            s.dma(r_[:, :, :n], RS[:, :, t0:t0 + n], writes=[r_b])
            os_, osb_ = osum
            sq_, sqb_ = sq
            s.op("dve", ins("tensor_tensor", out=os_[:, :, :n], in0=a_[:, :, :n], in1=b_[:, :, :n], op=ALU.add), reads=[a_b, b_b], writes=[osb_])
            s.op("act", ins("activation", out=sq_[:, :, :n], in_=os_[:, :, :n], func=AF.Square), reads=[osb_], writes=[sqb_])
            f_, f_b = ofin[ti % 2]
            for hh in range(4):
                ps, pb = s.ps()
                for vc in range(2):
                    s.mm(ps[:, :n], ones_bf[:], sq_[:, hh * 2 + vc, :n], vc == 0, vc == 1, reads=[sqb_, cb_const], writes=[pb])
                rt_, rtb = rs_[hh % 2]
                s.op("act", ins("activation", out=rt_[:, :n], in_=ps[:, :n], func=AF.Sqrt, scale=1.0 / 256, bias=EPSB[:, 0:1]), reads=[pb], writes=[rtb])
                s.op("dve", ins("reciprocal", out=rt_[:, :n], in_=rt_[:, :n]), reads=[rtb], writes=[rtb])
                for vc in range(2):
                    c = hh * 2 + vc
                    s.op("dve", ins("tensor_tensor", out=os_[:, c, :n], in0=os_[:, c, :n], in1=rt_[:, :n], op=ALU.mult), reads=[osb_, rtb], writes=[osb_])
                    s.op("dve", ins("scalar_tensor_tensor", out=f_[:, c, :n], in0=os_[:, c, :n], scalar=gn[:, vc:vc + 1], in1=r_[:, c, :n], op0=ALU.mult, op1=ALU.mult),
                         reads=[osb_, gnb, r_b], writes=[f_b])
            outproj_resid(rctx, l, 2, ti, Wo, wob, 8, f_, [f_b], Xsrc, Xsrc_b, X, X_b)

    def ssd_layer(l, Xsrc, Xsrc_b):
        XBu = U
        s.barrier()
        s.release()
        Wz = s.alloc([128, 8, 2048], BF16, "wz")
        wzb = load_w(Wz, w3_z, 8)
        Wx = s.alloc([128, 8, 3072], BF16, "wx")
        wxb = load_w(Wx, w3_xbc, 8)
        Wdt = s.alloc([128, 8, 64], BF16, "wdt")
        wdb = load_w(Wdt, w3_dt, 8)
        vec = s.alloc([128, 5, 32], F32, "vec")
        cbuf = Buf()
        s.dma(vec[:], ssd_vec.rearrange("a b -> (a b)").partition_broadcast(128).rearrange("p (a b) -> p a b", a=5), writes=[cbuf])
        avec = s.alloc([128, 64], F32, "avec")
        s.op("act", ins("activation", out=avec[:], in_=vec[:, 0:2, :].rearrange("p a b -> p (a b)"), func=AF.Exp), reads=[cbuf], writes=[cbuf])
        s.op("dve", ins("tensor_scalar", out=avec[:], in0=avec[:], scalar1=-1.0, scalar2=None, op0=ALU.mult), reads=[cbuf], writes=[cbuf])
        nctx = NormCtx(1)
        xbt = (s.alloc([128, 24, 512], BF16, "xbt"), [Buf() for _ in range(24)])
        zt = (s.alloc([128, 4, 2048], BF16, "zt"), Buf())
        dtt = (s.alloc([128, 4, 2, 64], F32, "dtt"), Buf())
        dtmp = (s.alloc([128, 64], F32, "dtmp"), Buf())
        S_b = [Buf() for _ in range(NT)]
        ev = 0
        for ti in range(NT):
            t0, n, cond, s0, s1 = TILES[ti]
            nb = n // 128
            h, hb = norm_tile(nctx, l, 0, ti, Xsrc, Xsrc_b)
            xb_, xbb = xbt
            for cc in range(24):
                ps, pb = s.ps()
                for kc in range(8):
                    s.mm(ps[:, :n], Wx[:, kc, cc * 128:(cc + 1) * 128], h[:, kc, :n], kc == 0, kc == 7, reads=[wxb[kc], hb], writes=[pb])
                if ev % 2 == 0:
                    s.op("act", ins("activation", out=xb_[:, cc, :n], in_=ps[:, :n], func=AF.Identity), reads=[pb], writes=[xbb[cc]])
                else:
                    s.op("dve", ins("tensor_copy", out=xb_[:, cc, :n], in_=ps[:, :n]), reads=[pb], writes=[xbb[cc]])
                ev += 1
            s.dma(XBu[:, 0:24, t0:t0 + n], xb_[:, :, :n], reads=xbb, writes=[S_b[ti]])
            z_, z_b = zt
            d_, d_b = dtt
            for tbk in range(nb):
                for vg in range(4):
                    ps, pb = s.ps()
                    for kc in range(8):
                        s.mm(ps[:, :512], h[:, kc, tbk * 128:(tbk + 1) * 128], Wz[:, kc, vg * 512:(vg + 1) * 512], kc == 0, kc == 7, reads=[wzb[kc], hb], writes=[pb])
                    s.op("act", ins("activation", out=z_[:, tbk, vg * 512:(vg + 1) * 512], in_=ps[:, :512], func=AF.Silu), reads=[pb], writes=[z_b])
                ps, pb = s.ps()
                for kc in range(8):
                    s.mm(ps[:, 0:64], h[:, kc, tbk * 128:(tbk + 1) * 128], Wdt[:, kc, :], kc == 0, kc == 7, reads=[wdb[kc], hb], writes=[pb])
                tm, tmb = dtmp
                s.op("dve", ins("tensor_tensor", out=tm[:], in0=ps[:, 0:64], in1=vec[:, 2:4, :].rearrange("p a b -> p (a b)"), op=ALU.add), reads=[pb, cbuf], writes=[tmb])
                s.op("act", ins("activation", out=tm[:], in_=tm[:], func=AF.Exp), reads=[tmb], writes=[tmb])
                s.op("act", ins("activation", out=d_[:, tbk, 0, :], in_=tm[:], func=AF.Ln, bias=1.0, scale=1.0), reads=[tmb], writes=[d_b])
                s.op("dve", ins("tensor_tensor", out=d_[:, tbk, 1, :], in0=d_[:, tbk, 0, :], in1=avec[:], op=ALU.mult), reads=[d_b, cbuf], writes=[d_b])
            s.dma(rows_view(ZS, 2048, t0, nb, 0, 2048), z_[:, 0:nb, :], reads=[z_b], writes=[S_b[ti]])
            s.dma(AP(DTS.tensor, t0 * 128, [[128, 128], [128 * 128, nb], [1, 128]]), d_[:, 0:nb, :, :].rearrange("p b a c -> p b (a c)"), reads=[d_b], writes=[S_b[ti]])
        chk(20)
        s.barrier()
        s.release()
        cw = s.alloc([128, 3, 24], F32, "scw")
        cbias = s.alloc([128, 24], F32, "scb")
        cwb = Buf()
        s.dma(cw[:], ssd_cw, writes=[cwb])
        s.dma(cbias[:], ssd_cb, writes=[cwb])
        ug = s.alloc([128, 24, 514], BF16, "sug")
        ugb = [Buf() for _ in range(2)]
        xc = (s.alloc([128, 24, 512], BF16, "xc"), [Buf() for _ in range(24)])
        tmps = [(s.alloc([128, 512], F32, "st"), Buf()) for _ in range(2)]
        xtt = (s.alloc([128, 4, 2048], BF16, "xtt"), Buf())
        btt = (s.alloc([128, 4, 512], BF16, "btt"), Buf())
        kp = 0
        for ti in range(NT):
            t0, n, cond, s0, s1 = TILES[ti]
            nb = n // 128
            lo = (t0 - 1) >= s0
            hi = (t0 + n) < s1
            for grp in range(2):
                gs_ = slice(grp * 12, (grp + 1) * 12)
                if not lo:
                    s.op("pool", ins("memset", ug[:, gs_, 0:1], 0.0), writes=[ugb[grp]])
                if not hi:
                    s.op("pool", ins("memset", ug[:, gs_, n + 1:n + 2], 0.0), writes=[ugb[grp]])
                c0 = 0 if lo else 1
                c1 = n + 2 if hi else n + 1
                s.dma(ug[:, gs_, c0:c1], XBu[:, gs_, t0 - 1 + c0:t0 - 1 + c1], writes=[ugb[grp]])
            xc_, xcb = xc
            for col in range(24):
                grp = col // 12
                tt, ttb = tmps[kp % 2]
                kp += 1
                s.op("act", ins("activation", out=tt[:, :n], in_=ug[:, col, 1:n + 1], func=AF.Identity, scale=cw[:, 1, col:col + 1], bias=cbias[:, col:col + 1]),
                     reads=[ugb[grp], cwb], writes=[ttb])
                s.op("dve", ins("scalar_tensor_tensor", out=tt[:, :n], in0=ug[:, col, 0:n], scalar=cw[:, 0, col:col + 1], in1=tt[:, :n], op0=ALU.mult, op1=ALU.add),
                     reads=[ugb[grp], cwb, ttb], writes=[ttb])
                s.op("dve", ins("scalar_tensor_tensor", out=tt[:, :n], in0=ug[:, col, 2:n + 2], scalar=cw[:, 2, col:col + 1], in1=tt[:, :n], op0=ALU.mult, op1=ALU.add),
                     reads=[ugb[grp], cwb, ttb], writes=[ttb])
                s.op("act", ins("activation", out=xc_[:, col, :n], in_=tt[:, :n], func=AF.Silu), reads=[ttb], writes=[xcb[col]])
            s.dma(QK[:, 0:8, t0:t0 + n], xc_[:, 16:24, :n], reads=xcb[16:24], writes=[S_b[ti]])
            x_, x_b = xtt
            b_, b_b = btt
            tv = 0
            for tbk in range(nb):
                for col in range(20):
                    ps, pb = s.ps()
                    psv = ps[:].bitcast(BF16)
                    s.op("pe", ins("transpose", psv[:, 0:128], xc_[:, col, tbk * 128:(tbk + 1) * 128], ident_bf[:]), reads=[xcb[col], cb_const], writes=[pb])
                    dst = x_[:, tbk, col * 128:(col + 1) * 128] if col < 16 else b_[:, tbk, (col - 16) * 128:(col - 15) * 128]
                    dbuf = x_b if col < 16 else b_b
                    if tv % 2 == 0:
                        s.op("act", ins("activation", out=dst, in_=psv[:, 0:128], func=AF.Identity), reads=[pb], writes=[dbuf])
                    else:
                        s.op("dve", ins("tensor_copy", out=dst, in_=psv[:, 0:128]), reads=[pb], writes=[dbuf])
                    tv += 1
            s.dma(rows_view(XTs, 2048, t0, nb, 0, 2048), x_[:, 0:nb, :], reads=[x_b], writes=[S_b[ti]])
            s.dma(rows_view(BTs, 512, t0, nb, 0, 512), b_[:, 0:nb, :], reads=[b_b], writes=[S_b[ti]])
        chk(21)
        s.barrier()
        s.release()
        vec = s.alloc([128, 5, 32], F32, "vec")
        negm = s.alloc([128, 2, 128], F32, "negm")
        cbuf = Buf()
        s.op("dve", ins("tensor_scalar", out=negm[:], in0=stage[:, 0:2, :], scalar1=-1.0, scalar2=30000.0, op0=ALU.add, op1=ALU.mult), reads=[bstage], writes=[cbuf])
        ST = [(s.alloc([128, 2048], F32, "ST"), [Buf() for _ in range(4)]) for _ in range(2)]
        SbT = [(s.alloc([128, 2048], BF16, "SbT"), [Buf() for _ in range(4)]) for _ in range(2)]
        bct = [(s.alloc([128, 8, 512], BF16, "bct"), Buf()) for _ in range(2)]
        xts = [(s.alloc([128, 4, 2048], BF16, "xts"), Buf()) for _ in range(2)]
        bts = [(s.alloc([128, 4, 512], BF16, "bts"), Buf()) for _ in range(2)]
        dts = [(s.alloc([128, 4, 2, 64], F32, "dts"), Buf()) for _ in range(2)]
        yts = [(s.alloc([128, 4, 2048], BF16, "yts"), Buf()) for _ in range(2)]
        cumT = [(s.alloc([128, 32], F32, "cumT"), Buf()) for _ in range(2)]
        cbT = [(s.alloc([128, 128], F32, "cbT"), Buf()) for _ in range(6)]
        lat = [(s.alloc([128, 4, 128], F32, "lat"), Buf()) for _ in range(3)]
        cB = [(s.alloc([128, 4, 128], F32, "cB"), Buf()) for _ in range(4)]
        seg = [(s.alloc([128, 4, 128], F32, "seg"), Buf()) for _ in range(3)]
        Et = [(s.alloc([128, 4, 128], F32, "Et"), Buf()) for _ in range(6)]
        ecB = [(s.alloc([128, 4, 128], F32, "ecB"), Buf()) for _ in range(8)]
        te = [(s.alloc([128, 4], F32, "te"), Buf()) for _ in range(3)]
        xs = [(s.alloc([128, 512], BF16, "xs"), Buf()) for _ in range(4)]
        Wt = [(s.alloc([128, 128], BF16, "Wt"), Buf()) for _ in range(8)]
        CEt = [(s.alloc([128, 128], BF16, "CEt"), Buf()) for _ in range(8)]
        SEQT = [list(range(8)), [8], [9]]
        st_in = [s3f, s3b]
        st_out = [o_sf, o_sb]
        Y_b = [Buf() for _ in range(NT)]
        kw = 0
        kq4 = 0
        kg = 0
        kct = 0
        for si, tiles in enumerate(SEQT):
            for d in range(2):
                St, Stb = ST[d]
                Sbt, Sbb = SbT[d]
                if si == 0:
                    s.dma(St[:], st_in[d], writes=Stb)
                else:
                    s.op("pool", ins("memset", St[:], 0.0), writes=Stb)
                for g in range(4):
                    s.op("act", ins("activation", out=Sbt[:, g * 512:(g + 1) * 512], in_=St[:, g * 512:(g + 1) * 512], func=AF.Identity), reads=[Stb[g]], writes=[Sbb[g]])
            nt_ = len(tiles)
            for step in range(nt_):
                cur = {}
                for d in range(2):
                    ti = tiles[step] if d == 0 else tiles[nt_ - 1 - step]
                    t0, n, cond, s0, s1 = TILES[ti]
                    nb = n // 128
                    bc_, bcb = bct[d]
                    x_, x_b = xts[d]
                    b_, b_b = bts[d]
                    d_, d_b = dts[d]
                    y_, y_b = yts[d]
                    s.dma(bc_[:, :, :n], QK[:, 0:8, t0:t0 + n], writes=[bcb])
                    s.dma(x_[:, 0:nb, :], rows_view(XTs, 2048, t0, nb, 0, 2048), writes=[x_b])
                    s.dma(b_[:, 0:nb, :], rows_view(BTs, 512, t0, nb, 0, 512), writes=[b_b])
                    s.dma(d_[:, 0:nb, :, :].rearrange("p b a c -> p b (a c)"), AP(DTS.tensor, t0 * 128, [[128, 128], [128 * 128, nb], [1, 128]]), writes=[d_b])
                    cur[d] = (ti, t0, n, nb)
                nbs = cur[0][3]
                batches = []
                for kc_ in range(nbs):
                    for d in range(2):
                        for g in range(4):
                            for hb_ in range(2):
                                batches.append((kc_, d, g, hb_))
                bst = {}
                LOOKB = 3

                def geo(bt_):
                    kc_, d, g, hb_ = bt_
                    ti, t0, n, nb = cur[d]
                    tbk = kc_ if d == 0 else nb - 1 - kc_
                    last = 127 if d == 0 else 0
                    d_, d_b = dts[d]
                    la = d_[:, tbk, 1, d * 32:(d + 1) * 32]
                    dtv = d_[:, tbk, 0, d * 32:(d + 1) * 32]
                    return tbk, last, slice(tbk * 128, (tbk + 1) * 128), la, dtv, d_b

                def stageA1(bt_):
                    nonlocal kq4, kg, kct
                    kc_, d, g, hb_ = bt_
                    tbk, last, tk, la, dtv, d_b = geo(bt_)
                    bc_, bcb = bct[d]
                    if g == 0 and hb_ == 0:
                        psc, pcb = s.ps()
                        s.mm(psc[:, 0:32], stage[:, d, :], la, True, True, reads=[bstage, d_b], writes=[pcb])
                        cT, cTb = cumT[kct % 2]
                        kct += 1
                        s.op("dve", ins("tensor_copy", out=cT[:], in_=psc[:, 0:32]), reads=[pcb], writes=[cTb])
                        bst[("cT", kc_, d)] = (cT, cTb)
                    if hb_ == 0:
                        ps, pb = s.ps()
                        s.mm(ps[:, 0:128], bc_[:, g, tk], bc_[:, 4 + g, tk], True, True, reads=[bcb], writes=[pb])
                        cb_, cb_b = cbT[kg % 6]
                        s.op("act", ins("activation", out=cb_[:], in_=ps[:, 0:128], func=AF.Identity), reads=[pb], writes=[cb_b])
                        bst[("grp", kc_, d, g)] = dict(cb=(cb_, cb_b), x4=xs[kg % 4], ecs=[])
                        kg += 1
                    h0 = g * 8 + hb_ * 4
                    lt, ltb = lat[kq4 % 3]
                    cb4, cb4b = cB[kq4 % 4]
                    sg, sgb = seg[kq4 % 3]
                    E_, E_b = Et[kq4 % 6]
                    ec, ecb = ecB[kq4 % 8]
                    te_, te_b = te[kq4 % 3]
                    kq4 += 1
                    bst[("w",) + bt_] = (cb4, cb4b, sg, sgb, E_, E_b, ec, ecb, te_, te_b)
                    s.op("pool", ins("tensor_tensor", out=lt[:], in0=stage[:, d, :].unsqueeze(1).to_broadcast([128, 4, 128]),
                                    in1=la[:, h0:h0 + 4].unsqueeze(2).to_broadcast([128, 4, 128]), op=ALU.mult),
                         reads=[bstage, d_b], writes=[ltb])
                    ps2, pb2 = s.ps()
                    s.mm(ps2[:, 0:512], ones_f[:], lt[:].rearrange("p a b -> p (a b)"), True, True, reads=[ltb, cb_const], writes=[pb2])
                    s.op("act", ins("activation", out=cb4[:].rearrange("p a b -> p (a b)"), in_=ps2[:, 0:512], func=AF.Identity), reads=[pb2], writes=[cb4b])

                def stageA2a(bt_):
                    kc_, d, g, hb_ = bt_
                    cT, cTb = bst[("cT", kc_, d)]
                    G = bst[("grp", kc_, d, g)]
                    cb4, cb4b, sg, sgb, E_, E_b, ec, ecb, te_, te_b = bst[("w",) + bt_]
                    h0 = g * 8 + hb_ * 4
                    G["ecs"].append((ec, ecb))
                    for hh in range(4):
                        hq = h0 + hh
                        s.op("dve", ins("scalar_tensor_tensor", out=sg[:, hh, :], in0=cb4[:, hh, :], scalar=cT[:, hq:hq + 1], in1=negm[:, d, :],
                                        op0=ALU.subtract, op1=ALU.add),
                             reads=[cb4b, cTb, cbuf], writes=[sgb])
                    s.op("act", ins("activation", out=E_[:], in_=sg[:], func=AF.Exp), reads=[sgb], writes=[E_b])
                    s.op("act", ins("activation", out=ec[:], in_=cb4[:], func=AF.Exp), reads=[cb4b], writes=[ecb])

                def stageA2b(bt_):
                    kc_, d, g, hb_ = bt_
                    tbk, last, tk, la, dtv, d_b = geo(bt_)
                    x_, x_b = xts[d]
                    G = bst[("grp", kc_, d, g)]
                    x4, x4b = G["x4"]
                    cb4, cb4b, sg, sgb, E_, E_b, ec, ecb, te_, te_b = bst.pop(("w",) + bt_)
                    h0 = g * 8 + hb_ * 4
                    s.op("pool", ins("tensor_tensor", out=te_[:], in0=E_[:, :, last], in1=dtv[:, h0:h0 + 4], op=ALU.mult),
                         reads=[E_b, d_b], writes=[te_b])
                    s.op("pool", ins("tensor_tensor",
                                    out=x4[:, hb_ * 256:(hb_ + 1) * 256].rearrange("p (a b) -> p a b", b=64),
                                    in0=x_[:, tbk, h0 * 64:(h0 + 4) * 64].rearrange("p (a b) -> p a b", b=64),
                                    in1=te_[:].unsqueeze(2).to_broadcast([128, 4, 64]), op=ALU.mult),
                         reads=[x_b, te_b], writes=[x4b])
                    bst[("bat",) + bt_] = (E_, E_b, ec, ecb)

                def stageB(bt_):
                    nonlocal kw
                    kc_, d, g, hb_ = bt_
                    tbk, last, tk, la, dtv, d_b = geo(bt_)
                    bc_, bcb = bct[d]
                    x_, x_b = xts[d]
                    b_, b_b = bts[d]
                    y_, y_b = yts[d]
                    St, Stb = ST[d]
                    Sbt, Sbb = SbT[d]
                    G = bst[("grp", kc_, d, g)]
                    cb_, cb_b = G["cb"]
                    x4, x4b = G["x4"]
                    if hb_ == 0:
                        G["psy"] = s.ps("b")
                    psy, pyb = G["psy"]
                    E_, E_b, ec, ecb = bst.pop(("bat",) + bt_)
                    h0 = g * 8 + hb_ * 4
                    for hh in range(4):
                        hq = h0 + hh
                        W_, W_b = Wt[kw % 8]
                        C_, C_b = CEt[kw % 8]
                        kw += 1
                        s.op("dve", ins("scalar_tensor_tensor", out=W_[:], in0=E_[:, hh, :], scalar=dtv[:, hq:hq + 1], in1=cb_[:], op0=ALU.mult, op1=ALU.mult),
                             reads=[E_b, d_b, cb_b], writes=[W_b])
                        s.op("pool", ins("tensor_tensor", out=C_[:], in0=bc_[:, 4 + g, tk], in1=ec[:, hh, :], op=ALU.mult),
                             reads=[bcb, ecb], writes=[C_b])
                        ycol = slice((hq % 8) * 64, (hq % 8 + 1) * 64)
                        s.mm(psy[:, ycol], W_[:], x_[:, tbk, hq * 64:(hq + 1) * 64], True, False, reads=[W_b, x_b], writes=[pyb])
                        s.mm(psy[:, ycol], C_[:], Sbt[:, hq * 64:(hq + 1) * 64], False, True, reads=[C_b, Sbb[g]], writes=[pyb])
                    if hb_ == 0:
                        return
                    s.op("act", ins("activation", out=y_[:, tbk, g * 512:(g + 1) * 512], in_=psy[:, 0:512], func=AF.Identity), reads=[pyb], writes=[y_b])
                    pss, psb_ = s.ps()
                    s.mm(pss[:, 0:512], b_[:, tbk, g * 128:(g + 1) * 128], x4[:], True, True, reads=[b_b, x4b], writes=[psb_])
                    ecs = G["ecs"]
                    for hh in range(8):
                        hq = g * 8 + hh
                        ec2, ecb2 = ecs[hh // 4]
                        s.op("dve", ins("scalar_tensor_tensor",
                                        out=St[:, hq * 64:(hq + 1) * 64], in0=St[:, hq * 64:(hq + 1) * 64], scalar=ec2[:, hh % 4, last:last + 1], in1=pss[:, hh * 64:(hh + 1) * 64], op0=ALU.mult, op1=ALU.add),
                             reads=[psb_, ecb2, Stb[g]], writes=[Stb[g]])
                    s.op("act", ins("activation", out=Sbt[:, g * 512:(g + 1) * 512], in_=St[:, g * 512:(g + 1) * 512], func=AF.Identity), reads=[Stb[g]], writes=[Sbb[g]])
                    del bst[("grp", kc_, d, g)]

                NB_ = len(batches)
                for t_ in range(NB_ + 3):
                    if t_ < NB_:
                        stageA1(batches[t_])
                    if 1 <= t_ < NB_ + 1:
                        stageA2a(batches[t_ - 1])
                    if 2 <= t_ < NB_ + 2:
                        stageA2b(batches[t_ - 2])
                    if t_ >= 3:
                        stageB(batches[t_ - 3])
                for d in range(2):
                    ti, t0, n, nb = cur[d]
                    y_, y_b = yts[d]
                    s.dma(rows_view(YD[d], 2048, t0, nb, 0, 2048), y_[:, 0:nb, :], reads=[y_b], writes=[Y_b[ti]])
            if si > 0:
                for d in range(2):
                    s.dma(st_out[d][si - 1], ST[d][0][:], reads=ST[d][1])
        chk(22)
        s.barrier()
        s.release()
        Wo = s.alloc([128, 16, 1024], BF16, "wo")
        wob = load_w(Wo, w3_o, 16)
        vec = s.alloc([128, 5, 32], F32, "vec")
        gnb = s.alloc([128, 2048], F32, "gnb")
        cbuf = Buf()
        s.dma(vec[:], ssd_vec.rearrange("a b -> (a b)").partition_broadcast(128).rearrange("p (a b) -> p a b", a=5), writes=[cbuf])
        s.dma(gnb[:], ssd_gn.partition_broadcast(128), writes=[cbuf])
        rctx = ResCtx()
        yf = (s.alloc([128, 4, 2048], BF16, "yf"), Buf())
        yb = (s.alloc([128, 4, 2048], BF16, "yb"), Buf())
        xt_ = (s.alloc([128, 4, 2048], BF16, "xt4"), Buf())
        zs = (s.alloc([128, 4, 2048], BF16, "zs"), Buf())
        yv = [(s.alloc([128, 2048], F32, "yv"), Buf()) for _ in range(2)]
        tv_ = (s.alloc([128, 2048], F32, "tv"), Buf())
        ynb = (s.alloc([128, 2048], BF16, "ynb"), Buf())
        ssq = [(s.alloc([128, 1], F32, "ssq"), Buf()) for _ in range(2)]
        oT = [(s.alloc([128, 16, 512], BF16, "oT"), Buf()) for _ in range(2)]
        kk = 0
        tv = 0
        for ti in range(NT):
            t0, n, cond, _, _ = TILES[ti]
            nb = n // 128
            s.dma(yf[0][:, 0:nb, :], rows_view(YD[0], 2048, t0, nb, 0, 2048), writes=[yf[1]])
            s.dma(yb[0][:, 0:nb, :], rows_view(YD[1], 2048, t0, nb, 0, 2048), writes=[yb[1]])
            s.dma(xt_[0][:, 0:nb, :], rows_view(XTs, 2048, t0, nb, 0, 2048), writes=[xt_[1]])
            s.dma(zs[0][:, 0:nb, :], rows_view(ZS, 2048, t0, nb, 0, 2048), writes=[zs[1]])
            o_, o_b = oT[ti % 2]
            for tbk in range(nb):
                y_, y_b = yv[kk % 2]
                sq_, sq_b = ssq[kk % 2]
                kk += 1
                t_, t_b = tv_
                s.op("dve", ins("tensor_tensor", out=y_[:], in0=yf[0][:, tbk, :], in1=yb[0][:, tbk, :], op=ALU.add), reads=[yf[1], yb[1]], writes=[y_b])
                s.op("pool", ins("tensor_tensor", out=t_[:].rearrange("p (a b) -> p a b", b=64), in0=xt_[0][:, tbk, :].rearrange("p (a b) -> p a b", b=64),
                                                              in1=vec[:, 4, :].unsqueeze(2).to_broadcast([128, 32, 64]), op=ALU.mult), reads=[xt_[1], cbuf], writes=[t_b])
                s.op("dve", ins("tensor_tensor", out=y_[:], in0=y_[:], in1=t_[:], op=ALU.add), reads=[y_b, t_b], writes=[y_b])
                s.op("dve", ins("tensor_tensor", out=y_[:], in0=y_[:], in1=zs[0][:, tbk, :], op=ALU.mult), reads=[y_b, zs[1]], writes=[y_b])
                s.op("pool", ins("memset", sq_[:], 0.0), writes=[sq_b])
                s.op("act", ins("activation", out=t_[:], in_=y_[:], func=AF.Square, accum_out=sq_[:, 0:1]), reads=[y_b, sq_b], writes=[t_b, sq_b])
                s.op("act", ins("activation", out=sq_[:], in_=sq_[:], func=AF.Sqrt, scale=1.0 / 2048, bias=EPSB[:, 0:1]), reads=[sq_b], writes=[sq_b])
                s.op("dve", ins("reciprocal", out=sq_[:], in_=sq_[:]), reads=[sq_b], writes=[sq_b])
                yn, ynb_ = ynb
                s.op("dve", ins("scalar_tensor_tensor", out=yn[:], in0=y_[:], scalar=sq_[:, 0:1], in1=gnb[:], op0=ALU.mult, op1=ALU.mult), reads=[y_b, sq_b, cbuf], writes=[ynb_])
                for c in range(16):
                    ps, pb = s.ps()
                    psv = ps[:].bitcast(BF16)
                    s.op("pe", ins("transpose", psv[:, 0:128], yn[:, c * 128:(c + 1) * 128], ident_bf[:]), reads=[ynb_, cb_const], writes=[pb])
                    if tv % 2 == 0:
                        s.op("act", ins("activation", out=o_[:, c, tbk * 128:(tbk + 1) * 128], in_=psv[:, 0:128], func=AF.Identity), reads=[pb], writes=[o_b])
                    else:
                        s.op("dve", ins("tensor_copy", out=o_[:, c, tbk * 128:(tbk + 1) * 128], in_=psv[:, 0:128]), reads=[pb], writes=[o_b])
                    tv += 1
            outproj_resid(rctx, l, 2, ti, Wo, wob, 16, o_, [o_b], Xsrc, Xsrc_b, X, X_b)

    import os
    STOP = int(os.environ.get("KSTOP", "99"))

    class _Stop(Exception):
        pass

    def chk(k):
        if STOP == k:
            raise _Stop()

    try:
        prologue()
        chk(0)
        Xsrc, Xsrc_b = xT, xT_b
        for l in range(NLAYERS):
            kind = l % 4
            if kind == 0:
                qkv_phase(l, w0_qk, 12, w0_v, 256, Xsrc, Xsrc_b, True, kout=(8, o_wk, 64), vout=o_wv)
                chk(1)
                attn_phase(l, "win", Xsrc, Xsrc_b)
                chk(2)
                outproj_phase(l, w0_o, 8, OS, None, Xsrc, Xsrc_b)
                chk(3)
            elif kind == 1:
                gla_layer(l, Xsrc, Xsrc_b)
            elif kind == 2:
                qkv_phase(l, w2_qk, 16, w2_v, 1024, Xsrc, Xsrc_b, True, kout=(8, o_dk, 128), vout=o_dv)
                attn_phase(l, "diff", Xsrc, Xsrc_b)
                outproj_phase(l, w2_o, 8, OS, None, Xsrc, Xsrc_b)
            else:
                ssd_layer(l, Xsrc, Xsrc_b)
            Xsrc, Xsrc_b = X, X_b
            ffn(l, Xsrc, Xsrc_b)
            chk(10 + l)
        s.barrier()
        s.release()
        nctx = NormCtx()
        for ti in range(NT):
            norm_tile(nctx, 0, 0, ti, Xsrc, Xsrc_b, final_out=yT)
    except _Stop:
        pass
    s.barrier()
    counts = s.emit()
    return nc, counts

def _fm(w):
    K, N = w.shape
    return np.ascontiguousarray(w.reshape(K // 128, 128, N).transpose(1, 0, 2))


def _pc(v):
    return np.ascontiguousarray(v.reshape(-1, 128).T)


def _consts():
    f32 = np.float32
    t = np.arange(LS)
    row = (t // 64).astype(f32)
    col = (t % 64).astype(f32)
    inv = (10000.0 ** (-np.arange(0, 32, 2, dtype=f32) / 32)).astype(f32)
    cos = np.zeros((128, LS), f32)
    sin = np.zeros((128, LS), f32)
    perm = np.zeros((128, 128), f32)
    for p in range(128):
        d = p % 64
        axis = d // 32
        idx = d % 16
        second = (d % 32) >= 16
        ang = (row if axis == 0 else col) * inv[idx]
        cos[p] = np.cos(ang)
        sin[p] = np.sin(ang) * (1.0 if second else -1.0)
        partner = p + 16 if not second else p - 16
        perm[partner, p] = 1.0
    j = np.arange(128)[:, None]
    i = np.arange(512)[None, :]
    wmask = np.zeros((128, 6, 512), f32)
    for mi in range(6):
        o = mi - 1
        wmask[:, mi, :] = (np.abs(i - (o * 128 + j)) <= 128).astype(f32)
    tri = np.zeros((128, 4, 128), f32)
    jj = np.arange(128)[:, None]
    ii = np.arange(128)[None, :]
    tri[:, 0, :] = (jj <= ii)
    tri[:, 1, :] = (jj >= ii)
    tri[0:64, 2, 0:64] = (jj[0:64] <= ii[:, 0:64])
    tri[0:64, 3, 0:64] = (jj[0:64] >= ii[:, 0:64])
    m64 = np.ones((128, 512), f32)
    m64[:, ::64] = 0.0
    return dict(rcos=cos, rsin=sin, perm=perm, wmask=wmask, tri=tri, m64=m64)


_PROG = {}


def _get_prog(nl):
    if nl not in _PROG:
        _PROG[nl] = build_program(nl)
    return _PROG[nl]


def kernel(NLAYERS=4, **inp):
    f32 = np.float32
    g = {k: np.asarray(v) for k, v in inp.items()}
    nc, counts = _get_prog(NLAYERS)
    common = dict(_consts())
    common["ada_w"] = np.ascontiguousarray(g["ada_w"].reshape(4, 8, 128, 6144).transpose(0, 2, 1, 3))
    common["ada_b"] = np.ascontiguousarray(g["ada_b"].reshape(4, 48, 128).transpose(2, 0, 1))
    common["nrm"] = np.ascontiguousarray(np.stack([g["norm_mix"], g["norm_ffn"]], axis=1).reshape(4, 2, 8, 128).transpose(3, 0, 1, 2))
    common["fnorm"] = _pc(g["final_norm"])
    common["w_up"] = np.ascontiguousarray(g["ffn_w_up"].reshape(4, 8, 128, 5632).transpose(0, 2, 1, 3))
    common["w_dn"] = np.ascontiguousarray(g["ffn_w_down"].reshape(4, 22, 128, 1024).transpose(0, 2, 1, 3))
    common["ffn_cw"] = np.ascontiguousarray(g["ffn_conv_w"].reshape(4, 3, 44, 128).transpose(3, 0, 1, 2))
    common["ffn_cb"] = np.ascontiguousarray(g["ffn_conv_b"].reshape(4, 44, 128).transpose(2, 0, 1))
    wq = g["win_w_qkv"][0]
    qcols = wq[:, 0:1024]
    kcols = wq[:, 1024:1280]
    vcols = wq[:, 1280:1536]
    kdup = np.concatenate([np.concatenate([kcols[:, h * 64:(h + 1) * 64]] * 2, axis=1) for h in range(4)], axis=1)
    common["w0_qk"] = _fm(np.concatenate([qcols, kdup], axis=1))
    common["w0_v"] = _fm(vcols)
    common["w0_o"] = _fm(g["win_w_o"][0])
    common["sink"] = np.ascontiguousarray(g["win_sink"][0])
    wg = g["gla_w_qkvr"][0]
    common["w1_qk"] = _fm(wg[:, 0:1024])
    common["w1_v"] = _fm(wg[:, 1024:2048])
    common["w1_r"] = _fm(wg[:, 2048:3072])
    common["w1_g1"] = _fm(np.concatenate([g["gla_w_gf1"][0], g["gla_w_gb1"][0]], axis=1))
    g2 = np.zeros((32, 1024), f32)
    g2[0:16, 0:512] = g["gla_w_gf2"][0]
    g2[16:32, 512:1024] = g["gla_w_gb2"][0]
    common["w1_g2"] = g2
    common["w1_gb"] = _pc(np.concatenate([g["gla_b_gf"][0], g["gla_b_gb"][0]]))
    common["w1_gn"] = _pc(g["gla_norm"][0])
    common["w1_o"] = _fm(g["gla_w_o"][0])
    wd = g["diff_w_qkv"][0]
    common["w2_qk"] = _fm(wd[:, 0:2048])
    common["w2_v"] = _fm(wd[:, 2048:3072])
    common["w2_o"] = _fm(g["diff_w_o"][0])
    common["lqk"] = np.ascontiguousarray(np.stack([g["diff_lq1"][0], g["diff_lk1"][0], g["diff_lq2"][0], g["diff_lk2"][0]]))
    common["w2_gn"] = np.ascontiguousarray(g["diff_norm"][0].reshape(128, 1))
    common.update(_host_ssd_common(g))
    in_maps = []
    for b in range(8):
        m = dict(common)
        xs = g["x_sample"][b]
        xp = g["x_prompt"][2 * b:2 * b + 2].reshape(512, 1024)
        xa = np.concatenate([xs, xp], axis=0)
        m["xT"] = np.ascontiguousarray(xa.T.reshape(8, 128, T).transpose(1, 0, 2))
        cond = np.stack([g["c"][b], g["c_ctx"]], axis=1)
        m["condT"] = np.ascontiguousarray(cond.reshape(8, 128, 2).transpose(1, 0, 2))
        ck = g["cache_win_k"][b, 0]
        kT = ck.transpose(2, 1, 0)
        m["ck0"] = np.ascontiguousarray(np.concatenate([kT, kT], axis=0))
        m["cv0"] = np.ascontiguousarray(g["cache_win_v"][b, 0].reshape(512, 256))
        m["s1f"] = np.ascontiguousarray(g["state_gla_fwd"][b, 0].transpose(1, 0, 2))
        m["s1b"] = np.ascontiguousarray(g["state_gla_bwd"][b, 0].transpose(1, 0, 2))
        dk = g["cache_diff_k"][b, 0]
        m["ck2"] = np.ascontiguousarray(dk.transpose(2, 3, 1, 0).reshape(128, 8, 512))
        m["cv2"] = np.ascontiguousarray(g["cache_diff_v"][b, 0].reshape(512, 1024))
        m.update(_host_ssd_core(g, b))
        in_maps.append(m)
    import os
    ncores = int(os.environ.get("KCORES", "8"))
    ktrace = os.environ.get("KTRACE", "") == "1"
    res = run_bass_kernel_spmd(nc, in_maps[:ncores], core_ids=list(range(ncores)), trace=ktrace) if ktrace else run_bass_kernel_spmd(nc, in_maps[:ncores], core_ids=list(range(ncores)))
    if ktrace:
        print("EXEC_TIME_NS", res.exec_time_ns)
    R = list(res.results) + [res.results[0]] * (8 - ncores)
    y_prompt = np.zeros((16, 256, 1024), f32)
    y_sample = np.zeros((8, 4096, 1024), f32)
    win_k = np.zeros((16, 1, 256, 4, 64), f32)
    win_v = np.zeros((16, 1, 256, 4, 64), f32)
    gla_f = np.zeros((16, 1, 4, 128, 256), f32)
    gla_b = np.zeros((16, 1, 4, 128, 256), f32)
    diff_k = np.zeros((16, 1, 256, 8, 2, 64), f32)
    diff_v = np.zeros((16, 1, 256, 8, 128), f32)
    ssd_f = np.zeros((16, 1, 32, 64, 128), f32)
    ssd_b = np.zeros((16, 1, 32, 64, 128), f32)
    for b in range(8):
        r = R[b]
        y = r["yT"].transpose(1, 0, 2).reshape(1024, T).T
        y_sample[b] = y[0:4096]
        y_prompt[2 * b:2 * b + 2] = y[4096:].reshape(2, 256, 1024)
        wk = r["o_wk"]
        win_k[2 * b:2 * b + 2, 0] = wk.transpose(2, 0, 1).reshape(2, 256, 4, 64)
        win_v[2 * b:2 * b + 2, 0] = r["o_wv"].reshape(2, 256, 4, 64)
        gla_f[2 * b:2 * b + 2, 0] = r["o_gf"].transpose(0, 2, 1, 3)
        gla_b[2 * b:2 * b + 2, 0] = r["o_gb"].transpose(0, 2, 1, 3)
        dk = r["o_dk"]
        diff_k[2 * b:2 * b + 2, 0] = dk.transpose(2, 0, 1).reshape(2, 256, 8, 2, 64)
        diff_v[2 * b:2 * b + 2, 0] = r["o_dv"].reshape(2, 256, 8, 128)
        _host_ssd_out(r, b, ssd_f, ssd_b)
    return (y_prompt, y_sample, win_k, win_v, gla_f, gla_b, diff_k, diff_v, ssd_f, ssd_b)


def _host_ssd_common(g):
    w = g["ssd_w_in"][0]
    out = {}
    out["w3_z"] = _fm(w[:, 0:2048])
    out["w3_xbc"] = _fm(w[:, 2048:5120])
    out["w3_dt"] = _fm(w[:, 5120:5184])
    out["w3_o"] = _fm(g["ssd_w_out"][0])
    out["ssd_cw"] = np.ascontiguousarray(g["ssd_conv_w"][0].reshape(3, 24, 128).transpose(2, 0, 1))
    out["ssd_cb"] = _pc(g["ssd_conv_b"][0])
    out["ssd_vec"] = np.ascontiguousarray(np.stack([g["ssd_a_log_f"][0], g["ssd_a_log_b"][0], g["ssd_dt_bias_f"][0], g["ssd_dt_bias_b"][0], g["ssd_d"][0]]))
    out["ssd_gn"] = np.ascontiguousarray(g["ssd_norm"][0])
    return out


def _host_ssd_core(g, b):
    return {"s3f": np.ascontiguousarray(g["state_ssd_fwd"][b, 0].transpose(2, 0, 1).reshape(128, 2048)),
            "s3b": np.ascontiguousarray(g["state_ssd_bwd"][b, 0].transpose(2, 0, 1).reshape(128, 2048))}


def _host_ssd_out(r, b, ssd_f, ssd_b):
    ssd_f[2 * b:2 * b + 2, 0] = r["o_sf"].reshape(2, 128, 32, 64).transpose(0, 2, 3, 1)
    ssd_b[2 * b:2 * b + 2, 0] = r["o_sb"].reshape(2, 128, 32, 64).transpose(0, 2, 3, 1)
```

```python
import numpy as np
import concourse.bass as bass
import concourse.mybir as mybir
from concourse.bass_utils import run_bass_kernel_spmd

F32 = mybir.dt.float32
BF16 = mybir.dt.bfloat16
AF = mybir.ActivationFunctionType
ALU = mybir.AluOpType
AX = mybir.AxisListType
AP = bass.AP

SB_BASE = 16512
SB_TOP = 229344
EPOCH = 30000


class Buf:
    __slots__ = ("w", "rs", "name")

    def __init__(self, name=""):
        self.w = None
        self.rs = []
        self.name = name


class Op:
    __slots__ = ("eng", "fn", "deps", "dma", "ms", "sem", "val", "need", "prev", "grp")

    def __init__(self, eng, fn, dma):
        self.eng = eng
        self.fn = fn
        self.dma = dma
        self.deps = []
        self.ms = None
        self.sem = None
        self.val = None
        self.need = False
        self.prev = None
        self.grp = None


class Sched:
    ENGS = ("pe", "act", "dve", "pool", "sp")

    def __init__(self, nc):
        self.nc = nc
        self.ops = {e: [] for e in self.ENGS}
        self.dmas_since_barrier = []
        self.all_dmas = []
        self.nps = 0
        self.nacc = 0
        self.psum = []
        for i in range(8):
            t = nc.alloc_psum_tensor("psb%d" % i, [128, 512], F32)
            self.psum.append((t, Buf("ps%d" % i)))
        self.sb_off = SB_BASE
        self.sb_mark = SB_BASE
        self.nalloc = 0

    def alloc(self, shape, dtype, name=None):
        nbytes = int(np.prod(shape[1:])) * (4 if dtype == F32 else 2)
        nbytes = (nbytes + 63) // 64 * 64
        off = self.sb_off
        assert off + nbytes <= SB_TOP, "SBUF overflow %d" % (off + nbytes - SB_TOP)
        self.sb_off += nbytes
        self.nalloc += 1
        t = self.nc.alloc_sbuf_tensor_at("sb%d_%s" % (self.nalloc, name or "t"), list(shape), dtype, offset=off)
        return t

    def mark(self):
        self.sb_mark = self.sb_off

    def release(self):
        self.sb_off = self.sb_mark

    def ps(self, pool="a"):
        if pool == "a":
            t, b = self.psum[self.nps % 6]
            self.nps += 1
        else:
            t, b = self.psum[6 + self.nacc % 2]
            self.nacc += 1
        return t, b

    def op(self, eng, fn, reads=(), writes=(), dma=False):
        o = Op(eng, fn, dma)
        deps = {}
        for b in reads:
            d = b.w
            if d is not None:
                if (not dma) and (not d.dma) and d.eng == eng and eng == "pe":
                    continue
                deps[id(d)] = d
        for b in writes:
            cand = list(b.rs)
            if b.w is not None:
                cand.append(b.w)
            for d in cand:
                if d is o:
                    continue
                if (not dma) and (not d.dma) and d.eng == eng:
                    continue
                deps[id(d)] = d
        o.deps = list(deps.values())
        for b in reads:
            if not dma:
                b.rs = [r for r in b.rs if r.dma or r.eng != eng]
            b.rs.append(o)
        for b in writes:
            b.w = o
            b.rs = []
        self.ops[eng].append(o)
        if dma:
            self.dmas_since_barrier.append(o)
            self.all_dmas.append(o)
        return o

    def barrier(self):
        lasts = []
        for e in self.ENGS:
            for o in reversed(self.ops[e]):
                if not o.dma and o.fn is not None:
                    lasts.append(o)
                    break
        deps = lasts + self.dmas_since_barrier
        self.dmas_since_barrier = []
        for e in self.ENGS:
            o = Op(e, None, False)
            o.deps = list(deps)
            self.ops[e].append(o)

    def dma(self, out, in_, reads=(), writes=(), eng=None):
        if eng is None:
            eng = "pool" if type(out.tensor).__name__.startswith("DRam") else "sp"
        return self.op(eng, lambda e: e.dma_start(out=out, in_=in_), reads, writes, dma=True)

    def mm(self, out, lhsT, rhs, start, stop, reads=(), writes=(), grp=None):
        o = self.op("pe", lambda e: e.matmul(out, lhsT, rhs, start=start, stop=stop), reads, writes)
        o.grp = grp
        return o

    def emit(self):
        nc = self.nc
        for e in self.ENGS:
            for o in self.ops[e]:
                for d in o.deps:
                    d.need = True
        sem_ctx = []
        import contextlib
        with contextlib.ExitStack() as st:
            tl = {}
            for e in ("pe", "act", "dve", "pool"):
                tl[e] = [st.enter_context(nc.semaphore("tl_%s_%d" % (e, i))) for i in range(5)]
            npool = {"sp": 36, "pool": 36, "act": 8}
            dpool = {e: [st.enter_context(nc.semaphore("dq_%s_%d" % (e, i))) for i in range(n)] for e, n in npool.items()}
            for e in self.ENGS:
                m = 0
                k = 0
                for o in self.ops[e]:
                    if o.fn is None:
                        continue
                    if o.dma:
                        P = len(dpool[e])
                        o.sem = dpool[e][k % P]
                        o.val = 16 * (k // P + 1)
                        k += 1
                    elif o.need:
                        o.sem = tl[e][m // EPOCH]
                        o.val = m % EPOCH + 1
                        m += 1
                assert m < EPOCH * 5, (e, m)
            final = {}
            for e, n in npool.items():
                for o in self.ops[e]:
                    if o.dma:
                        final[id(o.sem)] = (o.sem, o.val)

            def stream(e, eng):
                seen = {}

                def wait(sem, val):
                    if seen.get(id(sem), 0) < val:
                        eng.wait_ge(sem, val)
                        seen[id(sem)] = val

                ops_ = self.ops[e]
                i_ = 0
                while i_ < len(ops_):
                    o = ops_[i_]
                    j_ = i_ + 1
                    if o.grp is not None:
                        while j_ < len(ops_) and ops_[j_].grp == o.grp:
                            j_ += 1
                    for k_ in range(i_, j_):
                        for d in ops_[k_].deps:
                            wait(d.sem, d.val)
                    for k_ in range(i_, j_):
                        o = ops_[k_]
                        if o.fn is None:
                            continue
                        if o.dma and o.val > 16:
                            wait(o.sem, o.val - 16)
                        ins_ = o.fn(eng)
                        if o.dma:
                            ins_.then_inc(o.sem, 16)
                        elif o.need:
                            ins_.then_inc(o.sem, 1)
                    i_ = j_
                if e == "sp":
                    for sem, val in final.values():
                        wait(sem, val)

            with nc.Block() as block:
                @block.tensor
                def _(eng):
                    stream("pe", eng)

                @block.scalar
                def _(eng):
                    stream("act", eng)

                @block.vector
                def _(eng):
                    stream("dve", eng)

                @block.gpsimd
                def _(eng):
                    stream("pool", eng)

                @block.sync
                def _(eng):
                    stream("sp", eng)
        return {e: len(v) for e, v in self.ops.items()}


def ins(method, *a, **kw):
    return lambda e: getattr(e, method)(*a, **kw)


T = 4608
LS = 4096
TILES = [(i * 512, 512, 0, 0, 4096) for i in range(8)] + [(4096, 256, 1, 4096, 4352), (4352, 256, 1, 4352, 4608)]
EPS = 1e-6
DFF = 2816
LAM_INIT = 0.8 - 0.6 * float(np.exp(-0.3 * 2))


def build_program(NLAYERS=4, dbg=False):
    nc = bass.Bass("TRN2", target_bir_lowering=False)
    s = Sched(nc)
    I = {}
    O = {}

    def din(name, shape):
        I[name] = nc.dram_tensor(name, list(shape), F32, kind="ExternalInput").ap()
        return I[name]

    def dout(name, shape):
        O[name] = nc.dram_tensor(name, list(shape), F32, kind="ExternalOutput").ap()
        return O[name]

    def dscr(name, shape, dt):
        return nc.dram_tensor(name, list(shape), dt).ap()

    xT = din("xT", [128, 8, T])
    condT = din("condT", [128, 8, 2])
    ada_w = din("ada_w", [4, 128, 8, 6144])
    ada_b = din("ada_b", [128, 4, 48])
    nrm = din("nrm", [128, 4, 2, 8])
    fnorm = din("fnorm", [128, 8])
    w_up = din("w_up", [4, 128, 8, 5632])
    w_dn = din("w_dn", [4, 128, 22, 1024])
    ffn_cw = din("ffn_cw", [128, 4, 3, 44])
    ffn_cb = din("ffn_cb", [128, 4, 44])
    perm_in = din("perm", [128, 128])
    rcos = din("rcos", [128, LS])
    rsin = din("rsin", [128, LS])
    wmask_in = din("wmask", [128, 6, 512])
    tri_in = din("tri", [128, 4, 128])
    m64_in = din("m64", [128, 512])
    w0_qk = din("w0_qk", [128, 8, 1536])
    w0_v = din("w0_v", [128, 8, 256])
    w0_o = din("w0_o", [128, 8, 1024])
    sink_in = din("sink", [16])
    ck0 = din("ck0", [128, 4, 512])
    cv0 = din("cv0", [512, 256])
    w1_qk = din("w1_qk", [128, 8, 1024])
    w1_v = din("w1_v", [128, 8, 1024])
    w1_r = din("w1_r", [128, 8, 1024])
    w1_g1 = din("w1_g1", [128, 8, 32])
    w1_g2 = din("w1_g2", [32, 1024])
    w1_gb = din("w1_gb", [128, 8])
    w1_gn = din("w1_gn", [128, 2])
    w1_o = din("w1_o", [128, 8, 1024])
    s1f = din("s1f", [128, 4, 256])
    s1b = din("s1b", [128, 4, 256])
    w2_qk = din("w2_qk", [128, 8, 2048])
    w2_v = din("w2_v", [128, 8, 1024])
    w2_o = din("w2_o", [128, 8, 1024])
    lqk = din("lqk", [4, 64])
    w2_gn = din("w2_gn", [128, 1])
    ck2 = din("ck2", [128, 8, 512])
    cv2 = din("cv2", [512, 1024])

    w3_z = din("w3_z", [128, 8, 2048])
    w3_xbc = din("w3_xbc", [128, 8, 3072])
    w3_dt = din("w3_dt", [128, 8, 64])
    w3_o = din("w3_o", [128, 16, 1024])
    ssd_cw = din("ssd_cw", [128, 3, 24])
    ssd_cb = din("ssd_cb", [128, 24])
    ssd_vec = din("ssd_vec", [5, 32])
    ssd_gn = din("ssd_gn", [2048])
    s3f = din("s3f", [128, 2048])
    s3b = din("s3b", [128, 2048])

    yT = dout("yT", [128, 8, T])
    o_sf = dout("o_sf", [2, 128, 2048])
    o_sb = dout("o_sb", [2, 128, 2048])
    o_wk = dout("o_wk", [4, 64, 512])
    o_wv = dout("o_wv", [512, 256])
    o_gf = dout("o_gf", [2, 128, 4, 256])
    o_gb = dout("o_gb", [2, 128, 4, 256])
    o_dk = dout("o_dk", [8, 128, 512])
    o_dv = dout("o_dv", [512, 1024])

    X = dscr("X", [128, 8, T], F32)
    U = dscr("U", [128, 44, T], BF16)
    QK = dscr("QK", [128, 16, T], BF16)
    VS = dscr("VS", [T, 1024], BF16)
    OS = dscr("OS", [128, 8, T], BF16)
    OS2 = dscr("OS2", [128, 8, T], BF16)
    RS = dscr("RS", [128, 8, T], BF16)
    QD = dscr("QD", [2, 128, 4, T], BF16)
    KD = dscr("KD", [2, 128, 4, T], BF16)
    KR = dscr("KR", [T, 8, 128], BF16)
    ZS = dscr("ZS", [T, 2048], BF16)
    DTS = dscr("DTS", [T, 128], F32)
    XTs = dscr("XTs", [T, 2048], BF16)
    BTs = dscr("BTs", [T, 512], BF16)
    YD = [dscr("YD0", [T, 2048], BF16), dscr("YD1", [T, 2048], BF16)]

    NT = len(TILES)
    import os
    STOP = int(os.environ.get("KSTOP", "99"))

    def rows_view(dr, rowlen, r0, nb, c0, w):
        return AP(dr.tensor, r0 * rowlen + c0, [[rowlen, 128], [128 * rowlen, nb], [1, w]])

    def kr_view(r0, nb, j0, nj):
        return AP(KR.tensor, r0 * 1024 + j0 * 128, [[1024, 128], [128 * 1024, nb], [128, nj], [1, 128]])
    import os
    KQ = os.environ.get("KQ", "")

    def tb(name):
        return [Buf(name + str(i)) for i in range(NT)]

    xT_b = tb("xT")
    X_b = tb("X")

    MODS = s.alloc([128, 4, 6, 8, 2], F32, "mods")
    GS = s.alloc([128, 4, 2, 8, 2], F32, "gs")
    NRM = s.alloc([128, 4, 2, 8], F32, "nrm")
    FNRM = s.alloc([128, 8], F32, "fnrm")
    ones_bf = s.alloc([128, 128], BF16, "ones")
    ones_f = s.alloc([128, 128], F32, "onesf")
    ident_bf = s.alloc([128, 128], BF16, "ident")
    perm_bf = s.alloc([128, 128], BF16, "perm")
    tri_bf = s.alloc([128, 4, 128], BF16, "tri")
    m64 = s.alloc([128, 512], F32, "m64")
    cb_const = Buf("consts")
    stage = s.alloc([128, 4, 128], F32, "stage")
    bstage = Buf()

    s.dma(NRM[:], nrm, writes=[cb_const])
    s.dma(FNRM[:], fnorm, writes=[cb_const])
    s.dma(m64[:], m64_in, writes=[cb_const])
    s.op("pool", ins("memset", ones_f[:], 1.0), writes=[cb_const])
    s.op("dve", ins("tensor_copy", out=ones_bf[:], in_=ones_f[:]), reads=[cb_const], writes=[cb_const])
    s.dma(stage[:, 0, :], perm_in, writes=[bstage])
    s.op("dve", ins("tensor_copy", out=perm_bf[:], in_=stage[:, 0, :]), reads=[bstage], writes=[cb_const])
    s.op("pool", ins("memset", stage[:, 1, :], 1.0), reads=[], writes=[bstage])
    s.op("pool", ins("affine_select", out=stage[:, 1, :], in_=stage[:, 1, :], pattern=[[-1, 128]], compare_op=ALU.is_equal, fill=0.0, base=0, channel_multiplier=1), reads=[bstage], writes=[bstage])
    s.op("dve", ins("tensor_copy", out=ident_bf[:], in_=stage[:, 1, :]), reads=[bstage], writes=[cb_const])
    s.barrier()
    s.dma(stage[:], tri_in, writes=[bstage])
    s.op("dve", ins("tensor_copy", out=tri_bf[:], in_=stage[:]), reads=[bstage], writes=[cb_const])
    s.mark()

    def mod_ap(l, k, c, cond):
        return MODS[:, l, k, c, cond:cond + 1]

    def prologue():
        s.barrier()
        s.release()
        sc = s.alloc([128, 8, 2], F32, "sc")
        scb = Buf()
        adb = s.alloc([128, 4, 48], F32, "adb")
        adbb = Buf()
        s.dma(sc[:], condT, writes=[scb])
        s.dma(adb[:], ada_b, writes=[adbb])
        s.op("act", ins("activation", out=sc[:], in_=sc[:], func=AF.Silu), reads=[scb], writes=[scb])
        wb = [s.alloc([128, 8, 1024], F32, "adaw%d" % i) for i in range(2)]
        wbb = [[Buf() for _ in range(2)] for _ in range(2)]
        it = 0
        for l in range(NLAYERS):
            for g in range(6):
                w = wb[it % 2]
                bb = wbb[it % 2]
                for hh in range(2):
                    s.dma(w[:, hh * 4:(hh + 1) * 4, :], ada_w[l, :, hh * 4:(hh + 1) * 4, g * 1024:(g + 1) * 1024], writes=[bb[hh]], eng=("sp" if hh == 0 else "act"))
                ps, pb = s.ps()
                for cc in range(8):
                    for kc in range(8):
                        s.mm(ps[:, cc * 2:cc * 2 + 2], w[:, kc, cc * 128:(cc + 1) * 128], sc[:, kc, :], kc == 0, kc == 7, reads=[bb[kc // 4], scb], writes=[pb])
                s.op("dve", ins("tensor_tensor",
                    out=MODS[:, l, g, :, :], in0=ps[:, 0:16].rearrange("p (c t) -> p c t", t=2),
                    in1=adb[:, l, g * 8:(g + 1) * 8].unsqueeze(2).to_broadcast([128, 8, 2]), op=ALU.add),
                    reads=[pb, adbb], writes=[cb_const])
                it += 1
            for which in range(2):
                k = 1 if which == 0 else 4
                s.op("dve", ins("scalar_tensor_tensor",
                    out=GS[:, l, which, :, :], in0=MODS[:, l, k, :, :], scalar=1.0,
                    in1=NRM[:, l, which, :].unsqueeze(2).to_broadcast([128, 8, 2]), op0=ALU.add, op1=ALU.mult),
                    reads=[cb_const], writes=[cb_const])

    def load_w(dst, src, nk, split=1):
        bufs = []
        for kc in range(nk):
            b = Buf()
            s.dma(dst[:, kc, :], src[:, kc, :], writes=[b], eng="pool")
            bufs.append(b)
        return bufs

    class NormCtx:
        def __init__(self, nbuf=2):
            self.nbuf = nbuf
            self.xt = [(s.alloc([128, 8, 512], F32, "nxt"), Buf()) for _ in range(nbuf)]
            self.h = [(s.alloc([128, 8, 512], BF16, "nh"), Buf()) for _ in range(nbuf)]
            self.sq = (s.alloc([128, 8, 512], BF16, "nsq"), Buf())
            self.tmp = (s.alloc([128, 8, 512], F32, "ntmp"), Buf())
            self.rstd = [(s.alloc([128, 512], F32, "nrs"), Buf()) for _ in range(2)]
            self.k = 0

    def norm_tile(ctx, l, which, ti, Xsrc, Xsrc_b, final_out=None):
        t0, n, cond, _, _ = TILES[ti]
        k = ctx.k
        ctx.k += 1
        xt, xtb = ctx.xt[k % ctx.nbuf]
        h, hb = ctx.h[k % ctx.nbuf]
        sq, sqb = ctx.sq
        tmp, tmpb = ctx.tmp
        rstd, rb = ctx.rstd[k % 2]
        s.dma(xt[:, :, :n], Xsrc[:, :, t0:t0 + n], reads=[Xsrc_b[ti]], writes=[xtb])
        s.op("act", ins("activation", out=sq[:, :, :n], in_=xt[:, :, :n], func=AF.Square), reads=[xtb], writes=[sqb])
        ps, pb = s.ps()
        for c in range(8):
            s.mm(ps[:, :n], ones_bf[:], sq[:, c, :n], c == 0, c == 7, reads=[sqb, cb_const], writes=[pb])
        s.op("act", ins("activation", out=rstd[:, :n], in_=ps[:, :n], func=AF.Sqrt, scale=1.0 / 1024, bias=EPSB[:, 0:1]), reads=[pb], writes=[rb])
        s.op("dve", ins("reciprocal", out=rstd[:, :n], in_=rstd[:, :n]), reads=[rb], writes=[rb])
        s.op("dve", ins("tensor_tensor", out=tmp[:, :, :n], in0=xt[:, :, :n], in1=rstd[:, :n].unsqueeze(1).to_broadcast([128, 8, n]), op=ALU.mult),
             reads=[xtb, rb], writes=[tmpb])
        if final_out is None:
            for c in range(8):
                s.op("act", ins("activation", out=h[:, c, :n], in_=tmp[:, c, :n], func=AF.Identity,
                                                          scale=GS[:, l, which, c, cond:cond + 1], bias=mod_ap(l, 0 if which == 0 else 3, c, cond)),
                     reads=[tmpb, cb_const], writes=[hb])
            return h, hb
        else:
            for c in range(8):
                s.op("act", ins("activation", out=xt[:, c, :n], in_=tmp[:, c, :n], func=AF.Identity, scale=FNRM[:, c:c + 1]),
                     reads=[tmpb, cb_const], writes=[xtb])
            s.dma(final_out[:, :, t0:t0 + n], xt[:, :, :n], reads=[xtb])
            return None, None

    EPSB = s.alloc([128, 1], F32, "epsb")
    s.op("pool", ins("memset", EPSB[:], EPS), writes=[cb_const])
    s.mark()

    class ResCtx:
        def __init__(self):
            self.xo = [(s.alloc([128, 8, 512], F32, "xo"), Buf()) for _ in range(2)]
            self.k = 0

    def outproj_resid(rctx, l, gate_k, ti, W, wbufs, nk, rhs, rhsbufs, Xsrc, Xsrc_b, Xdst, Xdst_b):
        t0, n, cond, _, _ = TILES[ti]
        xo, xob = rctx.xo[rctx.k % 2]
        rctx.k += 1
        s.dma(xo[:, :, :n], Xsrc[:, :, t0:t0 + n], reads=[Xsrc_b[ti]], writes=[xob])
        for dc in range(8):
            ps, pb = s.ps()
            for kc in range(nk):
                s.mm(ps[:, :n], W[:, kc, dc * 128:(dc + 1) * 128], rhs[:, kc, :n], kc == 0, kc == nk - 1,
                     reads=[wbufs[kc]] + list(rhsbufs), writes=[pb])
            s.op("dve", ins("scalar_tensor_tensor", out=xo[:, dc, :n], in0=ps[:, :n], scalar=mod_ap(l, gate_k, dc, cond),
                                                                      in1=xo[:, dc, :n], op0=ALU.mult, op1=ALU.add),
                 reads=[pb, xob, cb_const], writes=[xob])
        s.dma(Xdst[:, :, t0:t0 + n], xo[:, :, :n], reads=[xob], writes=[Xdst_b[ti]])

    def ffn(l, Xsrc, Xsrc_b):
        s.barrier()
        s.release()
        Wup = s.alloc([128, 8, 5632], BF16, "wup")
        wb = load_w(Wup, w_up[l], 8)
        nctx = NormCtx()
        ub = [(s.alloc([128, 11, 512], BF16, "ub"), [Buf() for _ in range(11)]) for _ in range(2)]
        U_b = [[Buf() for _ in range(4)] for _ in range(NT)]
        ku = 0
        ev = 0
        for ti in range(NT):
            t0, n, cond, _, _ = TILES[ti]
            h, hb = norm_tile(nctx, l, 1, ti, Xsrc, Xsrc_b)
            for grp in range(4):
                ut, utb = ub[ku % 2]
                ku += 1
                for cc in range(11):
                    col = grp * 11 + cc
                    ps, pb = s.ps()
                    for kc in range(8):
                        s.mm(ps[:, :n], Wup[:, kc, col * 128:(col + 1) * 128], h[:, kc, :n], kc == 0, kc == 7, reads=[wb[kc], hb], writes=[pb])
                    if ev % 2 == 0:
                        s.op("act", ins("activation", out=ut[:, cc, :n], in_=ps[:, :n], func=AF.Identity), reads=[pb], writes=[utb[cc]])
                    else:
                        s.op("dve", ins("tensor_copy", out=ut[:, cc, :n], in_=ps[:, :n]), reads=[pb], writes=[utb[cc]])
                    ev += 1
                s.dma(U[:, grp * 11:(grp + 1) * 11, t0:t0 + n], ut[:, :, :n], reads=utb, writes=[U_b[ti][grp]])
        s.barrier()
        s.release()
        Wd = s.alloc([128, 22, 1024], BF16, "wd")
        wdb = load_w(Wd, w_dn[l], 22)
        cw = s.alloc([128, 3, 44], F32, "cw")
        cbias = s.alloc([128, 44], F32, "cbias")
        cwb = Buf()
        s.dma(cw[:], ffn_cw[:, l, :, :], writes=[cwb])
        s.dma(cbias[:], ffn_cb[:, l, :], writes=[cwb])
        ug = s.alloc([128, 44, 514], BF16, "ug")
        ugb = [Buf() for _ in range(4)]
        at = [(s.alloc([128, 22, 512], BF16, "at"), Buf()) for _ in range(2)]
        tmps = [[(s.alloc([128, 512], F32, "ft"), Buf()) for _ in range(3)] for _ in range(4)]
        rctx = ResCtx()
        kp = 0
        for ti in range(NT):
            t0, n, cond, s0, s1 = TILES[ti]
            lo = (t0 - 1) >= s0
            hi = (t0 + n) < s1
            for grp in range(4):
                gs_ = slice(grp * 11, (grp + 1) * 11)
                if not lo:
                    s.op("pool", ins("memset", ug[:, gs_, 0:1], 0.0), writes=[ugb[grp]])
                if not hi:
                    s.op("pool", ins("memset", ug[:, gs_, n + 1:n + 2], 0.0), writes=[ugb[grp]])
                c0 = 0 if lo else 1
                c1 = n + 2 if hi else n + 1
                rd = [U_b[ti][grp]]
                if lo:
                    rd.append(U_b[ti - 1][grp])
                if hi:
                    rd.append(U_b[ti + 1][grp])
                s.dma(ug[:, gs_, c0:c1], U[:, gs_, t0 - 1 + c0:t0 - 1 + c1], reads=rd, writes=[ugb[grp]])
            a, ab = at[ti % 2]
            pst = {}

            def f2s1(cc):
                nonlocal kp
                res = []
                for half in range(2):
                    col = cc + 22 * half
                    tt, ttb = tmps[kp % 4][half]
                    grp = col // 11
                    s.op("act", ins("activation", out=tt[:, :n], in_=ug[:, col, 1:n + 1], func=AF.Identity,
                                    scale=cw[:, 1, col:col + 1], bias=cbias[:, col:col + 1]),
                         reads=[ugb[grp], cwb], writes=[ttb])
                    res.append((tt, ttb, col, grp))
                sg, sgb = tmps[kp % 4][2]
                kp += 1
                pst[cc] = (res, sg, sgb)

            def f2s2(cc):
                res, sg, sgb = pst[cc]
                for (tt, ttb, col, grp) in res:
                    s.op("dve", ins("scalar_tensor_tensor", out=tt[:, :n], in0=ug[:, col, 0:n], scalar=cw[:, 0, col:col + 1],
                                    in1=tt[:, :n], op0=ALU.mult, op1=ALU.add),
                         reads=[ugb[grp], cwb, ttb], writes=[ttb])
                    s.op("dve", ins("scalar_tensor_tensor", out=tt[:, :n], in0=ug[:, col, 2:n + 2], scalar=cw[:, 2, col:col + 1],
                                    in1=tt[:, :n], op0=ALU.mult, op1=ALU.add),
                         reads=[ugb[grp], cwb, ttb], writes=[ttb])

            def f2s3(cc):
                res, sg, sgb = pst.pop(cc)
                s.op("act", ins("activation", out=sg[:, :n], in_=res[0][0][:, :n], func=AF.Silu), reads=[res[0][1]], writes=[sgb])
                s.op("pool", ins("tensor_tensor", out=a[:, cc, :n], in0=sg[:, :n], in1=res[1][0][:, :n], op=ALU.mult),
                     reads=[sgb, res[1][1]], writes=[ab])

            for t_ in range(22 + 2):
                if t_ < 22:
                    f2s1(t_)
                if 1 <= t_ < 23:
                    f2s2(t_ - 1)
                if t_ >= 2:
                    f2s3(t_ - 2)
            outproj_resid(rctx, l, 5, ti, Wd, wdb, 22, a, [ab], Xsrc, Xsrc_b, X, X_b)

    def qkv_phase(l, Wqk_src, nqk, Wv_src, nv, Xsrc, Xsrc_b, rope, kout=None, vout=None, post=None):
        s.barrier()
        s.release()
        Wqk = s.alloc([128, 8, nqk * 128], BF16, "wqk")
        wqb = load_w(Wqk, Wqk_src, 8)
        Wv = s.alloc([128, 8, nv], BF16, "wv")
        wvb = load_w(Wv, Wv_src, 8)
        nctx = NormCtx()
        qk = [(s.alloc([128, nqk, 512], BF16, "qk"), [Buf() for _ in range(nqk)]) for _ in range(2)]
        vt = [(s.alloc([128, 4, nv], BF16, "vt"), Buf()) for _ in range(2)]
        qb = [(s.alloc([128, 512], BF16, "qb"), Buf()) for _ in range(3)]
        t12 = [[(s.alloc([128, 512], F32, "rt"), Buf()) for _ in range(2)] for _ in range(3)]
        cs = [(s.alloc([128, 2, 512], F32, "cs"), Buf()) for _ in range(1)]
        kf = [(s.alloc([128, 256], F32, "kf"), Buf()) for _ in range(2)]
        vf = [(s.alloc([128, 512], F32, "vf"), Buf()) for _ in range(2)]
        QK_b = [Buf() for _ in range(NT)]
        VS_b = [Buf() for _ in range(NT)]
        if "L" in KQ:
            return
        kq = 0
        kk = 0
        kv = 0
        for ti in range(NT):
            t0, n, cond, s0, s1 = TILES[ti]
            h, hb = norm_tile(nctx, l, 0, ti, Xsrc, Xsrc_b)
            if "N" in KQ:
                continue
            qkt, qkb = qk[ti % 2]
            dorope = rope and cond == 0 and ("r" not in KQ)
            if dorope:
                cst, csb = cs[0]
                s.dma(cst[:, 0, :], rcos[:, t0:t0 + n], writes=[csb])
                s.dma(cst[:, 1, :], rsin[:, t0:t0 + n], writes=[csb])
            ropeq = []

            def rope_fin(item):
                cc_, q_, q_b, t1, t1b, t2, t2b = item
                ps2, pb2 = s.ps()
                s.mm(ps2[:, :n], perm_bf[:], q_[:, :n], True, True, reads=[q_b, cb_const], writes=[pb2])
                s.op("dve", ins("tensor_tensor", out=t1[:, :n], in0=q_[:, :n], in1=cst[:, 0, :n], op=ALU.mult),
                     reads=[q_b, csb], writes=[t1b])
                s.op("dve", ins("tensor_tensor", out=t2[:, :n], in0=ps2[:, :n], in1=cst[:, 1, :n], op=ALU.mult),
                     reads=[pb2, csb], writes=[t2b])
                s.op("pool", ins("tensor_tensor", out=qkt[:, cc_, :n], in0=t1[:, :n], in1=t2[:, :n], op=ALU.add),
                     reads=[t1b, t2b], writes=[qkb[cc_]])

            for cc in range(nqk):
                ps, pb = s.ps()
                for kc in range(8):
                    s.mm(ps[:, :n], Wqk[:, kc, cc * 128:(cc + 1) * 128], h[:, kc, :n], kc == 0, kc == 7, reads=[wqb[kc], hb], writes=[pb])
                if dorope:
                    q_, q_b = qb[kq % 3]
                    t1, t1b = t12[kq % 3][0]
                    t2, t2b = t12[kq % 3][1]
                    kq += 1
                    s.op("act", ins("activation", out=q_[:, :n], in_=ps[:, :n], func=AF.Identity), reads=[pb], writes=[q_b])
                    ropeq.append((cc, q_, q_b, t1, t1b, t2, t2b))
                    if len(ropeq) > 1:
                        rope_fin(ropeq.pop(0))
                else:
                    if kout is not None and cond == 1 and cc >= kout[0]:
                        kft, kfb = kf[kk % 2]
                        kk += 1
                        rows = kout[2]
                        s.op("dve", ins("tensor_copy", out=kft[:, :n], in_=ps[:, :n]), reads=[pb], writes=[kfb])
                        s.op("act", ins("activation", out=qkt[:, cc, :n], in_=kft[:, :n], func=AF.Identity), reads=[kfb], writes=[qkb[cc]])
                        s.dma(kout[1][cc - kout[0], :, t0 - LS:t0 - LS + n], kft[0:rows, :n], reads=[kfb])
                    else:
                        s.op("act", ins("activation", out=qkt[:, cc, :n], in_=ps[:, :n], func=AF.Identity), reads=[pb], writes=[qkb[cc]])
                if post is not None:
                    post(ti, cc, ps, pb)
            while ropeq:
                rope_fin(ropeq.pop(0))
            vtt, vtb = vt[ti % 2]
            for tbk in range(n // 128 if "v" not in KQ else 0):
                for vg in range((nv + 511) // 512):
                    w_ = min(512, nv - vg * 512)
                    ps, pb = s.ps()
                    for kc in range(8):
                        s.mm(ps[:, :w_], h[:, kc, tbk * 128:(tbk + 1) * 128], Wv[:, kc, vg * 512:vg * 512 + w_], kc == 0, kc == 7, reads=[wvb[kc], hb], writes=[pb])
                    if vout is not None and cond == 1:
                        vft, vfb = vf[kv % 2]
                        kv += 1
                        s.op("dve", ins("tensor_copy", out=vft[:, :w_], in_=ps[:, :w_]), reads=[pb], writes=[vfb])
                        s.op("act", ins("activation", out=vtt[:, tbk, vg * 512:vg * 512 + w_], in_=vft[:, :w_], func=AF.Identity),
                             reads=[vfb], writes=[vtb])
                        r0 = t0 - LS + tbk * 128
                        s.dma(vout[r0:r0 + 128, vg * 512:vg * 512 + w_], vft[:, :w_], reads=[vfb])
                    else:
                        s.op("act", ins("activation", out=vtt[:, tbk, vg * 512:vg * 512 + w_], in_=ps[:, :w_], func=AF.Identity),
                             reads=[pb], writes=[vtb])
            if "q" not in KQ:
                s.dma(QK[:, 0:nqk, t0:t0 + n], qkt[:, :, :n], reads=qkb, writes=[QK_b[ti]])
            if "v" not in KQ and "s" not in KQ:
              s.dma(rows_view(VS, 1024, t0, n // 128, 0, nv), vtt[:, 0:n // 128, :], reads=[vtb], writes=[VS_b[ti]])

    def attn_phase(l, kind, Xsrc, Xsrc_b):
        s.barrier()
        s.release()
        win = kind == "win"
        nunits = 8
        kbase = 8
        Kt = [(s.alloc([128, T], BF16, "kt"), Buf()) for _ in range(2)]
        Va = [(s.alloc([128, 36, 128], BF16, "va"), Buf()) for _ in range(2)]
        Qt = [(s.alloc([128, LS], BF16, "qt"), Buf()) for _ in range(2)]
        Ot = [(s.alloc([128, LS], BF16, "ot"), Buf()) for _ in range(2)]
        pT = [(s.alloc([128, 512], BF16, "pT"), Buf()) for _ in range(8)]
        rec = [(s.alloc([128, 512], F32, "rec"), Buf()) for _ in range(8)]
        OS_b = [[Buf() for _ in range(nunits)] for _ in range(3)]
        cbuf = Buf()
        if win:
            masks = s.alloc([128, 6, 512], BF16, "masks")
            s.dma(masks[:], wmask_in, writes=[cbuf], eng="pool")
            esink = s.alloc([128, 16], F32, "esink")
            s.dma(esink[:], sink_in.partition_broadcast(128), writes=[cbuf])
            s.op("act", ins("activation", out=esink[:], in_=esink[:], func=AF.Exp), reads=[cbuf], writes=[cbuf])
            for i in range(2):
                s.op("pool", ins("memset", Va[i][0][:, :, 64:128], 1.0), writes=[Va[i][1]])
        else:
            acc = [(s.alloc([128, 512], F32, "acc"), Buf()) for _ in range(8)]
            osb = [(s.alloc([128, 512], F32, "osb"), Buf()) for _ in range(8)]
            sqd = [(s.alloc([128, 512], BF16, "sqd"), Buf()) for _ in range(4)]
            lq = s.alloc([128, 4, 64], F32, "lq")
            lam = s.alloc([128, 4], F32, "lam")
            gsub = s.alloc([128, 1], F32, "gsub")
            s.dma(lq[:], lqk.rearrange("a b -> (a b)").partition_broadcast(128).rearrange("p (a b) -> p a b", a=4), writes=[cbuf])
            s.dma(gsub[:], w2_gn, writes=[cbuf])
            s.op("dve", ins("tensor_tensor", out=lq[:, 0, :], in0=lq[:, 0, :], in1=lq[:, 1, :], op=ALU.mult), reads=[cbuf], writes=[cbuf])
            s.op("dve", ins("tensor_tensor", out=lq[:, 2, :], in0=lq[:, 2, :], in1=lq[:, 3, :], op=ALU.mult), reads=[cbuf], writes=[cbuf])
            s.op("dve", ins("reduce_sum", out=lam[:, 0:1], in_=lq[:, 0, :], axis=AX.X), reads=[cbuf], writes=[cbuf])
            s.op("dve", ins("reduce_sum", out=lam[:, 1:2], in_=lq[:, 2, :], axis=AX.X), reads=[cbuf], writes=[cbuf])
            s.op("act", ins("activation", out=lam[:, 0:2], in_=lam[:, 0:2], func=AF.Exp), reads=[cbuf], writes=[cbuf])
            s.op("dve", ins("tensor_tensor", out=lam[:, 2:3], in0=lam[:, 1:2], in1=lam[:, 0:1], op=ALU.subtract), reads=[cbuf], writes=[cbuf])
            s.op("dve", ins("tensor_scalar", out=lam[:, 2:3], in0=lam[:, 2:3], scalar1=-LAM_INIT, scalar2=None, op0=ALU.add), reads=[cbuf], writes=[cbuf])
            s.op("dve", ins("tensor_scalar", out=lam[:, 3:4], in0=gsub[:], scalar1=1.0 - LAM_INIT, scalar2=None, op0=ALU.mult), reads=[cbuf], writes=[cbuf])
        ku = 0
        kp = 0
        kr = 0
        ka = 0
        SEQS = [(0, 4096, 0), (4096, 256, 1), (4352, 256, 2)]
        for (q0, L, si) in SEQS:
            sample = si == 0
            nkb_lat = L // 128
            for u in range(nunits):
                kt, ktb = Kt[ku % 2]
                va, vab = Va[ku % 2]
                qt, qtb = Qt[ku % 2]
                ot, otb = Ot[ku % 2]
                ku += 1
                s.dma(qt[:, 0:L], QK[:, u, q0:q0 + L], writes=[qtb])
                if win:
                    g = u // 2
                    s.dma(kt[:, 0:L], QK[:, kbase + g, q0:q0 + L], writes=[ktb])
                    s.dma(va[:, 0:nkb_lat, 0:64], rows_view(VS, 1024, q0, nkb_lat, g * 64, 64), writes=[vab])
                    if sample:
                        s.dma(kt[:, L:L + 512], ck0[:, g, :], writes=[ktb], eng="pool")
                        s.dma(va[:, 32:36, 0:64], rows_view(cv0, 256, 0, 4, g * 64, 64), writes=[vab], eng="pool")
                else:
                    s.dma(kt[:, 0:L], QK[:, kbase + u, q0:q0 + L], writes=[ktb])
                    s.dma(va[:, 0:nkb_lat, :], rows_view(VS, 1024, q0, nkb_lat, u * 128, 128), writes=[vab])
                    if sample:
                        s.dma(kt[:, L:L + 512], ck2[:, u, :], writes=[ktb], eng="pool")
                        s.dma(va[:, 32:36, :], rows_view(cv2, 1024, 0, 4, u * 128, 128), writes=[vab], eng="pool")
                nq = 512 if sample else 256
                LOOK = 4
                tasks = []
                for qi in range(L // nq):
                    for e_ in range(2):
                        if win and sample:
                            kbs = [(kb, kb - qi * 4 + 1) for kb in range(qi * 4 - 1, qi * 4 + 5) if 0 <= kb < 32] + [(32 + j, None) for j in range(4)]
                        elif sample:
                            kbs = [(kb, None) for kb in range(36)]
                        else:
                            kbs = [(kb, None) for kb in range(2)]
                        for i, (kb, mi) in enumerate(kbs):
                            tasks.append((qi, e_, i, kb, mi, i == len(kbs) - 1))
                stt = {}

                def stage1(tk_):
                    nonlocal kp, ka
                    qi, e_, i, kb, mi, lastb = tk_
                    qs = slice(qi * nq, (qi + 1) * nq)
                    pr = slice(e_ * 64, (e_ + 1) * 64)
                    if i == 0:
                        d_ = {}
                        d_["pso"], d_["pob"] = s.ps("b")
                        if not win:
                            d_["acs"] = {"pool": acc[ka % 8], "dve": acc[(ka + 1) % 8]}
                            d_["acn"] = {"pool": 0, "dve": 0}
                            ka += 2
                        stt[(qi, e_)] = d_
                    d_ = stt[(qi, e_)]
                    pss, psb_ = s.ps()
                    s.mm(pss[:, :nq], kt[pr, kb * 128:(kb + 1) * 128], qt[pr, qs], True, True, reads=[ktb, qtb], writes=[psb_], grp=("q", ku, cur_it[0] // 2))
                    p_, p_b = pT[kp % 8]
                    kp += 1
                    s.op("act", ins("activation", out=p_[:, :nq], in_=pss[:, :nq], func=AF.Exp, scale=0.125), reads=[psb_], writes=[p_b])
                    if mi is not None:
                        s.op("pool" if (kp % 2) else "dve", ins("tensor_tensor", out=p_[:, :nq], in0=p_[:, :nq], in1=masks[:, mi, :nq], op=ALU.mult),
                             reads=[p_b, cbuf], writes=[p_b])
                    if not win:
                        eng = "pool" if (i % 8) in (1, 4, 6) else "dve"
                        ac, acb = d_["acs"][eng]
                        if d_["acn"][eng] == 0:
                            s.op(eng, ins("tensor_copy", out=ac[:, :nq], in_=p_[:, :nq]), reads=[p_b], writes=[acb])
                        else:
                            s.op(eng, ins("tensor_tensor", out=ac[:, :nq], in0=ac[:, :nq], in1=p_[:, :nq], op=ALU.add), reads=[p_b, acb], writes=[acb])
                        d_["acn"][eng] += 1
                    d_[("p", i)] = (p_, p_b)

                def stage2(tk_):
                    nonlocal kr
                    qi, e_, i, kb, mi, lastb = tk_
                    qs = slice(qi * nq, (qi + 1) * nq)
                    pr = slice(e_ * 64, (e_ + 1) * 64)
                    d_ = stt[(qi, e_)]
                    pso, pob = d_["pso"], d_["pob"]
                    p_, p_b = d_.pop(("p", i))
                    s.mm(pso[:, :nq], va[:, kb, :], p_[:, :nq], i == 0, lastb, reads=[vab, p_b], writes=[pob], grp=("v", ku, cur_it[0] // 2))
                    if not lastb:
                        return
                    if win:
                        hh = u * 2 + e_
                        r_, r_b = rec[kr % 4]
                        kr += 1
                        s.op("dve", ins("tensor_scalar", out=r_[64:128, :nq], in0=pso[64:128, :nq], scalar1=esink[64:128, hh:hh + 1], scalar2=None, op0=ALU.add),
                             reads=[pob, cbuf], writes=[r_b])
                        s.op("dve", ins("reciprocal", out=r_[64:128, :nq], in_=r_[64:128, :nq]), reads=[r_b], writes=[r_b])
                        s.op("dve", ins("tensor_tensor", out=ot[pr, qs], in0=pso[0:64, :nq], in1=r_[64:128, :nq], op=ALU.mult),
                             reads=[pob, r_b], writes=[otb])
                        del stt[(qi, e_)]
                        return
                    acl = [d_["acs"][e2] for e2 in ("dve", "pool") if d_["acn"][e2] > 0]
                    o_, o_b = osb[kr % 8]
                    r_, r_b = rec[kr % 8]
                    kr += 1
                    s.op("dve", ins("tensor_copy", out=o_[:, :nq], in_=pso[:, :nq]), reads=[pob], writes=[o_b])
                    d_["o"] = (o_, o_b)
                    key = (qi, e_)

                    def st1():
                        psd, pdb = s.ps()
                        for ai, (ac, acb) in enumerate(acl):
                            s.mm(psd[:, :nq], ones_f[:], ac[:, :nq], ai == 0, ai == len(acl) - 1, reads=[acb, cb_const], writes=[pdb])
                        d_["psd"] = (psd, pdb)
                        defer(2, st2)

                    def st2():
                        psd, pdb = d_["psd"]
                        s.op("dve", ins("reciprocal", out=r_[:, :nq], in_=psd[:, :nq]), reads=[pdb], writes=[r_b])
                        defer(4, st3)

                    def st3():
                        s.op("dve", ins("tensor_tensor", out=o_[:, :nq], in0=o_[:, :nq], in1=r_[:, :nq], op=ALU.mult), reads=[o_b, r_b], writes=[o_b])
                        d_["done"] = True
                        if e_ == 1 or stt[(qi, 1)].get("done") if (qi, 1) in stt else False:
                            pass
                        if (qi, 0) in stt and (qi, 1) in stt and stt[(qi, 0)].get("done") and stt[(qi, 1)].get("done"):
                            defer(1, st4)

                    def st4():
                        (o0, o0b) = stt[(qi, 0)]["o"]
                        (o1, o1b) = stt[(qi, 1)]["o"]
                        s.op("dve", ins("scalar_tensor_tensor", out=o0[:, :nq], in0=o1[:, :nq], scalar=lam[:, 2:3], in1=o0[:, :nq], op0=ALU.mult, op1=ALU.add),
                             reads=[o0b, o1b, cbuf], writes=[o0b])
                        sq_, sq_b = sqd[qi % 4]
                        s.op("act", ins("activation", out=sq_[:, :nq], in_=o0[:, :nq], func=AF.Square), reads=[o0b], writes=[sq_b])
                        defer(2, st5)

                    def st5():
                        sq_, sq_b = sqd[qi % 4]
                        psn, pnb = s.ps()
                        s.mm(psn[:, :nq], ones_bf[:], sq_[:, :nq], True, True, reads=[sq_b, cb_const], writes=[pnb])
                        stt[(qi, 1)]["psn"] = (psn, pnb)
                        defer(2, st6)

                    def st6():
                        (o1, o1b) = stt[(qi, 1)]["o"]
                        psn, pnb = stt[(qi, 1)]["psn"]
                        s.op("act", ins("activation", out=o1[:, :nq], in_=psn[:, :nq], func=AF.Sqrt, scale=1.0 / 128, bias=EPSB[:, 0:1]), reads=[pnb, o1b], writes=[o1b])
                        defer(2, st7)

                    def st7():
                        (o1, o1b) = stt[(qi, 1)]["o"]
                        s.op("dve", ins("reciprocal", out=o1[:, :nq], in_=o1[:, :nq]), reads=[o1b], writes=[o1b])
                        defer(4, st8)

                    def st8():
                        (o0, o0b) = stt[(qi, 0)]["o"]
                        (o1, o1b) = stt[(qi, 1)]["o"]
                        s.op("dve", ins("scalar_tensor_tensor", out=ot[:, qs], in0=o0[:, :nq], scalar=lam[:, 3:4], in1=o1[:, :nq], op0=ALU.mult, op1=ALU.mult),
                             reads=[o0b, o1b, cbuf], writes=[otb])
                        del stt[(qi, 0)]
                        del stt[(qi, 1)]

                    defer(2, st1)

                pend = []
                cur_it = [0]

                def defer(dl, fn):
                    pend.append((cur_it[0] + dl, fn))

                def run_pending(flush=False):
                    while True:
                        ready = [p for p in pend if flush or p[0] <= cur_it[0]]
                        if not ready:
                            break
                        for p in ready:
                            pend.remove(p)
                        for due, fn in ready:
                            fn()
                        if not flush:
                            break

                for t2_ in range(0, len(tasks) + LOOK + 1, 2):
                    for t_ in (t2_, t2_ + 1):
                        cur_it[0] = t_
                        if t_ < len(tasks):
                            stage1(tasks[t_])
                    for t_ in (t2_, t2_ + 1):
                        cur_it[0] = t_
                        if LOOK <= t_ < len(tasks) + LOOK:
                            stage2(tasks[t_ - LOOK])
                    run_pending()
                while pend:
                    cur_it[0] += 1
                    run_pending(flush=True)
                s.dma(OS[:, u, q0:q0 + L], ot[:, 0:L], reads=[otb], writes=[OS_b[si][u]])
        return OS_b

    def outproj_phase(l, Wsrc, nk, Osrc, O_bufs_fn, Xsrc, Xsrc_b):
        s.barrier()
        s.release()
        Wo = s.alloc([128, nk, 1024], BF16, "wo")
        wob = load_w(Wo, Wsrc, nk)
        rctx = ResCtx()
        oin = [(s.alloc([128, nk, 512], BF16, "oin"), Buf()) for _ in range(2)]
        for ti in range(NT):
            t0, n, cond, _, _ = TILES[ti]
            o_, o_b = oin[ti % 2]
            s.dma(o_[:, :, :n], Osrc[:, :, t0:t0 + n], writes=[o_b])
            outproj_resid(rctx, l, 2, ti, Wo, wob, nk, o_, [o_b], Xsrc, Xsrc_b, X, X_b)

    def gla_layer(l, Xsrc, Xsrc_b):
        s.barrier()
        s.release()
        Wqk = s.alloc([128, 8, 1024], BF16, "gwqk")
        wqb = load_w(Wqk, w1_qk, 8)
        Wv = s.alloc([128, 8, 1024], BF16, "gwv")
        wvb = load_w(Wv, w1_v, 8)
        Wr = s.alloc([128, 8, 1024], BF16, "gwr")
        wrb = load_w(Wr, w1_r, 8)
        Wg1 = s.alloc([128, 8, 32], BF16, "gwg1")
        wg1b = load_w(Wg1, w1_g1, 8)
        Wg2 = s.alloc([32, 1024], BF16, "gwg2")
        cbuf = Buf()
        s.dma(Wg2[:], w1_g2, writes=[cbuf], eng="pool")
        gb = s.alloc([128, 8], F32, "ggb")
        s.dma(gb[:], w1_gb, writes=[cbuf])
        s.op("dve", ins("tensor_scalar", out=gb[:], in0=gb[:], scalar1=-1.0, scalar2=None, op0=ALU.mult), reads=[cbuf], writes=[cbuf])
        ELt = s.alloc([128, 2, 4, 72], F32, "el")
        nctx = NormCtx(1)
        qkf = [(s.alloc([128, 8, 512], F32, "qkf"), Buf()) for _ in range(1)]
        lg = nctx.xt[0]
        cs_ = (s.alloc([128, 8, 512], F32, "cs"), Buf())
        c2 = nctx.tmp
        ex = [(s.alloc([128, 512], F32, "ex"), Buf()) for _ in range(3)]
        t1b_ = (s.alloc([32, 512], BF16, "t1b"), Buf())
        qd_t = [(s.alloc([128, 2, 4, 512], BF16, "qd"), Buf()) for _ in range(1)]
        kd_t = [(s.alloc([128, 2, 4, 512], BF16, "kd"), Buf()) for _ in range(1)]
        krT = [(s.alloc([128, 512], BF16, "krT"), Buf()) for _ in range(2)]
        krt = [(s.alloc([128, 4, 8, 128], BF16, "krt"), Buf()) for _ in range(1)]
        rt = [(s.alloc([128, 8, 512], BF16, "rt"), Buf()) for _ in range(1)]
        vt = [(s.alloc([128, 4, 1024], BF16, "vt"), Buf()) for _ in range(1)]
        G_b = [Buf() for _ in range(NT)]
        kx = 0
        kk = 0
        for ti in range(NT):
            t0, n, cond, s0, s1 = TILES[ti]
            nch = n // 64
            h, hb = norm_tile(nctx, l, 0, ti, Xsrc, Xsrc_b)
            qf, qfb = qkf[0]
            for cc in range(8):
                ps, pb = s.ps()
                for kc in range(8):
                    s.mm(ps[:, :n], Wqk[:, kc, cc * 128:(cc + 1) * 128], h[:, kc, :n], kc == 0, kc == 7, reads=[wqb[kc], hb], writes=[pb])
                sc_ = (128.0 ** -0.5) if cc < 4 else 1.0
                s.op("act", ins("activation", out=qf[:, cc, :n], in_=ps[:, :n], func=AF.Identity, scale=sc_), reads=[pb], writes=[qfb])
            r_, r_b = rt[0]
            for cc in range(8):
                ps, pb = s.ps()
                for kc in range(8):
                    s.mm(ps[:, :n], Wr[:, kc, cc * 128:(cc + 1) * 128], h[:, kc, :n], kc == 0, kc == 7, reads=[wrb[kc], hb], writes=[pb])
                s.op("act", ins("activation", out=r_[:, cc, :n], in_=ps[:, :n], func=AF.Silu), reads=[pb], writes=[r_b])
            s.dma(RS[:, :, t0:t0 + n], r_[:, :, :n], reads=[r_b], writes=[G_b[ti]])
            v_, v_b = vt[0]
            for tbk in range(n // 128):
                for vg in range(2):
                    ps, pb = s.ps()
                    for kc in range(8):
                        s.mm(ps[:, :512], h[:, kc, tbk * 128:(tbk + 1) * 128], Wv[:, kc, vg * 512:(vg + 1) * 512], kc == 0, kc == 7, reads=[wvb[kc], hb], writes=[pb])
                    s.op("act", ins("activation", out=v_[:, tbk, vg * 512:(vg + 1) * 512], in_=ps[:, :512], func=AF.Identity), reads=[pb], writes=[v_b])
            s.dma(rows_view(VS, 1024, t0, n // 128, 0, 1024), v_[:, 0:n // 128, :], reads=[v_b], writes=[G_b[ti]])
            ps, pb = s.ps()
            for kc in range(8):
                s.mm(ps[0:32, :n], Wg1[:, kc, :], h[:, kc, :n], kc == 0, kc == 7, reads=[wg1b[kc], hb], writes=[pb])
            t1_, t1bb = t1b_
            s.op("act", ins("activation", out=t1_[:, :n], in_=ps[0:32, :n], func=AF.Identity), reads=[pb], writes=[t1bb])
            lgt, lgb = lg
            for j in range(8):
                ps, pb = s.ps()
                s.mm(ps[:, :n], Wg2[:, j * 128:(j + 1) * 128], t1_[:, :n], True, True, reads=[cbuf, t1bb], writes=[pb])
                s.op("act", ins("activation", out=lgt[:, j, :n], in_=ps[:, :n], func=AF.Exp, scale=-1.0, bias=gb[:, j:j + 1]), reads=[pb, cbuf], writes=[lgb])
            s.op("act", ins("activation", out=lgt[:, :, :n], in_=lgt[:, :, :n], func=AF.Ln, bias=1.0, scale=1.0), reads=[lgb], writes=[lgb])
            cst, csb = cs_
            for j in range(8):
                s.op("dve", ins("tensor_tensor_scan", out=cst[:, j, :n], data0=m64[:, :n], data1=lgt[:, j, :n], initial=0.0, op0=ALU.mult, op1=ALU.add),
                     reads=[lgb, cb_const], writes=[csb])
            c2t, c2b = c2
            csv = cst[:, :, :n].rearrange("p j (c t) -> p j c t", t=64)
            lgv = lgt[:, :, :n].rearrange("p j (c t) -> p j c t", t=64)
            c2v = c2t[:, :, :n].rearrange("p j (c t) -> p j c t", t=64)
            nf = 4
            s.op("dve", ins("tensor_tensor", out=c2v[:, 0:nf, :, :], in0=csv[:, 0:nf, :, :], in1=csv[:, 0:nf, :, 63:64].to_broadcast([128, nf, nch, 64]), op=ALU.subtract),
                 reads=[csb], writes=[c2b])
            s.op("dve", ins("tensor_tensor", out=c2v[:, nf:2 * nf, :, :], in0=lgv[:, nf:2 * nf, :, :], in1=csv[:, nf:2 * nf, :, :], op=ALU.subtract),
                 reads=[csb, lgb], writes=[c2b])
            ci0 = t0 // 64
            s.op("act", ins("activation", out=ELt[:, :, :, ci0:ci0 + nch].rearrange("p d h c -> p (d h) c"), in_=cst[:, :, :n].rearrange("p j (c t) -> p j c t", t=64)[:, :, :, 63],
                                               func=AF.Exp, scale=-1.0 / 16), reads=[csb], writes=[cbuf])
            s.op("dve", ins("tensor_tensor", out=lgv[:, nf:2 * nf, :, :], in0=c2v[:, nf:2 * nf, :, :], in1=csv[:, nf:2 * nf, :, 63:64].to_broadcast([128, nf, nch, 64]), op=ALU.add),
                 reads=[csb, c2b, lgb], writes=[lgb])
            qd_, qdb = qd_t[0]
            kd_, kdb = kd_t[0]
            krt_, krtb = krt[0]
            for j in range(8):
                d = j // 4
                hh = j % 4
                ea, eab = ex[0]
                eb, ebb = ex[1]
                ec, ecb = ex[2]
                csrc = cst if j < 4 else lgt
                s.op("act", ins("activation", out=ea[:, :n], in_=csrc[:, j, :n], func=AF.Exp, scale=-1.0 / 16), reads=[csb, lgb], writes=[eab])
                s.op("act", ins("activation", out=eb[:, :n], in_=csrc[:, j, :n], func=AF.Exp, scale=1.0 / 16), reads=[csb, lgb], writes=[ebb])
                s.op("act", ins("activation", out=ec[:, :n], in_=c2t[:, j, :n], func=AF.Exp, scale=1.0 / 16), reads=[c2b], writes=[ecb])
                s.op("dve", ins("tensor_tensor", out=qd_[:, d, hh, :n], in0=qf[:, hh, :n], in1=ea[:, :n], op=ALU.mult), reads=[qfb, eab], writes=[qdb])
                s.op("dve", ins("tensor_tensor", out=kd_[:, d, hh, :n], in0=qf[:, 4 + hh, :n], in1=eb[:, :n], op=ALU.mult), reads=[qfb, ebb], writes=[kdb])
                kT, kTb = krT[kx % 2]
                kx += 1
                s.op("pool", ins("tensor_tensor", out=kT[:, :n], in0=qf[:, 4 + hh, :n], in1=ec[:, :n], op=ALU.mult), reads=[qfb, ecb], writes=[kTb])
                for tbk in range(n // 128):
                    ps, pb = s.ps()
                    psv = ps[:].bitcast(BF16)
                    s.op("pe", ins("transpose", psv[:, 0:128], kT[:, tbk * 128:(tbk + 1) * 128], ident_bf[:]), reads=[kTb, cb_const], writes=[pb])
                    s.op("act", ins("activation", out=krt_[:, tbk, j, :], in_=psv[:, 0:128], func=AF.Identity), reads=[pb], writes=[krtb])
            for d in range(2):
                s.dma(QD[d, :, :, t0:t0 + n], qd_[:, d, :, :n], reads=[qdb], writes=[G_b[ti]])
                s.dma(KD[d, :, :, t0:t0 + n], kd_[:, d, :, :n], reads=[kdb], writes=[G_b[ti]])
            s.dma(kr_view(t0, n // 128, 0, 8), krt_[:, 0:n // 128, :, :], reads=[krtb], writes=[G_b[ti]])
        ELd = dscr("ELd", [128, 2, 4, 72], F32)
        elb = Buf()
        s.dma(ELd, ELt[:], reads=[cbuf], writes=[elb])
        s.barrier()
        s.release()
        EL = s.alloc([128, 2, 4, 72], F32, "el2")
        elb2 = Buf()
        s.dma(EL[:], ELd, writes=[elb2])
        S = [(s.alloc([128, 4, 256], F32, "S"), [Buf() for _ in range(4)]) for _ in range(2)]
        Sb = [(s.alloc([128, 4, 256], BF16, "Sb"), [Buf() for _ in range(4)]) for _ in range(2)]
        qd_t = [[(s.alloc([128, 4, 512], BF16, "qd"), Buf()) for _ in range(2)] for _ in range(2)]
        kd_t = [[(s.alloc([128, 4, 512], BF16, "kd"), Buf()) for _ in range(2)] for _ in range(2)]
        kr_t = [[(s.alloc([128, 4, 4, 128], BF16, "kr"), Buf()) for _ in range(2)] for _ in range(2)]
        v_t = [[(s.alloc([128, 4, 1024], BF16, "v"), Buf()) for _ in range(2)] for _ in range(2)]
        of_t = [[(s.alloc([128, 8, 512], BF16, "of"), Buf()) for _ in range(2)] for _ in range(2)]
        attm = [(s.alloc([128, 64], BF16, "attm"), Buf()) for _ in range(8)]
        OD_b = [[Buf() for _ in range(NT)] for _ in range(2)]
        ODs = [OS, OS2]
        kat = 0
        SEQT = [list(range(8)), [8], [9]]
        states_in = [s1f, s1b]
        states_out = [o_gf, o_gb]
        for si, tiles in enumerate(SEQT):
            for d in range(2):
                St, Stb = S[d]
                Sbt, Sbb = Sb[d]
                if si == 0:
                    s.dma(St[:], states_in[d], writes=Stb)
                else:
                    s.op("pool", ins("memset", St[:], 0.0), writes=Stb)
                for hh in range(4):
                    s.op("act", ins("activation", out=Sbt[:, hh, :], in_=St[:, hh, :], func=AF.Identity), reads=[Stb[hh]], writes=[Sbb[hh]])
            nt_ = len(tiles)
            for step in range(nt_):
                cur = {}
                for d in range(2):
                    ti = tiles[step] if d == 0 else tiles[nt_ - 1 - step]
                    t0, n, cond, s0, s1 = TILES[ti]
                    qd_, qdb = qd_t[d][step % 2]
                    kd_, kdb = kd_t[d][step % 2]
                    kr_, krb = kr_t[d][step % 2]
                    v_, vb_ = v_t[d][step % 2]
                    of_, ofb = of_t[d][step % 2]
                    s.dma(qd_[:, :, :n], QD[d, :, :, t0:t0 + n], writes=[qdb])
                    s.dma(kd_[:, :, :n], KD[d, :, :, t0:t0 + n], writes=[kdb])
                    s.dma(kr_[:, 0:n // 128, :, :], kr_view(t0, n // 128, d * 4, 4), writes=[krb])
                    s.dma(v_[:, 0:n // 128, :], rows_view(VS, 1024, t0, n // 128, 0, 1024), writes=[vb_])
                    cur[d] = (ti, t0, n, qd_, qdb, kd_, kdb, kr_, krb, v_, vb_, of_, ofb)
                nch = cur[0][2] // 64
                for kc_ in range(nch):
                    units = []
                    for d in range(2):
                        ti, t0, n, qd_, qdb, kd_, kdb, kr_, krb, v_, vb_, of_, ofb = cur[d]
                        k = kc_ if d == 0 else nch - 1 - kc_
                        for hh in range(4):
                            units.append(dict(d=d, hh=hh, k=k, ci=t0 // 64 + k, tbk=k // 2, hp=slice((k % 2) * 64, (k % 2) * 64 + 64),
                                              cs64=slice(k * 64, (k + 1) * 64), qd_=qd_, qdb=qdb, kd_=kd_, kdb=kdb, kr_=kr_, krb=krb, v_=v_, vb_=vb_, of_=of_, ofb=ofb))
                    psA, pbA = s.ps()
                    for ui, u_ in enumerate(units):
                        s.mm(psA[0:64, ui * 64:(ui + 1) * 64], u_["kd_"][:, u_["hh"], u_["cs64"]], u_["qd_"][:, u_["hh"], u_["cs64"]], True, True,
                             reads=[u_["kdb"], u_["qdb"]], writes=[pbA], grp=("ga", kat))
                    for ui, u_ in enumerate(units):
                        am, amb = attm[ui]
                        u_["am"] = (am, amb)
                        s.op("dve", ins("tensor_tensor", out=am[u_["hp"], :], in0=psA[0:64, ui * 64:(ui + 1) * 64], in1=tri_bf[0:64, 2 + u_["d"], 0:64], op=ALU.mult),
                             reads=[pbA, cb_const], writes=[amb])
                    psO = [s.ps("b") for _ in range(2)]
                    for ui, u_ in enumerate(units):
                        pso, pob = psO[ui // 4]
                        am, amb = u_["am"]
                        St, Stb = S[u_["d"]]
                        Sbt, Sbb = Sb[u_["d"]]
                        hh, hp, tbk = u_["hh"], u_["hp"], u_["tbk"]
                        c0 = (ui % 4) * 128
                        for vc in range(2):
                            s.mm(pso[:, c0 + vc * 64:c0 + (vc + 1) * 64], u_["v_"][hp, tbk, hh * 256 + vc * 128:hh * 256 + (vc + 1) * 128], am[hp, :], True, False,
                                 reads=[u_["vb_"], amb], writes=[pob])
                            s.mm(pso[:, c0 + vc * 64:c0 + (vc + 1) * 64], Sbt[:, hh, vc * 128:(vc + 1) * 128], u_["qd_"][:, hh, u_["cs64"]], False, True,
                                 reads=[Sbb[hh], u_["qdb"]], writes=[pob])
                    for ui, u_ in enumerate(units):
                        pso, pob = psO[ui // 4]
                        c0 = (ui % 4) * 128
                        hh = u_["hh"]
                        s.op("act", ins("activation", out=u_["of_"][:, hh * 2:hh * 2 + 2, u_["cs64"]], in_=pso[:, c0:c0 + 128].rearrange("p (v t) -> p v t", t=64), func=AF.Identity),
                             reads=[pob], writes=[u_["ofb"]])
                    psS = [s.ps() for _ in range(4)]
                    for ui, u_ in enumerate(units):
                        ps2, pb2 = psS[ui // 2]
                        c0 = (ui % 2) * 256
                        hh, hp, tbk = u_["hh"], u_["hp"], u_["tbk"]
                        s.mm(ps2[:, c0:c0 + 256], u_["kr_"][hp, tbk, hh, :], u_["v_"][hp, tbk, hh * 256:(hh + 1) * 256], True, True, reads=[u_["krb"], u_["vb_"]], writes=[pb2])
                    for ui, u_ in enumerate(units):
                        ps2, pb2 = psS[ui // 2]
                        c0 = (ui % 2) * 256
                        hh, d = u_["hh"], u_["d"]
                        St, Stb = S[d]
                        Sbt, Sbb = Sb[d]
                        ci = u_["ci"]
                        s.op("dve", ins("scalar_tensor_tensor", out=St[:, hh, :], in0=St[:, hh, :], scalar=EL[:, d, hh, ci:ci + 1], in1=ps2[:, c0:c0 + 256], op0=ALU.mult, op1=ALU.add),
                             reads=[pb2, elb2, Stb[hh]], writes=[Stb[hh]])
                        s.op("act", ins("activation", out=Sbt[:, hh, :], in_=St[:, hh, :], func=AF.Identity), reads=[Stb[hh]], writes=[Sbb[hh]])
                    kat += 1
                for d in range(2):
                    ti, t0, n, qd_, qdb, kd_, kdb, kr_, krb, v_, vb_, of_, ofb = cur[d]
                    s.dma(ODs[d][:, :, t0:t0 + n], of_[:, :, :n], reads=[ofb], writes=[OD_b[d][ti]])
            if si > 0:
                for d in range(2):
                    s.dma(states_out[d][si - 1], S[d][0][:], reads=S[d][1])
        s.barrier()
        s.release()
        Wo = s.alloc([128, 8, 1024], BF16, "wo")
        wob = load_w(Wo, w1_o, 8)
        gn = s.alloc([128, 2], F32, "gn")
        gnb = Buf()
        s.dma(gn[:], w1_gn, writes=[gnb])
        rctx = ResCtx()
        oa = [(s.alloc([128, 8, 512], BF16, "oa"), Buf()) for _ in range(2)]
        ob_ = [(s.alloc([128, 8, 512], BF16, "ob"), Buf()) for _ in range(2)]
        rr = [(s.alloc([128, 8, 512], BF16, "rr"), Buf()) for _ in range(2)]
        osum = (s.alloc([128, 8, 512], F32, "osum"), Buf())
        sq = (s.alloc([128, 8, 512], BF16, "sq"), Buf())
        rs_ = [(s.alloc([128, 512], F32, "rs"), Buf()) for _ in range(2)]
        ofin = [(s.alloc([128, 8, 512], BF16, "ofin"), Buf()) for _ in range(2)]
        for ti in range(NT):
            t0, n, cond, _, _ = TILES[ti]
            a_, a_b = oa[ti % 2]
            b_, b_b = ob_[ti % 2]
            r_, r_b = rr[ti % 2]
            s.dma(a_[:, :, :n], OS[:, :, t0:t0 + n], writes=[a_b])
            s.dma(b_[:, :, :n], OS2[:, :, t0:t0 + n], writes=[b_b])
            s.dma(r_[:, :, :n], RS[:, :, t0:t0 + n], writes=[r_b])
            os_, osb_ = osum
            sq_, sqb_ = sq
            s.op("dve", ins("tensor_tensor", out=os_[:, :, :n], in0=a_[:, :, :n], in1=b_[:, :, :n], op=ALU.add), reads=[a_b, b_b], writes=[osb_])
            s.op("act", ins("activation", out=sq_[:, :, :n], in_=os_[:, :, :n], func=AF.Square), reads=[osb_], writes=[sqb_])
            f_, f_b = ofin[ti % 2]
            for hh in range(4):
                ps, pb = s.ps()
                for vc in range(2):
                    s.mm(ps[:, :n], ones_bf[:], sq_[:, hh * 2 + vc, :n], vc == 0, vc == 1, reads=[sqb_, cb_const], writes=[pb])
                rt_, rtb = rs_[hh % 2]
                s.op("act", ins("activation", out=rt_[:, :n], in_=ps[:, :n], func=AF.Sqrt, scale=1.0 / 256, bias=EPSB[:, 0:1]), reads=[pb], writes=[rtb])
                s.op("dve", ins("reciprocal", out=rt_[:, :n], in_=rt_[:, :n]), reads=[rtb], writes=[rtb])
                for vc in range(2):
                    c = hh * 2 + vc
                    s.op("dve", ins("tensor_tensor", out=os_[:, c, :n], in0=os_[:, c, :n], in1=rt_[:, :n], op=ALU.mult), reads=[osb_, rtb], writes=[osb_])
                    s.op("dve", ins("scalar_tensor_tensor", out=f_[:, c, :n], in0=os_[:, c, :n], scalar=gn[:, vc:vc + 1], in1=r_[:, c, :n], op0=ALU.mult, op1=ALU.mult),
                         reads=[osb_, gnb, r_b], writes=[f_b])
            outproj_resid(rctx, l, 2, ti, Wo, wob, 8, f_, [f_b], Xsrc, Xsrc_b, X, X_b)

    def ssd_layer(l, Xsrc, Xsrc_b):
        XBu = U
        s.barrier()
        s.release()
        Wz = s.alloc([128, 8, 2048], BF16, "wz")
        wzb = load_w(Wz, w3_z, 8)
        Wx = s.alloc([128, 8, 3072], BF16, "wx")
        wxb = load_w(Wx, w3_xbc, 8)
        Wdt = s.alloc([128, 8, 64], BF16, "wdt")
        wdb = load_w(Wdt, w3_dt, 8)
        vec = s.alloc([128, 5, 32], F32, "vec")
        cbuf = Buf()
        s.dma(vec[:], ssd_vec.rearrange("a b -> (a b)").partition_broadcast(128).rearrange("p (a b) -> p a b", a=5), writes=[cbuf])
        avec = s.alloc([128, 64], F32, "avec")
        s.op("act", ins("activation", out=avec[:], in_=vec[:, 0:2, :].rearrange("p a b -> p (a b)"), func=AF.Exp), reads=[cbuf], writes=[cbuf])
        s.op("dve", ins("tensor_scalar", out=avec[:], in0=avec[:], scalar1=-1.0, scalar2=None, op0=ALU.mult), reads=[cbuf], writes=[cbuf])
        nctx = NormCtx(1)
        xbt = (s.alloc([128, 24, 512], BF16, "xbt"), [Buf() for _ in range(24)])
        zt = (s.alloc([128, 4, 2048], BF16, "zt"), Buf())
        dtt = (s.alloc([128, 4, 2, 64], F32, "dtt"), Buf())
        dtmp = (s.alloc([128, 64], F32, "dtmp"), Buf())
        S_b = [Buf() for _ in range(NT)]
        ev = 0
        for ti in range(NT):
            t0, n, cond, s0, s1 = TILES[ti]
            nb = n // 128
            h, hb = norm_tile(nctx, l, 0, ti, Xsrc, Xsrc_b)
            xb_, xbb = xbt
            for cc in range(24):
                ps, pb = s.ps()
                for kc in range(8):
                    s.mm(ps[:, :n], Wx[:, kc, cc * 128:(cc + 1) * 128], h[:, kc, :n], kc == 0, kc == 7, reads=[wxb[kc], hb], writes=[pb])
                if ev % 2 == 0:
                    s.op("act", ins("activation", out=xb_[:, cc, :n], in_=ps[:, :n], func=AF.Identity), reads=[pb], writes=[xbb[cc]])
                else:
                    s.op("dve", ins("tensor_copy", out=xb_[:, cc, :n], in_=ps[:, :n]), reads=[pb], writes=[xbb[cc]])
                ev += 1
            s.dma(XBu[:, 0:24, t0:t0 + n], xb_[:, :, :n], reads=xbb, writes=[S_b[ti]])
            z_, z_b = zt
            d_, d_b = dtt
            for tbk in range(nb):
                for vg in range(4):
                    ps, pb = s.ps()
                    for kc in range(8):
                        s.mm(ps[:, :512], h[:, kc, tbk * 128:(tbk + 1) * 128], Wz[:, kc, vg * 512:(vg + 1) * 512], kc == 0, kc == 7, reads=[wzb[kc], hb], writes=[pb])
                    s.op("act", ins("activation", out=z_[:, tbk, vg * 512:(vg + 1) * 512], in_=ps[:, :512], func=AF.Silu), reads=[pb], writes=[z_b])
                ps, pb = s.ps()
                for kc in range(8):
                    s.mm(ps[:, 0:64], h[:, kc, tbk * 128:(tbk + 1) * 128], Wdt[:, kc, :], kc == 0, kc == 7, reads=[wdb[kc], hb], writes=[pb])
                tm, tmb = dtmp
                s.op("dve", ins("tensor_tensor", out=tm[:], in0=ps[:, 0:64], in1=vec[:, 2:4, :].rearrange("p a b -> p (a b)"), op=ALU.add), reads=[pb, cbuf], writes=[tmb])
                s.op("act", ins("activation", out=tm[:], in_=tm[:], func=AF.Exp), reads=[tmb], writes=[tmb])
                s.op("act", ins("activation", out=d_[:, tbk, 0, :], in_=tm[:], func=AF.Ln, bias=1.0, scale=1.0), reads=[tmb], writes=[d_b])
                s.op("dve", ins("tensor_tensor", out=d_[:, tbk, 1, :], in0=d_[:, tbk, 0, :], in1=avec[:], op=ALU.mult), reads=[d_b, cbuf], writes=[d_b])
            s.dma(rows_view(ZS, 2048, t0, nb, 0, 2048), z_[:, 0:nb, :], reads=[z_b], writes=[S_b[ti]])
            s.dma(AP(DTS.tensor, t0 * 128, [[128, 128], [128 * 128, nb], [1, 128]]), d_[:, 0:nb, :, :].rearrange("p b a c -> p b (a c)"), reads=[d_b], writes=[S_b[ti]])
        chk(20)
        s.barrier()
        s.release()
        cw = s.alloc([128, 3, 24], F32, "scw")
        cbias = s.alloc([128, 24], F32, "scb")
        cwb = Buf()
        s.dma(cw[:], ssd_cw, writes=[cwb])
        s.dma(cbias[:], ssd_cb, writes=[cwb])
        ug = s.alloc([128, 24, 514], BF16, "sug")
        ugb = [Buf() for _ in range(2)]
        xc = (s.alloc([128, 24, 512], BF16, "xc"), [Buf() for _ in range(24)])
        tmps = [(s.alloc([128, 512], F32, "st"), Buf()) for _ in range(2)]
        xtt = (s.alloc([128, 4, 2048], BF16, "xtt"), Buf())
        btt = (s.alloc([128, 4, 512], BF16, "btt"), Buf())
        kp = 0
        for ti in range(NT):
            t0, n, cond, s0, s1 = TILES[ti]
            nb = n // 128
            lo = (t0 - 1) >= s0
            hi = (t0 + n) < s1
            for grp in range(2):
                gs_ = slice(grp * 12, (grp + 1) * 12)
                if not lo:
                    s.op("pool", ins("memset", ug[:, gs_, 0:1], 0.0), writes=[ugb[grp]])
                if not hi:
                    s.op("pool", ins("memset", ug[:, gs_, n + 1:n + 2], 0.0), writes=[ugb[grp]])
                c0 = 0 if lo else 1
                c1 = n + 2 if hi else n + 1
                s.dma(ug[:, gs_, c0:c1], XBu[:, gs_, t0 - 1 + c0:t0 - 1 + c1], writes=[ugb[grp]])
            xc_, xcb = xc
            for col in range(24):
                grp = col // 12
                tt, ttb = tmps[kp % 2]
                kp += 1
                s.op("act", ins("activation", out=tt[:, :n], in_=ug[:, col, 1:n + 1], func=AF.Identity, scale=cw[:, 1, col:col + 1], bias=cbias[:, col:col + 1]),
                     reads=[ugb[grp], cwb], writes=[ttb])
                s.op("dve", ins("scalar_tensor_tensor", out=tt[:, :n], in0=ug[:, col, 0:n], scalar=cw[:, 0, col:col + 1], in1=tt[:, :n], op0=ALU.mult, op1=ALU.add),
                     reads=[ugb[grp], cwb, ttb], writes=[ttb])
                s.op("dve", ins("scalar_tensor_tensor", out=tt[:, :n], in0=ug[:, col, 2:n + 2], scalar=cw[:, 2, col:col + 1], in1=tt[:, :n], op0=ALU.mult, op1=ALU.add),
                     reads=[ugb[grp], cwb, ttb], writes=[ttb])
                s.op("act", ins("activation", out=xc_[:, col, :n], in_=tt[:, :n], func=AF.Silu), reads=[ttb], writes=[xcb[col]])
            s.dma(QK[:, 0:8, t0:t0 + n], xc_[:, 16:24, :n], reads=xcb[16:24], writes=[S_b[ti]])
            x_, x_b = xtt
            b_, b_b = btt
            tv = 0
            for tbk in range(nb):
                for col in range(20):
                    ps, pb = s.ps()
                    psv = ps[:].bitcast(BF16)
                    s.op("pe", ins("transpose", psv[:, 0:128], xc_[:, col, tbk * 128:(tbk + 1) * 128], ident_bf[:]), reads=[xcb[col], cb_const], writes=[pb])
                    dst = x_[:, tbk, col * 128:(col + 1) * 128] if col < 16 else b_[:, tbk, (col - 16) * 128:(col - 15) * 128]
                    dbuf = x_b if col < 16 else b_b
                    if tv % 2 == 0:
                        s.op("act", ins("activation", out=dst, in_=psv[:, 0:128], func=AF.Identity), reads=[pb], writes=[dbuf])
                    else:
                        s.op("dve", ins("tensor_copy", out=dst, in_=psv[:, 0:128]), reads=[pb], writes=[dbuf])
                    tv += 1
            s.dma(rows_view(XTs, 2048, t0, nb, 0, 2048), x_[:, 0:nb, :], reads=[x_b], writes=[S_b[ti]])
            s.dma(rows_view(BTs, 512, t0, nb, 0, 512), b_[:, 0:nb, :], reads=[b_b], writes=[S_b[ti]])
        chk(21)
        s.barrier()
        s.release()
        vec = s.alloc([128, 5, 32], F32, "vec")
        negm = s.alloc([128, 2, 128], F32, "negm")
        cbuf = Buf()
        s.op("dve", ins("tensor_scalar", out=negm[:], in0=stage[:, 0:2, :], scalar1=-1.0, scalar2=30000.0, op0=ALU.add, op1=ALU.mult), reads=[bstage], writes=[cbuf])
        ST = [(s.alloc([128, 2048], F32, "ST"), [Buf() for _ in range(4)]) for _ in range(2)]
        SbT = [(s.alloc([128, 2048], BF16, "SbT"), [Buf() for _ in range(4)]) for _ in range(2)]
        bct = [(s.alloc([128, 8, 512], BF16, "bct"), Buf()) for _ in range(2)]
        xts = [(s.alloc([128, 4, 2048], BF16, "xts"), Buf()) for _ in range(2)]
        bts = [(s.alloc([128, 4, 512], BF16, "bts"), Buf()) for _ in range(2)]
        dts = [(s.alloc([128, 4, 2, 64], F32, "dts"), Buf()) for _ in range(2)]
        yts = [(s.alloc([128, 4, 2048], BF16, "yts"), Buf()) for _ in range(2)]
        cumT = [(s.alloc([128, 32], F32, "cumT"), Buf()) for _ in range(2)]
        cbT = [(s.alloc([128, 128], F32, "cbT"), Buf()) for _ in range(6)]
        lat = [(s.alloc([128, 4, 128], F32, "lat"), Buf()) for _ in range(3)]
        cB = [(s.alloc([128, 4, 128], F32, "cB"), Buf()) for _ in range(4)]
        seg = [(s.alloc([128, 4, 128], F32, "seg"), Buf()) for _ in range(3)]
        Et = [(s.alloc([128, 4, 128], F32, "Et"), Buf()) for _ in range(6)]
        ecB = [(s.alloc([128, 4, 128], F32, "ecB"), Buf()) for _ in range(8)]
        te = [(s.alloc([128, 4], F32, "te"), Buf()) for _ in range(3)]
        xs = [(s.alloc([128, 512], BF16, "xs"), Buf()) for _ in range(4)]
        Wt = [(s.alloc([128, 128], BF16, "Wt"), Buf()) for _ in range(8)]
        CEt = [(s.alloc([128, 128], BF16, "CEt"), Buf()) for _ in range(8)]
        SEQT = [list(range(8)), [8], [9]]
        st_in = [s3f, s3b]
        st_out = [o_sf, o_sb]
        Y_b = [Buf() for _ in range(NT)]
        kw = 0
        kq4 = 0
        kg = 0
        kct = 0
        for si, tiles in enumerate(SEQT):
            for d in range(2):
                St, Stb = ST[d]
                Sbt, Sbb = SbT[d]
                if si == 0:
                    s.dma(St[:], st_in[d], writes=Stb)
                else:
                    s.op("pool", ins("memset", St[:], 0.0), writes=Stb)
                for g in range(4):
                    s.op("act", ins("activation", out=Sbt[:, g * 512:(g + 1) * 512], in_=St[:, g * 512:(g + 1) * 512], func=AF.Identity), reads=[Stb[g]], writes=[Sbb[g]])
            nt_ = len(tiles)
            for step in range(nt_):
                cur = {}
                for d in range(2):
                    ti = tiles[step] if d == 0 else tiles[nt_ - 1 - step]
                    t0, n, cond, s0, s1 = TILES[ti]
                    nb = n // 128
                    bc_, bcb = bct[d]
                    x_, x_b = xts[d]
                    b_, b_b = bts[d]
                    d_, d_b = dts[d]
                    y_, y_b = yts[d]
                    s.dma(bc_[:, :, :n], QK[:, 0:8, t0:t0 + n], writes=[bcb])
                    s.dma(x_[:, 0:nb, :], rows_view(XTs, 2048, t0, nb, 0, 2048), writes=[x_b])
                    s.dma(b_[:, 0:nb, :], rows_view(BTs, 512, t0, nb, 0, 512), writes=[b_b])
                    s.dma(d_[:, 0:nb, :, :].rearrange("p b a c -> p b (a c)"), AP(DTS.tensor, t0 * 128, [[128, 128], [128 * 128, nb], [1, 128]]), writes=[d_b])
                    cur[d] = (ti, t0, n, nb)
                nbs = cur[0][3]
                batches = []
                for kc_ in range(nbs):
                    for d in range(2):
                        for g in range(4):
                            for hb_ in range(2):
                                batches.append((kc_, d, g, hb_))
                bst = {}
                LOOKB = 3

                def geo(bt_):
                    kc_, d, g, hb_ = bt_
                    ti, t0, n, nb = cur[d]
                    tbk = kc_ if d == 0 else nb - 1 - kc_
                    last = 127 if d == 0 else 0
                    d_, d_b = dts[d]
                    la = d_[:, tbk, 1, d * 32:(d + 1) * 32]
                    dtv = d_[:, tbk, 0, d * 32:(d + 1) * 32]
                    return tbk, last, slice(tbk * 128, (tbk + 1) * 128), la, dtv, d_b

                def stageA1(bt_):
                    nonlocal kq4, kg, kct
                    kc_, d, g, hb_ = bt_
                    tbk, last, tk, la, dtv, d_b = geo(bt_)
                    bc_, bcb = bct[d]
                    if g == 0 and hb_ == 0:
                        psc, pcb = s.ps()
                        s.mm(psc[:, 0:32], stage[:, d, :], la, True, True, reads=[bstage, d_b], writes=[pcb])
                        cT, cTb = cumT[kct % 2]
                        kct += 1
                        s.op("dve", ins("tensor_copy", out=cT[:], in_=psc[:, 0:32]), reads=[pcb], writes=[cTb])
                        bst[("cT", kc_, d)] = (cT, cTb)
                    if hb_ == 0:
                        ps, pb = s.ps()
                        s.mm(ps[:, 0:128], bc_[:, g, tk], bc_[:, 4 + g, tk], True, True, reads=[bcb], writes=[pb])
                        cb_, cb_b = cbT[kg % 6]
                        s.op("act", ins("activation", out=cb_[:], in_=ps[:, 0:128], func=AF.Identity), reads=[pb], writes=[cb_b])
                        bst[("grp", kc_, d, g)] = dict(cb=(cb_, cb_b), x4=xs[kg % 4], ecs=[])
                        kg += 1
                    h0 = g * 8 + hb_ * 4
                    lt, ltb = lat[kq4 % 3]
                    cb4, cb4b = cB[kq4 % 4]
                    sg, sgb = seg[kq4 % 3]
                    E_, E_b = Et[kq4 % 6]
                    ec, ecb = ecB[kq4 % 8]
                    te_, te_b = te[kq4 % 3]
                    kq4 += 1
                    bst[("w",) + bt_] = (cb4, cb4b, sg, sgb, E_, E_b, ec, ecb, te_, te_b)
                    s.op("pool", ins("tensor_tensor", out=lt[:], in0=stage[:, d, :].unsqueeze(1).to_broadcast([128, 4, 128]),
                                    in1=la[:, h0:h0 + 4].unsqueeze(2).to_broadcast([128, 4, 128]), op=ALU.mult),
                         reads=[bstage, d_b], writes=[ltb])
                    ps2, pb2 = s.ps()
                    s.mm(ps2[:, 0:512], ones_f[:], lt[:].rearrange("p a b -> p (a b)"), True, True, reads=[ltb, cb_const], writes=[pb2])
                    s.op("act", ins("activation", out=cb4[:].rearrange("p a b -> p (a b)"), in_=ps2[:, 0:512], func=AF.Identity), reads=[pb2], writes=[cb4b])

                def stageA2a(bt_):
                    kc_, d, g, hb_ = bt_
                    cT, cTb = bst[("cT", kc_, d)]
                    G = bst[("grp", kc_, d, g)]
                    cb4, cb4b, sg, sgb, E_, E_b, ec, ecb, te_, te_b = bst[("w",) + bt_]
                    h0 = g * 8 + hb_ * 4
                    G["ecs"].append((ec, ecb))
                    for hh in range(4):
                        hq = h0 + hh
                        s.op("dve", ins("scalar_tensor_tensor", out=sg[:, hh, :], in0=cb4[:, hh, :], scalar=cT[:, hq:hq + 1], in1=negm[:, d, :],
                                        op0=ALU.subtract, op1=ALU.add),
                             reads=[cb4b, cTb, cbuf], writes=[sgb])
                    s.op("act", ins("activation", out=E_[:], in_=sg[:], func=AF.Exp), reads=[sgb], writes=[E_b])
                    s.op("act", ins("activation", out=ec[:], in_=cb4[:], func=AF.Exp), reads=[cb4b], writes=[ecb])

                def stageA2b(bt_):
                    kc_, d, g, hb_ = bt_
                    tbk, last, tk, la, dtv, d_b = geo(bt_)
                    x_, x_b = xts[d]
                    G = bst[("grp", kc_, d, g)]
                    x4, x4b = G["x4"]
                    cb4, cb4b, sg, sgb, E_, E_b, ec, ecb, te_, te_b = bst.pop(("w",) + bt_)
                    h0 = g * 8 + hb_ * 4
                    s.op("pool", ins("tensor_tensor", out=te_[:], in0=E_[:, :, last], in1=dtv[:, h0:h0 + 4], op=ALU.mult),
                         reads=[E_b, d_b], writes=[te_b])
                    s.op("pool", ins("tensor_tensor",
                                    out=x4[:, hb_ * 256:(hb_ + 1) * 256].rearrange("p (a b) -> p a b", b=64),
                                    in0=x_[:, tbk, h0 * 64:(h0 + 4) * 64].rearrange("p (a b) -> p a b", b=64),
                                    in1=te_[:].unsqueeze(2).to_broadcast([128, 4, 64]), op=ALU.mult),
                         reads=[x_b, te_b], writes=[x4b])
                    bst[("bat",) + bt_] = (E_, E_b, ec, ecb)

                def stageB(bt_):
                    nonlocal kw
                    kc_, d, g, hb_ = bt_
                    tbk, last, tk, la, dtv, d_b = geo(bt_)
                    bc_, bcb = bct[d]
                    x_, x_b = xts[d]
                    b_, b_b = bts[d]
                    y_, y_b = yts[d]
                    St, Stb = ST[d]
                    Sbt, Sbb = SbT[d]
                    G = bst[("grp", kc_, d, g)]
                    cb_, cb_b = G["cb"]
                    x4, x4b = G["x4"]
                    if hb_ == 0:
                        G["psy"] = s.ps("b")
                    psy, pyb = G["psy"]
                    E_, E_b, ec, ecb = bst.pop(("bat",) + bt_)
                    h0 = g * 8 + hb_ * 4
                    for hh in range(4):
                        hq = h0 + hh
                        W_, W_b = Wt[kw % 8]
                        C_, C_b = CEt[kw % 8]
                        kw += 1
                        s.op("dve", ins("scalar_tensor_tensor", out=W_[:], in0=E_[:, hh, :], scalar=dtv[:, hq:hq + 1], in1=cb_[:], op0=ALU.mult, op1=ALU.mult),
                             reads=[E_b, d_b, cb_b], writes=[W_b])
                        s.op("pool", ins("tensor_tensor", out=C_[:], in0=bc_[:, 4 + g, tk], in1=ec[:, hh, :], op=ALU.mult),
                             reads=[bcb, ecb], writes=[C_b])
                        ycol = slice((hq % 8) * 64, (hq % 8 + 1) * 64)
                        s.mm(psy[:, ycol], W_[:], x_[:, tbk, hq * 64:(hq + 1) * 64], True, False, reads=[W_b, x_b], writes=[pyb])
                        s.mm(psy[:, ycol], C_[:], Sbt[:, hq * 64:(hq + 1) * 64], False, True, reads=[C_b, Sbb[g]], writes=[pyb])
                    if hb_ == 0:
                        return
                    s.op("act", ins("activation", out=y_[:, tbk, g * 512:(g + 1) * 512], in_=psy[:, 0:512], func=AF.Identity), reads=[pyb], writes=[y_b])
                    pss, psb_ = s.ps()
                    s.mm(pss[:, 0:512], b_[:, tbk, g * 128:(g + 1) * 128], x4[:], True, True, reads=[b_b, x4b], writes=[psb_])
                    ecs = G["ecs"]
                    for hh in range(8):
                        hq = g * 8 + hh
                        ec2, ecb2 = ecs[hh // 4]
                        s.op("dve", ins("scalar_tensor_tensor",
                                        out=St[:, hq * 64:(hq + 1) * 64], in0=St[:, hq * 64:(hq + 1) * 64], scalar=ec2[:, hh % 4, last:last + 1], in1=pss[:, hh * 64:(hh + 1) * 64], op0=ALU.mult, op1=ALU.add),
                             reads=[psb_, ecb2, Stb[g]], writes=[Stb[g]])
                    s.op("act", ins("activation", out=Sbt[:, g * 512:(g + 1) * 512], in_=St[:, g * 512:(g + 1) * 512], func=AF.Identity), reads=[Stb[g]], writes=[Sbb[g]])
                    del bst[("grp", kc_, d, g)]

                NB_ = len(batches)
                for t_ in range(NB_ + 3):
                    if t_ < NB_:
                        stageA1(batches[t_])
                    if 1 <= t_ < NB_ + 1:
                        stageA2a(batches[t_ - 1])
                    if 2 <= t_ < NB_ + 2:
                        stageA2b(batches[t_ - 2])
                    if t_ >= 3:
                        stageB(batches[t_ - 3])
                for d in range(2):
                    ti, t0, n, nb = cur[d]
                    y_, y_b = yts[d]
                    s.dma(rows_view(YD[d], 2048, t0, nb, 0, 2048), y_[:, 0:nb, :], reads=[y_b], writes=[Y_b[ti]])
            if si > 0:
                for d in range(2):
                    s.dma(st_out[d][si - 1], ST[d][0][:], reads=ST[d][1])
        chk(22)
        s.barrier()
        s.release()
        Wo = s.alloc([128, 16, 1024], BF16, "wo")
        wob = load_w(Wo, w3_o, 16)
        vec = s.alloc([128, 5, 32], F32, "vec")
        gnb = s.alloc([128, 2048], F32, "gnb")
        cbuf = Buf()
        s.dma(vec[:], ssd_vec.rearrange("a b -> (a b)").partition_broadcast(128).rearrange("p (a b) -> p a b", a=5), writes=[cbuf])
        s.dma(gnb[:], ssd_gn.partition_broadcast(128), writes=[cbuf])
        rctx = ResCtx()
        yf = (s.alloc([128, 4, 2048], BF16, "yf"), Buf())
        yb = (s.alloc([128, 4, 2048], BF16, "yb"), Buf())
        xt_ = (s.alloc([128, 4, 2048], BF16, "xt4"), Buf())
        zs = (s.alloc([128, 4, 2048], BF16, "zs"), Buf())
        yv = [(s.alloc([128, 2048], F32, "yv"), Buf()) for _ in range(2)]
        tv_ = (s.alloc([128, 2048], F32, "tv"), Buf())
        ynb = (s.alloc([128, 2048], BF16, "ynb"), Buf())
        ssq = [(s.alloc([128, 1], F32, "ssq"), Buf()) for _ in range(2)]
        oT = [(s.alloc([128, 16, 512], BF16, "oT"), Buf()) for _ in range(2)]
        kk = 0
        tv = 0
        for ti in range(NT):
            t0, n, cond, _, _ = TILES[ti]
            nb = n // 128
            s.dma(yf[0][:, 0:nb, :], rows_view(YD[0], 2048, t0, nb, 0, 2048), writes=[yf[1]])
            s.dma(yb[0][:, 0:nb, :], rows_view(YD[1], 2048, t0, nb, 0, 2048), writes=[yb[1]])
            s.dma(xt_[0][:, 0:nb, :], rows_view(XTs, 2048, t0, nb, 0, 2048), writes=[xt_[1]])
            s.dma(zs[0][:, 0:nb, :], rows_view(ZS, 2048, t0, nb, 0, 2048), writes=[zs[1]])
            o_, o_b = oT[ti % 2]
            for tbk in range(nb):
                y_, y_b = yv[kk % 2]
                sq_, sq_b = ssq[kk % 2]
                kk += 1
                t_, t_b = tv_
                s.op("dve", ins("tensor_tensor", out=y_[:], in0=yf[0][:, tbk, :], in1=yb[0][:, tbk, :], op=ALU.add), reads=[yf[1], yb[1]], writes=[y_b])
                s.op("pool", ins("tensor_tensor", out=t_[:].rearrange("p (a b) -> p a b", b=64), in0=xt_[0][:, tbk, :].rearrange("p (a b) -> p a b", b=64),
                                                              in1=vec[:, 4, :].unsqueeze(2).to_broadcast([128, 32, 64]), op=ALU.mult), reads=[xt_[1], cbuf], writes=[t_b])
                s.op("dve", ins("tensor_tensor", out=y_[:], in0=y_[:], in1=t_[:], op=ALU.add), reads=[y_b, t_b], writes=[y_b])
                s.op("dve", ins("tensor_tensor", out=y_[:], in0=y_[:], in1=zs[0][:, tbk, :], op=ALU.mult), reads=[y_b, zs[1]], writes=[y_b])
                s.op("pool", ins("memset", sq_[:], 0.0), writes=[sq_b])
                s.op("act", ins("activation", out=t_[:], in_=y_[:], func=AF.Square, accum_out=sq_[:, 0:1]), reads=[y_b, sq_b], writes=[t_b, sq_b])
                s.op("act", ins("activation", out=sq_[:], in_=sq_[:], func=AF.Sqrt, scale=1.0 / 2048, bias=EPSB[:, 0:1]), reads=[sq_b], writes=[sq_b])
                s.op("dve", ins("reciprocal", out=sq_[:], in_=sq_[:]), reads=[sq_b], writes=[sq_b])
                yn, ynb_ = ynb
                s.op("dve", ins("scalar_tensor_tensor", out=yn[:], in0=y_[:], scalar=sq_[:, 0:1], in1=gnb[:], op0=ALU.mult, op1=ALU.mult), reads=[y_b, sq_b, cbuf], writes=[ynb_])
                for c in range(16):
                    ps, pb = s.ps()
                    psv = ps[:].bitcast(BF16)
                    s.op("pe", ins("transpose", psv[:, 0:128], yn[:, c * 128:(c + 1) * 128], ident_bf[:]), reads=[ynb_, cb_const], writes=[pb])
                    if tv % 2 == 0:
                        s.op("act", ins("activation", out=o_[:, c, tbk * 128:(tbk + 1) * 128], in_=psv[:, 0:128], func=AF.Identity), reads=[pb], writes=[o_b])
                    else:
                        s.op("dve", ins("tensor_copy", out=o_[:, c, tbk * 128:(tbk + 1) * 128], in_=psv[:, 0:128]), reads=[pb], writes=[o_b])
                    tv += 1
            outproj_resid(rctx, l, 2, ti, Wo, wob, 16, o_, [o_b], Xsrc, Xsrc_b, X, X_b)

    import os
    STOP = int(os.environ.get("KSTOP", "99"))

    class _Stop(Exception):
        pass

    def chk(k):
        if STOP == k:
            raise _Stop()

    try:
        prologue()
        chk(0)
        Xsrc, Xsrc_b = xT, xT_b
        for l in range(NLAYERS):
            kind = l % 4
            if kind == 0:
                qkv_phase(l, w0_qk, 12, w0_v, 256, Xsrc, Xsrc_b, True, kout=(8, o_wk, 64), vout=o_wv)
                chk(1)
                attn_phase(l, "win", Xsrc, Xsrc_b)
                chk(2)
                outproj_phase(l, w0_o, 8, OS, None, Xsrc, Xsrc_b)
                chk(3)
            elif kind == 1:
                gla_layer(l, Xsrc, Xsrc_b)
            elif kind == 2:
                qkv_phase(l, w2_qk, 16, w2_v, 1024, Xsrc, Xsrc_b, True, kout=(8, o_dk, 128), vout=o_dv)
                attn_phase(l, "diff", Xsrc, Xsrc_b)
                outproj_phase(l, w2_o, 8, OS, None, Xsrc, Xsrc_b)
            else:
                ssd_layer(l, Xsrc, Xsrc_b)
            Xsrc, Xsrc_b = X, X_b
            ffn(l, Xsrc, Xsrc_b)
            chk(10 + l)
        s.barrier()
        s.release()
        nctx = NormCtx()
        for ti in range(NT):
            norm_tile(nctx, 0, 0, ti, Xsrc, Xsrc_b, final_out=yT)
    except _Stop:
        pass
    s.barrier()
    counts = s.emit()
    return nc, counts

def _fm(w):
    K, N = w.shape
    return np.ascontiguousarray(w.reshape(K // 128, 128, N).transpose(1, 0, 2))


def _pc(v):
    return np.ascontiguousarray(v.reshape(-1, 128).T)


def _consts():
    f32 = np.float32
    t = np.arange(LS)
    row = (t // 64).astype(f32)
    col = (t % 64).astype(f32)
    inv = (10000.0 ** (-np.arange(0, 32, 2, dtype=f32) / 32)).astype(f32)
    cos = np.zeros((128, LS), f32)
    sin = np.zeros((128, LS), f32)
    perm = np.zeros((128, 128), f32)
    for p in range(128):
        d = p % 64
        axis = d // 32
        idx = d % 16
        second = (d % 32) >= 16
        ang = (row if axis == 0 else col) * inv[idx]
        cos[p] = np.cos(ang)
        sin[p] = np.sin(ang) * (1.0 if second else -1.0)
        partner = p + 16 if not second else p - 16
        perm[partner, p] = 1.0
    j = np.arange(128)[:, None]
    i = np.arange(512)[None, :]
    wmask = np.zeros((128, 6, 512), f32)
    for mi in range(6):
        o = mi - 1
        wmask[:, mi, :] = (np.abs(i - (o * 128 + j)) <= 128).astype(f32)
    tri = np.zeros((128, 4, 128), f32)
    jj = np.arange(128)[:, None]
    ii = np.arange(128)[None, :]
    tri[:, 0, :] = (jj <= ii)
    tri[:, 1, :] = (jj >= ii)
    tri[0:64, 2, 0:64] = (jj[0:64] <= ii[:, 0:64])
    tri[0:64, 3, 0:64] = (jj[0:64] >= ii[:, 0:64])
    m64 = np.ones((128, 512), f32)
    m64[:, ::64] = 0.0
    return dict(rcos=cos, rsin=sin, perm=perm, wmask=wmask, tri=tri, m64=m64)


_PROG = {}


def _get_prog(nl):
    if nl not in _PROG:
        _PROG[nl] = build_program(nl)
    return _PROG[nl]


def kernel(NLAYERS=4, **inp):
    f32 = np.float32
    g = {k: np.asarray(v) for k, v in inp.items()}
    nc, counts = _get_prog(NLAYERS)
    common = dict(_consts())
    common["ada_w"] = np.ascontiguousarray(g["ada_w"].reshape(4, 8, 128, 6144).transpose(0, 2, 1, 3))
    common["ada_b"] = np.ascontiguousarray(g["ada_b"].reshape(4, 48, 128).transpose(2, 0, 1))
    common["nrm"] = np.ascontiguousarray(np.stack([g["norm_mix"], g["norm_ffn"]], axis=1).reshape(4, 2, 8, 128).transpose(3, 0, 1, 2))
    common["fnorm"] = _pc(g["final_norm"])
    common["w_up"] = np.ascontiguousarray(g["ffn_w_up"].reshape(4, 8, 128, 5632).transpose(0, 2, 1, 3))
    common["w_dn"] = np.ascontiguousarray(g["ffn_w_down"].reshape(4, 22, 128, 1024).transpose(0, 2, 1, 3))
    common["ffn_cw"] = np.ascontiguousarray(g["ffn_conv_w"].reshape(4, 3, 44, 128).transpose(3, 0, 1, 2))
    common["ffn_cb"] = np.ascontiguousarray(g["ffn_conv_b"].reshape(4, 44, 128).transpose(2, 0, 1))
    wq = g["win_w_qkv"][0]
    qcols = wq[:, 0:1024]
    kcols = wq[:, 1024:1280]
    vcols = wq[:, 1280:1536]
    kdup = np.concatenate([np.concatenate([kcols[:, h * 64:(h + 1) * 64]] * 2, axis=1) for h in range(4)], axis=1)
    common["w0_qk"] = _fm(np.concatenate([qcols, kdup], axis=1))
    common["w0_v"] = _fm(vcols)
    common["w0_o"] = _fm(g["win_w_o"][0])
    common["sink"] = np.ascontiguousarray(g["win_sink"][0])
    wg = g["gla_w_qkvr"][0]
    common["w1_qk"] = _fm(wg[:, 0:1024])
    common["w1_v"] = _fm(wg[:, 1024:2048])
    common["w1_r"] = _fm(wg[:, 2048:3072])
    common["w1_g1"] = _fm(np.concatenate([g["gla_w_gf1"][0], g["gla_w_gb1"][0]], axis=1))
    g2 = np.zeros((32, 1024), f32)
    g2[0:16, 0:512] = g["gla_w_gf2"][0]
    g2[16:32, 512:1024] = g["gla_w_gb2"][0]
    common["w1_g2"] = g2
    common["w1_gb"] = _pc(np.concatenate([g["gla_b_gf"][0], g["gla_b_gb"][0]]))
    common["w1_gn"] = _pc(g["gla_norm"][0])
    common["w1_o"] = _fm(g["gla_w_o"][0])
    wd = g["diff_w_qkv"][0]
    common["w2_qk"] = _fm(wd[:, 0:2048])
    common["w2_v"] = _fm(wd[:, 2048:3072])
    common["w2_o"] = _fm(g["diff_w_o"][0])
    common["lqk"] = np.ascontiguousarray(np.stack([g["diff_lq1"][0], g["diff_lk1"][0], g["diff_lq2"][0], g["diff_lk2"][0]]))
    common["w2_gn"] = np.ascontiguousarray(g["diff_norm"][0].reshape(128, 1))
    common.update(_host_ssd_common(g))
    in_maps = []
    for b in range(8):
        m = dict(common)
        xs = g["x_sample"][b]
        xp = g["x_prompt"][2 * b:2 * b + 2].reshape(512, 1024)
        xa = np.concatenate([xs, xp], axis=0)
        m["xT"] = np.ascontiguousarray(xa.T.reshape(8, 128, T).transpose(1, 0, 2))
        cond = np.stack([g["c"][b], g["c_ctx"]], axis=1)
        m["condT"] = np.ascontiguousarray(cond.reshape(8, 128, 2).transpose(1, 0, 2))
        ck = g["cache_win_k"][b, 0]
        kT = ck.transpose(2, 1, 0)
        m["ck0"] = np.ascontiguousarray(np.concatenate([kT, kT], axis=0))
        m["cv0"] = np.ascontiguousarray(g["cache_win_v"][b, 0].reshape(512, 256))
        m["s1f"] = np.ascontiguousarray(g["state_gla_fwd"][b, 0].transpose(1, 0, 2))
        m["s1b"] = np.ascontiguousarray(g["state_gla_bwd"][b, 0].transpose(1, 0, 2))
        dk = g["cache_diff_k"][b, 0]
        m["ck2"] = np.ascontiguousarray(dk.transpose(2, 3, 1, 0).reshape(128, 8, 512))
        m["cv2"] = np.ascontiguousarray(g["cache_diff_v"][b, 0].reshape(512, 1024))
        m.update(_host_ssd_core(g, b))
        in_maps.append(m)
    import os
    ncores = int(os.environ.get("KCORES", "8"))
    ktrace = os.environ.get("KTRACE", "") == "1"
    res = run_bass_kernel_spmd(nc, in_maps[:ncores], core_ids=list(range(ncores)), trace=ktrace) if ktrace else run_bass_kernel_spmd(nc, in_maps[:ncores], core_ids=list(range(ncores)))
    if ktrace:
        print("EXEC_TIME_NS", res.exec_time_ns)
    R = list(res.results) + [res.results[0]] * (8 - ncores)
    y_prompt = np.zeros((16, 256, 1024), f32)
    y_sample = np.zeros((8, 4096, 1024), f32)
    win_k = np.zeros((16, 1, 256, 4, 64), f32)
    win_v = np.zeros((16, 1, 256, 4, 64), f32)
    gla_f = np.zeros((16, 1, 4, 128, 256), f32)
    gla_b = np.zeros((16, 1, 4, 128, 256), f32)
    diff_k = np.zeros((16, 1, 256, 8, 2, 64), f32)
    diff_v = np.zeros((16, 1, 256, 8, 128), f32)
    ssd_f = np.zeros((16, 1, 32, 64, 128), f32)
    ssd_b = np.zeros((16, 1, 32, 64, 128), f32)
    for b in range(8):
        r = R[b]
        y = r["yT"].transpose(1, 0, 2).reshape(1024, T).T
        y_sample[b] = y[0:4096]
        y_prompt[2 * b:2 * b + 2] = y[4096:].reshape(2, 256, 1024)
        wk = r["o_wk"]
        win_k[2 * b:2 * b + 2, 0] = wk.transpose(2, 0, 1).reshape(2, 256, 4, 64)
        win_v[2 * b:2 * b + 2, 0] = r["o_wv"].reshape(2, 256, 4, 64)
        gla_f[2 * b:2 * b + 2, 0] = r["o_gf"].transpose(0, 2, 1, 3)
        gla_b[2 * b:2 * b + 2, 0] = r["o_gb"].transpose(0, 2, 1, 3)
        dk = r["o_dk"]
        diff_k[2 * b:2 * b + 2, 0] = dk.transpose(2, 0, 1).reshape(2, 256, 8, 2, 64)
        diff_v[2 * b:2 * b + 2, 0] = r["o_dv"].reshape(2, 256, 8, 128)
        _host_ssd_out(r, b, ssd_f, ssd_b)
    return (y_prompt, y_sample, win_k, win_v, gla_f, gla_b, diff_k, diff_v, ssd_f, ssd_b)


def _host_ssd_common(g):
    w = g["ssd_w_in"][0]
    out = {}
    out["w3_z"] = _fm(w[:, 0:2048])
    out["w3_xbc"] = _fm(w[:, 2048:5120])
    out["w3_dt"] = _fm(w[:, 5120:5184])
    out["w3_o"] = _fm(g["ssd_w_out"][0])
    out["ssd_cw"] = np.ascontiguousarray(g["ssd_conv_w"][0].reshape(3, 24, 128).transpose(2, 0, 1))
    out["ssd_cb"] = _pc(g["ssd_conv_b"][0])
    out["ssd_vec"] = np.ascontiguousarray(np.stack([g["ssd_a_log_f"][0], g["ssd_a_log_b"][0], g["ssd_dt_bias_f"][0], g["ssd_dt_bias_b"][0], g["ssd_d"][0]]))
    out["ssd_gn"] = np.ascontiguousarray(g["ssd_norm"][0])
    return out


def _host_ssd_core(g, b):
    return {"s3f": np.ascontiguousarray(g["state_ssd_fwd"][b, 0].transpose(2, 0, 1).reshape(128, 2048)),
            "s3b": np.ascontiguousarray(g["state_ssd_bwd"][b, 0].transpose(2, 0, 1).reshape(128, 2048))}


def _host_ssd_out(r, b, ssd_f, ssd_b):
    ssd_f[2 * b:2 * b + 2, 0] = r["o_sf"].reshape(2, 128, 32, 64).transpose(0, 2, 3, 1)
    ssd_b[2 * b:2 * b + 2, 0] = r["o_sb"].reshape(2, 128, 32, 64).transpose(0, 2, 3, 1)
```
